# Optimizing a Trainium2 kernel written in Bass

```python
import jax, jax.numpy as jnp
from jax import lax
import numpy as np

D_MODEL = 1024
BATCH = 2
SEQ = 8192
DEPTH = 2

GRID_W = 64
CTX_LEN = 256
N_EVEN = (DEPTH + 1) // 2
N_ODD = DEPTH // 2
EPS = 1e-6

RWKV_HEAD = 64
RWKV_HEADS = (3 * D_MODEL // 4) // RWKV_HEAD
RWKV_W = RWKV_HEADS * RWKV_HEAD
LORA_W = 64
LORA_A = 64
FOUR_GROUPS = 4
FOUR_W = D_MODEL - RWKV_W
FOUR_GROUP_W = FOUR_W // FOUR_GROUPS
MIX_W = RWKV_W + FOUR_W
N_SHIFT = 3 * RWKV_W + 2 * LORA_W + 2 * LORA_A
EVEN_IN = N_SHIFT + RWKV_W + 2 * FOUR_W
RWKV_GN_EPS = 64e-5

RET_HEADS = 8
RET_HEAD = D_MODEL // RET_HEADS
RET_W = RET_HEADS * RET_HEAD
ODD_IN = 4 * RET_W
RET_CHUNK = 128
ROPE_BASE = 10000.0

kernel_name = "hybrid_rwkv7_fnet_retention_dit"


def rmsnorm(x, g):
    xf = x.astype(jnp.float32)
    y = xf * lax.rsqrt(jnp.mean(xf * xf, axis=-1, keepdims=True) + EPS)
    return (y * g.astype(jnp.float32)).astype(x.dtype)


def qshift(u, rows):
    b, l, ch = u.shape
    g = u.reshape(b, rows, GRID_W, ch // 4, 4)
    left = jnp.pad(g[:, :, :-1, :, 0], ((0, 0), (0, 0), (1, 0), (0, 0)))
    right = jnp.pad(g[:, :, 1:, :, 1], ((0, 0), (0, 0), (0, 1), (0, 0)))
    up = jnp.pad(g[:, :-1, :, :, 2], ((0, 0), (1, 0), (0, 0), (0, 0)))
    down = jnp.pad(g[:, 1:, :, :, 3], ((0, 0), (0, 1), (0, 0), (0, 0)))
    return jnp.stack([left, right, up, down], axis=-1).reshape(b, l, ch)


def bishift(u):
    b, l, ch = u.shape
    g = u.reshape(b, l, ch // 2, 2)
    prev = jnp.pad(g[:, :-1, :, 0], ((0, 0), (1, 0), (0, 0)))
    nxt = jnp.pad(g[:, 1:, :, 1], ((0, 0), (0, 1), (0, 0)))
    return jnp.stack([prev, nxt], axis=-1).reshape(b, l, ch)


def rwkv_dir_terms(r, k0, v, wl, al, w0, w2, a0, a2, k_k, k_a, r_k):
    b, l, _ = r.shape
    hs = lambda t: t.reshape(b, l, RWKV_HEADS, RWKV_HEAD)
    w_log = -jax.nn.softplus(-(w0 + jnp.tanh(wl) @ w2)) - 0.5
    decay = jnp.exp(-jnp.exp(w_log))
    a = jax.nn.sigmoid(a0 + al @ a2)
    kk = hs(k0 * k_k)
    kk = kk / jnp.maximum(jnp.sqrt(jnp.sum(kk * kk, axis=-1, keepdims=True)), 1e-12)
    k = hs(k0 * (1.0 + (a - 1.0) * k_a))
    rh, vh = hs(r), hs(v)
    bonus = jnp.sum(rh * k * r_k, axis=-1, keepdims=True) * vh
    return rh, hs(decay), k, vh, kk, kk * hs(a), bonus


def rwkv7_scan(r, w, k, v, kk, bb, s0, reverse):
    def step(s, inp):
        r_t, w_t, k_t, v_t, kk_t, b_t = inp
        sa = jnp.einsum('bhij,bhj->bhi', s, kk_t)
        s = s * w_t[:, :, None, :] - sa[..., None] * b_t[:, :, None, :] + v_t[..., None] * k_t[:, :, None, :]
        return s, jnp.einsum('bhij,bhj->bhi', s, r_t)
    xs = tuple(jnp.moveaxis(t, 1, 0) for t in (r, w, k, v, kk, bb))
    s, ys = lax.scan(step, s0, xs, reverse=reverse)
    return s, jnp.moveaxis(ys, 0, 1)


def fourier_mix(u):
    b, l, _ = u.shape
    g = u.astype(jnp.float32).reshape(b, l, FOUR_GROUPS, FOUR_GROUP_W)
    f = jnp.fft.fft2(g, axes=(1, 3), norm='ortho').real
    return f.reshape(b, l, FOUR_W).astype(u.dtype)


def even_mixer(hc, hl, rows, w_in, mu, w0, w2, a0, a2, k_k, k_a, r_k, lnx_g, lnx_b, w_out, need_ctx):
    cuts = np.cumsum([RWKV_W, RWKV_W, RWKV_W, LORA_W, LORA_W, LORA_A]).tolist()

    def proj(h, shift_fn):
        u = h @ w_in
        s, gate_r, four, gate_f = jnp.split(u, [N_SHIFT, N_SHIFT + RWKV_W, N_SHIFT + RWKV_W + FOUR_W], axis=-1)
        s = (s + (shift_fn(s) - s) * mu).astype(jnp.float32)
        r, k0, v, wl_f, wl_b, al_f, al_b = jnp.split(s, cuts, axis=-1)
        return r, k0, v, (wl_f, wl_b), (al_f, al_b), gate_r, four, gate_f

    pc = proj(hc, bishift)
    pl = proj(hl, lambda t: qshift(t, rows))
    bsz = hl.shape[0]
    zeros = jnp.zeros((bsz, RWKV_HEADS, RWKV_HEAD, RWKV_HEAD), jnp.float32)
    oc, ol = 0.0, 0.0
    for d in range(2):
        tc = rwkv_dir_terms(pc[0], pc[1], pc[2], pc[3][d], pc[4][d], w0[d], w2[d], a0[d], a2[d], k_k, k_a, r_k)
        tl = rwkv_dir_terms(pl[0], pl[1], pl[2], pl[3][d], pl[4][d], w0[d], w2[d], a0[d], a2[d], k_k, k_a, r_k)
        s_ctx, yc = rwkv7_scan(*tc[:6], zeros, d == 1)
        _, yl = rwkv7_scan(*tl[:6], s_ctx, d == 1)
        oc = oc + yc + tc[6]
        ol = ol + yl + tl[6]

    def finish(o, gate_r, four, gate_f):
        b, l = o.shape[:2]
        mean = jnp.mean(o, axis=-1, keepdims=True)
        var = jnp.mean(jnp.square(o - mean), axis=-1, keepdims=True)
        o = ((o - mean) * lax.rsqrt(var + RWKV_GN_EPS)).reshape(b, l, RWKV_W) * lnx_g + lnx_b
        yr = o.astype(gate_r.dtype) * jax.nn.silu(gate_r)
        yf = fourier_mix(four) * jax.nn.silu(gate_f)
        return jnp.concatenate([yr, yf], axis=-1) @ w_out

    out_l = finish(ol, pl[5], pl[6], pl[7])
    out_c = finish(oc, pc[5], pc[6], pc[7]) if need_ctx else None
    return out_c, out_l


def axial_rope(t):
    l = t.shape[1]
    pos = jnp.arange(l)
    rc = jnp.stack([pos // GRID_W, pos % GRID_W], axis=-1).astype(jnp.float32)
    half = RET_HEAD // 2
    inv = ROPE_BASE ** (-jnp.arange(0, half, 2, dtype=jnp.float32) / half)
    ang = (rc[:, :, None] * inv).reshape(l, half)[:, None, :]
    cos, sin = jnp.cos(ang), jnp.sin(ang)
    t1, t2 = t[..., :half], t[..., half:]
    return jnp.concatenate([t1 * cos - t2 * sin, t1 * sin + t2 * cos], axis=-1)


def retention_scan(q, k, v, log_gamma, r0, strict):
    bsz, l, h, dk = q.shape
    dv = v.shape[-1]
    n_chunks = l // RET_CHUNK
    pos = jnp.arange(RET_CHUNK, dtype=jnp.float32)
    diff = pos[:, None] - pos[None, :]
    mask = diff > 0 if strict else diff >= 0
    dmat = jnp.where(mask[None], jnp.exp(jnp.where(mask, diff, 0.0)[None] * log_gamma[:, None, None]), 0.0)
    xi = jnp.exp((pos[:, None] + 1.0) * log_gamma[None, :])
    zeta = jnp.exp((RET_CHUNK - 1.0 - pos[:, None]) * log_gamma[None, :])
    g_chunk = jnp.exp(RET_CHUNK * log_gamma)

    def chunk(r, inp):
        qc, kc, vc = inp
        sc = jnp.einsum('bihd,bjhd->bhij', qc, kc) * dmat
        inner = jnp.einsum('bhij,bjhe->bihe', sc, vc)
        cross = jnp.einsum('bihd,bhde->bihe', qc, r) * xi[None, :, :, None]
        r = r * g_chunk[None, :, None, None] + jnp.einsum('bjhd,bjhe->bhde', kc * zeta[None, :, :, None], vc)
        return r, inner + cross

    to_chunks = lambda t: jnp.moveaxis(t.reshape(bsz, n_chunks, RET_CHUNK, h, t.shape[-1]), 1, 0)
    r, o = lax.scan(chunk, r0, (to_chunks(q), to_chunks(k), to_chunks(v)))
    return r, jnp.moveaxis(o, 0, 1).reshape(bsz, l, h, dv)


def odd_mixer(hc, hl, w_in, decay_logit, w_out, need_ctx):
    bsz = hl.shape[0]

    def proj(h, rope):
        u = h @ w_in
        q, k, v, g = jnp.split(u, 4, axis=-1)
        l = h.shape[1]
        hs = lambda t: t.astype(jnp.float32).reshape(bsz, l, RET_HEADS, RET_HEAD)
        q, k, v = hs(q), hs(k) * (RET_HEAD ** -0.5), hs(v)
        if rope:
            q, k = axial_rope(q), axial_rope(k)
        return q, k, v, g

    qc, kc, vc, gc = proj(hc, False)
    ql, kl, vl, gl = proj(hl, True)
    lg = jax.nn.log_sigmoid(decay_logit.astype(jnp.float32))
    zeros = jnp.zeros((bsz, RET_HEADS, RET_HEAD, RET_HEAD), jnp.float32)
    flip = lambda t: jnp.flip(t, axis=1)
    rc_f, oc_f = retention_scan(qc, kc, vc, lg[0], zeros, False)
    _, ol_f = retention_scan(ql, kl, vl, lg[0], rc_f, False)
    rc_b, oc_b = retention_scan(flip(qc), flip(kc), flip(vc), lg[1], zeros, True)
    _, ol_b = retention_scan(flip(ql), flip(kl), flip(vl), lg[1], rc_b, True)

    def finish(of, ob, g):
        o = of + flip(ob)
        o = o * lax.rsqrt(jnp.mean(o * o, axis=-1, keepdims=True) + EPS)
        o = o.reshape(o.shape[0], o.shape[1], RET_W).astype(g.dtype)
        return (o * jax.nn.silu(g)) @ w_out

    out_l = finish(ol_f, ol_b, gl)
    out_c = finish(oc_f, oc_b, gc) if need_ctx else None
    return out_c, out_l


def setup_inputs(seed: int = 0) -> dict:
    key = jax.random.key(seed)
    ks = jax.random.split(key, 24)
    f32 = jnp.float32
    nrm = lambda k, shape, s: jax.random.normal(k, shape, f32) * s
    D = D_MODEL
    gam = 1.0 - 2.0 ** (-5.0 - jnp.arange(RET_HEADS, dtype=f32))
    return {
        'x': nrm(ks[0], (BATCH, SEQ, D), 1.0),
        'c': nrm(ks[1], (BATCH, D), 1.0),
        'ctx': nrm(ks[2], (BATCH, CTX_LEN, D), 1.0),
        'c_ctx': nrm(ks[3], (D,), 1.0),
        'ada_w': nrm(ks[4], (DEPTH, D, 3 * D), D ** -0.5),
        'ada_b': nrm(ks[5], (DEPTH, 3 * D), 0.02),
        'norm_pre': 1.0 + nrm(ks[6], (DEPTH, D), 0.05),
        'norm_post': 1.0 + nrm(ks[7], (DEPTH, D), 0.05),
        'ev_w_in': nrm(ks[8], (N_EVEN, D, EVEN_IN), D ** -0.5),
        'ev_mu': jax.random.uniform(ks[9], (N_EVEN, N_SHIFT), f32),
        'ev_w0': jax.random.uniform(ks[10], (N_EVEN, 2, RWKV_W), f32, -4.0, 0.5),
        'ev_w2': nrm(ks[11], (N_EVEN, 2, LORA_W, RWKV_W), 0.1 * LORA_W ** -0.5),
        'ev_a0': nrm(ks[12], (N_EVEN, 2, RWKV_W), 0.5),
        'ev_a2': nrm(ks[13], (N_EVEN, 2, LORA_A, RWKV_W), 0.1 * LORA_A ** -0.5),
        'ev_k_k': 0.85 + nrm(ks[14], (N_EVEN, RWKV_W), 0.05),
        'ev_k_a': 1.0 + nrm(ks[15], (N_EVEN, RWKV_W), 0.05),
        'ev_r_k': nrm(ks[16], (N_EVEN, RWKV_HEADS, RWKV_HEAD), 0.1),
        'ev_lnx_g': 1.0 + nrm(ks[17], (N_EVEN, RWKV_W), 0.05),
        'ev_lnx_b': nrm(ks[18], (N_EVEN, RWKV_W), 0.02),
        'ev_w_out': nrm(ks[19], (N_EVEN, MIX_W, D), MIX_W ** -0.5),
        'od_w_in': nrm(ks[20], (N_ODD, D, ODD_IN), D ** -0.5),
        'od_decay_logit': (jnp.log(gam) - jnp.log1p(-gam)) + nrm(ks[21], (N_ODD, 2, RET_HEADS), 0.1),
        'od_w_out': nrm(ks[22], (N_ODD, RET_W, D), RET_W ** -0.5),
    }


def reference(x, c, ctx, c_ctx, ada_w, ada_b, norm_pre, norm_post, ev_w_in, ev_mu, ev_w0, ev_w2, ev_a0, ev_a2,
              ev_k_k, ev_k_a, ev_r_k, ev_lnx_g, ev_lnx_b, ev_w_out, od_w_in, od_decay_logit, od_w_out):
    rows = x.shape[1] // GRID_W
    xl, xc = x, ctx
    sc, scc = jax.nn.silu(c), jax.nn.silu(c_ctx)
    for layer in range(DEPTH):
        need_ctx = layer < DEPTH - 1
        shift_l, scale_l, gate_l = jnp.split(sc @ ada_w[layer] + ada_b[layer], 3, axis=-1)
        shift_c, scale_c, gate_c = jnp.split(scc @ ada_w[layer] + ada_b[layer], 3, axis=-1)
        hl = rmsnorm(xl, norm_pre[layer]) * (1.0 + scale_l[:, None, :]) + shift_l[:, None, :]
        hc = rmsnorm(xc, norm_pre[layer]) * (1.0 + scale_c) + shift_c
        if layer % 2 == 0:
            i = layer // 2
            oc, ol = even_mixer(hc, hl, rows, ev_w_in[i], ev_mu[i], ev_w0[i], ev_w2[i], ev_a0[i], ev_a2[i],
                                ev_k_k[i], ev_k_a[i], ev_r_k[i], ev_lnx_g[i], ev_lnx_b[i], ev_w_out[i], need_ctx)
        else:
            i = layer // 2
            oc, ol = odd_mixer(hc, hl, od_w_in[i], od_decay_logit[i], od_w_out[i], need_ctx)
        xl = xl + gate_l[:, None, :] * rmsnorm(ol, norm_post[layer])
        if need_ctx:
            xc = xc + gate_c * rmsnorm(oc, norm_post[layer])
    return xl
```

```python
import numpy as np
import concourse.bass as bass
import concourse.mybir as mybir
from contextlib import ExitStack
from concourse.bass_utils import run_bass_kernel_spmd

F32 = mybir.dt.float32
BF16 = mybir.dt.bfloat16
ALU = mybir.AluOpType
AF = mybir.ActivationFunctionType

ENGS = ['pe', 'act', 'dve', 'pool', 'sp']
DBG = {}
NDSEM = 16


class Buf:
    __slots__ = ('name', 'w', 'r', 'excl', 'strict')

    def __init__(self, name='', excl=False, strict=False):
        self.name = name
        self.w = None
        self.r = {}
        self.excl = excl
        self.strict = strict


class Op:
    __slots__ = ('eng', 'fn', 'deps', 'signal', 'sig', 'sem', 'target', 'isdma', 'final', 'cc')


class _Rec:
    def __init__(self):
        self.call = None

    def __getattr__(self, name):
        def f(*a, **k):
            self.call = (name, a, k)
            return self
        return f


def _flat(xs):
    out = []
    for x in xs:
        if isinstance(x, (list, tuple)):
            out.extend(_flat(x))
        else:
            out.append(x)
    return out


class KB:
    def __init__(self, nc):
        self.nc = nc
        self.ops = {e: [] for e in ENGS}
        self.phase = Buf('phase')
        self.es = ExitStack()
        self.nops = 0

    def sb(self, name, shape, dtype, es=None):
        self.nalloc = getattr(self, 'nalloc', 0) + 1
        return (es or self.es).enter_context(self.nc.sbuf_tensor('s%d_%s' % (self.nalloc, name), list(shape), dtype))

    def ps(self, name, shape, dtype=F32, es=None):
        self.nalloc = getattr(self, 'nalloc', 0) + 1
        return (es or self.es).enter_context(self.nc.psum_tensor('p%d_%s' % (self.nalloc, name), list(shape), dtype))

    def op(self, eng, fn, reads=(), writes=(), dma=False, final=False, nophase=False, cc=False):
        dma = dma or cc
        o = Op()
        rec = _Rec()
        fn(rec)
        assert rec.call is not None
        o.eng = eng; o.fn = rec.call; o.isdma = dma; o.signal = dma; o.deps = []; o.sig = 0
        o.sem = None; o.target = 0; o.final = final; o.cc = cc
        reads = _flat(reads)
        writes = _flat(writes)
        for b in list(reads):
            if b.excl:
                reads.remove(b)
                if b not in writes:
                    writes.append(b)
        if not nophase:
            reads.append(self.phase)
        deps = {}
        sdeps = set()
        for b in reads:
            if b.w is not None:
                deps[id(b.w)] = b.w
                if b.strict:
                    sdeps.add(id(b.w))
        for b in writes:
            if b.w is not None:
                deps[id(b.w)] = b.w
                if b.strict:
                    sdeps.add(id(b.w))
            for r in b.r.values():
                deps[id(r)] = r
                if b.strict:
                    sdeps.add(id(r))
        for d in deps.values():
            if d is o:
                continue
            if d.isdma or dma or d.eng != eng or id(d) in sdeps or (eng != 'pe' and DBG.get('strict_all', True)):
                o.deps.append(d)
                d.signal = True
        key = ('dma', self.nops) if dma else eng
        for b in reads:
            b.r[key] = o
        for b in writes:
            b.w = o
            b.r = {}
        self.ops[eng].append(o)
        self.nops += 1
        return o

    def barrier(self, tile_ap):
        self.op('dve', lambda e: e.memset(tile_ap, 0.0), writes=[self.phase], nophase=True)

    def emit(self):
        nc = self.nc
        es = self.es
        csem = {}
        for e in ['pe', 'act', 'dve', 'pool']:
            csem[e] = es.enter_context(nc.semaphore('c_' + e))
        dsem = {}
        for e in ENGS:
            if any(o.isdma for o in self.ops[e]):
                dsem[e] = [es.enter_context(nc.semaphore('d_%s%d' % (e, i))) for i in range(NDSEM)]
        for e in ENGS:
            cnt = 0
            nd = 0
            hist = []
            for o in self.ops[e]:
                if o.cc:
                    self.ncc = getattr(self, 'ncc', 0) + 1
                    o.sem = es.enter_context(nc.semaphore('ccs%d' % self.ncc))
                    o.target = 1
                elif o.isdma:
                    slot = nd % NDSEM
                    o.sem = dsem[e][slot]
                    o.target = 16 * (nd // NDSEM + 1)
                    if nd >= NDSEM:
                        o.deps.append(hist[nd - NDSEM])
                    hist.append(o)
                    nd += 1
                elif o.signal:
                    cnt += 1
                    o.sig = cnt
                    o.sem = csem[e]
                    o.target = cnt
        finals = [o for e in ENGS for o in self.ops[e] if o.final]

        def run(engname):
            def body(e):
                waited = {}
                for o in self.ops[engname]:
                    need = {}
                    for d in o.deps:
                        k = id(d.sem)
                        if waited.get(k, 0) < d.target and need.get(k, (None, 0))[1] < d.target:
                            need[k] = (d.sem, d.target)
                    for k, (sem, tgt) in need.items():
                        e.wait_ge(sem, tgt)
                        waited[k] = tgt
                    nm_, a_, k_ = o.fn
                    ins = getattr(e, nm_)(*a_, **k_)
                    if o.signal:
                        ins.then_inc(o.sem, 16 if (o.isdma and not o.cc) else 1)
                if engname == 'sp':
                    for d in finals:
                        k = id(d.sem)
                        if waited.get(k, 0) < d.target:
                            e.wait_ge(d.sem, d.target)
                            waited[k] = d.target
            return body

        with nc.Block() as block:
            block.tensor(run('pe'))
            block.scalar(run('act'))
            block.vector(run('dve'))
            block.gpsimd(run('pool'))
            block.sync(run('sp'))


class Ring:
    def __init__(self, kb, name, shape, dtype, n, psum=False, es=None, strict=False):
        self.n = n
        self.i = 0
        self.slots = []
        for j in range(n):
            t = (kb.ps if psum else kb.sb)('%s%d' % (name, j), shape, dtype, es=es)
            b = Buf('%s%d' % (name, j), excl=psum, strict=strict)
            self.slots.append((t, b))
            if not psum and DBG.get('init_rings', True):
                kb.op('pool', lambda e, t=t: e.memset(t[:], 0.0), writes=[b])

    def next(self):
        s = self.slots[self.i % self.n]
        self.i += 1
        return s


D = 1024
L = 8192
LC = 256
LT = L + LC
NCH = LT // 128
EPS = 1e-6
GN_EPS = 64e-5
FM_BLOCKS = {'r01': (0, 128), 'r2': (128, 64), 'k01': (192, 128), 'k2': (320, 64),
             'v01': (384, 128), 'v2': (512, 64), 'wl': (576, 128), 'al': (704, 128), 'four': (832, 64)}
NFM = 896
TILES = [(0, 256)] + [(256 + 512 * i, 512) for i in range(16)]


def dram_in(nc, name, shape, dtype=F32):
    return nc.dram_tensor(name, list(shape), dtype, kind="ExternalInput").ap()


def dram_out(nc, name, shape, dtype=F32):
    return nc.dram_tensor(name, list(shape), dtype, kind="ExternalOutput").ap()


def dram_tmp(nc, name, shape, dtype=F32, debug=False):
    return nc.dram_tensor(name, list(shape), dtype, kind="ExternalOutput" if debug else "Internal").ap()


def load_consts(kb, items, eng='sp'):
    bufs = []
    for t, src in items:
        b = Buf()
        kb.op(eng, (lambda e, t=t, src=src: e.dma_start(out=t[:], in_=src)), writes=[b], dma=True)
        bufs.append(b)
    return bufs


def adaln_vectors(kb, es, nc, sil_in, adaw, adab, normw, ncolblk, mod_tile=None):
    sil = kb.sb('sil', [128, 8, 2], F32, es)
    silb = Buf(strict=True)
    kb.op('sp', lambda e: e.dma_start(out=sil[:], in_=sil_in), writes=[silb], dma=True)
    kb.op('act', lambda e: e.activation(out=sil[:], in_=sil[:], func=AF.Silu), reads=[silb], writes=[silb])
    adb = kb.sb('adb', [128, ncolblk], F32, es)
    adbb = Buf(strict=True)
    kb.op('sp', lambda e: e.dma_start(out=adb[:], in_=adab), writes=[adbb], dma=True)
    mod = mod_tile if mod_tile is not None else kb.sb('mod', [128, ncolblk, 2], F32, es)
    modb = Buf(strict=True)
    psm = kb.ps('psmod', [128, ncolblk, 2], F32, es)
    psb = Buf(excl=True)
    wt = kb.sb('adaw', [128, 8, ncolblk * 128], F32, es)
    wb = Buf()
    for k in range(8):
        kb.op('sp', lambda e, k=k: e.dma_start(out=wt[:, k, :], in_=adaw[k * 128:(k + 1) * 128, :]), writes=[wb], dma=True)
    for cb in range(ncolblk):
        for k in range(8):
            kb.op('pe', lambda e, k=k, cb=cb: e.matmul(psm[:, cb, :], lhsT=wt[:, k, cb * 128:(cb + 1) * 128],
                                                      rhs=sil[:, k, :], start=(k == 0), stop=(k == 7)),
                  reads=[wb, silb], writes=[psb])
    for n in range(2):
        kb.op('dve', lambda e, n=n: e.tensor_tensor(out=mod[:, :, n], in0=psm[:, :, n], in1=adb[:, :], op=ALU.add),
              reads=[psb, adbb], writes=[modb])
    return mod, modb


def phase_proj(kb, nc, es, xin, Wsrc_list, G, shiftv, Gb, ident_bf, identb, emit_tile, nmod=2, get_x=None, shift_off=0):
    xring = Ring(kb, 'xt', [128, D], F32, 3, es=es)
    xnring = Ring(kb, 'xn', [128, D], BF16, 2, es=es)
    stat = Ring(kb, 'stat', [128, 4], F32, 4, es=es, strict=True)
    junk = kb.sb('junk', [128, D], BF16, es)
    junkb = Buf()
    hring = Ring(kb, 'hT', [128, 8, 512], BF16, 2, es=es)
    ptr = Ring(kb, 'ptr', [128, 8, 128], BF16, 2, psum=True, es=es)
    for (t0, T) in TILES:
        n = 1 if t0 < LC else 0
        hT, hb = hring.next()
        for sub in range(T // 128):
            r0 = t0 + sub * 128
            if get_x is not None:
                xt, xb = get_x(r0)
            else:
                xt, xb = xring.next()
                kb.op('sp', lambda e, xt=xt, r0=r0: e.dma_start(out=xt[:], in_=xin[r0:r0 + 128, :]), writes=[xb], dma=True)
            st, sb_ = stat.next()
            kb.op('act', lambda e, xt=xt, st=st: e.activation(out=junk[:], in_=xt[:], func=AF.Square, accum_out=st[:, 0:1]),
                  reads=[xb], writes=[junkb, sb_])
            kb.op('act', lambda e, st=st: e.activation(out=st[:, 1:2], in_=st[:, 0:1], func=AF.Sqrt, scale=1.0 / D, bias=Gb['eps'][:, 0:1]),
                  reads=[sb_, Gb['epsb']], writes=[sb_])
            kb.op('dve', lambda e, st=st: e.reciprocal(out=st[:, 2:3], in_=st[:, 1:2]), reads=[sb_], writes=[sb_])
            xn, xnb = xnring.next()
            kb.op('dve', lambda e, xn=xn, xt=xt, st=st: e.tensor_scalar(out=xn[:], in0=xt[:], scalar1=st[:, 2:3], scalar2=None, op0=ALU.mult),
                  reads=[xb, sb_], writes=[xnb])
            pt, pb = ptr.next()
            for k in range(8):
                kb.op('pe', lambda e, pt=pt, xn=xn, k=k: e.transpose(out=pt[:, k, :], in_=xn[:, k * 128:(k + 1) * 128], identity=ident_bf[:]),
                      reads=[xnb, identb], writes=[pb])
            for k in range(8):
                if k % 2 == 0:
                    kb.op('act', lambda e, pt=pt, hT=hT, k=k, sub=sub, n=n: e.activation(
                        out=hT[:, k, sub * 128:(sub + 1) * 128], in_=pt[:, k, :], func=AF.Identity,
                        scale=G[:, k, n:n + 1], bias=shiftv[:, shift_off + k, n:n + 1]), reads=[pb, Gb['G']], writes=[hb])
                else:
                    kb.op('dve', lambda e, pt=pt, hT=hT, k=k, sub=sub, n=n: e.tensor_scalar(
                        out=hT[:, k, sub * 128:(sub + 1) * 128], in0=pt[:, k, :], scalar1=G[:, k, n:n + 1],
                        scalar2=shiftv[:, shift_off + k, n:n + 1], op0=ALU.mult, op1=ALU.add), reads=[pb, Gb['G']], writes=[hb])
        emit_tile(t0, T, hT, hb)


def load_weight_bf16(kb, es, nc, name, src, ncols, es_keep):
    wbf = kb.sb(name, [128, 8, ncols], BF16, es_keep)
    wb = Buf()
    stg = Ring(kb, name + '_stg', [128, ncols], F32, 2, es=es)
    for k in range(8):
        s, sbuf_ = stg.next()
        kb.op('sp', lambda e, s=s, k=k: e.dma_start(out=s[:], in_=src[k * 128:(k + 1) * 128, :]), writes=[sbuf_], dma=True)
        eng = 'dve' if k % 2 == 0 else 'pool'
        kb.op(eng, lambda e, s=s, k=k: e.tensor_copy(out=wbf[:, k, :], in_=s[:]), reads=[sbuf_], writes=[wb])
    return wbf, wb


RW_PARAMS = {'mu': [128, 8], 'lanem': [128, 6], 'k_k': [128, 2], 'k_a': [128, 2], 'r_k': [128, 2], 'a0': [128, 2, 2],
             'w2': [128, 192], 'a2': [128, 192], 'w0': [1, 2, 192], 'maskSI': [128, 2, 256], 'maskSI2': [128, 2, 256], 'maskST': [128, 2, 3, 128],
             'tri': [128, 2, 256], 'ehead': [128, 2, 4], 'ones_bd': [128, 128], 'lnx_g': [128, 192], 'lnx_b': [128, 192]}


def part1(kb, nc, IN, YD, C):
    es = kb.es
    debug = False
    stop_after = None
    xin, sil_in, adaw, adab, normw, wfm, wg, cs64 = (IN[k_] for k_ in ('xin', 'sil_in', 'adaw', 'adab', 'normw', 'wfm', 'wg', 'cs64'))
    U = {nm: dram_tmp(nc, 'U_' + nm, [sz, LT], F32, debug) for nm, (off, sz) in FM_BLOCKS.items() if nm != 'four'}
    GT = dram_tmp(nc, 'GT', [LT, 256], F32, debug)
    HT = dram_tmp(nc, 'HT', [LT, 128], F32, debug)
    bar, ident_f, identfb, ident_bf, identb, epsT, epsb = C['bar'], C['ident_f'], C['identfb'], C['ident_bf'], C['identb'], C['epsT'], C['epsb']
    with ExitStack() as pa:
        mod, modb = adaln_vectors(kb, pa, nc, sil_in, adaw, adab, normw, 16)
        nw = kb.sb('nw', [128, 8], F32, pa)
        nwb, = load_consts(kb, [(nw, normw)])
        G = kb.sb('G', [128, 8, 2], F32, pa)
        Gbuf = Buf(strict=True)
        for n in range(2):
            kb.op('dve', lambda e, n=n: e.scalar_tensor_tensor(out=G[:, :, n], in0=mod[:, 8:16, n], scalar=1.0, in1=nw[:, :],
                                                                op0=ALU.add, op1=ALU.mult), reads=[modb, nwb], writes=[Gbuf])
        Gb = {'G': Gbuf, 'eps': epsT, 'epsb': epsb}
        shiftv = mod
        wbf, wbb = load_weight_bf16(kb, pa, nc, 'wfm_bf', wfm, NFM, pa)
        wgb, wgbb = load_weight_bf16(kb, pa, nc, 'wg_bf', wg, 256, pa)
        cs = kb.sb('cs64', [64, 128], F32, pa)
        csb, = load_consts(kb, [(cs, cs64)])
        pmm = Ring(kb, 'pmm', [128, 512], F32, 3, psum=True, es=pa)
        pg = Ring(kb, 'pg', [128, 512], F32, 1, psum=True, es=pa)
        ustg = Ring(kb, 'ustg', [128, 512], F32, 4, es=pa)
        gstg = Ring(kb, 'gstg', [128, 256], F32, 3, es=pa)
        hstg = Ring(kb, 'hstg', [128, 128], F32, 3, es=pa)
        fstg = Ring(kb, 'fstg', [64, 512], F32, 2, es=pa)
        cnt = [0]

        def emit_tile(t0, T, hT, hb):
            for nm, (off, sz) in FM_BLOCKS.items():
                ps, psb = pmm.next()
                for k in range(8):
                    kb.op('pe', lambda e, ps=ps, k=k, off=off, sz=sz: e.matmul(ps[0:sz, 0:T], lhsT=wbf[:, k, off:off + sz], rhs=hT[:, k, 0:T],
                                                                               start=(k == 0), stop=(k == 7)), reads=[wbb, hb], writes=[psb])
                if nm == 'four':
                    fs, fsb = fstg.next()
                    kb.op('act', lambda e, fs=fs, ps=ps: e.activation(out=fs[:, 0:T], in_=ps[0:64, 0:T], func=AF.Copy), reads=[psb], writes=[fsb])
                    for sub in range(T // 128):
                        pgt, pgb = pg.next()
                        kb.op('pe', lambda e, pgt=pgt, fs=fs, sub=sub: e.matmul(pgt[:, 0:128], lhsT=fs[:, sub * 128:(sub + 1) * 128], rhs=cs[:, :],
                                                                                  start=True, stop=True), reads=[fsb, csb], writes=[pgb])
                        hs, hsb = hstg.next()
                        kb.op('dve', lambda e, hs=hs, pgt=pgt: e.tensor_copy(out=hs[:], in_=pgt[:, 0:128]), reads=[pgb], writes=[hsb])
                        r0 = t0 + sub * 128
                        kb.op('pool', lambda e, hs=hs, r0=r0: e.dma_start(out=HT[r0:r0 + 128, :], in_=hs[:]), reads=[hsb], dma=True)
                else:
                    us, usb = ustg.next()
                    cnt[0] += 1
                    if cnt[0] % 2 == 0:
                        kb.op('act', lambda e, us=us, ps=ps, sz=sz: e.activation(out=us[0:sz, 0:T], in_=ps[0:sz, 0:T], func=AF.Copy), reads=[psb], writes=[usb])
                    else:
                        kb.op('dve', lambda e, us=us, ps=ps, sz=sz: e.tensor_copy(out=us[0:sz, 0:T], in_=ps[0:sz, 0:T]), reads=[psb], writes=[usb])
                    kb.op('pool', lambda e, us=us, nm=nm, sz=sz: e.dma_start(out=U[nm][:, t0:t0 + T], in_=us[0:sz, 0:T]), reads=[usb], dma=True)
            for sub in range(T // 128):
                pgt, pgb = pg.next()
                for k in range(8):
                    kb.op('pe', lambda e, pgt=pgt, k=k, sub=sub: e.matmul(pgt[:, 0:256], lhsT=hT[:, k, sub * 128:(sub + 1) * 128], rhs=wgb[:, k, :],
                                                                           start=(k == 0), stop=(k == 7)), reads=[wgbb, hb], writes=[pgb])
                gs, gsb = gstg.next()
                kb.op('act', lambda e, gs=gs, pgt=pgt: e.activation(out=gs[:], in_=pgt[:, 0:256], func=AF.Silu), reads=[pgb], writes=[gsb])
                r0 = t0 + sub * 128
                kb.op('pool', lambda e, gs=gs, r0=r0: e.dma_start(out=GT[r0:r0 + 128, :], in_=gs[:]), reads=[gsb], dma=True)

        phase_proj(kb, nc, pa, xin, None, G, shiftv, Gb, ident_bf, identb, emit_tile)
        kb.barrier(bar[:])
    BLm = ['r01', 'r2', 'k01', 'k2', 'v01', 'v2', 'wl', 'al']
    S = {nm: dram_tmp(nc, 'S_' + nm, [FM_BLOCKS[nm][1], LT], F32, debug) for nm in BLm}
    with ExitStack() as pm:
        mu_t = kb.sb('mu_m', [128, 8], F32, pm)
        lan_t = kb.sb('lan_m', [128, 6], F32, pm)
        mub_, lanb_ = load_consts(kb, [(mu_t, IN['p_mu']), (lan_t, IN['p_lanem'])])
        cf = kb.sb('coef_m', [128, 8, 7], F32, pm)
        cfb = Buf(strict=True)
        kb.op('dve', lambda e: e.tensor_scalar(out=cf[:, :, 0], in0=mu_t[:, :], scalar1=-1.0, scalar2=1.0, op0=ALU.mult, op1=ALU.add), reads=[mub_], writes=[cfb])
        for l in range(6):
            kb.op('dve', lambda e, l=l: e.tensor_scalar(out=cf[:, :, 1 + l], in0=mu_t[:, :], scalar1=lan_t[:, l:l + 1], scalar2=None, op0=ALU.mult), reads=[mub_, lanb_], writes=[cfb])
        wr = Ring(kb, 'winm', [128, 640], F32, 4, es=pm)
        so = Ring(kb, 'smo', [128, 512], F32, 4, es=pm)
        for (t0, T) in TILES:
            isctx = t0 < LC
            seg_lo, seg_hi = (0, LC) if isctx else (LC, LT)
            halo = 1 if isctx else 64
            lo = max(t0 - halo, seg_lo)
            hi = min(t0 + T + halo, seg_hi)
            for bi, nm in enumerate(BLm):
                sz = FM_BLOCKS[nm][1]
                w_, wb_ = wr.next()
                if lo > t0 - halo or hi < t0 + T + halo:
                    kb.op('pool', lambda e, w_=w_: e.memset(w_[:], 0.0), writes=[wb_])
                kb.op('sp', lambda e, w_=w_, nm=nm, sz=sz, lo=lo, hi=hi, t0=t0: e.dma_start(out=w_[0:sz, 64 + lo - t0:64 + hi - t0], in_=U[nm][:, lo:hi]), writes=[wb_], dma=True)
                o_, ob_ = so.next()
                kb.op('dve', lambda e, o_=o_, w_=w_, sz=sz, bi=bi, T=T: e.tensor_scalar(out=o_[0:sz, 0:T], in0=w_[0:sz, 64:64 + T], scalar1=cf[0:sz, bi, 0:1], scalar2=None, op0=ALU.mult),
                      reads=[wb_, cfb], writes=[ob_])
                terms = [(5, -1, None), (6, +1, None)] if isctx else [(1, -1, 'L'), (2, +1, 'R'), (3, -64, None), (4, +64, None)]
                for (ci, off, kind) in terms:
                    def f(e, o_=o_, w_=w_, sz=sz, bi=bi, ci=ci, off=off, kind=kind, T=T):
                        if kind is None:
                            oo = o_[0:sz, 0:T]
                            ii = w_[0:sz, 64 + off:64 + T + off]
                        else:
                            ov = o_[0:sz, 0:T].rearrange("p (r c) -> p r c", c=64)
                            iv = w_[0:sz, 64:64 + T].rearrange("p (r c) -> p r c", c=64)
                            if kind == 'L':
                                oo, ii = ov[:, :, 1:64], iv[:, :, 0:63]
                            else:
                                oo, ii = ov[:, :, 0:63], iv[:, :, 1:64]
                        return e.scalar_tensor_tensor(out=oo, in0=ii, scalar=cf[0:sz, bi, ci:ci + 1], in1=oo, op0=ALU.mult, op1=ALU.add)
                    kb.op('dve', f, reads=[wb_, cfb, ob_], writes=[ob_])
                kb.op('pool', lambda e, o_=o_, nm=nm, sz=sz, t0=t0, T=T: e.dma_start(out=S[nm][:, t0:t0 + T], in_=o_[0:sz, 0:T]), reads=[ob_], dma=True)
        kb.barrier(bar[:])
    if stop_after == 'A':
        return
    with ExitStack() as pb:
        P = {k_: IN['p_' + k_] for k_ in RW_PARAMS}
        YR = YD
        if stop_after != 'skipB':
            phase_rwkv(kb, nc, pb, S, GT, YR, P, ident_f, identfb, ident_bf, identb)
        kb.barrier(bar[:])
    if DBG.get('skip_fnet'):
        return
    PF = {k_: IN['f_' + k_] for k_ in FN_PARAMS}
    YF = YD
    ZS = dram_tmp(nc, 'ZS', [2, 128, 64, 128], F32, False)
    phase_fnet(kb, nc, HT, GT, YF, ZS, PF, bar)
    return


DUMPS = {}


def dump(kb, nc, name, ap, shape, bufs, dt=F32):
    if name in DUMPS:
        return
    t = nc.dram_tensor('dbg_' + name, list(shape), dt, kind="ExternalOutput").ap()
    DUMPS[name] = t
    kb.op('sp', lambda e: e.dma_start(out=t, in_=ap), reads=bufs, dma=True, final=True)

CHUNK_ORDER = {0: list(range(NCH)), 1: [1, 0] + list(range(NCH - 1, 1, -1))}


def phase_rwkv(kb, nc, es, U, GT, YR, P, ident_f, identfb, ident_bf, identb):
    def sbt(name, shape, dt=F32):
        return kb.sb(name, shape, dt, es)

    def const(name, shape, src, dt=F32):
        t = sbt(name, shape, dt)
        b, = load_consts(kb, [(t, src)])
        return t, b

    mu, mub = const('mu', [128, 8], P['mu'])
    lanem, lanemb = const('lanem', [128, 6], P['lanem'])
    kkp, kkpb = const('kkp', [128, 2], P['k_k'])
    kap, kapb = const('kap', [128, 2], P['k_a'])
    rkp, rkpb = const('rkp', [128, 2], P['r_k'])
    a0p, a0pb = const('a0p', [128, 2, 2], P['a0'])
    w2s, w2sb = const('w2s', [128, 192], P['w2'])
    a2s, a2sb = const('a2s', [128, 192], P['a2'])
    w0r, w0rb = const('w0r', [1, 2, 192], P['w0'])
    msi, msib = const('msi', [128, 2, 256], P['maskSI'])
    mst, mstb = const('mst', [128, 2, 3, 128], P['maskST'])
    msi2, msi2b = const('msi2', [128, 2, 256], P['maskSI2'])
    tri, trib = const('tri', [128, 2, 256], P['tri'])
    ehd, ehdb = const('ehd', [128, 2, 4], P['ehead'])
    obd, obdb = const('obd', [128, 128], P['ones_bd'])
    lng, lngb = const('lng', [128, 192], P['lnx_g'])
    lnb, lnbb = const('lnb', [128, 192], P['lnx_b'])
    ones1 = sbt('ones1', [1, 128])
    ones1b = Buf()
    kb.op('dve', lambda e: e.memset(ones1[:], 1.0), writes=[ones1b])
    coef = sbt('coef', [128, 8, 7])
    coefb = Buf(strict=True)
    kb.op('dve', lambda e: e.tensor_scalar(out=coef[:, :, 0], in0=mu[:, :], scalar1=-1.0, scalar2=1.0, op0=ALU.mult, op1=ALU.add),
          reads=[mub], writes=[coefb])
    for l in range(6):
        kb.op('dve', lambda e, l=l: e.tensor_scalar(out=coef[:, :, 1 + l], in0=mu[:, :], scalar1=lanem[:, l:l + 1], scalar2=None, op0=ALU.mult),
              reads=[mub, lanemb], writes=[coefb])
    omka = sbt('omka', [128, 2])
    omkab = Buf(strict=True)
    kb.op('dve', lambda e: e.tensor_scalar(out=omka[:], in0=kap[:], scalar1=-1.0, scalar2=1.0, op0=ALU.mult, op1=ALU.add), reads=[kapb], writes=[omkab])
    gneps = sbt('gneps', [128, 1])
    gnepsb = Buf()
    kb.op('dve', lambda e: e.memset(gneps[:], GN_EPS), writes=[gnepsb])

    yacc = sbt('yacc', [128, NCH, 192])
    yaccb = [Buf() for _ in range(NCH)]
    kb.op('pool', lambda e: e.memset(yacc[:], 0.0), writes=yaccb)
    ST = sbt('ST', [128, 2, 2, 64])
    STbf = sbt('STbf', [128, 2, 2, 64], BF16)
    STb = [[Buf() for _ in range(3)] for _ in range(2)]

    CR, HR = {}, {}
    for d_ in range(2):
        n_ = 'd%d' % d_
        CR[d_] = dict(
            winr=None, smr=Ring(kb, 'smix' + n_, [128, 8, 128], F32, 2, es=es),
            t32=Ring(kb, 't32' + n_, [128, 256], F32, 9, es=es), kkr=Ring(kb, 'kkt' + n_, [128, 2, 128], F32, 1, es=es),
            vtr=Ring(kb, 'vt32' + n_, [128, 192], F32, 2, es=es), vtbr=Ring(kb, 'vtbf' + n_, [128, 192], BF16, 2, es=es),
            sgr=Ring(kb, 'sg' + n_, [128, 192], F32, 1, es=es), e1r=Ring(kb, 'e1' + n_, [128, 2, 256], F32, 2, es=es),
            e2r=Ring(kb, 'e2' + n_, [128, 2, 128], F32, 1, es=es), arr=Ring(kb, 'ar' + n_, [128, 2, 256], BF16, 2, es=es),
            btr=Ring(kb, 'bt' + n_, [128, 2, 128], BF16, 2, es=es), ktr=Ring(kb, 'kt' + n_, [128, 2, 128], BF16, 2, es=es),
            bktr=Ring(kb, 'bkt' + n_, [128, 2, 192], BF16, 2, es=es), bcr=Ring(kb, 'bc' + n_, [128, 4], F32, 2, es=es, strict=True))
        for h_ in range(3):
            n2 = 'd%dh%d' % (d_, h_)
            HR[(d_, h_)] = dict(
                mm1r=Ring(kb, 'mm1' + n2, [128, 256], BF16, 2, es=es), mm2r=Ring(kb, 'mm2' + n2, [128, 256], BF16, 2, es=es),
                xpr=Ring(kb, 'xp' + n2, [128, 256], BF16, 4, es=es), xtr=Ring(kb, 'xt' + n2, [128, 128], BF16, 4, es=es),
                ivr=Ring(kb, 'iv' + n2, [128, 128], BF16, 8, es=es), tmr=Ring(kb, 'tm' + n2, [128, 128], BF16, 2, es=es),
                w1r=Ring(kb, 'w1' + n2, [128, 64], BF16, 2, es=es), utr=Ring(kb, 'ut' + n2, [128, 64], BF16, 2, es=es),
                tsr=Ring(kb, 'ts' + n2, [128, 64], F32, 3, es=es))
    jkr = Ring(kb, 'jk', [128, 64], F32, 2, es=es)
    gtr = Ring(kb, 'gt', [128, 192], F32, 2, es=es)
    fnr = Ring(kb, 'fn', [128, 192], F32, 2, es=es)
    str_ = Ring(kb, 'stt', [128, 8], F32, 6, es=es, strict=True)
    pprep = Ring(kb, 'pprep', [128, 512], F32, 2, psum=True, es=es)
    ptrb = kb.ps('ptrb', [128, 1024], BF16, es)
    ptrbufs = [Buf(excl=True)] * 4
    ptrc = [0]
    pgram = Ring(kb, 'pgram', [128, 512], F32, 1, psum=True, es=es)
    pinv = Ring(kb, 'pinv', [128, 512], F32, 2, psum=True, es=es)
    pseq = kb.ps('pseq', [128, 512], F32, es)
    pseq2 = kb.ps('pseq2', [128, 512], F32, es)
    pseqb = [Buf(excl=True)] * 4 + [Buf(excl=True)] * 4
    pseqt = [pseq] * 4 + [pseq2] * 4
    pseqc = [0]

    def pseq_next():
        i = pseqc[0] % 8
        pseqc[0] += 1
        return pseqt[i][:, (i % 4) * 64:(i % 4 + 1) * 64], pseqb[i], i

    def ptr_next():
        i = ptrc[0] % 4
        ptrc[0] += 1
        return i, ptrbufs[i]

    blkname = [('r01', 'k01', 'v01'), ('r2', 'k2', 'v2')]
    BL = ['r01', 'r2', 'k01', 'k2', 'v01', 'v2', 'wl', 'al']
    BI = {n: i for i, n in enumerate(BL)}
    BSZ = {n: FM_BLOCKS[n][1] for n in BL}
    alt = [0]

    def ew(fn, reads, writes):
        alt[0] += 1
        r_ = _Rec()
        fn(r_)
        stt_ = r_.call[0] == 'scalar_tensor_tensor'
        return kb.op('dve', fn, reads=reads, writes=writes)

    done = {}
    inflight = {}
    SMB = {}

    def chunk_body(d, c):
        winr, smr, t32, kkr, vtr, vtbr, sgr, e1r, e2r, arr, btr, ktr, bktr, bcr = (CR[d][k_] for k_ in (
            'winr', 'smr', 't32', 'kkr', 'vtr', 'vtbr', 'sgr', 'e1r', 'e2r', 'arr', 'btr', 'ktr', 'bktr', 'bcr'))
        isctx = c < 2
        t0 = c * 128
        seg_lo, seg_hi = (0, LC) if isctx else (LC, LT)
        halo = 1 if isctx else 64
        lo = max(t0 - halo, seg_lo)
        hi = min(t0 + 128 + halo, seg_hi)
        sm, smb0 = smr.next()
        smb = SMB.setdefault(id(smb0), [smb0] + [Buf() for _ in range(7)])
        for nm in BL:
            sz = BSZ[nm]
            kb.op('sp', lambda e, sm=sm, nm=nm, sz=sz, t0=t0: e.dma_start(out=sm[0:sz, BI[nm], :], in_=U[nm][:, t0:t0 + 128]), writes=[smb[BI[nm]]], dma=True)
        if DBG.get('stage', 99) < 2:
            return

        def S(nm):
            return sm[0:BSZ[nm], BI[nm], :]

        yield
        kkt, kktb = kkr.next()
        for bj, (rn, kn, vn) in enumerate(blkname if not DBG.get('skip_kk') else []):
            sz = BSZ[kn]
            q, qb = t32.next()
            kb.op('dve', lambda e, q=q, kn=kn, sz=sz, bj=bj: e.tensor_scalar(out=q[0:sz, 0:128], in0=S(kn), scalar1=kkp[0:sz, bj:bj + 1], scalar2=None, op0=ALU.mult),
                  reads=[smb, kkpb], writes=[qb])
            kb.op('pool', lambda e, q=q, sz=sz: e.tensor_tensor(out=q[0:sz, 128:256], in0=q[0:sz, 0:128], in1=q[0:sz, 0:128], op=ALU.mult), reads=[qb], writes=[qb])
            pp, ppb = pprep.next()
            kb.op('pe', lambda e, pp=pp, q=q, sz=sz: e.matmul(pp[0:sz, 0:128], lhsT=obd[0:sz, 0:sz], rhs=q[0:sz, 128:256], start=True, stop=True),
                  reads=[qb, obdb], writes=[ppb])
            nr, nrb = t32.next()
            kb.op('act', lambda e, nr=nr, pp=pp, sz=sz: e.activation(out=nr[0:sz, 0:128], in_=pp[0:sz, 0:128], func=AF.Sqrt), reads=[ppb], writes=[nrb])
            kb.op('dve', lambda e, nr=nr, sz=sz: e.tensor_scalar(out=nr[0:sz, 0:128], in0=nr[0:sz, 0:128], scalar1=1e-12, scalar2=None, op0=ALU.max), reads=[nrb], writes=[nrb])
            kb.op('dve', lambda e, nr=nr, sz=sz: e.reciprocal(out=nr[0:sz, 128:256], in_=nr[0:sz, 0:128]), reads=[nrb], writes=[nrb])
            kb.op('dve', lambda e, nr=nr, q=q, sz=sz, bj=bj, kkt=kkt: e.tensor_tensor(out=kkt[0:sz, bj, :], in0=q[0:sz, 0:128], in1=nr[0:sz, 128:256], op=ALU.mult),
                  reads=[nrb, qb], writes=[kktb])
        vt, vtb = vtr.next()
        vtbf, vtbfb = vtbr.next()
        pp, ppb = pprep.next()
        for bj, (rn, kn, vn) in enumerate(blkname if not DBG.get('skip_vt') else []):
            sz = BSZ[vn]
            kb.op('pe', lambda e, pp=pp, vn=vn, sz=sz, bj=bj: e.transpose(out=pp[:, bj * 128:bj * 128 + sz], in_=S(vn), identity=ident_f[0:sz, 0:sz]),
                  reads=[smb, identfb], writes=[ppb])
        kb.op('act', lambda e, pp=pp, vt=vt: e.activation(out=vt[:, :], in_=pp[:, 0:192], func=AF.Copy), reads=[ppb], writes=[vtb])
        kb.op('pool', lambda e, vt=vt, vtbf=vtbf: e.tensor_copy(out=vtbf[:, :], in_=vt[:, :]), reads=[vtb], writes=[vtbfb])

        if DBG.get('stage', 99) < 3:
            return
        yield
        th, thb = t32.next()
        kb.op('act', lambda e, th=th: e.activation(out=th[d * 64:(d + 1) * 64, 0:128], in_=sm[d * 64:(d + 1) * 64, BI['wl'], :], func=AF.Tanh),
              reads=[smb], writes=[thb])
        pp, ppb = pprep.next()
        kb.op('pe', lambda e, pp=pp, th=th: e.matmul(pp[:, 0:192], lhsT=th[d * 64:(d + 1) * 64, 0:128], rhs=w2s[d * 64:(d + 1) * 64, :], start=True, stop=False),
              reads=[thb, w2sb], writes=[ppb])
        kb.op('pe', lambda e, pp=pp: e.matmul(pp[:, 0:192], lhsT=ones1[0:1, :], rhs=w0r[0:1, d, :], start=False, stop=True),
              reads=[ones1b, w0rb], writes=[ppb])
        sg, sgb = sgr.next()
        kb.op('act', lambda e, sg=sg, pp=pp: e.activation(out=sg[:, :], in_=pp[:, 0:192], func=AF.Sigmoid), reads=[ppb], writes=[sgb])
        e1, e1b = e1r.next()
        e2, e2b = e2r.next()
        for bj in range(2):
            sz = 128 if bj == 0 else 64
            pp, ppb = pprep.next()
            kb.op('pe', lambda e, pp=pp, sg=sg, bj=bj, sz=sz: e.matmul(pp[0:sz, 0:256], lhsT=sg[:, bj * 128:bj * 128 + sz], rhs=tri[:, d, :], start=True, stop=True),
                  reads=[sgb, trib], writes=[ppb])
            kb.op('act', lambda e, pp=pp, e1=e1, bj=bj, sz=sz: e.activation(out=e1[0:sz, bj, :], in_=pp[0:sz, 0:256], func=AF.Exp), reads=[ppb], writes=[e1b])
            kb.op('act', lambda e, pp=pp, e2=e2, bj=bj, sz=sz: e.activation(out=e2[0:sz, bj, :], in_=pp[0:sz, 0:128], func=AF.Exp, scale=-1.0), reads=[ppb], writes=[e2b])
        if DBG.get('stage', 99) < 4:
            return
        yield
        ar, arb = arr.next()
        bt, btb = btr.next()
        kt, ktb = ktr.next()
        bc, bcb = bcr.next()
        pbon, pbonb, _ = pseq_next()
        for bj, (rn, kn, vn) in enumerate(blkname):
            sz = BSZ[kn]
            co = bj * 128
            pp, ppb = pprep.next()
            kb.op('pe', lambda e, pp=pp, sz=sz, co=co: e.matmul(pp[0:sz, 0:128], lhsT=a2s[d * 64:(d + 1) * 64, co:co + sz], rhs=sm[d * 64:(d + 1) * 64, BI['al'], :], start=True, stop=True),
                  reads=[a2sb, smb], writes=[ppb])
            av, avb = t32.next()
            kb.op('act', lambda e, av=av, pp=pp, sz=sz, bj=bj: e.activation(out=av[0:sz, 0:128], in_=pp[0:sz, 0:128], func=AF.Sigmoid, bias=a0p[0:sz, d, bj:bj + 1]),
                  reads=[ppb, a0pb], writes=[avb])
            kv, kvb = t32.next()
            ew(lambda e, kv=kv, av=av, sz=sz, bj=bj: e.tensor_scalar(out=kv[0:sz, 0:128], in0=av[0:sz, 0:128], scalar1=kap[0:sz, bj:bj + 1], scalar2=omka[0:sz, bj:bj + 1], op0=ALU.mult, op1=ALU.add),
               [avb, kapb, omkab], [kvb])
            ew(lambda e, kv=kv, kn=kn, sz=sz: e.tensor_tensor(out=kv[0:sz, 0:128], in0=kv[0:sz, 0:128], in1=S(kn), op=ALU.mult), [kvb, smb], [kvb])
            ew(lambda e, kv=kv, kt=kt, e2=e2, sz=sz, bj=bj: e.tensor_tensor(out=kt[0:sz, bj, :], in0=kv[0:sz, 0:128], in1=e2[0:sz, bj, :], op=ALU.mult), [kvb, e2b], [ktb])
            ew(lambda e, av=av, e2=e2, sz=sz, bj=bj: e.tensor_tensor(out=av[0:sz, 128:256], in0=av[0:sz, 0:128], in1=e2[0:sz, bj, :], op=ALU.mult), [avb, e2b], [avb])
            ew(lambda e, av=av, bt=bt, kkt=kkt, sz=sz, bj=bj: e.tensor_tensor(out=bt[0:sz, bj, :], in0=av[0:sz, 128:256], in1=kkt[0:sz, bj, :], op=ALU.mult), [avb, kktb], [btb])
            ew(lambda e, ar=ar, kkt=kkt, e1=e1, sz=sz, bj=bj: e.scalar_tensor_tensor(out=ar[0:sz, bj, 0:128], in0=kkt[0:sz, bj, :], scalar=-1.0, in1=e1[0:sz, bj, 128:256], op0=ALU.mult, op1=ALU.mult),
               [kktb, e1b], [arb])
            ew(lambda e, ar=ar, rn=rn, e1=e1, sz=sz, bj=bj: e.tensor_tensor(out=ar[0:sz, bj, 128:256], in0=S(rn), in1=e1[0:sz, bj, 0:128], op=ALU.mult), [smb, e1b], [arb])
            ew(lambda e, kv=kv, rn=rn, sz=sz, bj=bj: e.scalar_tensor_tensor(out=kv[0:sz, 128:256], in0=S(rn), scalar=rkp[0:sz, bj:bj + 1], in1=kv[0:sz, 0:128], op0=ALU.mult, op1=ALU.mult),
               [smb, rkpb, kvb], [kvb])
            kb.op('pe', lambda e, kv=kv, sz=sz, bj=bj: e.matmul(pbon[:, 0:4], lhsT=kv[0:sz, 128:256], rhs=ehd[0:sz, bj, :], start=(bj == 0), stop=(bj == 1)),
                  reads=[kvb, ehdb], writes=[pbonb])
        kb.op('dve', lambda e, bc=bc: e.tensor_copy(out=bc[:, :], in_=pbon[:, 0:4]), reads=[pbonb], writes=[bcb])
        yield
        bkt, bktb = bktr.next()
        for wi, (src, srcb) in enumerate([(bt, btb), (kt, ktb)]):
            pi, pib = ptr_next()
            for bj in range(2):
                sz = 128 if bj == 0 else 64
                kb.op('pe', lambda e, src=src, pi=pi, bj=bj, sz=sz: e.transpose(out=ptrb[:, pi * 256 + bj * 128:pi * 256 + bj * 128 + sz], in_=src[0:sz, bj, :], identity=ident_bf[0:sz, 0:sz]),
                      reads=[srcb, identb], writes=[pib])
            kb.op('act' if wi == 0 else 'dve',
                  (lambda e, pi=pi, bkt=bkt, wi=wi: e.activation(out=bkt[:, wi, :], in_=ptrb[:, pi * 256:pi * 256 + 192], func=AF.Copy)) if wi == 0 else
                  (lambda e, pi=pi, bkt=bkt, wi=wi: e.tensor_copy(out=bkt[:, wi, :], in_=ptrb[:, pi * 256:pi * 256 + 192])),
                  reads=[pib], writes=[bktb])
        tcol = 127 if d == 0 else 0

        if DBG.get('stage', 99) < 5:
            return
        first = done.get(c, 0) == 0
        assert c not in inflight
        inflight[c] = d

        def head_body(h):
            mm1r, mm2r, xpr, xtr, ivr, tmr, w1r, utr, tsr = (HR[(d, h)][k_] for k_ in ('mm1r', 'mm2r', 'xpr', 'xtr', 'ivr', 'tmr', 'w1r', 'utr', 'tsr'))
            bj = h // 2
            base = (h % 2) * 64
            ch0 = 64 * h
            hp = slice(base, base + 64)
            yield
            pg1, pg1b = pgram.next()
            kb.op('pe', lambda e, pg1=pg1, hp=hp, bj=bj: e.matmul(pg1[:, 0:256], lhsT=bt[hp, bj, :], rhs=ar[hp, bj, :], start=True, stop=True), reads=[btb, arb], writes=[pg1b])
            mm1, mm1b = mm1r.next()
            kb.op('dve', lambda e, mm1=mm1, pg1=pg1: e.tensor_tensor(out=mm1[:, :], in0=pg1[:, 0:256], in1=msi[:, d, :], op=ALU.mult), reads=[pg1b, msib], writes=[mm1b])
            pg2, pg2b = pgram.next()
            kb.op('pe', lambda e, pg2=pg2, hp=hp, bj=bj: e.matmul(pg2[:, 0:256], lhsT=kt[hp, bj, :], rhs=ar[hp, bj, :], start=True, stop=True), reads=[ktb, arb], writes=[pg2b])
            kb.op('pe', lambda e, pg2=pg2, hp=hp, bj=bj: e.matmul(pg2[:, 256:384], lhsT=ar[hp, bj, 0:128], rhs=bt[hp, bj, :], start=True, stop=True), reads=[btb, arb], writes=[pg2b])
            mm2, mm2b = mm2r.next()
            kb.op('dve', lambda e, mm2=mm2, pg2=pg2: e.tensor_tensor(out=mm2[:, :], in0=pg2[:, 0:256], in1=msi2[:, d, :], op=ALU.mult), reads=[pg2b, msi2b], writes=[mm2b])
            xt, xtb = xtr.next()
            kb.op('dve', lambda e, xt=xt, pg2=pg2: e.tensor_tensor(out=xt[:, :], in0=pg2[:, 256:384], in1=mst[:, d, 0, :], op=ALU.mult), reads=[pg2b, mstb], writes=[xtb])
            e1t, e1tb = ivr.next()
            kb.op('dve', lambda e, e1t=e1t, pg2=pg2: e.tensor_tensor(out=e1t[:, :], in0=pg2[:, 256:384], in1=mst[:, d, 1, :], op=ALU.mult), reads=[pg2b, mstb], writes=[e1tb])
            e2t, e2tb = ivr.next()
            kb.op('dve', lambda e, e2t=e2t, pg2=pg2: e.tensor_tensor(out=e2t[:, :], in0=pg2[:, 256:384], in1=mst[:, d, 2, :], op=ALU.mult), reads=[pg2b, mstb], writes=[e2tb])
            if DBG.get('stage', 99) < 6:
                return
            yield
            xp, xpb = xpr.next()
            kb.op('dve', lambda e, xp=xp, mm1=mm1: e.tensor_tensor(out=xp[:, 128:256], in0=mm1[:, 0:128], in1=ident_bf[:, :], op=ALU.add), reads=[mm1b, identb], writes=[xpb])
            pv, pvb = pinv.next()
            kb.op('pe', lambda e, pv=pv, xt=xt, mm1=mm1: e.matmul(pv[:, 0:128], lhsT=xt[:, :], rhs=mm1[:, 0:128], start=True, stop=True), reads=[xtb, mm1b], writes=[pvb])
            kb.op('pe', lambda e, pv=pv, xt=xt, mm1=mm1: e.matmul(pv[:, 256:384], lhsT=mm1[:, 0:128], rhs=xt[:, :], start=True, stop=True), reads=[xtb, mm1b], writes=[pvb])
            xt2, xt2b = xtr.next()
            kb.op('act', lambda e, xp=xp, pv=pv: e.activation(out=xp[:, 0:128], in_=pv[:, 0:128], func=AF.Copy), reads=[pvb], writes=[xpb])
            kb.op('dve', lambda e, xt2=xt2, pv=pv: e.tensor_copy(out=xt2[:, :], in_=pv[:, 256:384]), reads=[pvb], writes=[xt2b])
            curxp, curxpb, curxt, curxtb = xp, xpb, xt2, xt2b
            for lev in range(1, 4):
                pv, pvb = pinv.next()
                kb.op('pe', lambda e, pv=pv, cx=curxp, ct=curxt: e.matmul(pv[:, 0:256], lhsT=ct[:, :], rhs=cx[:, 0:256], start=True, stop=True), reads=[curxpb, curxtb], writes=[pvb])
                kb.op('pe', lambda e, pv=pv, cx=curxp, ct=curxt: e.matmul(pv[:, 256:384], lhsT=cx[:, 0:128], rhs=ct[:, :], start=True, stop=True), reads=[curxpb, curxtb], writes=[pvb])
                nxp, nxpb = xpr.next()
                nxt, nxtb = xtr.next()
                kb.op('act', lambda e, nxp=nxp, pv=pv: e.activation(out=nxp[:, 0:128], in_=pv[:, 0:128], func=AF.Copy), reads=[pvb], writes=[nxpb])
                kb.op('dve', lambda e, nxp=nxp, pv=pv, cx=curxp: e.tensor_tensor(out=nxp[:, 128:256], in0=pv[:, 128:256], in1=cx[:, 128:256], op=ALU.add), reads=[pvb, curxpb], writes=[nxpb])
                kb.op('act', lambda e, nxt=nxt, pv=pv: e.activation(out=nxt[:, :], in_=pv[:, 256:384], func=AF.Copy), reads=[pvb], writes=[nxtb])
                curxp, curxpb, curxt, curxtb = nxp, nxpb, nxt, nxtb
                yield
            pv, pvb = pinv.next()
            kb.op('pe', lambda e, pv=pv, cx=curxp, ct=curxt: e.matmul(pv[:, 0:128], lhsT=ct[:, :], rhs=cx[:, 128:256], start=True, stop=True), reads=[curxpb, curxtb], writes=[pvb])
            t32m, t32mb = ivr.next()
            kb.op('dve', lambda e, t32m=t32m, pv=pv, cx=curxp: e.tensor_tensor(out=t32m[:, :], in0=pv[:, 0:128], in1=cx[:, 128:256], op=ALU.add), reads=[pvb, curxpb], writes=[t32mb])
            yield
            pi, pib = ptr_next()
            kb.op('pe', lambda e, pi=pi, t32m=t32m: e.transpose(out=ptrb[:, pi * 256:pi * 256 + 128], in_=t32m[:, :], identity=ident_bf[:, :]), reads=[t32mb, identb], writes=[pib])
            t32t, t32tb = ivr.next()
            kb.op('act', lambda e, pi=pi, t32t=t32t: e.activation(out=t32t[:, :], in_=ptrb[:, pi * 256:pi * 256 + 128], func=AF.Copy), reads=[pib], writes=[t32tb])
            yield
            pv, pvb = pinv.next()
            kb.op('pe', lambda e, pv=pv, e1t=e1t, t32m=t32m: e.matmul(pv[:, 0:128], lhsT=e1t[:, :], rhs=t32m[:, :], start=True, stop=True), reads=[e1tb, t32mb], writes=[pvb])
            z1, z1b = ivr.next()
            kb.op('act', lambda e, z1=z1, pv=pv: e.activation(out=z1[:, :], in_=pv[:, 0:128], func=AF.Copy), reads=[pvb], writes=[z1b])
            pv, pvb = pinv.next()
            kb.op('pe', lambda e, pv=pv, t32t=t32t, z1=z1: e.matmul(pv[:, 0:128], lhsT=t32t[:, :], rhs=z1[:, :], start=True, stop=True), reads=[t32tb, z1b], writes=[pvb])
            kb.op('pe', lambda e, pv=pv, t32t=t32t, z1=z1: e.matmul(pv[:, 256:384], lhsT=z1[:, :], rhs=t32t[:, :], start=True, stop=True), reads=[t32tb, z1b], writes=[pvb])
            t64, t64b = ivr.next()
            t64t, t64tb = ivr.next()
            kb.op('dve', lambda e, t64=t64, pv=pv, t32m=t32m: e.tensor_tensor(out=t64[:, :], in0=pv[:, 0:128], in1=t32m[:, :], op=ALU.add), reads=[pvb, t32mb], writes=[t64b])
            kb.op('dve', lambda e, t64t=t64t, pv=pv, t32t=t32t: e.tensor_tensor(out=t64t[:, :], in0=pv[:, 256:384], in1=t32t[:, :], op=ALU.add), reads=[pvb, t32tb], writes=[t64tb])
            yield
            pv, pvb = pinv.next()
            kb.op('pe', lambda e, pv=pv, e2t=e2t, t64=t64: e.matmul(pv[:, 0:128], lhsT=e2t[:, :], rhs=t64[:, :], start=True, stop=True), reads=[e2tb, t64b], writes=[pvb])
            z2, z2b = ivr.next()
            kb.op('act', lambda e, z2=z2, pv=pv: e.activation(out=z2[:, :], in_=pv[:, 0:128], func=AF.Copy), reads=[pvb], writes=[z2b])
            pv, pvb = pinv.next()
            kb.op('pe', lambda e, pv=pv, t64t=t64t, z2=z2: e.matmul(pv[:, 0:128], lhsT=t64t[:, :], rhs=z2[:, :], start=True, stop=True), reads=[t64tb, z2b], writes=[pvb])
            tm, tmb = tmr.next()
            kb.op('dve', lambda e, tm=tm, pv=pv, t64=t64: e.tensor_tensor(out=tm[:, :], in0=pv[:, 0:128], in1=t64[:, :], op=ALU.add), reads=[pvb, t64b], writes=[tmb])
            if DBG.get('dump') == (d, c) and h == 0:
                dump(kb, nc, 'mm1', mm1[:, :], [128, 256], [mm1b], BF16)
                dump(kb, nc, 'mm2', mm2[:, :], [128, 256], [mm2b], BF16)
                dump(kb, nc, 'tm', tm[:, :], [128, 128], [tmb], BF16)
            yield
            stb = STb[d][h]
            p1, p1b, _ = pseq_next()
            kb.op('pe', lambda e, p1=p1, hp=hp, bj=bj: e.matmul(p1, lhsT=ar[hp, bj, 0:128], rhs=STbf[hp, d, bj, :], start=True, stop=False), reads=[arb, stb], writes=[p1b])
            kb.op('pe', lambda e, p1=p1, mm2=mm2, ch0=ch0: e.matmul(p1, lhsT=mm2[:, 0:128], rhs=vtbf[:, ch0:ch0 + 64], start=False, stop=True), reads=[mm2b, vtbfb], writes=[p1b])
            w1, w1b = w1r.next()
            kb.op('act', lambda e, w1=w1, p1=p1: e.activation(out=w1[:, :], in_=p1, func=AF.Copy), reads=[p1b], writes=[w1b])
            p2, p2b, _ = pseq_next()
            kb.op('pe', lambda e, p2=p2, tm=tm, w1=w1: e.matmul(p2, lhsT=tm[:, :], rhs=w1[:, :], start=True, stop=True), reads=[tmb, w1b], writes=[p2b])
            ut, utb = utr.next()
            kb.op('act', lambda e, ut=ut, p2=p2: e.activation(out=ut[:, :], in_=p2, func=AF.Copy), reads=[p2b], writes=[utb])
            p3, p3b, _ = pseq_next()
            kb.op('pe', lambda e, p3=p3, hp=hp, bj=bj: e.matmul(p3, lhsT=ar[hp, bj, 128:256], rhs=STbf[hp, d, bj, :], start=True, stop=False), reads=[arb, stb], writes=[p3b])
            kb.op('pe', lambda e, p3=p3, mm1=mm1, ut=ut: e.matmul(p3, lhsT=mm1[:, 128:256], rhs=ut[:, :], start=False, stop=False), reads=[mm1b, utb], writes=[p3b])
            kb.op('pe', lambda e, p3=p3, mm2=mm2, ch0=ch0: e.matmul(p3, lhsT=mm2[:, 128:256], rhs=vtbf[:, ch0:ch0 + 64], start=False, stop=True), reads=[mm2b, vtbfb], writes=[p3b])
            if DBG.get('dump') == (d, c) and h == 0:
                dump(kb, nc, 'w1', w1[:, :], [128, 64], [w1b], BF16)
                dump(kb, nc, 'ut', ut[:, :], [128, 64], [utb], BF16)
            ts, tsb = tsr.next()
            kb.op('dve', lambda e, ts=ts, p3=p3, ch0=ch0, h=h: e.scalar_tensor_tensor(out=ts[:, :], in0=vt[:, ch0:ch0 + 64], scalar=bc[:, h:h + 1], in1=p3, op0=ALU.mult, op1=ALU.add),
                  reads=[vtb, bcb, p3b], writes=[tsb])
            if first:
                kb.op('pool', lambda e, ts=ts, ch0=ch0, c=c: e.tensor_copy(out=yacc[:, c, ch0:ch0 + 64], in_=ts[:, :]), reads=[tsb], writes=[yaccb[c]])
            else:
                kb.op('pool', lambda e, ts=ts, ch0=ch0, c=c: e.tensor_tensor(out=yacc[:, c, ch0:ch0 + 64], in0=yacc[:, c, ch0:ch0 + 64], in1=ts[:, :], op=ALU.add), reads=[tsb, yaccb[c]], writes=[yaccb[c]])
            yield
            p4full, p4b, i4 = pseq_next()
            p4 = pseqt[i4][hp, (i4 % 4) * 64:(i4 % 4 + 1) * 64]
            kb.op('pe', lambda e, p4=p4, bkt=bkt, ut=ut, ch0=ch0: e.matmul(p4, lhsT=bkt[:, 0, ch0:ch0 + 64], rhs=ut[:, :], start=True, stop=False), reads=[bktb, utb], writes=[p4b])
            kb.op('pe', lambda e, p4=p4, bkt=bkt, ch0=ch0: e.matmul(p4, lhsT=bkt[:, 1, ch0:ch0 + 64], rhs=vtbf[:, ch0:ch0 + 64], start=False, stop=True), reads=[bktb, vtbfb], writes=[p4b])
            tq, tqb = tsr.next()
            kb.op('dve', lambda e, tq=tq, p4=p4, hp=hp, bj=bj: e.tensor_tensor(out=tq[hp, :], in0=p4, in1=ST[hp, d, bj, :], op=ALU.add), reads=[p4b, stb], writes=[tqb])
            kb.op('dve', lambda e, tq=tq, hp=hp, bj=bj: e.tensor_scalar(out=ST[hp, d, bj, :], in0=tq[hp, :], scalar1=e1[hp, bj, tcol:tcol + 1], scalar2=None, op0=ALU.mult), reads=[tqb, e1b], writes=[stb])
            kb.op('act', lambda e, tq=tq, hp=hp, bj=bj: e.activation(out=STbf[hp, d, bj, :], in_=tq[hp, :], func=AF.Identity, scale=e1[hp, bj, tcol:tcol + 1]), reads=[tqb, e1b], writes=[stb])

        hg = [head_body(h) for h in range(DBG.get('heads', 3))]
        while hg:
            for g_ in list(hg):
                try:
                    next(g_)
                except StopIteration:
                    hg.remove(g_)
            yield
        del inflight[c]
        done[c] = done.get(c, 0) + 1
        if DBG.get('dump') == (d, c):
            dump(kb, nc, 'sm', sm[:, :, :], [128, 8, 128], [smb])
            dump(kb, nc, 'kkt', kkt[:, :, :], [128, 2, 128], [kktb])
            dump(kb, nc, 'vt', vt[:, :], [128, 192], [vtb])
            dump(kb, nc, 'sg', sg[:, :], [128, 192], [sgb])
            dump(kb, nc, 'e1', e1[:, :, :], [128, 2, 256], [e1b])
            dump(kb, nc, 'e2', e2[:, :, :], [128, 2, 128], [e2b])
            dump(kb, nc, 'ar', ar[:, :, :], [128, 2, 256], [arb], BF16)
            dump(kb, nc, 'bt', bt[:, :, :], [128, 2, 128], [btb], BF16)
            dump(kb, nc, 'kt', kt[:, :, :], [128, 2, 128], [ktb], BF16)
            dump(kb, nc, 'bkt', bkt[:, :, :], [128, 2, 192], [bktb], BF16)
            dump(kb, nc, 'bc', bc[:, :], [128, 4], [bcb])
            dump(kb, nc, 'ST', ST[:, d, :, :], [128, 2, 64], STb[d])
            dump(kb, nc, 'yacc', yacc[:, c, :], [128, 192], [yaccb[c]])
        yield
        if done[c] == 2 and DBG.get('stage', 99) >= 8:
            gt, gtb = gtr.next()
            kb.op('sp', lambda e, gt=gt, t0=t0: e.dma_start(out=gt[:, :], in_=GT[t0:t0 + 128, 0:192]), writes=[gtb], dma=True)
            fn, fnb = fnr.next()
            for h in range(3):
                ch0 = 64 * h
                stt, sttb = str_.next()
                jk, jkb = jkr.next()
                kb.op('act', lambda e, stt=stt, jk=jk, c=c, ch0=ch0: e.activation(out=jk[:, :], in_=yacc[:, c, ch0:ch0 + 64], func=AF.Copy, accum_out=stt[:, 0:1]), reads=[yaccb[c]], writes=[sttb, jkb])
                kb.op('act', lambda e, stt=stt, jk=jk, c=c, ch0=ch0: e.activation(out=jk[:, :], in_=yacc[:, c, ch0:ch0 + 64], func=AF.Square, accum_out=stt[:, 1:2]), reads=[yaccb[c]], writes=[sttb, jkb])
                kb.op('dve', lambda e, stt=stt: e.tensor_scalar(out=stt[:, 6:7], in0=stt[:, 0:1], scalar1=1.0 / 64, scalar2=None, op0=ALU.mult), reads=[sttb], writes=[sttb])
                kb.op('dve', lambda e, stt=stt: e.tensor_tensor(out=stt[:, 3:4], in0=stt[:, 6:7], in1=stt[:, 6:7], op=ALU.mult), reads=[sttb], writes=[sttb])
                kb.op('dve', lambda e, stt=stt: e.scalar_tensor_tensor(out=stt[:, 7:8], in0=stt[:, 1:2], scalar=1.0 / 64, in1=stt[:, 3:4], op0=ALU.mult, op1=ALU.subtract), reads=[sttb], writes=[sttb])
                kb.op('act', lambda e, stt=stt: e.activation(out=stt[:, 4:5], in_=stt[:, 7:8], func=AF.Sqrt, bias=gneps[:, 0:1]), reads=[sttb, gnepsb], writes=[sttb])
                kb.op('dve', lambda e, stt=stt: e.reciprocal(out=stt[:, 5:6], in_=stt[:, 4:5]), reads=[sttb], writes=[sttb])
                kb.op('dve', lambda e, stt=stt, fn=fn, c=c, ch0=ch0: e.tensor_scalar(out=fn[:, ch0:ch0 + 64], in0=yacc[:, c, ch0:ch0 + 64], scalar1=stt[:, 6:7], scalar2=stt[:, 5:6], op0=ALU.subtract, op1=ALU.mult),
                      reads=[sttb, yaccb[c]], writes=[fnb])
            if DBG.get('dumpfin') == c:
                dump(kb, nc, 'stt', stt[:, :], [128, 8], [sttb])
                dump(kb, nc, 'fn0', fn[:, :], [128, 192], [fnb])
                dump(kb, nc, 'gt', gt[:, :], [128, 192], [gtb])
                dump(kb, nc, 'yaccf', yacc[:, c, :], [128, 192], [yaccb[c]])
                dump(kb, nc, 'lng', lng[:, :], [128, 192], [lngb])
            kb.op('pool', lambda e, fn=fn: e.tensor_tensor(out=fn[:, :], in0=fn[:, :], in1=lng[:, :], op=ALU.mult), reads=[fnb, lngb], writes=[fnb])
            kb.op('pool', lambda e, fn=fn: e.tensor_tensor(out=fn[:, :], in0=fn[:, :], in1=lnb[:, :], op=ALU.add), reads=[fnb, lnbb], writes=[fnb])
            kb.op('dve', lambda e, fn=fn, gt=gt: e.tensor_tensor(out=fn[:, :], in0=fn[:, :], in1=gt[:, :], op=ALU.mult), reads=[fnb, gtb], writes=[fnb])
            kb.op('pool', lambda e, fn=fn, t0=t0: e.dma_start(out=YR.rows(t0, 0, 192), in_=fn[:, :]), reads=[fnb], dma=True)


    def dir_gen(d):
        for c in (DBG['chunks'][d] if 'chunks' in DBG else CHUNK_ORDER[d]):
            yield from chunk_body(d, c)

    dirs = DBG.get('dirs', [0, 1])
    for d in dirs:
        kb.op('dve', lambda e, d=d: e.memset(ST[:, d, :, :], 0.0), writes=STb[d])
        kb.op('dve', lambda e, d=d: e.memset(STbf[:, d, :, :], 0.0), writes=STb[d])
    gens = [dir_gen(d) for d in dirs]
    if DBG.get('no_interleave'):
        for g_ in gens:
            for _ in g_:
                pass
    else:
        while gens:
            for g_ in list(gens):
                try:
                    next(g_)
                except StopIteration:
                    gens.remove(g_)


FN_PARAMS = {'c128': [128, 128], 'ns128': [128, 128], 'twc': [128, 64], 'tws': [128, 64], 'fl1': [128, 64], 'fl2': [128, 64],
             'c256': [128, 2, 256], 'ns256': [128, 2, 256]}


def phase_fnet(kb, nc, HT, GT, YF, ZS, P, bar):
    sc_l = 1.0 / float(np.sqrt(L * 64.0))
    sc_c = 1.0 / float(np.sqrt(LC * 64.0))
    with ExitStack() as c1:
        def const(name, shape, src):
            t = kb.sb(name, shape, F32, c1)
            b, = load_consts(kb, [(t, src)])
            return t, b
        c128, c128b = const('c128', [128, 128], P['c128'])
        ns128, ns128b = const('ns128', [128, 128], P['ns128'])
        twc, twcb = const('twc', [128, 64], P['twc'])
        tws, twsb = const('tws', [128, 64], P['tws'])
        c256, c256b = const('c256', [128, 2, 256], P['c256'])
        ns256, ns256b = const('ns256', [128, 2, 256], P['ns256'])
        hc = kb.sb('hctx', [128, 2, 128], F32, c1)
        hcb = Buf()
        gc = kb.sb('gctx', [128, 2, 64], F32, c1)
        gcb = Buf()
        for lt in range(2):
            kb.op('sp', lambda e, lt=lt: e.dma_start(out=hc[:, lt, :], in_=HT[lt * 128:(lt + 1) * 128, :]), writes=[hcb], dma=True)
            kb.op('sp', lambda e, lt=lt: e.dma_start(out=gc[:, lt, :], in_=GT[lt * 128:(lt + 1) * 128, 192:256]), writes=[gcb], dma=True)
        pc = Ring(kb, 'pfc', [128, 512], F32, 1, psum=True, es=c1)
        fo = Ring(kb, 'foc', [128, 64], F32, 2, es=c1)
        for lo in range(2):
            ps, psb = pc.next()
            for lt in range(2):
                kb.op('pe', lambda e, ps=ps, lt=lt, lo=lo: e.matmul(ps[:, 0:64], lhsT=c256[:, lt, lo * 128:(lo + 1) * 128], rhs=hc[:, lt, 0:64], start=(lt == 0), stop=False),
                      reads=[c256b, hcb], writes=[psb])
            for lt in range(2):
                kb.op('pe', lambda e, ps=ps, lt=lt, lo=lo: e.matmul(ps[:, 0:64], lhsT=ns256[:, lt, lo * 128:(lo + 1) * 128], rhs=hc[:, lt, 64:128], start=False, stop=(lt == 1)),
                      reads=[ns256b, hcb], writes=[psb])
            f, fb = fo.next()
            kb.op('dve', lambda e, f=f, ps=ps, lo=lo: e.scalar_tensor_tensor(out=f[:, :], in0=ps[:, 0:64], scalar=sc_c, in1=gc[:, lo, :], op0=ALU.mult, op1=ALU.mult),
                  reads=[psb, gcb], writes=[fb])
            kb.op('pool', lambda e, f=f, lo=lo: e.dma_start(out=YF.rows(lo * 128, 192, 256), in_=f[:, :]), reads=[fb], dma=True)
        h1 = kb.sb('h1', [128, 64, 128], F32, c1)
        h1b = Buf()
        HTl = HT[LC:LT, :].rearrange("(a b) c -> a b c", b=64)
        for j in range(4):
            kb.op('sp', lambda e, j=j: e.dma_start(out=h1[:, j * 16:(j + 1) * 16, :], in_=HTl[:, j * 16:(j + 1) * 16, :]), writes=[h1b], dma=True)
        py = Ring(kb, 'py', [128, 512], F32, 4, psum=True, es=c1)
        zt = Ring(kb, 'zt', [128, 4, 128], F32, 6, es=c1)
        zo = Ring(kb, 'zo', [128, 2, 4, 128], F32, 3, es=c1)
        for pc_ in range(16):
            b0 = pc_ * 4
            pr, prb = py.next()
            pi, pib = py.next()
            kb.op('pe', lambda e, pr=pr, b0=b0: e.matmul(pr[:, :], lhsT=c128[:, :], rhs=h1[:, b0:b0 + 4, :], start=True, stop=True), reads=[c128b, h1b], writes=[prb])
            kb.op('pe', lambda e, pi=pi, b0=b0: e.matmul(pi[:, :], lhsT=ns128[:, :], rhs=h1[:, b0:b0 + 4, :], start=True, stop=True), reads=[ns128b, h1b], writes=[pib])
            cb_ = twc[:, b0:b0 + 4].unsqueeze(2).to_broadcast([128, 4, 128])
            sb_ = tws[:, b0:b0 + 4].unsqueeze(2).to_broadcast([128, 4, 128])
            prv = pr[:, :].rearrange("p (b c) -> p b c", c=128)
            piv = pi[:, :].rearrange("p (b c) -> p b c", c=128)
            t1, t1b = zt.next()
            t2, t2b = zt.next()
            z, zb = zo.next()
            kb.op('dve', lambda e, t1=t1, prv=prv, cb_=cb_: e.tensor_tensor(out=t1[:, :, :], in0=prv, in1=cb_, op=ALU.mult), reads=[prb, twcb], writes=[t1b])
            kb.op('dve', lambda e, t2=t2, piv=piv, sb_=sb_: e.tensor_tensor(out=t2[:, :, :], in0=piv, in1=sb_, op=ALU.mult), reads=[pib, twsb], writes=[t2b])
            kb.op('pool', lambda e, z=z, t1=t1, t2=t2: e.tensor_tensor(out=z[:, 0, :, :], in0=t1[:, :, :], in1=t2[:, :, :], op=ALU.add), reads=[t1b, t2b], writes=[zb])
            t3, t3b = zt.next()
            t4, t4b = zt.next()
            kb.op('dve', lambda e, t3=t3, piv=piv, cb_=cb_: e.tensor_tensor(out=t3[:, :, :], in0=piv, in1=cb_, op=ALU.mult), reads=[pib, twcb], writes=[t3b])
            kb.op('dve', lambda e, t4=t4, prv=prv, sb_=sb_: e.tensor_tensor(out=t4[:, :, :], in0=prv, in1=sb_, op=ALU.mult), reads=[prb, twsb], writes=[t4b])
            kb.op('pool', lambda e, z=z, t3=t3, t4=t4: e.tensor_tensor(out=z[:, 1, :, :], in0=t3[:, :, :], in1=t4[:, :, :], op=ALU.subtract), reads=[t3b, t4b], writes=[zb])
            for ri in range(2):
                kb.op('pool', lambda e, z=z, ri=ri, b0=b0: e.dma_start(out=ZS[ri, :, b0:b0 + 4, :], in_=z[:, ri, :, :]), reads=[zb], dma=True)
        kb.barrier(bar[:])
    with ExitStack() as c2:
        fl1 = kb.sb('fl1', [128, 64], F32, c2)
        fl2 = kb.sb('fl2', [128, 64], F32, c2)
        fl1b, fl2b = load_consts(kb, [(fl1, P['fl1']), (fl2, P['fl2'])])
        rz1 = kb.sb('rz1', [128, 128, 64], F32, c2)
        rz2 = kb.sb('rz2', [128, 128, 64], F32, c2)
        g2 = kb.sb('g2', [64, 128, 64], F32, c2)
        fo2 = kb.sb('fo2', [64, 128, 64], F32, c2)
        rz1b = [Buf() for _ in range(4)]
        rz2b = [Buf() for _ in range(4)]
        g2b = Buf()
        fo2b = [Buf() for _ in range(4)]
        for qa in range(4):
            asl = slice(qa * 32, (qa + 1) * 32)
            for ri in range(2):
                src = ZS[ri].rearrange("a b c -> b a c")
                kb.op('sp', lambda e, ri=ri, src=src, asl=asl: e.dma_start(out=rz1[ri * 64:(ri + 1) * 64, asl, :], in_=src[:, asl, 0:64]), writes=[rz1b[qa]], dma=True)
                kb.op('sp', lambda e, ri=ri, src=src, asl=asl: e.dma_start(out=rz2[ri * 64:(ri + 1) * 64, asl, :], in_=src[:, asl, 64:128]), writes=[rz2b[qa]], dma=True)
        kb.op('sp', lambda e: e.dma_start(out=g2[:, :, :], in_=GT[LC:LT, 192:256].rearrange("(b a) c -> b a c", a=128)), writes=[g2b], dma=True)
        pf = Ring(kb, 'pf', [128, 512], F32, 2, psum=True, es=c2)
        for pc_ in range(16):
            a0 = pc_ * 8
            qa = pc_ // 4
            ps, psb = pf.next()
            kb.op('pe', lambda e, ps=ps, a0=a0: e.matmul(ps[0:64, :], lhsT=fl1[:, :], rhs=rz1[:, a0:a0 + 8, :], start=True, stop=False), reads=[fl1b, rz1b[qa]], writes=[psb])
            kb.op('pe', lambda e, ps=ps, a0=a0: e.matmul(ps[0:64, :], lhsT=fl2[:, :], rhs=rz2[:, a0:a0 + 8, :], start=False, stop=True), reads=[fl2b, rz2b[qa]], writes=[psb])
            kb.op('dve', lambda e, ps=ps, a0=a0: e.scalar_tensor_tensor(out=fo2[:, a0:a0 + 8, :], in0=ps[0:64, :].rearrange("p (a c) -> p a c", c=64), scalar=sc_l,
                                                                        in1=g2[:, a0:a0 + 8, :], op0=ALU.mult, op1=ALU.mult), reads=[psb, g2b], writes=[fo2b[qa]])
        allfo = fo2b
        for (dst, b0, b1) in YF.lat_groups():
            kb.op('pool', lambda e, dst=dst, b0=b0, b1=b1: e.dma_start(out=dst, in_=fo2[b0:b1, :, :]), reads=allfo, dma=True)
        kb.barrier(bar[:])


def gate_rows(kb, es, nc, sil, silb, adaw_gate, adab_row, npost_b, ones1, ones1b, keep_es, NG=None):
    if NG is None:
        NG = kb.sb('NG', [128, 2, D], F32, keep_es)
    NGb = Buf()
    wg = kb.sb('adawg', [128, 8, D], F32, es)
    wgb = Buf()
    for k in range(8):
        kb.op('sp', lambda e, k=k: e.dma_start(out=wg[:, k, :], in_=adaw_gate[k * 128:(k + 1) * 128, :]), writes=[wgb], dma=True)
    br = kb.sb('adabr', [1, D], F32, es)
    brb, = load_consts(kb, [(br, adab_row)])
    npb = kb.sb('npostb', [128, D], F32, es)
    npbb, = load_consts(kb, [(npb, npost_b)])
    onesq = kb.sb('onesq', [128, 128], F32, es)
    onesqb = Buf()
    kb.op('dve', lambda e: e.memset(onesq[:], 1.0), writes=[onesqb])
    srep = kb.sb('silrep', [128, 8, 128], F32, es)
    pg_ = Ring(kb, 'pgate', [128, 512], F32, 2, psum=True, es=es)
    for n in range(2):
        srb = Buf()
        for k in range(8):
            kb.op('dve', lambda e, k=k, n=n: e.tensor_scalar(out=srep[:, k, :], in0=onesq[:, :], scalar1=sil[:, k, n:n + 1], scalar2=None, op0=ALU.mult),
                  reads=[onesqb, silb], writes=[srb])
        for half in range(2):
            ps, psb = pg_.next()
            for k in range(8):
                kb.op('pe', lambda e, ps=ps, k=k, half=half: e.matmul(ps[:, :], lhsT=srep[:, k, :], rhs=wg[:, k, half * 512:(half + 1) * 512], start=(k == 0), stop=False),
                      reads=[srb, wgb], writes=[psb])
            kb.op('pe', lambda e, ps=ps, half=half: e.matmul(ps[:, :], lhsT=ones1[0:1, :], rhs=br[0:1, half * 512:(half + 1) * 512], start=False, stop=True),
                  reads=[ones1b, brb], writes=[psb])
            kb.op('dve', lambda e, ps=ps, n=n, half=half: e.tensor_tensor(out=NG[:, n, half * 512:(half + 1) * 512], in0=ps[:, :], in1=npb[:, half * 512:(half + 1) * 512], op=ALU.mult),
                  reads=[psb, npbb], writes=[NGb])
    return NG, NGb


class OutProj:
    def __init__(self, kb, es, nc, yT, w_out_src, xsrc, NG, NGb, epsT, epsb, nslot=3, ysrc=None, ident=None, npol=2):
        self.kb, self.yT, self.xsrc, self.NG, self.NGb, self.epsT, self.epsb = kb, yT, xsrc, NG, NGb, epsT, epsb
        self.ysrc, self.ident = ysrc, ident
        self.pref = {}
        self.ykbs = {}
        if ysrc is not None:
            self.ytok = Ring(kb, 'ytok', [128, 4, 256], F32, 2, es=es)
            self.ytokb = Ring(kb, 'ytokb', [128, 1024], BF16, 2, es=es)
            self.pyt = Ring(kb, 'pyt', [128, 8, 128], BF16, 1, psum=True, es=es)
        self.wo, self.wob = load_weight_bf16(kb, es, nc, 'wout_bf', w_out_src, D, es)
        self.ystg = Ring(kb, 'ystg', [128, 8, 128], F32, 2, es=es)
        self.ybf = Ring(kb, 'ybf', [128, 8, 128], BF16, 2, es=es)
        self.xin = Ring(kb, 'xres', [128, D], F32, 2, es=es)
        self.xout = Ring(kb, 'xnew', [128, D], F32, nslot, es=es)
        self.pol = Ring(kb, 'pol', [128, 512], F32, npol, psum=True, es=es)
        self.st = Ring(kb, 'ost', [128, 8], F32, 4, es=es, strict=True)
        self.junk = kb.sb('ojunk', [128, 512], BF16, es)
        self.junkb = Buf()
        self.tmp = Ring(kb, 'otmp', [128, D], F32, 2, es=es)

    def _pre(self, r0):
        kb = self.kb
        if self.ysrc is None:
            ys, ysb = self.ystg.next()
            yT = self.yT
            kb.op('sp', lambda e, ys=ys: e.dma_start(out=ys[:, :, :], in_=yT[:, r0:r0 + 128].rearrange("(k p) t -> p k t", p=128)), writes=[ysb], dma=True)
            yb, ybb = self.ybf.next()
            kb.op('pool', lambda e, yb=yb, ys=ys: e.tensor_copy(out=yb[:, :, :], in_=ys[:, :, :]), reads=[ysb], writes=[ybb])
        else:
            aps, sbufs = self.ysrc(r0)
            yk, ykb0 = self.ytok.next()
            ykb = self.ykbs.setdefault(id(ykb0), [ykb0] + [Buf() for _ in range(3)])
            for r, ap_ in enumerate(aps):
                kb.op('sp', lambda e, yk=yk, r=r, ap_=ap_: e.dma_start(out=yk[:, r, :], in_=ap_), reads=sbufs, writes=[ykb[r]], dma=True)
            ykb16, ykb16b = self.ytokb.next()
            kb.op('pool', lambda e, yk=yk, ykb16=ykb16: e.tensor_copy(out=ykb16[:, :], in_=yk[:, :, :].rearrange("p r c -> p (r c)")), reads=[ykb], writes=[ykb16b])
            pt, ptb = self.pyt.next()
            idt, idtb = self.ident
            for k in range(8):
                kb.op('pe', lambda e, pt=pt, ykb16=ykb16, k=k: e.transpose(out=pt[:, k, :], in_=ykb16[:, k * 128:(k + 1) * 128], identity=idt[:, :]), reads=[ykb16b, idtb], writes=[ptb])
            yb, ybb = self.ybf.next()
            kb.op('act', lambda e, yb=yb, pt=pt: e.activation(out=yb[:, :, :], in_=pt[:, :, :], func=AF.Copy), reads=[ptb], writes=[ybb])
        xi, xib = self.xin.next()
        kb.op('sp', lambda e, xi=xi: e.dma_start(out=xi[:, :], in_=self.xsrc[r0:r0 + 128, :]), writes=[xib], dma=True)
        self.pref[r0] = (yb, ybb, xi, xib)

    def tile(self, r0, n, nxt=None):
        kb = self.kb
        if r0 not in self.pref:
            self._pre(r0)
        if nxt is not None and nxt not in self.pref:
            self._pre(nxt)
        yb, ybb, xi, xib = self.pref.pop(r0)
        pss = [self.pol.next(), self.pol.next()]
        for half in range(2):
            ps, psb = pss[half]
            for k in range(8):
                kb.op('pe', lambda e, ps=ps, k=k, half=half, yb=yb: e.matmul(ps[:, :], lhsT=yb[:, k, :], rhs=self.wo[:, k, half * 512:(half + 1) * 512],
                                                                           start=(k == 0), stop=(k == 7)), reads=[ybb, self.wob], writes=[psb])
        st, stb = self.st.next()
        for half in range(2):
            ps, psb = pss[half]
            kb.op('act', lambda e, ps=ps, st=st, half=half: e.activation(out=self.junk[:, :], in_=ps[:, :], func=AF.Square, accum_out=st[:, half:half + 1]),
                  reads=[psb], writes=[stb, self.junkb])
        kb.op('dve', lambda e, st=st: e.tensor_tensor(out=st[:, 2:3], in0=st[:, 0:1], in1=st[:, 1:2], op=ALU.add), reads=[stb], writes=[stb])
        kb.op('act', lambda e, st=st: e.activation(out=st[:, 3:4], in_=st[:, 2:3], func=AF.Sqrt, scale=1.0 / D, bias=self.epsT[:, 0:1]), reads=[stb, self.epsb], writes=[stb])
        kb.op('dve', lambda e, st=st: e.reciprocal(out=st[:, 4:5], in_=st[:, 3:4]), reads=[stb], writes=[stb])
        tm_, tmb_ = self.tmp.next()
        for half in range(2):
            ps, psb = pss[half]
            kb.op('dve', lambda e, ps=ps, st=st, tm_=tm_, half=half: e.scalar_tensor_tensor(out=tm_[:, half * 512:(half + 1) * 512], in0=ps[:, :], scalar=st[:, 4:5],
                                                                                       in1=self.NG[:, n, half * 512:(half + 1) * 512], op0=ALU.mult, op1=ALU.mult),
                  reads=[psb, stb, self.NGb], writes=[tmb_])
        xo, xob = self.xout.next()
        kb.op('pool', lambda e, xo=xo, tm_=tm_, xi=xi: e.tensor_tensor(out=xo[:, :], in0=tm_[:, :], in1=xi[:, :], op=ALU.add), reads=[tmb_, xib], writes=[xob])
        return xo, xob


L3_TOK = 2048


def part3(kb, nc, IN, ZG, X1s, OUT, C):
    epsT, epsb, ones1, ones1b = C['epsT'], C['epsb'], C['ones1'], C['ones1b']
    with ExitStack() as g:
        sil = kb.sb('sil3', [128, 8, 2], F32, g)
        silb = Buf(strict=True)
        kb.op('sp', lambda e: e.dma_start(out=sil[:], in_=IN['sil_in']), writes=[silb], dma=True)
        kb.op('act', lambda e: e.activation(out=sil[:], in_=sil[:], func=AF.Silu), reads=[silb], writes=[silb])
        NG = kb.sb('NG3', [128, 2, D], F32, g)
        with ExitStack() as g0:
            NG, NGb = gate_rows(kb, g0, nc, sil, silb, IN['adawg1'], IN['adabr1'], IN['npostb1'], ones1, ones1b, g0, NG=NG)
            kb.barrier(C['bar'][:])
        op = OutProj(kb, g, nc, None, IN['wout1'], X1s, NG, NGb, epsT, epsb, nslot=4, ysrc=ZG.src, ident=(C['ident_bf'], C['identb']), npol=6)
        for i in range(L // 128):
            xo, xob = op.tile(i * 128, 0, nxt=((i + 1) * 128 if (i + 1) * 128 < L else None))
            kb.op('pool', lambda e, xo=xo, i=i: e.dma_start(out=OUT[i * 128:(i + 1) * 128, :], in_=xo[:, :]), reads=[xob], dma=True, final=True)
        kb.barrier(C['bar'][:])


RT_PARAMS = {'logit': [128, 4], 'diffT': [128, 2, 128], 'mask01T': [128, 2, 128], 'posxi': [128, 2, 128], 'poszeta': [128, 2]}


def part2(kb, nc, IN, YG, X1s, ZD, C):
    debug = False
    sil_in, adawg, adabr, npostb, adaw1, adab1, normw1, win1, ropec, ropes = (IN[k_] for k_ in (
        'sil_in', 'adawg0', 'adabr0', 'npostb0', 'adaw1', 'adab1', 'normw1', 'win1', 'ropec', 'ropes'))
    xin, wout0 = IN['xin'], IN['wout0p']
    RP = {k_: IN['r_' + k_] for k_ in RT_PARAMS}
    X1 = X1s
    Z = ZD
    QKV = dram_tmp(nc, 'QKV', [LT, 768], BF16, debug)
    GS = dram_tmp(nc, 'GS', [LT, 256], F32, debug)
    bar, ident_f, identfb, ident_bf, identb, epsT, epsb, ones1, ones1b = (C[k_] for k_ in (
        'bar', 'ident_f', 'identfb', 'ident_bf', 'identb', 'epsT', 'epsb', 'ones1', 'ones1b'))
    with ExitStack() as pa:
        sil = kb.sb('sil', [128, 8, 2], F32, pa)
        silb = Buf(strict=True)
        kb.op('sp', lambda e: e.dma_start(out=sil[:], in_=sil_in), writes=[silb], dma=True)
        kb.op('act', lambda e: e.activation(out=sil[:], in_=sil[:], func=AF.Silu), reads=[silb], writes=[silb])
        NG = kb.sb('NG', [128, 2, D], F32, pa)
        mod1 = kb.sb('mod1', [128, 16, 2], F32, pa)
        G1 = kb.sb('G1', [128, 8, 2], F32, pa)
        with ExitStack() as pg0:
            NG, NGb = gate_rows(kb, pg0, nc, sil, silb, adawg, adabr, npostb, ones1, ones1b, pg0, NG=NG)
            mod1, mod1b = adaln_vectors(kb, pg0, nc, sil_in, adaw1, adab1, normw1, 16, mod_tile=mod1)
            nw = kb.sb('nw1', [128, 8], F32, pg0)
            nwb, = load_consts(kb, [(nw, normw1)])
            G1buf = Buf(strict=True)
            for n in range(2):
                kb.op('dve', lambda e, n=n: e.scalar_tensor_tensor(out=G1[:, :, n], in0=mod1[:, 8:16, n], scalar=1.0, in1=nw[:, :], op0=ALU.add, op1=ALU.mult),
                      reads=[mod1b, nwb], writes=[G1buf])
            kb.barrier(bar[:])
        Gb = {'G': G1buf, 'eps': epsT, 'epsb': epsb}
        op0 = OutProj(kb, pa, nc, None, wout0, xin, NG, NGb, epsT, epsb, nslot=3, ysrc=YG.src, ident=(ident_bf, identb))
        w1, w1b = load_weight_bf16(kb, pa, nc, 'w1_bf', win1, D, pa)
        for k in range(8):
            kb.op('pool', lambda e, k=k: e.tensor_scalar(out=w1[:, k, 256:512], in0=w1[:, k, 256:512], scalar1=float(128 ** -0.5), scalar2=None, op0=ALU.mult), reads=[w1b], writes=[w1b])
        pqk = Ring(kb, 'pqk', [128, 512], F32, 2, psum=True, es=pa)
        csr = Ring(kb, 'cs', [128, 2, 64], F32, 2, es=pa)
        qkvr = Ring(kb, 'qkvt', [128, 768], BF16, 2, es=pa)
        rtmp = Ring(kb, 'rtmp', [128, 4, 64], F32, 4, es=pa)
        gsr = Ring(kb, 'gst', [128, 256], F32, 2, es=pa)

        rlist = [t0_ + sb_ * 128 for (t0_, T_) in TILES for sb_ in range(T_ // 128)]

        def get_x(r0):
            n = 1 if r0 < LC else 0
            ix = rlist.index(r0)
            xo, xob = op0.tile(r0, n, nxt=(rlist[ix + 1] if ix + 1 < len(rlist) else None))
            if r0 >= LC:
                kb.op('pool', lambda e, xo=xo, r0=r0: e.dma_start(out=X1[r0 - LC:r0 - LC + 128, :], in_=xo[:, :]), reads=[xob], dma=True)
            return xo, xob

        def emit_tile(t0, T, hT, hb):
            for sub in range(T // 128):
                r0 = t0 + sub * 128
                psA, psAb = pqk.next()
                psB, psBb = pqk.next()
                for half, (ps, psb) in enumerate([(psA, psAb), (psB, psBb)]):
                    for k in range(8):
                        kb.op('pe', lambda e, ps=ps, k=k, half=half, sub=sub: e.matmul(ps[:, :], lhsT=hT[:, k, sub * 128:(sub + 1) * 128], rhs=w1[:, k, half * 512:(half + 1) * 512],
                                                                                     start=(k == 0), stop=(k == 7)), reads=[hb, w1b], writes=[psb])
                qkv, qkvb = qkvr.next()
                if r0 >= LC:
                    cs, csb = csr.next()
                    kb.op('sp', lambda e, cs=cs, r0=r0: e.dma_start(out=cs[:, 0, :], in_=ropec[r0 - LC:r0 - LC + 128, :]), writes=[csb], dma=True)
                    kb.op('sp', lambda e, cs=cs, r0=r0: e.dma_start(out=cs[:, 1, :], in_=ropes[r0 - LC:r0 - LC + 128, :]), writes=[csb], dma=True)
                    pv_ = psA[:, :].rearrange("p (g h f) -> p g h f", g=4, h=2)
                    ov_ = qkv[:, 0:512].rearrange("p (g h f) -> p g h f", g=4, h=2)
                    cb_ = cs[:, 0, :].unsqueeze(1).to_broadcast([128, 4, 64])
                    sb_ = cs[:, 1, :].unsqueeze(1).to_broadcast([128, 4, 64])
                    ta, tab = rtmp.next()
                    tb, tbb = rtmp.next()
                    kb.op('dve', lambda e, ta=ta, pv_=pv_, cb_=cb_: e.tensor_tensor(out=ta[:, :, :], in0=pv_[:, :, 0, :], in1=cb_, op=ALU.mult), reads=[psAb, csb], writes=[tab])
                    kb.op('dve', lambda e, tb=tb, pv_=pv_, sb_=sb_: e.tensor_tensor(out=tb[:, :, :], in0=pv_[:, :, 1, :], in1=sb_, op=ALU.mult), reads=[psAb, csb], writes=[tbb])
                    kb.op('pool', lambda e, ta=ta, tb=tb, ov_=ov_: e.tensor_tensor(out=ov_[:, :, 0, :], in0=ta[:, :, :], in1=tb[:, :, :], op=ALU.subtract), reads=[tab, tbb], writes=[qkvb])
                    tc_, tcb = rtmp.next()
                    td, tdb = rtmp.next()
                    kb.op('dve', lambda e, tc_=tc_, pv_=pv_, sb_=sb_: e.tensor_tensor(out=tc_[:, :, :], in0=pv_[:, :, 0, :], in1=sb_, op=ALU.mult), reads=[psAb, csb], writes=[tcb])
                    kb.op('dve', lambda e, td=td, pv_=pv_, cb_=cb_: e.tensor_tensor(out=td[:, :, :], in0=pv_[:, :, 1, :], in1=cb_, op=ALU.mult), reads=[psAb, csb], writes=[tdb])
                    kb.op('pool', lambda e, tc_=tc_, td=td, ov_=ov_: e.tensor_tensor(out=ov_[:, :, 1, :], in0=tc_[:, :, :], in1=td[:, :, :], op=ALU.add), reads=[tcb, tdb], writes=[qkvb])
                else:
                    kb.op('act', lambda e, qkv=qkv, psA=psA: e.activation(out=qkv[:, 0:512], in_=psA[:, :], func=AF.Copy), reads=[psAb], writes=[qkvb])
                kb.op('act', lambda e, qkv=qkv, psB=psB: e.activation(out=qkv[:, 512:768], in_=psB[:, 0:256], func=AF.Copy), reads=[psBb], writes=[qkvb])
                gs, gsb = gsr.next()
                kb.op('act', lambda e, gs=gs, psB=psB: e.activation(out=gs[:, :], in_=psB[:, 256:512], func=AF.Silu), reads=[psBb], writes=[gsb])
                kb.op('pool', lambda e, qkv=qkv, r0=r0: e.dma_start(out=QKV[r0:r0 + 128, :], in_=qkv[:, :]), reads=[qkvb], dma=True)
                kb.op('pool', lambda e, gs=gs, r0=r0: e.dma_start(out=GS[r0:r0 + 128, :], in_=gs[:, :]), reads=[gsb], dma=True)

        phase_proj(kb, nc, pa, None, None, G1, mod1, Gb, ident_bf, identb, emit_tile, get_x=get_x)
        kb.barrier(bar[:])
    with ExitStack() as pr:
        phase_ret(kb, nc, pr, QKV, GS, Z, RP, ident_bf, identb, epsT, epsb)
        kb.barrier(bar[:])


def phase_ret(kb, nc, es, QKV, GS, Z, RP, ident_bf, identb, epsT, epsb):
    def const(name, shape, src):
        t = kb.sb(name, shape, F32, es)
        b, = load_consts(kb, [(t, src)])
        return t, b
    lgt, lgtb = const('lgt', [128, 4], RP['logit'])
    lgtb.strict = True
    diffT, diffTb = const('diffT', [128, 2, 128], RP['diffT'])
    m01, m01b = const('m01', [128, 2, 128], RP['mask01T'])
    pxi, pxib = const('pxi', [128, 2, 128], RP['posxi'])
    pze, pzeb = const('pze', [128, 2], RP['poszeta'])
    kb.op('act', lambda e: e.activation(out=lgt[:, :], in_=lgt[:, :], func=AF.Exp, scale=-1.0), reads=[lgtb], writes=[lgtb])
    kb.op('dve', lambda e: e.tensor_scalar(out=lgt[:, :], in0=lgt[:, :], scalar1=1.0, scalar2=None, op0=ALU.add), reads=[lgtb], writes=[lgtb])
    kb.op('act', lambda e: e.activation(out=lgt[:, :], in_=lgt[:, :], func=AF.Ln), reads=[lgtb], writes=[lgtb])
    kb.op('dve', lambda e: e.tensor_scalar(out=lgt[:, :], in0=lgt[:, :], scalar1=-1.0, scalar2=None, op0=ALU.mult), reads=[lgtb], writes=[lgtb])
    dmt = kb.sb('dmt', [128, 4, 128], BF16, es)
    xib = kb.sb('xib', [128, 4, 128], F32, es)
    zet = kb.sb('zet', [128, 8], F32, es)
    tabb = Buf(strict=True)
    tmpd = kb.sb('tmpd', [128, 128], F32, es)
    tmpdb = Buf()
    for hl in range(2):
        for dr in range(2):
            j = 2 * hl + dr
            kb.op('act', lambda e, j=j, dr=dr: e.activation(out=tmpd[:, :], in_=diffT[:, dr, :], func=AF.Exp, scale=lgt[:, j:j + 1]), reads=[diffTb, lgtb], writes=[tmpdb])
            kb.op('dve', lambda e, j=j, dr=dr: e.tensor_tensor(out=dmt[:, j, :], in0=tmpd[:, :], in1=m01[:, dr, :], op=ALU.mult), reads=[tmpdb, m01b], writes=[tabb])
            kb.op('act', lambda e, j=j, dr=dr: e.activation(out=xib[:, j, :], in_=pxi[:, dr, :], func=AF.Exp, scale=lgt[:, j:j + 1]), reads=[pxib, lgtb], writes=[tabb])
            kb.op('act', lambda e, j=j, dr=dr: e.activation(out=zet[:, j:j + 1], in_=pze[:, dr:dr + 1], func=AF.Exp, scale=lgt[:, j:j + 1]), reads=[pzeb, lgtb], writes=[tabb])
            kb.op('act', lambda e, j=j: e.activation(out=zet[:, 4 + j:5 + j], in_=lgt[:, j:j + 1], func=AF.Exp, scale=128.0), reads=[lgtb], writes=[tabb])
    oacc = kb.sb('oacc', [128, NCH, 256], F32, es)
    oaccb = [Buf() for _ in range(NCH)]
    kb.op('pool', lambda e: e.memset(oacc[:], 0.0), writes=oaccb)
    R = kb.sb('Rst', [128, 2, 128], F32, es)
    Rbf = kb.sb('Rbf', [128, 2, 128], BF16, es)
    Rb = [Buf(), Buf()]
    qr = Ring(kb, 'rq', [128, 768], BF16, 3, es=es)
    qtr = Ring(kb, 'rqT', [128, 128], BF16, 3, es=es)
    ktr_ = Ring(kb, 'rkT', [128, 128], BF16, 3, es=es)
    qxr = Ring(kb, 'rqx', [128, 128], BF16, 3, es=es)
    kzr = Ring(kb, 'rkz', [128, 128], BF16, 3, es=es)
    sdr = Ring(kb, 'rsd', [128, 128], BF16, 3, es=es)
    ptT = Ring(kb, 'rptT', [128, 1024], BF16, 2, psum=True, es=es)
    pS = Ring(kb, 'rpS', [128, 512], F32, 2, psum=True, es=es)
    pO = Ring(kb, 'rpO', [128, 512], F32, 2, psum=True, es=es)
    pR = Ring(kb, 'rpR', [128, 512], F32, 2, psum=True, es=es)
    gsr = Ring(kb, 'rgs', [128, 256], F32, 2, es=es)
    zr = Ring(kb, 'rz', [128, 256], F32, 2, es=es)
    sst = Ring(kb, 'rss', [128, 8], F32, 4, es=es, strict=True)
    junk = kb.sb('rjunk', [128, 128], BF16, es)
    junkb = Buf()
    for dr in range(2):
        kb.op('dve', lambda e: e.memset(R[:], 0.0), writes=Rb)
        kb.op('dve', lambda e: e.memset(Rbf[:], 0.0), writes=Rb)
        for c in CHUNK_ORDER[dr]:
            r0 = c * 128
            q, qb = qr.next()
            kb.op('sp', lambda e, q=q, r0=r0: e.dma_start(out=q[:, :], in_=QKV[r0:r0 + 128, :]), writes=[qb], dma=True)
            for hl in range(2):
                j = 2 * hl + dr
                vt_ = q[:, 512 + hl * 128:512 + (hl + 1) * 128]
                ktok = q[:, 256 + hl * 128:256 + (hl + 1) * 128]
                if c >= 2:
                    pt, ptb = ptT.next()
                    kb.op('pe', lambda e, pt=pt, q=q, hl=hl: e.transpose(out=pt[:, 0:128], in_=q[:, hl * 128:(hl + 1) * 128], identity=ident_bf[:, :]), reads=[qb, identb], writes=[ptb])
                    kb.op('pe', lambda e, pt=pt, ktok=ktok: e.transpose(out=pt[:, 128:256], in_=ktok, identity=ident_bf[:, :]), reads=[qb, identb], writes=[ptb])
                    qT, qTb = qtr.next()
                    kT, kTb = ktr_.next()
                    qx, qxb = qxr.next()
                    kb.op('act', lambda e, qT=qT, pt=pt: e.activation(out=qT[:, :], in_=pt[:, 0:128], func=AF.Copy), reads=[ptb], writes=[qTb])
                    kb.op('dve', lambda e, qx=qx, pt=pt, j=j: e.tensor_tensor(out=qx[:, :], in0=pt[:, 0:128], in1=xib[:, j, :], op=ALU.mult), reads=[ptb, tabb], writes=[qxb])
                    kb.op('act', lambda e, kT=kT, pt=pt: e.activation(out=kT[:, :], in_=pt[:, 128:256], func=AF.Copy), reads=[ptb], writes=[kTb])
                    ps, psb = pS.next()
                    kb.op('pe', lambda e, ps=ps, kT=kT, qT=qT: e.matmul(ps[:, 0:128], lhsT=kT[:, :], rhs=qT[:, :], start=True, stop=True), reads=[kTb, qTb], writes=[psb])
                    sd, sdb = sdr.next()
                    kb.op('dve', lambda e, sd=sd, ps=ps, j=j: e.tensor_tensor(out=sd[:, :], in0=ps[:, 0:128], in1=dmt[:, j, :], op=ALU.mult), reads=[psb, tabb], writes=[sdb])
                    po, pob = pO.next()
                    kb.op('pe', lambda e, po=po, sd=sd, vt_=vt_: e.matmul(po[:, 0:128], lhsT=sd[:, :], rhs=vt_, start=True, stop=False), reads=[sdb, qb], writes=[pob])
                    kb.op('pe', lambda e, po=po, qx=qx, hl=hl: e.matmul(po[:, 0:128], lhsT=qx[:, :], rhs=Rbf[:, hl, :], start=False, stop=True), reads=[qxb, Rb[hl]], writes=[pob])
                    if dr == 0:
                        kb.op('act', lambda e, po=po, c=c, hl=hl: e.activation(out=oacc[:, c, hl * 128:(hl + 1) * 128], in_=po[:, 0:128], func=AF.Copy), reads=[pob], writes=[oaccb[c]])
                    else:
                        kb.op('dve', lambda e, po=po, c=c, hl=hl: e.tensor_tensor(out=oacc[:, c, hl * 128:(hl + 1) * 128], in0=po[:, 0:128], in1=oacc[:, c, hl * 128:(hl + 1) * 128], op=ALU.add),
                              reads=[pob, oaccb[c]], writes=[oaccb[c]])
                kz, kzb = kzr.next()
                kb.op('pool', lambda e, kz=kz, ktok=ktok, j=j: e.tensor_scalar(out=kz[:, :], in0=ktok, scalar1=zet[:, j:j + 1], scalar2=None, op0=ALU.mult), reads=[qb, tabb], writes=[kzb])
                pr_, prb = pR.next()
                kb.op('pe', lambda e, pr_=pr_, kz=kz, vt_=vt_: e.matmul(pr_[:, 0:128], lhsT=kz[:, :], rhs=vt_, start=True, stop=True), reads=[kzb, qb], writes=[prb])
                kb.op('dve', lambda e, pr_=pr_, hl=hl, j=j: e.scalar_tensor_tensor(out=R[:, hl, :], in0=R[:, hl, :], scalar=zet[:, 4 + j:5 + j], in1=pr_[:, 0:128], op0=ALU.mult, op1=ALU.add),
                      reads=[prb, tabb, Rb[hl]], writes=[Rb[hl]])
                kb.op('act', lambda e, hl=hl: e.activation(out=Rbf[:, hl, :], in_=R[:, hl, :], func=AF.Copy), reads=[Rb[hl]], writes=[Rb[hl]])
            if dr == 1 and c >= 2:
                gs, gsb = gsr.next()
                kb.op('sp', lambda e, gs=gs, r0=r0: e.dma_start(out=gs[:, :], in_=GS[r0:r0 + 128, :]), writes=[gsb], dma=True)
                zt_, ztb = zr.next()
                for hl in range(2):
                    ss, ssb = sst.next()
                    kb.op('act', lambda e, ss=ss, c=c, hl=hl: e.activation(out=junk[:, :], in_=oacc[:, c, hl * 128:(hl + 1) * 128], func=AF.Square, accum_out=ss[:, 0:1]), reads=[oaccb[c]], writes=[ssb, junkb])
                    kb.op('act', lambda e, ss=ss: e.activation(out=ss[:, 1:2], in_=ss[:, 0:1], func=AF.Sqrt, scale=1.0 / 128, bias=epsT[:, 0:1]), reads=[ssb, epsb], writes=[ssb])
                    kb.op('dve', lambda e, ss=ss: e.reciprocal(out=ss[:, 2:3], in_=ss[:, 1:2]), reads=[ssb], writes=[ssb])
                    kb.op('dve', lambda e, ss=ss, zt_=zt_, gs=gs, c=c, hl=hl: e.scalar_tensor_tensor(out=zt_[:, hl * 128:(hl + 1) * 128], in0=oacc[:, c, hl * 128:(hl + 1) * 128], scalar=ss[:, 2:3],
                                                                                                   in1=gs[:, hl * 128:(hl + 1) * 128], op0=ALU.mult, op1=ALU.mult), reads=[ssb, oaccb[c], gsb], writes=[ztb])
                kb.op('pool', lambda e, zt_=zt_, r0=r0: e.dma_start(out=Z.rows(r0 - LC, 0, 256), in_=zt_[:, :]), reads=[ztb], dma=True)


class ChunkedDram:
    def __init__(self, nc, name, nrows, rows_per, width, ranks=1):
        self.rp = rows_per
        self.n = nrows // rows_per
        assert self.n * rows_per == nrows
        self.ranks = ranks
        self.tiles = [dram_tmp(nc, '%s%d' % (name, i), [ranks * rows_per, width], F32) for i in range(self.n)]
        self.bufs = [Buf() for _ in range(self.n)]

    def rows(self, r0, c0, c1, n=128):
        i, lr = r0 // self.rp, r0 % self.rp
        return self.tiles[i][lr:lr + n, c0:c1]

    def src(self, r0):
        i, lr = r0 // self.rp, r0 % self.rp
        return [self.tiles[i][r * self.rp + lr:r * self.rp + lr + 128, :] for r in range(self.ranks)], [self.bufs[i]]

    def lat_groups(self):
        out = []
        tpc = self.rp // 128
        for i in range(self.n):
            b0, b1 = max(tpc * i - 2, 0), min(tpc * i + tpc - 2, 64)
            if b1 <= b0:
                continue
            lrow = (b0 + 2) * 128 - i * self.rp
            out.append((self.tiles[i][lrow:lrow + (b1 - b0) * 128, 192:256].rearrange("(b a) c -> b a c", a=128), b0, b1))
        return out


GROUPS = [[0, 1, 2, 3], [4, 5, 6, 7]]


def all_gather(kb, src, dst):
    for i in range(src.n):
        kb.op('pool', lambda e, i=i: e.collective_compute("AllGather", ALU.bypass, replica_groups=GROUPS, ins=[src.tiles[i].opt()], outs=[dst.tiles[i].opt()]),
              writes=[dst.bufs[i]], cc=True)


FUSED_INPUTS = {'xin': [LT, D], 'sil_in': [128, 8, 2], 'adaw': [D, 2048], 'adab': [128, 16], 'normw': [128, 8], 'wfm': [D, NFM], 'wg': [D, 256],
                'cs64': [64, 128], 'ident': [128, 128],
                'wout0p': [D, D], 'adawg0': [D, D], 'adabr0': [1, D], 'npostb0': [128, D], 'adaw1': [D, 2048], 'adab1': [128, 16], 'normw1': [128, 8],
                'win1': [D, D], 'ropec': [L, 64], 'ropes': [L, 64],
                'wout1': [D, D], 'adawg1': [D, D], 'adabr1': [1, D], 'npostb1': [128, D]}


def build_fused():
    nc = bass.Bass("TRN2", target_bir_lowering=False)
    kb = KB(nc)
    IN = {k_: dram_in(nc, k_, shp) for k_, shp in FUSED_INPUTS.items()}
    for k_, shp in RW_PARAMS.items():
        IN['p_' + k_] = dram_in(nc, 'p_' + k_, shp)
    for k_, shp in FN_PARAMS.items():
        IN['f_' + k_] = dram_in(nc, 'f_' + k_, shp)
    for k_, shp in RT_PARAMS.items():
        IN['r_' + k_] = dram_in(nc, 'r_' + k_, shp)
    OUT = dram_out(nc, 'out', [L, D])
    YB = ChunkedDram(nc, 'Yb', LT, 768, 256)
    YG = ChunkedDram(nc, 'Yg', LT, 768, 256, ranks=4)
    ZB = ChunkedDram(nc, 'Zb', L, 512, 256)
    ZG = ChunkedDram(nc, 'Zg', L, 512, 256, ranks=4)
    X1s = dram_tmp(nc, 'X1s', [L, D], F32)
    C = {}
    C['bar'] = kb.sb('bar', [128, 1], F32)
    C['ident_f'] = kb.sb('ident_f', [128, 128], F32)
    C['ident_bf'] = kb.sb('ident_bf', [128, 128], BF16)
    C['identfb'], = load_consts(kb, [(C['ident_f'], IN['ident'])])
    C['identb'] = Buf()
    kb.op('dve', lambda e: e.tensor_copy(out=C['ident_bf'][:], in_=C['ident_f'][:]), reads=[C['identfb']], writes=[C['identb']])
    C['epsT'] = kb.sb('epsT', [128, 1], F32)
    C['epsb'] = Buf()
    kb.op('dve', lambda e: e.memset(C['epsT'][:], EPS), writes=[C['epsb']])
    C['ones1'] = kb.sb('ones1', [1, 128], F32)
    C['ones1b'] = Buf()
    kb.op('dve', lambda e: e.memset(C['ones1'][:], 1.0), writes=[C['ones1b']])
    part1(kb, nc, IN, YB, C)
    all_gather(kb, YB, YG)
    part2(kb, nc, IN, YG, X1s, ZB, C)
    all_gather(kb, ZB, ZG)
    part3(kb, nc, IN, ZG, X1s, OUT, C)
    return nc, kb


def l1_inputs(inp, core):
    b, q = core // 4, core % 4
    f = lambda a: np.ascontiguousarray(a, dtype=np.float32)
    xin = np.concatenate([inp['ctx'][b], inp['x'][b]], axis=0)
    sil = np.stack([inp['c'][b].reshape(8,128).T, inp['c_ctx'].reshape(8,128).T], axis=-1)
    adaw = inp['ada_w'][0][:, :2048]
    adab = inp['ada_b'][0][:2048].reshape(16,128).T
    normw = inp['norm_pre'][0].reshape(8,128).T
    W = inp['ev_w_in'][0]
    cols = np.concatenate([np.arange(192)+192*q, 768+np.arange(192)+192*q, 1536+np.arange(192)+192*q,
                           np.arange(2304,2432), np.arange(2432,2560), 3328+64*q+np.arange(64)])
    gcols = np.concatenate([2560+192*q+np.arange(192), 3584+64*q+np.arange(64)])
    c = np.arange(64)
    ang = 2*np.pi*np.outer(c,c)/64
    cs64 = np.concatenate([np.cos(ang), np.sin(ang)], axis=1)
    return dict(xin=f(xin), sil_in=f(sil), adaw=f(adaw), adab=f(adab), normw=f(normw), wfm=f(W[:, cols]), wg=f(W[:, gcols]),
                cs64=f(cs64), ident=np.eye(128, dtype=np.float32)), cols, gcols

def rw_params(inp, core):
    b, q = core // 4, core % 4
    f = lambda a: np.ascontiguousarray(a, dtype=np.float32)
    chs = 192*q + np.arange(192)
    def blk2(v):
        o = np.zeros((128,2), np.float32); o[:,0] = v[:128]; o[:64,1] = v[128:]; return o
    mu = inp['ev_mu'][0]
    mub = np.zeros((128,8), np.float32)
    for j, base in enumerate([0, 768, 1536]):
        m2 = blk2(mu[base+chs]); mub[:, 2*j] = m2[:,0]; mub[:, 2*j+1] = m2[:,1]
    mub[:, 6] = mu[2304:2432]; mub[:, 7] = mu[2432:2560]
    p = np.arange(128)
    lanem = np.stack([(p%4==0),(p%4==1),(p%4==2),(p%4==3),(p%2==0),(p%2==1)],axis=1).astype(np.float32)
    a0 = np.zeros((128,2,2), np.float32)
    for d in range(2): a0[:, d, :] = blk2(inp['ev_a0'][0][d][chs])
    w2 = np.concatenate([inp['ev_w2'][0][0][:, chs], inp['ev_w2'][0][1][:, chs]], axis=0)
    a2 = np.concatenate([inp['ev_a2'][0][0][:, chs], inp['ev_a2'][0][1][:, chs]], axis=0)
    w0 = np.stack([inp['ev_w0'][0][0][chs], inp['ev_w0'][0][1][chs]])[None]
    s = np.arange(128)[:,None]; t = np.arange(128)[None,:]
    strict = [(s<t), (s>t)]; incl = [(s<=t), (s>=t)]
    blk = lambda bs: (np.arange(128)[:,None]//bs == np.arange(128)[None,:]//bs)
    maskSI2 = np.stack([np.concatenate([strict[d], incl[d]],axis=1) for d in range(2)], axis=1).astype(np.float32)
    maskSI = np.stack([np.concatenate([strict[d] & blk(32), incl[d]],axis=1) for d in range(2)], axis=1).astype(np.float32)
    maskST = np.stack([np.stack([(strict[d] & blk(32)).T, (strict[d] & blk(64) & ~blk(32)).T, (strict[d] & ~blk(64)).T], axis=1) for d in range(2)], axis=1).astype(np.float32)
    cdec = -np.exp(-0.5)
    tri = np.stack([np.concatenate([incl[d], strict[d]],axis=1) for d in range(2)], axis=1).astype(np.float32)*cdec
    eh = np.zeros((128,2,4), np.float32); eh[:64,0,0]=1; eh[64:,0,1]=1; eh[:64,1,2]=1
    obd = np.zeros((128,128), np.float32); obd[:64,:64]=1; obd[64:,64:]=1
    return dict(p_mu=mub, p_lanem=lanem, p_k_k=blk2(inp['ev_k_k'][0][chs]), p_k_a=blk2(inp['ev_k_a'][0][chs]),
                p_r_k=blk2(inp['ev_r_k'][0].reshape(-1)[chs]), p_a0=a0, p_w2=f(w2), p_a2=f(a2), p_w0=f(w0),
                p_maskSI=f(maskSI), p_maskSI2=f(maskSI2), p_maskST=f(maskST), p_tri=f(tri), p_ehead=eh, p_ones_bd=obd,
                p_lnx_g=f(np.tile(inp['ev_lnx_g'][0][chs][None], (128,1))), p_lnx_b=f(np.tile(inp['ev_lnx_b'][0][chs][None], (128,1))))

def fn_params():
    f = lambda a: np.ascontiguousarray(a, dtype=np.float32)
    a = np.arange(128); b = np.arange(64)
    ang128 = 2*np.pi*np.outer(a,a)/128
    tw = 2*np.pi*np.outer(a, b)/8192
    ang64 = 2*np.pi*np.outer(b,b)/64
    fl1 = np.concatenate([np.cos(ang64), np.sin(ang64)], axis=0)
    fl2 = np.concatenate([-np.sin(ang64), np.cos(ang64)], axis=0)
    l = np.arange(256); ang256 = 2*np.pi*np.outer(l,l)/256
    c256 = np.cos(ang256).reshape(2,128,256).transpose(1,0,2); ns256 = (-np.sin(ang256)).reshape(2,128,256).transpose(1,0,2)
    return dict(f_c128=f(np.cos(ang128)), f_ns128=f(-np.sin(ang128)), f_twc=f(np.cos(tw)), f_tws=f(np.sin(tw)), f_fl1=f(fl1), f_fl2=f(fl2),
                f_c256=f(c256), f_ns256=f(ns256))


def gate_inputs(inp, b, layer):
    f = lambda a: np.ascontiguousarray(a, dtype=np.float32)
    sil = np.stack([inp['c'][b].reshape(8,128).T, inp['c_ctx'].reshape(8,128).T], axis=-1)
    return dict(sil_in=f(sil), adawg=f(inp['ada_w'][layer][:, 2048:3072]), adabr=f(inp['ada_b'][layer][2048:3072][None]),
                npostb=f(np.tile(inp['norm_post'][layer][None], (128,1))))
def l3_inputs(inp, core, z_full, x1_full):
    b, qtr = core // 4, core % 4
    f = lambda a: np.ascontiguousarray(a, dtype=np.float32)
    sl = slice(2048*qtr, 2048*(qtr+1))
    m = dict(zT=f(z_full[b][sl].T), x1=f(x1_full[b][sl]), wout=f(inp['od_w_out'][0]))
    m.update(gate_inputs(inp, b, 1))
    return m


def l2_inputs(inp, core, y_full):
    b, p = core // 4, core % 4
    f = lambda a: np.ascontiguousarray(a, dtype=np.float32)
    m = dict(xin=f(np.concatenate([inp['ctx'][b], inp['x'][b]], axis=0)), wout=f(inp['ev_w_out'][0]))
    if y_full is not None:
        m['yT'] = f(y_full[b].T)
    m.update(gate_inputs(inp, b, 0))
    m['adaw1'] = f(inp['ada_w'][1][:, :2048]); m['adab1'] = f(inp['ada_b'][1][:2048].reshape(16,128).T); m['normw1'] = f(inp['norm_pre'][1].reshape(8,128).T)
    cols = np.concatenate([off + 256*p + np.arange(256) for off in (0, 1024, 2048, 3072)])
    m['win1'] = f(inp['od_w_in'][0][:, cols])
    pos = np.arange(8192); row = pos//64; col = pos%64
    inv = 10000.0 ** (-np.arange(0, 64, 2, dtype=np.float32)/64)
    ang = np.concatenate([row[:,None]*inv[None], col[:,None]*inv[None]], axis=1).astype(np.float32)
    m['ropec'] = f(np.cos(ang)); m['ropes'] = f(np.sin(ang))
    m['ident'] = np.eye(128, dtype=np.float32)
    lg = inp['od_decay_logit'][0]
    logit = np.zeros((128,4), np.float32)
    for hl in range(2):
        for dr in range(2): logit[:, 2*hl+dr] = lg[dr][2*p+hl]
    j = np.arange(128)[:,None]; i = np.arange(128)[None,:]
    diffT = np.stack([(i-j)*np.ones((128,128)), (j-i)*np.ones((128,128))], axis=1)
    mask = np.stack([(i>=j), (j>i)], axis=1)
    posxi = np.stack([np.tile((np.arange(128)+1)[None], (128,1)), np.tile((128-np.arange(128))[None], (128,1))], axis=1)
    pze = np.stack([127-np.arange(128), np.arange(128)], axis=1)
    m.update(r_logit=logit, r_diffT=f(diffT), r_mask01T=f(mask), r_posxi=f(posxi), r_poszeta=f(pze))
    return m


def fused_inputs(inp, core):
    b, q = core // 4, core % 4
    f = lambda a: np.ascontiguousarray(a, dtype=np.float32)
    m, _, _ = l1_inputs(inp, core)
    m.update(rw_params(inp, core))
    m.update(fn_params())
    m2 = l2_inputs(inp, core, None)
    g0 = gate_inputs(inp, b, 0)
    g1 = gate_inputs(inp, b, 1)
    perm = np.concatenate([np.concatenate([192 * r + np.arange(192), 768 + 64 * r + np.arange(64)]) for r in range(4)])
    m['wout0p'] = f(inp['ev_w_out'][0][perm])
    m['adawg0'], m['adabr0'], m['npostb0'] = g0['adawg'], g0['adabr'], g0['npostb']
    m['adawg1'], m['adabr1'], m['npostb1'] = g1['adawg'], g1['adabr'], g1['npostb']
    m['wout1'] = f(inp['od_w_out'][0])
    for k_ in ('adaw1', 'adab1', 'normw1', 'win1', 'ropec', 'ropes', 'r_logit', 'r_diffT', 'r_mask01T', 'r_posxi', 'r_poszeta'):
        m[k_] = m2[k_]
    return m


def kernel(**inputs):
    inp = {k: np.asarray(v) for k, v in inputs.items()}
    cores = list(range(8))
    nc, kb = build_fused()
    kb.emit()
    maps = [fused_inputs(inp, core) for core in cores]
    res = run_bass_kernel_spmd(nc, maps, core_ids=cores).results
    return np.stack([res[0]['out'], res[4]['out']]).astype(np.float32)
```

```python
import numpy as np
import concourse.bass as bass
import concourse.mybir as mybir
from contextlib import ExitStack
from concourse.bass_utils import run_bass_kernel_spmd

F32 = mybir.dt.float32
BF16 = mybir.dt.bfloat16
ALU = mybir.AluOpType
AF = mybir.ActivationFunctionType

ENGS = ['pe', 'act', 'dve', 'pool', 'sp']
DBG = {}
NDSEM = 8


class Buf:
    __slots__ = ('name', 'w', 'r', 'excl', 'strict')

    def __init__(self, name='', excl=False, strict=False):
        self.name = name
        self.w = None
        self.r = {}
        self.excl = excl
        self.strict = strict


class Op:
    __slots__ = ('eng', 'fn', 'deps', 'signal', 'sig', 'sem', 'target', 'isdma', 'final', 'cc')


class _Rec:
    def __init__(self):
        self.call = None

    def __getattr__(self, name):
        def f(*a, **k):
            self.call = (name, a, k)
            return self
        return f


def _flat(xs):
    out = []
    for x in xs:
        if isinstance(x, (list, tuple)):
            out.extend(_flat(x))
        else:
            out.append(x)
    return out


class KB:
    def __init__(self, nc):
        self.nc = nc
        self.ops = {e: [] for e in ENGS}
        self.phase = Buf('phase')
        self.es = ExitStack()
        self.nops = 0

    def sb(self, name, shape, dtype, es=None):
        self.nalloc = getattr(self, 'nalloc', 0) + 1
        return (es or self.es).enter_context(self.nc.sbuf_tensor('s%d_%s' % (self.nalloc, name), list(shape), dtype))

    def ps(self, name, shape, dtype=F32, es=None):
        self.nalloc = getattr(self, 'nalloc', 0) + 1
        return (es or self.es).enter_context(self.nc.psum_tensor('p%d_%s' % (self.nalloc, name), list(shape), dtype))

    def op(self, eng, fn, reads=(), writes=(), dma=False, final=False, nophase=False, cc=False):
        dma = dma or cc
        o = Op()
        rec = _Rec()
        fn(rec)
        assert rec.call is not None
        o.eng = eng; o.fn = rec.call; o.isdma = dma; o.signal = dma; o.deps = []; o.sig = 0
        o.sem = None; o.target = 0; o.final = final; o.cc = cc
        reads = _flat(reads)
        writes = _flat(writes)
        for b in list(reads):
            if b.excl:
                reads.remove(b)
                if b not in writes:
                    writes.append(b)
        if not nophase:
            reads.append(self.phase)
        deps = {}
        sdeps = set()
        for b in reads:
            if b.w is not None:
                deps[id(b.w)] = b.w
                if b.strict:
                    sdeps.add(id(b.w))
        for b in writes:
            if b.w is not None:
                deps[id(b.w)] = b.w
                if b.strict:
                    sdeps.add(id(b.w))
            for r in b.r.values():
                deps[id(r)] = r
                if b.strict:
                    sdeps.add(id(r))
        for d in deps.values():
            if d is o:
                continue
            if d.isdma or dma or d.eng != eng or id(d) in sdeps or (eng != 'pe' and DBG.get('strict_all', True)):
                o.deps.append(d)
                d.signal = True
        key = ('dma', self.nops) if dma else eng
        for b in reads:
            b.r[key] = o
        for b in writes:
            b.w = o
            b.r = {}
        self.ops[eng].append(o)
        self.nops += 1
        return o

    def barrier(self, tile_ap):
        self.op('dve', lambda e: e.memset(tile_ap, 0.0), writes=[self.phase], nophase=True)

    def emit(self):
        nc = self.nc
        es = self.es
        csem = {}
        for e in ['pe', 'act', 'dve', 'pool']:
            csem[e] = es.enter_context(nc.semaphore('c_' + e))
        dsem = {}
        for e in ENGS:
            if any(o.isdma for o in self.ops[e]):
                dsem[e] = [es.enter_context(nc.semaphore('d_%s%d' % (e, i))) for i in range(NDSEM)]
        for e in ENGS:
            cnt = 0
            nd = 0
            hist = []
            for o in self.ops[e]:
                if o.cc:
                    self.ncc = getattr(self, 'ncc', 0) + 1
                    o.sem = es.enter_context(nc.semaphore('ccs%d' % self.ncc))
                    o.target = 1
                elif o.isdma:
                    slot = nd % NDSEM
                    o.sem = dsem[e][slot]
                    o.target = 16 * (nd // NDSEM + 1)
                    if nd >= NDSEM:
                        o.deps.append(hist[nd - NDSEM])
                    hist.append(o)
                    nd += 1
                elif o.signal:
                    cnt += 1
                    o.sig = cnt
                    o.sem = csem[e]
                    o.target = cnt
        finals = [o for e in ENGS for o in self.ops[e] if o.final]

        def run(engname):
            def body(e):
                waited = {}
                for o in self.ops[engname]:
                    need = {}
                    for d in o.deps:
                        k = id(d.sem)
                        if waited.get(k, 0) < d.target and need.get(k, (None, 0))[1] < d.target:
                            need[k] = (d.sem, d.target)
                    for k, (sem, tgt) in need.items():
                        e.wait_ge(sem, tgt)
                        waited[k] = tgt
                    nm_, a_, k_ = o.fn
                    ins = getattr(e, nm_)(*a_, **k_)
                    if o.signal:
                        ins.then_inc(o.sem, 16 if (o.isdma and not o.cc) else 1)
                if engname == 'sp':
                    for d in finals:
                        k = id(d.sem)
                        if waited.get(k, 0) < d.target:
                            e.wait_ge(d.sem, d.target)
                            waited[k] = d.target
            return body

        with nc.Block() as block:
            block.tensor(run('pe'))
            block.scalar(run('act'))
            block.vector(run('dve'))
            block.gpsimd(run('pool'))
            block.sync(run('sp'))


class Ring:
    def __init__(self, kb, name, shape, dtype, n, psum=False, es=None, strict=False):
        self.n = n
        self.i = 0
        self.slots = []
        for j in range(n):
            t = (kb.ps if psum else kb.sb)('%s%d' % (name, j), shape, dtype, es=es)
            b = Buf('%s%d' % (name, j), excl=psum, strict=strict)
            self.slots.append((t, b))
            if not psum and DBG.get('init_rings', True):
                kb.op('pool', lambda e, t=t: e.memset(t[:], 0.0), writes=[b])

    def next(self):
        s = self.slots[self.i % self.n]
        self.i += 1
        return s


D = 1024
L = 8192
LC = 256
LT = L + LC
NCH = LT // 128
EPS = 1e-6
GN_EPS = 64e-5
FM_BLOCKS = {'r01': (0, 128), 'r2': (128, 64), 'k01': (192, 128), 'k2': (320, 64),
             'v01': (384, 128), 'v2': (512, 64), 'wl': (576, 128), 'al': (704, 128), 'four': (832, 64)}
NFM = 896
TILES = [(0, 256)] + [(256 + 512 * i, 512) for i in range(16)]


def dram_in(nc, name, shape, dtype=F32):
    return nc.dram_tensor(name, list(shape), dtype, kind="ExternalInput").ap()


def dram_out(nc, name, shape, dtype=F32):
    return nc.dram_tensor(name, list(shape), dtype, kind="ExternalOutput").ap()


def dram_tmp(nc, name, shape, dtype=F32, debug=False):
    return nc.dram_tensor(name, list(shape), dtype, kind="ExternalOutput" if debug else "Internal").ap()


def load_consts(kb, items, eng='sp'):
    bufs = []
    for t, src in items:
        b = Buf()
        kb.op(eng, (lambda e, t=t, src=src: e.dma_start(out=t[:], in_=src)), writes=[b], dma=True)
        bufs.append(b)
    return bufs


def adaln_vectors(kb, es, nc, sil_in, adaw, adab, normw, ncolblk, mod_tile=None):
    sil = kb.sb('sil', [128, 8, 2], F32, es)
    silb = Buf(strict=True)
    kb.op('sp', lambda e: e.dma_start(out=sil[:], in_=sil_in), writes=[silb], dma=True)
    kb.op('act', lambda e: e.activation(out=sil[:], in_=sil[:], func=AF.Silu), reads=[silb], writes=[silb])
    adb = kb.sb('adb', [128, ncolblk], F32, es)
    adbb = Buf(strict=True)
    kb.op('sp', lambda e: e.dma_start(out=adb[:], in_=adab), writes=[adbb], dma=True)
    mod = mod_tile if mod_tile is not None else kb.sb('mod', [128, ncolblk, 2], F32, es)
    modb = Buf(strict=True)
    psm = kb.ps('psmod', [128, ncolblk, 2], F32, es)
    psb = Buf(excl=True)
    wt = kb.sb('adaw', [128, 8, ncolblk * 128], F32, es)
    wb = Buf()
    for k in range(8):
        kb.op('sp', lambda e, k=k: e.dma_start(out=wt[:, k, :], in_=adaw[k * 128:(k + 1) * 128, :]), writes=[wb], dma=True)
    for cb in range(ncolblk):
        for k in range(8):
            kb.op('pe', lambda e, k=k, cb=cb: e.matmul(psm[:, cb, :], lhsT=wt[:, k, cb * 128:(cb + 1) * 128],
                                                      rhs=sil[:, k, :], start=(k == 0), stop=(k == 7)),
                  reads=[wb, silb], writes=[psb])
    for n in range(2):
        kb.op('dve', lambda e, n=n: e.tensor_tensor(out=mod[:, :, n], in0=psm[:, :, n], in1=adb[:, :], op=ALU.add),
              reads=[psb, adbb], writes=[modb])
    return mod, modb


def phase_proj(kb, nc, es, xin, Wsrc_list, G, shiftv, Gb, ident_bf, identb, emit_tile, nmod=2, get_x=None, shift_off=0):
    xring = Ring(kb, 'xt', [128, D], F32, 3, es=es)
    xnring = Ring(kb, 'xn', [128, D], BF16, 2, es=es)
    stat = Ring(kb, 'stat', [128, 4], F32, 4, es=es, strict=True)
    junk = kb.sb('junk', [128, D], BF16, es)
    junkb = Buf()
    hring = Ring(kb, 'hT', [128, 8, 512], BF16, 2, es=es)
    ptr = Ring(kb, 'ptr', [128, 8, 128], BF16, 2, psum=True, es=es)
    for (t0, T) in TILES:
        n = 1 if t0 < LC else 0
        hT, hb = hring.next()
        for sub in range(T // 128):
            r0 = t0 + sub * 128
            if get_x is not None:
                xt, xb = get_x(r0)
            else:
                xt, xb = xring.next()
                kb.op('sp', lambda e, xt=xt, r0=r0: e.dma_start(out=xt[:], in_=xin[r0:r0 + 128, :]), writes=[xb], dma=True)
            st, sb_ = stat.next()
            kb.op('act', lambda e, xt=xt, st=st: e.activation(out=junk[:], in_=xt[:], func=AF.Square, accum_out=st[:, 0:1]),
                  reads=[xb], writes=[junkb, sb_])
            kb.op('act', lambda e, st=st: e.activation(out=st[:, 1:2], in_=st[:, 0:1], func=AF.Sqrt, scale=1.0 / D, bias=Gb['eps'][:, 0:1]),
                  reads=[sb_, Gb['epsb']], writes=[sb_])
            kb.op('dve', lambda e, st=st: e.reciprocal(out=st[:, 2:3], in_=st[:, 1:2]), reads=[sb_], writes=[sb_])
            xn, xnb = xnring.next()
            kb.op('dve', lambda e, xn=xn, xt=xt, st=st: e.tensor_scalar(out=xn[:], in0=xt[:], scalar1=st[:, 2:3], scalar2=None, op0=ALU.mult),
                  reads=[xb, sb_], writes=[xnb])
            pt, pb = ptr.next()
            for k in range(8):
                kb.op('pe', lambda e, pt=pt, xn=xn, k=k: e.transpose(out=pt[:, k, :], in_=xn[:, k * 128:(k + 1) * 128], identity=ident_bf[:]),
                      reads=[xnb, identb], writes=[pb])
            for k in range(8):
                if k % 2 == 0:
                    kb.op('act', lambda e, pt=pt, hT=hT, k=k, sub=sub, n=n: e.activation(
                        out=hT[:, k, sub * 128:(sub + 1) * 128], in_=pt[:, k, :], func=AF.Identity,
                        scale=G[:, k, n:n + 1], bias=shiftv[:, shift_off + k, n:n + 1]), reads=[pb, Gb['G']], writes=[hb])
                else:
                    kb.op('dve', lambda e, pt=pt, hT=hT, k=k, sub=sub, n=n: e.tensor_scalar(
                        out=hT[:, k, sub * 128:(sub + 1) * 128], in0=pt[:, k, :], scalar1=G[:, k, n:n + 1],
                        scalar2=shiftv[:, shift_off + k, n:n + 1], op0=ALU.mult, op1=ALU.add), reads=[pb, Gb['G']], writes=[hb])
        emit_tile(t0, T, hT, hb)


def load_weight_bf16(kb, es, nc, name, src, ncols, es_keep):
    wbf = kb.sb(name, [128, 8, ncols], BF16, es_keep)
    wb = Buf()
    stg = Ring(kb, name + '_stg', [128, ncols], F32, 2, es=es)
    for k in range(8):
        s, sbuf_ = stg.next()
        kb.op('sp', lambda e, s=s, k=k: e.dma_start(out=s[:], in_=src[k * 128:(k + 1) * 128, :]), writes=[sbuf_], dma=True)
        eng = 'dve' if k % 2 == 0 else 'pool'
        kb.op(eng, lambda e, s=s, k=k: e.tensor_copy(out=wbf[:, k, :], in_=s[:]), reads=[sbuf_], writes=[wb])
    return wbf, wb


RW_PARAMS = {'mu': [128, 8], 'lanem': [128, 6], 'k_k': [128, 2], 'k_a': [128, 2], 'r_k': [128, 2], 'a0': [128, 2, 2],
             'w2': [128, 192], 'a2': [128, 192], 'w0': [1, 2, 192], 'maskSI': [128, 2, 256], 'maskSI2': [128, 2, 256], 'maskST': [128, 2, 3, 128],
             'tri': [128, 2, 256], 'ehead': [128, 2, 4], 'ones_bd': [128, 128], 'lnx_g': [128, 192], 'lnx_b': [128, 192]}


def part1(kb, nc, IN, YD, C):
    es = kb.es
    debug = False
    stop_after = None
    xin, sil_in, adaw, adab, normw, wfm, wg, cs64 = (IN[k_] for k_ in ('xin', 'sil_in', 'adaw', 'adab', 'normw', 'wfm', 'wg', 'cs64'))
    U = {nm: dram_tmp(nc, 'U_' + nm, [sz, LT], F32, debug) for nm, (off, sz) in FM_BLOCKS.items() if nm != 'four'}
    GT = dram_tmp(nc, 'GT', [LT, 256], F32, debug)
    HT = dram_tmp(nc, 'HT', [LT, 128], F32, debug)
    bar, ident_f, identfb, ident_bf, identb, epsT, epsb = C['bar'], C['ident_f'], C['identfb'], C['ident_bf'], C['identb'], C['epsT'], C['epsb']
    with ExitStack() as pa:
        mod, modb = adaln_vectors(kb, pa, nc, sil_in, adaw, adab, normw, 16)
        nw = kb.sb('nw', [128, 8], F32, pa)
        nwb, = load_consts(kb, [(nw, normw)])
        G = kb.sb('G', [128, 8, 2], F32, pa)
        Gbuf = Buf(strict=True)
        for n in range(2):
            kb.op('dve', lambda e, n=n: e.scalar_tensor_tensor(out=G[:, :, n], in0=mod[:, 8:16, n], scalar=1.0, in1=nw[:, :],
                                                                op0=ALU.add, op1=ALU.mult), reads=[modb, nwb], writes=[Gbuf])
        Gb = {'G': Gbuf, 'eps': epsT, 'epsb': epsb}
        shiftv = mod
        wbf, wbb = load_weight_bf16(kb, pa, nc, 'wfm_bf', wfm, NFM, pa)
        wgb, wgbb = load_weight_bf16(kb, pa, nc, 'wg_bf', wg, 256, pa)
        cs = kb.sb('cs64', [64, 128], F32, pa)
        csb, = load_consts(kb, [(cs, cs64)])
        pmm = Ring(kb, 'pmm', [128, 512], F32, 3, psum=True, es=pa)
        pg = Ring(kb, 'pg', [128, 512], F32, 1, psum=True, es=pa)
        ustg = Ring(kb, 'ustg', [128, 512], F32, 4, es=pa)
        gstg = Ring(kb, 'gstg', [128, 256], F32, 3, es=pa)
        hstg = Ring(kb, 'hstg', [128, 128], F32, 3, es=pa)
        fstg = Ring(kb, 'fstg', [64, 512], F32, 2, es=pa)
        cnt = [0]

        def emit_tile(t0, T, hT, hb):
            for nm, (off, sz) in FM_BLOCKS.items():
                ps, psb = pmm.next()
                for k in range(8):
                    kb.op('pe', lambda e, ps=ps, k=k, off=off, sz=sz: e.matmul(ps[0:sz, 0:T], lhsT=wbf[:, k, off:off + sz], rhs=hT[:, k, 0:T],
                                                                               start=(k == 0), stop=(k == 7)), reads=[wbb, hb], writes=[psb])
                if nm == 'four':
                    fs, fsb = fstg.next()
                    kb.op('act', lambda e, fs=fs, ps=ps: e.activation(out=fs[:, 0:T], in_=ps[0:64, 0:T], func=AF.Copy), reads=[psb], writes=[fsb])
                    for sub in range(T // 128):
                        pgt, pgb = pg.next()
                        kb.op('pe', lambda e, pgt=pgt, fs=fs, sub=sub: e.matmul(pgt[:, 0:128], lhsT=fs[:, sub * 128:(sub + 1) * 128], rhs=cs[:, :],
                                                                                  start=True, stop=True), reads=[fsb, csb], writes=[pgb])
                        hs, hsb = hstg.next()
                        kb.op('dve', lambda e, hs=hs, pgt=pgt: e.tensor_copy(out=hs[:], in_=pgt[:, 0:128]), reads=[pgb], writes=[hsb])
                        r0 = t0 + sub * 128
                        kb.op('pool', lambda e, hs=hs, r0=r0: e.dma_start(out=HT[r0:r0 + 128, :], in_=hs[:]), reads=[hsb], dma=True)
                else:
                    us, usb = ustg.next()
                    cnt[0] += 1
                    if cnt[0] % 2 == 0:
                        kb.op('act', lambda e, us=us, ps=ps, sz=sz: e.activation(out=us[0:sz, 0:T], in_=ps[0:sz, 0:T], func=AF.Copy), reads=[psb], writes=[usb])
                    else:
                        kb.op('dve', lambda e, us=us, ps=ps, sz=sz: e.tensor_copy(out=us[0:sz, 0:T], in_=ps[0:sz, 0:T]), reads=[psb], writes=[usb])
                    kb.op('pool', lambda e, us=us, nm=nm, sz=sz: e.dma_start(out=U[nm][:, t0:t0 + T], in_=us[0:sz, 0:T]), reads=[usb], dma=True)
            for sub in range(T // 128):
                pgt, pgb = pg.next()
                for k in range(8):
                    kb.op('pe', lambda e, pgt=pgt, k=k, sub=sub: e.matmul(pgt[:, 0:256], lhsT=hT[:, k, sub * 128:(sub + 1) * 128], rhs=wgb[:, k, :],
                                                                           start=(k == 0), stop=(k == 7)), reads=[wgbb, hb], writes=[pgb])
                gs, gsb = gstg.next()
                kb.op('act', lambda e, gs=gs, pgt=pgt: e.activation(out=gs[:], in_=pgt[:, 0:256], func=AF.Silu), reads=[pgb], writes=[gsb])
                r0 = t0 + sub * 128
                kb.op('pool', lambda e, gs=gs, r0=r0: e.dma_start(out=GT[r0:r0 + 128, :], in_=gs[:]), reads=[gsb], dma=True)

        phase_proj(kb, nc, pa, xin, None, G, shiftv, Gb, ident_bf, identb, emit_tile)
        kb.barrier(bar[:])
    BLm = ['r01', 'r2', 'k01', 'k2', 'v01', 'v2', 'wl', 'al']
    S = {nm: dram_tmp(nc, 'S_' + nm, [FM_BLOCKS[nm][1], LT], F32, debug) for nm in BLm}
    with ExitStack() as pm:
        mu_t = kb.sb('mu_m', [128, 8], F32, pm)
        lan_t = kb.sb('lan_m', [128, 6], F32, pm)
        mub_, lanb_ = load_consts(kb, [(mu_t, IN['p_mu']), (lan_t, IN['p_lanem'])])
        cf = kb.sb('coef_m', [128, 8, 7], F32, pm)
        cfb = Buf(strict=True)
        kb.op('dve', lambda e: e.tensor_scalar(out=cf[:, :, 0], in0=mu_t[:, :], scalar1=-1.0, scalar2=1.0, op0=ALU.mult, op1=ALU.add), reads=[mub_], writes=[cfb])
        for l in range(6):
            kb.op('dve', lambda e, l=l: e.tensor_scalar(out=cf[:, :, 1 + l], in0=mu_t[:, :], scalar1=lan_t[:, l:l + 1], scalar2=None, op0=ALU.mult), reads=[mub_, lanb_], writes=[cfb])
        wr = Ring(kb, 'winm', [128, 640], F32, 4, es=pm)
        so = Ring(kb, 'smo', [128, 512], F32, 4, es=pm)
        for (t0, T) in TILES:
            isctx = t0 < LC
            seg_lo, seg_hi = (0, LC) if isctx else (LC, LT)
            halo = 1 if isctx else 64
            lo = max(t0 - halo, seg_lo)
            hi = min(t0 + T + halo, seg_hi)
            for bi, nm in enumerate(BLm):
                sz = FM_BLOCKS[nm][1]
                w_, wb_ = wr.next()
                if lo > t0 - halo or hi < t0 + T + halo:
                    kb.op('pool', lambda e, w_=w_: e.memset(w_[:], 0.0), writes=[wb_])
                kb.op('sp', lambda e, w_=w_, nm=nm, sz=sz, lo=lo, hi=hi, t0=t0: e.dma_start(out=w_[0:sz, 64 + lo - t0:64 + hi - t0], in_=U[nm][:, lo:hi]), writes=[wb_], dma=True)
                o_, ob_ = so.next()
                kb.op('dve', lambda e, o_=o_, w_=w_, sz=sz, bi=bi, T=T: e.tensor_scalar(out=o_[0:sz, 0:T], in0=w_[0:sz, 64:64 + T], scalar1=cf[0:sz, bi, 0:1], scalar2=None, op0=ALU.mult),
                      reads=[wb_, cfb], writes=[ob_])
                terms = [(5, -1, None), (6, +1, None)] if isctx else [(1, -1, 'L'), (2, +1, 'R'), (3, -64, None), (4, +64, None)]
                for (ci, off, kind) in terms:
                    def f(e, o_=o_, w_=w_, sz=sz, bi=bi, ci=ci, off=off, kind=kind, T=T):
                        if kind is None:
                            oo = o_[0:sz, 0:T]
                            ii = w_[0:sz, 64 + off:64 + T + off]
                        else:
                            ov = o_[0:sz, 0:T].rearrange("p (r c) -> p r c", c=64)
                            iv = w_[0:sz, 64:64 + T].rearrange("p (r c) -> p r c", c=64)
                            if kind == 'L':
                                oo, ii = ov[:, :, 1:64], iv[:, :, 0:63]
                            else:
                                oo, ii = ov[:, :, 0:63], iv[:, :, 1:64]
                        return e.scalar_tensor_tensor(out=oo, in0=ii, scalar=cf[0:sz, bi, ci:ci + 1], in1=oo, op0=ALU.mult, op1=ALU.add)
                    kb.op('dve', f, reads=[wb_, cfb, ob_], writes=[ob_])
                kb.op('pool', lambda e, o_=o_, nm=nm, sz=sz, t0=t0, T=T: e.dma_start(out=S[nm][:, t0:t0 + T], in_=o_[0:sz, 0:T]), reads=[ob_], dma=True)
        kb.barrier(bar[:])
    if stop_after == 'A':
        return
    with ExitStack() as pb:
        P = {k_: IN['p_' + k_] for k_ in RW_PARAMS}
        YR = YD
        if stop_after != 'skipB':
            phase_rwkv(kb, nc, pb, S, GT, YR, P, ident_f, identfb, ident_bf, identb)
        kb.barrier(bar[:])
    if DBG.get('skip_fnet'):
        return
    PF = {k_: IN['f_' + k_] for k_ in FN_PARAMS}
    YF = YD
    ZS = dram_tmp(nc, 'ZS', [2, 128, 64, 128], F32, False)
    phase_fnet(kb, nc, HT, GT, YF, ZS, PF, bar)
    return


DUMPS = {}


def dump(kb, nc, name, ap, shape, bufs, dt=F32):
    if name in DUMPS:
        return
    t = nc.dram_tensor('dbg_' + name, list(shape), dt, kind="ExternalOutput").ap()
    DUMPS[name] = t
    kb.op('sp', lambda e: e.dma_start(out=t, in_=ap), reads=bufs, dma=True, final=True)

CHUNK_ORDER = {0: list(range(NCH)), 1: [1, 0] + list(range(NCH - 1, 1, -1))}


def phase_rwkv(kb, nc, es, U, GT, YR, P, ident_f, identfb, ident_bf, identb):
    def sbt(name, shape, dt=F32):
        return kb.sb(name, shape, dt, es)

    def const(name, shape, src, dt=F32):
        t = sbt(name, shape, dt)
        b, = load_consts(kb, [(t, src)])
        return t, b

    mu, mub = const('mu', [128, 8], P['mu'])
    lanem, lanemb = const('lanem', [128, 6], P['lanem'])
    kkp, kkpb = const('kkp', [128, 2], P['k_k'])
    kap, kapb = const('kap', [128, 2], P['k_a'])
    rkp, rkpb = const('rkp', [128, 2], P['r_k'])
    a0p, a0pb = const('a0p', [128, 2, 2], P['a0'])
    w2s, w2sb = const('w2s', [128, 192], P['w2'])
    a2s, a2sb = const('a2s', [128, 192], P['a2'])
    w0r, w0rb = const('w0r', [1, 2, 192], P['w0'])
    msi, msib = const('msi', [128, 2, 256], P['maskSI'])
    mst, mstb = const('mst', [128, 2, 3, 128], P['maskST'])
    msi2, msi2b = const('msi2', [128, 2, 256], P['maskSI2'])
    tri, trib = const('tri', [128, 2, 256], P['tri'])
    ehd, ehdb = const('ehd', [128, 2, 4], P['ehead'])
    obd, obdb = const('obd', [128, 128], P['ones_bd'])
    lng, lngb = const('lng', [128, 192], P['lnx_g'])
    lnb, lnbb = const('lnb', [128, 192], P['lnx_b'])
    ones1 = sbt('ones1', [1, 128])
    ones1b = Buf()
    kb.op('dve', lambda e: e.memset(ones1[:], 1.0), writes=[ones1b])
    coef = sbt('coef', [128, 8, 7])
    coefb = Buf(strict=True)
    kb.op('dve', lambda e: e.tensor_scalar(out=coef[:, :, 0], in0=mu[:, :], scalar1=-1.0, scalar2=1.0, op0=ALU.mult, op1=ALU.add),
          reads=[mub], writes=[coefb])
    for l in range(6):
        kb.op('dve', lambda e, l=l: e.tensor_scalar(out=coef[:, :, 1 + l], in0=mu[:, :], scalar1=lanem[:, l:l + 1], scalar2=None, op0=ALU.mult),
              reads=[mub, lanemb], writes=[coefb])
    omka = sbt('omka', [128, 2])
    omkab = Buf(strict=True)
    kb.op('dve', lambda e: e.tensor_scalar(out=omka[:], in0=kap[:], scalar1=-1.0, scalar2=1.0, op0=ALU.mult, op1=ALU.add), reads=[kapb], writes=[omkab])
    gneps = sbt('gneps', [128, 1])
    gnepsb = Buf()
    kb.op('dve', lambda e: e.memset(gneps[:], GN_EPS), writes=[gnepsb])

    yacc = sbt('yacc', [128, NCH, 192])
    yaccb = [Buf() for _ in range(NCH)]
    kb.op('pool', lambda e: e.memset(yacc[:], 0.0), writes=yaccb)
    ST = sbt('ST', [128, 2, 2, 64])
    STbf = sbt('STbf', [128, 2, 2, 64], BF16)
    STb = [[Buf() for _ in range(3)] for _ in range(2)]

    CR, HR = {}, {}
    for d_ in range(2):
        n_ = 'd%d' % d_
        CR[d_] = dict(
            winr=None, smr=Ring(kb, 'smix' + n_, [128, 8, 128], F32, 2, es=es),
            t32=Ring(kb, 't32' + n_, [128, 256], F32, 9, es=es), kkr=Ring(kb, 'kkt' + n_, [128, 2, 128], F32, 1, es=es),
            vtr=Ring(kb, 'vt32' + n_, [128, 192], F32, 2, es=es), vtbr=Ring(kb, 'vtbf' + n_, [128, 192], BF16, 2, es=es),
            sgr=Ring(kb, 'sg' + n_, [128, 192], F32, 1, es=es), e1r=Ring(kb, 'e1' + n_, [128, 2, 256], F32, 2, es=es),
            e2r=Ring(kb, 'e2' + n_, [128, 2, 128], F32, 1, es=es), arr=Ring(kb, 'ar' + n_, [128, 2, 256], BF16, 2, es=es),
            btr=Ring(kb, 'bt' + n_, [128, 2, 128], BF16, 2, es=es), ktr=Ring(kb, 'kt' + n_, [128, 2, 128], BF16, 2, es=es),
            bktr=Ring(kb, 'bkt' + n_, [128, 2, 192], BF16, 2, es=es), bcr=Ring(kb, 'bc' + n_, [128, 4], F32, 2, es=es, strict=True))
        for h_ in range(3):
            n2 = 'd%dh%d' % (d_, h_)
            HR[(d_, h_)] = dict(
                mm1r=Ring(kb, 'mm1' + n2, [128, 256], BF16, 2, es=es), mm2r=Ring(kb, 'mm2' + n2, [128, 256], BF16, 2, es=es),
                xpr=Ring(kb, 'xp' + n2, [128, 256], BF16, 4, es=es), xtr=Ring(kb, 'xt' + n2, [128, 128], BF16, 4, es=es),
                ivr=Ring(kb, 'iv' + n2, [128, 128], BF16, 8, es=es), tmr=Ring(kb, 'tm' + n2, [128, 128], BF16, 2, es=es),
                w1r=Ring(kb, 'w1' + n2, [128, 64], BF16, 2, es=es), utr=Ring(kb, 'ut' + n2, [128, 64], BF16, 2, es=es),
                tsr=Ring(kb, 'ts' + n2, [128, 64], F32, 3, es=es))
    jkr = Ring(kb, 'jk', [128, 64], F32, 2, es=es)
    gtr = Ring(kb, 'gt', [128, 192], F32, 2, es=es)
    fnr = Ring(kb, 'fn', [128, 192], F32, 2, es=es)
    str_ = Ring(kb, 'stt', [128, 8], F32, 6, es=es, strict=True)
    pprep = Ring(kb, 'pprep', [128, 512], F32, 1, psum=True, es=es)
    ptrb = kb.ps('ptrb', [128, 1024], BF16, es)
    ptrbufs = [Buf(excl=True)] * 4
    ptrc = [0]
    pgram = Ring(kb, 'pgram', [128, 512], F32, 1, psum=True, es=es)
    pinv = Ring(kb, 'pinv', [128, 512], F32, 3, psum=True, es=es)
    pseq = kb.ps('pseq', [128, 512], F32, es)
    pseq2 = kb.ps('pseq2', [128, 512], F32, es)
    pseqb = [Buf(excl=True)] * 4 + [Buf(excl=True)] * 4
    pseqt = [pseq] * 4 + [pseq2] * 4
    pseqc = [0]

    def pseq_next():
        i = pseqc[0] % 8
        pseqc[0] += 1
        return pseqt[i][:, (i % 4) * 64:(i % 4 + 1) * 64], pseqb[i], i

    def ptr_next():
        i = ptrc[0] % 4
        ptrc[0] += 1
        return i, ptrbufs[i]

    blkname = [('r01', 'k01', 'v01'), ('r2', 'k2', 'v2')]
    BL = ['r01', 'r2', 'k01', 'k2', 'v01', 'v2', 'wl', 'al']
    BI = {n: i for i, n in enumerate(BL)}
    BSZ = {n: FM_BLOCKS[n][1] for n in BL}
    alt = [0]

    def ew(fn, reads, writes):
        alt[0] += 1
        r_ = _Rec()
        fn(r_)
        stt_ = r_.call[0] == 'scalar_tensor_tensor'
        return kb.op('dve' if (alt[0] % 3 or stt_) else 'pool', fn, reads=reads, writes=writes)

    done = {}
    inflight = {}
    SMB = {}

    def chunk_body(d, c):
        winr, smr, t32, kkr, vtr, vtbr, sgr, e1r, e2r, arr, btr, ktr, bktr, bcr = (CR[d][k_] for k_ in (
            'winr', 'smr', 't32', 'kkr', 'vtr', 'vtbr', 'sgr', 'e1r', 'e2r', 'arr', 'btr', 'ktr', 'bktr', 'bcr'))
        isctx = c < 2
        t0 = c * 128
        seg_lo, seg_hi = (0, LC) if isctx else (LC, LT)
        halo = 1 if isctx else 64
        lo = max(t0 - halo, seg_lo)
        hi = min(t0 + 128 + halo, seg_hi)
        sm, smb0 = smr.next()
        smb = SMB.setdefault(id(smb0), [smb0] + [Buf() for _ in range(7)])
        for nm in BL:
            sz = BSZ[nm]
            kb.op('sp', lambda e, sm=sm, nm=nm, sz=sz, t0=t0: e.dma_start(out=sm[0:sz, BI[nm], :], in_=U[nm][:, t0:t0 + 128]), writes=[smb[BI[nm]]], dma=True)
        if DBG.get('stage', 99) < 2:
            return

        def S(nm):
            return sm[0:BSZ[nm], BI[nm], :]

        yield
        kkt, kktb = kkr.next()
        for bj, (rn, kn, vn) in enumerate(blkname if not DBG.get('skip_kk') else []):
            sz = BSZ[kn]
            q, qb = t32.next()
            kb.op('dve', lambda e, q=q, kn=kn, sz=sz, bj=bj: e.tensor_scalar(out=q[0:sz, 0:128], in0=S(kn), scalar1=kkp[0:sz, bj:bj + 1], scalar2=None, op0=ALU.mult),
                  reads=[smb, kkpb], writes=[qb])
            kb.op('pool', lambda e, q=q, sz=sz: e.tensor_tensor(out=q[0:sz, 128:256], in0=q[0:sz, 0:128], in1=q[0:sz, 0:128], op=ALU.mult), reads=[qb], writes=[qb])
            pp, ppb = pprep.next()
            kb.op('pe', lambda e, pp=pp, q=q, sz=sz: e.matmul(pp[0:sz, 0:128], lhsT=obd[0:sz, 0:sz], rhs=q[0:sz, 128:256], start=True, stop=True),
                  reads=[qb, obdb], writes=[ppb])
            nr, nrb = t32.next()
            kb.op('act', lambda e, nr=nr, pp=pp, sz=sz: e.activation(out=nr[0:sz, 0:128], in_=pp[0:sz, 0:128], func=AF.Sqrt), reads=[ppb], writes=[nrb])
            kb.op('dve', lambda e, nr=nr, sz=sz: e.tensor_scalar(out=nr[0:sz, 0:128], in0=nr[0:sz, 0:128], scalar1=1e-12, scalar2=None, op0=ALU.max), reads=[nrb], writes=[nrb])
            kb.op('dve', lambda e, nr=nr, sz=sz: e.reciprocal(out=nr[0:sz, 128:256], in_=nr[0:sz, 0:128]), reads=[nrb], writes=[nrb])
            kb.op('dve', lambda e, nr=nr, q=q, sz=sz, bj=bj, kkt=kkt: e.tensor_tensor(out=kkt[0:sz, bj, :], in0=q[0:sz, 0:128], in1=nr[0:sz, 128:256], op=ALU.mult),
                  reads=[nrb, qb], writes=[kktb])
        vt, vtb = vtr.next()
        vtbf, vtbfb = vtbr.next()
        pp, ppb = pprep.next()
        for bj, (rn, kn, vn) in enumerate(blkname if not DBG.get('skip_vt') else []):
            sz = BSZ[vn]
            kb.op('pe', lambda e, pp=pp, vn=vn, sz=sz, bj=bj: e.transpose(out=pp[:, bj * 128:bj * 128 + sz], in_=S(vn), identity=ident_f[0:sz, 0:sz]),
                  reads=[smb, identfb], writes=[ppb])
        kb.op('act', lambda e, pp=pp, vt=vt: e.activation(out=vt[:, :], in_=pp[:, 0:192], func=AF.Copy), reads=[ppb], writes=[vtb])
        kb.op('pool', lambda e, vt=vt, vtbf=vtbf: e.tensor_copy(out=vtbf[:, :], in_=vt[:, :]), reads=[vtb], writes=[vtbfb])

        if DBG.get('stage', 99) < 3:
            return
        yield
        th, thb = t32.next()
        kb.op('act', lambda e, th=th: e.activation(out=th[d * 64:(d + 1) * 64, 0:128], in_=sm[d * 64:(d + 1) * 64, BI['wl'], :], func=AF.Tanh),
              reads=[smb], writes=[thb])
        pp, ppb = pprep.next()
        kb.op('pe', lambda e, pp=pp, th=th: e.matmul(pp[:, 0:192], lhsT=th[d * 64:(d + 1) * 64, 0:128], rhs=w2s[d * 64:(d + 1) * 64, :], start=True, stop=False),
              reads=[thb, w2sb], writes=[ppb])
        kb.op('pe', lambda e, pp=pp: e.matmul(pp[:, 0:192], lhsT=ones1[0:1, :], rhs=w0r[0:1, d, :], start=False, stop=True),
              reads=[ones1b, w0rb], writes=[ppb])
        sg, sgb = sgr.next()
        kb.op('act', lambda e, sg=sg, pp=pp: e.activation(out=sg[:, :], in_=pp[:, 0:192], func=AF.Sigmoid), reads=[ppb], writes=[sgb])
        e1, e1b = e1r.next()
        e2, e2b = e2r.next()
        for bj in range(2):
            sz = 128 if bj == 0 else 64
            pp, ppb = pprep.next()
            kb.op('pe', lambda e, pp=pp, sg=sg, bj=bj, sz=sz: e.matmul(pp[0:sz, 0:256], lhsT=sg[:, bj * 128:bj * 128 + sz], rhs=tri[:, d, :], start=True, stop=True),
                  reads=[sgb, trib], writes=[ppb])
            kb.op('act', lambda e, pp=pp, e1=e1, bj=bj, sz=sz: e.activation(out=e1[0:sz, bj, :], in_=pp[0:sz, 0:256], func=AF.Exp), reads=[ppb], writes=[e1b])
            kb.op('act', lambda e, pp=pp, e2=e2, bj=bj, sz=sz: e.activation(out=e2[0:sz, bj, :], in_=pp[0:sz, 0:128], func=AF.Exp, scale=-1.0), reads=[ppb], writes=[e2b])
        if DBG.get('stage', 99) < 4:
            return
        yield
        ar, arb = arr.next()
        bt, btb = btr.next()
        kt, ktb = ktr.next()
        bc, bcb = bcr.next()
        pbon, pbonb, _ = pseq_next()
        for bj, (rn, kn, vn) in enumerate(blkname):
            sz = BSZ[kn]
            co = bj * 128
            pp, ppb = pprep.next()
            kb.op('pe', lambda e, pp=pp, sz=sz, co=co: e.matmul(pp[0:sz, 0:128], lhsT=a2s[d * 64:(d + 1) * 64, co:co + sz], rhs=sm[d * 64:(d + 1) * 64, BI['al'], :], start=True, stop=True),
                  reads=[a2sb, smb], writes=[ppb])
            av, avb = t32.next()
            kb.op('act', lambda e, av=av, pp=pp, sz=sz, bj=bj: e.activation(out=av[0:sz, 0:128], in_=pp[0:sz, 0:128], func=AF.Sigmoid, bias=a0p[0:sz, d, bj:bj + 1]),
                  reads=[ppb, a0pb], writes=[avb])
            kv, kvb = t32.next()
            ew(lambda e, kv=kv, av=av, sz=sz, bj=bj: e.tensor_scalar(out=kv[0:sz, 0:128], in0=av[0:sz, 0:128], scalar1=kap[0:sz, bj:bj + 1], scalar2=omka[0:sz, bj:bj + 1], op0=ALU.mult, op1=ALU.add),
               [avb, kapb, omkab], [kvb])
            ew(lambda e, kv=kv, kn=kn, sz=sz: e.tensor_tensor(out=kv[0:sz, 0:128], in0=kv[0:sz, 0:128], in1=S(kn), op=ALU.mult), [kvb, smb], [kvb])
            ew(lambda e, kv=kv, kt=kt, e2=e2, sz=sz, bj=bj: e.tensor_tensor(out=kt[0:sz, bj, :], in0=kv[0:sz, 0:128], in1=e2[0:sz, bj, :], op=ALU.mult), [kvb, e2b], [ktb])
            ew(lambda e, av=av, e2=e2, sz=sz, bj=bj: e.tensor_tensor(out=av[0:sz, 128:256], in0=av[0:sz, 0:128], in1=e2[0:sz, bj, :], op=ALU.mult), [avb, e2b], [avb])
            ew(lambda e, av=av, bt=bt, kkt=kkt, sz=sz, bj=bj: e.tensor_tensor(out=bt[0:sz, bj, :], in0=av[0:sz, 128:256], in1=kkt[0:sz, bj, :], op=ALU.mult), [avb, kktb], [btb])
            ew(lambda e, ar=ar, kkt=kkt, e1=e1, sz=sz, bj=bj: e.scalar_tensor_tensor(out=ar[0:sz, bj, 0:128], in0=kkt[0:sz, bj, :], scalar=-1.0, in1=e1[0:sz, bj, 128:256], op0=ALU.mult, op1=ALU.mult),
               [kktb, e1b], [arb])
            ew(lambda e, ar=ar, rn=rn, e1=e1, sz=sz, bj=bj: e.tensor_tensor(out=ar[0:sz, bj, 128:256], in0=S(rn), in1=e1[0:sz, bj, 0:128], op=ALU.mult), [smb, e1b], [arb])
            ew(lambda e, kv=kv, rn=rn, sz=sz, bj=bj: e.scalar_tensor_tensor(out=kv[0:sz, 128:256], in0=S(rn), scalar=rkp[0:sz, bj:bj + 1], in1=kv[0:sz, 0:128], op0=ALU.mult, op1=ALU.mult),
               [smb, rkpb, kvb], [kvb])
            kb.op('pe', lambda e, kv=kv, sz=sz, bj=bj: e.matmul(pbon[:, 0:4], lhsT=kv[0:sz, 128:256], rhs=ehd[0:sz, bj, :], start=(bj == 0), stop=(bj == 1)),
                  reads=[kvb, ehdb], writes=[pbonb])
        kb.op('dve', lambda e, bc=bc: e.tensor_copy(out=bc[:, :], in_=pbon[:, 0:4]), reads=[pbonb], writes=[bcb])
        yield
        bkt, bktb = bktr.next()
        for wi, (src, srcb) in enumerate([(bt, btb), (kt, ktb)]):
            pi, pib = ptr_next()
            for bj in range(2):
                sz = 128 if bj == 0 else 64
                kb.op('pe', lambda e, src=src, pi=pi, bj=bj, sz=sz: e.transpose(out=ptrb[:, pi * 256 + bj * 128:pi * 256 + bj * 128 + sz], in_=src[0:sz, bj, :], identity=ident_bf[0:sz, 0:sz]),
                      reads=[srcb, identb], writes=[pib])
            kb.op('act' if wi == 0 else 'dve',
                  (lambda e, pi=pi, bkt=bkt, wi=wi: e.activation(out=bkt[:, wi, :], in_=ptrb[:, pi * 256:pi * 256 + 192], func=AF.Copy)) if wi == 0 else
                  (lambda e, pi=pi, bkt=bkt, wi=wi: e.tensor_copy(out=bkt[:, wi, :], in_=ptrb[:, pi * 256:pi * 256 + 192])),
                  reads=[pib], writes=[bktb])
        tcol = 127 if d == 0 else 0

        if DBG.get('stage', 99) < 5:
            return
        first = done.get(c, 0) == 0
        assert c not in inflight
        inflight[c] = d

        def head_body(h):
            mm1r, mm2r, xpr, xtr, ivr, tmr, w1r, utr, tsr = (HR[(d, h)][k_] for k_ in ('mm1r', 'mm2r', 'xpr', 'xtr', 'ivr', 'tmr', 'w1r', 'utr', 'tsr'))
            bj = h // 2
            base = (h % 2) * 64
            ch0 = 64 * h
            hp = slice(base, base + 64)
            yield
            pg1, pg1b = pgram.next()
            kb.op('pe', lambda e, pg1=pg1, hp=hp, bj=bj: e.matmul(pg1[:, 0:256], lhsT=bt[hp, bj, :], rhs=ar[hp, bj, :], start=True, stop=True), reads=[btb, arb], writes=[pg1b])
            mm1, mm1b = mm1r.next()
            kb.op('dve', lambda e, mm1=mm1, pg1=pg1: e.tensor_tensor(out=mm1[:, :], in0=pg1[:, 0:256], in1=msi[:, d, :], op=ALU.mult), reads=[pg1b, msib], writes=[mm1b])
            pg2, pg2b = pgram.next()
            kb.op('pe', lambda e, pg2=pg2, hp=hp, bj=bj: e.matmul(pg2[:, 0:256], lhsT=kt[hp, bj, :], rhs=ar[hp, bj, :], start=True, stop=True), reads=[ktb, arb], writes=[pg2b])
            kb.op('pe', lambda e, pg2=pg2, hp=hp, bj=bj: e.matmul(pg2[:, 256:384], lhsT=ar[hp, bj, 0:128], rhs=bt[hp, bj, :], start=True, stop=True), reads=[btb, arb], writes=[pg2b])
            mm2, mm2b = mm2r.next()
            kb.op('dve', lambda e, mm2=mm2, pg2=pg2: e.tensor_tensor(out=mm2[:, :], in0=pg2[:, 0:256], in1=msi2[:, d, :], op=ALU.mult), reads=[pg2b, msi2b], writes=[mm2b])
            xt, xtb = xtr.next()
            kb.op('dve', lambda e, xt=xt, pg2=pg2: e.tensor_tensor(out=xt[:, :], in0=pg2[:, 256:384], in1=mst[:, d, 0, :], op=ALU.mult), reads=[pg2b, mstb], writes=[xtb])
            e1t, e1tb = ivr.next()
            kb.op('dve', lambda e, e1t=e1t, pg2=pg2: e.tensor_tensor(out=e1t[:, :], in0=pg2[:, 256:384], in1=mst[:, d, 1, :], op=ALU.mult), reads=[pg2b, mstb], writes=[e1tb])
            e2t, e2tb = ivr.next()
            kb.op('dve', lambda e, e2t=e2t, pg2=pg2: e.tensor_tensor(out=e2t[:, :], in0=pg2[:, 256:384], in1=mst[:, d, 2, :], op=ALU.mult), reads=[pg2b, mstb], writes=[e2tb])
            if DBG.get('stage', 99) < 6:
                return
            yield
            xp, xpb = xpr.next()
            kb.op('pool', lambda e, xp=xp, mm1=mm1: e.tensor_tensor(out=xp[:, 128:256], in0=mm1[:, 0:128], in1=ident_bf[:, :], op=ALU.add), reads=[mm1b, identb], writes=[xpb])
            pv, pvb = pinv.next()
            kb.op('pe', lambda e, pv=pv, xt=xt, mm1=mm1: e.matmul(pv[:, 0:128], lhsT=xt[:, :], rhs=mm1[:, 0:128], start=True, stop=True), reads=[xtb, mm1b], writes=[pvb])
            kb.op('pe', lambda e, pv=pv, xt=xt, mm1=mm1: e.matmul(pv[:, 256:384], lhsT=mm1[:, 0:128], rhs=xt[:, :], start=True, stop=True), reads=[xtb, mm1b], writes=[pvb])
            xt2, xt2b = xtr.next()
            kb.op('act', lambda e, xp=xp, pv=pv: e.activation(out=xp[:, 0:128], in_=pv[:, 0:128], func=AF.Copy), reads=[pvb], writes=[xpb])
            kb.op('dve', lambda e, xt2=xt2, pv=pv: e.tensor_copy(out=xt2[:, :], in_=pv[:, 256:384]), reads=[pvb], writes=[xt2b])
            curxp, curxpb, curxt, curxtb = xp, xpb, xt2, xt2b
            for lev in range(1, 4):
                pv, pvb = pinv.next()
                kb.op('pe', lambda e, pv=pv, cx=curxp, ct=curxt: e.matmul(pv[:, 0:256], lhsT=ct[:, :], rhs=cx[:, 0:256], start=True, stop=True), reads=[curxpb, curxtb], writes=[pvb])
                kb.op('pe', lambda e, pv=pv, cx=curxp, ct=curxt: e.matmul(pv[:, 256:384], lhsT=cx[:, 0:128], rhs=ct[:, :], start=True, stop=True), reads=[curxpb, curxtb], writes=[pvb])
                nxp, nxpb = xpr.next()
                nxt, nxtb = xtr.next()
                kb.op('act', lambda e, nxp=nxp, pv=pv: e.activation(out=nxp[:, 0:128], in_=pv[:, 0:128], func=AF.Copy), reads=[pvb], writes=[nxpb])
                kb.op('dve', lambda e, nxp=nxp, pv=pv, cx=curxp: e.tensor_tensor(out=nxp[:, 128:256], in0=pv[:, 128:256], in1=cx[:, 128:256], op=ALU.add), reads=[pvb, curxpb], writes=[nxpb])
                kb.op('act', lambda e, nxt=nxt, pv=pv: e.activation(out=nxt[:, :], in_=pv[:, 256:384], func=AF.Copy), reads=[pvb], writes=[nxtb])
                curxp, curxpb, curxt, curxtb = nxp, nxpb, nxt, nxtb
                yield
            pv, pvb = pinv.next()
            kb.op('pe', lambda e, pv=pv, cx=curxp, ct=curxt: e.matmul(pv[:, 0:128], lhsT=ct[:, :], rhs=cx[:, 128:256], start=True, stop=True), reads=[curxpb, curxtb], writes=[pvb])
            t32m, t32mb = ivr.next()
            kb.op('dve', lambda e, t32m=t32m, pv=pv, cx=curxp: e.tensor_tensor(out=t32m[:, :], in0=pv[:, 0:128], in1=cx[:, 128:256], op=ALU.add), reads=[pvb, curxpb], writes=[t32mb])
            yield
            pi, pib = ptr_next()
            kb.op('pe', lambda e, pi=pi, t32m=t32m: e.transpose(out=ptrb[:, pi * 256:pi * 256 + 128], in_=t32m[:, :], identity=ident_bf[:, :]), reads=[t32mb, identb], writes=[pib])
            t32t, t32tb = ivr.next()
            kb.op('act', lambda e, pi=pi, t32t=t32t: e.activation(out=t32t[:, :], in_=ptrb[:, pi * 256:pi * 256 + 128], func=AF.Copy), reads=[pib], writes=[t32tb])
            yield
            pv, pvb = pinv.next()
            kb.op('pe', lambda e, pv=pv, e1t=e1t, t32m=t32m: e.matmul(pv[:, 0:128], lhsT=e1t[:, :], rhs=t32m[:, :], start=True, stop=True), reads=[e1tb, t32mb], writes=[pvb])
            z1, z1b = ivr.next()
            kb.op('act', lambda e, z1=z1, pv=pv: e.activation(out=z1[:, :], in_=pv[:, 0:128], func=AF.Copy), reads=[pvb], writes=[z1b])
            pv, pvb = pinv.next()
            kb.op('pe', lambda e, pv=pv, t32t=t32t, z1=z1: e.matmul(pv[:, 0:128], lhsT=t32t[:, :], rhs=z1[:, :], start=True, stop=True), reads=[t32tb, z1b], writes=[pvb])
            kb.op('pe', lambda e, pv=pv, t32t=t32t, z1=z1: e.matmul(pv[:, 256:384], lhsT=z1[:, :], rhs=t32t[:, :], start=True, stop=True), reads=[t32tb, z1b], writes=[pvb])
            t64, t64b = ivr.next()
            t64t, t64tb = ivr.next()
            kb.op('dve', lambda e, t64=t64, pv=pv, t32m=t32m: e.tensor_tensor(out=t64[:, :], in0=pv[:, 0:128], in1=t32m[:, :], op=ALU.add), reads=[pvb, t32mb], writes=[t64b])
            kb.op('dve', lambda e, t64t=t64t, pv=pv, t32t=t32t: e.tensor_tensor(out=t64t[:, :], in0=pv[:, 256:384], in1=t32t[:, :], op=ALU.add), reads=[pvb, t32tb], writes=[t64tb])
            yield
            pv, pvb = pinv.next()
            kb.op('pe', lambda e, pv=pv, e2t=e2t, t64=t64: e.matmul(pv[:, 0:128], lhsT=e2t[:, :], rhs=t64[:, :], start=True, stop=True), reads=[e2tb, t64b], writes=[pvb])
            z2, z2b = ivr.next()
            kb.op('act', lambda e, z2=z2, pv=pv: e.activation(out=z2[:, :], in_=pv[:, 0:128], func=AF.Copy), reads=[pvb], writes=[z2b])
            pv, pvb = pinv.next()
            kb.op('pe', lambda e, pv=pv, t64t=t64t, z2=z2: e.matmul(pv[:, 0:128], lhsT=t64t[:, :], rhs=z2[:, :], start=True, stop=True), reads=[t64tb, z2b], writes=[pvb])
            tm, tmb = tmr.next()
            kb.op('dve', lambda e, tm=tm, pv=pv, t64=t64: e.tensor_tensor(out=tm[:, :], in0=pv[:, 0:128], in1=t64[:, :], op=ALU.add), reads=[pvb, t64b], writes=[tmb])
            if DBG.get('dump') == (d, c) and h == 0:
                dump(kb, nc, 'mm1', mm1[:, :], [128, 256], [mm1b], BF16)
                dump(kb, nc, 'mm2', mm2[:, :], [128, 256], [mm2b], BF16)
                dump(kb, nc, 'tm', tm[:, :], [128, 128], [tmb], BF16)
            yield
            stb = STb[d][h]
            p1, p1b, _ = pseq_next()
            kb.op('pe', lambda e, p1=p1, hp=hp, bj=bj: e.matmul(p1, lhsT=ar[hp, bj, 0:128], rhs=STbf[hp, d, bj, :], start=True, stop=False), reads=[arb, stb], writes=[p1b])
            kb.op('pe', lambda e, p1=p1, mm2=mm2, ch0=ch0: e.matmul(p1, lhsT=mm2[:, 0:128], rhs=vtbf[:, ch0:ch0 + 64], start=False, stop=True), reads=[mm2b, vtbfb], writes=[p1b])
            w1, w1b = w1r.next()
            kb.op('act', lambda e, w1=w1, p1=p1: e.activation(out=w1[:, :], in_=p1, func=AF.Copy), reads=[p1b], writes=[w1b])
            p2, p2b, _ = pseq_next()
            kb.op('pe', lambda e, p2=p2, tm=tm, w1=w1: e.matmul(p2, lhsT=tm[:, :], rhs=w1[:, :], start=True, stop=True), reads=[tmb, w1b], writes=[p2b])
            ut, utb = utr.next()
            kb.op('act', lambda e, ut=ut, p2=p2: e.activation(out=ut[:, :], in_=p2, func=AF.Copy), reads=[p2b], writes=[utb])
            p3, p3b, _ = pseq_next()
            kb.op('pe', lambda e, p3=p3, hp=hp, bj=bj: e.matmul(p3, lhsT=ar[hp, bj, 128:256], rhs=STbf[hp, d, bj, :], start=True, stop=False), reads=[arb, stb], writes=[p3b])
            kb.op('pe', lambda e, p3=p3, mm1=mm1, ut=ut: e.matmul(p3, lhsT=mm1[:, 128:256], rhs=ut[:, :], start=False, stop=False), reads=[mm1b, utb], writes=[p3b])
            kb.op('pe', lambda e, p3=p3, mm2=mm2, ch0=ch0: e.matmul(p3, lhsT=mm2[:, 128:256], rhs=vtbf[:, ch0:ch0 + 64], start=False, stop=True), reads=[mm2b, vtbfb], writes=[p3b])
            if DBG.get('dump') == (d, c) and h == 0:
                dump(kb, nc, 'w1', w1[:, :], [128, 64], [w1b], BF16)
                dump(kb, nc, 'ut', ut[:, :], [128, 64], [utb], BF16)
            ts, tsb = tsr.next()
            kb.op('dve', lambda e, ts=ts, p3=p3, ch0=ch0, h=h: e.scalar_tensor_tensor(out=ts[:, :], in0=vt[:, ch0:ch0 + 64], scalar=bc[:, h:h + 1], in1=p3, op0=ALU.mult, op1=ALU.add),
                  reads=[vtb, bcb, p3b], writes=[tsb])
            if first:
                kb.op('pool', lambda e, ts=ts, ch0=ch0, c=c: e.tensor_copy(out=yacc[:, c, ch0:ch0 + 64], in_=ts[:, :]), reads=[tsb], writes=[yaccb[c]])
            else:
                kb.op('pool', lambda e, ts=ts, ch0=ch0, c=c: e.tensor_tensor(out=yacc[:, c, ch0:ch0 + 64], in0=yacc[:, c, ch0:ch0 + 64], in1=ts[:, :], op=ALU.add), reads=[tsb, yaccb[c]], writes=[yaccb[c]])
            yield
            p4full, p4b, i4 = pseq_next()
            p4 = pseqt[i4][hp, (i4 % 4) * 64:(i4 % 4 + 1) * 64]
            kb.op('pe', lambda e, p4=p4, bkt=bkt, ut=ut, ch0=ch0: e.matmul(p4, lhsT=bkt[:, 0, ch0:ch0 + 64], rhs=ut[:, :], start=True, stop=False), reads=[bktb, utb], writes=[p4b])
            kb.op('pe', lambda e, p4=p4, bkt=bkt, ch0=ch0: e.matmul(p4, lhsT=bkt[:, 1, ch0:ch0 + 64], rhs=vtbf[:, ch0:ch0 + 64], start=False, stop=True), reads=[bktb, vtbfb], writes=[p4b])
            tq, tqb = tsr.next()
            kb.op('dve', lambda e, tq=tq, p4=p4, hp=hp, bj=bj: e.tensor_tensor(out=tq[hp, :], in0=p4, in1=ST[hp, d, bj, :], op=ALU.add), reads=[p4b, stb], writes=[tqb])
            kb.op('dve', lambda e, tq=tq, hp=hp, bj=bj: e.tensor_scalar(out=ST[hp, d, bj, :], in0=tq[hp, :], scalar1=e1[hp, bj, tcol:tcol + 1], scalar2=None, op0=ALU.mult), reads=[tqb, e1b], writes=[stb])
            kb.op('act', lambda e, tq=tq, hp=hp, bj=bj: e.activation(out=STbf[hp, d, bj, :], in_=tq[hp, :], func=AF.Identity, scale=e1[hp, bj, tcol:tcol + 1]), reads=[tqb, e1b], writes=[stb])

        hg = [head_body(h) for h in range(DBG.get('heads', 3))]
        while hg:
            for g_ in list(hg):
                try:
                    next(g_)
                except StopIteration:
                    hg.remove(g_)
            yield
        del inflight[c]
        done[c] = done.get(c, 0) + 1
        if DBG.get('dump') == (d, c):
            dump(kb, nc, 'sm', sm[:, :, :], [128, 8, 128], [smb])
            dump(kb, nc, 'kkt', kkt[:, :, :], [128, 2, 128], [kktb])
            dump(kb, nc, 'vt', vt[:, :], [128, 192], [vtb])
            dump(kb, nc, 'sg', sg[:, :], [128, 192], [sgb])
            dump(kb, nc, 'e1', e1[:, :, :], [128, 2, 256], [e1b])
            dump(kb, nc, 'e2', e2[:, :, :], [128, 2, 128], [e2b])
            dump(kb, nc, 'ar', ar[:, :, :], [128, 2, 256], [arb], BF16)
            dump(kb, nc, 'bt', bt[:, :, :], [128, 2, 128], [btb], BF16)
            dump(kb, nc, 'kt', kt[:, :, :], [128, 2, 128], [ktb], BF16)
            dump(kb, nc, 'bkt', bkt[:, :, :], [128, 2, 192], [bktb], BF16)
            dump(kb, nc, 'bc', bc[:, :], [128, 4], [bcb])
            dump(kb, nc, 'ST', ST[:, d, :, :], [128, 2, 64], STb[d])
            dump(kb, nc, 'yacc', yacc[:, c, :], [128, 192], [yaccb[c]])
        yield
        if done[c] == 2 and DBG.get('stage', 99) >= 8:
            gt, gtb = gtr.next()
            kb.op('sp', lambda e, gt=gt, t0=t0: e.dma_start(out=gt[:, :], in_=GT[t0:t0 + 128, 0:192]), writes=[gtb], dma=True)
            fn, fnb = fnr.next()
            for h in range(3):
                ch0 = 64 * h
                stt, sttb = str_.next()
                jk, jkb = jkr.next()
                kb.op('act', lambda e, stt=stt, jk=jk, c=c, ch0=ch0: e.activation(out=jk[:, :], in_=yacc[:, c, ch0:ch0 + 64], func=AF.Copy, accum_out=stt[:, 0:1]), reads=[yaccb[c]], writes=[sttb, jkb])
                kb.op('act', lambda e, stt=stt, jk=jk, c=c, ch0=ch0: e.activation(out=jk[:, :], in_=yacc[:, c, ch0:ch0 + 64], func=AF.Square, accum_out=stt[:, 1:2]), reads=[yaccb[c]], writes=[sttb, jkb])
                kb.op('dve', lambda e, stt=stt: e.tensor_scalar(out=stt[:, 6:7], in0=stt[:, 0:1], scalar1=1.0 / 64, scalar2=None, op0=ALU.mult), reads=[sttb], writes=[sttb])
                kb.op('dve', lambda e, stt=stt: e.tensor_tensor(out=stt[:, 3:4], in0=stt[:, 6:7], in1=stt[:, 6:7], op=ALU.mult), reads=[sttb], writes=[sttb])
                kb.op('dve', lambda e, stt=stt: e.scalar_tensor_tensor(out=stt[:, 7:8], in0=stt[:, 1:2], scalar=1.0 / 64, in1=stt[:, 3:4], op0=ALU.mult, op1=ALU.subtract), reads=[sttb], writes=[sttb])
                kb.op('act', lambda e, stt=stt: e.activation(out=stt[:, 4:5], in_=stt[:, 7:8], func=AF.Sqrt, bias=gneps[:, 0:1]), reads=[sttb, gnepsb], writes=[sttb])
                kb.op('dve', lambda e, stt=stt: e.reciprocal(out=stt[:, 5:6], in_=stt[:, 4:5]), reads=[sttb], writes=[sttb])
                kb.op('dve', lambda e, stt=stt, fn=fn, c=c, ch0=ch0: e.tensor_scalar(out=fn[:, ch0:ch0 + 64], in0=yacc[:, c, ch0:ch0 + 64], scalar1=stt[:, 6:7], scalar2=stt[:, 5:6], op0=ALU.subtract, op1=ALU.mult),
                      reads=[sttb, yaccb[c]], writes=[fnb])
            if DBG.get('dumpfin') == c:
                dump(kb, nc, 'stt', stt[:, :], [128, 8], [sttb])
                dump(kb, nc, 'fn0', fn[:, :], [128, 192], [fnb])
                dump(kb, nc, 'gt', gt[:, :], [128, 192], [gtb])
                dump(kb, nc, 'yaccf', yacc[:, c, :], [128, 192], [yaccb[c]])
                dump(kb, nc, 'lng', lng[:, :], [128, 192], [lngb])
            kb.op('pool', lambda e, fn=fn: e.tensor_tensor(out=fn[:, :], in0=fn[:, :], in1=lng[:, :], op=ALU.mult), reads=[fnb, lngb], writes=[fnb])
            kb.op('pool', lambda e, fn=fn: e.tensor_tensor(out=fn[:, :], in0=fn[:, :], in1=lnb[:, :], op=ALU.add), reads=[fnb, lnbb], writes=[fnb])
            kb.op('dve', lambda e, fn=fn, gt=gt: e.tensor_tensor(out=fn[:, :], in0=fn[:, :], in1=gt[:, :], op=ALU.mult), reads=[fnb, gtb], writes=[fnb])
            kb.op('pool', lambda e, fn=fn, t0=t0: e.dma_start(out=YR.rows(t0, 0, 192), in_=fn[:, :]), reads=[fnb], dma=True)


    def dir_gen(d):
        for c in (DBG['chunks'][d] if 'chunks' in DBG else CHUNK_ORDER[d]):
            yield from chunk_body(d, c)

    dirs = DBG.get('dirs', [0, 1])
    for d in dirs:
        kb.op('dve', lambda e, d=d: e.memset(ST[:, d, :, :], 0.0), writes=STb[d])
        kb.op('dve', lambda e, d=d: e.memset(STbf[:, d, :, :], 0.0), writes=STb[d])
    gens = [dir_gen(d) for d in dirs]
    if DBG.get('no_interleave'):
        for g_ in gens:
            for _ in g_:
                pass
    else:
        while gens:
            for g_ in list(gens):
                try:
                    next(g_)
                except StopIteration:
                    gens.remove(g_)


FN_PARAMS = {'c128': [128, 128], 'ns128': [128, 128], 'twc': [128, 64], 'tws': [128, 64], 'fl1': [128, 64], 'fl2': [128, 64],
             'c256': [128, 2, 256], 'ns256': [128, 2, 256]}


def phase_fnet(kb, nc, HT, GT, YF, ZS, P, bar):
    sc_l = 1.0 / float(np.sqrt(L * 64.0))
    sc_c = 1.0 / float(np.sqrt(LC * 64.0))
    with ExitStack() as c1:
        def const(name, shape, src):
            t = kb.sb(name, shape, F32, c1)
            b, = load_consts(kb, [(t, src)])
            return t, b
        c128, c128b = const('c128', [128, 128], P['c128'])
        ns128, ns128b = const('ns128', [128, 128], P['ns128'])
        twc, twcb = const('twc', [128, 64], P['twc'])
        tws, twsb = const('tws', [128, 64], P['tws'])
        c256, c256b = const('c256', [128, 2, 256], P['c256'])
        ns256, ns256b = const('ns256', [128, 2, 256], P['ns256'])
        hc = kb.sb('hctx', [128, 2, 128], F32, c1)
        hcb = Buf()
        gc = kb.sb('gctx', [128, 2, 64], F32, c1)
        gcb = Buf()
        for lt in range(2):
            kb.op('sp', lambda e, lt=lt: e.dma_start(out=hc[:, lt, :], in_=HT[lt * 128:(lt + 1) * 128, :]), writes=[hcb], dma=True)
            kb.op('sp', lambda e, lt=lt: e.dma_start(out=gc[:, lt, :], in_=GT[lt * 128:(lt + 1) * 128, 192:256]), writes=[gcb], dma=True)
        pc = Ring(kb, 'pfc', [128, 512], F32, 1, psum=True, es=c1)
        fo = Ring(kb, 'foc', [128, 64], F32, 2, es=c1)
        for lo in range(2):
            ps, psb = pc.next()
            for lt in range(2):
                kb.op('pe', lambda e, ps=ps, lt=lt, lo=lo: e.matmul(ps[:, 0:64], lhsT=c256[:, lt, lo * 128:(lo + 1) * 128], rhs=hc[:, lt, 0:64], start=(lt == 0), stop=False),
                      reads=[c256b, hcb], writes=[psb])
            for lt in range(2):
                kb.op('pe', lambda e, ps=ps, lt=lt, lo=lo: e.matmul(ps[:, 0:64], lhsT=ns256[:, lt, lo * 128:(lo + 1) * 128], rhs=hc[:, lt, 64:128], start=False, stop=(lt == 1)),
                      reads=[ns256b, hcb], writes=[psb])
            f, fb = fo.next()
            kb.op('dve', lambda e, f=f, ps=ps, lo=lo: e.scalar_tensor_tensor(out=f[:, :], in0=ps[:, 0:64], scalar=sc_c, in1=gc[:, lo, :], op0=ALU.mult, op1=ALU.mult),
                  reads=[psb, gcb], writes=[fb])
            kb.op('pool', lambda e, f=f, lo=lo: e.dma_start(out=YF.rows(lo * 128, 192, 256), in_=f[:, :]), reads=[fb], dma=True)
        h1 = kb.sb('h1', [128, 64, 128], F32, c1)
        h1b = Buf()
        HTl = HT[LC:LT, :].rearrange("(a b) c -> a b c", b=64)
        for j in range(4):
            kb.op('sp', lambda e, j=j: e.dma_start(out=h1[:, j * 16:(j + 1) * 16, :], in_=HTl[:, j * 16:(j + 1) * 16, :]), writes=[h1b], dma=True)
        py = Ring(kb, 'py', [128, 512], F32, 4, psum=True, es=c1)
        zt = Ring(kb, 'zt', [128, 4, 128], F32, 6, es=c1)
        zo = Ring(kb, 'zo', [128, 2, 4, 128], F32, 3, es=c1)
        for pc_ in range(16):
            b0 = pc_ * 4
            pr, prb = py.next()
            pi, pib = py.next()
            kb.op('pe', lambda e, pr=pr, b0=b0: e.matmul(pr[:, :], lhsT=c128[:, :], rhs=h1[:, b0:b0 + 4, :], start=True, stop=True), reads=[c128b, h1b], writes=[prb])
            kb.op('pe', lambda e, pi=pi, b0=b0: e.matmul(pi[:, :], lhsT=ns128[:, :], rhs=h1[:, b0:b0 + 4, :], start=True, stop=True), reads=[ns128b, h1b], writes=[pib])
            cb_ = twc[:, b0:b0 + 4].unsqueeze(2).to_broadcast([128, 4, 128])
            sb_ = tws[:, b0:b0 + 4].unsqueeze(2).to_broadcast([128, 4, 128])
            prv = pr[:, :].rearrange("p (b c) -> p b c", c=128)
            piv = pi[:, :].rearrange("p (b c) -> p b c", c=128)
            t1, t1b = zt.next()
            t2, t2b = zt.next()
            z, zb = zo.next()
            kb.op('dve', lambda e, t1=t1, prv=prv, cb_=cb_: e.tensor_tensor(out=t1[:, :, :], in0=prv, in1=cb_, op=ALU.mult), reads=[prb, twcb], writes=[t1b])
            kb.op('dve', lambda e, t2=t2, piv=piv, sb_=sb_: e.tensor_tensor(out=t2[:, :, :], in0=piv, in1=sb_, op=ALU.mult), reads=[pib, twsb], writes=[t2b])
            kb.op('pool', lambda e, z=z, t1=t1, t2=t2: e.tensor_tensor(out=z[:, 0, :, :], in0=t1[:, :, :], in1=t2[:, :, :], op=ALU.add), reads=[t1b, t2b], writes=[zb])
            t3, t3b = zt.next()
            t4, t4b = zt.next()
            kb.op('dve', lambda e, t3=t3, piv=piv, cb_=cb_: e.tensor_tensor(out=t3[:, :, :], in0=piv, in1=cb_, op=ALU.mult), reads=[pib, twcb], writes=[t3b])
            kb.op('dve', lambda e, t4=t4, prv=prv, sb_=sb_: e.tensor_tensor(out=t4[:, :, :], in0=prv, in1=sb_, op=ALU.mult), reads=[prb, twsb], writes=[t4b])
            kb.op('pool', lambda e, z=z, t3=t3, t4=t4: e.tensor_tensor(out=z[:, 1, :, :], in0=t3[:, :, :], in1=t4[:, :, :], op=ALU.subtract), reads=[t3b, t4b], writes=[zb])
            for ri in range(2):
                kb.op('pool', lambda e, z=z, ri=ri, b0=b0: e.dma_start(out=ZS[ri, :, b0:b0 + 4, :], in_=z[:, ri, :, :]), reads=[zb], dma=True)
        kb.barrier(bar[:])
    with ExitStack() as c2:
        fl1 = kb.sb('fl1', [128, 64], F32, c2)
        fl2 = kb.sb('fl2', [128, 64], F32, c2)
        fl1b, fl2b = load_consts(kb, [(fl1, P['fl1']), (fl2, P['fl2'])])
        rz1 = kb.sb('rz1', [128, 128, 64], F32, c2)
        rz2 = kb.sb('rz2', [128, 128, 64], F32, c2)
        g2 = kb.sb('g2', [64, 128, 64], F32, c2)
        fo2 = kb.sb('fo2', [64, 128, 64], F32, c2)
        rz1b = [Buf() for _ in range(4)]
        rz2b = [Buf() for _ in range(4)]
        g2b = Buf()
        fo2b = [Buf() for _ in range(4)]
        for qa in range(4):
            asl = slice(qa * 32, (qa + 1) * 32)
            for ri in range(2):
                src = ZS[ri].rearrange("a b c -> b a c")
                kb.op('sp', lambda e, ri=ri, src=src, asl=asl: e.dma_start(out=rz1[ri * 64:(ri + 1) * 64, asl, :], in_=src[:, asl, 0:64]), writes=[rz1b[qa]], dma=True)
                kb.op('sp', lambda e, ri=ri, src=src, asl=asl: e.dma_start(out=rz2[ri * 64:(ri + 1) * 64, asl, :], in_=src[:, asl, 64:128]), writes=[rz2b[qa]], dma=True)
        kb.op('sp', lambda e: e.dma_start(out=g2[:, :, :], in_=GT[LC:LT, 192:256].rearrange("(b a) c -> b a c", a=128)), writes=[g2b], dma=True)
        pf = Ring(kb, 'pf', [128, 512], F32, 2, psum=True, es=c2)
        for pc_ in range(16):
            a0 = pc_ * 8
            qa = pc_ // 4
            ps, psb = pf.next()
            kb.op('pe', lambda e, ps=ps, a0=a0: e.matmul(ps[0:64, :], lhsT=fl1[:, :], rhs=rz1[:, a0:a0 + 8, :], start=True, stop=False), reads=[fl1b, rz1b[qa]], writes=[psb])
            kb.op('pe', lambda e, ps=ps, a0=a0: e.matmul(ps[0:64, :], lhsT=fl2[:, :], rhs=rz2[:, a0:a0 + 8, :], start=False, stop=True), reads=[fl2b, rz2b[qa]], writes=[psb])
            kb.op('dve', lambda e, ps=ps, a0=a0: e.scalar_tensor_tensor(out=fo2[:, a0:a0 + 8, :], in0=ps[0:64, :].rearrange("p (a c) -> p a c", c=64), scalar=sc_l,
                                                                        in1=g2[:, a0:a0 + 8, :], op0=ALU.mult, op1=ALU.mult), reads=[psb, g2b], writes=[fo2b[qa]])
        allfo = fo2b
        for (dst, b0, b1) in YF.lat_groups():
            kb.op('pool', lambda e, dst=dst, b0=b0, b1=b1: e.dma_start(out=dst, in_=fo2[b0:b1, :, :]), reads=allfo, dma=True)
        kb.barrier(bar[:])


def gate_rows(kb, es, nc, sil, silb, adaw_gate, adab_row, npost_b, ones1, ones1b, keep_es, NG=None):
    if NG is None:
        NG = kb.sb('NG', [128, 2, D], F32, keep_es)
    NGb = Buf()
    wg = kb.sb('adawg', [128, 8, D], F32, es)
    wgb = Buf()
    for k in range(8):
        kb.op('sp', lambda e, k=k: e.dma_start(out=wg[:, k, :], in_=adaw_gate[k * 128:(k + 1) * 128, :]), writes=[wgb], dma=True)
    br = kb.sb('adabr', [1, D], F32, es)
    brb, = load_consts(kb, [(br, adab_row)])
    npb = kb.sb('npostb', [128, D], F32, es)
    npbb, = load_consts(kb, [(npb, npost_b)])
    onesq = kb.sb('onesq', [128, 128], F32, es)
    onesqb = Buf()
    kb.op('dve', lambda e: e.memset(onesq[:], 1.0), writes=[onesqb])
    srep = kb.sb('silrep', [128, 8, 128], F32, es)
    pg_ = Ring(kb, 'pgate', [128, 512], F32, 2, psum=True, es=es)
    for n in range(2):
        srb = Buf()
        for k in range(8):
            kb.op('dve', lambda e, k=k, n=n: e.tensor_scalar(out=srep[:, k, :], in0=onesq[:, :], scalar1=sil[:, k, n:n + 1], scalar2=None, op0=ALU.mult),
                  reads=[onesqb, silb], writes=[srb])
        for half in range(2):
            ps, psb = pg_.next()
            for k in range(8):
                kb.op('pe', lambda e, ps=ps, k=k, half=half: e.matmul(ps[:, :], lhsT=srep[:, k, :], rhs=wg[:, k, half * 512:(half + 1) * 512], start=(k == 0), stop=False),
                      reads=[srb, wgb], writes=[psb])
            kb.op('pe', lambda e, ps=ps, half=half: e.matmul(ps[:, :], lhsT=ones1[0:1, :], rhs=br[0:1, half * 512:(half + 1) * 512], start=False, stop=True),
                  reads=[ones1b, brb], writes=[psb])
            kb.op('dve', lambda e, ps=ps, n=n, half=half: e.tensor_tensor(out=NG[:, n, half * 512:(half + 1) * 512], in0=ps[:, :], in1=npb[:, half * 512:(half + 1) * 512], op=ALU.mult),
                  reads=[psb, npbb], writes=[NGb])
    return NG, NGb


class OutProj:
    def __init__(self, kb, es, nc, yT, w_out_src, xsrc, NG, NGb, epsT, epsb, nslot=3, ysrc=None, ident=None, npol=2):
        self.kb, self.yT, self.xsrc, self.NG, self.NGb, self.epsT, self.epsb = kb, yT, xsrc, NG, NGb, epsT, epsb
        self.ysrc, self.ident = ysrc, ident
        self.pref = {}
        self.ykbs = {}
        if ysrc is not None:
            self.ytok = Ring(kb, 'ytok', [128, 4, 256], F32, 2, es=es)
            self.ytokb = Ring(kb, 'ytokb', [128, 1024], BF16, 2, es=es)
            self.pyt = Ring(kb, 'pyt', [128, 8, 128], BF16, 1, psum=True, es=es)
        self.wo, self.wob = load_weight_bf16(kb, es, nc, 'wout_bf', w_out_src, D, es)
        self.ystg = Ring(kb, 'ystg', [128, 8, 128], F32, 2, es=es)
        self.ybf = Ring(kb, 'ybf', [128, 8, 128], BF16, 2, es=es)
        self.xin = Ring(kb, 'xres', [128, D], F32, 2, es=es)
        self.xout = Ring(kb, 'xnew', [128, D], F32, nslot, es=es)
        self.pol = Ring(kb, 'pol', [128, 512], F32, npol, psum=True, es=es)
        self.st = Ring(kb, 'ost', [128, 8], F32, 4, es=es, strict=True)
        self.junk = kb.sb('ojunk', [128, 512], BF16, es)
        self.junkb = Buf()
        self.tmp = Ring(kb, 'otmp', [128, D], F32, 2, es=es)

    def _pre(self, r0):
        kb = self.kb
        if self.ysrc is None:
            ys, ysb = self.ystg.next()
            yT = self.yT
            kb.op('sp', lambda e, ys=ys: e.dma_start(out=ys[:, :, :], in_=yT[:, r0:r0 + 128].rearrange("(k p) t -> p k t", p=128)), writes=[ysb], dma=True)
            yb, ybb = self.ybf.next()
            kb.op('pool', lambda e, yb=yb, ys=ys: e.tensor_copy(out=yb[:, :, :], in_=ys[:, :, :]), reads=[ysb], writes=[ybb])
        else:
            aps, sbufs = self.ysrc(r0)
            yk, ykb0 = self.ytok.next()
            ykb = self.ykbs.setdefault(id(ykb0), [ykb0] + [Buf() for _ in range(3)])
            for r, ap_ in enumerate(aps):
                kb.op('sp', lambda e, yk=yk, r=r, ap_=ap_: e.dma_start(out=yk[:, r, :], in_=ap_), reads=sbufs, writes=[ykb[r]], dma=True)
            ykb16, ykb16b = self.ytokb.next()
            kb.op('pool', lambda e, yk=yk, ykb16=ykb16: e.tensor_copy(out=ykb16[:, :], in_=yk[:, :, :].rearrange("p r c -> p (r c)")), reads=[ykb], writes=[ykb16b])
            pt, ptb = self.pyt.next()
            idt, idtb = self.ident
            for k in range(8):
                kb.op('pe', lambda e, pt=pt, ykb16=ykb16, k=k: e.transpose(out=pt[:, k, :], in_=ykb16[:, k * 128:(k + 1) * 128], identity=idt[:, :]), reads=[ykb16b, idtb], writes=[ptb])
            yb, ybb = self.ybf.next()
            kb.op('act', lambda e, yb=yb, pt=pt: e.activation(out=yb[:, :, :], in_=pt[:, :, :], func=AF.Copy), reads=[ptb], writes=[ybb])
        xi, xib = self.xin.next()
        kb.op('sp', lambda e, xi=xi: e.dma_start(out=xi[:, :], in_=self.xsrc[r0:r0 + 128, :]), writes=[xib], dma=True)
        self.pref[r0] = (yb, ybb, xi, xib)

    def tile(self, r0, n, nxt=None):
        kb = self.kb
        if r0 not in self.pref:
            self._pre(r0)
        if nxt is not None and nxt not in self.pref:
            self._pre(nxt)
        yb, ybb, xi, xib = self.pref.pop(r0)
        pss = [self.pol.next(), self.pol.next()]
        for half in range(2):
            ps, psb = pss[half]
            for k in range(8):
                kb.op('pe', lambda e, ps=ps, k=k, half=half, yb=yb: e.matmul(ps[:, :], lhsT=yb[:, k, :], rhs=self.wo[:, k, half * 512:(half + 1) * 512],
                                                                           start=(k == 0), stop=(k == 7)), reads=[ybb, self.wob], writes=[psb])
        st, stb = self.st.next()
        for half in range(2):
            ps, psb = pss[half]
            kb.op('act', lambda e, ps=ps, st=st, half=half: e.activation(out=self.junk[:, :], in_=ps[:, :], func=AF.Square, accum_out=st[:, half:half + 1]),
                  reads=[psb], writes=[stb, self.junkb])
        kb.op('dve', lambda e, st=st: e.tensor_tensor(out=st[:, 2:3], in0=st[:, 0:1], in1=st[:, 1:2], op=ALU.add), reads=[stb], writes=[stb])
        kb.op('act', lambda e, st=st: e.activation(out=st[:, 3:4], in_=st[:, 2:3], func=AF.Sqrt, scale=1.0 / D, bias=self.epsT[:, 0:1]), reads=[stb, self.epsb], writes=[stb])
        kb.op('dve', lambda e, st=st: e.reciprocal(out=st[:, 4:5], in_=st[:, 3:4]), reads=[stb], writes=[stb])
        tm_, tmb_ = self.tmp.next()
        for half in range(2):
            ps, psb = pss[half]
            kb.op('dve', lambda e, ps=ps, st=st, tm_=tm_, half=half: e.scalar_tensor_tensor(out=tm_[:, half * 512:(half + 1) * 512], in0=ps[:, :], scalar=st[:, 4:5],
                                                                                       in1=self.NG[:, n, half * 512:(half + 1) * 512], op0=ALU.mult, op1=ALU.mult),
                  reads=[psb, stb, self.NGb], writes=[tmb_])
        xo, xob = self.xout.next()
        kb.op('pool', lambda e, xo=xo, tm_=tm_, xi=xi: e.tensor_tensor(out=xo[:, :], in0=tm_[:, :], in1=xi[:, :], op=ALU.add), reads=[tmb_, xib], writes=[xob])
        return xo, xob


L3_TOK = 2048


def part3(kb, nc, IN, ZG, X1s, OUT, C):
    epsT, epsb, ones1, ones1b = C['epsT'], C['epsb'], C['ones1'], C['ones1b']
    with ExitStack() as g:
        sil = kb.sb('sil3', [128, 8, 2], F32, g)
        silb = Buf(strict=True)
        kb.op('sp', lambda e: e.dma_start(out=sil[:], in_=IN['sil_in']), writes=[silb], dma=True)
        kb.op('act', lambda e: e.activation(out=sil[:], in_=sil[:], func=AF.Silu), reads=[silb], writes=[silb])
        NG = kb.sb('NG3', [128, 2, D], F32, g)
        with ExitStack() as g0:
            NG, NGb = gate_rows(kb, g0, nc, sil, silb, IN['adawg1'], IN['adabr1'], IN['npostb1'], ones1, ones1b, g0, NG=NG)
            kb.barrier(C['bar'][:])
        op = OutProj(kb, g, nc, None, IN['wout1'], X1s, NG, NGb, epsT, epsb, nslot=4, ysrc=ZG.src, ident=(C['ident_bf'], C['identb']), npol=6)
        for i in range(L // 128):
            xo, xob = op.tile(i * 128, 0, nxt=((i + 1) * 128 if (i + 1) * 128 < L else None))
            kb.op('pool', lambda e, xo=xo, i=i: e.dma_start(out=OUT[i * 128:(i + 1) * 128, :], in_=xo[:, :]), reads=[xob], dma=True, final=True)
        kb.barrier(C['bar'][:])


RT_PARAMS = {'logit': [128, 4], 'diffT': [128, 2, 128], 'mask01T': [128, 2, 128], 'posxi': [128, 2, 128], 'poszeta': [128, 2]}


def part2(kb, nc, IN, YG, X1s, ZD, C):
    debug = False
    sil_in, adawg, adabr, npostb, adaw1, adab1, normw1, win1, ropec, ropes = (IN[k_] for k_ in (
        'sil_in', 'adawg0', 'adabr0', 'npostb0', 'adaw1', 'adab1', 'normw1', 'win1', 'ropec', 'ropes'))
    xin, wout0 = IN['xin'], IN['wout0p']
    RP = {k_: IN['r_' + k_] for k_ in RT_PARAMS}
    X1 = X1s
    Z = ZD
    QKV = dram_tmp(nc, 'QKV', [LT, 768], BF16, debug)
    GS = dram_tmp(nc, 'GS', [LT, 256], F32, debug)
    bar, ident_f, identfb, ident_bf, identb, epsT, epsb, ones1, ones1b = (C[k_] for k_ in (
        'bar', 'ident_f', 'identfb', 'ident_bf', 'identb', 'epsT', 'epsb', 'ones1', 'ones1b'))
    with ExitStack() as pa:
        sil = kb.sb('sil', [128, 8, 2], F32, pa)
        silb = Buf(strict=True)
        kb.op('sp', lambda e: e.dma_start(out=sil[:], in_=sil_in), writes=[silb], dma=True)
        kb.op('act', lambda e: e.activation(out=sil[:], in_=sil[:], func=AF.Silu), reads=[silb], writes=[silb])
        NG = kb.sb('NG', [128, 2, D], F32, pa)
        mod1 = kb.sb('mod1', [128, 16, 2], F32, pa)
        G1 = kb.sb('G1', [128, 8, 2], F32, pa)
        with ExitStack() as pg0:
            NG, NGb = gate_rows(kb, pg0, nc, sil, silb, adawg, adabr, npostb, ones1, ones1b, pg0, NG=NG)
            mod1, mod1b = adaln_vectors(kb, pg0, nc, sil_in, adaw1, adab1, normw1, 16, mod_tile=mod1)
            nw = kb.sb('nw1', [128, 8], F32, pg0)
            nwb, = load_consts(kb, [(nw, normw1)])
            G1buf = Buf(strict=True)
            for n in range(2):
                kb.op('dve', lambda e, n=n: e.scalar_tensor_tensor(out=G1[:, :, n], in0=mod1[:, 8:16, n], scalar=1.0, in1=nw[:, :], op0=ALU.add, op1=ALU.mult),
                      reads=[mod1b, nwb], writes=[G1buf])
            kb.barrier(bar[:])
        Gb = {'G': G1buf, 'eps': epsT, 'epsb': epsb}
        op0 = OutProj(kb, pa, nc, None, wout0, xin, NG, NGb, epsT, epsb, nslot=3, ysrc=YG.src, ident=(ident_bf, identb))
        w1, w1b = load_weight_bf16(kb, pa, nc, 'w1_bf', win1, D, pa)
        for k in range(8):
            kb.op('pool', lambda e, k=k: e.tensor_scalar(out=w1[:, k, 256:512], in0=w1[:, k, 256:512], scalar1=float(128 ** -0.5), scalar2=None, op0=ALU.mult), reads=[w1b], writes=[w1b])
        pqk = Ring(kb, 'pqk', [128, 512], F32, 2, psum=True, es=pa)
        csr = Ring(kb, 'cs', [128, 2, 64], F32, 2, es=pa)
        qkvr = Ring(kb, 'qkvt', [128, 768], BF16, 2, es=pa)
        rtmp = Ring(kb, 'rtmp', [128, 4, 64], F32, 4, es=pa)
        gsr = Ring(kb, 'gst', [128, 256], F32, 2, es=pa)

        rlist = [t0_ + sb_ * 128 for (t0_, T_) in TILES for sb_ in range(T_ // 128)]

        def get_x(r0):
            n = 1 if r0 < LC else 0
            ix = rlist.index(r0)
            xo, xob = op0.tile(r0, n, nxt=(rlist[ix + 1] if ix + 1 < len(rlist) else None))
            if r0 >= LC:
                kb.op('pool', lambda e, xo=xo, r0=r0: e.dma_start(out=X1[r0 - LC:r0 - LC + 128, :], in_=xo[:, :]), reads=[xob], dma=True)
            return xo, xob

        def emit_tile(t0, T, hT, hb):
            for sub in range(T // 128):
                r0 = t0 + sub * 128
                psA, psAb = pqk.next()
                psB, psBb = pqk.next()
                for half, (ps, psb) in enumerate([(psA, psAb), (psB, psBb)]):
                    for k in range(8):
                        kb.op('pe', lambda e, ps=ps, k=k, half=half, sub=sub: e.matmul(ps[:, :], lhsT=hT[:, k, sub * 128:(sub + 1) * 128], rhs=w1[:, k, half * 512:(half + 1) * 512],
                                                                                     start=(k == 0), stop=(k == 7)), reads=[hb, w1b], writes=[psb])
                qkv, qkvb = qkvr.next()
                if r0 >= LC:
                    cs, csb = csr.next()
                    kb.op('sp', lambda e, cs=cs, r0=r0: e.dma_start(out=cs[:, 0, :], in_=ropec[r0 - LC:r0 - LC + 128, :]), writes=[csb], dma=True)
                    kb.op('sp', lambda e, cs=cs, r0=r0: e.dma_start(out=cs[:, 1, :], in_=ropes[r0 - LC:r0 - LC + 128, :]), writes=[csb], dma=True)
                    pv_ = psA[:, :].rearrange("p (g h f) -> p g h f", g=4, h=2)
                    ov_ = qkv[:, 0:512].rearrange("p (g h f) -> p g h f", g=4, h=2)
                    cb_ = cs[:, 0, :].unsqueeze(1).to_broadcast([128, 4, 64])
                    sb_ = cs[:, 1, :].unsqueeze(1).to_broadcast([128, 4, 64])
                    ta, tab = rtmp.next()
                    tb, tbb = rtmp.next()
                    kb.op('dve', lambda e, ta=ta, pv_=pv_, cb_=cb_: e.tensor_tensor(out=ta[:, :, :], in0=pv_[:, :, 0, :], in1=cb_, op=ALU.mult), reads=[psAb, csb], writes=[tab])
                    kb.op('dve', lambda e, tb=tb, pv_=pv_, sb_=sb_: e.tensor_tensor(out=tb[:, :, :], in0=pv_[:, :, 1, :], in1=sb_, op=ALU.mult), reads=[psAb, csb], writes=[tbb])
                    kb.op('pool', lambda e, ta=ta, tb=tb, ov_=ov_: e.tensor_tensor(out=ov_[:, :, 0, :], in0=ta[:, :, :], in1=tb[:, :, :], op=ALU.subtract), reads=[tab, tbb], writes=[qkvb])
                    tc_, tcb = rtmp.next()
                    td, tdb = rtmp.next()
                    kb.op('dve', lambda e, tc_=tc_, pv_=pv_, sb_=sb_: e.tensor_tensor(out=tc_[:, :, :], in0=pv_[:, :, 0, :], in1=sb_, op=ALU.mult), reads=[psAb, csb], writes=[tcb])
                    kb.op('dve', lambda e, td=td, pv_=pv_, cb_=cb_: e.tensor_tensor(out=td[:, :, :], in0=pv_[:, :, 1, :], in1=cb_, op=ALU.mult), reads=[psAb, csb], writes=[tdb])
                    kb.op('pool', lambda e, tc_=tc_, td=td, ov_=ov_: e.tensor_tensor(out=ov_[:, :, 1, :], in0=tc_[:, :, :], in1=td[:, :, :], op=ALU.add), reads=[tcb, tdb], writes=[qkvb])
                else:
                    kb.op('act', lambda e, qkv=qkv, psA=psA: e.activation(out=qkv[:, 0:512], in_=psA[:, :], func=AF.Copy), reads=[psAb], writes=[qkvb])
                kb.op('act', lambda e, qkv=qkv, psB=psB: e.activation(out=qkv[:, 512:768], in_=psB[:, 0:256], func=AF.Copy), reads=[psBb], writes=[qkvb])
                gs, gsb = gsr.next()
                kb.op('act', lambda e, gs=gs, psB=psB: e.activation(out=gs[:, :], in_=psB[:, 256:512], func=AF.Silu), reads=[psBb], writes=[gsb])
                kb.op('pool', lambda e, qkv=qkv, r0=r0: e.dma_start(out=QKV[r0:r0 + 128, :], in_=qkv[:, :]), reads=[qkvb], dma=True)
                kb.op('pool', lambda e, gs=gs, r0=r0: e.dma_start(out=GS[r0:r0 + 128, :], in_=gs[:, :]), reads=[gsb], dma=True)

        phase_proj(kb, nc, pa, None, None, G1, mod1, Gb, ident_bf, identb, emit_tile, get_x=get_x)
        kb.barrier(bar[:])
    with ExitStack() as pr:
        phase_ret(kb, nc, pr, QKV, GS, Z, RP, ident_bf, identb, epsT, epsb)
        kb.barrier(bar[:])


def phase_ret(kb, nc, es, QKV, GS, Z, RP, ident_bf, identb, epsT, epsb):
    def const(name, shape, src):
        t = kb.sb(name, shape, F32, es)
        b, = load_consts(kb, [(t, src)])
        return t, b
    lgt, lgtb = const('lgt', [128, 4], RP['logit'])
    lgtb.strict = True
    diffT, diffTb = const('diffT', [128, 2, 128], RP['diffT'])
    m01, m01b = const('m01', [128, 2, 128], RP['mask01T'])
    pxi, pxib = const('pxi', [128, 2, 128], RP['posxi'])
    pze, pzeb = const('pze', [128, 2], RP['poszeta'])
    kb.op('act', lambda e: e.activation(out=lgt[:, :], in_=lgt[:, :], func=AF.Exp, scale=-1.0), reads=[lgtb], writes=[lgtb])
    kb.op('dve', lambda e: e.tensor_scalar(out=lgt[:, :], in0=lgt[:, :], scalar1=1.0, scalar2=None, op0=ALU.add), reads=[lgtb], writes=[lgtb])
    kb.op('act', lambda e: e.activation(out=lgt[:, :], in_=lgt[:, :], func=AF.Ln), reads=[lgtb], writes=[lgtb])
    kb.op('dve', lambda e: e.tensor_scalar(out=lgt[:, :], in0=lgt[:, :], scalar1=-1.0, scalar2=None, op0=ALU.mult), reads=[lgtb], writes=[lgtb])
    dmt = kb.sb('dmt', [128, 4, 128], BF16, es)
    xib = kb.sb('xib', [128, 4, 128], F32, es)
    zet = kb.sb('zet', [128, 8], F32, es)
    tabb = Buf(strict=True)
    tmpd = kb.sb('tmpd', [128, 128], F32, es)
    tmpdb = Buf()
    for hl in range(2):
        for dr in range(2):
            j = 2 * hl + dr
            kb.op('act', lambda e, j=j, dr=dr: e.activation(out=tmpd[:, :], in_=diffT[:, dr, :], func=AF.Exp, scale=lgt[:, j:j + 1]), reads=[diffTb, lgtb], writes=[tmpdb])
            kb.op('dve', lambda e, j=j, dr=dr: e.tensor_tensor(out=dmt[:, j, :], in0=tmpd[:, :], in1=m01[:, dr, :], op=ALU.mult), reads=[tmpdb, m01b], writes=[tabb])
            kb.op('act', lambda e, j=j, dr=dr: e.activation(out=xib[:, j, :], in_=pxi[:, dr, :], func=AF.Exp, scale=lgt[:, j:j + 1]), reads=[pxib, lgtb], writes=[tabb])
            kb.op('act', lambda e, j=j, dr=dr: e.activation(out=zet[:, j:j + 1], in_=pze[:, dr:dr + 1], func=AF.Exp, scale=lgt[:, j:j + 1]), reads=[pzeb, lgtb], writes=[tabb])
            kb.op('act', lambda e, j=j: e.activation(out=zet[:, 4 + j:5 + j], in_=lgt[:, j:j + 1], func=AF.Exp, scale=128.0), reads=[lgtb], writes=[tabb])
    oacc = kb.sb('oacc', [128, NCH, 256], F32, es)
    oaccb = [Buf() for _ in range(NCH)]
    kb.op('pool', lambda e: e.memset(oacc[:], 0.0), writes=oaccb)
    R = kb.sb('Rst', [128, 2, 128], F32, es)
    Rbf = kb.sb('Rbf', [128, 2, 128], BF16, es)
    Rb = [Buf(), Buf()]
    qr = Ring(kb, 'rq', [128, 768], BF16, 3, es=es)
    qtr = Ring(kb, 'rqT', [128, 128], BF16, 3, es=es)
    ktr_ = Ring(kb, 'rkT', [128, 128], BF16, 3, es=es)
    qxr = Ring(kb, 'rqx', [128, 128], BF16, 3, es=es)
    kzr = Ring(kb, 'rkz', [128, 128], BF16, 3, es=es)
    sdr = Ring(kb, 'rsd', [128, 128], BF16, 3, es=es)
    ptT = Ring(kb, 'rptT', [128, 1024], BF16, 2, psum=True, es=es)
    pS = Ring(kb, 'rpS', [128, 512], F32, 2, psum=True, es=es)
    pO = Ring(kb, 'rpO', [128, 512], F32, 2, psum=True, es=es)
    pR = Ring(kb, 'rpR', [128, 512], F32, 2, psum=True, es=es)
    gsr = Ring(kb, 'rgs', [128, 256], F32, 2, es=es)
    zr = Ring(kb, 'rz', [128, 256], F32, 2, es=es)
    sst = Ring(kb, 'rss', [128, 8], F32, 4, es=es, strict=True)
    junk = kb.sb('rjunk', [128, 128], BF16, es)
    junkb = Buf()
    for dr in range(2):
        kb.op('dve', lambda e: e.memset(R[:], 0.0), writes=Rb)
        kb.op('dve', lambda e: e.memset(Rbf[:], 0.0), writes=Rb)
        for c in CHUNK_ORDER[dr]:
            r0 = c * 128
            q, qb = qr.next()
            kb.op('sp', lambda e, q=q, r0=r0: e.dma_start(out=q[:, :], in_=QKV[r0:r0 + 128, :]), writes=[qb], dma=True)
            for hl in range(2):
                j = 2 * hl + dr
                vt_ = q[:, 512 + hl * 128:512 + (hl + 1) * 128]
                ktok = q[:, 256 + hl * 128:256 + (hl + 1) * 128]
                if c >= 2:
                    pt, ptb = ptT.next()
                    kb.op('pe', lambda e, pt=pt, q=q, hl=hl: e.transpose(out=pt[:, 0:128], in_=q[:, hl * 128:(hl + 1) * 128], identity=ident_bf[:, :]), reads=[qb, identb], writes=[ptb])
                    kb.op('pe', lambda e, pt=pt, ktok=ktok: e.transpose(out=pt[:, 128:256], in_=ktok, identity=ident_bf[:, :]), reads=[qb, identb], writes=[ptb])
                    qT, qTb = qtr.next()
                    kT, kTb = ktr_.next()
                    qx, qxb = qxr.next()
                    kb.op('act', lambda e, qT=qT, pt=pt: e.activation(out=qT[:, :], in_=pt[:, 0:128], func=AF.Copy), reads=[ptb], writes=[qTb])
                    kb.op('dve', lambda e, qx=qx, pt=pt, j=j: e.tensor_tensor(out=qx[:, :], in0=pt[:, 0:128], in1=xib[:, j, :], op=ALU.mult), reads=[ptb, tabb], writes=[qxb])
                    kb.op('act', lambda e, kT=kT, pt=pt: e.activation(out=kT[:, :], in_=pt[:, 128:256], func=AF.Copy), reads=[ptb], writes=[kTb])
                    ps, psb = pS.next()
                    kb.op('pe', lambda e, ps=ps, kT=kT, qT=qT: e.matmul(ps[:, 0:128], lhsT=kT[:, :], rhs=qT[:, :], start=True, stop=True), reads=[kTb, qTb], writes=[psb])
                    sd, sdb = sdr.next()
                    kb.op('dve', lambda e, sd=sd, ps=ps, j=j: e.tensor_tensor(out=sd[:, :], in0=ps[:, 0:128], in1=dmt[:, j, :], op=ALU.mult), reads=[psb, tabb], writes=[sdb])
                    po, pob = pO.next()
                    kb.op('pe', lambda e, po=po, sd=sd, vt_=vt_: e.matmul(po[:, 0:128], lhsT=sd[:, :], rhs=vt_, start=True, stop=False), reads=[sdb, qb], writes=[pob])
                    kb.op('pe', lambda e, po=po, qx=qx, hl=hl: e.matmul(po[:, 0:128], lhsT=qx[:, :], rhs=Rbf[:, hl, :], start=False, stop=True), reads=[qxb, Rb[hl]], writes=[pob])
                    if dr == 0:
                        kb.op('act', lambda e, po=po, c=c, hl=hl: e.activation(out=oacc[:, c, hl * 128:(hl + 1) * 128], in_=po[:, 0:128], func=AF.Copy), reads=[pob], writes=[oaccb[c]])
                    else:
                        kb.op('dve', lambda e, po=po, c=c, hl=hl: e.tensor_tensor(out=oacc[:, c, hl * 128:(hl + 1) * 128], in0=po[:, 0:128], in1=oacc[:, c, hl * 128:(hl + 1) * 128], op=ALU.add),
                              reads=[pob, oaccb[c]], writes=[oaccb[c]])
                kz, kzb = kzr.next()
                kb.op('pool', lambda e, kz=kz, ktok=ktok, j=j: e.tensor_scalar(out=kz[:, :], in0=ktok, scalar1=zet[:, j:j + 1], scalar2=None, op0=ALU.mult), reads=[qb, tabb], writes=[kzb])
                pr_, prb = pR.next()
                kb.op('pe', lambda e, pr_=pr_, kz=kz, vt_=vt_: e.matmul(pr_[:, 0:128], lhsT=kz[:, :], rhs=vt_, start=True, stop=True), reads=[kzb, qb], writes=[prb])
                kb.op('dve', lambda e, pr_=pr_, hl=hl, j=j: e.scalar_tensor_tensor(out=R[:, hl, :], in0=R[:, hl, :], scalar=zet[:, 4 + j:5 + j], in1=pr_[:, 0:128], op0=ALU.mult, op1=ALU.add),
                      reads=[prb, tabb, Rb[hl]], writes=[Rb[hl]])
                kb.op('act', lambda e, hl=hl: e.activation(out=Rbf[:, hl, :], in_=R[:, hl, :], func=AF.Copy), reads=[Rb[hl]], writes=[Rb[hl]])
            if dr == 1 and c >= 2:
                gs, gsb = gsr.next()
                kb.op('sp', lambda e, gs=gs, r0=r0: e.dma_start(out=gs[:, :], in_=GS[r0:r0 + 128, :]), writes=[gsb], dma=True)
                zt_, ztb = zr.next()
                for hl in range(2):
                    ss, ssb = sst.next()
                    kb.op('act', lambda e, ss=ss, c=c, hl=hl: e.activation(out=junk[:, :], in_=oacc[:, c, hl * 128:(hl + 1) * 128], func=AF.Square, accum_out=ss[:, 0:1]), reads=[oaccb[c]], writes=[ssb, junkb])
                    kb.op('act', lambda e, ss=ss: e.activation(out=ss[:, 1:2], in_=ss[:, 0:1], func=AF.Sqrt, scale=1.0 / 128, bias=epsT[:, 0:1]), reads=[ssb, epsb], writes=[ssb])
                    kb.op('dve', lambda e, ss=ss: e.reciprocal(out=ss[:, 2:3], in_=ss[:, 1:2]), reads=[ssb], writes=[ssb])
                    kb.op('dve', lambda e, ss=ss, zt_=zt_, gs=gs, c=c, hl=hl: e.scalar_tensor_tensor(out=zt_[:, hl * 128:(hl + 1) * 128], in0=oacc[:, c, hl * 128:(hl + 1) * 128], scalar=ss[:, 2:3],
                                                                                                   in1=gs[:, hl * 128:(hl + 1) * 128], op0=ALU.mult, op1=ALU.mult), reads=[ssb, oaccb[c], gsb], writes=[ztb])
                kb.op('pool', lambda e, zt_=zt_, r0=r0: e.dma_start(out=Z.rows(r0 - LC, 0, 256), in_=zt_[:, :]), reads=[ztb], dma=True)


class ChunkedDram:
    def __init__(self, nc, name, nrows, rows_per, width, ranks=1):
        self.rp = rows_per
        self.n = nrows // rows_per
        assert self.n * rows_per == nrows
        self.ranks = ranks
        self.tiles = [dram_tmp(nc, '%s%d' % (name, i), [ranks * rows_per, width], F32) for i in range(self.n)]
        self.bufs = [Buf() for _ in range(self.n)]

    def rows(self, r0, c0, c1, n=128):
        i, lr = r0 // self.rp, r0 % self.rp
        return self.tiles[i][lr:lr + n, c0:c1]

    def src(self, r0):
        i, lr = r0 // self.rp, r0 % self.rp
        return [self.tiles[i][r * self.rp + lr:r * self.rp + lr + 128, :] for r in range(self.ranks)], [self.bufs[i]]

    def lat_groups(self):
        out = []
        tpc = self.rp // 128
        for i in range(self.n):
            b0, b1 = max(tpc * i - 2, 0), min(tpc * i + tpc - 2, 64)
            if b1 <= b0:
                continue
            lrow = (b0 + 2) * 128 - i * self.rp
            out.append((self.tiles[i][lrow:lrow + (b1 - b0) * 128, 192:256].rearrange("(b a) c -> b a c", a=128), b0, b1))
        return out


GROUPS = [[0, 1, 2, 3], [4, 5, 6, 7]]


def all_gather(kb, src, dst):
    for i in range(src.n):
        kb.op('pool', lambda e, i=i: e.collective_compute("AllGather", ALU.bypass, replica_groups=GROUPS, ins=[src.tiles[i].opt()], outs=[dst.tiles[i].opt()]),
              writes=[dst.bufs[i]], cc=True)


FUSED_INPUTS = {'xin': [LT, D], 'sil_in': [128, 8, 2], 'adaw': [D, 2048], 'adab': [128, 16], 'normw': [128, 8], 'wfm': [D, NFM], 'wg': [D, 256],
                'cs64': [64, 128], 'ident': [128, 128],
                'wout0p': [D, D], 'adawg0': [D, D], 'adabr0': [1, D], 'npostb0': [128, D], 'adaw1': [D, 2048], 'adab1': [128, 16], 'normw1': [128, 8],
                'win1': [D, D], 'ropec': [L, 64], 'ropes': [L, 64],
                'wout1': [D, D], 'adawg1': [D, D], 'adabr1': [1, D], 'npostb1': [128, D]}


def build_fused():
    nc = bass.Bass("TRN2", target_bir_lowering=False)
    kb = KB(nc)
    IN = {k_: dram_in(nc, k_, shp) for k_, shp in FUSED_INPUTS.items()}
    for k_, shp in RW_PARAMS.items():
        IN['p_' + k_] = dram_in(nc, 'p_' + k_, shp)
    for k_, shp in FN_PARAMS.items():
        IN['f_' + k_] = dram_in(nc, 'f_' + k_, shp)
    for k_, shp in RT_PARAMS.items():
        IN['r_' + k_] = dram_in(nc, 'r_' + k_, shp)
    OUT = dram_out(nc, 'out', [L, D])
    YB = ChunkedDram(nc, 'Yb', LT, 768, 256)
    YG = ChunkedDram(nc, 'Yg', LT, 768, 256, ranks=4)
    ZB = ChunkedDram(nc, 'Zb', L, 512, 256)
    ZG = ChunkedDram(nc, 'Zg', L, 512, 256, ranks=4)
    X1s = dram_tmp(nc, 'X1s', [L, D], F32)
    C = {}
    C['bar'] = kb.sb('bar', [128, 1], F32)
    C['ident_f'] = kb.sb('ident_f', [128, 128], F32)
    C['ident_bf'] = kb.sb('ident_bf', [128, 128], BF16)
    C['identfb'], = load_consts(kb, [(C['ident_f'], IN['ident'])])
    C['identb'] = Buf()
    kb.op('dve', lambda e: e.tensor_copy(out=C['ident_bf'][:], in_=C['ident_f'][:]), reads=[C['identfb']], writes=[C['identb']])
    C['epsT'] = kb.sb('epsT', [128, 1], F32)
    C['epsb'] = Buf()
    kb.op('dve', lambda e: e.memset(C['epsT'][:], EPS), writes=[C['epsb']])
    C['ones1'] = kb.sb('ones1', [1, 128], F32)
    C['ones1b'] = Buf()
    kb.op('dve', lambda e: e.memset(C['ones1'][:], 1.0), writes=[C['ones1b']])
    part1(kb, nc, IN, YB, C)
    all_gather(kb, YB, YG)
    part2(kb, nc, IN, YG, X1s, ZB, C)
    all_gather(kb, ZB, ZG)
    part3(kb, nc, IN, ZG, X1s, OUT, C)
    return nc, kb


def l1_inputs(inp, core):
    b, q = core // 4, core % 4
    f = lambda a: np.ascontiguousarray(a, dtype=np.float32)
    xin = np.concatenate([inp['ctx'][b], inp['x'][b]], axis=0)
    sil = np.stack([inp['c'][b].reshape(8,128).T, inp['c_ctx'].reshape(8,128).T], axis=-1)
    adaw = inp['ada_w'][0][:, :2048]
    adab = inp['ada_b'][0][:2048].reshape(16,128).T
    normw = inp['norm_pre'][0].reshape(8,128).T
    W = inp['ev_w_in'][0]
    cols = np.concatenate([np.arange(192)+192*q, 768+np.arange(192)+192*q, 1536+np.arange(192)+192*q,
                           np.arange(2304,2432), np.arange(2432,2560), 3328+64*q+np.arange(64)])
    gcols = np.concatenate([2560+192*q+np.arange(192), 3584+64*q+np.arange(64)])
    c = np.arange(64)
    ang = 2*np.pi*np.outer(c,c)/64
    cs64 = np.concatenate([np.cos(ang), np.sin(ang)], axis=1)
    return dict(xin=f(xin), sil_in=f(sil), adaw=f(adaw), adab=f(adab), normw=f(normw), wfm=f(W[:, cols]), wg=f(W[:, gcols]),
                cs64=f(cs64), ident=np.eye(128, dtype=np.float32)), cols, gcols

def rw_params(inp, core):
    b, q = core // 4, core % 4
    f = lambda a: np.ascontiguousarray(a, dtype=np.float32)
    chs = 192*q + np.arange(192)
    def blk2(v):
        o = np.zeros((128,2), np.float32); o[:,0] = v[:128]; o[:64,1] = v[128:]; return o
    mu = inp['ev_mu'][0]
    mub = np.zeros((128,8), np.float32)
    for j, base in enumerate([0, 768, 1536]):
        m2 = blk2(mu[base+chs]); mub[:, 2*j] = m2[:,0]; mub[:, 2*j+1] = m2[:,1]
    mub[:, 6] = mu[2304:2432]; mub[:, 7] = mu[2432:2560]
    p = np.arange(128)
    lanem = np.stack([(p%4==0),(p%4==1),(p%4==2),(p%4==3),(p%2==0),(p%2==1)],axis=1).astype(np.float32)
    a0 = np.zeros((128,2,2), np.float32)
    for d in range(2): a0[:, d, :] = blk2(inp['ev_a0'][0][d][chs])
    w2 = np.concatenate([inp['ev_w2'][0][0][:, chs], inp['ev_w2'][0][1][:, chs]], axis=0)
    a2 = np.concatenate([inp['ev_a2'][0][0][:, chs], inp['ev_a2'][0][1][:, chs]], axis=0)
    w0 = np.stack([inp['ev_w0'][0][0][chs], inp['ev_w0'][0][1][chs]])[None]
    s = np.arange(128)[:,None]; t = np.arange(128)[None,:]
    strict = [(s<t), (s>t)]; incl = [(s<=t), (s>=t)]
    blk = lambda bs: (np.arange(128)[:,None]//bs == np.arange(128)[None,:]//bs)
    maskSI2 = np.stack([np.concatenate([strict[d], incl[d]],axis=1) for d in range(2)], axis=1).astype(np.float32)
    maskSI = np.stack([np.concatenate([strict[d] & blk(32), incl[d]],axis=1) for d in range(2)], axis=1).astype(np.float32)
    maskST = np.stack([np.stack([(strict[d] & blk(32)).T, (strict[d] & blk(64) & ~blk(32)).T, (strict[d] & ~blk(64)).T], axis=1) for d in range(2)], axis=1).astype(np.float32)
    cdec = -np.exp(-0.5)
    tri = np.stack([np.concatenate([incl[d], strict[d]],axis=1) for d in range(2)], axis=1).astype(np.float32)*cdec
    eh = np.zeros((128,2,4), np.float32); eh[:64,0,0]=1; eh[64:,0,1]=1; eh[:64,1,2]=1
    obd = np.zeros((128,128), np.float32); obd[:64,:64]=1; obd[64:,64:]=1
    return dict(p_mu=mub, p_lanem=lanem, p_k_k=blk2(inp['ev_k_k'][0][chs]), p_k_a=blk2(inp['ev_k_a'][0][chs]),
                p_r_k=blk2(inp['ev_r_k'][0].reshape(-1)[chs]), p_a0=a0, p_w2=f(w2), p_a2=f(a2), p_w0=f(w0),
                p_maskSI=f(maskSI), p_maskSI2=f(maskSI2), p_maskST=f(maskST), p_tri=f(tri), p_ehead=eh, p_ones_bd=obd,
                p_lnx_g=f(np.tile(inp['ev_lnx_g'][0][chs][None], (128,1))), p_lnx_b=f(np.tile(inp['ev_lnx_b'][0][chs][None], (128,1))))

def fn_params():
    f = lambda a: np.ascontiguousarray(a, dtype=np.float32)
    a = np.arange(128); b = np.arange(64)
    ang128 = 2*np.pi*np.outer(a,a)/128
    tw = 2*np.pi*np.outer(a, b)/8192
    ang64 = 2*np.pi*np.outer(b,b)/64
    fl1 = np.concatenate([np.cos(ang64), np.sin(ang64)], axis=0)
    fl2 = np.concatenate([-np.sin(ang64), np.cos(ang64)], axis=0)
    l = np.arange(256); ang256 = 2*np.pi*np.outer(l,l)/256
    c256 = np.cos(ang256).reshape(2,128,256).transpose(1,0,2); ns256 = (-np.sin(ang256)).reshape(2,128,256).transpose(1,0,2)
    return dict(f_c128=f(np.cos(ang128)), f_ns128=f(-np.sin(ang128)), f_twc=f(np.cos(tw)), f_tws=f(np.sin(tw)), f_fl1=f(fl1), f_fl2=f(fl2),
                f_c256=f(c256), f_ns256=f(ns256))


def gate_inputs(inp, b, layer):
    f = lambda a: np.ascontiguousarray(a, dtype=np.float32)
    sil = np.stack([inp['c'][b].reshape(8,128).T, inp['c_ctx'].reshape(8,128).T], axis=-1)
    return dict(sil_in=f(sil), adawg=f(inp['ada_w'][layer][:, 2048:3072]), adabr=f(inp['ada_b'][layer][2048:3072][None]),
                npostb=f(np.tile(inp['norm_post'][layer][None], (128,1))))
def l3_inputs(inp, core, z_full, x1_full):
    b, qtr = core // 4, core % 4
    f = lambda a: np.ascontiguousarray(a, dtype=np.float32)
    sl = slice(2048*qtr, 2048*(qtr+1))
    m = dict(zT=f(z_full[b][sl].T), x1=f(x1_full[b][sl]), wout=f(inp['od_w_out'][0]))
    m.update(gate_inputs(inp, b, 1))
    return m


def l2_inputs(inp, core, y_full):
    b, p = core // 4, core % 4
    f = lambda a: np.ascontiguousarray(a, dtype=np.float32)
    m = dict(xin=f(np.concatenate([inp['ctx'][b], inp['x'][b]], axis=0)), wout=f(inp['ev_w_out'][0]))
    if y_full is not None:
        m['yT'] = f(y_full[b].T)
    m.update(gate_inputs(inp, b, 0))
    m['adaw1'] = f(inp['ada_w'][1][:, :2048]); m['adab1'] = f(inp['ada_b'][1][:2048].reshape(16,128).T); m['normw1'] = f(inp['norm_pre'][1].reshape(8,128).T)
    cols = np.concatenate([off + 256*p + np.arange(256) for off in (0, 1024, 2048, 3072)])
    m['win1'] = f(inp['od_w_in'][0][:, cols])
    pos = np.arange(8192); row = pos//64; col = pos%64
    inv = 10000.0 ** (-np.arange(0, 64, 2, dtype=np.float32)/64)
    ang = np.concatenate([row[:,None]*inv[None], col[:,None]*inv[None]], axis=1).astype(np.float32)
    m['ropec'] = f(np.cos(ang)); m['ropes'] = f(np.sin(ang))
    m['ident'] = np.eye(128, dtype=np.float32)
    lg = inp['od_decay_logit'][0]
    logit = np.zeros((128,4), np.float32)
    for hl in range(2):
        for dr in range(2): logit[:, 2*hl+dr] = lg[dr][2*p+hl]
    j = np.arange(128)[:,None]; i = np.arange(128)[None,:]
    diffT = np.stack([(i-j)*np.ones((128,128)), (j-i)*np.ones((128,128))], axis=1)
    mask = np.stack([(i>=j), (j>i)], axis=1)
    posxi = np.stack([np.tile((np.arange(128)+1)[None], (128,1)), np.tile((128-np.arange(128))[None], (128,1))], axis=1)
    pze = np.stack([127-np.arange(128), np.arange(128)], axis=1)
    m.update(r_logit=logit, r_diffT=f(diffT), r_mask01T=f(mask), r_posxi=f(posxi), r_poszeta=f(pze))
    return m


def fused_inputs(inp, core):
    b, q = core // 4, core % 4
    f = lambda a: np.ascontiguousarray(a, dtype=np.float32)
    m, _, _ = l1_inputs(inp, core)
    m.update(rw_params(inp, core))
    m.update(fn_params())
    m2 = l2_inputs(inp, core, None)
    g0 = gate_inputs(inp, b, 0)
    g1 = gate_inputs(inp, b, 1)
    perm = np.concatenate([np.concatenate([192 * r + np.arange(192), 768 + 64 * r + np.arange(64)]) for r in range(4)])
    m['wout0p'] = f(inp['ev_w_out'][0][perm])
    m['adawg0'], m['adabr0'], m['npostb0'] = g0['adawg'], g0['adabr'], g0['npostb']
    m['adawg1'], m['adabr1'], m['npostb1'] = g1['adawg'], g1['adabr'], g1['npostb']
    m['wout1'] = f(inp['od_w_out'][0])
    for k_ in ('adaw1', 'adab1', 'normw1', 'win1', 'ropec', 'ropes', 'r_logit', 'r_diffT', 'r_mask01T', 'r_posxi', 'r_poszeta'):
        m[k_] = m2[k_]
    return m


def kernel(**inputs):
    inp = {k: np.asarray(v) for k, v in inputs.items()}
    cores = list(range(8))
    nc, kb = build_fused()
    kb.emit()
    maps = [fused_inputs(inp, core) for core in cores]
    res = run_bass_kernel_spmd(nc, maps, core_ids=cores).results
    return np.stack([res[0]['out'], res[4]['out']]).astype(np.float32)
```

```python
import numpy as np
import concourse.bass as bass
import concourse.mybir as mybir
from contextlib import ExitStack
from concourse.bass_utils import run_bass_kernel_spmd

F32 = mybir.dt.float32
BF16 = mybir.dt.bfloat16
ALU = mybir.AluOpType
AF = mybir.ActivationFunctionType

ENGS = ['pe', 'act', 'dve', 'pool', 'sp']
DBG = {}
NDSEM = 8


class Buf:
    __slots__ = ('name', 'w', 'r', 'excl', 'strict')

    def __init__(self, name='', excl=False, strict=False):
        self.name = name
        self.w = None
        self.r = {}
        self.excl = excl
        self.strict = strict


class Op:
    __slots__ = ('eng', 'fn', 'deps', 'signal', 'sig', 'sem', 'target', 'isdma', 'final', 'cc')


class _Rec:
    def __init__(self):
        self.call = None

    def __getattr__(self, name):
        def f(*a, **k):
            self.call = (name, a, k)
            return self
        return f


def _flat(xs):
    out = []
    for x in xs:
        if isinstance(x, (list, tuple)):
            out.extend(_flat(x))
        else:
            out.append(x)
    return out


class KB:
    def __init__(self, nc):
        self.nc = nc
        self.ops = {e: [] for e in ENGS}
        self.phase = Buf('phase')
        self.es = ExitStack()
        self.nops = 0

    def sb(self, name, shape, dtype, es=None):
        self.nalloc = getattr(self, 'nalloc', 0) + 1
        return (es or self.es).enter_context(self.nc.sbuf_tensor('s%d_%s' % (self.nalloc, name), list(shape), dtype))

    def ps(self, name, shape, dtype=F32, es=None):
        self.nalloc = getattr(self, 'nalloc', 0) + 1
        return (es or self.es).enter_context(self.nc.psum_tensor('p%d_%s' % (self.nalloc, name), list(shape), dtype))

    def op(self, eng, fn, reads=(), writes=(), dma=False, final=False, nophase=False, cc=False):
        dma = dma or cc
        o = Op()
        rec = _Rec()
        fn(rec)
        assert rec.call is not None
        o.eng = eng; o.fn = rec.call; o.isdma = dma; o.signal = dma; o.deps = []; o.sig = 0
        o.sem = None; o.target = 0; o.final = final; o.cc = cc
        reads = _flat(reads)
        writes = _flat(writes)
        for b in list(reads):
            if b.excl:
                reads.remove(b)
                if b not in writes:
                    writes.append(b)
        if not nophase:
            reads.append(self.phase)
        deps = {}
        sdeps = set()
        for b in reads:
            if b.w is not None:
                deps[id(b.w)] = b.w
                if b.strict:
                    sdeps.add(id(b.w))
        for b in writes:
            if b.w is not None:
                deps[id(b.w)] = b.w
                if b.strict:
                    sdeps.add(id(b.w))
            for r in b.r.values():
                deps[id(r)] = r
                if b.strict:
                    sdeps.add(id(r))
        for d in deps.values():
            if d is o:
                continue
            if d.isdma or dma or d.eng != eng or id(d) in sdeps or (eng != 'pe' and DBG.get('strict_all', True)):
                o.deps.append(d)
                d.signal = True
        key = ('dma', self.nops) if dma else eng
        for b in reads:
            b.r[key] = o
        for b in writes:
            b.w = o
            b.r = {}
        self.ops[eng].append(o)
        self.nops += 1
        return o

    def barrier(self, tile_ap):
        self.op('dve', lambda e: e.memset(tile_ap, 0.0), writes=[self.phase], nophase=True)

    def emit(self):
        nc = self.nc
        es = self.es
        csem = {}
        for e in ['pe', 'act', 'dve', 'pool']:
            csem[e] = es.enter_context(nc.semaphore('c_' + e))
        dsem = {}
        for e in ENGS:
            if any(o.isdma for o in self.ops[e]):
                dsem[e] = [es.enter_context(nc.semaphore('d_%s%d' % (e, i))) for i in range(NDSEM)]
        for e in ENGS:
            cnt = 0
            nd = 0
            hist = []
            for o in self.ops[e]:
                if o.cc:
                    self.ncc = getattr(self, 'ncc', 0) + 1
                    o.sem = es.enter_context(nc.semaphore('ccs%d' % self.ncc))
                    o.target = 1
                elif o.isdma:
                    slot = nd % NDSEM
                    o.sem = dsem[e][slot]
                    o.target = 16 * (nd // NDSEM + 1)
                    if nd >= NDSEM:
                        o.deps.append(hist[nd - NDSEM])
                    hist.append(o)
                    nd += 1
                elif o.signal:
                    cnt += 1
                    o.sig = cnt
                    o.sem = csem[e]
                    o.target = cnt
        finals = [o for e in ENGS for o in self.ops[e] if o.final]

        def run(engname):
            def body(e):
                waited = {}
                for o in self.ops[engname]:
                    need = {}
                    for d in o.deps:
                        k = id(d.sem)
                        if waited.get(k, 0) < d.target and need.get(k, (None, 0))[1] < d.target:
                            need[k] = (d.sem, d.target)
                    for k, (sem, tgt) in need.items():
                        e.wait_ge(sem, tgt)
                        waited[k] = tgt
                    nm_, a_, k_ = o.fn
                    ins = getattr(e, nm_)(*a_, **k_)
                    if o.signal:
                        ins.then_inc(o.sem, 16 if (o.isdma and not o.cc) else 1)
                if engname == 'sp':
                    for d in finals:
                        k = id(d.sem)
                        if waited.get(k, 0) < d.target:
                            e.wait_ge(d.sem, d.target)
                            waited[k] = d.target
            return body

        with nc.Block() as block:
            block.tensor(run('pe'))
            block.scalar(run('act'))
            block.vector(run('dve'))
            block.gpsimd(run('pool'))
            block.sync(run('sp'))


class Ring:
    def __init__(self, kb, name, shape, dtype, n, psum=False, es=None, strict=False):
        self.n = n
        self.i = 0
        self.slots = []
        for j in range(n):
            t = (kb.ps if psum else kb.sb)('%s%d' % (name, j), shape, dtype, es=es)
            b = Buf('%s%d' % (name, j), excl=psum, strict=strict)
            self.slots.append((t, b))
            if not psum and DBG.get('init_rings', True):
                kb.op('pool', lambda e, t=t: e.memset(t[:], 0.0), writes=[b])

    def next(self):
        s = self.slots[self.i % self.n]
        self.i += 1
        return s


D = 1024
L = 8192
LC = 256
LT = L + LC
NCH = LT // 128
EPS = 1e-6
GN_EPS = 64e-5
FM_BLOCKS = {'r01': (0, 128), 'r2': (128, 64), 'k01': (192, 128), 'k2': (320, 64),
             'v01': (384, 128), 'v2': (512, 64), 'wl': (576, 128), 'al': (704, 128), 'four': (832, 64)}
NFM = 896
TILES = [(0, 256)] + [(256 + 512 * i, 512) for i in range(16)]


def dram_in(nc, name, shape, dtype=F32):
    return nc.dram_tensor(name, list(shape), dtype, kind="ExternalInput").ap()


def dram_out(nc, name, shape, dtype=F32):
    return nc.dram_tensor(name, list(shape), dtype, kind="ExternalOutput").ap()


def dram_tmp(nc, name, shape, dtype=F32, debug=False):
    return nc.dram_tensor(name, list(shape), dtype, kind="ExternalOutput" if debug else "Internal").ap()


def load_consts(kb, items, eng='sp'):
    bufs = []
    for t, src in items:
        b = Buf()
        kb.op(eng, (lambda e, t=t, src=src: e.dma_start(out=t[:], in_=src)), writes=[b], dma=True)
        bufs.append(b)
    return bufs


def adaln_vectors(kb, es, nc, sil_in, adaw, adab, normw, ncolblk, mod_tile=None):
    sil = kb.sb('sil', [128, 8, 2], F32, es)
    silb = Buf(strict=True)
    kb.op('sp', lambda e: e.dma_start(out=sil[:], in_=sil_in), writes=[silb], dma=True)
    kb.op('act', lambda e: e.activation(out=sil[:], in_=sil[:], func=AF.Silu), reads=[silb], writes=[silb])
    adb = kb.sb('adb', [128, ncolblk], F32, es)
    adbb = Buf(strict=True)
    kb.op('sp', lambda e: e.dma_start(out=adb[:], in_=adab), writes=[adbb], dma=True)
    mod = mod_tile if mod_tile is not None else kb.sb('mod', [128, ncolblk, 2], F32, es)
    modb = Buf(strict=True)
    psm = kb.ps('psmod', [128, ncolblk, 2], F32, es)
    psb = Buf(excl=True)
    wt = kb.sb('adaw', [128, 8, ncolblk * 128], F32, es)
    wb = Buf()
    for k in range(8):
        kb.op('sp', lambda e, k=k: e.dma_start(out=wt[:, k, :], in_=adaw[k * 128:(k + 1) * 128, :]), writes=[wb], dma=True)
    for cb in range(ncolblk):
        for k in range(8):
            kb.op('pe', lambda e, k=k, cb=cb: e.matmul(psm[:, cb, :], lhsT=wt[:, k, cb * 128:(cb + 1) * 128],
                                                      rhs=sil[:, k, :], start=(k == 0), stop=(k == 7)),
                  reads=[wb, silb], writes=[psb])
    for n in range(2):
        kb.op('dve', lambda e, n=n: e.tensor_tensor(out=mod[:, :, n], in0=psm[:, :, n], in1=adb[:, :], op=ALU.add),
              reads=[psb, adbb], writes=[modb])
    return mod, modb


def phase_proj(kb, nc, es, xin, Wsrc_list, G, shiftv, Gb, ident_bf, identb, emit_tile, nmod=2, get_x=None, shift_off=0):
    xring = Ring(kb, 'xt', [128, D], F32, 3, es=es)
    xnring = Ring(kb, 'xn', [128, D], BF16, 2, es=es)
    stat = Ring(kb, 'stat', [128, 4], F32, 4, es=es, strict=True)
    junk = kb.sb('junk', [128, D], BF16, es)
    junkb = Buf()
    hring = Ring(kb, 'hT', [128, 8, 512], BF16, 2, es=es)
    ptr = Ring(kb, 'ptr', [128, 8, 128], BF16, 2, psum=True, es=es)
    for (t0, T) in TILES:
        n = 1 if t0 < LC else 0
        hT, hb = hring.next()
        for sub in range(T // 128):
            r0 = t0 + sub * 128
            if get_x is not None:
                xt, xb = get_x(r0)
            else:
                xt, xb = xring.next()
                kb.op('sp', lambda e, xt=xt, r0=r0: e.dma_start(out=xt[:], in_=xin[r0:r0 + 128, :]), writes=[xb], dma=True)
            st, sb_ = stat.next()
            kb.op('act', lambda e, xt=xt, st=st: e.activation(out=junk[:], in_=xt[:], func=AF.Square, accum_out=st[:, 0:1]),
                  reads=[xb], writes=[junkb, sb_])
            kb.op('act', lambda e, st=st: e.activation(out=st[:, 1:2], in_=st[:, 0:1], func=AF.Sqrt, scale=1.0 / D, bias=Gb['eps'][:, 0:1]),
                  reads=[sb_, Gb['epsb']], writes=[sb_])
            kb.op('dve', lambda e, st=st: e.reciprocal(out=st[:, 2:3], in_=st[:, 1:2]), reads=[sb_], writes=[sb_])
            xn, xnb = xnring.next()
            kb.op('dve', lambda e, xn=xn, xt=xt, st=st: e.tensor_scalar(out=xn[:], in0=xt[:], scalar1=st[:, 2:3], scalar2=None, op0=ALU.mult),
                  reads=[xb, sb_], writes=[xnb])
            pt, pb = ptr.next()
            for k in range(8):
                kb.op('pe', lambda e, pt=pt, xn=xn, k=k: e.transpose(out=pt[:, k, :], in_=xn[:, k * 128:(k + 1) * 128], identity=ident_bf[:]),
                      reads=[xnb, identb], writes=[pb])
            for k in range(8):
                if k % 2 == 0:
                    kb.op('act', lambda e, pt=pt, hT=hT, k=k, sub=sub, n=n: e.activation(
                        out=hT[:, k, sub * 128:(sub + 1) * 128], in_=pt[:, k, :], func=AF.Identity,
                        scale=G[:, k, n:n + 1], bias=shiftv[:, shift_off + k, n:n + 1]), reads=[pb, Gb['G']], writes=[hb])
                else:
                    kb.op('dve', lambda e, pt=pt, hT=hT, k=k, sub=sub, n=n: e.tensor_scalar(
                        out=hT[:, k, sub * 128:(sub + 1) * 128], in0=pt[:, k, :], scalar1=G[:, k, n:n + 1],
                        scalar2=shiftv[:, shift_off + k, n:n + 1], op0=ALU.mult, op1=ALU.add), reads=[pb, Gb['G']], writes=[hb])
        emit_tile(t0, T, hT, hb)


def load_weight_bf16(kb, es, nc, name, src, ncols, es_keep):
    wbf = kb.sb(name, [128, 8, ncols], BF16, es_keep)
    wb = Buf()
    stg = Ring(kb, name + '_stg', [128, ncols], F32, 2, es=es)
    for k in range(8):
        s, sbuf_ = stg.next()
        kb.op('sp', lambda e, s=s, k=k: e.dma_start(out=s[:], in_=src[k * 128:(k + 1) * 128, :]), writes=[sbuf_], dma=True)
        eng = 'dve' if k % 2 == 0 else 'pool'
        kb.op(eng, lambda e, s=s, k=k: e.tensor_copy(out=wbf[:, k, :], in_=s[:]), reads=[sbuf_], writes=[wb])
    return wbf, wb


RW_PARAMS = {'mu': [128, 8], 'lanem': [128, 6], 'k_k': [128, 2], 'k_a': [128, 2], 'r_k': [128, 2], 'a0': [128, 2, 2],
             'w2': [128, 192], 'a2': [128, 192], 'w0': [1, 2, 192], 'maskSI': [128, 2, 256], 'maskSI2': [128, 2, 256], 'maskST': [128, 2, 3, 128],
             'tri': [128, 2, 256], 'ehead': [128, 2, 4], 'ones_bd': [128, 128], 'lnx_g': [128, 192], 'lnx_b': [128, 192]}


def part1(kb, nc, IN, YD, C):
    es = kb.es
    debug = False
    stop_after = None
    xin, sil_in, adaw, adab, normw, wfm, wg, cs64 = (IN[k_] for k_ in ('xin', 'sil_in', 'adaw', 'adab', 'normw', 'wfm', 'wg', 'cs64'))
    U = {nm: dram_tmp(nc, 'U_' + nm, [sz, LT], F32, debug) for nm, (off, sz) in FM_BLOCKS.items() if nm != 'four'}
    GT = dram_tmp(nc, 'GT', [LT, 256], F32, debug)
    HT = dram_tmp(nc, 'HT', [LT, 128], F32, debug)
    bar, ident_f, identfb, ident_bf, identb, epsT, epsb = C['bar'], C['ident_f'], C['identfb'], C['ident_bf'], C['identb'], C['epsT'], C['epsb']
    with ExitStack() as pa:
        mod, modb = adaln_vectors(kb, pa, nc, sil_in, adaw, adab, normw, 16)
        nw = kb.sb('nw', [128, 8], F32, pa)
        nwb, = load_consts(kb, [(nw, normw)])
        G = kb.sb('G', [128, 8, 2], F32, pa)
        Gbuf = Buf(strict=True)
        for n in range(2):
            kb.op('dve', lambda e, n=n: e.scalar_tensor_tensor(out=G[:, :, n], in0=mod[:, 8:16, n], scalar=1.0, in1=nw[:, :],
                                                                op0=ALU.add, op1=ALU.mult), reads=[modb, nwb], writes=[Gbuf])
        Gb = {'G': Gbuf, 'eps': epsT, 'epsb': epsb}
        shiftv = mod
        wbf, wbb = load_weight_bf16(kb, pa, nc, 'wfm_bf', wfm, NFM, pa)
        wgb, wgbb = load_weight_bf16(kb, pa, nc, 'wg_bf', wg, 256, pa)
        cs = kb.sb('cs64', [64, 128], F32, pa)
        csb, = load_consts(kb, [(cs, cs64)])
        pmm = Ring(kb, 'pmm', [128, 512], F32, 3, psum=True, es=pa)
        pg = Ring(kb, 'pg', [128, 512], F32, 1, psum=True, es=pa)
        ustg = Ring(kb, 'ustg', [128, 512], F32, 4, es=pa)
        gstg = Ring(kb, 'gstg', [128, 256], F32, 3, es=pa)
        hstg = Ring(kb, 'hstg', [128, 128], F32, 3, es=pa)
        fstg = Ring(kb, 'fstg', [64, 512], F32, 2, es=pa)
        cnt = [0]

        def emit_tile(t0, T, hT, hb):
            for nm, (off, sz) in FM_BLOCKS.items():
                ps, psb = pmm.next()
                for k in range(8):
                    kb.op('pe', lambda e, ps=ps, k=k, off=off, sz=sz: e.matmul(ps[0:sz, 0:T], lhsT=wbf[:, k, off:off + sz], rhs=hT[:, k, 0:T],
                                                                               start=(k == 0), stop=(k == 7)), reads=[wbb, hb], writes=[psb])
                if nm == 'four':
                    fs, fsb = fstg.next()
                    kb.op('act', lambda e, fs=fs, ps=ps: e.activation(out=fs[:, 0:T], in_=ps[0:64, 0:T], func=AF.Copy), reads=[psb], writes=[fsb])
                    for sub in range(T // 128):
                        pgt, pgb = pg.next()
                        kb.op('pe', lambda e, pgt=pgt, fs=fs, sub=sub: e.matmul(pgt[:, 0:128], lhsT=fs[:, sub * 128:(sub + 1) * 128], rhs=cs[:, :],
                                                                                  start=True, stop=True), reads=[fsb, csb], writes=[pgb])
                        hs, hsb = hstg.next()
                        kb.op('dve', lambda e, hs=hs, pgt=pgt: e.tensor_copy(out=hs[:], in_=pgt[:, 0:128]), reads=[pgb], writes=[hsb])
                        r0 = t0 + sub * 128
                        kb.op('pool', lambda e, hs=hs, r0=r0: e.dma_start(out=HT[r0:r0 + 128, :], in_=hs[:]), reads=[hsb], dma=True)
                else:
                    us, usb = ustg.next()
                    cnt[0] += 1
                    if cnt[0] % 2 == 0:
                        kb.op('act', lambda e, us=us, ps=ps, sz=sz: e.activation(out=us[0:sz, 0:T], in_=ps[0:sz, 0:T], func=AF.Copy), reads=[psb], writes=[usb])
                    else:
                        kb.op('dve', lambda e, us=us, ps=ps, sz=sz: e.tensor_copy(out=us[0:sz, 0:T], in_=ps[0:sz, 0:T]), reads=[psb], writes=[usb])
                    kb.op('pool', lambda e, us=us, nm=nm, sz=sz: e.dma_start(out=U[nm][:, t0:t0 + T], in_=us[0:sz, 0:T]), reads=[usb], dma=True)
            for sub in range(T // 128):
                pgt, pgb = pg.next()
                for k in range(8):
                    kb.op('pe', lambda e, pgt=pgt, k=k, sub=sub: e.matmul(pgt[:, 0:256], lhsT=hT[:, k, sub * 128:(sub + 1) * 128], rhs=wgb[:, k, :],
                                                                           start=(k == 0), stop=(k == 7)), reads=[wgbb, hb], writes=[pgb])
                gs, gsb = gstg.next()
                kb.op('act', lambda e, gs=gs, pgt=pgt: e.activation(out=gs[:], in_=pgt[:, 0:256], func=AF.Silu), reads=[pgb], writes=[gsb])
                r0 = t0 + sub * 128
                kb.op('pool', lambda e, gs=gs, r0=r0: e.dma_start(out=GT[r0:r0 + 128, :], in_=gs[:]), reads=[gsb], dma=True)

        phase_proj(kb, nc, pa, xin, None, G, shiftv, Gb, ident_bf, identb, emit_tile)
        kb.barrier(bar[:])
    BLm = ['r01', 'r2', 'k01', 'k2', 'v01', 'v2', 'wl', 'al']
    S = {nm: dram_tmp(nc, 'S_' + nm, [FM_BLOCKS[nm][1], LT], F32, debug) for nm in BLm}
    with ExitStack() as pm:
        mu_t = kb.sb('mu_m', [128, 8], F32, pm)
        lan_t = kb.sb('lan_m', [128, 6], F32, pm)
        mub_, lanb_ = load_consts(kb, [(mu_t, IN['p_mu']), (lan_t, IN['p_lanem'])])
        cf = kb.sb('coef_m', [128, 8, 7], F32, pm)
        cfb = Buf(strict=True)
        kb.op('dve', lambda e: e.tensor_scalar(out=cf[:, :, 0], in0=mu_t[:, :], scalar1=-1.0, scalar2=1.0, op0=ALU.mult, op1=ALU.add), reads=[mub_], writes=[cfb])
        for l in range(6):
            kb.op('dve', lambda e, l=l: e.tensor_scalar(out=cf[:, :, 1 + l], in0=mu_t[:, :], scalar1=lan_t[:, l:l + 1], scalar2=None, op0=ALU.mult), reads=[mub_, lanb_], writes=[cfb])
        wr = Ring(kb, 'winm', [128, 640], F32, 4, es=pm)
        so = Ring(kb, 'smo', [128, 512], F32, 4, es=pm)
        for (t0, T) in TILES:
            isctx = t0 < LC
            seg_lo, seg_hi = (0, LC) if isctx else (LC, LT)
            halo = 1 if isctx else 64
            lo = max(t0 - halo, seg_lo)
            hi = min(t0 + T + halo, seg_hi)
            for bi, nm in enumerate(BLm):
                sz = FM_BLOCKS[nm][1]
                w_, wb_ = wr.next()
                if lo > t0 - halo or hi < t0 + T + halo:
                    kb.op('pool', lambda e, w_=w_: e.memset(w_[:], 0.0), writes=[wb_])
                kb.op('sp', lambda e, w_=w_, nm=nm, sz=sz, lo=lo, hi=hi, t0=t0: e.dma_start(out=w_[0:sz, 64 + lo - t0:64 + hi - t0], in_=U[nm][:, lo:hi]), writes=[wb_], dma=True)
                o_, ob_ = so.next()
                kb.op('dve', lambda e, o_=o_, w_=w_, sz=sz, bi=bi, T=T: e.tensor_scalar(out=o_[0:sz, 0:T], in0=w_[0:sz, 64:64 + T], scalar1=cf[0:sz, bi, 0:1], scalar2=None, op0=ALU.mult),
                      reads=[wb_, cfb], writes=[ob_])
                terms = [(5, -1, None), (6, +1, None)] if isctx else [(1, -1, 'L'), (2, +1, 'R'), (3, -64, None), (4, +64, None)]
                for (ci, off, kind) in terms:
                    def f(e, o_=o_, w_=w_, sz=sz, bi=bi, ci=ci, off=off, kind=kind, T=T):
                        if kind is None:
                            oo = o_[0:sz, 0:T]
                            ii = w_[0:sz, 64 + off:64 + T + off]
                        else:
                            ov = o_[0:sz, 0:T].rearrange("p (r c) -> p r c", c=64)
                            iv = w_[0:sz, 64:64 + T].rearrange("p (r c) -> p r c", c=64)
                            if kind == 'L':
                                oo, ii = ov[:, :, 1:64], iv[:, :, 0:63]
                            else:
                                oo, ii = ov[:, :, 0:63], iv[:, :, 1:64]
                        return e.scalar_tensor_tensor(out=oo, in0=ii, scalar=cf[0:sz, bi, ci:ci + 1], in1=oo, op0=ALU.mult, op1=ALU.add)
                    kb.op('dve', f, reads=[wb_, cfb, ob_], writes=[ob_])
                kb.op('pool', lambda e, o_=o_, nm=nm, sz=sz, t0=t0, T=T: e.dma_start(out=S[nm][:, t0:t0 + T], in_=o_[0:sz, 0:T]), reads=[ob_], dma=True)
        kb.barrier(bar[:])
    if stop_after == 'A':
        return
    with ExitStack() as pb:
        P = {k_: IN['p_' + k_] for k_ in RW_PARAMS}
        YR = YD
        if stop_after != 'skipB':
            phase_rwkv(kb, nc, pb, S, GT, YR, P, ident_f, identfb, ident_bf, identb)
        kb.barrier(bar[:])
    if DBG.get('skip_fnet'):
        return
    PF = {k_: IN['f_' + k_] for k_ in FN_PARAMS}
    YF = YD
    ZS = dram_tmp(nc, 'ZS', [2, 128, 64, 128], F32, False)
    phase_fnet(kb, nc, HT, GT, YF, ZS, PF, bar)
    return


DUMPS = {}


def dump(kb, nc, name, ap, shape, bufs, dt=F32):
    if name in DUMPS:
        return
    t = nc.dram_tensor('dbg_' + name, list(shape), dt, kind="ExternalOutput").ap()
    DUMPS[name] = t
    kb.op('sp', lambda e: e.dma_start(out=t, in_=ap), reads=bufs, dma=True, final=True)

CHUNK_ORDER = {0: list(range(NCH)), 1: [1, 0] + list(range(NCH - 1, 1, -1))}


def phase_rwkv(kb, nc, es, U, GT, YR, P, ident_f, identfb, ident_bf, identb):
    def sbt(name, shape, dt=F32):
        return kb.sb(name, shape, dt, es)

    def const(name, shape, src, dt=F32):
        t = sbt(name, shape, dt)
        b, = load_consts(kb, [(t, src)])
        return t, b

    mu, mub = const('mu', [128, 8], P['mu'])
    lanem, lanemb = const('lanem', [128, 6], P['lanem'])
    kkp, kkpb = const('kkp', [128, 2], P['k_k'])
    kap, kapb = const('kap', [128, 2], P['k_a'])
    rkp, rkpb = const('rkp', [128, 2], P['r_k'])
    a0p, a0pb = const('a0p', [128, 2, 2], P['a0'])
    w2s, w2sb = const('w2s', [128, 192], P['w2'])
    a2s, a2sb = const('a2s', [128, 192], P['a2'])
    w0r, w0rb = const('w0r', [1, 2, 192], P['w0'])
    msi, msib = const('msi', [128, 2, 256], P['maskSI'])
    mst, mstb = const('mst', [128, 2, 3, 128], P['maskST'])
    msi2, msi2b = const('msi2', [128, 2, 256], P['maskSI2'])
    tri, trib = const('tri', [128, 2, 256], P['tri'])
    ehd, ehdb = const('ehd', [128, 2, 4], P['ehead'])
    obd, obdb = const('obd', [128, 128], P['ones_bd'])
    lng, lngb = const('lng', [128, 192], P['lnx_g'])
    lnb, lnbb = const('lnb', [128, 192], P['lnx_b'])
    ones1 = sbt('ones1', [1, 128])
    ones1b = Buf()
    kb.op('dve', lambda e: e.memset(ones1[:], 1.0), writes=[ones1b])
    coef = sbt('coef', [128, 8, 7])
    coefb = Buf(strict=True)
    kb.op('dve', lambda e: e.tensor_scalar(out=coef[:, :, 0], in0=mu[:, :], scalar1=-1.0, scalar2=1.0, op0=ALU.mult, op1=ALU.add),
          reads=[mub], writes=[coefb])
    for l in range(6):
        kb.op('dve', lambda e, l=l: e.tensor_scalar(out=coef[:, :, 1 + l], in0=mu[:, :], scalar1=lanem[:, l:l + 1], scalar2=None, op0=ALU.mult),
              reads=[mub, lanemb], writes=[coefb])
    omka = sbt('omka', [128, 2])
    omkab = Buf(strict=True)
    kb.op('dve', lambda e: e.tensor_scalar(out=omka[:], in0=kap[:], scalar1=-1.0, scalar2=1.0, op0=ALU.mult, op1=ALU.add), reads=[kapb], writes=[omkab])
    gneps = sbt('gneps', [128, 1])
    gnepsb = Buf()
    kb.op('dve', lambda e: e.memset(gneps[:], GN_EPS), writes=[gnepsb])

    yacc = sbt('yacc', [128, NCH, 192])
    yaccb = [Buf() for _ in range(NCH)]
    kb.op('pool', lambda e: e.memset(yacc[:], 0.0), writes=yaccb)
    ST = sbt('ST', [128, 2, 2, 64])
    STbf = sbt('STbf', [128, 2, 2, 64], BF16)
    STb = [[Buf() for _ in range(3)] for _ in range(2)]

    CR, HR = {}, {}
    for d_ in range(2):
        n_ = 'd%d' % d_
        CR[d_] = dict(
            winr=None, smr=Ring(kb, 'smix' + n_, [128, 8, 128], F32, 2, es=es),
            t32=Ring(kb, 't32' + n_, [128, 256], F32, 9, es=es), kkr=Ring(kb, 'kkt' + n_, [128, 2, 128], F32, 1, es=es),
            vtr=Ring(kb, 'vt32' + n_, [128, 192], F32, 2, es=es), vtbr=Ring(kb, 'vtbf' + n_, [128, 192], BF16, 2, es=es),
            sgr=Ring(kb, 'sg' + n_, [128, 192], F32, 1, es=es), e1r=Ring(kb, 'e1' + n_, [128, 2, 256], F32, 2, es=es),
            e2r=Ring(kb, 'e2' + n_, [128, 2, 128], F32, 1, es=es), arr=Ring(kb, 'ar' + n_, [128, 2, 256], BF16, 2, es=es),
            btr=Ring(kb, 'bt' + n_, [128, 2, 128], BF16, 2, es=es), ktr=Ring(kb, 'kt' + n_, [128, 2, 128], BF16, 2, es=es),
            bktr=Ring(kb, 'bkt' + n_, [128, 2, 192], BF16, 2, es=es), bcr=Ring(kb, 'bc' + n_, [128, 4], F32, 2, es=es, strict=True))
        for h_ in range(3):
            n2 = 'd%dh%d' % (d_, h_)
            HR[(d_, h_)] = dict(
                mm1r=Ring(kb, 'mm1' + n2, [128, 256], BF16, 2, es=es), mm2r=Ring(kb, 'mm2' + n2, [128, 256], BF16, 2, es=es),
                xpr=Ring(kb, 'xp' + n2, [128, 256], BF16, 4, es=es), xtr=Ring(kb, 'xt' + n2, [128, 128], BF16, 4, es=es),
                ivr=Ring(kb, 'iv' + n2, [128, 128], BF16, 8, es=es), tmr=Ring(kb, 'tm' + n2, [128, 128], BF16, 2, es=es),
                w1r=Ring(kb, 'w1' + n2, [128, 64], BF16, 2, es=es), utr=Ring(kb, 'ut' + n2, [128, 64], BF16, 2, es=es),
                tsr=Ring(kb, 'ts' + n2, [128, 64], F32, 3, es=es))
    jkr = Ring(kb, 'jk', [128, 64], F32, 2, es=es)
    gtr = Ring(kb, 'gt', [128, 192], F32, 2, es=es)
    fnr = Ring(kb, 'fn', [128, 192], F32, 2, es=es)
    str_ = Ring(kb, 'stt', [128, 8], F32, 6, es=es, strict=True)
    pprep = Ring(kb, 'pprep', [128, 512], F32, 2, psum=True, es=es)
    ptrb = kb.ps('ptrb', [128, 1024], BF16, es)
    ptrbufs = [Buf(excl=True)] * 4
    ptrc = [0]
    pgram = Ring(kb, 'pgram', [128, 512], F32, 1, psum=True, es=es)
    pinv = Ring(kb, 'pinv', [128, 512], F32, 2, psum=True, es=es)
    pseq = kb.ps('pseq', [128, 512], F32, es)
    pseq2 = kb.ps('pseq2', [128, 512], F32, es)
    pseqb = [Buf(excl=True)] * 4 + [Buf(excl=True)] * 4
    pseqt = [pseq] * 4 + [pseq2] * 4
    pseqc = [0]

    def pseq_next():
        i = pseqc[0] % 8
        pseqc[0] += 1
        return pseqt[i][:, (i % 4) * 64:(i % 4 + 1) * 64], pseqb[i], i

    def ptr_next():
        i = ptrc[0] % 4
        ptrc[0] += 1
        return i, ptrbufs[i]

    blkname = [('r01', 'k01', 'v01'), ('r2', 'k2', 'v2')]
    BL = ['r01', 'r2', 'k01', 'k2', 'v01', 'v2', 'wl', 'al']
    BI = {n: i for i, n in enumerate(BL)}
    BSZ = {n: FM_BLOCKS[n][1] for n in BL}
    alt = [0]

    def ew(fn, reads, writes):
        alt[0] += 1
        r_ = _Rec()
        fn(r_)
        stt_ = r_.call[0] == 'scalar_tensor_tensor'
        return kb.op('dve' if (alt[0] % 3 or stt_) else 'pool', fn, reads=reads, writes=writes)

    done = {}
    inflight = {}
    SMB = {}

    def chunk_body(d, c):
        winr, smr, t32, kkr, vtr, vtbr, sgr, e1r, e2r, arr, btr, ktr, bktr, bcr = (CR[d][k_] for k_ in (
            'winr', 'smr', 't32', 'kkr', 'vtr', 'vtbr', 'sgr', 'e1r', 'e2r', 'arr', 'btr', 'ktr', 'bktr', 'bcr'))
        isctx = c < 2
        t0 = c * 128
        seg_lo, seg_hi = (0, LC) if isctx else (LC, LT)
        halo = 1 if isctx else 64
        lo = max(t0 - halo, seg_lo)
        hi = min(t0 + 128 + halo, seg_hi)
        sm, smb0 = smr.next()
        smb = SMB.setdefault(id(smb0), [smb0] + [Buf() for _ in range(7)])
        for nm in BL:
            sz = BSZ[nm]
            kb.op('sp', lambda e, sm=sm, nm=nm, sz=sz, t0=t0: e.dma_start(out=sm[0:sz, BI[nm], :], in_=U[nm][:, t0:t0 + 128]), writes=[smb[BI[nm]]], dma=True)
        if DBG.get('stage', 99) < 2:
            return

        def S(nm):
            return sm[0:BSZ[nm], BI[nm], :]

        yield
        kkt, kktb = kkr.next()
        for bj, (rn, kn, vn) in enumerate(blkname if not DBG.get('skip_kk') else []):
            sz = BSZ[kn]
            q, qb = t32.next()
            kb.op('dve', lambda e, q=q, kn=kn, sz=sz, bj=bj: e.tensor_scalar(out=q[0:sz, 0:128], in0=S(kn), scalar1=kkp[0:sz, bj:bj + 1], scalar2=None, op0=ALU.mult),
                  reads=[smb, kkpb], writes=[qb])
            kb.op('pool', lambda e, q=q, sz=sz: e.tensor_tensor(out=q[0:sz, 128:256], in0=q[0:sz, 0:128], in1=q[0:sz, 0:128], op=ALU.mult), reads=[qb], writes=[qb])
            pp, ppb = pprep.next()
            kb.op('pe', lambda e, pp=pp, q=q, sz=sz: e.matmul(pp[0:sz, 0:128], lhsT=obd[0:sz, 0:sz], rhs=q[0:sz, 128:256], start=True, stop=True),
                  reads=[qb, obdb], writes=[ppb])
            nr, nrb = t32.next()
            kb.op('act', lambda e, nr=nr, pp=pp, sz=sz: e.activation(out=nr[0:sz, 0:128], in_=pp[0:sz, 0:128], func=AF.Sqrt), reads=[ppb], writes=[nrb])
            kb.op('dve', lambda e, nr=nr, sz=sz: e.tensor_scalar(out=nr[0:sz, 0:128], in0=nr[0:sz, 0:128], scalar1=1e-12, scalar2=None, op0=ALU.max), reads=[nrb], writes=[nrb])
            kb.op('dve', lambda e, nr=nr, sz=sz: e.reciprocal(out=nr[0:sz, 128:256], in_=nr[0:sz, 0:128]), reads=[nrb], writes=[nrb])
            kb.op('dve', lambda e, nr=nr, q=q, sz=sz, bj=bj, kkt=kkt: e.tensor_tensor(out=kkt[0:sz, bj, :], in0=q[0:sz, 0:128], in1=nr[0:sz, 128:256], op=ALU.mult),
                  reads=[nrb, qb], writes=[kktb])
        vt, vtb = vtr.next()
        vtbf, vtbfb = vtbr.next()
        pp, ppb = pprep.next()
        for bj, (rn, kn, vn) in enumerate(blkname if not DBG.get('skip_vt') else []):
            sz = BSZ[vn]
            kb.op('pe', lambda e, pp=pp, vn=vn, sz=sz, bj=bj: e.transpose(out=pp[:, bj * 128:bj * 128 + sz], in_=S(vn), identity=ident_f[0:sz, 0:sz]),
                  reads=[smb, identfb], writes=[ppb])
        kb.op('act', lambda e, pp=pp, vt=vt: e.activation(out=vt[:, :], in_=pp[:, 0:192], func=AF.Copy), reads=[ppb], writes=[vtb])
        kb.op('pool', lambda e, vt=vt, vtbf=vtbf: e.tensor_copy(out=vtbf[:, :], in_=vt[:, :]), reads=[vtb], writes=[vtbfb])

        if DBG.get('stage', 99) < 3:
            return
        yield
        th, thb = t32.next()
        kb.op('act', lambda e, th=th: e.activation(out=th[d * 64:(d + 1) * 64, 0:128], in_=sm[d * 64:(d + 1) * 64, BI['wl'], :], func=AF.Tanh),
              reads=[smb], writes=[thb])
        pp, ppb = pprep.next()
        kb.op('pe', lambda e, pp=pp, th=th: e.matmul(pp[:, 0:192], lhsT=th[d * 64:(d + 1) * 64, 0:128], rhs=w2s[d * 64:(d + 1) * 64, :], start=True, stop=False),
              reads=[thb, w2sb], writes=[ppb])
        kb.op('pe', lambda e, pp=pp: e.matmul(pp[:, 0:192], lhsT=ones1[0:1, :], rhs=w0r[0:1, d, :], start=False, stop=True),
              reads=[ones1b, w0rb], writes=[ppb])
        sg, sgb = sgr.next()
        kb.op('act', lambda e, sg=sg, pp=pp: e.activation(out=sg[:, :], in_=pp[:, 0:192], func=AF.Sigmoid), reads=[ppb], writes=[sgb])
        e1, e1b = e1r.next()
        e2, e2b = e2r.next()
        for bj in range(2):
            sz = 128 if bj == 0 else 64
            pp, ppb = pprep.next()
            kb.op('pe', lambda e, pp=pp, sg=sg, bj=bj, sz=sz: e.matmul(pp[0:sz, 0:256], lhsT=sg[:, bj * 128:bj * 128 + sz], rhs=tri[:, d, :], start=True, stop=True),
                  reads=[sgb, trib], writes=[ppb])
            kb.op('act', lambda e, pp=pp, e1=e1, bj=bj, sz=sz: e.activation(out=e1[0:sz, bj, :], in_=pp[0:sz, 0:256], func=AF.Exp), reads=[ppb], writes=[e1b])
            kb.op('act', lambda e, pp=pp, e2=e2, bj=bj, sz=sz: e.activation(out=e2[0:sz, bj, :], in_=pp[0:sz, 0:128], func=AF.Exp, scale=-1.0), reads=[ppb], writes=[e2b])
        if DBG.get('stage', 99) < 4:
            return
        yield
        ar, arb = arr.next()
        bt, btb = btr.next()
        kt, ktb = ktr.next()
        bc, bcb = bcr.next()
        pbon, pbonb, _ = pseq_next()
        for bj, (rn, kn, vn) in enumerate(blkname):
            sz = BSZ[kn]
            co = bj * 128
            pp, ppb = pprep.next()
            kb.op('pe', lambda e, pp=pp, sz=sz, co=co: e.matmul(pp[0:sz, 0:128], lhsT=a2s[d * 64:(d + 1) * 64, co:co + sz], rhs=sm[d * 64:(d + 1) * 64, BI['al'], :], start=True, stop=True),
                  reads=[a2sb, smb], writes=[ppb])
            av, avb = t32.next()
            kb.op('act', lambda e, av=av, pp=pp, sz=sz, bj=bj: e.activation(out=av[0:sz, 0:128], in_=pp[0:sz, 0:128], func=AF.Sigmoid, bias=a0p[0:sz, d, bj:bj + 1]),
                  reads=[ppb, a0pb], writes=[avb])
            kv, kvb = t32.next()
            ew(lambda e, kv=kv, av=av, sz=sz, bj=bj: e.tensor_scalar(out=kv[0:sz, 0:128], in0=av[0:sz, 0:128], scalar1=kap[0:sz, bj:bj + 1], scalar2=omka[0:sz, bj:bj + 1], op0=ALU.mult, op1=ALU.add),
               [avb, kapb, omkab], [kvb])
            ew(lambda e, kv=kv, kn=kn, sz=sz: e.tensor_tensor(out=kv[0:sz, 0:128], in0=kv[0:sz, 0:128], in1=S(kn), op=ALU.mult), [kvb, smb], [kvb])
            ew(lambda e, kv=kv, kt=kt, e2=e2, sz=sz, bj=bj: e.tensor_tensor(out=kt[0:sz, bj, :], in0=kv[0:sz, 0:128], in1=e2[0:sz, bj, :], op=ALU.mult), [kvb, e2b], [ktb])
            ew(lambda e, av=av, e2=e2, sz=sz, bj=bj: e.tensor_tensor(out=av[0:sz, 128:256], in0=av[0:sz, 0:128], in1=e2[0:sz, bj, :], op=ALU.mult), [avb, e2b], [avb])
            ew(lambda e, av=av, bt=bt, kkt=kkt, sz=sz, bj=bj: e.tensor_tensor(out=bt[0:sz, bj, :], in0=av[0:sz, 128:256], in1=kkt[0:sz, bj, :], op=ALU.mult), [avb, kktb], [btb])
            ew(lambda e, ar=ar, kkt=kkt, e1=e1, sz=sz, bj=bj: e.scalar_tensor_tensor(out=ar[0:sz, bj, 0:128], in0=kkt[0:sz, bj, :], scalar=-1.0, in1=e1[0:sz, bj, 128:256], op0=ALU.mult, op1=ALU.mult),
               [kktb, e1b], [arb])
            ew(lambda e, ar=ar, rn=rn, e1=e1, sz=sz, bj=bj: e.tensor_tensor(out=ar[0:sz, bj, 128:256], in0=S(rn), in1=e1[0:sz, bj, 0:128], op=ALU.mult), [smb, e1b], [arb])
            ew(lambda e, kv=kv, rn=rn, sz=sz, bj=bj: e.scalar_tensor_tensor(out=kv[0:sz, 128:256], in0=S(rn), scalar=rkp[0:sz, bj:bj + 1], in1=kv[0:sz, 0:128], op0=ALU.mult, op1=ALU.mult),
               [smb, rkpb, kvb], [kvb])
            kb.op('pe', lambda e, kv=kv, sz=sz, bj=bj: e.matmul(pbon[:, 0:4], lhsT=kv[0:sz, 128:256], rhs=ehd[0:sz, bj, :], start=(bj == 0), stop=(bj == 1)),
                  reads=[kvb, ehdb], writes=[pbonb])
        kb.op('dve', lambda e, bc=bc: e.tensor_copy(out=bc[:, :], in_=pbon[:, 0:4]), reads=[pbonb], writes=[bcb])
        yield
        bkt, bktb = bktr.next()
        for wi, (src, srcb) in enumerate([(bt, btb), (kt, ktb)]):
            pi, pib = ptr_next()
            for bj in range(2):
                sz = 128 if bj == 0 else 64
                kb.op('pe', lambda e, src=src, pi=pi, bj=bj, sz=sz: e.transpose(out=ptrb[:, pi * 256 + bj * 128:pi * 256 + bj * 128 + sz], in_=src[0:sz, bj, :], identity=ident_bf[0:sz, 0:sz]),
                      reads=[srcb, identb], writes=[pib])
            kb.op('act' if wi == 0 else 'dve',
                  (lambda e, pi=pi, bkt=bkt, wi=wi: e.activation(out=bkt[:, wi, :], in_=ptrb[:, pi * 256:pi * 256 + 192], func=AF.Copy)) if wi == 0 else
                  (lambda e, pi=pi, bkt=bkt, wi=wi: e.tensor_copy(out=bkt[:, wi, :], in_=ptrb[:, pi * 256:pi * 256 + 192])),
                  reads=[pib], writes=[bktb])
        tcol = 127 if d == 0 else 0

        if DBG.get('stage', 99) < 5:
            return
        first = done.get(c, 0) == 0
        assert c not in inflight
        inflight[c] = d

        def head_body(h):
            mm1r, mm2r, xpr, xtr, ivr, tmr, w1r, utr, tsr = (HR[(d, h)][k_] for k_ in ('mm1r', 'mm2r', 'xpr', 'xtr', 'ivr', 'tmr', 'w1r', 'utr', 'tsr'))
            bj = h // 2
            base = (h % 2) * 64
            ch0 = 64 * h
            hp = slice(base, base + 64)
            yield
            pg1, pg1b = pgram.next()
            kb.op('pe', lambda e, pg1=pg1, hp=hp, bj=bj: e.matmul(pg1[:, 0:256], lhsT=bt[hp, bj, :], rhs=ar[hp, bj, :], start=True, stop=True), reads=[btb, arb], writes=[pg1b])
            mm1, mm1b = mm1r.next()
            kb.op('dve', lambda e, mm1=mm1, pg1=pg1: e.tensor_tensor(out=mm1[:, :], in0=pg1[:, 0:256], in1=msi[:, d, :], op=ALU.mult), reads=[pg1b, msib], writes=[mm1b])
            pg2, pg2b = pgram.next()
            kb.op('pe', lambda e, pg2=pg2, hp=hp, bj=bj: e.matmul(pg2[:, 0:256], lhsT=kt[hp, bj, :], rhs=ar[hp, bj, :], start=True, stop=True), reads=[ktb, arb], writes=[pg2b])
            kb.op('pe', lambda e, pg2=pg2, hp=hp, bj=bj: e.matmul(pg2[:, 256:384], lhsT=ar[hp, bj, 0:128], rhs=bt[hp, bj, :], start=True, stop=True), reads=[btb, arb], writes=[pg2b])
            mm2, mm2b = mm2r.next()
            kb.op('dve', lambda e, mm2=mm2, pg2=pg2: e.tensor_tensor(out=mm2[:, :], in0=pg2[:, 0:256], in1=msi2[:, d, :], op=ALU.mult), reads=[pg2b, msi2b], writes=[mm2b])
            xt, xtb = xtr.next()
            kb.op('dve', lambda e, xt=xt, pg2=pg2: e.tensor_tensor(out=xt[:, :], in0=pg2[:, 256:384], in1=mst[:, d, 0, :], op=ALU.mult), reads=[pg2b, mstb], writes=[xtb])
            e1t, e1tb = ivr.next()
            kb.op('dve', lambda e, e1t=e1t, pg2=pg2: e.tensor_tensor(out=e1t[:, :], in0=pg2[:, 256:384], in1=mst[:, d, 1, :], op=ALU.mult), reads=[pg2b, mstb], writes=[e1tb])
            e2t, e2tb = ivr.next()
            kb.op('dve', lambda e, e2t=e2t, pg2=pg2: e.tensor_tensor(out=e2t[:, :], in0=pg2[:, 256:384], in1=mst[:, d, 2, :], op=ALU.mult), reads=[pg2b, mstb], writes=[e2tb])
            if DBG.get('stage', 99) < 6:
                return
            yield
            xp, xpb = xpr.next()
            kb.op('pool', lambda e, xp=xp, mm1=mm1: e.tensor_tensor(out=xp[:, 128:256], in0=mm1[:, 0:128], in1=ident_bf[:, :], op=ALU.add), reads=[mm1b, identb], writes=[xpb])
            pv, pvb = pinv.next()
            kb.op('pe', lambda e, pv=pv, xt=xt, mm1=mm1: e.matmul(pv[:, 0:128], lhsT=xt[:, :], rhs=mm1[:, 0:128], start=True, stop=True), reads=[xtb, mm1b], writes=[pvb])
            kb.op('pe', lambda e, pv=pv, xt=xt, mm1=mm1: e.matmul(pv[:, 256:384], lhsT=mm1[:, 0:128], rhs=xt[:, :], start=True, stop=True), reads=[xtb, mm1b], writes=[pvb])
            xt2, xt2b = xtr.next()
            kb.op('act', lambda e, xp=xp, pv=pv: e.activation(out=xp[:, 0:128], in_=pv[:, 0:128], func=AF.Copy), reads=[pvb], writes=[xpb])
            kb.op('dve', lambda e, xt2=xt2, pv=pv: e.tensor_copy(out=xt2[:, :], in_=pv[:, 256:384]), reads=[pvb], writes=[xt2b])
            curxp, curxpb, curxt, curxtb = xp, xpb, xt2, xt2b
            for lev in range(1, 4):
                pv, pvb = pinv.next()
                kb.op('pe', lambda e, pv=pv, cx=curxp, ct=curxt: e.matmul(pv[:, 0:256], lhsT=ct[:, :], rhs=cx[:, 0:256], start=True, stop=True), reads=[curxpb, curxtb], writes=[pvb])
                kb.op('pe', lambda e, pv=pv, cx=curxp, ct=curxt: e.matmul(pv[:, 256:384], lhsT=cx[:, 0:128], rhs=ct[:, :], start=True, stop=True), reads=[curxpb, curxtb], writes=[pvb])
                nxp, nxpb = xpr.next()
                nxt, nxtb = xtr.next()
                kb.op('act', lambda e, nxp=nxp, pv=pv: e.activation(out=nxp[:, 0:128], in_=pv[:, 0:128], func=AF.Copy), reads=[pvb], writes=[nxpb])
                kb.op('dve', lambda e, nxp=nxp, pv=pv, cx=curxp: e.tensor_tensor(out=nxp[:, 128:256], in0=pv[:, 128:256], in1=cx[:, 128:256], op=ALU.add), reads=[pvb, curxpb], writes=[nxpb])
                kb.op('act', lambda e, nxt=nxt, pv=pv: e.activation(out=nxt[:, :], in_=pv[:, 256:384], func=AF.Copy), reads=[pvb], writes=[nxtb])
                curxp, curxpb, curxt, curxtb = nxp, nxpb, nxt, nxtb
                yield
            pv, pvb = pinv.next()
            kb.op('pe', lambda e, pv=pv, cx=curxp, ct=curxt: e.matmul(pv[:, 0:128], lhsT=ct[:, :], rhs=cx[:, 128:256], start=True, stop=True), reads=[curxpb, curxtb], writes=[pvb])
            t32m, t32mb = ivr.next()
            kb.op('dve', lambda e, t32m=t32m, pv=pv, cx=curxp: e.tensor_tensor(out=t32m[:, :], in0=pv[:, 0:128], in1=cx[:, 128:256], op=ALU.add), reads=[pvb, curxpb], writes=[t32mb])
            yield
            pi, pib = ptr_next()
            kb.op('pe', lambda e, pi=pi, t32m=t32m: e.transpose(out=ptrb[:, pi * 256:pi * 256 + 128], in_=t32m[:, :], identity=ident_bf[:, :]), reads=[t32mb, identb], writes=[pib])
            t32t, t32tb = ivr.next()
            kb.op('act', lambda e, pi=pi, t32t=t32t: e.activation(out=t32t[:, :], in_=ptrb[:, pi * 256:pi * 256 + 128], func=AF.Copy), reads=[pib], writes=[t32tb])
            yield
            pv, pvb = pinv.next()
            kb.op('pe', lambda e, pv=pv, e1t=e1t, t32m=t32m: e.matmul(pv[:, 0:128], lhsT=e1t[:, :], rhs=t32m[:, :], start=True, stop=True), reads=[e1tb, t32mb], writes=[pvb])
            z1, z1b = ivr.next()
            kb.op('act', lambda e, z1=z1, pv=pv: e.activation(out=z1[:, :], in_=pv[:, 0:128], func=AF.Copy), reads=[pvb], writes=[z1b])
            pv, pvb = pinv.next()
            kb.op('pe', lambda e, pv=pv, t32t=t32t, z1=z1: e.matmul(pv[:, 0:128], lhsT=t32t[:, :], rhs=z1[:, :], start=True, stop=True), reads=[t32tb, z1b], writes=[pvb])
            kb.op('pe', lambda e, pv=pv, t32t=t32t, z1=z1: e.matmul(pv[:, 256:384], lhsT=z1[:, :], rhs=t32t[:, :], start=True, stop=True), reads=[t32tb, z1b], writes=[pvb])
            t64, t64b = ivr.next()
            t64t, t64tb = ivr.next()
            kb.op('dve', lambda e, t64=t64, pv=pv, t32m=t32m: e.tensor_tensor(out=t64[:, :], in0=pv[:, 0:128], in1=t32m[:, :], op=ALU.add), reads=[pvb, t32mb], writes=[t64b])
            kb.op('dve', lambda e, t64t=t64t, pv=pv, t32t=t32t: e.tensor_tensor(out=t64t[:, :], in0=pv[:, 256:384], in1=t32t[:, :], op=ALU.add), reads=[pvb, t32tb], writes=[t64tb])
            yield
            pv, pvb = pinv.next()
            kb.op('pe', lambda e, pv=pv, e2t=e2t, t64=t64: e.matmul(pv[:, 0:128], lhsT=e2t[:, :], rhs=t64[:, :], start=True, stop=True), reads=[e2tb, t64b], writes=[pvb])
            z2, z2b = ivr.next()
            kb.op('act', lambda e, z2=z2, pv=pv: e.activation(out=z2[:, :], in_=pv[:, 0:128], func=AF.Copy), reads=[pvb], writes=[z2b])
            pv, pvb = pinv.next()
            kb.op('pe', lambda e, pv=pv, t64t=t64t, z2=z2: e.matmul(pv[:, 0:128], lhsT=t64t[:, :], rhs=z2[:, :], start=True, stop=True), reads=[t64tb, z2b], writes=[pvb])
            tm, tmb = tmr.next()
            kb.op('dve', lambda e, tm=tm, pv=pv, t64=t64: e.tensor_tensor(out=tm[:, :], in0=pv[:, 0:128], in1=t64[:, :], op=ALU.add), reads=[pvb, t64b], writes=[tmb])
            if DBG.get('dump') == (d, c) and h == 0:
                dump(kb, nc, 'mm1', mm1[:, :], [128, 256], [mm1b], BF16)
                dump(kb, nc, 'mm2', mm2[:, :], [128, 256], [mm2b], BF16)
                dump(kb, nc, 'tm', tm[:, :], [128, 128], [tmb], BF16)
            yield
            stb = STb[d][h]
            p1, p1b, _ = pseq_next()
            kb.op('pe', lambda e, p1=p1, hp=hp, bj=bj: e.matmul(p1, lhsT=ar[hp, bj, 0:128], rhs=STbf[hp, d, bj, :], start=True, stop=False), reads=[arb, stb], writes=[p1b])
            kb.op('pe', lambda e, p1=p1, mm2=mm2, ch0=ch0: e.matmul(p1, lhsT=mm2[:, 0:128], rhs=vtbf[:, ch0:ch0 + 64], start=False, stop=True), reads=[mm2b, vtbfb], writes=[p1b])
            w1, w1b = w1r.next()
            kb.op('act', lambda e, w1=w1, p1=p1: e.activation(out=w1[:, :], in_=p1, func=AF.Copy), reads=[p1b], writes=[w1b])
            p2, p2b, _ = pseq_next()
            kb.op('pe', lambda e, p2=p2, tm=tm, w1=w1: e.matmul(p2, lhsT=tm[:, :], rhs=w1[:, :], start=True, stop=True), reads=[tmb, w1b], writes=[p2b])
            ut, utb = utr.next()
            kb.op('act', lambda e, ut=ut, p2=p2: e.activation(out=ut[:, :], in_=p2, func=AF.Copy), reads=[p2b], writes=[utb])
            p3, p3b, _ = pseq_next()
            kb.op('pe', lambda e, p3=p3, hp=hp, bj=bj: e.matmul(p3, lhsT=ar[hp, bj, 128:256], rhs=STbf[hp, d, bj, :], start=True, stop=False), reads=[arb, stb], writes=[p3b])
            kb.op('pe', lambda e, p3=p3, mm1=mm1, ut=ut: e.matmul(p3, lhsT=mm1[:, 128:256], rhs=ut[:, :], start=False, stop=False), reads=[mm1b, utb], writes=[p3b])
            kb.op('pe', lambda e, p3=p3, mm2=mm2, ch0=ch0: e.matmul(p3, lhsT=mm2[:, 128:256], rhs=vtbf[:, ch0:ch0 + 64], start=False, stop=True), reads=[mm2b, vtbfb], writes=[p3b])
            if DBG.get('dump') == (d, c) and h == 0:
                dump(kb, nc, 'w1', w1[:, :], [128, 64], [w1b], BF16)
                dump(kb, nc, 'ut', ut[:, :], [128, 64], [utb], BF16)
            ts, tsb = tsr.next()
            kb.op('dve', lambda e, ts=ts, p3=p3, ch0=ch0, h=h: e.scalar_tensor_tensor(out=ts[:, :], in0=vt[:, ch0:ch0 + 64], scalar=bc[:, h:h + 1], in1=p3, op0=ALU.mult, op1=ALU.add),
                  reads=[vtb, bcb, p3b], writes=[tsb])
            if first:
                kb.op('pool', lambda e, ts=ts, ch0=ch0, c=c: e.tensor_copy(out=yacc[:, c, ch0:ch0 + 64], in_=ts[:, :]), reads=[tsb], writes=[yaccb[c]])
            else:
                kb.op('pool', lambda e, ts=ts, ch0=ch0, c=c: e.tensor_tensor(out=yacc[:, c, ch0:ch0 + 64], in0=yacc[:, c, ch0:ch0 + 64], in1=ts[:, :], op=ALU.add), reads=[tsb, yaccb[c]], writes=[yaccb[c]])
            yield
            p4full, p4b, i4 = pseq_next()
            p4 = pseqt[i4][hp, (i4 % 4) * 64:(i4 % 4 + 1) * 64]
            kb.op('pe', lambda e, p4=p4, bkt=bkt, ut=ut, ch0=ch0: e.matmul(p4, lhsT=bkt[:, 0, ch0:ch0 + 64], rhs=ut[:, :], start=True, stop=False), reads=[bktb, utb], writes=[p4b])
            kb.op('pe', lambda e, p4=p4, bkt=bkt, ch0=ch0: e.matmul(p4, lhsT=bkt[:, 1, ch0:ch0 + 64], rhs=vtbf[:, ch0:ch0 + 64], start=False, stop=True), reads=[bktb, vtbfb], writes=[p4b])
            tq, tqb = tsr.next()
            kb.op('dve', lambda e, tq=tq, p4=p4, hp=hp, bj=bj: e.tensor_tensor(out=tq[hp, :], in0=p4, in1=ST[hp, d, bj, :], op=ALU.add), reads=[p4b, stb], writes=[tqb])
            kb.op('dve', lambda e, tq=tq, hp=hp, bj=bj: e.tensor_scalar(out=ST[hp, d, bj, :], in0=tq[hp, :], scalar1=e1[hp, bj, tcol:tcol + 1], scalar2=None, op0=ALU.mult), reads=[tqb, e1b], writes=[stb])
            kb.op('act', lambda e, tq=tq, hp=hp, bj=bj: e.activation(out=STbf[hp, d, bj, :], in_=tq[hp, :], func=AF.Identity, scale=e1[hp, bj, tcol:tcol + 1]), reads=[tqb, e1b], writes=[stb])

        hg = [head_body(h) for h in range(DBG.get('heads', 3))]
        while hg:
            for g_ in list(hg):
                try:
                    next(g_)
                except StopIteration:
                    hg.remove(g_)
            yield
        del inflight[c]
        done[c] = done.get(c, 0) + 1
        if DBG.get('dump') == (d, c):
            dump(kb, nc, 'sm', sm[:, :, :], [128, 8, 128], [smb])
            dump(kb, nc, 'kkt', kkt[:, :, :], [128, 2, 128], [kktb])
            dump(kb, nc, 'vt', vt[:, :], [128, 192], [vtb])
            dump(kb, nc, 'sg', sg[:, :], [128, 192], [sgb])
            dump(kb, nc, 'e1', e1[:, :, :], [128, 2, 256], [e1b])
            dump(kb, nc, 'e2', e2[:, :, :], [128, 2, 128], [e2b])
            dump(kb, nc, 'ar', ar[:, :, :], [128, 2, 256], [arb], BF16)
            dump(kb, nc, 'bt', bt[:, :, :], [128, 2, 128], [btb], BF16)
            dump(kb, nc, 'kt', kt[:, :, :], [128, 2, 128], [ktb], BF16)
            dump(kb, nc, 'bkt', bkt[:, :, :], [128, 2, 192], [bktb], BF16)
            dump(kb, nc, 'bc', bc[:, :], [128, 4], [bcb])
            dump(kb, nc, 'ST', ST[:, d, :, :], [128, 2, 64], STb[d])
            dump(kb, nc, 'yacc', yacc[:, c, :], [128, 192], [yaccb[c]])
        yield
        if done[c] == 2 and DBG.get('stage', 99) >= 8:
            gt, gtb = gtr.next()
            kb.op('sp', lambda e, gt=gt, t0=t0: e.dma_start(out=gt[:, :], in_=GT[t0:t0 + 128, 0:192]), writes=[gtb], dma=True)
            fn, fnb = fnr.next()
            for h in range(3):
                ch0 = 64 * h
                stt, sttb = str_.next()
                jk, jkb = jkr.next()
                kb.op('act', lambda e, stt=stt, jk=jk, c=c, ch0=ch0: e.activation(out=jk[:, :], in_=yacc[:, c, ch0:ch0 + 64], func=AF.Copy, accum_out=stt[:, 0:1]), reads=[yaccb[c]], writes=[sttb, jkb])
                kb.op('act', lambda e, stt=stt, jk=jk, c=c, ch0=ch0: e.activation(out=jk[:, :], in_=yacc[:, c, ch0:ch0 + 64], func=AF.Square, accum_out=stt[:, 1:2]), reads=[yaccb[c]], writes=[sttb, jkb])
                kb.op('dve', lambda e, stt=stt: e.tensor_scalar(out=stt[:, 6:7], in0=stt[:, 0:1], scalar1=1.0 / 64, scalar2=None, op0=ALU.mult), reads=[sttb], writes=[sttb])
                kb.op('dve', lambda e, stt=stt: e.tensor_tensor(out=stt[:, 3:4], in0=stt[:, 6:7], in1=stt[:, 6:7], op=ALU.mult), reads=[sttb], writes=[sttb])
                kb.op('dve', lambda e, stt=stt: e.scalar_tensor_tensor(out=stt[:, 7:8], in0=stt[:, 1:2], scalar=1.0 / 64, in1=stt[:, 3:4], op0=ALU.mult, op1=ALU.subtract), reads=[sttb], writes=[sttb])
                kb.op('act', lambda e, stt=stt: e.activation(out=stt[:, 4:5], in_=stt[:, 7:8], func=AF.Sqrt, bias=gneps[:, 0:1]), reads=[sttb, gnepsb], writes=[sttb])
                kb.op('dve', lambda e, stt=stt: e.reciprocal(out=stt[:, 5:6], in_=stt[:, 4:5]), reads=[sttb], writes=[sttb])
                kb.op('dve', lambda e, stt=stt, fn=fn, c=c, ch0=ch0: e.tensor_scalar(out=fn[:, ch0:ch0 + 64], in0=yacc[:, c, ch0:ch0 + 64], scalar1=stt[:, 6:7], scalar2=stt[:, 5:6], op0=ALU.subtract, op1=ALU.mult),
                      reads=[sttb, yaccb[c]], writes=[fnb])
            if DBG.get('dumpfin') == c:
                dump(kb, nc, 'stt', stt[:, :], [128, 8], [sttb])
                dump(kb, nc, 'fn0', fn[:, :], [128, 192], [fnb])
                dump(kb, nc, 'gt', gt[:, :], [128, 192], [gtb])
                dump(kb, nc, 'yaccf', yacc[:, c, :], [128, 192], [yaccb[c]])
                dump(kb, nc, 'lng', lng[:, :], [128, 192], [lngb])
            kb.op('pool', lambda e, fn=fn: e.tensor_tensor(out=fn[:, :], in0=fn[:, :], in1=lng[:, :], op=ALU.mult), reads=[fnb, lngb], writes=[fnb])
            kb.op('pool', lambda e, fn=fn: e.tensor_tensor(out=fn[:, :], in0=fn[:, :], in1=lnb[:, :], op=ALU.add), reads=[fnb, lnbb], writes=[fnb])
            kb.op('dve', lambda e, fn=fn, gt=gt: e.tensor_tensor(out=fn[:, :], in0=fn[:, :], in1=gt[:, :], op=ALU.mult), reads=[fnb, gtb], writes=[fnb])
            kb.op('pool', lambda e, fn=fn, t0=t0: e.dma_start(out=YR.rows(t0, 0, 192), in_=fn[:, :]), reads=[fnb], dma=True)


    def dir_gen(d):
        for c in (DBG['chunks'][d] if 'chunks' in DBG else CHUNK_ORDER[d]):
            yield from chunk_body(d, c)

    dirs = DBG.get('dirs', [0, 1])
    for d in dirs:
        kb.op('dve', lambda e, d=d: e.memset(ST[:, d, :, :], 0.0), writes=STb[d])
        kb.op('dve', lambda e, d=d: e.memset(STbf[:, d, :, :], 0.0), writes=STb[d])
    gens = [dir_gen(d) for d in dirs]
    if DBG.get('no_interleave'):
        for g_ in gens:
            for _ in g_:
                pass
    else:
        while gens:
            for g_ in list(gens):
                try:
                    next(g_)
                except StopIteration:
                    gens.remove(g_)


FN_PARAMS = {'c128': [128, 128], 'ns128': [128, 128], 'twc': [128, 64], 'tws': [128, 64], 'fl1': [128, 64], 'fl2': [128, 64],
             'c256': [128, 2, 256], 'ns256': [128, 2, 256]}


def phase_fnet(kb, nc, HT, GT, YF, ZS, P, bar):
    sc_l = 1.0 / float(np.sqrt(L * 64.0))
    sc_c = 1.0 / float(np.sqrt(LC * 64.0))
    with ExitStack() as c1:
        def const(name, shape, src):
            t = kb.sb(name, shape, F32, c1)
            b, = load_consts(kb, [(t, src)])
            return t, b
        c128, c128b = const('c128', [128, 128], P['c128'])
        ns128, ns128b = const('ns128', [128, 128], P['ns128'])
        twc, twcb = const('twc', [128, 64], P['twc'])
        tws, twsb = const('tws', [128, 64], P['tws'])
        c256, c256b = const('c256', [128, 2, 256], P['c256'])
        ns256, ns256b = const('ns256', [128, 2, 256], P['ns256'])
        hc = kb.sb('hctx', [128, 2, 128], F32, c1)
        hcb = Buf()
        gc = kb.sb('gctx', [128, 2, 64], F32, c1)
        gcb = Buf()
        for lt in range(2):
            kb.op('sp', lambda e, lt=lt: e.dma_start(out=hc[:, lt, :], in_=HT[lt * 128:(lt + 1) * 128, :]), writes=[hcb], dma=True)
            kb.op('sp', lambda e, lt=lt: e.dma_start(out=gc[:, lt, :], in_=GT[lt * 128:(lt + 1) * 128, 192:256]), writes=[gcb], dma=True)
        pc = Ring(kb, 'pfc', [128, 512], F32, 1, psum=True, es=c1)
        fo = Ring(kb, 'foc', [128, 64], F32, 2, es=c1)
        for lo in range(2):
            ps, psb = pc.next()
            for lt in range(2):
                kb.op('pe', lambda e, ps=ps, lt=lt, lo=lo: e.matmul(ps[:, 0:64], lhsT=c256[:, lt, lo * 128:(lo + 1) * 128], rhs=hc[:, lt, 0:64], start=(lt == 0), stop=False),
                      reads=[c256b, hcb], writes=[psb])
            for lt in range(2):
                kb.op('pe', lambda e, ps=ps, lt=lt, lo=lo: e.matmul(ps[:, 0:64], lhsT=ns256[:, lt, lo * 128:(lo + 1) * 128], rhs=hc[:, lt, 64:128], start=False, stop=(lt == 1)),
                      reads=[ns256b, hcb], writes=[psb])
            f, fb = fo.next()
            kb.op('dve', lambda e, f=f, ps=ps, lo=lo: e.scalar_tensor_tensor(out=f[:, :], in0=ps[:, 0:64], scalar=sc_c, in1=gc[:, lo, :], op0=ALU.mult, op1=ALU.mult),
                  reads=[psb, gcb], writes=[fb])
            kb.op('pool', lambda e, f=f, lo=lo: e.dma_start(out=YF.rows(lo * 128, 192, 256), in_=f[:, :]), reads=[fb], dma=True)
        h1 = kb.sb('h1', [128, 64, 128], F32, c1)
        h1b = Buf()
        HTl = HT[LC:LT, :].rearrange("(a b) c -> a b c", b=64)
        for j in range(4):
            kb.op('sp', lambda e, j=j: e.dma_start(out=h1[:, j * 16:(j + 1) * 16, :], in_=HTl[:, j * 16:(j + 1) * 16, :]), writes=[h1b], dma=True)
        py = Ring(kb, 'py', [128, 512], F32, 4, psum=True, es=c1)
        zt = Ring(kb, 'zt', [128, 4, 128], F32, 6, es=c1)
        zo = Ring(kb, 'zo', [128, 2, 4, 128], F32, 3, es=c1)
        for pc_ in range(16):
            b0 = pc_ * 4
            pr, prb = py.next()
            pi, pib = py.next()
            kb.op('pe', lambda e, pr=pr, b0=b0: e.matmul(pr[:, :], lhsT=c128[:, :], rhs=h1[:, b0:b0 + 4, :], start=True, stop=True), reads=[c128b, h1b], writes=[prb])
            kb.op('pe', lambda e, pi=pi, b0=b0: e.matmul(pi[:, :], lhsT=ns128[:, :], rhs=h1[:, b0:b0 + 4, :], start=True, stop=True), reads=[ns128b, h1b], writes=[pib])
            cb_ = twc[:, b0:b0 + 4].unsqueeze(2).to_broadcast([128, 4, 128])
            sb_ = tws[:, b0:b0 + 4].unsqueeze(2).to_broadcast([128, 4, 128])
            prv = pr[:, :].rearrange("p (b c) -> p b c", c=128)
            piv = pi[:, :].rearrange("p (b c) -> p b c", c=128)
            t1, t1b = zt.next()
            t2, t2b = zt.next()
            z, zb = zo.next()
            kb.op('dve', lambda e, t1=t1, prv=prv, cb_=cb_: e.tensor_tensor(out=t1[:, :, :], in0=prv, in1=cb_, op=ALU.mult), reads=[prb, twcb], writes=[t1b])
            kb.op('dve', lambda e, t2=t2, piv=piv, sb_=sb_: e.tensor_tensor(out=t2[:, :, :], in0=piv, in1=sb_, op=ALU.mult), reads=[pib, twsb], writes=[t2b])
            kb.op('pool', lambda e, z=z, t1=t1, t2=t2: e.tensor_tensor(out=z[:, 0, :, :], in0=t1[:, :, :], in1=t2[:, :, :], op=ALU.add), reads=[t1b, t2b], writes=[zb])
            t3, t3b = zt.next()
            t4, t4b = zt.next()
            kb.op('dve', lambda e, t3=t3, piv=piv, cb_=cb_: e.tensor_tensor(out=t3[:, :, :], in0=piv, in1=cb_, op=ALU.mult), reads=[pib, twcb], writes=[t3b])
            kb.op('dve', lambda e, t4=t4, prv=prv, sb_=sb_: e.tensor_tensor(out=t4[:, :, :], in0=prv, in1=sb_, op=ALU.mult), reads=[prb, twsb], writes=[t4b])
            kb.op('pool', lambda e, z=z, t3=t3, t4=t4: e.tensor_tensor(out=z[:, 1, :, :], in0=t3[:, :, :], in1=t4[:, :, :], op=ALU.subtract), reads=[t3b, t4b], writes=[zb])
            for ri in range(2):
                kb.op('pool', lambda e, z=z, ri=ri, b0=b0: e.dma_start(out=ZS[ri, :, b0:b0 + 4, :], in_=z[:, ri, :, :]), reads=[zb], dma=True)
        kb.barrier(bar[:])
    with ExitStack() as c2:
        fl1 = kb.sb('fl1', [128, 64], F32, c2)
        fl2 = kb.sb('fl2', [128, 64], F32, c2)
        fl1b, fl2b = load_consts(kb, [(fl1, P['fl1']), (fl2, P['fl2'])])
        rz1 = kb.sb('rz1', [128, 128, 64], F32, c2)
        rz2 = kb.sb('rz2', [128, 128, 64], F32, c2)
        g2 = kb.sb('g2', [64, 128, 64], F32, c2)
        fo2 = kb.sb('fo2', [64, 128, 64], F32, c2)
        rz1b = [Buf() for _ in range(4)]
        rz2b = [Buf() for _ in range(4)]
        g2b = Buf()
        fo2b = [Buf() for _ in range(4)]
        for qa in range(4):
            asl = slice(qa * 32, (qa + 1) * 32)
            for ri in range(2):
                src = ZS[ri].rearrange("a b c -> b a c")
                kb.op('sp', lambda e, ri=ri, src=src, asl=asl: e.dma_start(out=rz1[ri * 64:(ri + 1) * 64, asl, :], in_=src[:, asl, 0:64]), writes=[rz1b[qa]], dma=True)
                kb.op('sp', lambda e, ri=ri, src=src, asl=asl: e.dma_start(out=rz2[ri * 64:(ri + 1) * 64, asl, :], in_=src[:, asl, 64:128]), writes=[rz2b[qa]], dma=True)
        kb.op('sp', lambda e: e.dma_start(out=g2[:, :, :], in_=GT[LC:LT, 192:256].rearrange("(b a) c -> b a c", a=128)), writes=[g2b], dma=True)
        pf = Ring(kb, 'pf', [128, 512], F32, 2, psum=True, es=c2)
        for pc_ in range(16):
            a0 = pc_ * 8
            qa = pc_ // 4
            ps, psb = pf.next()
            kb.op('pe', lambda e, ps=ps, a0=a0: e.matmul(ps[0:64, :], lhsT=fl1[:, :], rhs=rz1[:, a0:a0 + 8, :], start=True, stop=False), reads=[fl1b, rz1b[qa]], writes=[psb])
            kb.op('pe', lambda e, ps=ps, a0=a0: e.matmul(ps[0:64, :], lhsT=fl2[:, :], rhs=rz2[:, a0:a0 + 8, :], start=False, stop=True), reads=[fl2b, rz2b[qa]], writes=[psb])
            kb.op('dve', lambda e, ps=ps, a0=a0: e.scalar_tensor_tensor(out=fo2[:, a0:a0 + 8, :], in0=ps[0:64, :].rearrange("p (a c) -> p a c", c=64), scalar=sc_l,
                                                                        in1=g2[:, a0:a0 + 8, :], op0=ALU.mult, op1=ALU.mult), reads=[psb, g2b], writes=[fo2b[qa]])
        allfo = fo2b
        for (dst, b0, b1) in YF.lat_groups():
            kb.op('pool', lambda e, dst=dst, b0=b0, b1=b1: e.dma_start(out=dst, in_=fo2[b0:b1, :, :]), reads=allfo, dma=True)
        kb.barrier(bar[:])


def gate_rows(kb, es, nc, sil, silb, adaw_gate, adab_row, npost_b, ones1, ones1b, keep_es, NG=None):
    if NG is None:
        NG = kb.sb('NG', [128, 2, D], F32, keep_es)
    NGb = Buf()
    wg = kb.sb('adawg', [128, 8, D], F32, es)
    wgb = Buf()
    for k in range(8):
        kb.op('sp', lambda e, k=k: e.dma_start(out=wg[:, k, :], in_=adaw_gate[k * 128:(k + 1) * 128, :]), writes=[wgb], dma=True)
    br = kb.sb('adabr', [1, D], F32, es)
    brb, = load_consts(kb, [(br, adab_row)])
    npb = kb.sb('npostb', [128, D], F32, es)
    npbb, = load_consts(kb, [(npb, npost_b)])
    onesq = kb.sb('onesq', [128, 128], F32, es)
    onesqb = Buf()
    kb.op('dve', lambda e: e.memset(onesq[:], 1.0), writes=[onesqb])
    srep = kb.sb('silrep', [128, 8, 128], F32, es)
    pg_ = Ring(kb, 'pgate', [128, 512], F32, 2, psum=True, es=es)
    for n in range(2):
        srb = Buf()
        for k in range(8):
            kb.op('dve', lambda e, k=k, n=n: e.tensor_scalar(out=srep[:, k, :], in0=onesq[:, :], scalar1=sil[:, k, n:n + 1], scalar2=None, op0=ALU.mult),
                  reads=[onesqb, silb], writes=[srb])
        for half in range(2):
            ps, psb = pg_.next()
            for k in range(8):
                kb.op('pe', lambda e, ps=ps, k=k, half=half: e.matmul(ps[:, :], lhsT=srep[:, k, :], rhs=wg[:, k, half * 512:(half + 1) * 512], start=(k == 0), stop=False),
                      reads=[srb, wgb], writes=[psb])
            kb.op('pe', lambda e, ps=ps, half=half: e.matmul(ps[:, :], lhsT=ones1[0:1, :], rhs=br[0:1, half * 512:(half + 1) * 512], start=False, stop=True),
                  reads=[ones1b, brb], writes=[psb])
            kb.op('dve', lambda e, ps=ps, n=n, half=half: e.tensor_tensor(out=NG[:, n, half * 512:(half + 1) * 512], in0=ps[:, :], in1=npb[:, half * 512:(half + 1) * 512], op=ALU.mult),
                  reads=[psb, npbb], writes=[NGb])
    return NG, NGb


class OutProj:
    def __init__(self, kb, es, nc, yT, w_out_src, xsrc, NG, NGb, epsT, epsb, nslot=3, ysrc=None, ident=None, npol=2):
        self.kb, self.yT, self.xsrc, self.NG, self.NGb, self.epsT, self.epsb = kb, yT, xsrc, NG, NGb, epsT, epsb
        self.ysrc, self.ident = ysrc, ident
        self.pref = {}
        self.ykbs = {}
        if ysrc is not None:
            self.ytok = Ring(kb, 'ytok', [128, 4, 256], F32, 2, es=es)
            self.ytokb = Ring(kb, 'ytokb', [128, 1024], BF16, 2, es=es)
            self.pyt = Ring(kb, 'pyt', [128, 8, 128], BF16, 1, psum=True, es=es)
        self.wo, self.wob = load_weight_bf16(kb, es, nc, 'wout_bf', w_out_src, D, es)
        self.ystg = Ring(kb, 'ystg', [128, 8, 128], F32, 2, es=es)
        self.ybf = Ring(kb, 'ybf', [128, 8, 128], BF16, 2, es=es)
        self.xin = Ring(kb, 'xres', [128, D], F32, 2, es=es)
        self.xout = Ring(kb, 'xnew', [128, D], F32, nslot, es=es)
        self.pol = Ring(kb, 'pol', [128, 512], F32, npol, psum=True, es=es)
        self.st = Ring(kb, 'ost', [128, 8], F32, 4, es=es, strict=True)
        self.junk = kb.sb('ojunk', [128, 512], BF16, es)
        self.junkb = Buf()
        self.tmp = Ring(kb, 'otmp', [128, D], F32, 2, es=es)

    def _pre(self, r0):
        kb = self.kb
        if self.ysrc is None:
            ys, ysb = self.ystg.next()
            yT = self.yT
            kb.op('sp', lambda e, ys=ys: e.dma_start(out=ys[:, :, :], in_=yT[:, r0:r0 + 128].rearrange("(k p) t -> p k t", p=128)), writes=[ysb], dma=True)
            yb, ybb = self.ybf.next()
            kb.op('pool', lambda e, yb=yb, ys=ys: e.tensor_copy(out=yb[:, :, :], in_=ys[:, :, :]), reads=[ysb], writes=[ybb])
        else:
            aps, sbufs = self.ysrc(r0)
            ykb16, ykb0 = self.ytokb.next()
            ykb16b = self.ykbs.setdefault(id(ykb0), [ykb0] + [Buf() for _ in range(3)])
            for r, ap_ in enumerate(aps):
                kb.op('sp', lambda e, ykb16=ykb16, r=r, ap_=ap_: e.dma_start(out=ykb16[:, r * 256:(r + 1) * 256], in_=ap_), reads=sbufs, writes=[ykb16b[r]], dma=True)
            pt, ptb = self.pyt.next()
            idt, idtb = self.ident
            for k in range(8):
                kb.op('pe', lambda e, pt=pt, ykb16=ykb16, k=k: e.transpose(out=pt[:, k, :], in_=ykb16[:, k * 128:(k + 1) * 128], identity=idt[:, :]), reads=[ykb16b, idtb], writes=[ptb])
            yb, ybb = self.ybf.next()
            kb.op('act', lambda e, yb=yb, pt=pt: e.activation(out=yb[:, :, :], in_=pt[:, :, :], func=AF.Copy), reads=[ptb], writes=[ybb])
        xi, xib = self.xin.next()
        kb.op('sp', lambda e, xi=xi: e.dma_start(out=xi[:, :], in_=self.xsrc[r0:r0 + 128, :]), writes=[xib], dma=True)
        self.pref[r0] = (yb, ybb, xi, xib)

    def tile(self, r0, n, nxt=None):
        kb = self.kb
        if r0 not in self.pref:
            self._pre(r0)
        if nxt is not None and nxt not in self.pref:
            self._pre(nxt)
        yb, ybb, xi, xib = self.pref.pop(r0)
        pss = [self.pol.next(), self.pol.next()]
        for half in range(2):
            ps, psb = pss[half]
            for k in range(8):
                kb.op('pe', lambda e, ps=ps, k=k, half=half, yb=yb: e.matmul(ps[:, :], lhsT=yb[:, k, :], rhs=self.wo[:, k, half * 512:(half + 1) * 512],
                                                                           start=(k == 0), stop=(k == 7)), reads=[ybb, self.wob], writes=[psb])
        st, stb = self.st.next()
        for half in range(2):
            ps, psb = pss[half]
            kb.op('act', lambda e, ps=ps, st=st, half=half: e.activation(out=self.junk[:, :], in_=ps[:, :], func=AF.Square, accum_out=st[:, half:half + 1]),
                  reads=[psb], writes=[stb, self.junkb])
        kb.op('dve', lambda e, st=st: e.tensor_tensor(out=st[:, 2:3], in0=st[:, 0:1], in1=st[:, 1:2], op=ALU.add), reads=[stb], writes=[stb])
        kb.op('act', lambda e, st=st: e.activation(out=st[:, 3:4], in_=st[:, 2:3], func=AF.Sqrt, scale=1.0 / D, bias=self.epsT[:, 0:1]), reads=[stb, self.epsb], writes=[stb])
        kb.op('dve', lambda e, st=st: e.reciprocal(out=st[:, 4:5], in_=st[:, 3:4]), reads=[stb], writes=[stb])
        tm_, tmb_ = self.tmp.next()
        for half in range(2):
            ps, psb = pss[half]
            kb.op('dve', lambda e, ps=ps, st=st, tm_=tm_, half=half: e.scalar_tensor_tensor(out=tm_[:, half * 512:(half + 1) * 512], in0=ps[:, :], scalar=st[:, 4:5],
                                                                                       in1=self.NG[:, n, half * 512:(half + 1) * 512], op0=ALU.mult, op1=ALU.mult),
                  reads=[psb, stb, self.NGb], writes=[tmb_])
        xo, xob = self.xout.next()
        kb.op('pool', lambda e, xo=xo, tm_=tm_, xi=xi: e.tensor_tensor(out=xo[:, :], in0=tm_[:, :], in1=xi[:, :], op=ALU.add), reads=[tmb_, xib], writes=[xob])
        return xo, xob


L3_TOK = 2048


def part3(kb, nc, IN, ZG, X1s, OUT, C):
    epsT, epsb, ones1, ones1b = C['epsT'], C['epsb'], C['ones1'], C['ones1b']
    with ExitStack() as g:
        sil = kb.sb('sil3', [128, 8, 2], F32, g)
        silb = Buf(strict=True)
        kb.op('sp', lambda e: e.dma_start(out=sil[:], in_=IN['sil_in']), writes=[silb], dma=True)
        kb.op('act', lambda e: e.activation(out=sil[:], in_=sil[:], func=AF.Silu), reads=[silb], writes=[silb])
        NG = kb.sb('NG3', [128, 2, D], F32, g)
        with ExitStack() as g0:
            NG, NGb = gate_rows(kb, g0, nc, sil, silb, IN['adawg1'], IN['adabr1'], IN['npostb1'], ones1, ones1b, g0, NG=NG)
            kb.barrier(C['bar'][:])
        op = OutProj(kb, g, nc, None, IN['wout1'], X1s, NG, NGb, epsT, epsb, nslot=4, ysrc=ZG.src, ident=(C['ident_bf'], C['identb']), npol=6)
        for i in range(L // 128):
            xo, xob = op.tile(i * 128, 0, nxt=((i + 1) * 128 if (i + 1) * 128 < L else None))
            kb.op('pool', lambda e, xo=xo, i=i: e.dma_start(out=OUT[i * 128:(i + 1) * 128, :], in_=xo[:, :]), reads=[xob], dma=True, final=True)
        kb.barrier(C['bar'][:])


RT_PARAMS = {'logit': [128, 4], 'diffT': [128, 2, 128], 'mask01T': [128, 2, 128], 'posxi': [128, 2, 128], 'poszeta': [128, 2]}


def part2(kb, nc, IN, YG, X1s, ZD, C):
    debug = False
    sil_in, adawg, adabr, npostb, adaw1, adab1, normw1, win1, ropec, ropes = (IN[k_] for k_ in (
        'sil_in', 'adawg0', 'adabr0', 'npostb0', 'adaw1', 'adab1', 'normw1', 'win1', 'ropec', 'ropes'))
    xin, wout0 = IN['xin'], IN['wout0p']
    RP = {k_: IN['r_' + k_] for k_ in RT_PARAMS}
    X1 = X1s
    Z = ZD
    QKV = dram_tmp(nc, 'QKV', [LT, 768], BF16, debug)
    GS = dram_tmp(nc, 'GS', [LT, 256], F32, debug)
    bar, ident_f, identfb, ident_bf, identb, epsT, epsb, ones1, ones1b = (C[k_] for k_ in (
        'bar', 'ident_f', 'identfb', 'ident_bf', 'identb', 'epsT', 'epsb', 'ones1', 'ones1b'))
    with ExitStack() as pa:
        sil = kb.sb('sil', [128, 8, 2], F32, pa)
        silb = Buf(strict=True)
        kb.op('sp', lambda e: e.dma_start(out=sil[:], in_=sil_in), writes=[silb], dma=True)
        kb.op('act', lambda e: e.activation(out=sil[:], in_=sil[:], func=AF.Silu), reads=[silb], writes=[silb])
        NG = kb.sb('NG', [128, 2, D], F32, pa)
        mod1 = kb.sb('mod1', [128, 16, 2], F32, pa)
        G1 = kb.sb('G1', [128, 8, 2], F32, pa)
        with ExitStack() as pg0:
            NG, NGb = gate_rows(kb, pg0, nc, sil, silb, adawg, adabr, npostb, ones1, ones1b, pg0, NG=NG)
            mod1, mod1b = adaln_vectors(kb, pg0, nc, sil_in, adaw1, adab1, normw1, 16, mod_tile=mod1)
            nw = kb.sb('nw1', [128, 8], F32, pg0)
            nwb, = load_consts(kb, [(nw, normw1)])
            G1buf = Buf(strict=True)
            for n in range(2):
                kb.op('dve', lambda e, n=n: e.scalar_tensor_tensor(out=G1[:, :, n], in0=mod1[:, 8:16, n], scalar=1.0, in1=nw[:, :], op0=ALU.add, op1=ALU.mult),
                      reads=[mod1b, nwb], writes=[G1buf])
            kb.barrier(bar[:])
        Gb = {'G': G1buf, 'eps': epsT, 'epsb': epsb}
        op0 = OutProj(kb, pa, nc, None, wout0, xin, NG, NGb, epsT, epsb, nslot=3, ysrc=YG.src, ident=(ident_bf, identb))
        w1, w1b = load_weight_bf16(kb, pa, nc, 'w1_bf', win1, D, pa)
        for k in range(8):
            kb.op('pool', lambda e, k=k: e.tensor_scalar(out=w1[:, k, 256:512], in0=w1[:, k, 256:512], scalar1=float(128 ** -0.5), scalar2=None, op0=ALU.mult), reads=[w1b], writes=[w1b])
        pqk = Ring(kb, 'pqk', [128, 512], F32, 2, psum=True, es=pa)
        csr = Ring(kb, 'cs', [128, 2, 64], F32, 2, es=pa)
        qkvr = Ring(kb, 'qkvt', [128, 768], BF16, 2, es=pa)
        rtmp = Ring(kb, 'rtmp', [128, 4, 64], F32, 4, es=pa)
        gsr = Ring(kb, 'gst', [128, 256], F32, 2, es=pa)

        rlist = [t0_ + sb_ * 128 for (t0_, T_) in TILES for sb_ in range(T_ // 128)]

        def get_x(r0):
            n = 1 if r0 < LC else 0
            ix = rlist.index(r0)
            xo, xob = op0.tile(r0, n, nxt=(rlist[ix + 1] if ix + 1 < len(rlist) else None))
            if r0 >= LC:
                kb.op('pool', lambda e, xo=xo, r0=r0: e.dma_start(out=X1[r0 - LC:r0 - LC + 128, :], in_=xo[:, :]), reads=[xob], dma=True)
            return xo, xob

        def emit_tile(t0, T, hT, hb):
            for sub in range(T // 128):
                r0 = t0 + sub * 128
                psA, psAb = pqk.next()
                psB, psBb = pqk.next()
                for half, (ps, psb) in enumerate([(psA, psAb), (psB, psBb)]):
                    for k in range(8):
                        kb.op('pe', lambda e, ps=ps, k=k, half=half, sub=sub: e.matmul(ps[:, :], lhsT=hT[:, k, sub * 128:(sub + 1) * 128], rhs=w1[:, k, half * 512:(half + 1) * 512],
                                                                                     start=(k == 0), stop=(k == 7)), reads=[hb, w1b], writes=[psb])
                qkv, qkvb = qkvr.next()
                if r0 >= LC:
                    cs, csb = csr.next()
                    kb.op('sp', lambda e, cs=cs, r0=r0: e.dma_start(out=cs[:, 0, :], in_=ropec[r0 - LC:r0 - LC + 128, :]), writes=[csb], dma=True)
                    kb.op('sp', lambda e, cs=cs, r0=r0: e.dma_start(out=cs[:, 1, :], in_=ropes[r0 - LC:r0 - LC + 128, :]), writes=[csb], dma=True)
                    pv_ = psA[:, :].rearrange("p (g h f) -> p g h f", g=4, h=2)
                    ov_ = qkv[:, 0:512].rearrange("p (g h f) -> p g h f", g=4, h=2)
                    cb_ = cs[:, 0, :].unsqueeze(1).to_broadcast([128, 4, 64])
                    sb_ = cs[:, 1, :].unsqueeze(1).to_broadcast([128, 4, 64])
                    ta, tab = rtmp.next()
                    tb, tbb = rtmp.next()
                    kb.op('dve', lambda e, ta=ta, pv_=pv_, cb_=cb_: e.tensor_tensor(out=ta[:, :, :], in0=pv_[:, :, 0, :], in1=cb_, op=ALU.mult), reads=[psAb, csb], writes=[tab])
                    kb.op('dve', lambda e, tb=tb, pv_=pv_, sb_=sb_: e.tensor_tensor(out=tb[:, :, :], in0=pv_[:, :, 1, :], in1=sb_, op=ALU.mult), reads=[psAb, csb], writes=[tbb])
                    kb.op('pool', lambda e, ta=ta, tb=tb, ov_=ov_: e.tensor_tensor(out=ov_[:, :, 0, :], in0=ta[:, :, :], in1=tb[:, :, :], op=ALU.subtract), reads=[tab, tbb], writes=[qkvb])
                    tc_, tcb = rtmp.next()
                    td, tdb = rtmp.next()
                    kb.op('dve', lambda e, tc_=tc_, pv_=pv_, sb_=sb_: e.tensor_tensor(out=tc_[:, :, :], in0=pv_[:, :, 0, :], in1=sb_, op=ALU.mult), reads=[psAb, csb], writes=[tcb])
                    kb.op('dve', lambda e, td=td, pv_=pv_, cb_=cb_: e.tensor_tensor(out=td[:, :, :], in0=pv_[:, :, 1, :], in1=cb_, op=ALU.mult), reads=[psAb, csb], writes=[tdb])
                    kb.op('pool', lambda e, tc_=tc_, td=td, ov_=ov_: e.tensor_tensor(out=ov_[:, :, 1, :], in0=tc_[:, :, :], in1=td[:, :, :], op=ALU.add), reads=[tcb, tdb], writes=[qkvb])
                else:
                    kb.op('act', lambda e, qkv=qkv, psA=psA: e.activation(out=qkv[:, 0:512], in_=psA[:, :], func=AF.Copy), reads=[psAb], writes=[qkvb])
                kb.op('act', lambda e, qkv=qkv, psB=psB: e.activation(out=qkv[:, 512:768], in_=psB[:, 0:256], func=AF.Copy), reads=[psBb], writes=[qkvb])
                gs, gsb = gsr.next()
                kb.op('act', lambda e, gs=gs, psB=psB: e.activation(out=gs[:, :], in_=psB[:, 256:512], func=AF.Silu), reads=[psBb], writes=[gsb])
                kb.op('pool', lambda e, qkv=qkv, r0=r0: e.dma_start(out=QKV[r0:r0 + 128, :], in_=qkv[:, :]), reads=[qkvb], dma=True)
                kb.op('pool', lambda e, gs=gs, r0=r0: e.dma_start(out=GS[r0:r0 + 128, :], in_=gs[:, :]), reads=[gsb], dma=True)

        phase_proj(kb, nc, pa, None, None, G1, mod1, Gb, ident_bf, identb, emit_tile, get_x=get_x)
        kb.barrier(bar[:])
    with ExitStack() as pr:
        phase_ret(kb, nc, pr, QKV, GS, Z, RP, ident_bf, identb, epsT, epsb)
        kb.barrier(bar[:])


def phase_ret(kb, nc, es, QKV, GS, Z, RP, ident_bf, identb, epsT, epsb):
    def const(name, shape, src):
        t = kb.sb(name, shape, F32, es)
        b, = load_consts(kb, [(t, src)])
        return t, b
    lgt, lgtb = const('lgt', [128, 4], RP['logit'])
    lgtb.strict = True
    diffT, diffTb = const('diffT', [128, 2, 128], RP['diffT'])
    m01, m01b = const('m01', [128, 2, 128], RP['mask01T'])
    pxi, pxib = const('pxi', [128, 2, 128], RP['posxi'])
    pze, pzeb = const('pze', [128, 2], RP['poszeta'])
    kb.op('act', lambda e: e.activation(out=lgt[:, :], in_=lgt[:, :], func=AF.Exp, scale=-1.0), reads=[lgtb], writes=[lgtb])
    kb.op('dve', lambda e: e.tensor_scalar(out=lgt[:, :], in0=lgt[:, :], scalar1=1.0, scalar2=None, op0=ALU.add), reads=[lgtb], writes=[lgtb])
    kb.op('act', lambda e: e.activation(out=lgt[:, :], in_=lgt[:, :], func=AF.Ln), reads=[lgtb], writes=[lgtb])
    kb.op('dve', lambda e: e.tensor_scalar(out=lgt[:, :], in0=lgt[:, :], scalar1=-1.0, scalar2=None, op0=ALU.mult), reads=[lgtb], writes=[lgtb])
    dmt = kb.sb('dmt', [128, 4, 128], BF16, es)
    xib = kb.sb('xib', [128, 4, 128], F32, es)
    zet = kb.sb('zet', [128, 8], F32, es)
    tabb = Buf(strict=True)
    tmpd = kb.sb('tmpd', [128, 128], F32, es)
    tmpdb = Buf()
    for hl in range(2):
        for dr in range(2):
            j = 2 * hl + dr
            kb.op('act', lambda e, j=j, dr=dr: e.activation(out=tmpd[:, :], in_=diffT[:, dr, :], func=AF.Exp, scale=lgt[:, j:j + 1]), reads=[diffTb, lgtb], writes=[tmpdb])
            kb.op('dve', lambda e, j=j, dr=dr: e.tensor_tensor(out=dmt[:, j, :], in0=tmpd[:, :], in1=m01[:, dr, :], op=ALU.mult), reads=[tmpdb, m01b], writes=[tabb])
            kb.op('act', lambda e, j=j, dr=dr: e.activation(out=xib[:, j, :], in_=pxi[:, dr, :], func=AF.Exp, scale=lgt[:, j:j + 1]), reads=[pxib, lgtb], writes=[tabb])
            kb.op('act', lambda e, j=j, dr=dr: e.activation(out=zet[:, j:j + 1], in_=pze[:, dr:dr + 1], func=AF.Exp, scale=lgt[:, j:j + 1]), reads=[pzeb, lgtb], writes=[tabb])
            kb.op('act', lambda e, j=j: e.activation(out=zet[:, 4 + j:5 + j], in_=lgt[:, j:j + 1], func=AF.Exp, scale=128.0), reads=[lgtb], writes=[tabb])
    oacc = kb.sb('oacc', [128, NCH, 256], F32, es)
    oaccb = [Buf() for _ in range(NCH)]
    kb.op('pool', lambda e: e.memset(oacc[:], 0.0), writes=oaccb)
    R = kb.sb('Rst', [128, 2, 128], F32, es)
    Rbf = kb.sb('Rbf', [128, 2, 128], BF16, es)
    Rb = [Buf(), Buf()]
    qr = Ring(kb, 'rq', [128, 768], BF16, 3, es=es)
    qtr = Ring(kb, 'rqT', [128, 128], BF16, 3, es=es)
    ktr_ = Ring(kb, 'rkT', [128, 128], BF16, 3, es=es)
    qxr = Ring(kb, 'rqx', [128, 128], BF16, 3, es=es)
    kzr = Ring(kb, 'rkz', [128, 128], BF16, 3, es=es)
    sdr = Ring(kb, 'rsd', [128, 128], BF16, 3, es=es)
    ptT = Ring(kb, 'rptT', [128, 1024], BF16, 2, psum=True, es=es)
    pS = Ring(kb, 'rpS', [128, 512], F32, 2, psum=True, es=es)
    pO = Ring(kb, 'rpO', [128, 512], F32, 2, psum=True, es=es)
    pR = Ring(kb, 'rpR', [128, 512], F32, 2, psum=True, es=es)
    gsr = Ring(kb, 'rgs', [128, 256], F32, 2, es=es)
    zr = Ring(kb, 'rz', [128, 256], F32, 2, es=es)
    sst = Ring(kb, 'rss', [128, 8], F32, 4, es=es, strict=True)
    junk = kb.sb('rjunk', [128, 128], BF16, es)
    junkb = Buf()
    for dr in range(2):
        kb.op('dve', lambda e: e.memset(R[:], 0.0), writes=Rb)
        kb.op('dve', lambda e: e.memset(Rbf[:], 0.0), writes=Rb)
        for c in CHUNK_ORDER[dr]:
            r0 = c * 128
            q, qb = qr.next()
            kb.op('sp', lambda e, q=q, r0=r0: e.dma_start(out=q[:, :], in_=QKV[r0:r0 + 128, :]), writes=[qb], dma=True)
            for hl in range(2):
                j = 2 * hl + dr
                vt_ = q[:, 512 + hl * 128:512 + (hl + 1) * 128]
                ktok = q[:, 256 + hl * 128:256 + (hl + 1) * 128]
                if c >= 2:
                    pt, ptb = ptT.next()
                    kb.op('pe', lambda e, pt=pt, q=q, hl=hl: e.transpose(out=pt[:, 0:128], in_=q[:, hl * 128:(hl + 1) * 128], identity=ident_bf[:, :]), reads=[qb, identb], writes=[ptb])
                    kb.op('pe', lambda e, pt=pt, ktok=ktok: e.transpose(out=pt[:, 128:256], in_=ktok, identity=ident_bf[:, :]), reads=[qb, identb], writes=[ptb])
                    qT, qTb = qtr.next()
                    kT, kTb = ktr_.next()
                    qx, qxb = qxr.next()
                    kb.op('act', lambda e, qT=qT, pt=pt: e.activation(out=qT[:, :], in_=pt[:, 0:128], func=AF.Copy), reads=[ptb], writes=[qTb])
                    kb.op('dve', lambda e, qx=qx, pt=pt, j=j: e.tensor_tensor(out=qx[:, :], in0=pt[:, 0:128], in1=xib[:, j, :], op=ALU.mult), reads=[ptb, tabb], writes=[qxb])
                    kb.op('act', lambda e, kT=kT, pt=pt: e.activation(out=kT[:, :], in_=pt[:, 128:256], func=AF.Copy), reads=[ptb], writes=[kTb])
                    ps, psb = pS.next()
                    kb.op('pe', lambda e, ps=ps, kT=kT, qT=qT: e.matmul(ps[:, 0:128], lhsT=kT[:, :], rhs=qT[:, :], start=True, stop=True), reads=[kTb, qTb], writes=[psb])
                    sd, sdb = sdr.next()
                    kb.op('dve', lambda e, sd=sd, ps=ps, j=j: e.tensor_tensor(out=sd[:, :], in0=ps[:, 0:128], in1=dmt[:, j, :], op=ALU.mult), reads=[psb, tabb], writes=[sdb])
                    po, pob = pO.next()
                    kb.op('pe', lambda e, po=po, sd=sd, vt_=vt_: e.matmul(po[:, 0:128], lhsT=sd[:, :], rhs=vt_, start=True, stop=False), reads=[sdb, qb], writes=[pob])
                    kb.op('pe', lambda e, po=po, qx=qx, hl=hl: e.matmul(po[:, 0:128], lhsT=qx[:, :], rhs=Rbf[:, hl, :], start=False, stop=True), reads=[qxb, Rb[hl]], writes=[pob])
                    if dr == 0:
                        kb.op('act', lambda e, po=po, c=c, hl=hl: e.activation(out=oacc[:, c, hl * 128:(hl + 1) * 128], in_=po[:, 0:128], func=AF.Copy), reads=[pob], writes=[oaccb[c]])
                    else:
                        kb.op('dve', lambda e, po=po, c=c, hl=hl: e.tensor_tensor(out=oacc[:, c, hl * 128:(hl + 1) * 128], in0=po[:, 0:128], in1=oacc[:, c, hl * 128:(hl + 1) * 128], op=ALU.add),
                              reads=[pob, oaccb[c]], writes=[oaccb[c]])
                kz, kzb = kzr.next()
                kb.op('pool', lambda e, kz=kz, ktok=ktok, j=j: e.tensor_scalar(out=kz[:, :], in0=ktok, scalar1=zet[:, j:j + 1], scalar2=None, op0=ALU.mult), reads=[qb, tabb], writes=[kzb])
                pr_, prb = pR.next()
                kb.op('pe', lambda e, pr_=pr_, kz=kz, vt_=vt_: e.matmul(pr_[:, 0:128], lhsT=kz[:, :], rhs=vt_, start=True, stop=True), reads=[kzb, qb], writes=[prb])
                kb.op('dve', lambda e, pr_=pr_, hl=hl, j=j: e.scalar_tensor_tensor(out=R[:, hl, :], in0=R[:, hl, :], scalar=zet[:, 4 + j:5 + j], in1=pr_[:, 0:128], op0=ALU.mult, op1=ALU.add),
                      reads=[prb, tabb, Rb[hl]], writes=[Rb[hl]])
                kb.op('act', lambda e, hl=hl: e.activation(out=Rbf[:, hl, :], in_=R[:, hl, :], func=AF.Copy), reads=[Rb[hl]], writes=[Rb[hl]])
            if dr == 1 and c >= 2:
                gs, gsb = gsr.next()
                kb.op('sp', lambda e, gs=gs, r0=r0: e.dma_start(out=gs[:, :], in_=GS[r0:r0 + 128, :]), writes=[gsb], dma=True)
                zt_, ztb = zr.next()
                for hl in range(2):
                    ss, ssb = sst.next()
                    kb.op('act', lambda e, ss=ss, c=c, hl=hl: e.activation(out=junk[:, :], in_=oacc[:, c, hl * 128:(hl + 1) * 128], func=AF.Square, accum_out=ss[:, 0:1]), reads=[oaccb[c]], writes=[ssb, junkb])
                    kb.op('act', lambda e, ss=ss: e.activation(out=ss[:, 1:2], in_=ss[:, 0:1], func=AF.Sqrt, scale=1.0 / 128, bias=epsT[:, 0:1]), reads=[ssb, epsb], writes=[ssb])
                    kb.op('dve', lambda e, ss=ss: e.reciprocal(out=ss[:, 2:3], in_=ss[:, 1:2]), reads=[ssb], writes=[ssb])
                    kb.op('dve', lambda e, ss=ss, zt_=zt_, gs=gs, c=c, hl=hl: e.scalar_tensor_tensor(out=zt_[:, hl * 128:(hl + 1) * 128], in0=oacc[:, c, hl * 128:(hl + 1) * 128], scalar=ss[:, 2:3],
                                                                                                   in1=gs[:, hl * 128:(hl + 1) * 128], op0=ALU.mult, op1=ALU.mult), reads=[ssb, oaccb[c], gsb], writes=[ztb])
                kb.op('pool', lambda e, zt_=zt_, r0=r0: e.dma_start(out=Z.rows(r0 - LC, 0, 256), in_=zt_[:, :]), reads=[ztb], dma=True)


class ChunkedDram:
    def __init__(self, nc, name, nrows, rows_per, width, ranks=1, dtype=BF16):
        self.rp = rows_per
        self.n = nrows // rows_per
        assert self.n * rows_per == nrows
        self.ranks = ranks
        self.tiles = [dram_tmp(nc, '%s%d' % (name, i), [ranks * rows_per, width], dtype) for i in range(self.n)]
        self.bufs = [Buf() for _ in range(self.n)]

    def rows(self, r0, c0, c1, n=128):
        i, lr = r0 // self.rp, r0 % self.rp
        return self.tiles[i][lr:lr + n, c0:c1]

    def src(self, r0):
        i, lr = r0 // self.rp, r0 % self.rp
        return [self.tiles[i][r * self.rp + lr:r * self.rp + lr + 128, :] for r in range(self.ranks)], [self.bufs[i]]

    def lat_groups(self):
        out = []
        tpc = self.rp // 128
        for i in range(self.n):
            b0, b1 = max(tpc * i - 2, 0), min(tpc * i + tpc - 2, 64)
            if b1 <= b0:
                continue
            lrow = (b0 + 2) * 128 - i * self.rp
            out.append((self.tiles[i][lrow:lrow + (b1 - b0) * 128, 192:256].rearrange("(b a) c -> b a c", a=128), b0, b1))
        return out


GROUPS = [[0, 1, 2, 3], [4, 5, 6, 7]]


def all_gather(kb, src, dst):
    for i in range(src.n):
        kb.op('pool', lambda e, i=i: e.collective_compute("AllGather", ALU.bypass, replica_groups=GROUPS, ins=[src.tiles[i].opt()], outs=[dst.tiles[i].opt()]),
              writes=[dst.bufs[i]], cc=True)


FUSED_INPUTS = {'xin': [LT, D], 'sil_in': [128, 8, 2], 'adaw': [D, 2048], 'adab': [128, 16], 'normw': [128, 8], 'wfm': [D, NFM], 'wg': [D, 256],
                'cs64': [64, 128], 'ident': [128, 128],
                'wout0p': [D, D], 'adawg0': [D, D], 'adabr0': [1, D], 'npostb0': [128, D], 'adaw1': [D, 2048], 'adab1': [128, 16], 'normw1': [128, 8],
                'win1': [D, D], 'ropec': [L, 64], 'ropes': [L, 64],
                'wout1': [D, D], 'adawg1': [D, D], 'adabr1': [1, D], 'npostb1': [128, D]}


def build_fused():
    nc = bass.Bass("TRN2", target_bir_lowering=False)
    kb = KB(nc)
    IN = {k_: dram_in(nc, k_, shp) for k_, shp in FUSED_INPUTS.items()}
    for k_, shp in RW_PARAMS.items():
        IN['p_' + k_] = dram_in(nc, 'p_' + k_, shp)
    for k_, shp in FN_PARAMS.items():
        IN['f_' + k_] = dram_in(nc, 'f_' + k_, shp)
    for k_, shp in RT_PARAMS.items():
        IN['r_' + k_] = dram_in(nc, 'r_' + k_, shp)
    OUT = dram_out(nc, 'out', [L, D])
    YB = ChunkedDram(nc, 'Yb', LT, 768, 256)
    YG = ChunkedDram(nc, 'Yg', LT, 768, 256, ranks=4)
    ZB = ChunkedDram(nc, 'Zb', L, 512, 256)
    ZG = ChunkedDram(nc, 'Zg', L, 512, 256, ranks=4)
    X1s = dram_tmp(nc, 'X1s', [L, D], F32)
    C = {}
    C['bar'] = kb.sb('bar', [128, 1], F32)
    C['ident_f'] = kb.sb('ident_f', [128, 128], F32)
    C['ident_bf'] = kb.sb('ident_bf', [128, 128], BF16)
    C['identfb'], = load_consts(kb, [(C['ident_f'], IN['ident'])])
    C['identb'] = Buf()
    kb.op('dve', lambda e: e.tensor_copy(out=C['ident_bf'][:], in_=C['ident_f'][:]), reads=[C['identfb']], writes=[C['identb']])
    C['epsT'] = kb.sb('epsT', [128, 1], F32)
    C['epsb'] = Buf()
    kb.op('dve', lambda e: e.memset(C['epsT'][:], EPS), writes=[C['epsb']])
    C['ones1'] = kb.sb('ones1', [1, 128], F32)
    C['ones1b'] = Buf()
    kb.op('dve', lambda e: e.memset(C['ones1'][:], 1.0), writes=[C['ones1b']])
    part1(kb, nc, IN, YB, C)
    all_gather(kb, YB, YG)
    part2(kb, nc, IN, YG, X1s, ZB, C)
    all_gather(kb, ZB, ZG)
    part3(kb, nc, IN, ZG, X1s, OUT, C)
    return nc, kb


def l1_inputs(inp, core):
    b, q = core // 4, core % 4
    f = lambda a: np.ascontiguousarray(a, dtype=np.float32)
    xin = np.concatenate([inp['ctx'][b], inp['x'][b]], axis=0)
    sil = np.stack([inp['c'][b].reshape(8,128).T, inp['c_ctx'].reshape(8,128).T], axis=-1)
    adaw = inp['ada_w'][0][:, :2048]
    adab = inp['ada_b'][0][:2048].reshape(16,128).T
    normw = inp['norm_pre'][0].reshape(8,128).T
    W = inp['ev_w_in'][0]
    cols = np.concatenate([np.arange(192)+192*q, 768+np.arange(192)+192*q, 1536+np.arange(192)+192*q,
                           np.arange(2304,2432), np.arange(2432,2560), 3328+64*q+np.arange(64)])
    gcols = np.concatenate([2560+192*q+np.arange(192), 3584+64*q+np.arange(64)])
    c = np.arange(64)
    ang = 2*np.pi*np.outer(c,c)/64
    cs64 = np.concatenate([np.cos(ang), np.sin(ang)], axis=1)
    return dict(xin=f(xin), sil_in=f(sil), adaw=f(adaw), adab=f(adab), normw=f(normw), wfm=f(W[:, cols]), wg=f(W[:, gcols]),
                cs64=f(cs64), ident=np.eye(128, dtype=np.float32)), cols, gcols

def rw_params(inp, core):
    b, q = core // 4, core % 4
    f = lambda a: np.ascontiguousarray(a, dtype=np.float32)
    chs = 192*q + np.arange(192)
    def blk2(v):
        o = np.zeros((128,2), np.float32); o[:,0] = v[:128]; o[:64,1] = v[128:]; return o
    mu = inp['ev_mu'][0]
    mub = np.zeros((128,8), np.float32)
    for j, base in enumerate([0, 768, 1536]):
        m2 = blk2(mu[base+chs]); mub[:, 2*j] = m2[:,0]; mub[:, 2*j+1] = m2[:,1]
    mub[:, 6] = mu[2304:2432]; mub[:, 7] = mu[2432:2560]
    p = np.arange(128)
    lanem = np.stack([(p%4==0),(p%4==1),(p%4==2),(p%4==3),(p%2==0),(p%2==1)],axis=1).astype(np.float32)
    a0 = np.zeros((128,2,2), np.float32)
    for d in range(2): a0[:, d, :] = blk2(inp['ev_a0'][0][d][chs])
    w2 = np.concatenate([inp['ev_w2'][0][0][:, chs], inp['ev_w2'][0][1][:, chs]], axis=0)
    a2 = np.concatenate([inp['ev_a2'][0][0][:, chs], inp['ev_a2'][0][1][:, chs]], axis=0)
    w0 = np.stack([inp['ev_w0'][0][0][chs], inp['ev_w0'][0][1][chs]])[None]
    s = np.arange(128)[:,None]; t = np.arange(128)[None,:]
    strict = [(s<t), (s>t)]; incl = [(s<=t), (s>=t)]
    blk = lambda bs: (np.arange(128)[:,None]//bs == np.arange(128)[None,:]//bs)
    maskSI2 = np.stack([np.concatenate([strict[d], incl[d]],axis=1) for d in range(2)], axis=1).astype(np.float32)
    maskSI = np.stack([np.concatenate([strict[d] & blk(32), incl[d]],axis=1) for d in range(2)], axis=1).astype(np.float32)
    maskST = np.stack([np.stack([(strict[d] & blk(32)).T, (strict[d] & blk(64) & ~blk(32)).T, (strict[d] & ~blk(64)).T], axis=1) for d in range(2)], axis=1).astype(np.float32)
    cdec = -np.exp(-0.5)
    tri = np.stack([np.concatenate([incl[d], strict[d]],axis=1) for d in range(2)], axis=1).astype(np.float32)*cdec
    eh = np.zeros((128,2,4), np.float32); eh[:64,0,0]=1; eh[64:,0,1]=1; eh[:64,1,2]=1
    obd = np.zeros((128,128), np.float32); obd[:64,:64]=1; obd[64:,64:]=1
    return dict(p_mu=mub, p_lanem=lanem, p_k_k=blk2(inp['ev_k_k'][0][chs]), p_k_a=blk2(inp['ev_k_a'][0][chs]),
                p_r_k=blk2(inp['ev_r_k'][0].reshape(-1)[chs]), p_a0=a0, p_w2=f(w2), p_a2=f(a2), p_w0=f(w0),
                p_maskSI=f(maskSI), p_maskSI2=f(maskSI2), p_maskST=f(maskST), p_tri=f(tri), p_ehead=eh, p_ones_bd=obd,
                p_lnx_g=f(np.tile(inp['ev_lnx_g'][0][chs][None], (128,1))), p_lnx_b=f(np.tile(inp['ev_lnx_b'][0][chs][None], (128,1))))

def fn_params():
    f = lambda a: np.ascontiguousarray(a, dtype=np.float32)
    a = np.arange(128); b = np.arange(64)
    ang128 = 2*np.pi*np.outer(a,a)/128
    tw = 2*np.pi*np.outer(a, b)/8192
    ang64 = 2*np.pi*np.outer(b,b)/64
    fl1 = np.concatenate([np.cos(ang64), np.sin(ang64)], axis=0)
    fl2 = np.concatenate([-np.sin(ang64), np.cos(ang64)], axis=0)
    l = np.arange(256); ang256 = 2*np.pi*np.outer(l,l)/256
    c256 = np.cos(ang256).reshape(2,128,256).transpose(1,0,2); ns256 = (-np.sin(ang256)).reshape(2,128,256).transpose(1,0,2)
    return dict(f_c128=f(np.cos(ang128)), f_ns128=f(-np.sin(ang128)), f_twc=f(np.cos(tw)), f_tws=f(np.sin(tw)), f_fl1=f(fl1), f_fl2=f(fl2),
                f_c256=f(c256), f_ns256=f(ns256))


def gate_inputs(inp, b, layer):
    f = lambda a: np.ascontiguousarray(a, dtype=np.float32)
    sil = np.stack([inp['c'][b].reshape(8,128).T, inp['c_ctx'].reshape(8,128).T], axis=-1)
    return dict(sil_in=f(sil), adawg=f(inp['ada_w'][layer][:, 2048:3072]), adabr=f(inp['ada_b'][layer][2048:3072][None]),
                npostb=f(np.tile(inp['norm_post'][layer][None], (128,1))))
def l3_inputs(inp, core, z_full, x1_full):
    b, qtr = core // 4, core % 4
    f = lambda a: np.ascontiguousarray(a, dtype=np.float32)
    sl = slice(2048*qtr, 2048*(qtr+1))
    m = dict(zT=f(z_full[b][sl].T), x1=f(x1_full[b][sl]), wout=f(inp['od_w_out'][0]))
    m.update(gate_inputs(inp, b, 1))
    return m


def l2_inputs(inp, core, y_full):
    b, p = core // 4, core % 4
    f = lambda a: np.ascontiguousarray(a, dtype=np.float32)
    m = dict(xin=f(np.concatenate([inp['ctx'][b], inp['x'][b]], axis=0)), wout=f(inp['ev_w_out'][0]))
    if y_full is not None:
        m['yT'] = f(y_full[b].T)
    m.update(gate_inputs(inp, b, 0))
    m['adaw1'] = f(inp['ada_w'][1][:, :2048]); m['adab1'] = f(inp['ada_b'][1][:2048].reshape(16,128).T); m['normw1'] = f(inp['norm_pre'][1].reshape(8,128).T)
    cols = np.concatenate([off + 256*p + np.arange(256) for off in (0, 1024, 2048, 3072)])
    m['win1'] = f(inp['od_w_in'][0][:, cols])
    pos = np.arange(8192); row = pos//64; col = pos%64
    inv = 10000.0 ** (-np.arange(0, 64, 2, dtype=np.float32)/64)
    ang = np.concatenate([row[:,None]*inv[None], col[:,None]*inv[None]], axis=1).astype(np.float32)
    m['ropec'] = f(np.cos(ang)); m['ropes'] = f(np.sin(ang))
    m['ident'] = np.eye(128, dtype=np.float32)
    lg = inp['od_decay_logit'][0]
    logit = np.zeros((128,4), np.float32)
    for hl in range(2):
        for dr in range(2): logit[:, 2*hl+dr] = lg[dr][2*p+hl]
    j = np.arange(128)[:,None]; i = np.arange(128)[None,:]
    diffT = np.stack([(i-j)*np.ones((128,128)), (j-i)*np.ones((128,128))], axis=1)
    mask = np.stack([(i>=j), (j>i)], axis=1)
    posxi = np.stack([np.tile((np.arange(128)+1)[None], (128,1)), np.tile((128-np.arange(128))[None], (128,1))], axis=1)
    pze = np.stack([127-np.arange(128), np.arange(128)], axis=1)
    m.update(r_logit=logit, r_diffT=f(diffT), r_mask01T=f(mask), r_posxi=f(posxi), r_poszeta=f(pze))
    return m


def fused_inputs(inp, core):
    b, q = core // 4, core % 4
    f = lambda a: np.ascontiguousarray(a, dtype=np.float32)
    m, _, _ = l1_inputs(inp, core)
    m.update(rw_params(inp, core))
    m.update(fn_params())
    m2 = l2_inputs(inp, core, None)
    g0 = gate_inputs(inp, b, 0)
    g1 = gate_inputs(inp, b, 1)
    perm = np.concatenate([np.concatenate([192 * r + np.arange(192), 768 + 64 * r + np.arange(64)]) for r in range(4)])
    m['wout0p'] = f(inp['ev_w_out'][0][perm])
    m['adawg0'], m['adabr0'], m['npostb0'] = g0['adawg'], g0['adabr'], g0['npostb']
    m['adawg1'], m['adabr1'], m['npostb1'] = g1['adawg'], g1['adabr'], g1['npostb']
    m['wout1'] = f(inp['od_w_out'][0])
    for k_ in ('adaw1', 'adab1', 'normw1', 'win1', 'ropec', 'ropes', 'r_logit', 'r_diffT', 'r_mask01T', 'r_posxi', 'r_poszeta'):
        m[k_] = m2[k_]
    return m


def kernel(**inputs):
    inp = {k: np.asarray(v) for k, v in inputs.items()}
    cores = list(range(8))
    nc, kb = build_fused()
    kb.emit()
    maps = [fused_inputs(inp, core) for core in cores]
    res = run_bass_kernel_spmd(nc, maps, core_ids=cores).results
    return np.stack([res[0]['out'], res[4]['out']]).astype(np.float32)
```

```python
import numpy as np
import concourse.bass as bass
import concourse.mybir as mybir
from contextlib import ExitStack
from concourse.bass_utils import run_bass_kernel_spmd

F32 = mybir.dt.float32
BF16 = mybir.dt.bfloat16
ALU = mybir.AluOpType
AF = mybir.ActivationFunctionType

ENGS = ['pe', 'act', 'dve', 'pool', 'sp']
DBG = {}
NDSEM = 8


class Buf:
    __slots__ = ('name', 'w', 'r', 'excl', 'strict')

    def __init__(self, name='', excl=False, strict=False):
        self.name = name
        self.w = None
        self.r = {}
        self.excl = excl
        self.strict = strict


class Op:
    __slots__ = ('eng', 'fn', 'deps', 'signal', 'sig', 'sem', 'target', 'isdma', 'final', 'cc')


class _Rec:
    def __init__(self):
        self.call = None

    def __getattr__(self, name):
        def f(*a, **k):
            self.call = (name, a, k)
            return self
        return f


def _flat(xs):
    out = []
    for x in xs:
        if isinstance(x, (list, tuple)):
            out.extend(_flat(x))
        else:
            out.append(x)
    return out


class KB:
    def __init__(self, nc):
        self.nc = nc
        self.ops = {e: [] for e in ENGS}
        self.phase = Buf('phase')
        self.es = ExitStack()
        self.nops = 0

    def sb(self, name, shape, dtype, es=None):
        self.nalloc = getattr(self, 'nalloc', 0) + 1
        return (es or self.es).enter_context(self.nc.sbuf_tensor('s%d_%s' % (self.nalloc, name), list(shape), dtype))

    def ps(self, name, shape, dtype=F32, es=None):
        self.nalloc = getattr(self, 'nalloc', 0) + 1
        return (es or self.es).enter_context(self.nc.psum_tensor('p%d_%s' % (self.nalloc, name), list(shape), dtype))

    def op(self, eng, fn, reads=(), writes=(), dma=False, final=False, nophase=False, cc=False):
        dma = dma or cc
        o = Op()
        rec = _Rec()
        fn(rec)
        assert rec.call is not None
        o.eng = eng; o.fn = rec.call; o.isdma = dma; o.signal = dma; o.deps = []; o.sig = 0
        o.sem = None; o.target = 0; o.final = final; o.cc = cc
        reads = _flat(reads)
        writes = _flat(writes)
        for b in list(reads):
            if b.excl:
                reads.remove(b)
                if b not in writes:
                    writes.append(b)
        if not nophase:
            reads.append(self.phase)
        deps = {}
        sdeps = set()
        for b in reads:
            if b.w is not None:
                deps[id(b.w)] = b.w
                if b.strict:
                    sdeps.add(id(b.w))
        for b in writes:
            if b.w is not None:
                deps[id(b.w)] = b.w
                if b.strict:
                    sdeps.add(id(b.w))
            for r in b.r.values():
                deps[id(r)] = r
                if b.strict:
                    sdeps.add(id(r))
        for d in deps.values():
            if d is o:
                continue
            if d.isdma or dma or d.eng != eng or id(d) in sdeps or (eng != 'pe' and DBG.get('strict_all', True)):
                o.deps.append(d)
                d.signal = True
        key = ('dma', self.nops) if dma else eng
        for b in reads:
            b.r[key] = o
        for b in writes:
            b.w = o
            b.r = {}
        self.ops[eng].append(o)
        self.nops += 1
        return o

    def barrier(self, tile_ap):
        self.op('dve', lambda e: e.memset(tile_ap, 0.0), writes=[self.phase], nophase=True)

    def emit(self):
        nc = self.nc
        es = self.es
        csem = {}
        for e in ['pe', 'act', 'dve', 'pool']:
            csem[e] = es.enter_context(nc.semaphore('c_' + e))
        dsem = {}
        for e in ENGS:
            if any(o.isdma for o in self.ops[e]):
                dsem[e] = [es.enter_context(nc.semaphore('d_%s%d' % (e, i))) for i in range(NDSEM)]
        for e in ENGS:
            cnt = 0
            nd = 0
            hist = []
            for o in self.ops[e]:
                if o.cc:
                    self.ncc = getattr(self, 'ncc', 0) + 1
                    o.sem = es.enter_context(nc.semaphore('ccs%d' % self.ncc))
                    o.target = 1
                elif o.isdma:
                    slot = nd % NDSEM
                    o.sem = dsem[e][slot]
                    o.target = 16 * (nd // NDSEM + 1)
                    if nd >= NDSEM:
                        o.deps.append(hist[nd - NDSEM])
                    hist.append(o)
                    nd += 1
                elif o.signal:
                    cnt += 1
                    o.sig = cnt
                    o.sem = csem[e]
                    o.target = cnt
        finals = [o for e in ENGS for o in self.ops[e] if o.final]

        def run(engname):
            def body(e):
                waited = {}
                for o in self.ops[engname]:
                    need = {}
                    for d in o.deps:
                        k = id(d.sem)
                        if waited.get(k, 0) < d.target and need.get(k, (None, 0))[1] < d.target:
                            need[k] = (d.sem, d.target)
                    for k, (sem, tgt) in need.items():
                        e.wait_ge(sem, tgt)
                        waited[k] = tgt
                    nm_, a_, k_ = o.fn
                    ins = getattr(e, nm_)(*a_, **k_)
                    if o.signal:
                        ins.then_inc(o.sem, 16 if (o.isdma and not o.cc) else 1)
                if engname == 'sp':
                    for d in finals:
                        k = id(d.sem)
                        if waited.get(k, 0) < d.target:
                            e.wait_ge(d.sem, d.target)
                            waited[k] = d.target
            return body

        with nc.Block() as block:
            block.tensor(run('pe'))
            block.scalar(run('act'))
            block.vector(run('dve'))
            block.gpsimd(run('pool'))
            block.sync(run('sp'))


class Ring:
    def __init__(self, kb, name, shape, dtype, n, psum=False, es=None, strict=False):
        self.n = n
        self.i = 0
        self.slots = []
        for j in range(n):
            t = (kb.ps if psum else kb.sb)('%s%d' % (name, j), shape, dtype, es=es)
            b = Buf('%s%d' % (name, j), excl=psum, strict=strict)
            self.slots.append((t, b))
            if not psum and DBG.get('init_rings', True):
                kb.op('pool', lambda e, t=t: e.memset(t[:], 0.0), writes=[b])

    def next(self):
        s = self.slots[self.i % self.n]
        self.i += 1
        return s


D = 1024
L = 8192
LC = 256
LT = L + LC
NCH = LT // 128
EPS = 1e-6
GN_EPS = 64e-5
FM_BLOCKS = {'r01': (0, 128), 'r2': (128, 64), 'k01': (192, 128), 'k2': (320, 64),
             'v01': (384, 128), 'v2': (512, 64), 'wl': (576, 128), 'al': (704, 128), 'four': (832, 64)}
NFM = 896
TILES = [(0, 256)] + [(256 + 512 * i, 512) for i in range(16)]


def dram_in(nc, name, shape, dtype=F32):
    return nc.dram_tensor(name, list(shape), dtype, kind="ExternalInput").ap()


def dram_out(nc, name, shape, dtype=F32):
    return nc.dram_tensor(name, list(shape), dtype, kind="ExternalOutput").ap()


def dram_tmp(nc, name, shape, dtype=F32, debug=False):
    return nc.dram_tensor(name, list(shape), dtype, kind="ExternalOutput" if debug else "Internal").ap()


def load_consts(kb, items, eng='sp'):
    bufs = []
    for t, src in items:
        b = Buf()
        kb.op(eng, (lambda e, t=t, src=src: e.dma_start(out=t[:], in_=src)), writes=[b], dma=True)
        bufs.append(b)
    return bufs


def adaln_vectors(kb, es, nc, sil_in, adaw, adab, normw, ncolblk, mod_tile=None):
    sil = kb.sb('sil', [128, 8, 2], F32, es)
    silb = Buf(strict=True)
    kb.op('sp', lambda e: e.dma_start(out=sil[:], in_=sil_in), writes=[silb], dma=True)
    kb.op('act', lambda e: e.activation(out=sil[:], in_=sil[:], func=AF.Silu), reads=[silb], writes=[silb])
    adb = kb.sb('adb', [128, ncolblk], F32, es)
    adbb = Buf(strict=True)
    kb.op('sp', lambda e: e.dma_start(out=adb[:], in_=adab), writes=[adbb], dma=True)
    mod = mod_tile if mod_tile is not None else kb.sb('mod', [128, ncolblk, 2], F32, es)
    modb = Buf(strict=True)
    psm = kb.ps('psmod', [128, ncolblk, 2], F32, es)
    psb = Buf(excl=True)
    wt = kb.sb('adaw', [128, 8, ncolblk * 128], F32, es)
    wb = Buf()
    for k in range(8):
        kb.op('sp', lambda e, k=k: e.dma_start(out=wt[:, k, :], in_=adaw[k * 128:(k + 1) * 128, :]), writes=[wb], dma=True)
    for cb in range(ncolblk):
        for k in range(8):
            kb.op('pe', lambda e, k=k, cb=cb: e.matmul(psm[:, cb, :], lhsT=wt[:, k, cb * 128:(cb + 1) * 128],
                                                      rhs=sil[:, k, :], start=(k == 0), stop=(k == 7)),
                  reads=[wb, silb], writes=[psb])
    for n in range(2):
        kb.op('dve', lambda e, n=n: e.tensor_tensor(out=mod[:, :, n], in0=psm[:, :, n], in1=adb[:, :], op=ALU.add),
              reads=[psb, adbb], writes=[modb])
    return mod, modb


def phase_proj(kb, nc, es, xin, Wsrc_list, G, shiftv, Gb, ident_bf, identb, emit_tile, nmod=2, get_x=None, shift_off=0):
    xring = Ring(kb, 'xt', [128, D], F32, 3, es=es)
    xnring = Ring(kb, 'xn', [128, D], BF16, 2, es=es)
    stat = Ring(kb, 'stat', [128, 4], F32, 4, es=es, strict=True)
    junk = kb.sb('junk', [128, D], BF16, es)
    junkb = Buf()
    hring = Ring(kb, 'hT', [128, 8, 512], BF16, 2, es=es)
    ptr = Ring(kb, 'ptr', [128, 8, 128], BF16, 2, psum=True, es=es)
    for (t0, T) in TILES:
        n = 1 if t0 < LC else 0
        hT, hb = hring.next()
        for sub in range(T // 128):
            r0 = t0 + sub * 128
            if get_x is not None:
                xt, xb = get_x(r0)
            else:
                xt, xb = xring.next()
                kb.op('sp', lambda e, xt=xt, r0=r0: e.dma_start(out=xt[:], in_=xin[r0:r0 + 128, :]), writes=[xb], dma=True)
            st, sb_ = stat.next()
            kb.op('act', lambda e, xt=xt, st=st: e.activation(out=junk[:], in_=xt[:], func=AF.Square, accum_out=st[:, 0:1]),
                  reads=[xb], writes=[junkb, sb_])
            kb.op('act', lambda e, st=st: e.activation(out=st[:, 1:2], in_=st[:, 0:1], func=AF.Sqrt, scale=1.0 / D, bias=Gb['eps'][:, 0:1]),
                  reads=[sb_, Gb['epsb']], writes=[sb_])
            kb.op('dve', lambda e, st=st: e.reciprocal(out=st[:, 2:3], in_=st[:, 1:2]), reads=[sb_], writes=[sb_])
            xn, xnb = xnring.next()
            kb.op('dve', lambda e, xn=xn, xt=xt, st=st: e.tensor_scalar(out=xn[:], in0=xt[:], scalar1=st[:, 2:3], scalar2=None, op0=ALU.mult),
                  reads=[xb, sb_], writes=[xnb])
            pt, pb = ptr.next()
            for k in range(8):
                kb.op('pe', lambda e, pt=pt, xn=xn, k=k: e.transpose(out=pt[:, k, :], in_=xn[:, k * 128:(k + 1) * 128], identity=ident_bf[:]),
                      reads=[xnb, identb], writes=[pb])
            for k in range(8):
                if k % 2 == 0:
                    kb.op('act', lambda e, pt=pt, hT=hT, k=k, sub=sub, n=n: e.activation(
                        out=hT[:, k, sub * 128:(sub + 1) * 128], in_=pt[:, k, :], func=AF.Identity,
                        scale=G[:, k, n:n + 1], bias=shiftv[:, shift_off + k, n:n + 1]), reads=[pb, Gb['G']], writes=[hb])
                else:
                    kb.op('dve', lambda e, pt=pt, hT=hT, k=k, sub=sub, n=n: e.tensor_scalar(
                        out=hT[:, k, sub * 128:(sub + 1) * 128], in0=pt[:, k, :], scalar1=G[:, k, n:n + 1],
                        scalar2=shiftv[:, shift_off + k, n:n + 1], op0=ALU.mult, op1=ALU.add), reads=[pb, Gb['G']], writes=[hb])
        emit_tile(t0, T, hT, hb)


def load_weight_bf16(kb, es, nc, name, src, ncols, es_keep):
    wbf = kb.sb(name, [128, 8, ncols], BF16, es_keep)
    wb = Buf()
    stg = Ring(kb, name + '_stg', [128, ncols], F32, 2, es=es)
    for k in range(8):
        s, sbuf_ = stg.next()
        kb.op('sp', lambda e, s=s, k=k: e.dma_start(out=s[:], in_=src[k * 128:(k + 1) * 128, :]), writes=[sbuf_], dma=True)
        eng = 'dve' if k % 2 == 0 else 'pool'
        kb.op(eng, lambda e, s=s, k=k: e.tensor_copy(out=wbf[:, k, :], in_=s[:]), reads=[sbuf_], writes=[wb])
    return wbf, wb


RW_PARAMS = {'mu': [128, 8], 'lanem': [128, 6], 'k_k': [128, 2], 'k_a': [128, 2], 'r_k': [128, 2], 'a0': [128, 2, 2],
             'w2': [128, 192], 'a2': [128, 192], 'w0': [1, 2, 192], 'maskSI': [128, 2, 256], 'maskSI2': [128, 2, 256], 'maskST': [128, 2, 3, 128],
             'tri': [128, 2, 256], 'ehead': [128, 2, 4], 'ones_bd': [128, 128], 'lnx_g': [128, 192], 'lnx_b': [128, 192]}


def part1(kb, nc, IN, YD, C):
    es = kb.es
    debug = False
    stop_after = None
    xin, sil_in, adaw, adab, normw, wfm, wg, cs64 = (IN[k_] for k_ in ('xin', 'sil_in', 'adaw', 'adab', 'normw', 'wfm', 'wg', 'cs64'))
    U = {nm: dram_tmp(nc, 'U_' + nm, [sz, LT], F32, debug) for nm, (off, sz) in FM_BLOCKS.items() if nm != 'four'}
    GT = dram_tmp(nc, 'GT', [LT, 256], F32, debug)
    HT = dram_tmp(nc, 'HT', [LT, 128], F32, debug)
    bar, ident_f, identfb, ident_bf, identb, epsT, epsb = C['bar'], C['ident_f'], C['identfb'], C['ident_bf'], C['identb'], C['epsT'], C['epsb']
    with ExitStack() as pa:
        mod, modb = adaln_vectors(kb, pa, nc, sil_in, adaw, adab, normw, 16)
        nw = kb.sb('nw', [128, 8], F32, pa)
        nwb, = load_consts(kb, [(nw, normw)])
        G = kb.sb('G', [128, 8, 2], F32, pa)
        Gbuf = Buf(strict=True)
        for n in range(2):
            kb.op('dve', lambda e, n=n: e.scalar_tensor_tensor(out=G[:, :, n], in0=mod[:, 8:16, n], scalar=1.0, in1=nw[:, :],
                                                                op0=ALU.add, op1=ALU.mult), reads=[modb, nwb], writes=[Gbuf])
        Gb = {'G': Gbuf, 'eps': epsT, 'epsb': epsb}
        shiftv = mod
        wbf, wbb = load_weight_bf16(kb, pa, nc, 'wfm_bf', wfm, NFM, pa)
        wgb, wgbb = load_weight_bf16(kb, pa, nc, 'wg_bf', wg, 256, pa)
        cs = kb.sb('cs64', [64, 128], F32, pa)
        csb, = load_consts(kb, [(cs, cs64)])
        pmm = Ring(kb, 'pmm', [128, 512], F32, 3, psum=True, es=pa)
        pg = Ring(kb, 'pg', [128, 512], F32, 1, psum=True, es=pa)
        ustg = Ring(kb, 'ustg', [128, 512], F32, 4, es=pa)
        gstg = Ring(kb, 'gstg', [128, 256], F32, 3, es=pa)
        hstg = Ring(kb, 'hstg', [128, 128], F32, 3, es=pa)
        fstg = Ring(kb, 'fstg', [64, 512], F32, 2, es=pa)
        cnt = [0]

        def emit_tile(t0, T, hT, hb):
            for nm, (off, sz) in FM_BLOCKS.items():
                ps, psb = pmm.next()
                for k in range(8):
                    kb.op('pe', lambda e, ps=ps, k=k, off=off, sz=sz: e.matmul(ps[0:sz, 0:T], lhsT=wbf[:, k, off:off + sz], rhs=hT[:, k, 0:T],
                                                                               start=(k == 0), stop=(k == 7)), reads=[wbb, hb], writes=[psb])
                if nm == 'four':
                    fs, fsb = fstg.next()
                    kb.op('act', lambda e, fs=fs, ps=ps: e.activation(out=fs[:, 0:T], in_=ps[0:64, 0:T], func=AF.Copy), reads=[psb], writes=[fsb])
                    for sub in range(T // 128):
                        pgt, pgb = pg.next()
                        kb.op('pe', lambda e, pgt=pgt, fs=fs, sub=sub: e.matmul(pgt[:, 0:128], lhsT=fs[:, sub * 128:(sub + 1) * 128], rhs=cs[:, :],
                                                                                  start=True, stop=True), reads=[fsb, csb], writes=[pgb])
                        hs, hsb = hstg.next()
                        kb.op('dve', lambda e, hs=hs, pgt=pgt: e.tensor_copy(out=hs[:], in_=pgt[:, 0:128]), reads=[pgb], writes=[hsb])
                        r0 = t0 + sub * 128
                        kb.op('pool', lambda e, hs=hs, r0=r0: e.dma_start(out=HT[r0:r0 + 128, :], in_=hs[:]), reads=[hsb], dma=True)
                else:
                    us, usb = ustg.next()
                    cnt[0] += 1
                    if cnt[0] % 2 == 0:
                        kb.op('act', lambda e, us=us, ps=ps, sz=sz: e.activation(out=us[0:sz, 0:T], in_=ps[0:sz, 0:T], func=AF.Copy), reads=[psb], writes=[usb])
                    else:
                        kb.op('dve', lambda e, us=us, ps=ps, sz=sz: e.tensor_copy(out=us[0:sz, 0:T], in_=ps[0:sz, 0:T]), reads=[psb], writes=[usb])
                    kb.op('pool', lambda e, us=us, nm=nm, sz=sz: e.dma_start(out=U[nm][:, t0:t0 + T], in_=us[0:sz, 0:T]), reads=[usb], dma=True)
            for sub in range(T // 128):
                pgt, pgb = pg.next()
                for k in range(8):
                    kb.op('pe', lambda e, pgt=pgt, k=k, sub=sub: e.matmul(pgt[:, 0:256], lhsT=hT[:, k, sub * 128:(sub + 1) * 128], rhs=wgb[:, k, :],
                                                                           start=(k == 0), stop=(k == 7)), reads=[wgbb, hb], writes=[pgb])
                gs, gsb = gstg.next()
                kb.op('act', lambda e, gs=gs, pgt=pgt: e.activation(out=gs[:], in_=pgt[:, 0:256], func=AF.Silu), reads=[pgb], writes=[gsb])
                r0 = t0 + sub * 128
                kb.op('pool', lambda e, gs=gs, r0=r0: e.dma_start(out=GT[r0:r0 + 128, :], in_=gs[:]), reads=[gsb], dma=True)

        phase_proj(kb, nc, pa, xin, None, G, shiftv, Gb, ident_bf, identb, emit_tile)
        kb.barrier(bar[:])
    BLm = ['r01', 'r2', 'k01', 'k2', 'v01', 'v2', 'wl', 'al']
    S = {nm: dram_tmp(nc, 'S_' + nm, [FM_BLOCKS[nm][1], LT], F32, debug) for nm in BLm}
    with ExitStack() as pm:
        mu_t = kb.sb('mu_m', [128, 8], F32, pm)
        lan_t = kb.sb('lan_m', [128, 6], F32, pm)
        mub_, lanb_ = load_consts(kb, [(mu_t, IN['p_mu']), (lan_t, IN['p_lanem'])])
        cf = kb.sb('coef_m', [128, 8, 7], F32, pm)
        cfb = Buf(strict=True)
        kb.op('dve', lambda e: e.tensor_scalar(out=cf[:, :, 0], in0=mu_t[:, :], scalar1=-1.0, scalar2=1.0, op0=ALU.mult, op1=ALU.add), reads=[mub_], writes=[cfb])
        for l in range(6):
            kb.op('dve', lambda e, l=l: e.tensor_scalar(out=cf[:, :, 1 + l], in0=mu_t[:, :], scalar1=lan_t[:, l:l + 1], scalar2=None, op0=ALU.mult), reads=[mub_, lanb_], writes=[cfb])
        wr = Ring(kb, 'winm', [128, 640], F32, 4, es=pm)
        so = Ring(kb, 'smo', [128, 512], F32, 4, es=pm)
        for (t0, T) in TILES:
            isctx = t0 < LC
            seg_lo, seg_hi = (0, LC) if isctx else (LC, LT)
            halo = 1 if isctx else 64
            lo = max(t0 - halo, seg_lo)
            hi = min(t0 + T + halo, seg_hi)
            for bi, nm in enumerate(BLm):
                sz = FM_BLOCKS[nm][1]
                w_, wb_ = wr.next()
                if lo > t0 - halo or hi < t0 + T + halo:
                    kb.op('pool', lambda e, w_=w_: e.memset(w_[:], 0.0), writes=[wb_])
                kb.op('sp', lambda e, w_=w_, nm=nm, sz=sz, lo=lo, hi=hi, t0=t0: e.dma_start(out=w_[0:sz, 64 + lo - t0:64 + hi - t0], in_=U[nm][:, lo:hi]), writes=[wb_], dma=True)
                o_, ob_ = so.next()
                kb.op('dve', lambda e, o_=o_, w_=w_, sz=sz, bi=bi, T=T: e.tensor_scalar(out=o_[0:sz, 0:T], in0=w_[0:sz, 64:64 + T], scalar1=cf[0:sz, bi, 0:1], scalar2=None, op0=ALU.mult),
                      reads=[wb_, cfb], writes=[ob_])
                terms = [(5, -1, None), (6, +1, None)] if isctx else [(1, -1, 'L'), (2, +1, 'R'), (3, -64, None), (4, +64, None)]
                for (ci, off, kind) in terms:
                    def f(e, o_=o_, w_=w_, sz=sz, bi=bi, ci=ci, off=off, kind=kind, T=T):
                        if kind is None:
                            oo = o_[0:sz, 0:T]
                            ii = w_[0:sz, 64 + off:64 + T + off]
                        else:
                            ov = o_[0:sz, 0:T].rearrange("p (r c) -> p r c", c=64)
                            iv = w_[0:sz, 64:64 + T].rearrange("p (r c) -> p r c", c=64)
                            if kind == 'L':
                                oo, ii = ov[:, :, 1:64], iv[:, :, 0:63]
                            else:
                                oo, ii = ov[:, :, 0:63], iv[:, :, 1:64]
                        return e.scalar_tensor_tensor(out=oo, in0=ii, scalar=cf[0:sz, bi, ci:ci + 1], in1=oo, op0=ALU.mult, op1=ALU.add)
                    kb.op('dve', f, reads=[wb_, cfb, ob_], writes=[ob_])
                kb.op('pool', lambda e, o_=o_, nm=nm, sz=sz, t0=t0, T=T: e.dma_start(out=S[nm][:, t0:t0 + T], in_=o_[0:sz, 0:T]), reads=[ob_], dma=True)
        kb.barrier(bar[:])
    if stop_after == 'A':
        return
    with ExitStack() as pb:
        P = {k_: IN['p_' + k_] for k_ in RW_PARAMS}
        YR = YD
        if stop_after != 'skipB':
            phase_rwkv(kb, nc, pb, S, GT, YR, P, ident_f, identfb, ident_bf, identb)
        kb.barrier(bar[:])
    if DBG.get('skip_fnet'):
        return
    PF = {k_: IN['f_' + k_] for k_ in FN_PARAMS}
    YF = YD
    ZS = dram_tmp(nc, 'ZS', [2, 128, 64, 128], F32, False)
    phase_fnet(kb, nc, HT, GT, YF, ZS, PF, bar)
    return


DUMPS = {}


def dump(kb, nc, name, ap, shape, bufs, dt=F32):
    if name in DUMPS:
        return
    t = nc.dram_tensor('dbg_' + name, list(shape), dt, kind="ExternalOutput").ap()
    DUMPS[name] = t
    kb.op('sp', lambda e: e.dma_start(out=t, in_=ap), reads=bufs, dma=True, final=True)

CHUNK_ORDER = {0: list(range(NCH)), 1: [1, 0] + list(range(NCH - 1, 1, -1))}


def phase_rwkv(kb, nc, es, U, GT, YR, P, ident_f, identfb, ident_bf, identb):
    def sbt(name, shape, dt=F32):
        return kb.sb(name, shape, dt, es)

    def const(name, shape, src, dt=F32):
        t = sbt(name, shape, dt)
        b, = load_consts(kb, [(t, src)])
        return t, b

    mu, mub = const('mu', [128, 8], P['mu'])
    lanem, lanemb = const('lanem', [128, 6], P['lanem'])
    kkp, kkpb = const('kkp', [128, 2], P['k_k'])
    kap, kapb = const('kap', [128, 2], P['k_a'])
    rkp, rkpb = const('rkp', [128, 2], P['r_k'])
    a0p, a0pb = const('a0p', [128, 2, 2], P['a0'])
    w2s, w2sb = const('w2s', [128, 192], P['w2'])
    a2s, a2sb = const('a2s', [128, 192], P['a2'])
    w0r, w0rb = const('w0r', [1, 2, 192], P['w0'])
    msi, msib = const('msi', [128, 2, 256], P['maskSI'])
    mst, mstb = const('mst', [128, 2, 3, 128], P['maskST'])
    msi2, msi2b = const('msi2', [128, 2, 256], P['maskSI2'])
    tri, trib = const('tri', [128, 2, 256], P['tri'])
    ehd, ehdb = const('ehd', [128, 2, 4], P['ehead'])
    obd, obdb = const('obd', [128, 128], P['ones_bd'])
    lng, lngb = const('lng', [128, 192], P['lnx_g'])
    lnb, lnbb = const('lnb', [128, 192], P['lnx_b'])
    ones1 = sbt('ones1', [1, 128])
    ones1b = Buf()
    kb.op('dve', lambda e: e.memset(ones1[:], 1.0), writes=[ones1b])
    coef = sbt('coef', [128, 8, 7])
    coefb = Buf(strict=True)
    kb.op('dve', lambda e: e.tensor_scalar(out=coef[:, :, 0], in0=mu[:, :], scalar1=-1.0, scalar2=1.0, op0=ALU.mult, op1=ALU.add),
          reads=[mub], writes=[coefb])
    for l in range(6):
        kb.op('dve', lambda e, l=l: e.tensor_scalar(out=coef[:, :, 1 + l], in0=mu[:, :], scalar1=lanem[:, l:l + 1], scalar2=None, op0=ALU.mult),
              reads=[mub, lanemb], writes=[coefb])
    omka = sbt('omka', [128, 2])
    omkab = Buf(strict=True)
    kb.op('dve', lambda e: e.tensor_scalar(out=omka[:], in0=kap[:], scalar1=-1.0, scalar2=1.0, op0=ALU.mult, op1=ALU.add), reads=[kapb], writes=[omkab])
    gneps = sbt('gneps', [128, 1])
    gnepsb = Buf()
    kb.op('dve', lambda e: e.memset(gneps[:], GN_EPS), writes=[gnepsb])

    yacc = sbt('yacc', [128, NCH, 192])
    yaccb = [Buf() for _ in range(NCH)]
    kb.op('pool', lambda e: e.memset(yacc[:], 0.0), writes=yaccb)
    ST = sbt('ST', [128, 2, 2, 64])
    STbf = sbt('STbf', [128, 2, 2, 64], BF16)
    STb = [[Buf() for _ in range(3)] for _ in range(2)]

    CR, HR = {}, {}
    for d_ in range(2):
        n_ = 'd%d' % d_
        CR[d_] = dict(
            winr=None, smr=Ring(kb, 'smix' + n_, [128, 8, 128], F32, 2, es=es),
            t32=Ring(kb, 't32' + n_, [128, 256], F32, 9, es=es), kkr=Ring(kb, 'kkt' + n_, [128, 2, 128], F32, 1, es=es),
            vtr=Ring(kb, 'vt32' + n_, [128, 192], F32, 2, es=es), vtbr=Ring(kb, 'vtbf' + n_, [128, 192], BF16, 2, es=es),
            sgr=Ring(kb, 'sg' + n_, [128, 192], F32, 1, es=es), e1r=Ring(kb, 'e1' + n_, [128, 2, 256], F32, 2, es=es),
            e2r=Ring(kb, 'e2' + n_, [128, 2, 128], F32, 1, es=es), arr=Ring(kb, 'ar' + n_, [128, 2, 256], BF16, 2, es=es),
            btr=Ring(kb, 'bt' + n_, [128, 2, 128], BF16, 2, es=es), ktr=Ring(kb, 'kt' + n_, [128, 2, 128], BF16, 2, es=es),
            bktr=Ring(kb, 'bkt' + n_, [128, 2, 192], BF16, 2, es=es), bcr=Ring(kb, 'bc' + n_, [128, 4], F32, 2, es=es, strict=True))
        for h_ in range(3):
            n2 = 'd%dh%d' % (d_, h_)
            HR[(d_, h_)] = dict(
                mm1r=Ring(kb, 'mm1' + n2, [128, 256], BF16, 2, es=es), mm2r=Ring(kb, 'mm2' + n2, [128, 256], BF16, 2, es=es),
                xpr=Ring(kb, 'xp' + n2, [128, 256], BF16, 4, es=es), xtr=Ring(kb, 'xt' + n2, [128, 128], BF16, 4, es=es),
                ivr=Ring(kb, 'iv' + n2, [128, 128], BF16, 8, es=es), tmr=Ring(kb, 'tm' + n2, [128, 128], BF16, 2, es=es),
                w1r=Ring(kb, 'w1' + n2, [128, 64], BF16, 2, es=es), utr=Ring(kb, 'ut' + n2, [128, 64], BF16, 2, es=es),
                tsr=Ring(kb, 'ts' + n2, [128, 64], F32, 3, es=es))
    jkr = Ring(kb, 'jk', [128, 64], F32, 2, es=es)
    gtr = Ring(kb, 'gt', [128, 192], F32, 2, es=es)
    fnr = Ring(kb, 'fn', [128, 192], F32, 2, es=es)
    str_ = Ring(kb, 'stt', [128, 8], F32, 6, es=es, strict=True)
    pprep = Ring(kb, 'pprep', [128, 512], F32, 2, psum=True, es=es)
    ptrb = kb.ps('ptrb', [128, 1024], BF16, es)
    ptrbufs = [Buf(excl=True)] * 4
    ptrc = [0]
    pgram = Ring(kb, 'pgram', [128, 512], F32, 1, psum=True, es=es)
    pinv = Ring(kb, 'pinv', [128, 512], F32, 2, psum=True, es=es)
    pseq = kb.ps('pseq', [128, 512], F32, es)
    pseq2 = kb.ps('pseq2', [128, 512], F32, es)
    pseqb = [Buf(excl=True)] * 4 + [Buf(excl=True)] * 4
    pseqt = [pseq] * 4 + [pseq2] * 4
    pseqc = [0]

    def pseq_next():
        i = pseqc[0] % 8
        pseqc[0] += 1
        return pseqt[i][:, (i % 4) * 64:(i % 4 + 1) * 64], pseqb[i], i

    def ptr_next():
        i = ptrc[0] % 4
        ptrc[0] += 1
        return i, ptrbufs[i]

    blkname = [('r01', 'k01', 'v01'), ('r2', 'k2', 'v2')]
    BL = ['r01', 'r2', 'k01', 'k2', 'v01', 'v2', 'wl', 'al']
    BI = {n: i for i, n in enumerate(BL)}
    BSZ = {n: FM_BLOCKS[n][1] for n in BL}
    alt = [0]

    def ew(fn, reads, writes):
        alt[0] += 1
        r_ = _Rec()
        fn(r_)
        stt_ = r_.call[0] == 'scalar_tensor_tensor'
        return kb.op('dve' if (alt[0] % 3 or stt_) else 'pool', fn, reads=reads, writes=writes)

    done = {}
    inflight = {}
    SMB = {}

    def chunk_body(d, c):
        winr, smr, t32, kkr, vtr, vtbr, sgr, e1r, e2r, arr, btr, ktr, bktr, bcr = (CR[d][k_] for k_ in (
            'winr', 'smr', 't32', 'kkr', 'vtr', 'vtbr', 'sgr', 'e1r', 'e2r', 'arr', 'btr', 'ktr', 'bktr', 'bcr'))
        isctx = c < 2
        t0 = c * 128
        seg_lo, seg_hi = (0, LC) if isctx else (LC, LT)
        halo = 1 if isctx else 64
        lo = max(t0 - halo, seg_lo)
        hi = min(t0 + 128 + halo, seg_hi)
        sm, smb0 = smr.next()
        smb = SMB.setdefault(id(smb0), [smb0] + [Buf() for _ in range(7)])
        for nm in BL:
            sz = BSZ[nm]
            kb.op('sp', lambda e, sm=sm, nm=nm, sz=sz, t0=t0: e.dma_start(out=sm[0:sz, BI[nm], :], in_=U[nm][:, t0:t0 + 128]), writes=[smb[BI[nm]]], dma=True)
        if DBG.get('stage', 99) < 2:
            return

        def S(nm):
            return sm[0:BSZ[nm], BI[nm], :]

        yield
        kkt, kktb = kkr.next()
        for bj, (rn, kn, vn) in enumerate(blkname if not DBG.get('skip_kk') else []):
            sz = BSZ[kn]
            q, qb = t32.next()
            kb.op('dve', lambda e, q=q, kn=kn, sz=sz, bj=bj: e.tensor_scalar(out=q[0:sz, 0:128], in0=S(kn), scalar1=kkp[0:sz, bj:bj + 1], scalar2=None, op0=ALU.mult),
                  reads=[smb, kkpb], writes=[qb])
            kb.op('pool', lambda e, q=q, sz=sz: e.tensor_tensor(out=q[0:sz, 128:256], in0=q[0:sz, 0:128], in1=q[0:sz, 0:128], op=ALU.mult), reads=[qb], writes=[qb])
            pp, ppb = pprep.next()
            kb.op('pe', lambda e, pp=pp, q=q, sz=sz: e.matmul(pp[0:sz, 0:128], lhsT=obd[0:sz, 0:sz], rhs=q[0:sz, 128:256], start=True, stop=True),
                  reads=[qb, obdb], writes=[ppb])
            nr, nrb = t32.next()
            kb.op('act', lambda e, nr=nr, pp=pp, sz=sz: e.activation(out=nr[0:sz, 0:128], in_=pp[0:sz, 0:128], func=AF.Sqrt), reads=[ppb], writes=[nrb])
            kb.op('dve', lambda e, nr=nr, sz=sz: e.tensor_scalar(out=nr[0:sz, 0:128], in0=nr[0:sz, 0:128], scalar1=1e-12, scalar2=None, op0=ALU.max), reads=[nrb], writes=[nrb])
            kb.op('dve', lambda e, nr=nr, sz=sz: e.reciprocal(out=nr[0:sz, 128:256], in_=nr[0:sz, 0:128]), reads=[nrb], writes=[nrb])
            kb.op('dve', lambda e, nr=nr, q=q, sz=sz, bj=bj, kkt=kkt: e.tensor_tensor(out=kkt[0:sz, bj, :], in0=q[0:sz, 0:128], in1=nr[0:sz, 128:256], op=ALU.mult),
                  reads=[nrb, qb], writes=[kktb])
        vt, vtb = vtr.next()
        vtbf, vtbfb = vtbr.next()
        pp, ppb = pprep.next()
        for bj, (rn, kn, vn) in enumerate(blkname if not DBG.get('skip_vt') else []):
            sz = BSZ[vn]
            kb.op('pe', lambda e, pp=pp, vn=vn, sz=sz, bj=bj: e.transpose(out=pp[:, bj * 128:bj * 128 + sz], in_=S(vn), identity=ident_f[0:sz, 0:sz]),
                  reads=[smb, identfb], writes=[ppb])
        kb.op('act', lambda e, pp=pp, vt=vt: e.activation(out=vt[:, :], in_=pp[:, 0:192], func=AF.Copy), reads=[ppb], writes=[vtb])
        kb.op('pool', lambda e, vt=vt, vtbf=vtbf: e.tensor_copy(out=vtbf[:, :], in_=vt[:, :]), reads=[vtb], writes=[vtbfb])

        if DBG.get('stage', 99) < 3:
            return
        yield
        th, thb = t32.next()
        kb.op('act', lambda e, th=th: e.activation(out=th[d * 64:(d + 1) * 64, 0:128], in_=sm[d * 64:(d + 1) * 64, BI['wl'], :], func=AF.Tanh),
              reads=[smb], writes=[thb])
        pp, ppb = pprep.next()
        kb.op('pe', lambda e, pp=pp, th=th: e.matmul(pp[:, 0:192], lhsT=th[d * 64:(d + 1) * 64, 0:128], rhs=w2s[d * 64:(d + 1) * 64, :], start=True, stop=False),
              reads=[thb, w2sb], writes=[ppb])
        kb.op('pe', lambda e, pp=pp: e.matmul(pp[:, 0:192], lhsT=ones1[0:1, :], rhs=w0r[0:1, d, :], start=False, stop=True),
              reads=[ones1b, w0rb], writes=[ppb])
        sg, sgb = sgr.next()
        kb.op('act', lambda e, sg=sg, pp=pp: e.activation(out=sg[:, :], in_=pp[:, 0:192], func=AF.Sigmoid), reads=[ppb], writes=[sgb])
        e1, e1b = e1r.next()
        e2, e2b = e2r.next()
        for bj in range(2):
            sz = 128 if bj == 0 else 64
            pp, ppb = pprep.next()
            kb.op('pe', lambda e, pp=pp, sg=sg, bj=bj, sz=sz: e.matmul(pp[0:sz, 0:256], lhsT=sg[:, bj * 128:bj * 128 + sz], rhs=tri[:, d, :], start=True, stop=True),
                  reads=[sgb, trib], writes=[ppb])
            kb.op('act', lambda e, pp=pp, e1=e1, bj=bj, sz=sz: e.activation(out=e1[0:sz, bj, :], in_=pp[0:sz, 0:256], func=AF.Exp), reads=[ppb], writes=[e1b])
            kb.op('act', lambda e, pp=pp, e2=e2, bj=bj, sz=sz: e.activation(out=e2[0:sz, bj, :], in_=pp[0:sz, 0:128], func=AF.Exp, scale=-1.0), reads=[ppb], writes=[e2b])
        if DBG.get('stage', 99) < 4:
            return
        yield
        ar, arb = arr.next()
        bt, btb = btr.next()
        kt, ktb = ktr.next()
        bc, bcb = bcr.next()
        pbon, pbonb, _ = pseq_next()
        for bj, (rn, kn, vn) in enumerate(blkname):
            sz = BSZ[kn]
            co = bj * 128
            pp, ppb = pprep.next()
            kb.op('pe', lambda e, pp=pp, sz=sz, co=co: e.matmul(pp[0:sz, 0:128], lhsT=a2s[d * 64:(d + 1) * 64, co:co + sz], rhs=sm[d * 64:(d + 1) * 64, BI['al'], :], start=True, stop=True),
                  reads=[a2sb, smb], writes=[ppb])
            av, avb = t32.next()
            kb.op('act', lambda e, av=av, pp=pp, sz=sz, bj=bj: e.activation(out=av[0:sz, 0:128], in_=pp[0:sz, 0:128], func=AF.Sigmoid, bias=a0p[0:sz, d, bj:bj + 1]),
                  reads=[ppb, a0pb], writes=[avb])
            kv, kvb = t32.next()
            ew(lambda e, kv=kv, av=av, sz=sz, bj=bj: e.tensor_scalar(out=kv[0:sz, 0:128], in0=av[0:sz, 0:128], scalar1=kap[0:sz, bj:bj + 1], scalar2=omka[0:sz, bj:bj + 1], op0=ALU.mult, op1=ALU.add),
               [avb, kapb, omkab], [kvb])
            ew(lambda e, kv=kv, kn=kn, sz=sz: e.tensor_tensor(out=kv[0:sz, 0:128], in0=kv[0:sz, 0:128], in1=S(kn), op=ALU.mult), [kvb, smb], [kvb])
            ew(lambda e, kv=kv, kt=kt, e2=e2, sz=sz, bj=bj: e.tensor_tensor(out=kt[0:sz, bj, :], in0=kv[0:sz, 0:128], in1=e2[0:sz, bj, :], op=ALU.mult), [kvb, e2b], [ktb])
            ew(lambda e, av=av, e2=e2, sz=sz, bj=bj: e.tensor_tensor(out=av[0:sz, 128:256], in0=av[0:sz, 0:128], in1=e2[0:sz, bj, :], op=ALU.mult), [avb, e2b], [avb])
            ew(lambda e, av=av, bt=bt, kkt=kkt, sz=sz, bj=bj: e.tensor_tensor(out=bt[0:sz, bj, :], in0=av[0:sz, 128:256], in1=kkt[0:sz, bj, :], op=ALU.mult), [avb, kktb], [btb])
            ew(lambda e, ar=ar, kkt=kkt, e1=e1, sz=sz, bj=bj: e.scalar_tensor_tensor(out=ar[0:sz, bj, 0:128], in0=kkt[0:sz, bj, :], scalar=-1.0, in1=e1[0:sz, bj, 128:256], op0=ALU.mult, op1=ALU.mult),
               [kktb, e1b], [arb])
            ew(lambda e, ar=ar, rn=rn, e1=e1, sz=sz, bj=bj: e.tensor_tensor(out=ar[0:sz, bj, 128:256], in0=S(rn), in1=e1[0:sz, bj, 0:128], op=ALU.mult), [smb, e1b], [arb])
            ew(lambda e, kv=kv, rn=rn, sz=sz, bj=bj: e.scalar_tensor_tensor(out=kv[0:sz, 128:256], in0=S(rn), scalar=rkp[0:sz, bj:bj + 1], in1=kv[0:sz, 0:128], op0=ALU.mult, op1=ALU.mult),
               [smb, rkpb, kvb], [kvb])
            kb.op('pe', lambda e, kv=kv, sz=sz, bj=bj: e.matmul(pbon[:, 0:4], lhsT=kv[0:sz, 128:256], rhs=ehd[0:sz, bj, :], start=(bj == 0), stop=(bj == 1)),
                  reads=[kvb, ehdb], writes=[pbonb])
        kb.op('dve', lambda e, bc=bc: e.tensor_copy(out=bc[:, :], in_=pbon[:, 0:4]), reads=[pbonb], writes=[bcb])
        yield
        bkt, bktb = bktr.next()
        for wi, (src, srcb) in enumerate([(bt, btb), (kt, ktb)]):
            pi, pib = ptr_next()
            for bj in range(2):
                sz = 128 if bj == 0 else 64
                kb.op('pe', lambda e, src=src, pi=pi, bj=bj, sz=sz: e.transpose(out=ptrb[:, pi * 256 + bj * 128:pi * 256 + bj * 128 + sz], in_=src[0:sz, bj, :], identity=ident_bf[0:sz, 0:sz]),
                      reads=[srcb, identb], writes=[pib])
            kb.op('act' if wi == 0 else 'dve',
                  (lambda e, pi=pi, bkt=bkt, wi=wi: e.activation(out=bkt[:, wi, :], in_=ptrb[:, pi * 256:pi * 256 + 192], func=AF.Copy)) if wi == 0 else
                  (lambda e, pi=pi, bkt=bkt, wi=wi: e.tensor_copy(out=bkt[:, wi, :], in_=ptrb[:, pi * 256:pi * 256 + 192])),
                  reads=[pib], writes=[bktb])
        tcol = 127 if d == 0 else 0

        if DBG.get('stage', 99) < 5:
            return
        first = done.get(c, 0) == 0
        assert c not in inflight
        inflight[c] = d

        def head_body(h):
            mm1r, mm2r, xpr, xtr, ivr, tmr, w1r, utr, tsr = (HR[(d, h)][k_] for k_ in ('mm1r', 'mm2r', 'xpr', 'xtr', 'ivr', 'tmr', 'w1r', 'utr', 'tsr'))
            bj = h // 2
            base = (h % 2) * 64
            ch0 = 64 * h
            hp = slice(base, base + 64)
            yield
            pg1, pg1b = pgram.next()
            kb.op('pe', lambda e, pg1=pg1, hp=hp, bj=bj: e.matmul(pg1[:, 0:256], lhsT=bt[hp, bj, :], rhs=ar[hp, bj, :], start=True, stop=True), reads=[btb, arb], writes=[pg1b])
            mm1, mm1b = mm1r.next()
            kb.op('dve', lambda e, mm1=mm1, pg1=pg1: e.tensor_tensor(out=mm1[:, :], in0=pg1[:, 0:256], in1=msi[:, d, :], op=ALU.mult), reads=[pg1b, msib], writes=[mm1b])
            pg2, pg2b = pgram.next()
            kb.op('pe', lambda e, pg2=pg2, hp=hp, bj=bj: e.matmul(pg2[:, 0:256], lhsT=kt[hp, bj, :], rhs=ar[hp, bj, :], start=True, stop=True), reads=[ktb, arb], writes=[pg2b])
            kb.op('pe', lambda e, pg2=pg2, hp=hp, bj=bj: e.matmul(pg2[:, 256:384], lhsT=ar[hp, bj, 0:128], rhs=bt[hp, bj, :], start=True, stop=True), reads=[btb, arb], writes=[pg2b])
            mm2, mm2b = mm2r.next()
            kb.op('dve', lambda e, mm2=mm2, pg2=pg2: e.tensor_tensor(out=mm2[:, :], in0=pg2[:, 0:256], in1=msi2[:, d, :], op=ALU.mult), reads=[pg2b, msi2b], writes=[mm2b])
            xt, xtb = xtr.next()
            kb.op('dve', lambda e, xt=xt, pg2=pg2: e.tensor_tensor(out=xt[:, :], in0=pg2[:, 256:384], in1=mst[:, d, 0, :], op=ALU.mult), reads=[pg2b, mstb], writes=[xtb])
            e1t, e1tb = ivr.next()
            kb.op('dve', lambda e, e1t=e1t, pg2=pg2: e.tensor_tensor(out=e1t[:, :], in0=pg2[:, 256:384], in1=mst[:, d, 1, :], op=ALU.mult), reads=[pg2b, mstb], writes=[e1tb])
            e2t, e2tb = ivr.next()
            kb.op('dve', lambda e, e2t=e2t, pg2=pg2: e.tensor_tensor(out=e2t[:, :], in0=pg2[:, 256:384], in1=mst[:, d, 2, :], op=ALU.mult), reads=[pg2b, mstb], writes=[e2tb])
            if DBG.get('stage', 99) < 6:
                return
            yield
            xp, xpb = xpr.next()
            kb.op('pool', lambda e, xp=xp, mm1=mm1: e.tensor_tensor(out=xp[:, 128:256], in0=mm1[:, 0:128], in1=ident_bf[:, :], op=ALU.add), reads=[mm1b, identb], writes=[xpb])
            pv, pvb = pinv.next()
            kb.op('pe', lambda e, pv=pv, xt=xt, mm1=mm1: e.matmul(pv[:, 0:128], lhsT=xt[:, :], rhs=mm1[:, 0:128], start=True, stop=True), reads=[xtb, mm1b], writes=[pvb])
            kb.op('pe', lambda e, pv=pv, xt=xt, mm1=mm1: e.matmul(pv[:, 256:384], lhsT=mm1[:, 0:128], rhs=xt[:, :], start=True, stop=True), reads=[xtb, mm1b], writes=[pvb])
            xt2, xt2b = xtr.next()
            kb.op('act', lambda e, xp=xp, pv=pv: e.activation(out=xp[:, 0:128], in_=pv[:, 0:128], func=AF.Copy), reads=[pvb], writes=[xpb])
            kb.op('dve', lambda e, xt2=xt2, pv=pv: e.tensor_copy(out=xt2[:, :], in_=pv[:, 256:384]), reads=[pvb], writes=[xt2b])
            curxp, curxpb, curxt, curxtb = xp, xpb, xt2, xt2b
            for lev in range(1, 4):
                pv, pvb = pinv.next()
                kb.op('pe', lambda e, pv=pv, cx=curxp, ct=curxt: e.matmul(pv[:, 0:256], lhsT=ct[:, :], rhs=cx[:, 0:256], start=True, stop=True), reads=[curxpb, curxtb], writes=[pvb])
                kb.op('pe', lambda e, pv=pv, cx=curxp, ct=curxt: e.matmul(pv[:, 256:384], lhsT=cx[:, 0:128], rhs=ct[:, :], start=True, stop=True), reads=[curxpb, curxtb], writes=[pvb])
                nxp, nxpb = xpr.next()
                nxt, nxtb = xtr.next()
                kb.op('act', lambda e, nxp=nxp, pv=pv: e.activation(out=nxp[:, 0:128], in_=pv[:, 0:128], func=AF.Copy), reads=[pvb], writes=[nxpb])
                kb.op('dve', lambda e, nxp=nxp, pv=pv, cx=curxp: e.tensor_tensor(out=nxp[:, 128:256], in0=pv[:, 128:256], in1=cx[:, 128:256], op=ALU.add), reads=[pvb, curxpb], writes=[nxpb])
                kb.op('act', lambda e, nxt=nxt, pv=pv: e.activation(out=nxt[:, :], in_=pv[:, 256:384], func=AF.Copy), reads=[pvb], writes=[nxtb])
                curxp, curxpb, curxt, curxtb = nxp, nxpb, nxt, nxtb
                yield
            pv, pvb = pinv.next()
            kb.op('pe', lambda e, pv=pv, cx=curxp, ct=curxt: e.matmul(pv[:, 0:128], lhsT=ct[:, :], rhs=cx[:, 128:256], start=True, stop=True), reads=[curxpb, curxtb], writes=[pvb])
            t32m, t32mb = ivr.next()
            kb.op('dve', lambda e, t32m=t32m, pv=pv, cx=curxp: e.tensor_tensor(out=t32m[:, :], in0=pv[:, 0:128], in1=cx[:, 128:256], op=ALU.add), reads=[pvb, curxpb], writes=[t32mb])
            yield
            pi, pib = ptr_next()
            kb.op('pe', lambda e, pi=pi, t32m=t32m: e.transpose(out=ptrb[:, pi * 256:pi * 256 + 128], in_=t32m[:, :], identity=ident_bf[:, :]), reads=[t32mb, identb], writes=[pib])
            t32t, t32tb = ivr.next()
            kb.op('act', lambda e, pi=pi, t32t=t32t: e.activation(out=t32t[:, :], in_=ptrb[:, pi * 256:pi * 256 + 128], func=AF.Copy), reads=[pib], writes=[t32tb])
            yield
            pv, pvb = pinv.next()
            kb.op('pe', lambda e, pv=pv, e1t=e1t, t32m=t32m: e.matmul(pv[:, 0:128], lhsT=e1t[:, :], rhs=t32m[:, :], start=True, stop=True), reads=[e1tb, t32mb], writes=[pvb])
            z1, z1b = ivr.next()
            kb.op('act', lambda e, z1=z1, pv=pv: e.activation(out=z1[:, :], in_=pv[:, 0:128], func=AF.Copy), reads=[pvb], writes=[z1b])
            pv, pvb = pinv.next()
            kb.op('pe', lambda e, pv=pv, t32t=t32t, z1=z1: e.matmul(pv[:, 0:128], lhsT=t32t[:, :], rhs=z1[:, :], start=True, stop=True), reads=[t32tb, z1b], writes=[pvb])
            kb.op('pe', lambda e, pv=pv, t32t=t32t, z1=z1: e.matmul(pv[:, 256:384], lhsT=z1[:, :], rhs=t32t[:, :], start=True, stop=True), reads=[t32tb, z1b], writes=[pvb])
            t64, t64b = ivr.next()
            t64t, t64tb = ivr.next()
            kb.op('dve', lambda e, t64=t64, pv=pv, t32m=t32m: e.tensor_tensor(out=t64[:, :], in0=pv[:, 0:128], in1=t32m[:, :], op=ALU.add), reads=[pvb, t32mb], writes=[t64b])
            kb.op('dve', lambda e, t64t=t64t, pv=pv, t32t=t32t: e.tensor_tensor(out=t64t[:, :], in0=pv[:, 256:384], in1=t32t[:, :], op=ALU.add), reads=[pvb, t32tb], writes=[t64tb])
            yield
            pv, pvb = pinv.next()
            kb.op('pe', lambda e, pv=pv, e2t=e2t, t64=t64: e.matmul(pv[:, 0:128], lhsT=e2t[:, :], rhs=t64[:, :], start=True, stop=True), reads=[e2tb, t64b], writes=[pvb])
            z2, z2b = ivr.next()
            kb.op('act', lambda e, z2=z2, pv=pv: e.activation(out=z2[:, :], in_=pv[:, 0:128], func=AF.Copy), reads=[pvb], writes=[z2b])
            pv, pvb = pinv.next()
            kb.op('pe', lambda e, pv=pv, t64t=t64t, z2=z2: e.matmul(pv[:, 0:128], lhsT=t64t[:, :], rhs=z2[:, :], start=True, stop=True), reads=[t64tb, z2b], writes=[pvb])
            tm, tmb = tmr.next()
            kb.op('dve', lambda e, tm=tm, pv=pv, t64=t64: e.tensor_tensor(out=tm[:, :], in0=pv[:, 0:128], in1=t64[:, :], op=ALU.add), reads=[pvb, t64b], writes=[tmb])
            if DBG.get('dump') == (d, c) and h == 0:
                dump(kb, nc, 'mm1', mm1[:, :], [128, 256], [mm1b], BF16)
                dump(kb, nc, 'mm2', mm2[:, :], [128, 256], [mm2b], BF16)
                dump(kb, nc, 'tm', tm[:, :], [128, 128], [tmb], BF16)
            yield
            stb = STb[d][h]
            p1, p1b, _ = pseq_next()
            kb.op('pe', lambda e, p1=p1, hp=hp, bj=bj: e.matmul(p1, lhsT=ar[hp, bj, 0:128], rhs=STbf[hp, d, bj, :], start=True, stop=False), reads=[arb, stb], writes=[p1b])
            kb.op('pe', lambda e, p1=p1, mm2=mm2, ch0=ch0: e.matmul(p1, lhsT=mm2[:, 0:128], rhs=vtbf[:, ch0:ch0 + 64], start=False, stop=True), reads=[mm2b, vtbfb], writes=[p1b])
            w1, w1b = w1r.next()
            kb.op('act', lambda e, w1=w1, p1=p1: e.activation(out=w1[:, :], in_=p1, func=AF.Copy), reads=[p1b], writes=[w1b])
            p2, p2b, _ = pseq_next()
            kb.op('pe', lambda e, p2=p2, tm=tm, w1=w1: e.matmul(p2, lhsT=tm[:, :], rhs=w1[:, :], start=True, stop=True), reads=[tmb, w1b], writes=[p2b])
            ut, utb = utr.next()
            kb.op('act', lambda e, ut=ut, p2=p2: e.activation(out=ut[:, :], in_=p2, func=AF.Copy), reads=[p2b], writes=[utb])
            p3, p3b, _ = pseq_next()
            kb.op('pe', lambda e, p3=p3, hp=hp, bj=bj: e.matmul(p3, lhsT=ar[hp, bj, 128:256], rhs=STbf[hp, d, bj, :], start=True, stop=False), reads=[arb, stb], writes=[p3b])
            kb.op('pe', lambda e, p3=p3, mm1=mm1, ut=ut: e.matmul(p3, lhsT=mm1[:, 128:256], rhs=ut[:, :], start=False, stop=False), reads=[mm1b, utb], writes=[p3b])
            kb.op('pe', lambda e, p3=p3, mm2=mm2, ch0=ch0: e.matmul(p3, lhsT=mm2[:, 128:256], rhs=vtbf[:, ch0:ch0 + 64], start=False, stop=True), reads=[mm2b, vtbfb], writes=[p3b])
            if DBG.get('dump') == (d, c) and h == 0:
                dump(kb, nc, 'w1', w1[:, :], [128, 64], [w1b], BF16)
                dump(kb, nc, 'ut', ut[:, :], [128, 64], [utb], BF16)
            ts, tsb = tsr.next()
            kb.op('dve', lambda e, ts=ts, p3=p3, ch0=ch0, h=h: e.scalar_tensor_tensor(out=ts[:, :], in0=vt[:, ch0:ch0 + 64], scalar=bc[:, h:h + 1], in1=p3, op0=ALU.mult, op1=ALU.add),
                  reads=[vtb, bcb, p3b], writes=[tsb])
            if first:
                kb.op('pool', lambda e, ts=ts, ch0=ch0, c=c: e.tensor_copy(out=yacc[:, c, ch0:ch0 + 64], in_=ts[:, :]), reads=[tsb], writes=[yaccb[c]])
            else:
                kb.op('pool', lambda e, ts=ts, ch0=ch0, c=c: e.tensor_tensor(out=yacc[:, c, ch0:ch0 + 64], in0=yacc[:, c, ch0:ch0 + 64], in1=ts[:, :], op=ALU.add), reads=[tsb, yaccb[c]], writes=[yaccb[c]])
            yield
            p4full, p4b, i4 = pseq_next()
            p4 = pseqt[i4][hp, (i4 % 4) * 64:(i4 % 4 + 1) * 64]
            kb.op('pe', lambda e, p4=p4, bkt=bkt, ut=ut, ch0=ch0: e.matmul(p4, lhsT=bkt[:, 0, ch0:ch0 + 64], rhs=ut[:, :], start=True, stop=False), reads=[bktb, utb], writes=[p4b])
            kb.op('pe', lambda e, p4=p4, bkt=bkt, ch0=ch0: e.matmul(p4, lhsT=bkt[:, 1, ch0:ch0 + 64], rhs=vtbf[:, ch0:ch0 + 64], start=False, stop=True), reads=[bktb, vtbfb], writes=[p4b])
            tq, tqb = tsr.next()
            kb.op('dve', lambda e, tq=tq, p4=p4, hp=hp, bj=bj: e.tensor_tensor(out=tq[hp, :], in0=p4, in1=ST[hp, d, bj, :], op=ALU.add), reads=[p4b, stb], writes=[tqb])
            kb.op('dve', lambda e, tq=tq, hp=hp, bj=bj: e.tensor_scalar(out=ST[hp, d, bj, :], in0=tq[hp, :], scalar1=e1[hp, bj, tcol:tcol + 1], scalar2=None, op0=ALU.mult), reads=[tqb, e1b], writes=[stb])
            kb.op('act', lambda e, tq=tq, hp=hp, bj=bj: e.activation(out=STbf[hp, d, bj, :], in_=tq[hp, :], func=AF.Identity, scale=e1[hp, bj, tcol:tcol + 1]), reads=[tqb, e1b], writes=[stb])

        hg = [head_body(h) for h in range(DBG.get('heads', 3))]
        while hg:
            for g_ in list(hg):
                try:
                    next(g_)
                except StopIteration:
                    hg.remove(g_)
            yield
        del inflight[c]
        done[c] = done.get(c, 0) + 1
        if DBG.get('dump') == (d, c):
            dump(kb, nc, 'sm', sm[:, :, :], [128, 8, 128], [smb])
            dump(kb, nc, 'kkt', kkt[:, :, :], [128, 2, 128], [kktb])
            dump(kb, nc, 'vt', vt[:, :], [128, 192], [vtb])
            dump(kb, nc, 'sg', sg[:, :], [128, 192], [sgb])
            dump(kb, nc, 'e1', e1[:, :, :], [128, 2, 256], [e1b])
            dump(kb, nc, 'e2', e2[:, :, :], [128, 2, 128], [e2b])
            dump(kb, nc, 'ar', ar[:, :, :], [128, 2, 256], [arb], BF16)
            dump(kb, nc, 'bt', bt[:, :, :], [128, 2, 128], [btb], BF16)
            dump(kb, nc, 'kt', kt[:, :, :], [128, 2, 128], [ktb], BF16)
            dump(kb, nc, 'bkt', bkt[:, :, :], [128, 2, 192], [bktb], BF16)
            dump(kb, nc, 'bc', bc[:, :], [128, 4], [bcb])
            dump(kb, nc, 'ST', ST[:, d, :, :], [128, 2, 64], STb[d])
            dump(kb, nc, 'yacc', yacc[:, c, :], [128, 192], [yaccb[c]])
        yield
        if done[c] == 2 and DBG.get('stage', 99) >= 8:
            gt, gtb = gtr.next()
            kb.op('sp', lambda e, gt=gt, t0=t0: e.dma_start(out=gt[:, :], in_=GT[t0:t0 + 128, 0:192]), writes=[gtb], dma=True)
            fn, fnb = fnr.next()
            for h in range(3):
                ch0 = 64 * h
                stt, sttb = str_.next()
                jk, jkb = jkr.next()
                kb.op('act', lambda e, stt=stt, jk=jk, c=c, ch0=ch0: e.activation(out=jk[:, :], in_=yacc[:, c, ch0:ch0 + 64], func=AF.Copy, accum_out=stt[:, 0:1]), reads=[yaccb[c]], writes=[sttb, jkb])
                kb.op('act', lambda e, stt=stt, jk=jk, c=c, ch0=ch0: e.activation(out=jk[:, :], in_=yacc[:, c, ch0:ch0 + 64], func=AF.Square, accum_out=stt[:, 1:2]), reads=[yaccb[c]], writes=[sttb, jkb])
                kb.op('dve', lambda e, stt=stt: e.tensor_scalar(out=stt[:, 6:7], in0=stt[:, 0:1], scalar1=1.0 / 64, scalar2=None, op0=ALU.mult), reads=[sttb], writes=[sttb])
                kb.op('dve', lambda e, stt=stt: e.tensor_tensor(out=stt[:, 3:4], in0=stt[:, 6:7], in1=stt[:, 6:7], op=ALU.mult), reads=[sttb], writes=[sttb])
                kb.op('dve', lambda e, stt=stt: e.scalar_tensor_tensor(out=stt[:, 7:8], in0=stt[:, 1:2], scalar=1.0 / 64, in1=stt[:, 3:4], op0=ALU.mult, op1=ALU.subtract), reads=[sttb], writes=[sttb])
                kb.op('act', lambda e, stt=stt: e.activation(out=stt[:, 4:5], in_=stt[:, 7:8], func=AF.Sqrt, bias=gneps[:, 0:1]), reads=[sttb, gnepsb], writes=[sttb])
                kb.op('dve', lambda e, stt=stt: e.reciprocal(out=stt[:, 5:6], in_=stt[:, 4:5]), reads=[sttb], writes=[sttb])
                kb.op('dve', lambda e, stt=stt, fn=fn, c=c, ch0=ch0: e.tensor_scalar(out=fn[:, ch0:ch0 + 64], in0=yacc[:, c, ch0:ch0 + 64], scalar1=stt[:, 6:7], scalar2=stt[:, 5:6], op0=ALU.subtract, op1=ALU.mult),
                      reads=[sttb, yaccb[c]], writes=[fnb])
            if DBG.get('dumpfin') == c:
                dump(kb, nc, 'stt', stt[:, :], [128, 8], [sttb])
                dump(kb, nc, 'fn0', fn[:, :], [128, 192], [fnb])
                dump(kb, nc, 'gt', gt[:, :], [128, 192], [gtb])
                dump(kb, nc, 'yaccf', yacc[:, c, :], [128, 192], [yaccb[c]])
                dump(kb, nc, 'lng', lng[:, :], [128, 192], [lngb])
            kb.op('pool', lambda e, fn=fn: e.tensor_tensor(out=fn[:, :], in0=fn[:, :], in1=lng[:, :], op=ALU.mult), reads=[fnb, lngb], writes=[fnb])
            kb.op('pool', lambda e, fn=fn: e.tensor_tensor(out=fn[:, :], in0=fn[:, :], in1=lnb[:, :], op=ALU.add), reads=[fnb, lnbb], writes=[fnb])
            kb.op('dve', lambda e, fn=fn, gt=gt: e.tensor_tensor(out=fn[:, :], in0=fn[:, :], in1=gt[:, :], op=ALU.mult), reads=[fnb, gtb], writes=[fnb])
            kb.op('pool', lambda e, fn=fn, t0=t0: e.dma_start(out=YR.rows(t0, 0, 192), in_=fn[:, :]), reads=[fnb], dma=True)


    def dir_gen(d):
        for c in (DBG['chunks'][d] if 'chunks' in DBG else CHUNK_ORDER[d]):
            yield from chunk_body(d, c)

    dirs = DBG.get('dirs', [0, 1])
    for d in dirs:
        kb.op('dve', lambda e, d=d: e.memset(ST[:, d, :, :], 0.0), writes=STb[d])
        kb.op('dve', lambda e, d=d: e.memset(STbf[:, d, :, :], 0.0), writes=STb[d])
    gens = [dir_gen(d) for d in dirs]
    if DBG.get('no_interleave'):
        for g_ in gens:
            for _ in g_:
                pass
    else:
        while gens:
            for g_ in list(gens):
                try:
                    next(g_)
                except StopIteration:
                    gens.remove(g_)


FN_PARAMS = {'c128': [128, 128], 'ns128': [128, 128], 'twc': [128, 64], 'tws': [128, 64], 'fl1': [128, 64], 'fl2': [128, 64],
             'c256': [128, 2, 256], 'ns256': [128, 2, 256]}


def phase_fnet(kb, nc, HT, GT, YF, ZS, P, bar):
    sc_l = 1.0 / float(np.sqrt(L * 64.0))
    sc_c = 1.0 / float(np.sqrt(LC * 64.0))
    with ExitStack() as c1:
        def const(name, shape, src):
            t = kb.sb(name, shape, F32, c1)
            b, = load_consts(kb, [(t, src)])
            return t, b
        c128, c128b = const('c128', [128, 128], P['c128'])
        ns128, ns128b = const('ns128', [128, 128], P['ns128'])
        twc, twcb = const('twc', [128, 64], P['twc'])
        tws, twsb = const('tws', [128, 64], P['tws'])
        c256, c256b = const('c256', [128, 2, 256], P['c256'])
        ns256, ns256b = const('ns256', [128, 2, 256], P['ns256'])
        hc = kb.sb('hctx', [128, 2, 128], F32, c1)
        hcb = Buf()
        gc = kb.sb('gctx', [128, 2, 64], F32, c1)
        gcb = Buf()
        for lt in range(2):
            kb.op('sp', lambda e, lt=lt: e.dma_start(out=hc[:, lt, :], in_=HT[lt * 128:(lt + 1) * 128, :]), writes=[hcb], dma=True)
            kb.op('sp', lambda e, lt=lt: e.dma_start(out=gc[:, lt, :], in_=GT[lt * 128:(lt + 1) * 128, 192:256]), writes=[gcb], dma=True)
        pc = Ring(kb, 'pfc', [128, 512], F32, 1, psum=True, es=c1)
        fo = Ring(kb, 'foc', [128, 64], F32, 2, es=c1)
        for lo in range(2):
            ps, psb = pc.next()
            for lt in range(2):
                kb.op('pe', lambda e, ps=ps, lt=lt, lo=lo: e.matmul(ps[:, 0:64], lhsT=c256[:, lt, lo * 128:(lo + 1) * 128], rhs=hc[:, lt, 0:64], start=(lt == 0), stop=False),
                      reads=[c256b, hcb], writes=[psb])
            for lt in range(2):
                kb.op('pe', lambda e, ps=ps, lt=lt, lo=lo: e.matmul(ps[:, 0:64], lhsT=ns256[:, lt, lo * 128:(lo + 1) * 128], rhs=hc[:, lt, 64:128], start=False, stop=(lt == 1)),
                      reads=[ns256b, hcb], writes=[psb])
            f, fb = fo.next()
            kb.op('dve', lambda e, f=f, ps=ps, lo=lo: e.scalar_tensor_tensor(out=f[:, :], in0=ps[:, 0:64], scalar=sc_c, in1=gc[:, lo, :], op0=ALU.mult, op1=ALU.mult),
                  reads=[psb, gcb], writes=[fb])
            kb.op('pool', lambda e, f=f, lo=lo: e.dma_start(out=YF.rows(lo * 128, 192, 256), in_=f[:, :]), reads=[fb], dma=True)
        h1 = kb.sb('h1', [128, 64, 128], F32, c1)
        h1b = Buf()
        HTl = HT[LC:LT, :].rearrange("(a b) c -> a b c", b=64)
        for j in range(4):
            kb.op('sp', lambda e, j=j: e.dma_start(out=h1[:, j * 16:(j + 1) * 16, :], in_=HTl[:, j * 16:(j + 1) * 16, :]), writes=[h1b], dma=True)
        py = Ring(kb, 'py', [128, 512], F32, 4, psum=True, es=c1)
        zt = Ring(kb, 'zt', [128, 4, 128], F32, 6, es=c1)
        zo = Ring(kb, 'zo', [128, 2, 4, 128], F32, 3, es=c1)
        for pc_ in range(16):
            b0 = pc_ * 4
            pr, prb = py.next()
            pi, pib = py.next()
            kb.op('pe', lambda e, pr=pr, b0=b0: e.matmul(pr[:, :], lhsT=c128[:, :], rhs=h1[:, b0:b0 + 4, :], start=True, stop=True), reads=[c128b, h1b], writes=[prb])
            kb.op('pe', lambda e, pi=pi, b0=b0: e.matmul(pi[:, :], lhsT=ns128[:, :], rhs=h1[:, b0:b0 + 4, :], start=True, stop=True), reads=[ns128b, h1b], writes=[pib])
            cb_ = twc[:, b0:b0 + 4].unsqueeze(2).to_broadcast([128, 4, 128])
            sb_ = tws[:, b0:b0 + 4].unsqueeze(2).to_broadcast([128, 4, 128])
            prv = pr[:, :].rearrange("p (b c) -> p b c", c=128)
            piv = pi[:, :].rearrange("p (b c) -> p b c", c=128)
            t1, t1b = zt.next()
            t2, t2b = zt.next()
            z, zb = zo.next()
            kb.op('dve', lambda e, t1=t1, prv=prv, cb_=cb_: e.tensor_tensor(out=t1[:, :, :], in0=prv, in1=cb_, op=ALU.mult), reads=[prb, twcb], writes=[t1b])
            kb.op('dve', lambda e, t2=t2, piv=piv, sb_=sb_: e.tensor_tensor(out=t2[:, :, :], in0=piv, in1=sb_, op=ALU.mult), reads=[pib, twsb], writes=[t2b])
            kb.op('pool', lambda e, z=z, t1=t1, t2=t2: e.tensor_tensor(out=z[:, 0, :, :], in0=t1[:, :, :], in1=t2[:, :, :], op=ALU.add), reads=[t1b, t2b], writes=[zb])
            t3, t3b = zt.next()
            t4, t4b = zt.next()
            kb.op('dve', lambda e, t3=t3, piv=piv, cb_=cb_: e.tensor_tensor(out=t3[:, :, :], in0=piv, in1=cb_, op=ALU.mult), reads=[pib, twcb], writes=[t3b])
            kb.op('dve', lambda e, t4=t4, prv=prv, sb_=sb_: e.tensor_tensor(out=t4[:, :, :], in0=prv, in1=sb_, op=ALU.mult), reads=[prb, twsb], writes=[t4b])
            kb.op('pool', lambda e, z=z, t3=t3, t4=t4: e.tensor_tensor(out=z[:, 1, :, :], in0=t3[:, :, :], in1=t4[:, :, :], op=ALU.subtract), reads=[t3b, t4b], writes=[zb])
            for ri in range(2):
                kb.op('pool', lambda e, z=z, ri=ri, b0=b0: e.dma_start(out=ZS[ri, :, b0:b0 + 4, :], in_=z[:, ri, :, :]), reads=[zb], dma=True)
        kb.barrier(bar[:])
    with ExitStack() as c2:
        fl1 = kb.sb('fl1', [128, 64], F32, c2)
        fl2 = kb.sb('fl2', [128, 64], F32, c2)
        fl1b, fl2b = load_consts(kb, [(fl1, P['fl1']), (fl2, P['fl2'])])
        rz1 = kb.sb('rz1', [128, 128, 64], F32, c2)
        rz2 = kb.sb('rz2', [128, 128, 64], F32, c2)
        g2 = kb.sb('g2', [64, 128, 64], F32, c2)
        fo2 = kb.sb('fo2', [64, 128, 64], F32, c2)
        rz1b = [Buf() for _ in range(4)]
        rz2b = [Buf() for _ in range(4)]
        g2b = Buf()
        fo2b = [Buf() for _ in range(4)]
        for qa in range(4):
            asl = slice(qa * 32, (qa + 1) * 32)
            for ri in range(2):
                src = ZS[ri].rearrange("a b c -> b a c")
                kb.op('sp', lambda e, ri=ri, src=src, asl=asl: e.dma_start(out=rz1[ri * 64:(ri + 1) * 64, asl, :], in_=src[:, asl, 0:64]), writes=[rz1b[qa]], dma=True)
                kb.op('sp', lambda e, ri=ri, src=src, asl=asl: e.dma_start(out=rz2[ri * 64:(ri + 1) * 64, asl, :], in_=src[:, asl, 64:128]), writes=[rz2b[qa]], dma=True)
        kb.op('sp', lambda e: e.dma_start(out=g2[:, :, :], in_=GT[LC:LT, 192:256].rearrange("(b a) c -> b a c", a=128)), writes=[g2b], dma=True)
        pf = Ring(kb, 'pf', [128, 512], F32, 2, psum=True, es=c2)
        for pc_ in range(16):
            a0 = pc_ * 8
            qa = pc_ // 4
            ps, psb = pf.next()
            kb.op('pe', lambda e, ps=ps, a0=a0: e.matmul(ps[0:64, :], lhsT=fl1[:, :], rhs=rz1[:, a0:a0 + 8, :], start=True, stop=False), reads=[fl1b, rz1b[qa]], writes=[psb])
            kb.op('pe', lambda e, ps=ps, a0=a0: e.matmul(ps[0:64, :], lhsT=fl2[:, :], rhs=rz2[:, a0:a0 + 8, :], start=False, stop=True), reads=[fl2b, rz2b[qa]], writes=[psb])
            kb.op('dve', lambda e, ps=ps, a0=a0: e.scalar_tensor_tensor(out=fo2[:, a0:a0 + 8, :], in0=ps[0:64, :].rearrange("p (a c) -> p a c", c=64), scalar=sc_l,
                                                                        in1=g2[:, a0:a0 + 8, :], op0=ALU.mult, op1=ALU.mult), reads=[psb, g2b], writes=[fo2b[qa]])
        allfo = fo2b
        for (dst, b0, b1) in YF.lat_groups():
            kb.op('pool', lambda e, dst=dst, b0=b0, b1=b1: e.dma_start(out=dst, in_=fo2[b0:b1, :, :]), reads=allfo, dma=True)
        kb.barrier(bar[:])


def gate_rows(kb, es, nc, sil, silb, adaw_gate, adab_row, npost_b, ones1, ones1b, keep_es, NG=None):
    if NG is None:
        NG = kb.sb('NG', [128, 2, D], F32, keep_es)
    NGb = Buf()
    wg = kb.sb('adawg', [128, 8, D], F32, es)
    wgb = Buf()
    for k in range(8):
        kb.op('sp', lambda e, k=k: e.dma_start(out=wg[:, k, :], in_=adaw_gate[k * 128:(k + 1) * 128, :]), writes=[wgb], dma=True)
    br = kb.sb('adabr', [1, D], F32, es)
    brb, = load_consts(kb, [(br, adab_row)])
    npb = kb.sb('npostb', [128, D], F32, es)
    npbb, = load_consts(kb, [(npb, npost_b)])
    onesq = kb.sb('onesq', [128, 128], F32, es)
    onesqb = Buf()
    kb.op('dve', lambda e: e.memset(onesq[:], 1.0), writes=[onesqb])
    srep = kb.sb('silrep', [128, 8, 128], F32, es)
    pg_ = Ring(kb, 'pgate', [128, 512], F32, 2, psum=True, es=es)
    for n in range(2):
        srb = Buf()
        for k in range(8):
            kb.op('dve', lambda e, k=k, n=n: e.tensor_scalar(out=srep[:, k, :], in0=onesq[:, :], scalar1=sil[:, k, n:n + 1], scalar2=None, op0=ALU.mult),
                  reads=[onesqb, silb], writes=[srb])
        for half in range(2):
            ps, psb = pg_.next()
            for k in range(8):
                kb.op('pe', lambda e, ps=ps, k=k, half=half: e.matmul(ps[:, :], lhsT=srep[:, k, :], rhs=wg[:, k, half * 512:(half + 1) * 512], start=(k == 0), stop=False),
                      reads=[srb, wgb], writes=[psb])
            kb.op('pe', lambda e, ps=ps, half=half: e.matmul(ps[:, :], lhsT=ones1[0:1, :], rhs=br[0:1, half * 512:(half + 1) * 512], start=False, stop=True),
                  reads=[ones1b, brb], writes=[psb])
            kb.op('dve', lambda e, ps=ps, n=n, half=half: e.tensor_tensor(out=NG[:, n, half * 512:(half + 1) * 512], in0=ps[:, :], in1=npb[:, half * 512:(half + 1) * 512], op=ALU.mult),
                  reads=[psb, npbb], writes=[NGb])
    return NG, NGb


class OutProj:
    def __init__(self, kb, es, nc, yT, w_out_src, xsrc, NG, NGb, epsT, epsb, nslot=3, ysrc=None, ident=None, npol=2):
        self.kb, self.yT, self.xsrc, self.NG, self.NGb, self.epsT, self.epsb = kb, yT, xsrc, NG, NGb, epsT, epsb
        self.ysrc, self.ident = ysrc, ident
        self.pref = {}
        self.ykbs = {}
        if ysrc is not None:
            self.ytok = Ring(kb, 'ytok', [128, 4, 256], F32, 2, es=es)
            self.ytokb = Ring(kb, 'ytokb', [128, 1024], BF16, 2, es=es)
            self.pyt = Ring(kb, 'pyt', [128, 8, 128], BF16, 1, psum=True, es=es)
        self.wo, self.wob = load_weight_bf16(kb, es, nc, 'wout_bf', w_out_src, D, es)
        self.ystg = Ring(kb, 'ystg', [128, 8, 128], F32, 2, es=es)
        self.ybf = Ring(kb, 'ybf', [128, 8, 128], BF16, 2, es=es)
        self.xin = Ring(kb, 'xres', [128, D], F32, 2, es=es)
        self.xout = Ring(kb, 'xnew', [128, D], F32, nslot, es=es)
        self.pol = Ring(kb, 'pol', [128, 512], F32, npol, psum=True, es=es)
        self.st = Ring(kb, 'ost', [128, 8], F32, 4, es=es, strict=True)
        self.junk = kb.sb('ojunk', [128, 512], BF16, es)
        self.junkb = Buf()
        self.tmp = Ring(kb, 'otmp', [128, D], F32, 2, es=es)

    def _pre(self, r0):
        kb = self.kb
        if self.ysrc is None:
            ys, ysb = self.ystg.next()
            yT = self.yT
            kb.op('sp', lambda e, ys=ys: e.dma_start(out=ys[:, :, :], in_=yT[:, r0:r0 + 128].rearrange("(k p) t -> p k t", p=128)), writes=[ysb], dma=True)
            yb, ybb = self.ybf.next()
            kb.op('pool', lambda e, yb=yb, ys=ys: e.tensor_copy(out=yb[:, :, :], in_=ys[:, :, :]), reads=[ysb], writes=[ybb])
        else:
            aps, sbufs = self.ysrc(r0)
            ykb16, ykb0 = self.ytokb.next()
            ykb16b = self.ykbs.setdefault(id(ykb0), [ykb0] + [Buf() for _ in range(3)])
            for r, ap_ in enumerate(aps):
                kb.op('sp', lambda e, ykb16=ykb16, r=r, ap_=ap_: e.dma_start(out=ykb16[:, r * 256:(r + 1) * 256], in_=ap_), reads=sbufs, writes=[ykb16b[r]], dma=True)
            pt, ptb = self.pyt.next()
            idt, idtb = self.ident
            for k in range(8):
                kb.op('pe', lambda e, pt=pt, ykb16=ykb16, k=k: e.transpose(out=pt[:, k, :], in_=ykb16[:, k * 128:(k + 1) * 128], identity=idt[:, :]), reads=[ykb16b, idtb], writes=[ptb])
            yb, ybb = self.ybf.next()
            kb.op('act', lambda e, yb=yb, pt=pt: e.activation(out=yb[:, :, :], in_=pt[:, :, :], func=AF.Copy), reads=[ptb], writes=[ybb])
        xi, xib = self.xin.next()
        kb.op('sp', lambda e, xi=xi: e.dma_start(out=xi[:, :], in_=self.xsrc[r0:r0 + 128, :]), writes=[xib], dma=True)
        self.pref[r0] = (yb, ybb, xi, xib)

    def tile(self, r0, n, nxt=None):
        kb = self.kb
        if r0 not in self.pref:
            self._pre(r0)
        if nxt is not None and nxt not in self.pref:
            self._pre(nxt)
        yb, ybb, xi, xib = self.pref.pop(r0)
        pss = [self.pol.next(), self.pol.next()]
        for half in range(2):
            ps, psb = pss[half]
            for k in range(8):
                kb.op('pe', lambda e, ps=ps, k=k, half=half, yb=yb: e.matmul(ps[:, :], lhsT=yb[:, k, :], rhs=self.wo[:, k, half * 512:(half + 1) * 512],
                                                                           start=(k == 0), stop=(k == 7)), reads=[ybb, self.wob], writes=[psb])
        st, stb = self.st.next()
        for half in range(2):
            ps, psb = pss[half]
            kb.op('act', lambda e, ps=ps, st=st, half=half: e.activation(out=self.junk[:, :], in_=ps[:, :], func=AF.Square, accum_out=st[:, half:half + 1]),
                  reads=[psb], writes=[stb, self.junkb])
        kb.op('dve', lambda e, st=st: e.tensor_tensor(out=st[:, 2:3], in0=st[:, 0:1], in1=st[:, 1:2], op=ALU.add), reads=[stb], writes=[stb])
        kb.op('act', lambda e, st=st: e.activation(out=st[:, 3:4], in_=st[:, 2:3], func=AF.Sqrt, scale=1.0 / D, bias=self.epsT[:, 0:1]), reads=[stb, self.epsb], writes=[stb])
        kb.op('dve', lambda e, st=st: e.reciprocal(out=st[:, 4:5], in_=st[:, 3:4]), reads=[stb], writes=[stb])
        tm_, tmb_ = self.tmp.next()
        for half in range(2):
            ps, psb = pss[half]
            kb.op('dve', lambda e, ps=ps, st=st, tm_=tm_, half=half: e.scalar_tensor_tensor(out=tm_[:, half * 512:(half + 1) * 512], in0=ps[:, :], scalar=st[:, 4:5],
                                                                                       in1=self.NG[:, n, half * 512:(half + 1) * 512], op0=ALU.mult, op1=ALU.mult),
                  reads=[psb, stb, self.NGb], writes=[tmb_])
        xo, xob = self.xout.next()
        kb.op('dve', lambda e, xo=xo, tm_=tm_, xi=xi: e.tensor_tensor(out=xo[:, :], in0=tm_[:, :], in1=xi[:, :], op=ALU.add), reads=[tmb_, xib], writes=[xob])
        return xo, xob


L3_TOK = 2048


def part3(kb, nc, IN, ZG, X1s, OUT, C):
    epsT, epsb, ones1, ones1b = C['epsT'], C['epsb'], C['ones1'], C['ones1b']
    with ExitStack() as g:
        sil = kb.sb('sil3', [128, 8, 2], F32, g)
        silb = Buf(strict=True)
        kb.op('sp', lambda e: e.dma_start(out=sil[:], in_=IN['sil_in']), writes=[silb], dma=True)
        kb.op('act', lambda e: e.activation(out=sil[:], in_=sil[:], func=AF.Silu), reads=[silb], writes=[silb])
        NG = kb.sb('NG3', [128, 2, D], F32, g)
        with ExitStack() as g0:
            NG, NGb = gate_rows(kb, g0, nc, sil, silb, IN['adawg1'], IN['adabr1'], IN['npostb1'], ones1, ones1b, g0, NG=NG)
            kb.barrier(C['bar'][:])
        op = OutProj(kb, g, nc, None, IN['wout1'], X1s, NG, NGb, epsT, epsb, nslot=4, ysrc=ZG.src, ident=(C['ident_bf'], C['identb']), npol=6)
        for i in range(L // 128):
            xo, xob = op.tile(i * 128, 0, nxt=((i + 1) * 128 if (i + 1) * 128 < L else None))
            kb.op('pool', lambda e, xo=xo, i=i: e.dma_start(out=OUT[i * 128:(i + 1) * 128, :], in_=xo[:, :]), reads=[xob], dma=True, final=True)
        kb.barrier(C['bar'][:])


RT_PARAMS = {'logit': [128, 4], 'diffT': [128, 2, 128], 'mask01T': [128, 2, 128], 'posxi': [128, 2, 128], 'poszeta': [128, 2]}


def part2(kb, nc, IN, YG, X1s, ZD, C):
    debug = False
    sil_in, adawg, adabr, npostb, adaw1, adab1, normw1, win1, ropec, ropes = (IN[k_] for k_ in (
        'sil_in', 'adawg0', 'adabr0', 'npostb0', 'adaw1', 'adab1', 'normw1', 'win1', 'ropec', 'ropes'))
    xin, wout0 = IN['xin'], IN['wout0p']
    RP = {k_: IN['r_' + k_] for k_ in RT_PARAMS}
    X1 = X1s
    Z = ZD
    QKV = dram_tmp(nc, 'QKV', [LT, 768], BF16, debug)
    GS = dram_tmp(nc, 'GS', [LT, 256], F32, debug)
    bar, ident_f, identfb, ident_bf, identb, epsT, epsb, ones1, ones1b = (C[k_] for k_ in (
        'bar', 'ident_f', 'identfb', 'ident_bf', 'identb', 'epsT', 'epsb', 'ones1', 'ones1b'))
    with ExitStack() as pa:
        sil = kb.sb('sil', [128, 8, 2], F32, pa)
        silb = Buf(strict=True)
        kb.op('sp', lambda e: e.dma_start(out=sil[:], in_=sil_in), writes=[silb], dma=True)
        kb.op('act', lambda e: e.activation(out=sil[:], in_=sil[:], func=AF.Silu), reads=[silb], writes=[silb])
        NG = kb.sb('NG', [128, 2, D], F32, pa)
        mod1 = kb.sb('mod1', [128, 16, 2], F32, pa)
        G1 = kb.sb('G1', [128, 8, 2], F32, pa)
        with ExitStack() as pg0:
            NG, NGb = gate_rows(kb, pg0, nc, sil, silb, adawg, adabr, npostb, ones1, ones1b, pg0, NG=NG)
            mod1, mod1b = adaln_vectors(kb, pg0, nc, sil_in, adaw1, adab1, normw1, 16, mod_tile=mod1)
            nw = kb.sb('nw1', [128, 8], F32, pg0)
            nwb, = load_consts(kb, [(nw, normw1)])
            G1buf = Buf(strict=True)
            for n in range(2):
                kb.op('dve', lambda e, n=n: e.scalar_tensor_tensor(out=G1[:, :, n], in0=mod1[:, 8:16, n], scalar=1.0, in1=nw[:, :], op0=ALU.add, op1=ALU.mult),
                      reads=[mod1b, nwb], writes=[G1buf])
            kb.barrier(bar[:])
        Gb = {'G': G1buf, 'eps': epsT, 'epsb': epsb}
        op0 = OutProj(kb, pa, nc, None, wout0, xin, NG, NGb, epsT, epsb, nslot=3, ysrc=YG.src, ident=(ident_bf, identb))
        w1, w1b = load_weight_bf16(kb, pa, nc, 'w1_bf', win1, D, pa)
        for k in range(8):
            kb.op('pool', lambda e, k=k: e.tensor_scalar(out=w1[:, k, 256:512], in0=w1[:, k, 256:512], scalar1=float(128 ** -0.5), scalar2=None, op0=ALU.mult), reads=[w1b], writes=[w1b])
        pqk = Ring(kb, 'pqk', [128, 512], F32, 2, psum=True, es=pa)
        csr = Ring(kb, 'cs', [128, 2, 64], F32, 2, es=pa)
        qkvr = Ring(kb, 'qkvt', [128, 768], BF16, 2, es=pa)
        rtmp = Ring(kb, 'rtmp', [128, 4, 64], F32, 4, es=pa)
        gsr = Ring(kb, 'gst', [128, 256], F32, 2, es=pa)

        rlist = [t0_ + sb_ * 128 for (t0_, T_) in TILES for sb_ in range(T_ // 128)]

        def get_x(r0):
            n = 1 if r0 < LC else 0
            ix = rlist.index(r0)
            xo, xob = op0.tile(r0, n, nxt=(rlist[ix + 1] if ix + 1 < len(rlist) else None))
            if r0 >= LC:
                kb.op('pool', lambda e, xo=xo, r0=r0: e.dma_start(out=X1[r0 - LC:r0 - LC + 128, :], in_=xo[:, :]), reads=[xob], dma=True)
            return xo, xob

        def emit_tile(t0, T, hT, hb):
            for sub in range(T // 128):
                r0 = t0 + sub * 128
                psA, psAb = pqk.next()
                psB, psBb = pqk.next()
                for half, (ps, psb) in enumerate([(psA, psAb), (psB, psBb)]):
                    for k in range(8):
                        kb.op('pe', lambda e, ps=ps, k=k, half=half, sub=sub: e.matmul(ps[:, :], lhsT=hT[:, k, sub * 128:(sub + 1) * 128], rhs=w1[:, k, half * 512:(half + 1) * 512],
                                                                                     start=(k == 0), stop=(k == 7)), reads=[hb, w1b], writes=[psb])
                qkv, qkvb = qkvr.next()
                if r0 >= LC:
                    cs, csb = csr.next()
                    kb.op('sp', lambda e, cs=cs, r0=r0: e.dma_start(out=cs[:, 0, :], in_=ropec[r0 - LC:r0 - LC + 128, :]), writes=[csb], dma=True)
                    kb.op('sp', lambda e, cs=cs, r0=r0: e.dma_start(out=cs[:, 1, :], in_=ropes[r0 - LC:r0 - LC + 128, :]), writes=[csb], dma=True)
                    pv_ = psA[:, :].rearrange("p (g h f) -> p g h f", g=4, h=2)
                    ov_ = qkv[:, 0:512].rearrange("p (g h f) -> p g h f", g=4, h=2)
                    cb_ = cs[:, 0, :].unsqueeze(1).to_broadcast([128, 4, 64])
                    sb_ = cs[:, 1, :].unsqueeze(1).to_broadcast([128, 4, 64])
                    ta, tab = rtmp.next()
                    tb, tbb = rtmp.next()
                    kb.op('dve', lambda e, ta=ta, pv_=pv_, cb_=cb_: e.tensor_tensor(out=ta[:, :, :], in0=pv_[:, :, 0, :], in1=cb_, op=ALU.mult), reads=[psAb, csb], writes=[tab])
                    kb.op('dve', lambda e, tb=tb, pv_=pv_, sb_=sb_: e.tensor_tensor(out=tb[:, :, :], in0=pv_[:, :, 1, :], in1=sb_, op=ALU.mult), reads=[psAb, csb], writes=[tbb])
                    kb.op('pool', lambda e, ta=ta, tb=tb, ov_=ov_: e.tensor_tensor(out=ov_[:, :, 0, :], in0=ta[:, :, :], in1=tb[:, :, :], op=ALU.subtract), reads=[tab, tbb], writes=[qkvb])
                    tc_, tcb = rtmp.next()
                    td, tdb = rtmp.next()
                    kb.op('dve', lambda e, tc_=tc_, pv_=pv_, sb_=sb_: e.tensor_tensor(out=tc_[:, :, :], in0=pv_[:, :, 0, :], in1=sb_, op=ALU.mult), reads=[psAb, csb], writes=[tcb])
                    kb.op('dve', lambda e, td=td, pv_=pv_, cb_=cb_: e.tensor_tensor(out=td[:, :, :], in0=pv_[:, :, 1, :], in1=cb_, op=ALU.mult), reads=[psAb, csb], writes=[tdb])
                    kb.op('pool', lambda e, tc_=tc_, td=td, ov_=ov_: e.tensor_tensor(out=ov_[:, :, 1, :], in0=tc_[:, :, :], in1=td[:, :, :], op=ALU.add), reads=[tcb, tdb], writes=[qkvb])
                else:
                    kb.op('act', lambda e, qkv=qkv, psA=psA: e.activation(out=qkv[:, 0:512], in_=psA[:, :], func=AF.Copy), reads=[psAb], writes=[qkvb])
                kb.op('act', lambda e, qkv=qkv, psB=psB: e.activation(out=qkv[:, 512:768], in_=psB[:, 0:256], func=AF.Copy), reads=[psBb], writes=[qkvb])
                gs, gsb = gsr.next()
                kb.op('act', lambda e, gs=gs, psB=psB: e.activation(out=gs[:, :], in_=psB[:, 256:512], func=AF.Silu), reads=[psBb], writes=[gsb])
                kb.op('pool', lambda e, qkv=qkv, r0=r0: e.dma_start(out=QKV[r0:r0 + 128, :], in_=qkv[:, :]), reads=[qkvb], dma=True)
                kb.op('pool', lambda e, gs=gs, r0=r0: e.dma_start(out=GS[r0:r0 + 128, :], in_=gs[:, :]), reads=[gsb], dma=True)

        phase_proj(kb, nc, pa, None, None, G1, mod1, Gb, ident_bf, identb, emit_tile, get_x=get_x)
        kb.barrier(bar[:])
    with ExitStack() as pr:
        phase_ret(kb, nc, pr, QKV, GS, Z, RP, ident_bf, identb, epsT, epsb)
        kb.barrier(bar[:])


def phase_ret(kb, nc, es, QKV, GS, Z, RP, ident_bf, identb, epsT, epsb):
    def const(name, shape, src):
        t = kb.sb(name, shape, F32, es)
        b, = load_consts(kb, [(t, src)])
        return t, b
    lgt, lgtb = const('lgt', [128, 4], RP['logit'])
    lgtb.strict = True
    diffT, diffTb = const('diffT', [128, 2, 128], RP['diffT'])
    m01, m01b = const('m01', [128, 2, 128], RP['mask01T'])
    pxi, pxib = const('pxi', [128, 2, 128], RP['posxi'])
    pze, pzeb = const('pze', [128, 2], RP['poszeta'])
    kb.op('act', lambda e: e.activation(out=lgt[:, :], in_=lgt[:, :], func=AF.Exp, scale=-1.0), reads=[lgtb], writes=[lgtb])
    kb.op('dve', lambda e: e.tensor_scalar(out=lgt[:, :], in0=lgt[:, :], scalar1=1.0, scalar2=None, op0=ALU.add), reads=[lgtb], writes=[lgtb])
    kb.op('act', lambda e: e.activation(out=lgt[:, :], in_=lgt[:, :], func=AF.Ln), reads=[lgtb], writes=[lgtb])
    kb.op('dve', lambda e: e.tensor_scalar(out=lgt[:, :], in0=lgt[:, :], scalar1=-1.0, scalar2=None, op0=ALU.mult), reads=[lgtb], writes=[lgtb])
    dmt = kb.sb('dmt', [128, 4, 128], BF16, es)
    xib = kb.sb('xib', [128, 4, 128], F32, es)
    zet = kb.sb('zet', [128, 8], F32, es)
    tabb = Buf(strict=True)
    tmpd = kb.sb('tmpd', [128, 128], F32, es)
    tmpdb = Buf()
    for hl in range(2):
        for dr in range(2):
            j = 2 * hl + dr
            kb.op('act', lambda e, j=j, dr=dr: e.activation(out=tmpd[:, :], in_=diffT[:, dr, :], func=AF.Exp, scale=lgt[:, j:j + 1]), reads=[diffTb, lgtb], writes=[tmpdb])
            kb.op('dve', lambda e, j=j, dr=dr: e.tensor_tensor(out=dmt[:, j, :], in0=tmpd[:, :], in1=m01[:, dr, :], op=ALU.mult), reads=[tmpdb, m01b], writes=[tabb])
            kb.op('act', lambda e, j=j, dr=dr: e.activation(out=xib[:, j, :], in_=pxi[:, dr, :], func=AF.Exp, scale=lgt[:, j:j + 1]), reads=[pxib, lgtb], writes=[tabb])
            kb.op('act', lambda e, j=j, dr=dr: e.activation(out=zet[:, j:j + 1], in_=pze[:, dr:dr + 1], func=AF.Exp, scale=lgt[:, j:j + 1]), reads=[pzeb, lgtb], writes=[tabb])
            kb.op('act', lambda e, j=j: e.activation(out=zet[:, 4 + j:5 + j], in_=lgt[:, j:j + 1], func=AF.Exp, scale=128.0), reads=[lgtb], writes=[tabb])
    oacc = kb.sb('oacc', [128, NCH, 256], F32, es)
    oaccb = [Buf() for _ in range(NCH)]
    kb.op('pool', lambda e: e.memset(oacc[:], 0.0), writes=oaccb)
    R = kb.sb('Rst', [128, 2, 128], F32, es)
    Rbf = kb.sb('Rbf', [128, 2, 128], BF16, es)
    Rb = [Buf(), Buf()]
    qr = Ring(kb, 'rq', [128, 768], BF16, 3, es=es)
    qtr = Ring(kb, 'rqT', [128, 128], BF16, 3, es=es)
    ktr_ = Ring(kb, 'rkT', [128, 128], BF16, 3, es=es)
    qxr = Ring(kb, 'rqx', [128, 128], BF16, 3, es=es)
    kzr = Ring(kb, 'rkz', [128, 128], BF16, 3, es=es)
    sdr = Ring(kb, 'rsd', [128, 128], BF16, 3, es=es)
    ptT = Ring(kb, 'rptT', [128, 1024], BF16, 2, psum=True, es=es)
    pS = Ring(kb, 'rpS', [128, 512], F32, 2, psum=True, es=es)
    pO = Ring(kb, 'rpO', [128, 512], F32, 2, psum=True, es=es)
    pR = Ring(kb, 'rpR', [128, 512], F32, 2, psum=True, es=es)
    gsr = Ring(kb, 'rgs', [128, 256], F32, 2, es=es)
    zr = Ring(kb, 'rz', [128, 256], F32, 2, es=es)
    sst = Ring(kb, 'rss', [128, 8], F32, 4, es=es, strict=True)
    junk = kb.sb('rjunk', [128, 128], BF16, es)
    junkb = Buf()
    for dr in range(2):
        kb.op('dve', lambda e: e.memset(R[:], 0.0), writes=Rb)
        kb.op('dve', lambda e: e.memset(Rbf[:], 0.0), writes=Rb)
        for c in CHUNK_ORDER[dr]:
            r0 = c * 128
            q, qb = qr.next()
            kb.op('sp', lambda e, q=q, r0=r0: e.dma_start(out=q[:, :], in_=QKV[r0:r0 + 128, :]), writes=[qb], dma=True)
            for hl in range(2):
                j = 2 * hl + dr
                vt_ = q[:, 512 + hl * 128:512 + (hl + 1) * 128]
                ktok = q[:, 256 + hl * 128:256 + (hl + 1) * 128]
                if c >= 2:
                    pt, ptb = ptT.next()
                    kb.op('pe', lambda e, pt=pt, q=q, hl=hl: e.transpose(out=pt[:, 0:128], in_=q[:, hl * 128:(hl + 1) * 128], identity=ident_bf[:, :]), reads=[qb, identb], writes=[ptb])
                    kb.op('pe', lambda e, pt=pt, ktok=ktok: e.transpose(out=pt[:, 128:256], in_=ktok, identity=ident_bf[:, :]), reads=[qb, identb], writes=[ptb])
                    qT, qTb = qtr.next()
                    kT, kTb = ktr_.next()
                    qx, qxb = qxr.next()
                    kb.op('act', lambda e, qT=qT, pt=pt: e.activation(out=qT[:, :], in_=pt[:, 0:128], func=AF.Copy), reads=[ptb], writes=[qTb])
                    kb.op('dve', lambda e, qx=qx, pt=pt, j=j: e.tensor_tensor(out=qx[:, :], in0=pt[:, 0:128], in1=xib[:, j, :], op=ALU.mult), reads=[ptb, tabb], writes=[qxb])
                    kb.op('act', lambda e, kT=kT, pt=pt: e.activation(out=kT[:, :], in_=pt[:, 128:256], func=AF.Copy), reads=[ptb], writes=[kTb])
                    ps, psb = pS.next()
                    kb.op('pe', lambda e, ps=ps, kT=kT, qT=qT: e.matmul(ps[:, 0:128], lhsT=kT[:, :], rhs=qT[:, :], start=True, stop=True), reads=[kTb, qTb], writes=[psb])
                    sd, sdb = sdr.next()
                    kb.op('dve', lambda e, sd=sd, ps=ps, j=j: e.tensor_tensor(out=sd[:, :], in0=ps[:, 0:128], in1=dmt[:, j, :], op=ALU.mult), reads=[psb, tabb], writes=[sdb])
                    po, pob = pO.next()
                    kb.op('pe', lambda e, po=po, sd=sd, vt_=vt_: e.matmul(po[:, 0:128], lhsT=sd[:, :], rhs=vt_, start=True, stop=False), reads=[sdb, qb], writes=[pob])
                    kb.op('pe', lambda e, po=po, qx=qx, hl=hl: e.matmul(po[:, 0:128], lhsT=qx[:, :], rhs=Rbf[:, hl, :], start=False, stop=True), reads=[qxb, Rb[hl]], writes=[pob])
                    if dr == 0:
                        kb.op('act', lambda e, po=po, c=c, hl=hl: e.activation(out=oacc[:, c, hl * 128:(hl + 1) * 128], in_=po[:, 0:128], func=AF.Copy), reads=[pob], writes=[oaccb[c]])
                    else:
                        kb.op('dve', lambda e, po=po, c=c, hl=hl: e.tensor_tensor(out=oacc[:, c, hl * 128:(hl + 1) * 128], in0=po[:, 0:128], in1=oacc[:, c, hl * 128:(hl + 1) * 128], op=ALU.add),
                              reads=[pob, oaccb[c]], writes=[oaccb[c]])
                kz, kzb = kzr.next()
                kb.op('pool', lambda e, kz=kz, ktok=ktok, j=j: e.tensor_scalar(out=kz[:, :], in0=ktok, scalar1=zet[:, j:j + 1], scalar2=None, op0=ALU.mult), reads=[qb, tabb], writes=[kzb])
                pr_, prb = pR.next()
                kb.op('pe', lambda e, pr_=pr_, kz=kz, vt_=vt_: e.matmul(pr_[:, 0:128], lhsT=kz[:, :], rhs=vt_, start=True, stop=True), reads=[kzb, qb], writes=[prb])
                kb.op('dve', lambda e, pr_=pr_, hl=hl, j=j: e.scalar_tensor_tensor(out=R[:, hl, :], in0=R[:, hl, :], scalar=zet[:, 4 + j:5 + j], in1=pr_[:, 0:128], op0=ALU.mult, op1=ALU.add),
                      reads=[prb, tabb, Rb[hl]], writes=[Rb[hl]])
                kb.op('act', lambda e, hl=hl: e.activation(out=Rbf[:, hl, :], in_=R[:, hl, :], func=AF.Copy), reads=[Rb[hl]], writes=[Rb[hl]])
            if dr == 1 and c >= 2:
                gs, gsb = gsr.next()
                kb.op('sp', lambda e, gs=gs, r0=r0: e.dma_start(out=gs[:, :], in_=GS[r0:r0 + 128, :]), writes=[gsb], dma=True)
                zt_, ztb = zr.next()
                for hl in range(2):
                    ss, ssb = sst.next()
                    kb.op('act', lambda e, ss=ss, c=c, hl=hl: e.activation(out=junk[:, :], in_=oacc[:, c, hl * 128:(hl + 1) * 128], func=AF.Square, accum_out=ss[:, 0:1]), reads=[oaccb[c]], writes=[ssb, junkb])
                    kb.op('act', lambda e, ss=ss: e.activation(out=ss[:, 1:2], in_=ss[:, 0:1], func=AF.Sqrt, scale=1.0 / 128, bias=epsT[:, 0:1]), reads=[ssb, epsb], writes=[ssb])
                    kb.op('dve', lambda e, ss=ss: e.reciprocal(out=ss[:, 2:3], in_=ss[:, 1:2]), reads=[ssb], writes=[ssb])
                    kb.op('dve', lambda e, ss=ss, zt_=zt_, gs=gs, c=c, hl=hl: e.scalar_tensor_tensor(out=zt_[:, hl * 128:(hl + 1) * 128], in0=oacc[:, c, hl * 128:(hl + 1) * 128], scalar=ss[:, 2:3],
                                                                                                   in1=gs[:, hl * 128:(hl + 1) * 128], op0=ALU.mult, op1=ALU.mult), reads=[ssb, oaccb[c], gsb], writes=[ztb])
                kb.op('pool', lambda e, zt_=zt_, r0=r0: e.dma_start(out=Z.rows(r0 - LC, 0, 256), in_=zt_[:, :]), reads=[ztb], dma=True)


class ChunkedDram:
    def __init__(self, nc, name, nrows, rows_per, width, ranks=1, dtype=BF16):
        self.rp = rows_per
        self.n = nrows // rows_per
        assert self.n * rows_per == nrows
        self.ranks = ranks
        self.tiles = [dram_tmp(nc, '%s%d' % (name, i), [ranks * rows_per, width], dtype) for i in range(self.n)]
        self.bufs = [Buf() for _ in range(self.n)]

    def rows(self, r0, c0, c1, n=128):
        i, lr = r0 // self.rp, r0 % self.rp
        return self.tiles[i][lr:lr + n, c0:c1]

    def src(self, r0):
        i, lr = r0 // self.rp, r0 % self.rp
        return [self.tiles[i][r * self.rp + lr:r * self.rp + lr + 128, :] for r in range(self.ranks)], [self.bufs[i]]

    def lat_groups(self):
        out = []
        tpc = self.rp // 128
        for i in range(self.n):
            b0, b1 = max(tpc * i - 2, 0), min(tpc * i + tpc - 2, 64)
            if b1 <= b0:
                continue
            lrow = (b0 + 2) * 128 - i * self.rp
            out.append((self.tiles[i][lrow:lrow + (b1 - b0) * 128, 192:256].rearrange("(b a) c -> b a c", a=128), b0, b1))
        return out


GROUPS = [[0, 1, 2, 3], [4, 5, 6, 7]]


def all_gather(kb, src, dst):
    for i in range(src.n):
        kb.op('pool', lambda e, i=i: e.collective_compute("AllGather", ALU.bypass, replica_groups=GROUPS, ins=[src.tiles[i].opt()], outs=[dst.tiles[i].opt()]),
              writes=[dst.bufs[i]], cc=True)


FUSED_INPUTS = {'xin': [LT, D], 'sil_in': [128, 8, 2], 'adaw': [D, 2048], 'adab': [128, 16], 'normw': [128, 8], 'wfm': [D, NFM], 'wg': [D, 256],
                'cs64': [64, 128], 'ident': [128, 128],
                'wout0p': [D, D], 'adawg0': [D, D], 'adabr0': [1, D], 'npostb0': [128, D], 'adaw1': [D, 2048], 'adab1': [128, 16], 'normw1': [128, 8],
                'win1': [D, D], 'ropec': [L, 64], 'ropes': [L, 64],
                'wout1': [D, D], 'adawg1': [D, D], 'adabr1': [1, D], 'npostb1': [128, D]}


def build_fused():
    nc = bass.Bass("TRN2", target_bir_lowering=False)
    kb = KB(nc)
    IN = {k_: dram_in(nc, k_, shp) for k_, shp in FUSED_INPUTS.items()}
    for k_, shp in RW_PARAMS.items():
        IN['p_' + k_] = dram_in(nc, 'p_' + k_, shp)
    for k_, shp in FN_PARAMS.items():
        IN['f_' + k_] = dram_in(nc, 'f_' + k_, shp)
    for k_, shp in RT_PARAMS.items():
        IN['r_' + k_] = dram_in(nc, 'r_' + k_, shp)
    OUT = dram_out(nc, 'out', [L, D])
    YB = ChunkedDram(nc, 'Yb', LT, 768, 256)
    YG = ChunkedDram(nc, 'Yg', LT, 768, 256, ranks=4)
    ZB = ChunkedDram(nc, 'Zb', L, 512, 256)
    ZG = ChunkedDram(nc, 'Zg', L, 512, 256, ranks=4)
    X1s = dram_tmp(nc, 'X1s', [L, D], F32)
    C = {}
    C['bar'] = kb.sb('bar', [128, 1], F32)
    C['ident_f'] = kb.sb('ident_f', [128, 128], F32)
    C['ident_bf'] = kb.sb('ident_bf', [128, 128], BF16)
    C['identfb'], = load_consts(kb, [(C['ident_f'], IN['ident'])])
    C['identb'] = Buf()
    kb.op('dve', lambda e: e.tensor_copy(out=C['ident_bf'][:], in_=C['ident_f'][:]), reads=[C['identfb']], writes=[C['identb']])
    C['epsT'] = kb.sb('epsT', [128, 1], F32)
    C['epsb'] = Buf()
    kb.op('dve', lambda e: e.memset(C['epsT'][:], EPS), writes=[C['epsb']])
    C['ones1'] = kb.sb('ones1', [1, 128], F32)
    C['ones1b'] = Buf()
    kb.op('dve', lambda e: e.memset(C['ones1'][:], 1.0), writes=[C['ones1b']])
    part1(kb, nc, IN, YB, C)
    all_gather(kb, YB, YG)
    part2(kb, nc, IN, YG, X1s, ZB, C)
    all_gather(kb, ZB, ZG)
    part3(kb, nc, IN, ZG, X1s, OUT, C)
    return nc, kb


def l1_inputs(inp, core):
    b, q = core // 4, core % 4
    f = lambda a: np.ascontiguousarray(a, dtype=np.float32)
    xin = np.concatenate([inp['ctx'][b], inp['x'][b]], axis=0)
    sil = np.stack([inp['c'][b].reshape(8,128).T, inp['c_ctx'].reshape(8,128).T], axis=-1)
    adaw = inp['ada_w'][0][:, :2048]
    adab = inp['ada_b'][0][:2048].reshape(16,128).T
    normw = inp['norm_pre'][0].reshape(8,128).T
    W = inp['ev_w_in'][0]
    cols = np.concatenate([np.arange(192)+192*q, 768+np.arange(192)+192*q, 1536+np.arange(192)+192*q,
                           np.arange(2304,2432), np.arange(2432,2560), 3328+64*q+np.arange(64)])
    gcols = np.concatenate([2560+192*q+np.arange(192), 3584+64*q+np.arange(64)])
    c = np.arange(64)
    ang = 2*np.pi*np.outer(c,c)/64
    cs64 = np.concatenate([np.cos(ang), np.sin(ang)], axis=1)
    return dict(xin=f(xin), sil_in=f(sil), adaw=f(adaw), adab=f(adab), normw=f(normw), wfm=f(W[:, cols]), wg=f(W[:, gcols]),
                cs64=f(cs64), ident=np.eye(128, dtype=np.float32)), cols, gcols

def rw_params(inp, core):
    b, q = core // 4, core % 4
    f = lambda a: np.ascontiguousarray(a, dtype=np.float32)
    chs = 192*q + np.arange(192)
    def blk2(v):
        o = np.zeros((128,2), np.float32); o[:,0] = v[:128]; o[:64,1] = v[128:]; return o
    mu = inp['ev_mu'][0]
    mub = np.zeros((128,8), np.float32)
    for j, base in enumerate([0, 768, 1536]):
        m2 = blk2(mu[base+chs]); mub[:, 2*j] = m2[:,0]; mub[:, 2*j+1] = m2[:,1]
    mub[:, 6] = mu[2304:2432]; mub[:, 7] = mu[2432:2560]
    p = np.arange(128)
    lanem = np.stack([(p%4==0),(p%4==1),(p%4==2),(p%4==3),(p%2==0),(p%2==1)],axis=1).astype(np.float32)
    a0 = np.zeros((128,2,2), np.float32)
    for d in range(2): a0[:, d, :] = blk2(inp['ev_a0'][0][d][chs])
    w2 = np.concatenate([inp['ev_w2'][0][0][:, chs], inp['ev_w2'][0][1][:, chs]], axis=0)
    a2 = np.concatenate([inp['ev_a2'][0][0][:, chs], inp['ev_a2'][0][1][:, chs]], axis=0)
    w0 = np.stack([inp['ev_w0'][0][0][chs], inp['ev_w0'][0][1][chs]])[None]
    s = np.arange(128)[:,None]; t = np.arange(128)[None,:]
    strict = [(s<t), (s>t)]; incl = [(s<=t), (s>=t)]
    blk = lambda bs: (np.arange(128)[:,None]//bs == np.arange(128)[None,:]//bs)
    maskSI2 = np.stack([np.concatenate([strict[d], incl[d]],axis=1) for d in range(2)], axis=1).astype(np.float32)
    maskSI = np.stack([np.concatenate([strict[d] & blk(32), incl[d]],axis=1) for d in range(2)], axis=1).astype(np.float32)
    maskST = np.stack([np.stack([(strict[d] & blk(32)).T, (strict[d] & blk(64) & ~blk(32)).T, (strict[d] & ~blk(64)).T], axis=1) for d in range(2)], axis=1).astype(np.float32)
    cdec = -np.exp(-0.5)
    tri = np.stack([np.concatenate([incl[d], strict[d]],axis=1) for d in range(2)], axis=1).astype(np.float32)*cdec
    eh = np.zeros((128,2,4), np.float32); eh[:64,0,0]=1; eh[64:,0,1]=1; eh[:64,1,2]=1
    obd = np.zeros((128,128), np.float32); obd[:64,:64]=1; obd[64:,64:]=1
    return dict(p_mu=mub, p_lanem=lanem, p_k_k=blk2(inp['ev_k_k'][0][chs]), p_k_a=blk2(inp['ev_k_a'][0][chs]),
                p_r_k=blk2(inp['ev_r_k'][0].reshape(-1)[chs]), p_a0=a0, p_w2=f(w2), p_a2=f(a2), p_w0=f(w0),
                p_maskSI=f(maskSI), p_maskSI2=f(maskSI2), p_maskST=f(maskST), p_tri=f(tri), p_ehead=eh, p_ones_bd=obd,
                p_lnx_g=f(np.tile(inp['ev_lnx_g'][0][chs][None], (128,1))), p_lnx_b=f(np.tile(inp['ev_lnx_b'][0][chs][None], (128,1))))

def fn_params():
    f = lambda a: np.ascontiguousarray(a, dtype=np.float32)
    a = np.arange(128); b = np.arange(64)
    ang128 = 2*np.pi*np.outer(a,a)/128
    tw = 2*np.pi*np.outer(a, b)/8192
    ang64 = 2*np.pi*np.outer(b,b)/64
    fl1 = np.concatenate([np.cos(ang64), np.sin(ang64)], axis=0)
    fl2 = np.concatenate([-np.sin(ang64), np.cos(ang64)], axis=0)
    l = np.arange(256); ang256 = 2*np.pi*np.outer(l,l)/256
    c256 = np.cos(ang256).reshape(2,128,256).transpose(1,0,2); ns256 = (-np.sin(ang256)).reshape(2,128,256).transpose(1,0,2)
    return dict(f_c128=f(np.cos(ang128)), f_ns128=f(-np.sin(ang128)), f_twc=f(np.cos(tw)), f_tws=f(np.sin(tw)), f_fl1=f(fl1), f_fl2=f(fl2),
                f_c256=f(c256), f_ns256=f(ns256))


def gate_inputs(inp, b, layer):
    f = lambda a: np.ascontiguousarray(a, dtype=np.float32)
    sil = np.stack([inp['c'][b].reshape(8,128).T, inp['c_ctx'].reshape(8,128).T], axis=-1)
    return dict(sil_in=f(sil), adawg=f(inp['ada_w'][layer][:, 2048:3072]), adabr=f(inp['ada_b'][layer][2048:3072][None]),
                npostb=f(np.tile(inp['norm_post'][layer][None], (128,1))))
def l3_inputs(inp, core, z_full, x1_full):
    b, qtr = core // 4, core % 4
    f = lambda a: np.ascontiguousarray(a, dtype=np.float32)
    sl = slice(2048*qtr, 2048*(qtr+1))
    m = dict(zT=f(z_full[b][sl].T), x1=f(x1_full[b][sl]), wout=f(inp['od_w_out'][0]))
    m.update(gate_inputs(inp, b, 1))
    return m


def l2_inputs(inp, core, y_full):
    b, p = core // 4, core % 4
    f = lambda a: np.ascontiguousarray(a, dtype=np.float32)
    m = dict(xin=f(np.concatenate([inp['ctx'][b], inp['x'][b]], axis=0)), wout=f(inp['ev_w_out'][0]))
    if y_full is not None:
        m['yT'] = f(y_full[b].T)
    m.update(gate_inputs(inp, b, 0))
    m['adaw1'] = f(inp['ada_w'][1][:, :2048]); m['adab1'] = f(inp['ada_b'][1][:2048].reshape(16,128).T); m['normw1'] = f(inp['norm_pre'][1].reshape(8,128).T)
    cols = np.concatenate([off + 256*p + np.arange(256) for off in (0, 1024, 2048, 3072)])
    m['win1'] = f(inp['od_w_in'][0][:, cols])
    pos = np.arange(8192); row = pos//64; col = pos%64
    inv = 10000.0 ** (-np.arange(0, 64, 2, dtype=np.float32)/64)
    ang = np.concatenate([row[:,None]*inv[None], col[:,None]*inv[None]], axis=1).astype(np.float32)
    m['ropec'] = f(np.cos(ang)); m['ropes'] = f(np.sin(ang))
    m['ident'] = np.eye(128, dtype=np.float32)
    lg = inp['od_decay_logit'][0]
    logit = np.zeros((128,4), np.float32)
    for hl in range(2):
        for dr in range(2): logit[:, 2*hl+dr] = lg[dr][2*p+hl]
    j = np.arange(128)[:,None]; i = np.arange(128)[None,:]
    diffT = np.stack([(i-j)*np.ones((128,128)), (j-i)*np.ones((128,128))], axis=1)
    mask = np.stack([(i>=j), (j>i)], axis=1)
    posxi = np.stack([np.tile((np.arange(128)+1)[None], (128,1)), np.tile((128-np.arange(128))[None], (128,1))], axis=1)
    pze = np.stack([127-np.arange(128), np.arange(128)], axis=1)
    m.update(r_logit=logit, r_diffT=f(diffT), r_mask01T=f(mask), r_posxi=f(posxi), r_poszeta=f(pze))
    return m


def fused_inputs(inp, core):
    b, q = core // 4, core % 4
    f = lambda a: np.ascontiguousarray(a, dtype=np.float32)
    m, _, _ = l1_inputs(inp, core)
    m.update(rw_params(inp, core))
    m.update(fn_params())
    m2 = l2_inputs(inp, core, None)
    g0 = gate_inputs(inp, b, 0)
    g1 = gate_inputs(inp, b, 1)
    perm = np.concatenate([np.concatenate([192 * r + np.arange(192), 768 + 64 * r + np.arange(64)]) for r in range(4)])
    m['wout0p'] = f(inp['ev_w_out'][0][perm])
    m['adawg0'], m['adabr0'], m['npostb0'] = g0['adawg'], g0['adabr'], g0['npostb']
    m['adawg1'], m['adabr1'], m['npostb1'] = g1['adawg'], g1['adabr'], g1['npostb']
    m['wout1'] = f(inp['od_w_out'][0])
    for k_ in ('adaw1', 'adab1', 'normw1', 'win1', 'ropec', 'ropes', 'r_logit', 'r_diffT', 'r_mask01T', 'r_posxi', 'r_poszeta'):
        m[k_] = m2[k_]
    return m


def kernel(**inputs):
    inp = {k: np.asarray(v) for k, v in inputs.items()}
    cores = list(range(8))
    nc, kb = build_fused()
    kb.emit()
    maps = [fused_inputs(inp, core) for core in cores]
    res = run_bass_kernel_spmd(nc, maps, core_ids=cores).results
    return np.stack([res[0]['out'], res[4]['out']]).astype(np.float32)
```

```python
import numpy as np
import concourse.bass as bass
import concourse.mybir as mybir
from contextlib import ExitStack
from concourse.bass_utils import run_bass_kernel_spmd

F32 = mybir.dt.float32
BF16 = mybir.dt.bfloat16
ALU = mybir.AluOpType
AF = mybir.ActivationFunctionType

ENGS = ['pe', 'act', 'dve', 'pool', 'sp']
DBG = {}
NDSEM = 8


class Buf:
    __slots__ = ('name', 'w', 'r', 'excl', 'strict')

    def __init__(self, name='', excl=False, strict=False):
        self.name = name
        self.w = None
        self.r = {}
        self.excl = excl
        self.strict = strict


class Op:
    __slots__ = ('eng', 'fn', 'deps', 'signal', 'sig', 'sem', 'target', 'isdma', 'final', 'cc')


class _Rec:
    def __init__(self):
        self.call = None

    def __getattr__(self, name):
        def f(*a, **k):
            self.call = (name, a, k)
            return self
        return f


def _flat(xs):
    out = []
    for x in xs:
        if isinstance(x, (list, tuple)):
            out.extend(_flat(x))
        else:
            out.append(x)
    return out


class KB:
    def __init__(self, nc):
        self.nc = nc
        self.ops = {e: [] for e in ENGS}
        self.phase = Buf('phase')
        self.es = ExitStack()
        self.nops = 0

    def sb(self, name, shape, dtype, es=None):
        self.nalloc = getattr(self, 'nalloc', 0) + 1
        return (es or self.es).enter_context(self.nc.sbuf_tensor('s%d_%s' % (self.nalloc, name), list(shape), dtype))

    def ps(self, name, shape, dtype=F32, es=None):
        self.nalloc = getattr(self, 'nalloc', 0) + 1
        return (es or self.es).enter_context(self.nc.psum_tensor('p%d_%s' % (self.nalloc, name), list(shape), dtype))

    def op(self, eng, fn, reads=(), writes=(), dma=False, final=False, nophase=False, cc=False):
        dma = dma or cc
        o = Op()
        rec = _Rec()
        fn(rec)
        assert rec.call is not None
        o.eng = eng; o.fn = rec.call; o.isdma = dma; o.signal = dma; o.deps = []; o.sig = 0
        o.sem = None; o.target = 0; o.final = final; o.cc = cc
        reads = _flat(reads)
        writes = _flat(writes)
        for b in list(reads):
            if b.excl:
                reads.remove(b)
                if b not in writes:
                    writes.append(b)
        if not nophase:
            reads.append(self.phase)
        deps = {}
        sdeps = set()
        for b in reads:
            if b.w is not None:
                deps[id(b.w)] = b.w
                if b.strict:
                    sdeps.add(id(b.w))
        for b in writes:
            if b.w is not None:
                deps[id(b.w)] = b.w
                if b.strict:
                    sdeps.add(id(b.w))
            for r in b.r.values():
                deps[id(r)] = r
                if b.strict:
                    sdeps.add(id(r))
        for d in deps.values():
            if d is o:
                continue
            if d.isdma or dma or d.eng != eng or id(d) in sdeps or (eng != 'pe' and DBG.get('strict_all', True)):
                o.deps.append(d)
                d.signal = True
        key = ('dma', self.nops) if dma else eng
        for b in reads:
            b.r[key] = o
        for b in writes:
            b.w = o
            b.r = {}
        self.ops[eng].append(o)
        self.nops += 1
        return o

    def barrier(self, tile_ap):
        self.op('dve', lambda e: e.memset(tile_ap, 0.0), writes=[self.phase], nophase=True)

    def emit(self):
        nc = self.nc
        es = self.es
        csem = {}
        for e in ['pe', 'act', 'dve', 'pool']:
            csem[e] = es.enter_context(nc.semaphore('c_' + e))
        dsem = {}
        for e in ENGS:
            if any(o.isdma for o in self.ops[e]):
                dsem[e] = [es.enter_context(nc.semaphore('d_%s%d' % (e, i))) for i in range(NDSEM)]
        for e in ENGS:
            cnt = 0
            nd = 0
            hist = []
            for o in self.ops[e]:
                if o.cc:
                    self.ncc = getattr(self, 'ncc', 0) + 1
                    o.sem = es.enter_context(nc.semaphore('ccs%d' % self.ncc))
                    o.target = 1
                elif o.isdma:
                    slot = nd % NDSEM
                    o.sem = dsem[e][slot]
                    o.target = 16 * (nd // NDSEM + 1)
                    if nd >= NDSEM:
                        o.deps.append(hist[nd - NDSEM])
                    hist.append(o)
                    nd += 1
                elif o.signal:
                    cnt += 1
                    o.sig = cnt
                    o.sem = csem[e]
                    o.target = cnt
        finals = [o for e in ENGS for o in self.ops[e] if o.final]

        def run(engname):
            def body(e):
                waited = {}
                for o in self.ops[engname]:
                    need = {}
                    for d in o.deps:
                        k = id(d.sem)
                        if waited.get(k, 0) < d.target and need.get(k, (None, 0))[1] < d.target:
                            need[k] = (d.sem, d.target)
                    for k, (sem, tgt) in need.items():
                        e.wait_ge(sem, tgt)
                        waited[k] = tgt
                    nm_, a_, k_ = o.fn
                    ins = getattr(e, nm_)(*a_, **k_)
                    if o.signal:
                        ins.then_inc(o.sem, 16 if (o.isdma and not o.cc) else 1)
                if engname == 'sp':
                    for d in finals:
                        k = id(d.sem)
                        if waited.get(k, 0) < d.target:
                            e.wait_ge(d.sem, d.target)
                            waited[k] = d.target
            return body

        with nc.Block() as block:
            block.tensor(run('pe'))
            block.scalar(run('act'))
            block.vector(run('dve'))
            block.gpsimd(run('pool'))
            block.sync(run('sp'))


class Ring:
    def __init__(self, kb, name, shape, dtype, n, psum=False, es=None, strict=False):
        self.n = n
        self.i = 0
        self.slots = []
        for j in range(n):
            t = (kb.ps if psum else kb.sb)('%s%d' % (name, j), shape, dtype, es=es)
            b = Buf('%s%d' % (name, j), excl=psum, strict=strict)
            self.slots.append((t, b))
            if not psum and DBG.get('init_rings', True):
                kb.op('pool', lambda e, t=t: e.memset(t[:], 0.0), writes=[b])

    def next(self):
        s = self.slots[self.i % self.n]
        self.i += 1
        return s


D = 1024
L = 8192
LC = 256
LT = L + LC
NCH = LT // 128
EPS = 1e-6
GN_EPS = 64e-5
FM_BLOCKS = {'r01': (0, 128), 'r2': (128, 64), 'k01': (192, 128), 'k2': (320, 64),
             'v01': (384, 128), 'v2': (512, 64), 'wl': (576, 128), 'al': (704, 128), 'four': (832, 64)}
NFM = 896
TILES = [(0, 256)] + [(256 + 512 * i, 512) for i in range(16)]


def dram_in(nc, name, shape, dtype=F32):
    return nc.dram_tensor(name, list(shape), dtype, kind="ExternalInput").ap()


def dram_out(nc, name, shape, dtype=F32):
    return nc.dram_tensor(name, list(shape), dtype, kind="ExternalOutput").ap()


def dram_tmp(nc, name, shape, dtype=F32, debug=False):
    return nc.dram_tensor(name, list(shape), dtype, kind="ExternalOutput" if debug else "Internal").ap()


def load_consts(kb, items, eng='sp'):
    bufs = []
    for t, src in items:
        b = Buf()
        kb.op(eng, (lambda e, t=t, src=src: e.dma_start(out=t[:], in_=src)), writes=[b], dma=True)
        bufs.append(b)
    return bufs


def adaln_vectors(kb, es, nc, sil_in, adaw, adab, normw, ncolblk, mod_tile=None):
    sil = kb.sb('sil', [128, 8, 2], F32, es)
    silb = Buf(strict=True)
    kb.op('sp', lambda e: e.dma_start(out=sil[:], in_=sil_in), writes=[silb], dma=True)
    kb.op('act', lambda e: e.activation(out=sil[:], in_=sil[:], func=AF.Silu), reads=[silb], writes=[silb])
    adb = kb.sb('adb', [128, ncolblk], F32, es)
    adbb = Buf(strict=True)
    kb.op('sp', lambda e: e.dma_start(out=adb[:], in_=adab), writes=[adbb], dma=True)
    mod = mod_tile if mod_tile is not None else kb.sb('mod', [128, ncolblk, 2], F32, es)
    modb = Buf(strict=True)
    psm = kb.ps('psmod', [128, ncolblk, 2], F32, es)
    psb = Buf(excl=True)
    wt = kb.sb('adaw', [128, 8, ncolblk * 128], F32, es)
    wb = Buf()
    for k in range(8):
        kb.op('sp', lambda e, k=k: e.dma_start(out=wt[:, k, :], in_=adaw[k * 128:(k + 1) * 128, :]), writes=[wb], dma=True)
    for cb in range(ncolblk):
        for k in range(8):
            kb.op('pe', lambda e, k=k, cb=cb: e.matmul(psm[:, cb, :], lhsT=wt[:, k, cb * 128:(cb + 1) * 128],
                                                      rhs=sil[:, k, :], start=(k == 0), stop=(k == 7)),
                  reads=[wb, silb], writes=[psb])
    for n in range(2):
        kb.op('dve', lambda e, n=n: e.tensor_tensor(out=mod[:, :, n], in0=psm[:, :, n], in1=adb[:, :], op=ALU.add),
              reads=[psb, adbb], writes=[modb])
    return mod, modb


def phase_proj(kb, nc, es, xin, Wsrc_list, G, shiftv, Gb, ident_bf, identb, emit_tile, nmod=2, get_x=None, shift_off=0):
    xring = Ring(kb, 'xt', [128, D], F32, 3, es=es)
    xnring = Ring(kb, 'xn', [128, D], BF16, 2, es=es)
    stat = Ring(kb, 'stat', [128, 4], F32, 4, es=es, strict=True)
    junk = kb.sb('junk', [128, D], BF16, es)
    junkb = Buf()
    hring = Ring(kb, 'hT', [128, 8, 512], BF16, 2, es=es)
    ptr = Ring(kb, 'ptr', [128, 8, 128], BF16, 2, psum=True, es=es)
    for (t0, T) in TILES:
        n = 1 if t0 < LC else 0
        hT, hb = hring.next()
        for sub in range(T // 128):
            r0 = t0 + sub * 128
            if get_x is not None:
                xt, xb = get_x(r0)
            else:
                xt, xb = xring.next()
                kb.op('sp', lambda e, xt=xt, r0=r0: e.dma_start(out=xt[:], in_=xin[r0:r0 + 128, :]), writes=[xb], dma=True)
            st, sb_ = stat.next()
            kb.op('act', lambda e, xt=xt, st=st: e.activation(out=junk[:], in_=xt[:], func=AF.Square, accum_out=st[:, 0:1]),
                  reads=[xb], writes=[junkb, sb_])
            kb.op('act', lambda e, st=st: e.activation(out=st[:, 1:2], in_=st[:, 0:1], func=AF.Sqrt, scale=1.0 / D, bias=Gb['eps'][:, 0:1]),
                  reads=[sb_, Gb['epsb']], writes=[sb_])
            kb.op('dve', lambda e, st=st: e.reciprocal(out=st[:, 2:3], in_=st[:, 1:2]), reads=[sb_], writes=[sb_])
            xn, xnb = xnring.next()
            kb.op('dve', lambda e, xn=xn, xt=xt, st=st: e.tensor_scalar(out=xn[:], in0=xt[:], scalar1=st[:, 2:3], scalar2=None, op0=ALU.mult),
                  reads=[xb, sb_], writes=[xnb])
            pt, pb = ptr.next()
            for k in range(8):
                kb.op('pe', lambda e, pt=pt, xn=xn, k=k: e.transpose(out=pt[:, k, :], in_=xn[:, k * 128:(k + 1) * 128], identity=ident_bf[:]),
                      reads=[xnb, identb], writes=[pb])
            for k in range(8):
                if k % 2 == 0:
                    kb.op('act', lambda e, pt=pt, hT=hT, k=k, sub=sub, n=n: e.activation(
                        out=hT[:, k, sub * 128:(sub + 1) * 128], in_=pt[:, k, :], func=AF.Identity,
                        scale=G[:, k, n:n + 1], bias=shiftv[:, shift_off + k, n:n + 1]), reads=[pb, Gb['G']], writes=[hb])
                else:
                    kb.op('dve', lambda e, pt=pt, hT=hT, k=k, sub=sub, n=n: e.tensor_scalar(
                        out=hT[:, k, sub * 128:(sub + 1) * 128], in0=pt[:, k, :], scalar1=G[:, k, n:n + 1],
                        scalar2=shiftv[:, shift_off + k, n:n + 1], op0=ALU.mult, op1=ALU.add), reads=[pb, Gb['G']], writes=[hb])
        emit_tile(t0, T, hT, hb)


def load_weight_bf16(kb, es, nc, name, src, ncols, es_keep):
    wbf = kb.sb(name, [128, 8, ncols], BF16, es_keep)
    wb = Buf()
    stg = Ring(kb, name + '_stg', [128, ncols], F32, 2, es=es)
    for k in range(8):
        s, sbuf_ = stg.next()
        kb.op('sp', lambda e, s=s, k=k: e.dma_start(out=s[:], in_=src[k * 128:(k + 1) * 128, :]), writes=[sbuf_], dma=True)
        eng = 'dve' if k % 2 == 0 else 'pool'
        kb.op(eng, lambda e, s=s, k=k: e.tensor_copy(out=wbf[:, k, :], in_=s[:]), reads=[sbuf_], writes=[wb])
    return wbf, wb


RW_PARAMS = {'mu': [128, 8], 'lanem': [128, 6], 'k_k': [128, 2], 'k_a': [128, 2], 'r_k': [128, 2], 'a0': [128, 2, 2],
             'w2': [128, 192], 'a2': [128, 192], 'w0': [1, 2, 192], 'maskSI': [128, 2, 256], 'maskSI2': [128, 2, 256], 'maskST': [128, 2, 3, 128],
             'tri': [128, 2, 256], 'ehead': [128, 2, 4], 'ones_bd': [128, 128], 'lnx_g': [128, 192], 'lnx_b': [128, 192]}


def part1(kb, nc, IN, YD, C):
    es = kb.es
    debug = False
    stop_after = None
    xin, sil_in, adaw, adab, normw, wfm, wg, cs64 = (IN[k_] for k_ in ('xin', 'sil_in', 'adaw', 'adab', 'normw', 'wfm', 'wg', 'cs64'))
    U = {nm: dram_tmp(nc, 'U_' + nm, [sz, LT], F32, debug) for nm, (off, sz) in FM_BLOCKS.items() if nm != 'four'}
    GT = dram_tmp(nc, 'GT', [LT, 256], F32, debug)
    HT = dram_tmp(nc, 'HT', [LT, 128], F32, debug)
    bar, ident_f, identfb, ident_bf, identb, epsT, epsb = C['bar'], C['ident_f'], C['identfb'], C['ident_bf'], C['identb'], C['epsT'], C['epsb']
    with ExitStack() as pa:
        mod, modb = adaln_vectors(kb, pa, nc, sil_in, adaw, adab, normw, 16)
        nw = kb.sb('nw', [128, 8], F32, pa)
        nwb, = load_consts(kb, [(nw, normw)])
        G = kb.sb('G', [128, 8, 2], F32, pa)
        Gbuf = Buf(strict=True)
        for n in range(2):
            kb.op('dve', lambda e, n=n: e.scalar_tensor_tensor(out=G[:, :, n], in0=mod[:, 8:16, n], scalar=1.0, in1=nw[:, :],
                                                                op0=ALU.add, op1=ALU.mult), reads=[modb, nwb], writes=[Gbuf])
        Gb = {'G': Gbuf, 'eps': epsT, 'epsb': epsb}
        shiftv = mod
        wbf, wbb = load_weight_bf16(kb, pa, nc, 'wfm_bf', wfm, NFM, pa)
        wgb, wgbb = load_weight_bf16(kb, pa, nc, 'wg_bf', wg, 256, pa)
        cs = kb.sb('cs64', [64, 128], F32, pa)
        csb, = load_consts(kb, [(cs, cs64)])
        pmm = Ring(kb, 'pmm', [128, 512], F32, 3, psum=True, es=pa)
        pg = Ring(kb, 'pg', [128, 512], F32, 1, psum=True, es=pa)
        ustg = Ring(kb, 'ustg', [128, 512], F32, 4, es=pa)
        gstg = Ring(kb, 'gstg', [128, 256], F32, 3, es=pa)
        hstg = Ring(kb, 'hstg', [128, 128], F32, 3, es=pa)
        fstg = Ring(kb, 'fstg', [64, 512], F32, 2, es=pa)
        cnt = [0]

        def emit_tile(t0, T, hT, hb):
            for nm, (off, sz) in FM_BLOCKS.items():
                ps, psb = pmm.next()
                for k in range(8):
                    kb.op('pe', lambda e, ps=ps, k=k, off=off, sz=sz: e.matmul(ps[0:sz, 0:T], lhsT=wbf[:, k, off:off + sz], rhs=hT[:, k, 0:T],
                                                                               start=(k == 0), stop=(k == 7)), reads=[wbb, hb], writes=[psb])
                if nm == 'four':
                    fs, fsb = fstg.next()
                    kb.op('act', lambda e, fs=fs, ps=ps: e.activation(out=fs[:, 0:T], in_=ps[0:64, 0:T], func=AF.Copy), reads=[psb], writes=[fsb])
                    for sub in range(T // 128):
                        pgt, pgb = pg.next()
                        kb.op('pe', lambda e, pgt=pgt, fs=fs, sub=sub: e.matmul(pgt[:, 0:128], lhsT=fs[:, sub * 128:(sub + 1) * 128], rhs=cs[:, :],
                                                                                  start=True, stop=True), reads=[fsb, csb], writes=[pgb])
                        hs, hsb = hstg.next()
                        kb.op('dve', lambda e, hs=hs, pgt=pgt: e.tensor_copy(out=hs[:], in_=pgt[:, 0:128]), reads=[pgb], writes=[hsb])
                        r0 = t0 + sub * 128
                        kb.op('pool', lambda e, hs=hs, r0=r0: e.dma_start(out=HT[r0:r0 + 128, :], in_=hs[:]), reads=[hsb], dma=True)
                else:
                    us, usb = ustg.next()
                    cnt[0] += 1
                    if cnt[0] % 2 == 0:
                        kb.op('act', lambda e, us=us, ps=ps, sz=sz: e.activation(out=us[0:sz, 0:T], in_=ps[0:sz, 0:T], func=AF.Copy), reads=[psb], writes=[usb])
                    else:
                        kb.op('dve', lambda e, us=us, ps=ps, sz=sz: e.tensor_copy(out=us[0:sz, 0:T], in_=ps[0:sz, 0:T]), reads=[psb], writes=[usb])
                    kb.op('pool', lambda e, us=us, nm=nm, sz=sz: e.dma_start(out=U[nm][:, t0:t0 + T], in_=us[0:sz, 0:T]), reads=[usb], dma=True)
            for sub in range(T // 128):
                pgt, pgb = pg.next()
                for k in range(8):
                    kb.op('pe', lambda e, pgt=pgt, k=k, sub=sub: e.matmul(pgt[:, 0:256], lhsT=hT[:, k, sub * 128:(sub + 1) * 128], rhs=wgb[:, k, :],
                                                                           start=(k == 0), stop=(k == 7)), reads=[wgbb, hb], writes=[pgb])
                gs, gsb = gstg.next()
                kb.op('act', lambda e, gs=gs, pgt=pgt: e.activation(out=gs[:], in_=pgt[:, 0:256], func=AF.Silu), reads=[pgb], writes=[gsb])
                r0 = t0 + sub * 128
                kb.op('pool', lambda e, gs=gs, r0=r0: e.dma_start(out=GT[r0:r0 + 128, :], in_=gs[:]), reads=[gsb], dma=True)

        phase_proj(kb, nc, pa, xin, None, G, shiftv, Gb, ident_bf, identb, emit_tile)
        kb.barrier(bar[:])
    BLm = ['r01', 'r2', 'k01', 'k2', 'v01', 'v2', 'wl', 'al']
    S = {nm: dram_tmp(nc, 'S_' + nm, [FM_BLOCKS[nm][1], LT], F32, debug) for nm in BLm}
    with ExitStack() as pm:
        mu_t = kb.sb('mu_m', [128, 8], F32, pm)
        lan_t = kb.sb('lan_m', [128, 6], F32, pm)
        mub_, lanb_ = load_consts(kb, [(mu_t, IN['p_mu']), (lan_t, IN['p_lanem'])])
        cf = kb.sb('coef_m', [128, 8, 7], F32, pm)
        cfb = Buf(strict=True)
        kb.op('dve', lambda e: e.tensor_scalar(out=cf[:, :, 0], in0=mu_t[:, :], scalar1=-1.0, scalar2=1.0, op0=ALU.mult, op1=ALU.add), reads=[mub_], writes=[cfb])
        for l in range(6):
            kb.op('dve', lambda e, l=l: e.tensor_scalar(out=cf[:, :, 1 + l], in0=mu_t[:, :], scalar1=lan_t[:, l:l + 1], scalar2=None, op0=ALU.mult), reads=[mub_, lanb_], writes=[cfb])
        wr = Ring(kb, 'winm', [128, 640], F32, 4, es=pm)
        so = Ring(kb, 'smo', [128, 512], F32, 4, es=pm)
        for (t0, T) in TILES:
            isctx = t0 < LC
            seg_lo, seg_hi = (0, LC) if isctx else (LC, LT)
            halo = 1 if isctx else 64
            lo = max(t0 - halo, seg_lo)
            hi = min(t0 + T + halo, seg_hi)
            for bi, nm in enumerate(BLm):
                sz = FM_BLOCKS[nm][1]
                w_, wb_ = wr.next()
                if lo > t0 - halo or hi < t0 + T + halo:
                    kb.op('pool', lambda e, w_=w_: e.memset(w_[:], 0.0), writes=[wb_])
                kb.op('sp', lambda e, w_=w_, nm=nm, sz=sz, lo=lo, hi=hi, t0=t0: e.dma_start(out=w_[0:sz, 64 + lo - t0:64 + hi - t0], in_=U[nm][:, lo:hi]), writes=[wb_], dma=True)
                o_, ob_ = so.next()
                kb.op('dve', lambda e, o_=o_, w_=w_, sz=sz, bi=bi, T=T: e.tensor_scalar(out=o_[0:sz, 0:T], in0=w_[0:sz, 64:64 + T], scalar1=cf[0:sz, bi, 0:1], scalar2=None, op0=ALU.mult),
                      reads=[wb_, cfb], writes=[ob_])
                terms = [(5, -1, None), (6, +1, None)] if isctx else [(1, -1, 'L'), (2, +1, 'R'), (3, -64, None), (4, +64, None)]
                for (ci, off, kind) in terms:
                    def f(e, o_=o_, w_=w_, sz=sz, bi=bi, ci=ci, off=off, kind=kind, T=T):
                        if kind is None:
                            oo = o_[0:sz, 0:T]
                            ii = w_[0:sz, 64 + off:64 + T + off]
                        else:
                            ov = o_[0:sz, 0:T].rearrange("p (r c) -> p r c", c=64)
                            iv = w_[0:sz, 64:64 + T].rearrange("p (r c) -> p r c", c=64)
                            if kind == 'L':
                                oo, ii = ov[:, :, 1:64], iv[:, :, 0:63]
                            else:
                                oo, ii = ov[:, :, 0:63], iv[:, :, 1:64]
                        return e.scalar_tensor_tensor(out=oo, in0=ii, scalar=cf[0:sz, bi, ci:ci + 1], in1=oo, op0=ALU.mult, op1=ALU.add)
                    kb.op('dve', f, reads=[wb_, cfb, ob_], writes=[ob_])
                kb.op('pool', lambda e, o_=o_, nm=nm, sz=sz, t0=t0, T=T: e.dma_start(out=S[nm][:, t0:t0 + T], in_=o_[0:sz, 0:T]), reads=[ob_], dma=True)
        kb.barrier(bar[:])
    if stop_after == 'A':
        return
    with ExitStack() as pb:
        P = {k_: IN['p_' + k_] for k_ in RW_PARAMS}
        YR = YD
        if stop_after != 'skipB':
            phase_rwkv(kb, nc, pb, S, GT, YR, P, ident_f, identfb, ident_bf, identb)
        kb.barrier(bar[:])
    if DBG.get('skip_fnet'):
        return
    PF = {k_: IN['f_' + k_] for k_ in FN_PARAMS}
    YF = YD
    ZS = dram_tmp(nc, 'ZS', [2, 128, 64, 128], F32, False)
    phase_fnet(kb, nc, HT, GT, YF, ZS, PF, bar)
    return


DUMPS = {}


def dump(kb, nc, name, ap, shape, bufs, dt=F32):
    if name in DUMPS:
        return
    t = nc.dram_tensor('dbg_' + name, list(shape), dt, kind="ExternalOutput").ap()
    DUMPS[name] = t
    kb.op('sp', lambda e: e.dma_start(out=t, in_=ap), reads=bufs, dma=True, final=True)

CHUNK_ORDER = {0: list(range(NCH)), 1: [1, 0] + list(range(NCH - 1, 1, -1))}


def phase_rwkv(kb, nc, es, U, GT, YR, P, ident_f, identfb, ident_bf, identb):
    def sbt(name, shape, dt=F32):
        return kb.sb(name, shape, dt, es)

    def const(name, shape, src, dt=F32):
        t = sbt(name, shape, dt)
        b, = load_consts(kb, [(t, src)])
        return t, b

    mu, mub = const('mu', [128, 8], P['mu'])
    lanem, lanemb = const('lanem', [128, 6], P['lanem'])
    kkp, kkpb = const('kkp', [128, 2], P['k_k'])
    kap, kapb = const('kap', [128, 2], P['k_a'])
    rkp, rkpb = const('rkp', [128, 2], P['r_k'])
    a0p, a0pb = const('a0p', [128, 2, 2], P['a0'])
    w2s, w2sb = const('w2s', [128, 192], P['w2'])
    a2s, a2sb = const('a2s', [128, 192], P['a2'])
    w0r, w0rb = const('w0r', [1, 2, 192], P['w0'])
    msi, msib = const('msi', [128, 2, 256], P['maskSI'])
    mst, mstb = const('mst', [128, 2, 3, 128], P['maskST'])
    msi2, msi2b = const('msi2', [128, 2, 256], P['maskSI2'])
    tri, trib = const('tri', [128, 2, 256], P['tri'])
    ehd, ehdb = const('ehd', [128, 2, 4], P['ehead'])
    obd, obdb = const('obd', [128, 128], P['ones_bd'])
    lng, lngb = const('lng', [128, 192], P['lnx_g'])
    lnb, lnbb = const('lnb', [128, 192], P['lnx_b'])
    ones1 = sbt('ones1', [1, 128])
    ones1b = Buf()
    kb.op('dve', lambda e: e.memset(ones1[:], 1.0), writes=[ones1b])
    coef = sbt('coef', [128, 8, 7])
    coefb = Buf(strict=True)
    kb.op('dve', lambda e: e.tensor_scalar(out=coef[:, :, 0], in0=mu[:, :], scalar1=-1.0, scalar2=1.0, op0=ALU.mult, op1=ALU.add),
          reads=[mub], writes=[coefb])
    for l in range(6):
        kb.op('dve', lambda e, l=l: e.tensor_scalar(out=coef[:, :, 1 + l], in0=mu[:, :], scalar1=lanem[:, l:l + 1], scalar2=None, op0=ALU.mult),
              reads=[mub, lanemb], writes=[coefb])
    omka = sbt('omka', [128, 2])
    omkab = Buf(strict=True)
    kb.op('dve', lambda e: e.tensor_scalar(out=omka[:], in0=kap[:], scalar1=-1.0, scalar2=1.0, op0=ALU.mult, op1=ALU.add), reads=[kapb], writes=[omkab])
    gneps = sbt('gneps', [128, 1])
    gnepsb = Buf()
    kb.op('dve', lambda e: e.memset(gneps[:], GN_EPS), writes=[gnepsb])

    yacc = sbt('yacc', [128, NCH, 192])
    yaccb = [Buf() for _ in range(NCH)]
    kb.op('pool', lambda e: e.memset(yacc[:], 0.0), writes=yaccb)
    ST = sbt('ST', [128, 2, 2, 64])
    STbf = sbt('STbf', [128, 2, 2, 64], BF16)
    STb = [[Buf() for _ in range(3)] for _ in range(2)]

    CR, HR = {}, {}
    for d_ in range(2):
        n_ = 'd%d' % d_
        CR[d_] = dict(
            winr=None, smr=Ring(kb, 'smix' + n_, [128, 8, 128], F32, 2, es=es),
            t32=Ring(kb, 't32' + n_, [128, 256], F32, 9, es=es), kkr=Ring(kb, 'kkt' + n_, [128, 2, 128], F32, 1, es=es),
            vtr=Ring(kb, 'vt32' + n_, [128, 192], F32, 2, es=es), vtbr=Ring(kb, 'vtbf' + n_, [128, 192], BF16, 2, es=es),
            sgr=Ring(kb, 'sg' + n_, [128, 192], F32, 1, es=es), e1r=Ring(kb, 'e1' + n_, [128, 2, 256], F32, 2, es=es),
            e2r=Ring(kb, 'e2' + n_, [128, 2, 128], F32, 1, es=es), arr=Ring(kb, 'ar' + n_, [128, 2, 256], BF16, 2, es=es),
            btr=Ring(kb, 'bt' + n_, [128, 2, 128], BF16, 2, es=es), ktr=Ring(kb, 'kt' + n_, [128, 2, 128], BF16, 2, es=es),
            bktr=Ring(kb, 'bkt' + n_, [128, 2, 192], BF16, 2, es=es), bcr=Ring(kb, 'bc' + n_, [128, 4], F32, 2, es=es, strict=True))
        for h_ in range(3):
            n2 = 'd%dh%d' % (d_, h_)
            HR[(d_, h_)] = dict(
                mm1r=Ring(kb, 'mm1' + n2, [128, 256], BF16, 2, es=es), mm2r=Ring(kb, 'mm2' + n2, [128, 256], BF16, 2, es=es),
                xpr=Ring(kb, 'xp' + n2, [128, 256], BF16, 4, es=es), xtr=Ring(kb, 'xt' + n2, [128, 128], BF16, 4, es=es),
                ivr=Ring(kb, 'iv' + n2, [128, 128], BF16, 8, es=es), tmr=Ring(kb, 'tm' + n2, [128, 128], BF16, 2, es=es),
                w1r=Ring(kb, 'w1' + n2, [128, 64], BF16, 2, es=es), utr=Ring(kb, 'ut' + n2, [128, 64], BF16, 2, es=es),
                tsr=Ring(kb, 'ts' + n2, [128, 64], F32, 3, es=es))
    jkr = Ring(kb, 'jk', [128, 64], F32, 2, es=es)
    gtr = Ring(kb, 'gt', [128, 192], F32, 2, es=es)
    fnr = Ring(kb, 'fn', [128, 192], F32, 2, es=es)
    str_ = Ring(kb, 'stt', [128, 8], F32, 6, es=es, strict=True)
    pprep = Ring(kb, 'pprep', [128, 512], F32, 2, psum=True, es=es)
    ptrb = kb.ps('ptrb', [128, 1024], BF16, es)
    ptrbufs = [Buf(excl=True)] * 4
    ptrc = [0]
    pgram = Ring(kb, 'pgram', [128, 512], F32, 1, psum=True, es=es)
    pinv = Ring(kb, 'pinv', [128, 512], F32, 2, psum=True, es=es)
    pseq = kb.ps('pseq', [128, 512], F32, es)
    pseq2 = kb.ps('pseq2', [128, 512], F32, es)
    pseqb = [Buf(excl=True)] * 4 + [Buf(excl=True)] * 4
    pseqt = [pseq] * 4 + [pseq2] * 4
    pseqc = [0]

    def pseq_next():
        i = pseqc[0] % 8
        pseqc[0] += 1
        return pseqt[i][:, (i % 4) * 64:(i % 4 + 1) * 64], pseqb[i], i

    def ptr_next():
        i = ptrc[0] % 4
        ptrc[0] += 1
        return i, ptrbufs[i]

    blkname = [('r01', 'k01', 'v01'), ('r2', 'k2', 'v2')]
    BL = ['r01', 'r2', 'k01', 'k2', 'v01', 'v2', 'wl', 'al']
    BI = {n: i for i, n in enumerate(BL)}
    BSZ = {n: FM_BLOCKS[n][1] for n in BL}
    alt = [0]

    def ew(fn, reads, writes):
        alt[0] += 1
        r_ = _Rec()
        fn(r_)
        stt_ = r_.call[0] == 'scalar_tensor_tensor'
        return kb.op('dve' if (alt[0] % 3 or stt_) else 'pool', fn, reads=reads, writes=writes)

    done = {}
    inflight = {}
    SMB = {}

    def chunk_body(d, c):
        winr, smr, t32, kkr, vtr, vtbr, sgr, e1r, e2r, arr, btr, ktr, bktr, bcr = (CR[d][k_] for k_ in (
            'winr', 'smr', 't32', 'kkr', 'vtr', 'vtbr', 'sgr', 'e1r', 'e2r', 'arr', 'btr', 'ktr', 'bktr', 'bcr'))
        isctx = c < 2
        t0 = c * 128
        seg_lo, seg_hi = (0, LC) if isctx else (LC, LT)
        halo = 1 if isctx else 64
        lo = max(t0 - halo, seg_lo)
        hi = min(t0 + 128 + halo, seg_hi)
        sm, smb0 = smr.next()
        smb = SMB.setdefault(id(smb0), [smb0] + [Buf() for _ in range(7)])
        for nm in BL:
            sz = BSZ[nm]
            kb.op('sp', lambda e, sm=sm, nm=nm, sz=sz, t0=t0: e.dma_start(out=sm[0:sz, BI[nm], :], in_=U[nm][:, t0:t0 + 128]), writes=[smb[BI[nm]]], dma=True)
        if DBG.get('stage', 99) < 2:
            return

        def S(nm):
            return sm[0:BSZ[nm], BI[nm], :]

        yield
        kkt, kktb = kkr.next()
        for bj, (rn, kn, vn) in enumerate(blkname if not DBG.get('skip_kk') else []):
            sz = BSZ[kn]
            q, qb = t32.next()
            kb.op('dve', lambda e, q=q, kn=kn, sz=sz, bj=bj: e.tensor_scalar(out=q[0:sz, 0:128], in0=S(kn), scalar1=kkp[0:sz, bj:bj + 1], scalar2=None, op0=ALU.mult),
                  reads=[smb, kkpb], writes=[qb])
            kb.op('pool', lambda e, q=q, sz=sz: e.tensor_tensor(out=q[0:sz, 128:256], in0=q[0:sz, 0:128], in1=q[0:sz, 0:128], op=ALU.mult), reads=[qb], writes=[qb])
            pp, ppb = pprep.next()
            kb.op('pe', lambda e, pp=pp, q=q, sz=sz: e.matmul(pp[0:sz, 0:128], lhsT=obd[0:sz, 0:sz], rhs=q[0:sz, 128:256], start=True, stop=True),
                  reads=[qb, obdb], writes=[ppb])
            nr, nrb = t32.next()
            kb.op('act', lambda e, nr=nr, pp=pp, sz=sz: e.activation(out=nr[0:sz, 0:128], in_=pp[0:sz, 0:128], func=AF.Sqrt), reads=[ppb], writes=[nrb])
            kb.op('dve', lambda e, nr=nr, sz=sz: e.tensor_scalar(out=nr[0:sz, 0:128], in0=nr[0:sz, 0:128], scalar1=1e-12, scalar2=None, op0=ALU.max), reads=[nrb], writes=[nrb])
            kb.op('dve', lambda e, nr=nr, sz=sz: e.reciprocal(out=nr[0:sz, 128:256], in_=nr[0:sz, 0:128]), reads=[nrb], writes=[nrb])
            kb.op('dve', lambda e, nr=nr, q=q, sz=sz, bj=bj, kkt=kkt: e.tensor_tensor(out=kkt[0:sz, bj, :], in0=q[0:sz, 0:128], in1=nr[0:sz, 128:256], op=ALU.mult),
                  reads=[nrb, qb], writes=[kktb])
        vt, vtb = vtr.next()
        vtbf, vtbfb = vtbr.next()
        pp, ppb = pprep.next()
        for bj, (rn, kn, vn) in enumerate(blkname if not DBG.get('skip_vt') else []):
            sz = BSZ[vn]
            kb.op('pe', lambda e, pp=pp, vn=vn, sz=sz, bj=bj: e.transpose(out=pp[:, bj * 128:bj * 128 + sz], in_=S(vn), identity=ident_f[0:sz, 0:sz]),
                  reads=[smb, identfb], writes=[ppb])
        kb.op('act', lambda e, pp=pp, vt=vt: e.activation(out=vt[:, :], in_=pp[:, 0:192], func=AF.Copy), reads=[ppb], writes=[vtb])
        kb.op('pool', lambda e, vt=vt, vtbf=vtbf: e.tensor_copy(out=vtbf[:, :], in_=vt[:, :]), reads=[vtb], writes=[vtbfb])

        if DBG.get('stage', 99) < 3:
            return
        yield
        th, thb = t32.next()
        kb.op('act', lambda e, th=th: e.activation(out=th[d * 64:(d + 1) * 64, 0:128], in_=sm[d * 64:(d + 1) * 64, BI['wl'], :], func=AF.Tanh),
              reads=[smb], writes=[thb])
        pp, ppb = pprep.next()
        kb.op('pe', lambda e, pp=pp, th=th: e.matmul(pp[:, 0:192], lhsT=th[d * 64:(d + 1) * 64, 0:128], rhs=w2s[d * 64:(d + 1) * 64, :], start=True, stop=False),
              reads=[thb, w2sb], writes=[ppb])
        kb.op('pe', lambda e, pp=pp: e.matmul(pp[:, 0:192], lhsT=ones1[0:1, :], rhs=w0r[0:1, d, :], start=False, stop=True),
              reads=[ones1b, w0rb], writes=[ppb])
        sg, sgb = sgr.next()
        kb.op('act', lambda e, sg=sg, pp=pp: e.activation(out=sg[:, :], in_=pp[:, 0:192], func=AF.Sigmoid), reads=[ppb], writes=[sgb])
        e1, e1b = e1r.next()
        e2, e2b = e2r.next()
        for bj in range(2):
            sz = 128 if bj == 0 else 64
            pp, ppb = pprep.next()
            kb.op('pe', lambda e, pp=pp, sg=sg, bj=bj, sz=sz: e.matmul(pp[0:sz, 0:256], lhsT=sg[:, bj * 128:bj * 128 + sz], rhs=tri[:, d, :], start=True, stop=True),
                  reads=[sgb, trib], writes=[ppb])
            kb.op('act', lambda e, pp=pp, e1=e1, bj=bj, sz=sz: e.activation(out=e1[0:sz, bj, :], in_=pp[0:sz, 0:256], func=AF.Exp), reads=[ppb], writes=[e1b])
            kb.op('act', lambda e, pp=pp, e2=e2, bj=bj, sz=sz: e.activation(out=e2[0:sz, bj, :], in_=pp[0:sz, 0:128], func=AF.Exp, scale=-1.0), reads=[ppb], writes=[e2b])
        if DBG.get('stage', 99) < 4:
            return
        yield
        ar, arb = arr.next()
        bt, btb = btr.next()
        kt, ktb = ktr.next()
        bc, bcb = bcr.next()
        pbon, pbonb, _ = pseq_next()
        for bj, (rn, kn, vn) in enumerate(blkname):
            sz = BSZ[kn]
            co = bj * 128
            pp, ppb = pprep.next()
            kb.op('pe', lambda e, pp=pp, sz=sz, co=co: e.matmul(pp[0:sz, 0:128], lhsT=a2s[d * 64:(d + 1) * 64, co:co + sz], rhs=sm[d * 64:(d + 1) * 64, BI['al'], :], start=True, stop=True),
                  reads=[a2sb, smb], writes=[ppb])
            av, avb = t32.next()
            kb.op('act', lambda e, av=av, pp=pp, sz=sz, bj=bj: e.activation(out=av[0:sz, 0:128], in_=pp[0:sz, 0:128], func=AF.Sigmoid, bias=a0p[0:sz, d, bj:bj + 1]),
                  reads=[ppb, a0pb], writes=[avb])
            kv, kvb = t32.next()
            ew(lambda e, kv=kv, av=av, sz=sz, bj=bj: e.tensor_scalar(out=kv[0:sz, 0:128], in0=av[0:sz, 0:128], scalar1=kap[0:sz, bj:bj + 1], scalar2=omka[0:sz, bj:bj + 1], op0=ALU.mult, op1=ALU.add),
               [avb, kapb, omkab], [kvb])
            ew(lambda e, kv=kv, kn=kn, sz=sz: e.tensor_tensor(out=kv[0:sz, 0:128], in0=kv[0:sz, 0:128], in1=S(kn), op=ALU.mult), [kvb, smb], [kvb])
            ew(lambda e, kv=kv, kt=kt, e2=e2, sz=sz, bj=bj: e.tensor_tensor(out=kt[0:sz, bj, :], in0=kv[0:sz, 0:128], in1=e2[0:sz, bj, :], op=ALU.mult), [kvb, e2b], [ktb])
            ew(lambda e, av=av, e2=e2, sz=sz, bj=bj: e.tensor_tensor(out=av[0:sz, 128:256], in0=av[0:sz, 0:128], in1=e2[0:sz, bj, :], op=ALU.mult), [avb, e2b], [avb])
            ew(lambda e, av=av, bt=bt, kkt=kkt, sz=sz, bj=bj: e.tensor_tensor(out=bt[0:sz, bj, :], in0=av[0:sz, 128:256], in1=kkt[0:sz, bj, :], op=ALU.mult), [avb, kktb], [btb])
            ew(lambda e, ar=ar, kkt=kkt, e1=e1, sz=sz, bj=bj: e.scalar_tensor_tensor(out=ar[0:sz, bj, 0:128], in0=kkt[0:sz, bj, :], scalar=-1.0, in1=e1[0:sz, bj, 128:256], op0=ALU.mult, op1=ALU.mult),
               [kktb, e1b], [arb])
            ew(lambda e, ar=ar, rn=rn, e1=e1, sz=sz, bj=bj: e.tensor_tensor(out=ar[0:sz, bj, 128:256], in0=S(rn), in1=e1[0:sz, bj, 0:128], op=ALU.mult), [smb, e1b], [arb])
            ew(lambda e, kv=kv, rn=rn, sz=sz, bj=bj: e.scalar_tensor_tensor(out=kv[0:sz, 128:256], in0=S(rn), scalar=rkp[0:sz, bj:bj + 1], in1=kv[0:sz, 0:128], op0=ALU.mult, op1=ALU.mult),
               [smb, rkpb, kvb], [kvb])
            kb.op('pe', lambda e, kv=kv, sz=sz, bj=bj: e.matmul(pbon[:, 0:4], lhsT=kv[0:sz, 128:256], rhs=ehd[0:sz, bj, :], start=(bj == 0), stop=(bj == 1)),
                  reads=[kvb, ehdb], writes=[pbonb])
        kb.op('dve', lambda e, bc=bc: e.tensor_copy(out=bc[:, :], in_=pbon[:, 0:4]), reads=[pbonb], writes=[bcb])
        yield
        bkt, bktb = bktr.next()
        for wi, (src, srcb) in enumerate([(bt, btb), (kt, ktb)]):
            pi, pib = ptr_next()
            for bj in range(2):
                sz = 128 if bj == 0 else 64
                kb.op('pe', lambda e, src=src, pi=pi, bj=bj, sz=sz: e.transpose(out=ptrb[:, pi * 256 + bj * 128:pi * 256 + bj * 128 + sz], in_=src[0:sz, bj, :], identity=ident_bf[0:sz, 0:sz]),
                      reads=[srcb, identb], writes=[pib])
            kb.op('act' if wi == 0 else 'dve',
                  (lambda e, pi=pi, bkt=bkt, wi=wi: e.activation(out=bkt[:, wi, :], in_=ptrb[:, pi * 256:pi * 256 + 192], func=AF.Copy)) if wi == 0 else
                  (lambda e, pi=pi, bkt=bkt, wi=wi: e.tensor_copy(out=bkt[:, wi, :], in_=ptrb[:, pi * 256:pi * 256 + 192])),
                  reads=[pib], writes=[bktb])
        tcol = 127 if d == 0 else 0

        if DBG.get('stage', 99) < 5:
            return
        first = done.get(c, 0) == 0
        assert c not in inflight
        inflight[c] = d

        def head_body(h):
            mm1r, mm2r, xpr, xtr, ivr, tmr, w1r, utr, tsr = (HR[(d, h)][k_] for k_ in ('mm1r', 'mm2r', 'xpr', 'xtr', 'ivr', 'tmr', 'w1r', 'utr', 'tsr'))
            bj = h // 2
            base = (h % 2) * 64
            ch0 = 64 * h
            hp = slice(base, base + 64)
            yield
            pg1, pg1b = pgram.next()
            kb.op('pe', lambda e, pg1=pg1, hp=hp, bj=bj: e.matmul(pg1[:, 0:256], lhsT=bt[hp, bj, :], rhs=ar[hp, bj, :], start=True, stop=True), reads=[btb, arb], writes=[pg1b])
            mm1, mm1b = mm1r.next()
            kb.op('dve', lambda e, mm1=mm1, pg1=pg1: e.tensor_tensor(out=mm1[:, :], in0=pg1[:, 0:256], in1=msi[:, d, :], op=ALU.mult), reads=[pg1b, msib], writes=[mm1b])
            pg2, pg2b = pgram.next()
            kb.op('pe', lambda e, pg2=pg2, hp=hp, bj=bj: e.matmul(pg2[:, 0:256], lhsT=kt[hp, bj, :], rhs=ar[hp, bj, :], start=True, stop=True), reads=[ktb, arb], writes=[pg2b])
            kb.op('pe', lambda e, pg2=pg2, hp=hp, bj=bj: e.matmul(pg2[:, 256:384], lhsT=ar[hp, bj, 0:128], rhs=bt[hp, bj, :], start=True, stop=True), reads=[btb, arb], writes=[pg2b])
            mm2, mm2b = mm2r.next()
            kb.op('dve', lambda e, mm2=mm2, pg2=pg2: e.tensor_tensor(out=mm2[:, :], in0=pg2[:, 0:256], in1=msi2[:, d, :], op=ALU.mult), reads=[pg2b, msi2b], writes=[mm2b])
            xt, xtb = xtr.next()
            kb.op('dve', lambda e, xt=xt, pg2=pg2: e.tensor_tensor(out=xt[:, :], in0=pg2[:, 256:384], in1=mst[:, d, 0, :], op=ALU.mult), reads=[pg2b, mstb], writes=[xtb])
            e1t, e1tb = ivr.next()
            kb.op('dve', lambda e, e1t=e1t, pg2=pg2: e.tensor_tensor(out=e1t[:, :], in0=pg2[:, 256:384], in1=mst[:, d, 1, :], op=ALU.mult), reads=[pg2b, mstb], writes=[e1tb])
            e2t, e2tb = ivr.next()
            kb.op('dve', lambda e, e2t=e2t, pg2=pg2: e.tensor_tensor(out=e2t[:, :], in0=pg2[:, 256:384], in1=mst[:, d, 2, :], op=ALU.mult), reads=[pg2b, mstb], writes=[e2tb])
            if DBG.get('stage', 99) < 6:
                return
            yield
            xp, xpb = xpr.next()
            kb.op('pool', lambda e, xp=xp, mm1=mm1: e.tensor_tensor(out=xp[:, 128:256], in0=mm1[:, 0:128], in1=ident_bf[:, :], op=ALU.add), reads=[mm1b, identb], writes=[xpb])
            pv, pvb = pinv.next()
            kb.op('pe', lambda e, pv=pv, xt=xt, mm1=mm1: e.matmul(pv[:, 0:128], lhsT=xt[:, :], rhs=mm1[:, 0:128], start=True, stop=True), reads=[xtb, mm1b], writes=[pvb])
            kb.op('pe', lambda e, pv=pv, xt=xt, mm1=mm1: e.matmul(pv[:, 256:384], lhsT=mm1[:, 0:128], rhs=xt[:, :], start=True, stop=True), reads=[xtb, mm1b], writes=[pvb])
            xt2, xt2b = xtr.next()
            kb.op('act', lambda e, xp=xp, pv=pv: e.activation(out=xp[:, 0:128], in_=pv[:, 0:128], func=AF.Copy), reads=[pvb], writes=[xpb])
            kb.op('dve', lambda e, xt2=xt2, pv=pv: e.tensor_copy(out=xt2[:, :], in_=pv[:, 256:384]), reads=[pvb], writes=[xt2b])
            curxp, curxpb, curxt, curxtb = xp, xpb, xt2, xt2b
            for lev in range(1, 4):
                pv, pvb = pinv.next()
                kb.op('pe', lambda e, pv=pv, cx=curxp, ct=curxt: e.matmul(pv[:, 0:256], lhsT=ct[:, :], rhs=cx[:, 0:256], start=True, stop=True), reads=[curxpb, curxtb], writes=[pvb])
                kb.op('pe', lambda e, pv=pv, cx=curxp, ct=curxt: e.matmul(pv[:, 256:384], lhsT=cx[:, 0:128], rhs=ct[:, :], start=True, stop=True), reads=[curxpb, curxtb], writes=[pvb])
                nxp, nxpb = xpr.next()
                nxt, nxtb = xtr.next()
                kb.op('act', lambda e, nxp=nxp, pv=pv: e.activation(out=nxp[:, 0:128], in_=pv[:, 0:128], func=AF.Copy), reads=[pvb], writes=[nxpb])
                kb.op('dve', lambda e, nxp=nxp, pv=pv, cx=curxp: e.tensor_tensor(out=nxp[:, 128:256], in0=pv[:, 128:256], in1=cx[:, 128:256], op=ALU.add), reads=[pvb, curxpb], writes=[nxpb])
                kb.op('act', lambda e, nxt=nxt, pv=pv: e.activation(out=nxt[:, :], in_=pv[:, 256:384], func=AF.Copy), reads=[pvb], writes=[nxtb])
                curxp, curxpb, curxt, curxtb = nxp, nxpb, nxt, nxtb
                yield
            pv, pvb = pinv.next()
            kb.op('pe', lambda e, pv=pv, cx=curxp, ct=curxt: e.matmul(pv[:, 0:128], lhsT=ct[:, :], rhs=cx[:, 128:256], start=True, stop=True), reads=[curxpb, curxtb], writes=[pvb])
            t32m, t32mb = ivr.next()
            kb.op('dve', lambda e, t32m=t32m, pv=pv, cx=curxp: e.tensor_tensor(out=t32m[:, :], in0=pv[:, 0:128], in1=cx[:, 128:256], op=ALU.add), reads=[pvb, curxpb], writes=[t32mb])
            yield
            pi, pib = ptr_next()
            kb.op('pe', lambda e, pi=pi, t32m=t32m: e.transpose(out=ptrb[:, pi * 256:pi * 256 + 128], in_=t32m[:, :], identity=ident_bf[:, :]), reads=[t32mb, identb], writes=[pib])
            t32t, t32tb = ivr.next()
            kb.op('act', lambda e, pi=pi, t32t=t32t: e.activation(out=t32t[:, :], in_=ptrb[:, pi * 256:pi * 256 + 128], func=AF.Copy), reads=[pib], writes=[t32tb])
            yield
            pv, pvb = pinv.next()
            kb.op('pe', lambda e, pv=pv, e1t=e1t, t32m=t32m: e.matmul(pv[:, 0:128], lhsT=e1t[:, :], rhs=t32m[:, :], start=True, stop=True), reads=[e1tb, t32mb], writes=[pvb])
            z1, z1b = ivr.next()
            kb.op('act', lambda e, z1=z1, pv=pv: e.activation(out=z1[:, :], in_=pv[:, 0:128], func=AF.Copy), reads=[pvb], writes=[z1b])
            pv, pvb = pinv.next()
            kb.op('pe', lambda e, pv=pv, t32t=t32t, z1=z1: e.matmul(pv[:, 0:128], lhsT=t32t[:, :], rhs=z1[:, :], start=True, stop=True), reads=[t32tb, z1b], writes=[pvb])
            kb.op('pe', lambda e, pv=pv, t32t=t32t, z1=z1: e.matmul(pv[:, 256:384], lhsT=z1[:, :], rhs=t32t[:, :], start=True, stop=True), reads=[t32tb, z1b], writes=[pvb])
            t64, t64b = ivr.next()
            t64t, t64tb = ivr.next()
            kb.op('dve', lambda e, t64=t64, pv=pv, t32m=t32m: e.tensor_tensor(out=t64[:, :], in0=pv[:, 0:128], in1=t32m[:, :], op=ALU.add), reads=[pvb, t32mb], writes=[t64b])
            kb.op('dve', lambda e, t64t=t64t, pv=pv, t32t=t32t: e.tensor_tensor(out=t64t[:, :], in0=pv[:, 256:384], in1=t32t[:, :], op=ALU.add), reads=[pvb, t32tb], writes=[t64tb])
            yield
            pv, pvb = pinv.next()
            kb.op('pe', lambda e, pv=pv, e2t=e2t, t64=t64: e.matmul(pv[:, 0:128], lhsT=e2t[:, :], rhs=t64[:, :], start=True, stop=True), reads=[e2tb, t64b], writes=[pvb])
            z2, z2b = ivr.next()
            kb.op('act', lambda e, z2=z2, pv=pv: e.activation(out=z2[:, :], in_=pv[:, 0:128], func=AF.Copy), reads=[pvb], writes=[z2b])
            pv, pvb = pinv.next()
            kb.op('pe', lambda e, pv=pv, t64t=t64t, z2=z2: e.matmul(pv[:, 0:128], lhsT=t64t[:, :], rhs=z2[:, :], start=True, stop=True), reads=[t64tb, z2b], writes=[pvb])
            tm, tmb = tmr.next()
            kb.op('dve', lambda e, tm=tm, pv=pv, t64=t64: e.tensor_tensor(out=tm[:, :], in0=pv[:, 0:128], in1=t64[:, :], op=ALU.add), reads=[pvb, t64b], writes=[tmb])
            if DBG.get('dump') == (d, c) and h == 0:
                dump(kb, nc, 'mm1', mm1[:, :], [128, 256], [mm1b], BF16)
                dump(kb, nc, 'mm2', mm2[:, :], [128, 256], [mm2b], BF16)
                dump(kb, nc, 'tm', tm[:, :], [128, 128], [tmb], BF16)
            yield
            stb = STb[d][h]
            p1, p1b, _ = pseq_next()
            kb.op('pe', lambda e, p1=p1, hp=hp, bj=bj: e.matmul(p1, lhsT=ar[hp, bj, 0:128], rhs=STbf[hp, d, bj, :], start=True, stop=False), reads=[arb, stb], writes=[p1b])
            kb.op('pe', lambda e, p1=p1, mm2=mm2, ch0=ch0: e.matmul(p1, lhsT=mm2[:, 0:128], rhs=vtbf[:, ch0:ch0 + 64], start=False, stop=True), reads=[mm2b, vtbfb], writes=[p1b])
            w1, w1b = w1r.next()
            kb.op('act', lambda e, w1=w1, p1=p1: e.activation(out=w1[:, :], in_=p1, func=AF.Copy), reads=[p1b], writes=[w1b])
            p2, p2b, _ = pseq_next()
            kb.op('pe', lambda e, p2=p2, tm=tm, w1=w1: e.matmul(p2, lhsT=tm[:, :], rhs=w1[:, :], start=True, stop=True), reads=[tmb, w1b], writes=[p2b])
            ut, utb = utr.next()
            kb.op('act', lambda e, ut=ut, p2=p2: e.activation(out=ut[:, :], in_=p2, func=AF.Copy), reads=[p2b], writes=[utb])
            p3, p3b, _ = pseq_next()
            kb.op('pe', lambda e, p3=p3, hp=hp, bj=bj: e.matmul(p3, lhsT=ar[hp, bj, 128:256], rhs=STbf[hp, d, bj, :], start=True, stop=False), reads=[arb, stb], writes=[p3b])
            kb.op('pe', lambda e, p3=p3, mm1=mm1, ut=ut: e.matmul(p3, lhsT=mm1[:, 128:256], rhs=ut[:, :], start=False, stop=False), reads=[mm1b, utb], writes=[p3b])
            kb.op('pe', lambda e, p3=p3, mm2=mm2, ch0=ch0: e.matmul(p3, lhsT=mm2[:, 128:256], rhs=vtbf[:, ch0:ch0 + 64], start=False, stop=True), reads=[mm2b, vtbfb], writes=[p3b])
            if DBG.get('dump') == (d, c) and h == 0:
                dump(kb, nc, 'w1', w1[:, :], [128, 64], [w1b], BF16)
                dump(kb, nc, 'ut', ut[:, :], [128, 64], [utb], BF16)
            ts, tsb = tsr.next()
            kb.op('dve', lambda e, ts=ts, p3=p3, ch0=ch0, h=h: e.scalar_tensor_tensor(out=ts[:, :], in0=vt[:, ch0:ch0 + 64], scalar=bc[:, h:h + 1], in1=p3, op0=ALU.mult, op1=ALU.add),
                  reads=[vtb, bcb, p3b], writes=[tsb])
            if first:
                kb.op('pool', lambda e, ts=ts, ch0=ch0, c=c: e.tensor_copy(out=yacc[:, c, ch0:ch0 + 64], in_=ts[:, :]), reads=[tsb], writes=[yaccb[c]])
            else:
                kb.op('pool', lambda e, ts=ts, ch0=ch0, c=c: e.tensor_tensor(out=yacc[:, c, ch0:ch0 + 64], in0=yacc[:, c, ch0:ch0 + 64], in1=ts[:, :], op=ALU.add), reads=[tsb, yaccb[c]], writes=[yaccb[c]])
            yield
            p4full, p4b, i4 = pseq_next()
            p4 = pseqt[i4][hp, (i4 % 4) * 64:(i4 % 4 + 1) * 64]
            kb.op('pe', lambda e, p4=p4, bkt=bkt, ut=ut, ch0=ch0: e.matmul(p4, lhsT=bkt[:, 0, ch0:ch0 + 64], rhs=ut[:, :], start=True, stop=False), reads=[bktb, utb], writes=[p4b])
            kb.op('pe', lambda e, p4=p4, bkt=bkt, ch0=ch0: e.matmul(p4, lhsT=bkt[:, 1, ch0:ch0 + 64], rhs=vtbf[:, ch0:ch0 + 64], start=False, stop=True), reads=[bktb, vtbfb], writes=[p4b])
            tq, tqb = tsr.next()
            kb.op('dve', lambda e, tq=tq, p4=p4, hp=hp, bj=bj: e.tensor_tensor(out=tq[hp, :], in0=p4, in1=ST[hp, d, bj, :], op=ALU.add), reads=[p4b, stb], writes=[tqb])
            kb.op('dve', lambda e, tq=tq, hp=hp, bj=bj: e.tensor_scalar(out=ST[hp, d, bj, :], in0=tq[hp, :], scalar1=e1[hp, bj, tcol:tcol + 1], scalar2=None, op0=ALU.mult), reads=[tqb, e1b], writes=[stb])
            kb.op('act', lambda e, tq=tq, hp=hp, bj=bj: e.activation(out=STbf[hp, d, bj, :], in_=tq[hp, :], func=AF.Identity, scale=e1[hp, bj, tcol:tcol + 1]), reads=[tqb, e1b], writes=[stb])

        hg = [head_body(h) for h in range(DBG.get('heads', 3))]
        while hg:
            for g_ in list(hg):
                try:
                    next(g_)
                except StopIteration:
                    hg.remove(g_)
            yield
        del inflight[c]
        done[c] = done.get(c, 0) + 1
        if DBG.get('dump') == (d, c):
            dump(kb, nc, 'sm', sm[:, :, :], [128, 8, 128], [smb])
            dump(kb, nc, 'kkt', kkt[:, :, :], [128, 2, 128], [kktb])
            dump(kb, nc, 'vt', vt[:, :], [128, 192], [vtb])
            dump(kb, nc, 'sg', sg[:, :], [128, 192], [sgb])
            dump(kb, nc, 'e1', e1[:, :, :], [128, 2, 256], [e1b])
            dump(kb, nc, 'e2', e2[:, :, :], [128, 2, 128], [e2b])
            dump(kb, nc, 'ar', ar[:, :, :], [128, 2, 256], [arb], BF16)
            dump(kb, nc, 'bt', bt[:, :, :], [128, 2, 128], [btb], BF16)
            dump(kb, nc, 'kt', kt[:, :, :], [128, 2, 128], [ktb], BF16)
            dump(kb, nc, 'bkt', bkt[:, :, :], [128, 2, 192], [bktb], BF16)
            dump(kb, nc, 'bc', bc[:, :], [128, 4], [bcb])
            dump(kb, nc, 'ST', ST[:, d, :, :], [128, 2, 64], STb[d])
            dump(kb, nc, 'yacc', yacc[:, c, :], [128, 192], [yaccb[c]])
        yield
        if done[c] == 2 and DBG.get('stage', 99) >= 8:
            gt, gtb = gtr.next()
            kb.op('sp', lambda e, gt=gt, t0=t0: e.dma_start(out=gt[:, :], in_=GT[t0:t0 + 128, 0:192]), writes=[gtb], dma=True)
            fn, fnb = fnr.next()
            for h in range(3):
                ch0 = 64 * h
                stt, sttb = str_.next()
                jk, jkb = jkr.next()
                kb.op('act', lambda e, stt=stt, jk=jk, c=c, ch0=ch0: e.activation(out=jk[:, :], in_=yacc[:, c, ch0:ch0 + 64], func=AF.Copy, accum_out=stt[:, 0:1]), reads=[yaccb[c]], writes=[sttb, jkb])
                kb.op('act', lambda e, stt=stt, jk=jk, c=c, ch0=ch0: e.activation(out=jk[:, :], in_=yacc[:, c, ch0:ch0 + 64], func=AF.Square, accum_out=stt[:, 1:2]), reads=[yaccb[c]], writes=[sttb, jkb])
                kb.op('dve', lambda e, stt=stt: e.tensor_scalar(out=stt[:, 6:7], in0=stt[:, 0:1], scalar1=1.0 / 64, scalar2=None, op0=ALU.mult), reads=[sttb], writes=[sttb])
                kb.op('dve', lambda e, stt=stt: e.tensor_tensor(out=stt[:, 3:4], in0=stt[:, 6:7], in1=stt[:, 6:7], op=ALU.mult), reads=[sttb], writes=[sttb])
                kb.op('dve', lambda e, stt=stt: e.scalar_tensor_tensor(out=stt[:, 7:8], in0=stt[:, 1:2], scalar=1.0 / 64, in1=stt[:, 3:4], op0=ALU.mult, op1=ALU.subtract), reads=[sttb], writes=[sttb])
                kb.op('act', lambda e, stt=stt: e.activation(out=stt[:, 4:5], in_=stt[:, 7:8], func=AF.Sqrt, bias=gneps[:, 0:1]), reads=[sttb, gnepsb], writes=[sttb])
                kb.op('dve', lambda e, stt=stt: e.reciprocal(out=stt[:, 5:6], in_=stt[:, 4:5]), reads=[sttb], writes=[sttb])
                kb.op('dve', lambda e, stt=stt, fn=fn, c=c, ch0=ch0: e.tensor_scalar(out=fn[:, ch0:ch0 + 64], in0=yacc[:, c, ch0:ch0 + 64], scalar1=stt[:, 6:7], scalar2=stt[:, 5:6], op0=ALU.subtract, op1=ALU.mult),
                      reads=[sttb, yaccb[c]], writes=[fnb])
            if DBG.get('dumpfin') == c:
                dump(kb, nc, 'stt', stt[:, :], [128, 8], [sttb])
                dump(kb, nc, 'fn0', fn[:, :], [128, 192], [fnb])
                dump(kb, nc, 'gt', gt[:, :], [128, 192], [gtb])
                dump(kb, nc, 'yaccf', yacc[:, c, :], [128, 192], [yaccb[c]])
                dump(kb, nc, 'lng', lng[:, :], [128, 192], [lngb])
            kb.op('pool', lambda e, fn=fn: e.tensor_tensor(out=fn[:, :], in0=fn[:, :], in1=lng[:, :], op=ALU.mult), reads=[fnb, lngb], writes=[fnb])
            kb.op('pool', lambda e, fn=fn: e.tensor_tensor(out=fn[:, :], in0=fn[:, :], in1=lnb[:, :], op=ALU.add), reads=[fnb, lnbb], writes=[fnb])
            kb.op('dve', lambda e, fn=fn, gt=gt: e.tensor_tensor(out=fn[:, :], in0=fn[:, :], in1=gt[:, :], op=ALU.mult), reads=[fnb, gtb], writes=[fnb])
            kb.op('pool', lambda e, fn=fn, t0=t0: e.dma_start(out=YR.rows(t0, 0, 192), in_=fn[:, :]), reads=[fnb], dma=True)


    def dir_gen(d):
        for c in (DBG['chunks'][d] if 'chunks' in DBG else CHUNK_ORDER[d]):
            yield from chunk_body(d, c)

    dirs = DBG.get('dirs', [0, 1])
    for d in dirs:
        kb.op('dve', lambda e, d=d: e.memset(ST[:, d, :, :], 0.0), writes=STb[d])
        kb.op('dve', lambda e, d=d: e.memset(STbf[:, d, :, :], 0.0), writes=STb[d])
    gens = [dir_gen(d) for d in dirs]
    if DBG.get('no_interleave'):
        for g_ in gens:
            for _ in g_:
                pass
    else:
        while gens:
            for g_ in list(gens):
                try:
                    next(g_)
                except StopIteration:
                    gens.remove(g_)


FN_PARAMS = {'c128': [128, 128], 'ns128': [128, 128], 'twc': [128, 64], 'tws': [128, 64], 'fl1': [128, 64], 'fl2': [128, 64],
             'c256': [128, 2, 256], 'ns256': [128, 2, 256]}


def phase_fnet(kb, nc, HT, GT, YF, ZS, P, bar):
    sc_l = 1.0 / float(np.sqrt(L * 64.0))
    sc_c = 1.0 / float(np.sqrt(LC * 64.0))
    with ExitStack() as c1:
        def const(name, shape, src):
            t = kb.sb(name, shape, F32, c1)
            b, = load_consts(kb, [(t, src)])
            return t, b
        c128, c128b = const('c128', [128, 128], P['c128'])
        ns128, ns128b = const('ns128', [128, 128], P['ns128'])
        twc, twcb = const('twc', [128, 64], P['twc'])
        tws, twsb = const('tws', [128, 64], P['tws'])
        c256, c256b = const('c256', [128, 2, 256], P['c256'])
        ns256, ns256b = const('ns256', [128, 2, 256], P['ns256'])
        hc = kb.sb('hctx', [128, 2, 128], F32, c1)
        hcb = Buf()
        gc = kb.sb('gctx', [128, 2, 64], F32, c1)
        gcb = Buf()
        for lt in range(2):
            kb.op('sp', lambda e, lt=lt: e.dma_start(out=hc[:, lt, :], in_=HT[lt * 128:(lt + 1) * 128, :]), writes=[hcb], dma=True)
            kb.op('sp', lambda e, lt=lt: e.dma_start(out=gc[:, lt, :], in_=GT[lt * 128:(lt + 1) * 128, 192:256]), writes=[gcb], dma=True)
        pc = Ring(kb, 'pfc', [128, 512], F32, 1, psum=True, es=c1)
        fo = Ring(kb, 'foc', [128, 64], F32, 2, es=c1)
        for lo in range(2):
            ps, psb = pc.next()
            for lt in range(2):
                kb.op('pe', lambda e, ps=ps, lt=lt, lo=lo: e.matmul(ps[:, 0:64], lhsT=c256[:, lt, lo * 128:(lo + 1) * 128], rhs=hc[:, lt, 0:64], start=(lt == 0), stop=False),
                      reads=[c256b, hcb], writes=[psb])
            for lt in range(2):
                kb.op('pe', lambda e, ps=ps, lt=lt, lo=lo: e.matmul(ps[:, 0:64], lhsT=ns256[:, lt, lo * 128:(lo + 1) * 128], rhs=hc[:, lt, 64:128], start=False, stop=(lt == 1)),
                      reads=[ns256b, hcb], writes=[psb])
            f, fb = fo.next()
            kb.op('dve', lambda e, f=f, ps=ps, lo=lo: e.scalar_tensor_tensor(out=f[:, :], in0=ps[:, 0:64], scalar=sc_c, in1=gc[:, lo, :], op0=ALU.mult, op1=ALU.mult),
                  reads=[psb, gcb], writes=[fb])
            kb.op('pool', lambda e, f=f, lo=lo: e.dma_start(out=YF.rows(lo * 128, 192, 256), in_=f[:, :]), reads=[fb], dma=True)
        h1 = kb.sb('h1', [128, 64, 128], F32, c1)
        h1b = Buf()
        HTl = HT[LC:LT, :].rearrange("(a b) c -> a b c", b=64)
        for j in range(4):
            kb.op('sp', lambda e, j=j: e.dma_start(out=h1[:, j * 16:(j + 1) * 16, :], in_=HTl[:, j * 16:(j + 1) * 16, :]), writes=[h1b], dma=True)
        py = Ring(kb, 'py', [128, 512], F32, 4, psum=True, es=c1)
        zt = Ring(kb, 'zt', [128, 4, 128], F32, 6, es=c1)
        zo = Ring(kb, 'zo', [128, 2, 4, 128], F32, 3, es=c1)
        for pc_ in range(16):
            b0 = pc_ * 4
            pr, prb = py.next()
            pi, pib = py.next()
            kb.op('pe', lambda e, pr=pr, b0=b0: e.matmul(pr[:, :], lhsT=c128[:, :], rhs=h1[:, b0:b0 + 4, :], start=True, stop=True), reads=[c128b, h1b], writes=[prb])
            kb.op('pe', lambda e, pi=pi, b0=b0: e.matmul(pi[:, :], lhsT=ns128[:, :], rhs=h1[:, b0:b0 + 4, :], start=True, stop=True), reads=[ns128b, h1b], writes=[pib])
            cb_ = twc[:, b0:b0 + 4].unsqueeze(2).to_broadcast([128, 4, 128])
            sb_ = tws[:, b0:b0 + 4].unsqueeze(2).to_broadcast([128, 4, 128])
            prv = pr[:, :].rearrange("p (b c) -> p b c", c=128)
            piv = pi[:, :].rearrange("p (b c) -> p b c", c=128)
            t1, t1b = zt.next()
            t2, t2b = zt.next()
            z, zb = zo.next()
            kb.op('dve', lambda e, t1=t1, prv=prv, cb_=cb_: e.tensor_tensor(out=t1[:, :, :], in0=prv, in1=cb_, op=ALU.mult), reads=[prb, twcb], writes=[t1b])
            kb.op('dve', lambda e, t2=t2, piv=piv, sb_=sb_: e.tensor_tensor(out=t2[:, :, :], in0=piv, in1=sb_, op=ALU.mult), reads=[pib, twsb], writes=[t2b])
            kb.op('pool', lambda e, z=z, t1=t1, t2=t2: e.tensor_tensor(out=z[:, 0, :, :], in0=t1[:, :, :], in1=t2[:, :, :], op=ALU.add), reads=[t1b, t2b], writes=[zb])
            t3, t3b = zt.next()
            t4, t4b = zt.next()
            kb.op('dve', lambda e, t3=t3, piv=piv, cb_=cb_: e.tensor_tensor(out=t3[:, :, :], in0=piv, in1=cb_, op=ALU.mult), reads=[pib, twcb], writes=[t3b])
            kb.op('dve', lambda e, t4=t4, prv=prv, sb_=sb_: e.tensor_tensor(out=t4[:, :, :], in0=prv, in1=sb_, op=ALU.mult), reads=[prb, twsb], writes=[t4b])
            kb.op('pool', lambda e, z=z, t3=t3, t4=t4: e.tensor_tensor(out=z[:, 1, :, :], in0=t3[:, :, :], in1=t4[:, :, :], op=ALU.subtract), reads=[t3b, t4b], writes=[zb])
            for ri in range(2):
                kb.op('pool', lambda e, z=z, ri=ri, b0=b0: e.dma_start(out=ZS[ri, :, b0:b0 + 4, :], in_=z[:, ri, :, :]), reads=[zb], dma=True)
        kb.barrier(bar[:])
    with ExitStack() as c2:
        fl1 = kb.sb('fl1', [128, 64], F32, c2)
        fl2 = kb.sb('fl2', [128, 64], F32, c2)
        fl1b, fl2b = load_consts(kb, [(fl1, P['fl1']), (fl2, P['fl2'])])
        rz1 = kb.sb('rz1', [128, 128, 64], F32, c2)
        rz2 = kb.sb('rz2', [128, 128, 64], F32, c2)
        g2 = kb.sb('g2', [64, 128, 64], F32, c2)
        fo2 = kb.sb('fo2', [64, 128, 64], F32, c2)
        rz1b = [Buf() for _ in range(4)]
        rz2b = [Buf() for _ in range(4)]
        g2b = Buf()
        fo2b = [Buf() for _ in range(4)]
        for qa in range(4):
            asl = slice(qa * 32, (qa + 1) * 32)
            for ri in range(2):
                src = ZS[ri].rearrange("a b c -> b a c")
                kb.op('sp', lambda e, ri=ri, src=src, asl=asl: e.dma_start(out=rz1[ri * 64:(ri + 1) * 64, asl, :], in_=src[:, asl, 0:64]), writes=[rz1b[qa]], dma=True)
                kb.op('sp', lambda e, ri=ri, src=src, asl=asl: e.dma_start(out=rz2[ri * 64:(ri + 1) * 64, asl, :], in_=src[:, asl, 64:128]), writes=[rz2b[qa]], dma=True)
        kb.op('sp', lambda e: e.dma_start(out=g2[:, :, :], in_=GT[LC:LT, 192:256].rearrange("(b a) c -> b a c", a=128)), writes=[g2b], dma=True)
        pf = Ring(kb, 'pf', [128, 512], F32, 2, psum=True, es=c2)
        for pc_ in range(16):
            a0 = pc_ * 8
            qa = pc_ // 4
            ps, psb = pf.next()
            kb.op('pe', lambda e, ps=ps, a0=a0: e.matmul(ps[0:64, :], lhsT=fl1[:, :], rhs=rz1[:, a0:a0 + 8, :], start=True, stop=False), reads=[fl1b, rz1b[qa]], writes=[psb])
            kb.op('pe', lambda e, ps=ps, a0=a0: e.matmul(ps[0:64, :], lhsT=fl2[:, :], rhs=rz2[:, a0:a0 + 8, :], start=False, stop=True), reads=[fl2b, rz2b[qa]], writes=[psb])
            kb.op('dve', lambda e, ps=ps, a0=a0: e.scalar_tensor_tensor(out=fo2[:, a0:a0 + 8, :], in0=ps[0:64, :].rearrange("p (a c) -> p a c", c=64), scalar=sc_l,
                                                                        in1=g2[:, a0:a0 + 8, :], op0=ALU.mult, op1=ALU.mult), reads=[psb, g2b], writes=[fo2b[qa]])
        allfo = fo2b
        for (dst, b0, b1) in YF.lat_groups():
            kb.op('pool', lambda e, dst=dst, b0=b0, b1=b1: e.dma_start(out=dst, in_=fo2[b0:b1, :, :]), reads=allfo, dma=True)
        kb.barrier(bar[:])


def gate_rows(kb, es, nc, sil, silb, adaw_gate, adab_row, npost_b, ones1, ones1b, keep_es, NG=None):
    if NG is None:
        NG = kb.sb('NG', [128, 2, D], F32, keep_es)
    NGb = Buf()
    wg = kb.sb('adawg', [128, 8, D], F32, es)
    wgb = Buf()
    for k in range(8):
        kb.op('sp', lambda e, k=k: e.dma_start(out=wg[:, k, :], in_=adaw_gate[k * 128:(k + 1) * 128, :]), writes=[wgb], dma=True)
    br = kb.sb('adabr', [1, D], F32, es)
    brb, = load_consts(kb, [(br, adab_row)])
    npb = kb.sb('npostb', [128, D], F32, es)
    npbb, = load_consts(kb, [(npb, npost_b)])
    onesq = kb.sb('onesq', [128, 128], F32, es)
    onesqb = Buf()
    kb.op('dve', lambda e: e.memset(onesq[:], 1.0), writes=[onesqb])
    srep = kb.sb('silrep', [128, 8, 128], F32, es)
    pg_ = Ring(kb, 'pgate', [128, 512], F32, 2, psum=True, es=es)
    for n in range(2):
        srb = Buf()
        for k in range(8):
            kb.op('dve', lambda e, k=k, n=n: e.tensor_scalar(out=srep[:, k, :], in0=onesq[:, :], scalar1=sil[:, k, n:n + 1], scalar2=None, op0=ALU.mult),
                  reads=[onesqb, silb], writes=[srb])
        for half in range(2):
            ps, psb = pg_.next()
            for k in range(8):
                kb.op('pe', lambda e, ps=ps, k=k, half=half: e.matmul(ps[:, :], lhsT=srep[:, k, :], rhs=wg[:, k, half * 512:(half + 1) * 512], start=(k == 0), stop=False),
                      reads=[srb, wgb], writes=[psb])
            kb.op('pe', lambda e, ps=ps, half=half: e.matmul(ps[:, :], lhsT=ones1[0:1, :], rhs=br[0:1, half * 512:(half + 1) * 512], start=False, stop=True),
                  reads=[ones1b, brb], writes=[psb])
            kb.op('dve', lambda e, ps=ps, n=n, half=half: e.tensor_tensor(out=NG[:, n, half * 512:(half + 1) * 512], in0=ps[:, :], in1=npb[:, half * 512:(half + 1) * 512], op=ALU.mult),
                  reads=[psb, npbb], writes=[NGb])
    return NG, NGb


class OutProj:
    def __init__(self, kb, es, nc, yT, w_out_src, xsrc, NG, NGb, epsT, epsb, nslot=3, ysrc=None, ident=None, npol=2):
        self.kb, self.yT, self.xsrc, self.NG, self.NGb, self.epsT, self.epsb = kb, yT, xsrc, NG, NGb, epsT, epsb
        self.ysrc, self.ident = ysrc, ident
        self.pref = {}
        self.ykbs = {}
        if ysrc is not None:
            self.ytok = Ring(kb, 'ytok', [128, 4, 256], F32, 2, es=es)
            self.ytokb = Ring(kb, 'ytokb', [128, 1024], BF16, 2, es=es)
            self.pyt = Ring(kb, 'pyt', [128, 8, 128], BF16, 1, psum=True, es=es)
        self.wo, self.wob = load_weight_bf16(kb, es, nc, 'wout_bf', w_out_src, D, es)
        self.ystg = Ring(kb, 'ystg', [128, 8, 128], F32, 2, es=es)
        self.ybf = Ring(kb, 'ybf', [128, 8, 128], BF16, 2, es=es)
        self.xin = Ring(kb, 'xres', [128, D], F32, 2, es=es)
        self.xout = Ring(kb, 'xnew', [128, D], F32, nslot, es=es)
        self.pol = Ring(kb, 'pol', [128, 512], F32, npol, psum=True, es=es)
        self.st = Ring(kb, 'ost', [128, 8], F32, 4, es=es, strict=True)
        self.junk = kb.sb('ojunk', [128, 512], BF16, es)
        self.junkb = Buf()
        self.tmp = Ring(kb, 'otmp', [128, D], F32, 2, es=es)

    def _pre(self, r0):
        kb = self.kb
        if self.ysrc is None:
            ys, ysb = self.ystg.next()
            yT = self.yT
            kb.op('sp', lambda e, ys=ys: e.dma_start(out=ys[:, :, :], in_=yT[:, r0:r0 + 128].rearrange("(k p) t -> p k t", p=128)), writes=[ysb], dma=True)
            yb, ybb = self.ybf.next()
            kb.op('pool', lambda e, yb=yb, ys=ys: e.tensor_copy(out=yb[:, :, :], in_=ys[:, :, :]), reads=[ysb], writes=[ybb])
        else:
            aps, sbufs = self.ysrc(r0)
            ykb16, ykb0 = self.ytokb.next()
            ykb16b = self.ykbs.setdefault(id(ykb0), [ykb0] + [Buf() for _ in range(3)])
            for r, ap_ in enumerate(aps):
                kb.op('sp', lambda e, ykb16=ykb16, r=r, ap_=ap_: e.dma_start(out=ykb16[:, r * 256:(r + 1) * 256], in_=ap_), reads=sbufs, writes=[ykb16b[r]], dma=True)
            pt, ptb = self.pyt.next()
            idt, idtb = self.ident
            for k in range(8):
                kb.op('pe', lambda e, pt=pt, ykb16=ykb16, k=k: e.transpose(out=pt[:, k, :], in_=ykb16[:, k * 128:(k + 1) * 128], identity=idt[:, :]), reads=[ykb16b, idtb], writes=[ptb])
            yb, ybb = self.ybf.next()
            kb.op('act', lambda e, yb=yb, pt=pt: e.activation(out=yb[:, :, :], in_=pt[:, :, :], func=AF.Copy), reads=[ptb], writes=[ybb])
        xi, xib = self.xin.next()
        kb.op('sp', lambda e, xi=xi: e.dma_start(out=xi[:, :], in_=self.xsrc[r0:r0 + 128, :]), writes=[xib], dma=True)
        self.pref[r0] = (yb, ybb, xi, xib)

    def tile(self, r0, n, nxt=None):
        kb = self.kb
        if r0 not in self.pref:
            self._pre(r0)
        if nxt is not None and nxt not in self.pref:
            self._pre(nxt)
        yb, ybb, xi, xib = self.pref.pop(r0)
        pss = [self.pol.next(), self.pol.next()]
        for half in range(2):
            ps, psb = pss[half]
            for k in range(8):
                kb.op('pe', lambda e, ps=ps, k=k, half=half, yb=yb: e.matmul(ps[:, :], lhsT=yb[:, k, :], rhs=self.wo[:, k, half * 512:(half + 1) * 512],
                                                                           start=(k == 0), stop=(k == 7)), reads=[ybb, self.wob], writes=[psb])
        st, stb = self.st.next()
        for half in range(2):
            ps, psb = pss[half]
            kb.op('act', lambda e, ps=ps, st=st, half=half: e.activation(out=self.junk[:, :], in_=ps[:, :], func=AF.Square, accum_out=st[:, half:half + 1]),
                  reads=[psb], writes=[stb, self.junkb])
        kb.op('dve', lambda e, st=st: e.tensor_tensor(out=st[:, 2:3], in0=st[:, 0:1], in1=st[:, 1:2], op=ALU.add), reads=[stb], writes=[stb])
        kb.op('act', lambda e, st=st: e.activation(out=st[:, 3:4], in_=st[:, 2:3], func=AF.Sqrt, scale=1.0 / D, bias=self.epsT[:, 0:1]), reads=[stb, self.epsb], writes=[stb])
        kb.op('dve', lambda e, st=st: e.reciprocal(out=st[:, 4:5], in_=st[:, 3:4]), reads=[stb], writes=[stb])
        tm_, tmb_ = self.tmp.next()
        for half in range(2):
            ps, psb = pss[half]
            kb.op('dve', lambda e, ps=ps, st=st, tm_=tm_, half=half: e.scalar_tensor_tensor(out=tm_[:, half * 512:(half + 1) * 512], in0=ps[:, :], scalar=st[:, 4:5],
                                                                                       in1=self.NG[:, n, half * 512:(half + 1) * 512], op0=ALU.mult, op1=ALU.mult),
                  reads=[psb, stb, self.NGb], writes=[tmb_])
        xo, xob = self.xout.next()
        kb.op('dve', lambda e, xo=xo, tm_=tm_, xi=xi: e.tensor_tensor(out=xo[:, :], in0=tm_[:, :], in1=xi[:, :], op=ALU.add), reads=[tmb_, xib], writes=[xob])
        return xo, xob


L3_TOK = 2048


def part3(kb, nc, IN, ZG, X1s, OUT, C):
    epsT, epsb, ones1, ones1b = C['epsT'], C['epsb'], C['ones1'], C['ones1b']
    with ExitStack() as g:
        sil = kb.sb('sil3', [128, 8, 2], F32, g)
        silb = Buf(strict=True)
        kb.op('sp', lambda e: e.dma_start(out=sil[:], in_=IN['sil_in']), writes=[silb], dma=True)
        kb.op('act', lambda e: e.activation(out=sil[:], in_=sil[:], func=AF.Silu), reads=[silb], writes=[silb])
        NG = kb.sb('NG3', [128, 2, D], F32, g)
        with ExitStack() as g0:
            NG, NGb = gate_rows(kb, g0, nc, sil, silb, IN['adawg1'], IN['adabr1'], IN['npostb1'], ones1, ones1b, g0, NG=NG)
            kb.barrier(C['bar'][:])
        op = OutProj(kb, g, nc, None, IN['wout1'], X1s, NG, NGb, epsT, epsb, nslot=4, ysrc=ZG.src, ident=(C['ident_bf'], C['identb']), npol=6)
        for i in range(L // 128):
            xo, xob = op.tile(i * 128, 0, nxt=((i + 1) * 128 if (i + 1) * 128 < L else None))
            kb.op('pool', lambda e, xo=xo, i=i: e.dma_start(out=OUT[i * 128:(i + 1) * 128, :], in_=xo[:, :]), reads=[xob], dma=True, final=True)
        kb.barrier(C['bar'][:])


RT_PARAMS = {'logit': [128, 4], 'diffT': [128, 2, 128], 'mask01T': [128, 2, 128], 'posxi': [128, 2, 128], 'poszeta': [128, 2]}


def part2(kb, nc, IN, YG, X1s, ZD, C):
    debug = False
    sil_in, adawg, adabr, npostb, adaw1, adab1, normw1, win1, ropec, ropes = (IN[k_] for k_ in (
        'sil_in', 'adawg0', 'adabr0', 'npostb0', 'adaw1', 'adab1', 'normw1', 'win1', 'ropec', 'ropes'))
    xin, wout0 = IN['xin'], IN['wout0p']
    RP = {k_: IN['r_' + k_] for k_ in RT_PARAMS}
    X1 = X1s
    Z = ZD
    QKV = dram_tmp(nc, 'QKV', [LT, 768], BF16, debug)
    GS = dram_tmp(nc, 'GS', [LT, 256], F32, debug)
    bar, ident_f, identfb, ident_bf, identb, epsT, epsb, ones1, ones1b = (C[k_] for k_ in (
        'bar', 'ident_f', 'identfb', 'ident_bf', 'identb', 'epsT', 'epsb', 'ones1', 'ones1b'))
    with ExitStack() as pa:
        sil = kb.sb('sil', [128, 8, 2], F32, pa)
        silb = Buf(strict=True)
        kb.op('sp', lambda e: e.dma_start(out=sil[:], in_=sil_in), writes=[silb], dma=True)
        kb.op('act', lambda e: e.activation(out=sil[:], in_=sil[:], func=AF.Silu), reads=[silb], writes=[silb])
        NG = kb.sb('NG', [128, 2, D], F32, pa)
        mod1 = kb.sb('mod1', [128, 16, 2], F32, pa)
        G1 = kb.sb('G1', [128, 8, 2], F32, pa)
        with ExitStack() as pg0:
            NG, NGb = gate_rows(kb, pg0, nc, sil, silb, adawg, adabr, npostb, ones1, ones1b, pg0, NG=NG)
            mod1, mod1b = adaln_vectors(kb, pg0, nc, sil_in, adaw1, adab1, normw1, 16, mod_tile=mod1)
            nw = kb.sb('nw1', [128, 8], F32, pg0)
            nwb, = load_consts(kb, [(nw, normw1)])
            G1buf = Buf(strict=True)
            for n in range(2):
                kb.op('dve', lambda e, n=n: e.scalar_tensor_tensor(out=G1[:, :, n], in0=mod1[:, 8:16, n], scalar=1.0, in1=nw[:, :], op0=ALU.add, op1=ALU.mult),
                      reads=[mod1b, nwb], writes=[G1buf])
            kb.barrier(bar[:])
        Gb = {'G': G1buf, 'eps': epsT, 'epsb': epsb}
        op0 = OutProj(kb, pa, nc, None, wout0, xin, NG, NGb, epsT, epsb, nslot=3, ysrc=YG.src, ident=(ident_bf, identb))
        w1, w1b = load_weight_bf16(kb, pa, nc, 'w1_bf', win1, D, pa)
        for k in range(8):
            kb.op('pool', lambda e, k=k: e.tensor_scalar(out=w1[:, k, 256:512], in0=w1[:, k, 256:512], scalar1=float(128 ** -0.5), scalar2=None, op0=ALU.mult), reads=[w1b], writes=[w1b])
        pqk = Ring(kb, 'pqk', [128, 512], F32, 2, psum=True, es=pa)
        csr = Ring(kb, 'cs', [128, 2, 64], F32, 2, es=pa)
        qkvr = Ring(kb, 'qkvt', [128, 768], BF16, 2, es=pa)
        rtmp = Ring(kb, 'rtmp', [128, 4, 64], F32, 4, es=pa)
        gsr = Ring(kb, 'gst', [128, 256], F32, 2, es=pa)

        rlist = [t0_ + sb_ * 128 for (t0_, T_) in TILES for sb_ in range(T_ // 128)]

        def get_x(r0):
            n = 1 if r0 < LC else 0
            ix = rlist.index(r0)
            xo, xob = op0.tile(r0, n, nxt=(rlist[ix + 1] if ix + 1 < len(rlist) else None))
            if r0 >= LC:
                kb.op('pool', lambda e, xo=xo, r0=r0: e.dma_start(out=X1[r0 - LC:r0 - LC + 128, :], in_=xo[:, :]), reads=[xob], dma=True)
            return xo, xob

        def emit_tile(t0, T, hT, hb):
            for sub in range(T // 128):
                r0 = t0 + sub * 128
                psA, psAb = pqk.next()
                psB, psBb = pqk.next()
                for half, (ps, psb) in enumerate([(psA, psAb), (psB, psBb)]):
                    for k in range(8):
                        kb.op('pe', lambda e, ps=ps, k=k, half=half, sub=sub: e.matmul(ps[:, :], lhsT=hT[:, k, sub * 128:(sub + 1) * 128], rhs=w1[:, k, half * 512:(half + 1) * 512],
                                                                                     start=(k == 0), stop=(k == 7)), reads=[hb, w1b], writes=[psb])
                qkv, qkvb = qkvr.next()
                if r0 >= LC:
                    cs, csb = csr.next()
                    kb.op('sp', lambda e, cs=cs, r0=r0: e.dma_start(out=cs[:, 0, :], in_=ropec[r0 - LC:r0 - LC + 128, :]), writes=[csb], dma=True)
                    kb.op('sp', lambda e, cs=cs, r0=r0: e.dma_start(out=cs[:, 1, :], in_=ropes[r0 - LC:r0 - LC + 128, :]), writes=[csb], dma=True)
                    pv_ = psA[:, :].rearrange("p (g h f) -> p g h f", g=4, h=2)
                    ov_ = qkv[:, 0:512].rearrange("p (g h f) -> p g h f", g=4, h=2)
                    cb_ = cs[:, 0, :].unsqueeze(1).to_broadcast([128, 4, 64])
                    sb_ = cs[:, 1, :].unsqueeze(1).to_broadcast([128, 4, 64])
                    ta, tab = rtmp.next()
                    tb, tbb = rtmp.next()
                    kb.op('dve', lambda e, ta=ta, pv_=pv_, cb_=cb_: e.tensor_tensor(out=ta[:, :, :], in0=pv_[:, :, 0, :], in1=cb_, op=ALU.mult), reads=[psAb, csb], writes=[tab])
                    kb.op('dve', lambda e, tb=tb, pv_=pv_, sb_=sb_: e.tensor_tensor(out=tb[:, :, :], in0=pv_[:, :, 1, :], in1=sb_, op=ALU.mult), reads=[psAb, csb], writes=[tbb])
                    kb.op('pool', lambda e, ta=ta, tb=tb, ov_=ov_: e.tensor_tensor(out=ov_[:, :, 0, :], in0=ta[:, :, :], in1=tb[:, :, :], op=ALU.subtract), reads=[tab, tbb], writes=[qkvb])
                    tc_, tcb = rtmp.next()
                    td, tdb = rtmp.next()
                    kb.op('dve', lambda e, tc_=tc_, pv_=pv_, sb_=sb_: e.tensor_tensor(out=tc_[:, :, :], in0=pv_[:, :, 0, :], in1=sb_, op=ALU.mult), reads=[psAb, csb], writes=[tcb])
                    kb.op('dve', lambda e, td=td, pv_=pv_, cb_=cb_: e.tensor_tensor(out=td[:, :, :], in0=pv_[:, :, 1, :], in1=cb_, op=ALU.mult), reads=[psAb, csb], writes=[tdb])
                    kb.op('pool', lambda e, tc_=tc_, td=td, ov_=ov_: e.tensor_tensor(out=ov_[:, :, 1, :], in0=tc_[:, :, :], in1=td[:, :, :], op=ALU.add), reads=[tcb, tdb], writes=[qkvb])
                else:
                    kb.op('act', lambda e, qkv=qkv, psA=psA: e.activation(out=qkv[:, 0:512], in_=psA[:, :], func=AF.Copy), reads=[psAb], writes=[qkvb])
                kb.op('act', lambda e, qkv=qkv, psB=psB: e.activation(out=qkv[:, 512:768], in_=psB[:, 0:256], func=AF.Copy), reads=[psBb], writes=[qkvb])
                gs, gsb = gsr.next()
                kb.op('act', lambda e, gs=gs, psB=psB: e.activation(out=gs[:, :], in_=psB[:, 256:512], func=AF.Silu), reads=[psBb], writes=[gsb])
                kb.op('pool', lambda e, qkv=qkv, r0=r0: e.dma_start(out=QKV[r0:r0 + 128, :], in_=qkv[:, :]), reads=[qkvb], dma=True)
                kb.op('pool', lambda e, gs=gs, r0=r0: e.dma_start(out=GS[r0:r0 + 128, :], in_=gs[:, :]), reads=[gsb], dma=True)

        phase_proj(kb, nc, pa, None, None, G1, mod1, Gb, ident_bf, identb, emit_tile, get_x=get_x)
        kb.barrier(bar[:])
    with ExitStack() as pr:
        phase_ret(kb, nc, pr, QKV, GS, Z, RP, ident_bf, identb, epsT, epsb)
        kb.barrier(bar[:])


def phase_ret(kb, nc, es, QKV, GS, Z, RP, ident_bf, identb, epsT, epsb):
    def const(name, shape, src):
        t = kb.sb(name, shape, F32, es)
        b, = load_consts(kb, [(t, src)])
        return t, b
    lgt, lgtb = const('lgt', [128, 4], RP['logit'])
    lgtb.strict = True
    diffT, diffTb = const('diffT', [128, 2, 128], RP['diffT'])
    m01, m01b = const('m01', [128, 2, 128], RP['mask01T'])
    pxi, pxib = const('pxi', [128, 2, 128], RP['posxi'])
    pze, pzeb = const('pze', [128, 2], RP['poszeta'])
    kb.op('act', lambda e: e.activation(out=lgt[:, :], in_=lgt[:, :], func=AF.Exp, scale=-1.0), reads=[lgtb], writes=[lgtb])
    kb.op('dve', lambda e: e.tensor_scalar(out=lgt[:, :], in0=lgt[:, :], scalar1=1.0, scalar2=None, op0=ALU.add), reads=[lgtb], writes=[lgtb])
    kb.op('act', lambda e: e.activation(out=lgt[:, :], in_=lgt[:, :], func=AF.Ln), reads=[lgtb], writes=[lgtb])
    kb.op('dve', lambda e: e.tensor_scalar(out=lgt[:, :], in0=lgt[:, :], scalar1=-1.0, scalar2=None, op0=ALU.mult), reads=[lgtb], writes=[lgtb])
    dmt = kb.sb('dmt', [128, 4, 128], BF16, es)
    xib = kb.sb('xib', [128, 4, 128], F32, es)
    zet = kb.sb('zet', [128, 8], F32, es)
    tabb = Buf(strict=True)
    tmpd = kb.sb('tmpd', [128, 128], F32, es)
    tmpdb = Buf()
    for hl in range(2):
        for dr in range(2):
            j = 2 * hl + dr
            kb.op('act', lambda e, j=j, dr=dr: e.activation(out=tmpd[:, :], in_=diffT[:, dr, :], func=AF.Exp, scale=lgt[:, j:j + 1]), reads=[diffTb, lgtb], writes=[tmpdb])
            kb.op('dve', lambda e, j=j, dr=dr: e.tensor_tensor(out=dmt[:, j, :], in0=tmpd[:, :], in1=m01[:, dr, :], op=ALU.mult), reads=[tmpdb, m01b], writes=[tabb])
            kb.op('act', lambda e, j=j, dr=dr: e.activation(out=xib[:, j, :], in_=pxi[:, dr, :], func=AF.Exp, scale=lgt[:, j:j + 1]), reads=[pxib, lgtb], writes=[tabb])
            kb.op('act', lambda e, j=j, dr=dr: e.activation(out=zet[:, j:j + 1], in_=pze[:, dr:dr + 1], func=AF.Exp, scale=lgt[:, j:j + 1]), reads=[pzeb, lgtb], writes=[tabb])
            kb.op('act', lambda e, j=j: e.activation(out=zet[:, 4 + j:5 + j], in_=lgt[:, j:j + 1], func=AF.Exp, scale=128.0), reads=[lgtb], writes=[tabb])
    oacc = kb.sb('oacc', [128, NCH, 256], F32, es)
    oaccb = [Buf() for _ in range(NCH)]
    kb.op('pool', lambda e: e.memset(oacc[:], 0.0), writes=oaccb)
    R = kb.sb('Rst', [128, 2, 128], F32, es)
    Rbf = kb.sb('Rbf', [128, 2, 128], BF16, es)
    Rb = [Buf(), Buf()]
    qr = Ring(kb, 'rq', [128, 768], BF16, 3, es=es)
    qtr = Ring(kb, 'rqT', [128, 128], BF16, 3, es=es)
    ktr_ = Ring(kb, 'rkT', [128, 128], BF16, 3, es=es)
    qxr = Ring(kb, 'rqx', [128, 128], BF16, 3, es=es)
    kzr = Ring(kb, 'rkz', [128, 128], BF16, 3, es=es)
    sdr = Ring(kb, 'rsd', [128, 128], BF16, 3, es=es)
    ptT = Ring(kb, 'rptT', [128, 1024], BF16, 2, psum=True, es=es)
    pS = Ring(kb, 'rpS', [128, 512], F32, 2, psum=True, es=es)
    pO = Ring(kb, 'rpO', [128, 512], F32, 2, psum=True, es=es)
    pR = Ring(kb, 'rpR', [128, 512], F32, 2, psum=True, es=es)
    gsr = Ring(kb, 'rgs', [128, 256], F32, 2, es=es)
    zr = Ring(kb, 'rz', [128, 256], F32, 2, es=es)
    sst = Ring(kb, 'rss', [128, 8], F32, 4, es=es, strict=True)
    junk = kb.sb('rjunk', [128, 128], BF16, es)
    junkb = Buf()
    for dr in range(2):
        kb.op('dve', lambda e: e.memset(R[:], 0.0), writes=Rb)
        kb.op('dve', lambda e: e.memset(Rbf[:], 0.0), writes=Rb)
        for c in CHUNK_ORDER[dr]:
            r0 = c * 128
            q, qb = qr.next()
            kb.op('sp', lambda e, q=q, r0=r0: e.dma_start(out=q[:, :], in_=QKV[r0:r0 + 128, :]), writes=[qb], dma=True)
            for hl in range(2):
                j = 2 * hl + dr
                vt_ = q[:, 512 + hl * 128:512 + (hl + 1) * 128]
                ktok = q[:, 256 + hl * 128:256 + (hl + 1) * 128]
                if c >= 2:
                    pt, ptb = ptT.next()
                    kb.op('pe', lambda e, pt=pt, q=q, hl=hl: e.transpose(out=pt[:, 0:128], in_=q[:, hl * 128:(hl + 1) * 128], identity=ident_bf[:, :]), reads=[qb, identb], writes=[ptb])
                    kb.op('pe', lambda e, pt=pt, ktok=ktok: e.transpose(out=pt[:, 128:256], in_=ktok, identity=ident_bf[:, :]), reads=[qb, identb], writes=[ptb])
                    qT, qTb = qtr.next()
                    kT, kTb = ktr_.next()
                    qx, qxb = qxr.next()
                    kb.op('act', lambda e, qT=qT, pt=pt: e.activation(out=qT[:, :], in_=pt[:, 0:128], func=AF.Copy), reads=[ptb], writes=[qTb])
                    kb.op('dve', lambda e, qx=qx, pt=pt, j=j: e.tensor_tensor(out=qx[:, :], in0=pt[:, 0:128], in1=xib[:, j, :], op=ALU.mult), reads=[ptb, tabb], writes=[qxb])
                    kb.op('act', lambda e, kT=kT, pt=pt: e.activation(out=kT[:, :], in_=pt[:, 128:256], func=AF.Copy), reads=[ptb], writes=[kTb])
                    ps, psb = pS.next()
                    kb.op('pe', lambda e, ps=ps, kT=kT, qT=qT: e.matmul(ps[:, 0:128], lhsT=kT[:, :], rhs=qT[:, :], start=True, stop=True), reads=[kTb, qTb], writes=[psb])
                    sd, sdb = sdr.next()
                    kb.op('dve', lambda e, sd=sd, ps=ps, j=j: e.tensor_tensor(out=sd[:, :], in0=ps[:, 0:128], in1=dmt[:, j, :], op=ALU.mult), reads=[psb, tabb], writes=[sdb])
                    po, pob = pO.next()
                    kb.op('pe', lambda e, po=po, sd=sd, vt_=vt_: e.matmul(po[:, 0:128], lhsT=sd[:, :], rhs=vt_, start=True, stop=False), reads=[sdb, qb], writes=[pob])
                    kb.op('pe', lambda e, po=po, qx=qx, hl=hl: e.matmul(po[:, 0:128], lhsT=qx[:, :], rhs=Rbf[:, hl, :], start=False, stop=True), reads=[qxb, Rb[hl]], writes=[pob])
                    if dr == 0:
                        kb.op('act', lambda e, po=po, c=c, hl=hl: e.activation(out=oacc[:, c, hl * 128:(hl + 1) * 128], in_=po[:, 0:128], func=AF.Copy), reads=[pob], writes=[oaccb[c]])
                    else:
                        kb.op('dve', lambda e, po=po, c=c, hl=hl: e.tensor_tensor(out=oacc[:, c, hl * 128:(hl + 1) * 128], in0=po[:, 0:128], in1=oacc[:, c, hl * 128:(hl + 1) * 128], op=ALU.add),
                              reads=[pob, oaccb[c]], writes=[oaccb[c]])
                kz, kzb = kzr.next()
                kb.op('dve', lambda e, kz=kz, ktok=ktok, j=j: e.tensor_scalar(out=kz[:, :], in0=ktok, scalar1=zet[:, j:j + 1], scalar2=None, op0=ALU.mult), reads=[qb, tabb], writes=[kzb])
                pr_, prb = pR.next()
                kb.op('pe', lambda e, pr_=pr_, kz=kz, vt_=vt_: e.matmul(pr_[:, 0:128], lhsT=kz[:, :], rhs=vt_, start=True, stop=True), reads=[kzb, qb], writes=[prb])
                kb.op('dve', lambda e, pr_=pr_, hl=hl, j=j: e.scalar_tensor_tensor(out=R[:, hl, :], in0=R[:, hl, :], scalar=zet[:, 4 + j:5 + j], in1=pr_[:, 0:128], op0=ALU.mult, op1=ALU.add),
                      reads=[prb, tabb, Rb[hl]], writes=[Rb[hl]])
                kb.op('act', lambda e, hl=hl: e.activation(out=Rbf[:, hl, :], in_=R[:, hl, :], func=AF.Copy), reads=[Rb[hl]], writes=[Rb[hl]])
            if dr == 1 and c >= 2:
                gs, gsb = gsr.next()
                kb.op('sp', lambda e, gs=gs, r0=r0: e.dma_start(out=gs[:, :], in_=GS[r0:r0 + 128, :]), writes=[gsb], dma=True)
                zt_, ztb = zr.next()
                for hl in range(2):
                    ss, ssb = sst.next()
                    kb.op('act', lambda e, ss=ss, c=c, hl=hl: e.activation(out=junk[:, :], in_=oacc[:, c, hl * 128:(hl + 1) * 128], func=AF.Square, accum_out=ss[:, 0:1]), reads=[oaccb[c]], writes=[ssb, junkb])
                    kb.op('act', lambda e, ss=ss: e.activation(out=ss[:, 1:2], in_=ss[:, 0:1], func=AF.Sqrt, scale=1.0 / 128, bias=epsT[:, 0:1]), reads=[ssb, epsb], writes=[ssb])
                    kb.op('dve', lambda e, ss=ss: e.reciprocal(out=ss[:, 2:3], in_=ss[:, 1:2]), reads=[ssb], writes=[ssb])
                    kb.op('dve', lambda e, ss=ss, zt_=zt_, gs=gs, c=c, hl=hl: e.scalar_tensor_tensor(out=zt_[:, hl * 128:(hl + 1) * 128], in0=oacc[:, c, hl * 128:(hl + 1) * 128], scalar=ss[:, 2:3],
                                                                                                   in1=gs[:, hl * 128:(hl + 1) * 128], op0=ALU.mult, op1=ALU.mult), reads=[ssb, oaccb[c], gsb], writes=[ztb])
                kb.op('pool', lambda e, zt_=zt_, r0=r0: e.dma_start(out=Z.rows(r0 - LC, 0, 256), in_=zt_[:, :]), reads=[ztb], dma=True)


class ChunkedDram:
    def __init__(self, nc, name, nrows, rows_per, width, ranks=1, dtype=BF16):
        self.rp = rows_per
        self.n = nrows // rows_per
        assert self.n * rows_per == nrows
        self.ranks = ranks
        self.tiles = [dram_tmp(nc, '%s%d' % (name, i), [ranks * rows_per, width], dtype) for i in range(self.n)]
        self.bufs = [Buf() for _ in range(self.n)]

    def rows(self, r0, c0, c1, n=128):
        i, lr = r0 // self.rp, r0 % self.rp
        return self.tiles[i][lr:lr + n, c0:c1]

    def src(self, r0):
        i, lr = r0 // self.rp, r0 % self.rp
        return [self.tiles[i][r * self.rp + lr:r * self.rp + lr + 128, :] for r in range(self.ranks)], [self.bufs[i]]

    def lat_groups(self):
        out = []
        tpc = self.rp // 128
        for i in range(self.n):
            b0, b1 = max(tpc * i - 2, 0), min(tpc * i + tpc - 2, 64)
            if b1 <= b0:
                continue
            lrow = (b0 + 2) * 128 - i * self.rp
            out.append((self.tiles[i][lrow:lrow + (b1 - b0) * 128, 192:256].rearrange("(b a) c -> b a c", a=128), b0, b1))
        return out


GROUPS = [[0, 1, 2, 3], [4, 5, 6, 7]]


def all_gather(kb, src, dst):
    for i in range(src.n):
        kb.op('pool', lambda e, i=i: e.collective_compute("AllGather", ALU.bypass, replica_groups=GROUPS, ins=[src.tiles[i].opt()], outs=[dst.tiles[i].opt()]),
              writes=[dst.bufs[i]], cc=True)


FUSED_INPUTS = {'xin': [LT, D], 'sil_in': [128, 8, 2], 'adaw': [D, 2048], 'adab': [128, 16], 'normw': [128, 8], 'wfm': [D, NFM], 'wg': [D, 256],
                'cs64': [64, 128], 'ident': [128, 128],
                'wout0p': [D, D], 'adawg0': [D, D], 'adabr0': [1, D], 'npostb0': [128, D], 'adaw1': [D, 2048], 'adab1': [128, 16], 'normw1': [128, 8],
                'win1': [D, D], 'ropec': [L, 64], 'ropes': [L, 64],
                'wout1': [D, D], 'adawg1': [D, D], 'adabr1': [1, D], 'npostb1': [128, D]}


def build_fused():
    nc = bass.Bass("TRN2", target_bir_lowering=False)
    kb = KB(nc)
    IN = {k_: dram_in(nc, k_, shp) for k_, shp in FUSED_INPUTS.items()}
    for k_, shp in RW_PARAMS.items():
        IN['p_' + k_] = dram_in(nc, 'p_' + k_, shp)
    for k_, shp in FN_PARAMS.items():
        IN['f_' + k_] = dram_in(nc, 'f_' + k_, shp)
    for k_, shp in RT_PARAMS.items():
        IN['r_' + k_] = dram_in(nc, 'r_' + k_, shp)
    OUT = dram_out(nc, 'out', [L, D])
    YB = ChunkedDram(nc, 'Yb', LT, 768, 256)
    YG = ChunkedDram(nc, 'Yg', LT, 768, 256, ranks=4)
    ZB = ChunkedDram(nc, 'Zb', L, 512, 256)
    ZG = ChunkedDram(nc, 'Zg', L, 512, 256, ranks=4)
    X1s = dram_tmp(nc, 'X1s', [L, D], F32)
    C = {}
    C['bar'] = kb.sb('bar', [128, 1], F32)
    C['ident_f'] = kb.sb('ident_f', [128, 128], F32)
    C['ident_bf'] = kb.sb('ident_bf', [128, 128], BF16)
    C['identfb'], = load_consts(kb, [(C['ident_f'], IN['ident'])])
    C['identb'] = Buf()
    kb.op('dve', lambda e: e.tensor_copy(out=C['ident_bf'][:], in_=C['ident_f'][:]), reads=[C['identfb']], writes=[C['identb']])
    C['epsT'] = kb.sb('epsT', [128, 1], F32)
    C['epsb'] = Buf()
    kb.op('dve', lambda e: e.memset(C['epsT'][:], EPS), writes=[C['epsb']])
    C['ones1'] = kb.sb('ones1', [1, 128], F32)
    C['ones1b'] = Buf()
    kb.op('dve', lambda e: e.memset(C['ones1'][:], 1.0), writes=[C['ones1b']])
    part1(kb, nc, IN, YB, C)
    all_gather(kb, YB, YG)
    part2(kb, nc, IN, YG, X1s, ZB, C)
    all_gather(kb, ZB, ZG)
    part3(kb, nc, IN, ZG, X1s, OUT, C)
    return nc, kb


def l1_inputs(inp, core):
    b, q = core // 4, core % 4
    f = lambda a: np.ascontiguousarray(a, dtype=np.float32)
    xin = np.concatenate([inp['ctx'][b], inp['x'][b]], axis=0)
    sil = np.stack([inp['c'][b].reshape(8,128).T, inp['c_ctx'].reshape(8,128).T], axis=-1)
    adaw = inp['ada_w'][0][:, :2048]
    adab = inp['ada_b'][0][:2048].reshape(16,128).T
    normw = inp['norm_pre'][0].reshape(8,128).T
    W = inp['ev_w_in'][0]
    cols = np.concatenate([np.arange(192)+192*q, 768+np.arange(192)+192*q, 1536+np.arange(192)+192*q,
                           np.arange(2304,2432), np.arange(2432,2560), 3328+64*q+np.arange(64)])
    gcols = np.concatenate([2560+192*q+np.arange(192), 3584+64*q+np.arange(64)])
    c = np.arange(64)
    ang = 2*np.pi*np.outer(c,c)/64
    cs64 = np.concatenate([np.cos(ang), np.sin(ang)], axis=1)
    return dict(xin=f(xin), sil_in=f(sil), adaw=f(adaw), adab=f(adab), normw=f(normw), wfm=f(W[:, cols]), wg=f(W[:, gcols]),
                cs64=f(cs64), ident=np.eye(128, dtype=np.float32)), cols, gcols

def rw_params(inp, core):
    b, q = core // 4, core % 4
    f = lambda a: np.ascontiguousarray(a, dtype=np.float32)
    chs = 192*q + np.arange(192)
    def blk2(v):
        o = np.zeros((128,2), np.float32); o[:,0] = v[:128]; o[:64,1] = v[128:]; return o
    mu = inp['ev_mu'][0]
    mub = np.zeros((128,8), np.float32)
    for j, base in enumerate([0, 768, 1536]):
        m2 = blk2(mu[base+chs]); mub[:, 2*j] = m2[:,0]; mub[:, 2*j+1] = m2[:,1]
    mub[:, 6] = mu[2304:2432]; mub[:, 7] = mu[2432:2560]
    p = np.arange(128)
    lanem = np.stack([(p%4==0),(p%4==1),(p%4==2),(p%4==3),(p%2==0),(p%2==1)],axis=1).astype(np.float32)
    a0 = np.zeros((128,2,2), np.float32)
    for d in range(2): a0[:, d, :] = blk2(inp['ev_a0'][0][d][chs])
    w2 = np.concatenate([inp['ev_w2'][0][0][:, chs], inp['ev_w2'][0][1][:, chs]], axis=0)
    a2 = np.concatenate([inp['ev_a2'][0][0][:, chs], inp['ev_a2'][0][1][:, chs]], axis=0)
    w0 = np.stack([inp['ev_w0'][0][0][chs], inp['ev_w0'][0][1][chs]])[None]
    s = np.arange(128)[:,None]; t = np.arange(128)[None,:]
    strict = [(s<t), (s>t)]; incl = [(s<=t), (s>=t)]
    blk = lambda bs: (np.arange(128)[:,None]//bs == np.arange(128)[None,:]//bs)
    maskSI2 = np.stack([np.concatenate([strict[d], incl[d]],axis=1) for d in range(2)], axis=1).astype(np.float32)
    maskSI = np.stack([np.concatenate([strict[d] & blk(32), incl[d]],axis=1) for d in range(2)], axis=1).astype(np.float32)
    maskST = np.stack([np.stack([(strict[d] & blk(32)).T, (strict[d] & blk(64) & ~blk(32)).T, (strict[d] & ~blk(64)).T], axis=1) for d in range(2)], axis=1).astype(np.float32)
    cdec = -np.exp(-0.5)
    tri = np.stack([np.concatenate([incl[d], strict[d]],axis=1) for d in range(2)], axis=1).astype(np.float32)*cdec
    eh = np.zeros((128,2,4), np.float32); eh[:64,0,0]=1; eh[64:,0,1]=1; eh[:64,1,2]=1
    obd = np.zeros((128,128), np.float32); obd[:64,:64]=1; obd[64:,64:]=1
    return dict(p_mu=mub, p_lanem=lanem, p_k_k=blk2(inp['ev_k_k'][0][chs]), p_k_a=blk2(inp['ev_k_a'][0][chs]),
                p_r_k=blk2(inp['ev_r_k'][0].reshape(-1)[chs]), p_a0=a0, p_w2=f(w2), p_a2=f(a2), p_w0=f(w0),
                p_maskSI=f(maskSI), p_maskSI2=f(maskSI2), p_maskST=f(maskST), p_tri=f(tri), p_ehead=eh, p_ones_bd=obd,
                p_lnx_g=f(np.tile(inp['ev_lnx_g'][0][chs][None], (128,1))), p_lnx_b=f(np.tile(inp['ev_lnx_b'][0][chs][None], (128,1))))

def fn_params():
    f = lambda a: np.ascontiguousarray(a, dtype=np.float32)
    a = np.arange(128); b = np.arange(64)
    ang128 = 2*np.pi*np.outer(a,a)/128
    tw = 2*np.pi*np.outer(a, b)/8192
    ang64 = 2*np.pi*np.outer(b,b)/64
    fl1 = np.concatenate([np.cos(ang64), np.sin(ang64)], axis=0)
    fl2 = np.concatenate([-np.sin(ang64), np.cos(ang64)], axis=0)
    l = np.arange(256); ang256 = 2*np.pi*np.outer(l,l)/256
    c256 = np.cos(ang256).reshape(2,128,256).transpose(1,0,2); ns256 = (-np.sin(ang256)).reshape(2,128,256).transpose(1,0,2)
    return dict(f_c128=f(np.cos(ang128)), f_ns128=f(-np.sin(ang128)), f_twc=f(np.cos(tw)), f_tws=f(np.sin(tw)), f_fl1=f(fl1), f_fl2=f(fl2),
                f_c256=f(c256), f_ns256=f(ns256))


def gate_inputs(inp, b, layer):
    f = lambda a: np.ascontiguousarray(a, dtype=np.float32)
    sil = np.stack([inp['c'][b].reshape(8,128).T, inp['c_ctx'].reshape(8,128).T], axis=-1)
    return dict(sil_in=f(sil), adawg=f(inp['ada_w'][layer][:, 2048:3072]), adabr=f(inp['ada_b'][layer][2048:3072][None]),
                npostb=f(np.tile(inp['norm_post'][layer][None], (128,1))))
def l3_inputs(inp, core, z_full, x1_full):
    b, qtr = core // 4, core % 4
    f = lambda a: np.ascontiguousarray(a, dtype=np.float32)
    sl = slice(2048*qtr, 2048*(qtr+1))
    m = dict(zT=f(z_full[b][sl].T), x1=f(x1_full[b][sl]), wout=f(inp['od_w_out'][0]))
    m.update(gate_inputs(inp, b, 1))
    return m


def l2_inputs(inp, core, y_full):
    b, p = core // 4, core % 4
    f = lambda a: np.ascontiguousarray(a, dtype=np.float32)
    m = dict(xin=f(np.concatenate([inp['ctx'][b], inp['x'][b]], axis=0)), wout=f(inp['ev_w_out'][0]))
    if y_full is not None:
        m['yT'] = f(y_full[b].T)
    m.update(gate_inputs(inp, b, 0))
    m['adaw1'] = f(inp['ada_w'][1][:, :2048]); m['adab1'] = f(inp['ada_b'][1][:2048].reshape(16,128).T); m['normw1'] = f(inp['norm_pre'][1].reshape(8,128).T)
    cols = np.concatenate([off + 256*p + np.arange(256) for off in (0, 1024, 2048, 3072)])
    m['win1'] = f(inp['od_w_in'][0][:, cols])
    pos = np.arange(8192); row = pos//64; col = pos%64
    inv = 10000.0 ** (-np.arange(0, 64, 2, dtype=np.float32)/64)
    ang = np.concatenate([row[:,None]*inv[None], col[:,None]*inv[None]], axis=1).astype(np.float32)
    m['ropec'] = f(np.cos(ang)); m['ropes'] = f(np.sin(ang))
    m['ident'] = np.eye(128, dtype=np.float32)
    lg = inp['od_decay_logit'][0]
    logit = np.zeros((128,4), np.float32)
    for hl in range(2):
        for dr in range(2): logit[:, 2*hl+dr] = lg[dr][2*p+hl]
    j = np.arange(128)[:,None]; i = np.arange(128)[None,:]
    diffT = np.stack([(i-j)*np.ones((128,128)), (j-i)*np.ones((128,128))], axis=1)
    mask = np.stack([(i>=j), (j>i)], axis=1)
    posxi = np.stack([np.tile((np.arange(128)+1)[None], (128,1)), np.tile((128-np.arange(128))[None], (128,1))], axis=1)
    pze = np.stack([127-np.arange(128), np.arange(128)], axis=1)
    m.update(r_logit=logit, r_diffT=f(diffT), r_mask01T=f(mask), r_posxi=f(posxi), r_poszeta=f(pze))
    return m


def fused_inputs(inp, core):
    b, q = core // 4, core % 4
    f = lambda a: np.ascontiguousarray(a, dtype=np.float32)
    m, _, _ = l1_inputs(inp, core)
    m.update(rw_params(inp, core))
    m.update(fn_params())
    m2 = l2_inputs(inp, core, None)
    g0 = gate_inputs(inp, b, 0)
    g1 = gate_inputs(inp, b, 1)
    perm = np.concatenate([np.concatenate([192 * r + np.arange(192), 768 + 64 * r + np.arange(64)]) for r in range(4)])
    m['wout0p'] = f(inp['ev_w_out'][0][perm])
    m['adawg0'], m['adabr0'], m['npostb0'] = g0['adawg'], g0['adabr'], g0['npostb']
    m['adawg1'], m['adabr1'], m['npostb1'] = g1['adawg'], g1['adabr'], g1['npostb']
    m['wout1'] = f(inp['od_w_out'][0])
    for k_ in ('adaw1', 'adab1', 'normw1', 'win1', 'ropec', 'ropes', 'r_logit', 'r_diffT', 'r_mask01T', 'r_posxi', 'r_poszeta'):
        m[k_] = m2[k_]
    return m


def kernel(**inputs):
    inp = {k: np.asarray(v) for k, v in inputs.items()}
    cores = list(range(8))
    nc, kb = build_fused()
    kb.emit()
    maps = [fused_inputs(inp, core) for core in cores]
    res = run_bass_kernel_spmd(nc, maps, core_ids=cores).results
    return np.stack([res[0]['out'], res[4]['out']]).astype(np.float32)
```

```python
import numpy as np
import concourse.bass as bass
import concourse.mybir as mybir
from contextlib import ExitStack
from concourse.bass_utils import run_bass_kernel_spmd

F32 = mybir.dt.float32
BF16 = mybir.dt.bfloat16
ALU = mybir.AluOpType
AF = mybir.ActivationFunctionType

ENGS = ['pe', 'act', 'dve', 'pool', 'sp']
DBG = {}
NDSEM = 8


class Buf:
    __slots__ = ('name', 'w', 'r', 'excl', 'strict')

    def __init__(self, name='', excl=False, strict=False):
        self.name = name
        self.w = None
        self.r = {}
        self.excl = excl
        self.strict = strict


class Op:
    __slots__ = ('eng', 'fn', 'deps', 'signal', 'sig', 'sem', 'target', 'isdma', 'final', 'cc')


class _Rec:
    def __init__(self):
        self.call = None

    def __getattr__(self, name):
        def f(*a, **k):
            self.call = (name, a, k)
            return self
        return f


def _flat(xs):
    out = []
    for x in xs:
        if isinstance(x, (list, tuple)):
            out.extend(_flat(x))
        else:
            out.append(x)
    return out


class KB:
    def __init__(self, nc):
        self.nc = nc
        self.ops = {e: [] for e in ENGS}
        self.phase = Buf('phase')
        self.es = ExitStack()
        self.nops = 0

    def sb(self, name, shape, dtype, es=None):
        self.nalloc = getattr(self, 'nalloc', 0) + 1
        return (es or self.es).enter_context(self.nc.sbuf_tensor('s%d_%s' % (self.nalloc, name), list(shape), dtype))

    def ps(self, name, shape, dtype=F32, es=None):
        self.nalloc = getattr(self, 'nalloc', 0) + 1
        return (es or self.es).enter_context(self.nc.psum_tensor('p%d_%s' % (self.nalloc, name), list(shape), dtype))

    def op(self, eng, fn, reads=(), writes=(), dma=False, final=False, nophase=False, cc=False):
        dma = dma or cc
        o = Op()
        rec = _Rec()
        fn(rec)
        assert rec.call is not None
        o.eng = eng; o.fn = rec.call; o.isdma = dma; o.signal = dma; o.deps = []; o.sig = 0
        o.sem = None; o.target = 0; o.final = final; o.cc = cc
        reads = _flat(reads)
        writes = _flat(writes)
        for b in list(reads):
            if b.excl:
                reads.remove(b)
                if b not in writes:
                    writes.append(b)
        if not nophase:
            reads.append(self.phase)
        deps = {}
        sdeps = set()
        for b in reads:
            if b.w is not None:
                deps[id(b.w)] = b.w
                if b.strict:
                    sdeps.add(id(b.w))
        for b in writes:
            if b.w is not None:
                deps[id(b.w)] = b.w
                if b.strict:
                    sdeps.add(id(b.w))
            for r in b.r.values():
                deps[id(r)] = r
                if b.strict:
                    sdeps.add(id(r))
        for d in deps.values():
            if d is o:
                continue
            if d.isdma or dma or d.eng != eng or id(d) in sdeps or (eng != 'pe' and DBG.get('strict_all', True)):
                o.deps.append(d)
                d.signal = True
        key = ('dma', self.nops) if dma else eng
        for b in reads:
            b.r[key] = o
        for b in writes:
            b.w = o
            b.r = {}
        self.ops[eng].append(o)
        self.nops += 1
        return o

    def barrier(self, tile_ap):
        self.op('dve', lambda e: e.memset(tile_ap, 0.0), writes=[self.phase], nophase=True)

    def emit(self):
        nc = self.nc
        es = self.es
        csem = {}
        for e in ['pe', 'act', 'dve', 'pool']:
            csem[e] = es.enter_context(nc.semaphore('c_' + e))
        dsem = {}
        for e in ENGS:
            if any(o.isdma for o in self.ops[e]):
                dsem[e] = [es.enter_context(nc.semaphore('d_%s%d' % (e, i))) for i in range(NDSEM)]
        for e in ENGS:
            cnt = 0
            nd = 0
            hist = []
            for o in self.ops[e]:
                if o.cc:
                    self.ncc = getattr(self, 'ncc', 0) + 1
                    o.sem = es.enter_context(nc.semaphore('ccs%d' % self.ncc))
                    o.target = 1
                elif o.isdma:
                    slot = nd % NDSEM
                    o.sem = dsem[e][slot]
                    o.target = 16 * (nd // NDSEM + 1)
                    if nd >= NDSEM:
                        o.deps.append(hist[nd - NDSEM])
                    hist.append(o)
                    nd += 1
                elif o.signal:
                    cnt += 1
                    o.sig = cnt
                    o.sem = csem[e]
                    o.target = cnt
        finals = [o for e in ENGS for o in self.ops[e] if o.final]

        def run(engname):
            def body(e):
                waited = {}
                for o in self.ops[engname]:
                    need = {}
                    for d in o.deps:
                        k = id(d.sem)
                        if waited.get(k, 0) < d.target and need.get(k, (None, 0))[1] < d.target:
                            need[k] = (d.sem, d.target)
                    for k, (sem, tgt) in need.items():
                        e.wait_ge(sem, tgt)
                        waited[k] = tgt
                    nm_, a_, k_ = o.fn
                    ins = getattr(e, nm_)(*a_, **k_)
                    if o.signal:
                        ins.then_inc(o.sem, 16 if (o.isdma and not o.cc) else 1)
                if engname == 'sp':
                    for d in finals:
                        k = id(d.sem)
                        if waited.get(k, 0) < d.target:
                            e.wait_ge(d.sem, d.target)
                            waited[k] = d.target
            return body

        with nc.Block() as block:
            block.tensor(run('pe'))
            block.scalar(run('act'))
            block.vector(run('dve'))
            block.gpsimd(run('pool'))
            block.sync(run('sp'))


class Ring:
    def __init__(self, kb, name, shape, dtype, n, psum=False, es=None, strict=False):
        self.n = n
        self.i = 0
        self.slots = []
        for j in range(n):
            t = (kb.ps if psum else kb.sb)('%s%d' % (name, j), shape, dtype, es=es)
            b = Buf('%s%d' % (name, j), excl=psum, strict=strict)
            self.slots.append((t, b))
            if not psum and DBG.get('init_rings', True):
                kb.op('pool', lambda e, t=t: e.memset(t[:], 0.0), writes=[b])

    def next(self):
        s = self.slots[self.i % self.n]
        self.i += 1
        return s


D = 1024
L = 8192
LC = 256
LT = L + LC
NCH = LT // 128
EPS = 1e-6
GN_EPS = 64e-5
FM_BLOCKS = {'r01': (0, 128), 'r2': (128, 64), 'k01': (192, 128), 'k2': (320, 64),
             'v01': (384, 128), 'v2': (512, 64), 'wl': (576, 128), 'al': (704, 128), 'four': (832, 64)}
NFM = 896
TILES = [(0, 256)] + [(256 + 512 * i, 512) for i in range(16)]


def dram_in(nc, name, shape, dtype=F32):
    return nc.dram_tensor(name, list(shape), dtype, kind="ExternalInput").ap()


def dram_out(nc, name, shape, dtype=F32):
    return nc.dram_tensor(name, list(shape), dtype, kind="ExternalOutput").ap()


def dram_tmp(nc, name, shape, dtype=F32, debug=False):
    return nc.dram_tensor(name, list(shape), dtype, kind="ExternalOutput" if debug else "Internal").ap()


def load_consts(kb, items, eng='sp'):
    bufs = []
    for t, src in items:
        b = Buf()
        kb.op(eng, (lambda e, t=t, src=src: e.dma_start(out=t[:], in_=src)), writes=[b], dma=True)
        bufs.append(b)
    return bufs


def adaln_vectors(kb, es, nc, sil_in, adaw, adab, normw, ncolblk, mod_tile=None):
    sil = kb.sb('sil', [128, 8, 2], F32, es)
    silb = Buf(strict=True)
    kb.op('sp', lambda e: e.dma_start(out=sil[:], in_=sil_in), writes=[silb], dma=True)
    kb.op('act', lambda e: e.activation(out=sil[:], in_=sil[:], func=AF.Silu), reads=[silb], writes=[silb])
    adb = kb.sb('adb', [128, ncolblk], F32, es)
    adbb = Buf(strict=True)
    kb.op('sp', lambda e: e.dma_start(out=adb[:], in_=adab), writes=[adbb], dma=True)
    mod = mod_tile if mod_tile is not None else kb.sb('mod', [128, ncolblk, 2], F32, es)
    modb = Buf(strict=True)
    psm = kb.ps('psmod', [128, ncolblk, 2], F32, es)
    psb = Buf(excl=True)
    wt = kb.sb('adaw', [128, 8, ncolblk * 128], F32, es)
    wb = Buf()
    for k in range(8):
        kb.op('sp', lambda e, k=k: e.dma_start(out=wt[:, k, :], in_=adaw[k * 128:(k + 1) * 128, :]), writes=[wb], dma=True)
    for cb in range(ncolblk):
        for k in range(8):
            kb.op('pe', lambda e, k=k, cb=cb: e.matmul(psm[:, cb, :], lhsT=wt[:, k, cb * 128:(cb + 1) * 128],
                                                      rhs=sil[:, k, :], start=(k == 0), stop=(k == 7)),
                  reads=[wb, silb], writes=[psb])
    for n in range(2):
        kb.op('dve', lambda e, n=n: e.tensor_tensor(out=mod[:, :, n], in0=psm[:, :, n], in1=adb[:, :], op=ALU.add),
              reads=[psb, adbb], writes=[modb])
    return mod, modb


def phase_proj(kb, nc, es, xin, Wsrc_list, G, shiftv, Gb, ident_bf, identb, emit_tile, nmod=2, get_x=None, shift_off=0):
    xring = Ring(kb, 'xt', [128, D], F32, 3, es=es)
    xnring = Ring(kb, 'xn', [128, D], BF16, 2, es=es)
    stat = Ring(kb, 'stat', [128, 4], F32, 4, es=es, strict=True)
    junk = kb.sb('junk', [128, D], BF16, es)
    junkb = Buf()
    hring = Ring(kb, 'hT', [128, 8, 512], BF16, 2, es=es)
    ptr = Ring(kb, 'ptr', [128, 8, 128], BF16, 2, psum=True, es=es)
    for (t0, T) in TILES:
        n = 1 if t0 < LC else 0
        hT, hb = hring.next()
        for sub in range(T // 128):
            r0 = t0 + sub * 128
            if get_x is not None:
                xt, xb = get_x(r0)
            else:
                xt, xb = xring.next()
                kb.op('sp', lambda e, xt=xt, r0=r0: e.dma_start(out=xt[:], in_=xin[r0:r0 + 128, :]), writes=[xb], dma=True)
            st, sb_ = stat.next()
            kb.op('act', lambda e, xt=xt, st=st: e.activation(out=junk[:], in_=xt[:], func=AF.Square, accum_out=st[:, 0:1]),
                  reads=[xb], writes=[junkb, sb_])
            kb.op('act', lambda e, st=st: e.activation(out=st[:, 1:2], in_=st[:, 0:1], func=AF.Sqrt, scale=1.0 / D, bias=Gb['eps'][:, 0:1]),
                  reads=[sb_, Gb['epsb']], writes=[sb_])
            kb.op('dve', lambda e, st=st: e.reciprocal(out=st[:, 2:3], in_=st[:, 1:2]), reads=[sb_], writes=[sb_])
            xn, xnb = xnring.next()
            kb.op('dve', lambda e, xn=xn, xt=xt, st=st: e.tensor_scalar(out=xn[:], in0=xt[:], scalar1=st[:, 2:3], scalar2=None, op0=ALU.mult),
                  reads=[xb, sb_], writes=[xnb])
            pt, pb = ptr.next()
            for k in range(8):
                kb.op('pe', lambda e, pt=pt, xn=xn, k=k: e.transpose(out=pt[:, k, :], in_=xn[:, k * 128:(k + 1) * 128], identity=ident_bf[:]),
                      reads=[xnb, identb], writes=[pb])
            for k in range(8):
                if k % 2 == 0:
                    kb.op('act', lambda e, pt=pt, hT=hT, k=k, sub=sub, n=n: e.activation(
                        out=hT[:, k, sub * 128:(sub + 1) * 128], in_=pt[:, k, :], func=AF.Identity,
                        scale=G[:, k, n:n + 1], bias=shiftv[:, shift_off + k, n:n + 1]), reads=[pb, Gb['G']], writes=[hb])
                else:
                    kb.op('dve', lambda e, pt=pt, hT=hT, k=k, sub=sub, n=n: e.tensor_scalar(
                        out=hT[:, k, sub * 128:(sub + 1) * 128], in0=pt[:, k, :], scalar1=G[:, k, n:n + 1],
                        scalar2=shiftv[:, shift_off + k, n:n + 1], op0=ALU.mult, op1=ALU.add), reads=[pb, Gb['G']], writes=[hb])
        emit_tile(t0, T, hT, hb)


def load_weight_bf16(kb, es, nc, name, src, ncols, es_keep):
    wbf = kb.sb(name, [128, 8, ncols], BF16, es_keep)
    wb = Buf()
    stg = Ring(kb, name + '_stg', [128, ncols], F32, 2, es=es)
    for k in range(8):
        s, sbuf_ = stg.next()
        kb.op('sp', lambda e, s=s, k=k: e.dma_start(out=s[:], in_=src[k * 128:(k + 1) * 128, :]), writes=[sbuf_], dma=True)
        eng = 'dve' if k % 2 == 0 else 'pool'
        kb.op(eng, lambda e, s=s, k=k: e.tensor_copy(out=wbf[:, k, :], in_=s[:]), reads=[sbuf_], writes=[wb])
    return wbf, wb


RW_PARAMS = {'mu': [128, 8], 'lanem': [128, 6], 'k_k': [128, 2], 'k_a': [128, 2], 'r_k': [128, 2], 'a0': [128, 2, 2],
             'w2': [128, 192], 'a2': [128, 192], 'w0': [1, 2, 192], 'maskSI': [128, 2, 256], 'maskSI2': [128, 2, 256], 'maskST': [128, 2, 3, 128],
             'tri': [128, 2, 256], 'ehead': [128, 2, 4], 'ones_bd': [128, 128], 'lnx_g': [128, 192], 'lnx_b': [128, 192]}


def part1(kb, nc, IN, YD, C):
    es = kb.es
    debug = False
    stop_after = None
    xin, sil_in, adaw, adab, normw, wfm, wg, cs64 = (IN[k_] for k_ in ('xin', 'sil_in', 'adaw', 'adab', 'normw', 'wfm', 'wg', 'cs64'))
    U = {nm: dram_tmp(nc, 'U_' + nm, [sz, LT], F32, debug) for nm, (off, sz) in FM_BLOCKS.items() if nm != 'four'}
    GT = dram_tmp(nc, 'GT', [LT, 256], F32, debug)
    HT = dram_tmp(nc, 'HT', [LT, 128], F32, debug)
    bar, ident_f, identfb, ident_bf, identb, epsT, epsb = C['bar'], C['ident_f'], C['identfb'], C['ident_bf'], C['identb'], C['epsT'], C['epsb']
    with ExitStack() as pa:
        mod, modb = adaln_vectors(kb, pa, nc, sil_in, adaw, adab, normw, 16)
        nw = kb.sb('nw', [128, 8], F32, pa)
        nwb, = load_consts(kb, [(nw, normw)])
        G = kb.sb('G', [128, 8, 2], F32, pa)
        Gbuf = Buf(strict=True)
        for n in range(2):
            kb.op('dve', lambda e, n=n: e.scalar_tensor_tensor(out=G[:, :, n], in0=mod[:, 8:16, n], scalar=1.0, in1=nw[:, :],
                                                                op0=ALU.add, op1=ALU.mult), reads=[modb, nwb], writes=[Gbuf])
        Gb = {'G': Gbuf, 'eps': epsT, 'epsb': epsb}
        shiftv = mod
        wbf, wbb = load_weight_bf16(kb, pa, nc, 'wfm_bf', wfm, NFM, pa)
        wgb, wgbb = load_weight_bf16(kb, pa, nc, 'wg_bf', wg, 256, pa)
        cs = kb.sb('cs64', [64, 128], F32, pa)
        csb, = load_consts(kb, [(cs, cs64)])
        pmm = Ring(kb, 'pmm', [128, 512], F32, 3, psum=True, es=pa)
        pg = Ring(kb, 'pg', [128, 512], F32, 1, psum=True, es=pa)
        ustg = Ring(kb, 'ustg', [128, 512], F32, 4, es=pa)
        gstg = Ring(kb, 'gstg', [128, 256], F32, 3, es=pa)
        hstg = Ring(kb, 'hstg', [128, 128], F32, 3, es=pa)
        fstg = Ring(kb, 'fstg', [64, 512], F32, 2, es=pa)
        cnt = [0]

        def emit_tile(t0, T, hT, hb):
            for nm, (off, sz) in FM_BLOCKS.items():
                ps, psb = pmm.next()
                for k in range(8):
                    kb.op('pe', lambda e, ps=ps, k=k, off=off, sz=sz: e.matmul(ps[0:sz, 0:T], lhsT=wbf[:, k, off:off + sz], rhs=hT[:, k, 0:T],
                                                                               start=(k == 0), stop=(k == 7)), reads=[wbb, hb], writes=[psb])
                if nm == 'four':
                    fs, fsb = fstg.next()
                    kb.op('act', lambda e, fs=fs, ps=ps: e.activation(out=fs[:, 0:T], in_=ps[0:64, 0:T], func=AF.Copy), reads=[psb], writes=[fsb])
                    for sub in range(T // 128):
                        pgt, pgb = pg.next()
                        kb.op('pe', lambda e, pgt=pgt, fs=fs, sub=sub: e.matmul(pgt[:, 0:128], lhsT=fs[:, sub * 128:(sub + 1) * 128], rhs=cs[:, :],
                                                                                  start=True, stop=True), reads=[fsb, csb], writes=[pgb])
                        hs, hsb = hstg.next()
                        kb.op('dve', lambda e, hs=hs, pgt=pgt: e.tensor_copy(out=hs[:], in_=pgt[:, 0:128]), reads=[pgb], writes=[hsb])
                        r0 = t0 + sub * 128
                        kb.op('pool', lambda e, hs=hs, r0=r0: e.dma_start(out=HT[r0:r0 + 128, :], in_=hs[:]), reads=[hsb], dma=True)
                else:
                    us, usb = ustg.next()
                    cnt[0] += 1
                    if cnt[0] % 2 == 0:
                        kb.op('act', lambda e, us=us, ps=ps, sz=sz: e.activation(out=us[0:sz, 0:T], in_=ps[0:sz, 0:T], func=AF.Copy), reads=[psb], writes=[usb])
                    else:
                        kb.op('dve', lambda e, us=us, ps=ps, sz=sz: e.tensor_copy(out=us[0:sz, 0:T], in_=ps[0:sz, 0:T]), reads=[psb], writes=[usb])
                    kb.op('pool', lambda e, us=us, nm=nm, sz=sz: e.dma_start(out=U[nm][:, t0:t0 + T], in_=us[0:sz, 0:T]), reads=[usb], dma=True)
            for sub in range(T // 128):
                pgt, pgb = pg.next()
                for k in range(8):
                    kb.op('pe', lambda e, pgt=pgt, k=k, sub=sub: e.matmul(pgt[:, 0:256], lhsT=hT[:, k, sub * 128:(sub + 1) * 128], rhs=wgb[:, k, :],
                                                                           start=(k == 0), stop=(k == 7)), reads=[wgbb, hb], writes=[pgb])
                gs, gsb = gstg.next()
                kb.op('act', lambda e, gs=gs, pgt=pgt: e.activation(out=gs[:], in_=pgt[:, 0:256], func=AF.Silu), reads=[pgb], writes=[gsb])
                r0 = t0 + sub * 128
                kb.op('pool', lambda e, gs=gs, r0=r0: e.dma_start(out=GT[r0:r0 + 128, :], in_=gs[:]), reads=[gsb], dma=True)

        phase_proj(kb, nc, pa, xin, None, G, shiftv, Gb, ident_bf, identb, emit_tile)
        kb.barrier(bar[:])
    BLm = ['r01', 'r2', 'k01', 'k2', 'v01', 'v2', 'wl', 'al']
    S = {nm: dram_tmp(nc, 'S_' + nm, [FM_BLOCKS[nm][1], LT], F32, debug) for nm in BLm}
    with ExitStack() as pm:
        mu_t = kb.sb('mu_m', [128, 8], F32, pm)
        lan_t = kb.sb('lan_m', [128, 6], F32, pm)
        mub_, lanb_ = load_consts(kb, [(mu_t, IN['p_mu']), (lan_t, IN['p_lanem'])])
        cf = kb.sb('coef_m', [128, 8, 7], F32, pm)
        cfb = Buf(strict=True)
        kb.op('dve', lambda e: e.tensor_scalar(out=cf[:, :, 0], in0=mu_t[:, :], scalar1=-1.0, scalar2=1.0, op0=ALU.mult, op1=ALU.add), reads=[mub_], writes=[cfb])
        for l in range(6):
            kb.op('dve', lambda e, l=l: e.tensor_scalar(out=cf[:, :, 1 + l], in0=mu_t[:, :], scalar1=lan_t[:, l:l + 1], scalar2=None, op0=ALU.mult), reads=[mub_, lanb_], writes=[cfb])
        wr = Ring(kb, 'winm', [128, 640], F32, 4, es=pm)
        so = Ring(kb, 'smo', [128, 512], F32, 4, es=pm)
        for (t0, T) in TILES:
            isctx = t0 < LC
            seg_lo, seg_hi = (0, LC) if isctx else (LC, LT)
            halo = 1 if isctx else 64
            lo = max(t0 - halo, seg_lo)
            hi = min(t0 + T + halo, seg_hi)
            for bi, nm in enumerate(BLm):
                sz = FM_BLOCKS[nm][1]
                w_, wb_ = wr.next()
                if lo > t0 - halo or hi < t0 + T + halo:
                    kb.op('pool', lambda e, w_=w_: e.memset(w_[:], 0.0), writes=[wb_])
                kb.op('sp', lambda e, w_=w_, nm=nm, sz=sz, lo=lo, hi=hi, t0=t0: e.dma_start(out=w_[0:sz, 64 + lo - t0:64 + hi - t0], in_=U[nm][:, lo:hi]), writes=[wb_], dma=True)
                o_, ob_ = so.next()
                kb.op('dve', lambda e, o_=o_, w_=w_, sz=sz, bi=bi, T=T: e.tensor_scalar(out=o_[0:sz, 0:T], in0=w_[0:sz, 64:64 + T], scalar1=cf[0:sz, bi, 0:1], scalar2=None, op0=ALU.mult),
                      reads=[wb_, cfb], writes=[ob_])
                terms = [(5, -1, None), (6, +1, None)] if isctx else [(1, -1, 'L'), (2, +1, 'R'), (3, -64, None), (4, +64, None)]
                for (ci, off, kind) in terms:
                    def f(e, o_=o_, w_=w_, sz=sz, bi=bi, ci=ci, off=off, kind=kind, T=T):
                        if kind is None:
                            oo = o_[0:sz, 0:T]
                            ii = w_[0:sz, 64 + off:64 + T + off]
                        else:
                            ov = o_[0:sz, 0:T].rearrange("p (r c) -> p r c", c=64)
                            iv = w_[0:sz, 64:64 + T].rearrange("p (r c) -> p r c", c=64)
                            if kind == 'L':
                                oo, ii = ov[:, :, 1:64], iv[:, :, 0:63]
                            else:
                                oo, ii = ov[:, :, 0:63], iv[:, :, 1:64]
                        return e.scalar_tensor_tensor(out=oo, in0=ii, scalar=cf[0:sz, bi, ci:ci + 1], in1=oo, op0=ALU.mult, op1=ALU.add)
                    kb.op('dve', f, reads=[wb_, cfb, ob_], writes=[ob_])
                kb.op('pool', lambda e, o_=o_, nm=nm, sz=sz, t0=t0, T=T: e.dma_start(out=S[nm][:, t0:t0 + T], in_=o_[0:sz, 0:T]), reads=[ob_], dma=True)
        kb.barrier(bar[:])
    if stop_after == 'A':
        return
    with ExitStack() as pb:
        P = {k_: IN['p_' + k_] for k_ in RW_PARAMS}
        YR = YD
        if stop_after != 'skipB':
            phase_rwkv(kb, nc, pb, S, GT, YR, P, ident_f, identfb, ident_bf, identb)
        kb.barrier(bar[:])
    if DBG.get('skip_fnet'):
        return
    PF = {k_: IN['f_' + k_] for k_ in FN_PARAMS}
    YF = YD
    ZS = dram_tmp(nc, 'ZS', [2, 128, 64, 128], F32, False)
    phase_fnet(kb, nc, HT, GT, YF, ZS, PF, bar)
    return


DUMPS = {}


def dump(kb, nc, name, ap, shape, bufs, dt=F32):
    if name in DUMPS:
        return
    t = nc.dram_tensor('dbg_' + name, list(shape), dt, kind="ExternalOutput").ap()
    DUMPS[name] = t
    kb.op('sp', lambda e: e.dma_start(out=t, in_=ap), reads=bufs, dma=True, final=True)

CHUNK_ORDER = {0: list(range(NCH)), 1: [1, 0] + list(range(NCH - 1, 1, -1))}


def phase_rwkv(kb, nc, es, U, GT, YR, P, ident_f, identfb, ident_bf, identb):
    def sbt(name, shape, dt=F32):
        return kb.sb(name, shape, dt, es)

    def const(name, shape, src, dt=F32):
        t = sbt(name, shape, dt)
        b, = load_consts(kb, [(t, src)])
        return t, b

    mu, mub = const('mu', [128, 8], P['mu'])
    lanem, lanemb = const('lanem', [128, 6], P['lanem'])
    kkp, kkpb = const('kkp', [128, 2], P['k_k'])
    kap, kapb = const('kap', [128, 2], P['k_a'])
    rkp, rkpb = const('rkp', [128, 2], P['r_k'])
    a0p, a0pb = const('a0p', [128, 2, 2], P['a0'])
    w2s, w2sb = const('w2s', [128, 192], P['w2'])
    a2s, a2sb = const('a2s', [128, 192], P['a2'])
    w0r, w0rb = const('w0r', [1, 2, 192], P['w0'])
    msi, msib = const('msi', [128, 2, 256], P['maskSI'])
    mst, mstb = const('mst', [128, 2, 3, 128], P['maskST'])
    msi2, msi2b = const('msi2', [128, 2, 256], P['maskSI2'])
    tri, trib = const('tri', [128, 2, 256], P['tri'])
    ehd, ehdb = const('ehd', [128, 2, 4], P['ehead'])
    obd, obdb = const('obd', [128, 128], P['ones_bd'])
    lng, lngb = const('lng', [128, 192], P['lnx_g'])
    lnb, lnbb = const('lnb', [128, 192], P['lnx_b'])
    ones1 = sbt('ones1', [1, 128])
    ones1b = Buf()
    kb.op('dve', lambda e: e.memset(ones1[:], 1.0), writes=[ones1b])
    coef = sbt('coef', [128, 8, 7])
    coefb = Buf(strict=True)
    kb.op('dve', lambda e: e.tensor_scalar(out=coef[:, :, 0], in0=mu[:, :], scalar1=-1.0, scalar2=1.0, op0=ALU.mult, op1=ALU.add),
          reads=[mub], writes=[coefb])
    for l in range(6):
        kb.op('dve', lambda e, l=l: e.tensor_scalar(out=coef[:, :, 1 + l], in0=mu[:, :], scalar1=lanem[:, l:l + 1], scalar2=None, op0=ALU.mult),
              reads=[mub, lanemb], writes=[coefb])
    omka = sbt('omka', [128, 2])
    omkab = Buf(strict=True)
    kb.op('dve', lambda e: e.tensor_scalar(out=omka[:], in0=kap[:], scalar1=-1.0, scalar2=1.0, op0=ALU.mult, op1=ALU.add), reads=[kapb], writes=[omkab])
    gneps = sbt('gneps', [128, 1])
    gnepsb = Buf()
    kb.op('dve', lambda e: e.memset(gneps[:], GN_EPS), writes=[gnepsb])

    yacc = sbt('yacc', [128, NCH, 192])
    yaccb = [Buf() for _ in range(NCH)]
    kb.op('pool', lambda e: e.memset(yacc[:], 0.0), writes=yaccb)
    ST = sbt('ST', [128, 2, 2, 64])
    STbf = sbt('STbf', [128, 2, 2, 64], BF16)
    STb = [[Buf() for _ in range(3)] for _ in range(2)]

    CR, HR = {}, {}
    for d_ in range(2):
        n_ = 'd%d' % d_
        CR[d_] = dict(
            winr=None, smr=Ring(kb, 'smix' + n_, [128, 8, 128], F32, 2, es=es),
            t32=Ring(kb, 't32' + n_, [128, 256], F32, 9, es=es), kkr=Ring(kb, 'kkt' + n_, [128, 2, 128], F32, 1, es=es),
            vtr=Ring(kb, 'vt32' + n_, [128, 192], F32, 2, es=es), vtbr=Ring(kb, 'vtbf' + n_, [128, 192], BF16, 2, es=es),
            sgr=Ring(kb, 'sg' + n_, [128, 192], F32, 1, es=es), e1r=Ring(kb, 'e1' + n_, [128, 2, 256], F32, 2, es=es),
            e2r=Ring(kb, 'e2' + n_, [128, 2, 128], F32, 1, es=es), arr=Ring(kb, 'ar' + n_, [128, 2, 256], BF16, 2, es=es),
            btr=Ring(kb, 'bt' + n_, [128, 2, 128], BF16, 2, es=es), ktr=Ring(kb, 'kt' + n_, [128, 2, 128], BF16, 2, es=es),
            bktr=Ring(kb, 'bkt' + n_, [128, 2, 192], BF16, 2, es=es), bcr=Ring(kb, 'bc' + n_, [128, 4], F32, 2, es=es, strict=True))
        for h_ in range(3):
            n2 = 'd%dh%d' % (d_, h_)
            HR[(d_, h_)] = dict(
                mm1r=Ring(kb, 'mm1' + n2, [128, 256], BF16, 2, es=es), mm2r=Ring(kb, 'mm2' + n2, [128, 256], BF16, 2, es=es),
                xpr=Ring(kb, 'xp' + n2, [128, 256], BF16, 4, es=es), xtr=Ring(kb, 'xt' + n2, [128, 128], BF16, 4, es=es),
                ivr=Ring(kb, 'iv' + n2, [128, 128], BF16, 8, es=es), tmr=Ring(kb, 'tm' + n2, [128, 128], BF16, 2, es=es),
                w1r=Ring(kb, 'w1' + n2, [128, 64], BF16, 2, es=es), utr=Ring(kb, 'ut' + n2, [128, 64], BF16, 2, es=es),
                tsr=Ring(kb, 'ts' + n2, [128, 64], F32, 3, es=es))
    jkr = Ring(kb, 'jk', [128, 64], F32, 2, es=es)
    gtr = Ring(kb, 'gt', [128, 192], F32, 2, es=es)
    fnr = Ring(kb, 'fn', [128, 192], F32, 2, es=es)
    str_ = Ring(kb, 'stt', [128, 8], F32, 6, es=es, strict=True)
    pprep = Ring(kb, 'pprep', [128, 512], F32, 2, psum=True, es=es)
    ptrb = kb.ps('ptrb', [128, 1024], BF16, es)
    ptrbufs = [Buf(excl=True)] * 4
    ptrc = [0]
    pgram = Ring(kb, 'pgram', [128, 512], F32, 1, psum=True, es=es)
    pinv = Ring(kb, 'pinv', [128, 512], F32, 2, psum=True, es=es)
    pseq = kb.ps('pseq', [128, 512], F32, es)
    pseq2 = kb.ps('pseq2', [128, 512], F32, es)
    pseqb = [Buf(excl=True)] * 4 + [Buf(excl=True)] * 4
    pseqt = [pseq] * 4 + [pseq2] * 4
    pseqc = [0]

    def pseq_next():
        i = pseqc[0] % 8
        pseqc[0] += 1
        return pseqt[i][:, (i % 4) * 64:(i % 4 + 1) * 64], pseqb[i], i

    def ptr_next():
        i = ptrc[0] % 4
        ptrc[0] += 1
        return i, ptrbufs[i]

    blkname = [('r01', 'k01', 'v01'), ('r2', 'k2', 'v2')]
    BL = ['r01', 'r2', 'k01', 'k2', 'v01', 'v2', 'wl', 'al']
    BI = {n: i for i, n in enumerate(BL)}
    BSZ = {n: FM_BLOCKS[n][1] for n in BL}
    alt = [0]

    def ew(fn, reads, writes):
        alt[0] += 1
        r_ = _Rec()
        fn(r_)
        stt_ = r_.call[0] == 'scalar_tensor_tensor'
        return kb.op('dve' if (alt[0] % 3 or stt_) else 'pool', fn, reads=reads, writes=writes)

    done = {}
    inflight = {}
    SMB = {}

    def chunk_body(d, c):
        winr, smr, t32, kkr, vtr, vtbr, sgr, e1r, e2r, arr, btr, ktr, bktr, bcr = (CR[d][k_] for k_ in (
            'winr', 'smr', 't32', 'kkr', 'vtr', 'vtbr', 'sgr', 'e1r', 'e2r', 'arr', 'btr', 'ktr', 'bktr', 'bcr'))
        isctx = c < 2
        t0 = c * 128
        seg_lo, seg_hi = (0, LC) if isctx else (LC, LT)
        halo = 1 if isctx else 64
        lo = max(t0 - halo, seg_lo)
        hi = min(t0 + 128 + halo, seg_hi)
        sm, smb0 = smr.next()
        smb = SMB.setdefault(id(smb0), [smb0] + [Buf() for _ in range(7)])
        for nm in BL:
            sz = BSZ[nm]
            kb.op('sp', lambda e, sm=sm, nm=nm, sz=sz, t0=t0: e.dma_start(out=sm[0:sz, BI[nm], :], in_=U[nm][:, t0:t0 + 128]), writes=[smb[BI[nm]]], dma=True)
        if DBG.get('stage', 99) < 2:
            return

        def S(nm):
            return sm[0:BSZ[nm], BI[nm], :]

        yield
        kkt, kktb = kkr.next()
        for bj, (rn, kn, vn) in enumerate(blkname if not DBG.get('skip_kk') else []):
            sz = BSZ[kn]
            q, qb = t32.next()
            kb.op('dve', lambda e, q=q, kn=kn, sz=sz, bj=bj: e.tensor_scalar(out=q[0:sz, 0:128], in0=S(kn), scalar1=kkp[0:sz, bj:bj + 1], scalar2=None, op0=ALU.mult),
                  reads=[smb, kkpb], writes=[qb])
            kb.op('pool', lambda e, q=q, sz=sz: e.tensor_tensor(out=q[0:sz, 128:256], in0=q[0:sz, 0:128], in1=q[0:sz, 0:128], op=ALU.mult), reads=[qb], writes=[qb])
            pp, ppb = pprep.next()
            kb.op('pe', lambda e, pp=pp, q=q, sz=sz: e.matmul(pp[0:sz, 0:128], lhsT=obd[0:sz, 0:sz], rhs=q[0:sz, 128:256], start=True, stop=True),
                  reads=[qb, obdb], writes=[ppb])
            nr, nrb = t32.next()
            kb.op('act', lambda e, nr=nr, pp=pp, sz=sz: e.activation(out=nr[0:sz, 0:128], in_=pp[0:sz, 0:128], func=AF.Sqrt), reads=[ppb], writes=[nrb])
            kb.op('dve', lambda e, nr=nr, sz=sz: e.tensor_scalar(out=nr[0:sz, 0:128], in0=nr[0:sz, 0:128], scalar1=1e-12, scalar2=None, op0=ALU.max), reads=[nrb], writes=[nrb])
            kb.op('dve', lambda e, nr=nr, sz=sz: e.reciprocal(out=nr[0:sz, 128:256], in_=nr[0:sz, 0:128]), reads=[nrb], writes=[nrb])
            kb.op('dve', lambda e, nr=nr, q=q, sz=sz, bj=bj, kkt=kkt: e.tensor_tensor(out=kkt[0:sz, bj, :], in0=q[0:sz, 0:128], in1=nr[0:sz, 128:256], op=ALU.mult),
                  reads=[nrb, qb], writes=[kktb])
        vt, vtb = vtr.next()
        vtbf, vtbfb = vtbr.next()
        pp, ppb = pprep.next()
        for bj, (rn, kn, vn) in enumerate(blkname if not DBG.get('skip_vt') else []):
            sz = BSZ[vn]
            kb.op('pe', lambda e, pp=pp, vn=vn, sz=sz, bj=bj: e.transpose(out=pp[:, bj * 128:bj * 128 + sz], in_=S(vn), identity=ident_f[0:sz, 0:sz]),
                  reads=[smb, identfb], writes=[ppb])
        kb.op('act', lambda e, pp=pp, vt=vt: e.activation(out=vt[:, :], in_=pp[:, 0:192], func=AF.Copy), reads=[ppb], writes=[vtb])
        kb.op('pool', lambda e, vt=vt, vtbf=vtbf: e.tensor_copy(out=vtbf[:, :], in_=vt[:, :]), reads=[vtb], writes=[vtbfb])

        if DBG.get('stage', 99) < 3:
            return
        yield
        th, thb = t32.next()
        kb.op('act', lambda e, th=th: e.activation(out=th[d * 64:(d + 1) * 64, 0:128], in_=sm[d * 64:(d + 1) * 64, BI['wl'], :], func=AF.Tanh),
              reads=[smb], writes=[thb])
        pp, ppb = pprep.next()
        kb.op('pe', lambda e, pp=pp, th=th: e.matmul(pp[:, 0:192], lhsT=th[d * 64:(d + 1) * 64, 0:128], rhs=w2s[d * 64:(d + 1) * 64, :], start=True, stop=False),
              reads=[thb, w2sb], writes=[ppb])
        kb.op('pe', lambda e, pp=pp: e.matmul(pp[:, 0:192], lhsT=ones1[0:1, :], rhs=w0r[0:1, d, :], start=False, stop=True),
              reads=[ones1b, w0rb], writes=[ppb])
        sg, sgb = sgr.next()
        kb.op('act', lambda e, sg=sg, pp=pp: e.activation(out=sg[:, :], in_=pp[:, 0:192], func=AF.Sigmoid), reads=[ppb], writes=[sgb])
        e1, e1b = e1r.next()
        e2, e2b = e2r.next()
        for bj in range(2):
            sz = 128 if bj == 0 else 64
            pp, ppb = pprep.next()
            kb.op('pe', lambda e, pp=pp, sg=sg, bj=bj, sz=sz: e.matmul(pp[0:sz, 0:256], lhsT=sg[:, bj * 128:bj * 128 + sz], rhs=tri[:, d, :], start=True, stop=True),
                  reads=[sgb, trib], writes=[ppb])
            kb.op('act', lambda e, pp=pp, e1=e1, bj=bj, sz=sz: e.activation(out=e1[0:sz, bj, :], in_=pp[0:sz, 0:256], func=AF.Exp), reads=[ppb], writes=[e1b])
            kb.op('act', lambda e, pp=pp, e2=e2, bj=bj, sz=sz: e.activation(out=e2[0:sz, bj, :], in_=pp[0:sz, 0:128], func=AF.Exp, scale=-1.0), reads=[ppb], writes=[e2b])
        if DBG.get('stage', 99) < 4:
            return
        yield
        ar, arb = arr.next()
        bt, btb = btr.next()
        kt, ktb = ktr.next()
        bc, bcb = bcr.next()
        pbon, pbonb, _ = pseq_next()
        for bj, (rn, kn, vn) in enumerate(blkname):
            sz = BSZ[kn]
            co = bj * 128
            pp, ppb = pprep.next()
            kb.op('pe', lambda e, pp=pp, sz=sz, co=co: e.matmul(pp[0:sz, 0:128], lhsT=a2s[d * 64:(d + 1) * 64, co:co + sz], rhs=sm[d * 64:(d + 1) * 64, BI['al'], :], start=True, stop=True),
                  reads=[a2sb, smb], writes=[ppb])
            av, avb = t32.next()
            kb.op('act', lambda e, av=av, pp=pp, sz=sz, bj=bj: e.activation(out=av[0:sz, 0:128], in_=pp[0:sz, 0:128], func=AF.Sigmoid, bias=a0p[0:sz, d, bj:bj + 1]),
                  reads=[ppb, a0pb], writes=[avb])
            kv, kvb = t32.next()
            ew(lambda e, kv=kv, av=av, sz=sz, bj=bj: e.tensor_scalar(out=kv[0:sz, 0:128], in0=av[0:sz, 0:128], scalar1=kap[0:sz, bj:bj + 1], scalar2=omka[0:sz, bj:bj + 1], op0=ALU.mult, op1=ALU.add),
               [avb, kapb, omkab], [kvb])
            ew(lambda e, kv=kv, kn=kn, sz=sz: e.tensor_tensor(out=kv[0:sz, 0:128], in0=kv[0:sz, 0:128], in1=S(kn), op=ALU.mult), [kvb, smb], [kvb])
            ew(lambda e, kv=kv, kt=kt, e2=e2, sz=sz, bj=bj: e.tensor_tensor(out=kt[0:sz, bj, :], in0=kv[0:sz, 0:128], in1=e2[0:sz, bj, :], op=ALU.mult), [kvb, e2b], [ktb])
            ew(lambda e, av=av, e2=e2, sz=sz, bj=bj: e.tensor_tensor(out=av[0:sz, 128:256], in0=av[0:sz, 0:128], in1=e2[0:sz, bj, :], op=ALU.mult), [avb, e2b], [avb])
            ew(lambda e, av=av, bt=bt, kkt=kkt, sz=sz, bj=bj: e.tensor_tensor(out=bt[0:sz, bj, :], in0=av[0:sz, 128:256], in1=kkt[0:sz, bj, :], op=ALU.mult), [avb, kktb], [btb])
            ew(lambda e, ar=ar, kkt=kkt, e1=e1, sz=sz, bj=bj: e.scalar_tensor_tensor(out=ar[0:sz, bj, 0:128], in0=kkt[0:sz, bj, :], scalar=-1.0, in1=e1[0:sz, bj, 128:256], op0=ALU.mult, op1=ALU.mult),
               [kktb, e1b], [arb])
            ew(lambda e, ar=ar, rn=rn, e1=e1, sz=sz, bj=bj: e.tensor_tensor(out=ar[0:sz, bj, 128:256], in0=S(rn), in1=e1[0:sz, bj, 0:128], op=ALU.mult), [smb, e1b], [arb])
            ew(lambda e, kv=kv, rn=rn, sz=sz, bj=bj: e.scalar_tensor_tensor(out=kv[0:sz, 128:256], in0=S(rn), scalar=rkp[0:sz, bj:bj + 1], in1=kv[0:sz, 0:128], op0=ALU.mult, op1=ALU.mult),
               [smb, rkpb, kvb], [kvb])
            kb.op('pe', lambda e, kv=kv, sz=sz, bj=bj: e.matmul(pbon[:, 0:4], lhsT=kv[0:sz, 128:256], rhs=ehd[0:sz, bj, :], start=(bj == 0), stop=(bj == 1)),
                  reads=[kvb, ehdb], writes=[pbonb])
        kb.op('dve', lambda e, bc=bc: e.tensor_copy(out=bc[:, :], in_=pbon[:, 0:4]), reads=[pbonb], writes=[bcb])
        yield
        bkt, bktb = bktr.next()
        for wi, (src, srcb) in enumerate([(bt, btb), (kt, ktb)]):
            pi, pib = ptr_next()
            for bj in range(2):
                sz = 128 if bj == 0 else 64
                kb.op('pe', lambda e, src=src, pi=pi, bj=bj, sz=sz: e.transpose(out=ptrb[:, pi * 256 + bj * 128:pi * 256 + bj * 128 + sz], in_=src[0:sz, bj, :], identity=ident_bf[0:sz, 0:sz]),
                      reads=[srcb, identb], writes=[pib])
            kb.op('act' if wi == 0 else 'dve',
                  (lambda e, pi=pi, bkt=bkt, wi=wi: e.activation(out=bkt[:, wi, :], in_=ptrb[:, pi * 256:pi * 256 + 192], func=AF.Copy)) if wi == 0 else
                  (lambda e, pi=pi, bkt=bkt, wi=wi: e.tensor_copy(out=bkt[:, wi, :], in_=ptrb[:, pi * 256:pi * 256 + 192])),
                  reads=[pib], writes=[bktb])
        tcol = 127 if d == 0 else 0

        if DBG.get('stage', 99) < 5:
            return
        first = done.get(c, 0) == 0
        assert c not in inflight
        inflight[c] = d

        def head_body(h):
            mm1r, mm2r, xpr, xtr, ivr, tmr, w1r, utr, tsr = (HR[(d, h)][k_] for k_ in ('mm1r', 'mm2r', 'xpr', 'xtr', 'ivr', 'tmr', 'w1r', 'utr', 'tsr'))
            bj = h // 2
            base = (h % 2) * 64
            ch0 = 64 * h
            hp = slice(base, base + 64)
            yield
            pg1, pg1b = pgram.next()
            kb.op('pe', lambda e, pg1=pg1, hp=hp, bj=bj: e.matmul(pg1[:, 0:256], lhsT=bt[hp, bj, :], rhs=ar[hp, bj, :], start=True, stop=True), reads=[btb, arb], writes=[pg1b])
            mm1, mm1b = mm1r.next()
            kb.op('dve', lambda e, mm1=mm1, pg1=pg1: e.tensor_tensor(out=mm1[:, :], in0=pg1[:, 0:256], in1=msi[:, d, :], op=ALU.mult), reads=[pg1b, msib], writes=[mm1b])
            pg2, pg2b = pgram.next()
            kb.op('pe', lambda e, pg2=pg2, hp=hp, bj=bj: e.matmul(pg2[:, 0:256], lhsT=kt[hp, bj, :], rhs=ar[hp, bj, :], start=True, stop=True), reads=[ktb, arb], writes=[pg2b])
            kb.op('pe', lambda e, pg2=pg2, hp=hp, bj=bj: e.matmul(pg2[:, 256:384], lhsT=ar[hp, bj, 0:128], rhs=bt[hp, bj, :], start=True, stop=True), reads=[btb, arb], writes=[pg2b])
            mm2, mm2b = mm2r.next()
            kb.op('dve', lambda e, mm2=mm2, pg2=pg2: e.tensor_tensor(out=mm2[:, :], in0=pg2[:, 0:256], in1=msi2[:, d, :], op=ALU.mult), reads=[pg2b, msi2b], writes=[mm2b])
            xt, xtb = xtr.next()
            kb.op('dve', lambda e, xt=xt, pg2=pg2: e.tensor_tensor(out=xt[:, :], in0=pg2[:, 256:384], in1=mst[:, d, 0, :], op=ALU.mult), reads=[pg2b, mstb], writes=[xtb])
            e1t, e1tb = ivr.next()
            kb.op('dve', lambda e, e1t=e1t, pg2=pg2: e.tensor_tensor(out=e1t[:, :], in0=pg2[:, 256:384], in1=mst[:, d, 1, :], op=ALU.mult), reads=[pg2b, mstb], writes=[e1tb])
            e2t, e2tb = ivr.next()
            kb.op('dve', lambda e, e2t=e2t, pg2=pg2: e.tensor_tensor(out=e2t[:, :], in0=pg2[:, 256:384], in1=mst[:, d, 2, :], op=ALU.mult), reads=[pg2b, mstb], writes=[e2tb])
            if DBG.get('stage', 99) < 6:
                return
            yield
            xp, xpb = xpr.next()
            kb.op('pool', lambda e, xp=xp, mm1=mm1: e.tensor_tensor(out=xp[:, 128:256], in0=mm1[:, 0:128], in1=ident_bf[:, :], op=ALU.add), reads=[mm1b, identb], writes=[xpb])
            pv, pvb = pinv.next()
            kb.op('pe', lambda e, pv=pv, xt=xt, mm1=mm1: e.matmul(pv[:, 0:128], lhsT=xt[:, :], rhs=mm1[:, 0:128], start=True, stop=True), reads=[xtb, mm1b], writes=[pvb])
            kb.op('pe', lambda e, pv=pv, xt=xt, mm1=mm1: e.matmul(pv[:, 256:384], lhsT=mm1[:, 0:128], rhs=xt[:, :], start=True, stop=True), reads=[xtb, mm1b], writes=[pvb])
            xt2, xt2b = xtr.next()
            kb.op('act', lambda e, xp=xp, pv=pv: e.activation(out=xp[:, 0:128], in_=pv[:, 0:128], func=AF.Copy), reads=[pvb], writes=[xpb])
            kb.op('dve', lambda e, xt2=xt2, pv=pv: e.tensor_copy(out=xt2[:, :], in_=pv[:, 256:384]), reads=[pvb], writes=[xt2b])
            curxp, curxpb, curxt, curxtb = xp, xpb, xt2, xt2b
            for lev in range(1, 4):
                pv, pvb = pinv.next()
                kb.op('pe', lambda e, pv=pv, cx=curxp, ct=curxt: e.matmul(pv[:, 0:256], lhsT=ct[:, :], rhs=cx[:, 0:256], start=True, stop=True), reads=[curxpb, curxtb], writes=[pvb])
                kb.op('pe', lambda e, pv=pv, cx=curxp, ct=curxt: e.matmul(pv[:, 256:384], lhsT=cx[:, 0:128], rhs=ct[:, :], start=True, stop=True), reads=[curxpb, curxtb], writes=[pvb])
                nxp, nxpb = xpr.next()
                nxt, nxtb = xtr.next()
                kb.op('act', lambda e, nxp=nxp, pv=pv: e.activation(out=nxp[:, 0:128], in_=pv[:, 0:128], func=AF.Copy), reads=[pvb], writes=[nxpb])
                kb.op('dve', lambda e, nxp=nxp, pv=pv, cx=curxp: e.tensor_tensor(out=nxp[:, 128:256], in0=pv[:, 128:256], in1=cx[:, 128:256], op=ALU.add), reads=[pvb, curxpb], writes=[nxpb])
                kb.op('act', lambda e, nxt=nxt, pv=pv: e.activation(out=nxt[:, :], in_=pv[:, 256:384], func=AF.Copy), reads=[pvb], writes=[nxtb])
                curxp, curxpb, curxt, curxtb = nxp, nxpb, nxt, nxtb
                yield
            pv, pvb = pinv.next()
            kb.op('pe', lambda e, pv=pv, cx=curxp, ct=curxt: e.matmul(pv[:, 0:128], lhsT=ct[:, :], rhs=cx[:, 128:256], start=True, stop=True), reads=[curxpb, curxtb], writes=[pvb])
            t32m, t32mb = ivr.next()
            kb.op('dve', lambda e, t32m=t32m, pv=pv, cx=curxp: e.tensor_tensor(out=t32m[:, :], in0=pv[:, 0:128], in1=cx[:, 128:256], op=ALU.add), reads=[pvb, curxpb], writes=[t32mb])
            yield
            pi, pib = ptr_next()
            kb.op('pe', lambda e, pi=pi, t32m=t32m: e.transpose(out=ptrb[:, pi * 256:pi * 256 + 128], in_=t32m[:, :], identity=ident_bf[:, :]), reads=[t32mb, identb], writes=[pib])
            t32t, t32tb = ivr.next()
            kb.op('act', lambda e, pi=pi, t32t=t32t: e.activation(out=t32t[:, :], in_=ptrb[:, pi * 256:pi * 256 + 128], func=AF.Copy), reads=[pib], writes=[t32tb])
            yield
            pv, pvb = pinv.next()
            kb.op('pe', lambda e, pv=pv, e1t=e1t, t32m=t32m: e.matmul(pv[:, 0:128], lhsT=e1t[:, :], rhs=t32m[:, :], start=True, stop=True), reads=[e1tb, t32mb], writes=[pvb])
            z1, z1b = ivr.next()
            kb.op('act', lambda e, z1=z1, pv=pv: e.activation(out=z1[:, :], in_=pv[:, 0:128], func=AF.Copy), reads=[pvb], writes=[z1b])
            pv, pvb = pinv.next()
            kb.op('pe', lambda e, pv=pv, t32t=t32t, z1=z1: e.matmul(pv[:, 0:128], lhsT=t32t[:, :], rhs=z1[:, :], start=True, stop=True), reads=[t32tb, z1b], writes=[pvb])
            kb.op('pe', lambda e, pv=pv, t32t=t32t, z1=z1: e.matmul(pv[:, 256:384], lhsT=z1[:, :], rhs=t32t[:, :], start=True, stop=True), reads=[t32tb, z1b], writes=[pvb])
            t64, t64b = ivr.next()
            t64t, t64tb = ivr.next()
            kb.op('dve', lambda e, t64=t64, pv=pv, t32m=t32m: e.tensor_tensor(out=t64[:, :], in0=pv[:, 0:128], in1=t32m[:, :], op=ALU.add), reads=[pvb, t32mb], writes=[t64b])
            kb.op('dve', lambda e, t64t=t64t, pv=pv, t32t=t32t: e.tensor_tensor(out=t64t[:, :], in0=pv[:, 256:384], in1=t32t[:, :], op=ALU.add), reads=[pvb, t32tb], writes=[t64tb])
            yield
            pv, pvb = pinv.next()
            kb.op('pe', lambda e, pv=pv, e2t=e2t, t64=t64: e.matmul(pv[:, 0:128], lhsT=e2t[:, :], rhs=t64[:, :], start=True, stop=True), reads=[e2tb, t64b], writes=[pvb])
            z2, z2b = ivr.next()
            kb.op('act', lambda e, z2=z2, pv=pv: e.activation(out=z2[:, :], in_=pv[:, 0:128], func=AF.Copy), reads=[pvb], writes=[z2b])
            pv, pvb = pinv.next()
            kb.op('pe', lambda e, pv=pv, t64t=t64t, z2=z2: e.matmul(pv[:, 0:128], lhsT=t64t[:, :], rhs=z2[:, :], start=True, stop=True), reads=[t64tb, z2b], writes=[pvb])
            tm, tmb = tmr.next()
            kb.op('dve', lambda e, tm=tm, pv=pv, t64=t64: e.tensor_tensor(out=tm[:, :], in0=pv[:, 0:128], in1=t64[:, :], op=ALU.add), reads=[pvb, t64b], writes=[tmb])
            if DBG.get('dump') == (d, c) and h == 0:
                dump(kb, nc, 'mm1', mm1[:, :], [128, 256], [mm1b], BF16)
                dump(kb, nc, 'mm2', mm2[:, :], [128, 256], [mm2b], BF16)
                dump(kb, nc, 'tm', tm[:, :], [128, 128], [tmb], BF16)
            yield
            stb = STb[d][h]
            p1, p1b, _ = pseq_next()
            kb.op('pe', lambda e, p1=p1, hp=hp, bj=bj: e.matmul(p1, lhsT=ar[hp, bj, 0:128], rhs=STbf[hp, d, bj, :], start=True, stop=False), reads=[arb, stb], writes=[p1b])
            kb.op('pe', lambda e, p1=p1, mm2=mm2, ch0=ch0: e.matmul(p1, lhsT=mm2[:, 0:128], rhs=vtbf[:, ch0:ch0 + 64], start=False, stop=True), reads=[mm2b, vtbfb], writes=[p1b])
            w1, w1b = w1r.next()
            kb.op('act', lambda e, w1=w1, p1=p1: e.activation(out=w1[:, :], in_=p1, func=AF.Copy), reads=[p1b], writes=[w1b])
            p2, p2b, _ = pseq_next()
            kb.op('pe', lambda e, p2=p2, tm=tm, w1=w1: e.matmul(p2, lhsT=tm[:, :], rhs=w1[:, :], start=True, stop=True), reads=[tmb, w1b], writes=[p2b])
            ut, utb = utr.next()
            kb.op('act', lambda e, ut=ut, p2=p2: e.activation(out=ut[:, :], in_=p2, func=AF.Copy), reads=[p2b], writes=[utb])
            p3, p3b, _ = pseq_next()
            kb.op('pe', lambda e, p3=p3, hp=hp, bj=bj: e.matmul(p3, lhsT=ar[hp, bj, 128:256], rhs=STbf[hp, d, bj, :], start=True, stop=False), reads=[arb, stb], writes=[p3b])
            kb.op('pe', lambda e, p3=p3, mm1=mm1, ut=ut: e.matmul(p3, lhsT=mm1[:, 128:256], rhs=ut[:, :], start=False, stop=False), reads=[mm1b, utb], writes=[p3b])
            kb.op('pe', lambda e, p3=p3, mm2=mm2, ch0=ch0: e.matmul(p3, lhsT=mm2[:, 128:256], rhs=vtbf[:, ch0:ch0 + 64], start=False, stop=True), reads=[mm2b, vtbfb], writes=[p3b])
            if DBG.get('dump') == (d, c) and h == 0:
                dump(kb, nc, 'w1', w1[:, :], [128, 64], [w1b], BF16)
                dump(kb, nc, 'ut', ut[:, :], [128, 64], [utb], BF16)
            ts, tsb = tsr.next()
            kb.op('dve', lambda e, ts=ts, p3=p3, ch0=ch0, h=h: e.scalar_tensor_tensor(out=ts[:, :], in0=vt[:, ch0:ch0 + 64], scalar=bc[:, h:h + 1], in1=p3, op0=ALU.mult, op1=ALU.add),
                  reads=[vtb, bcb, p3b], writes=[tsb])
            if first:
                kb.op('pool', lambda e, ts=ts, ch0=ch0, c=c: e.tensor_copy(out=yacc[:, c, ch0:ch0 + 64], in_=ts[:, :]), reads=[tsb], writes=[yaccb[c]])
            else:
                kb.op('pool', lambda e, ts=ts, ch0=ch0, c=c: e.tensor_tensor(out=yacc[:, c, ch0:ch0 + 64], in0=yacc[:, c, ch0:ch0 + 64], in1=ts[:, :], op=ALU.add), reads=[tsb, yaccb[c]], writes=[yaccb[c]])
            yield
            p4full, p4b, i4 = pseq_next()
            p4 = pseqt[i4][hp, (i4 % 4) * 64:(i4 % 4 + 1) * 64]
            kb.op('pe', lambda e, p4=p4, bkt=bkt, ut=ut, ch0=ch0: e.matmul(p4, lhsT=bkt[:, 0, ch0:ch0 + 64], rhs=ut[:, :], start=True, stop=False), reads=[bktb, utb], writes=[p4b])
            kb.op('pe', lambda e, p4=p4, bkt=bkt, ch0=ch0: e.matmul(p4, lhsT=bkt[:, 1, ch0:ch0 + 64], rhs=vtbf[:, ch0:ch0 + 64], start=False, stop=True), reads=[bktb, vtbfb], writes=[p4b])
            tq, tqb = tsr.next()
            kb.op('dve', lambda e, tq=tq, p4=p4, hp=hp, bj=bj: e.tensor_tensor(out=tq[hp, :], in0=p4, in1=ST[hp, d, bj, :], op=ALU.add), reads=[p4b, stb], writes=[tqb])
            kb.op('dve', lambda e, tq=tq, hp=hp, bj=bj: e.tensor_scalar(out=ST[hp, d, bj, :], in0=tq[hp, :], scalar1=e1[hp, bj, tcol:tcol + 1], scalar2=None, op0=ALU.mult), reads=[tqb, e1b], writes=[stb])
            kb.op('act', lambda e, tq=tq, hp=hp, bj=bj: e.activation(out=STbf[hp, d, bj, :], in_=tq[hp, :], func=AF.Identity, scale=e1[hp, bj, tcol:tcol + 1]), reads=[tqb, e1b], writes=[stb])

        hg = [head_body(h) for h in range(DBG.get('heads', 3))]
        while hg:
            for g_ in list(hg):
                try:
                    next(g_)
                except StopIteration:
                    hg.remove(g_)
            yield
        del inflight[c]
        done[c] = done.get(c, 0) + 1
        if DBG.get('dump') == (d, c):
            dump(kb, nc, 'sm', sm[:, :, :], [128, 8, 128], [smb])
            dump(kb, nc, 'kkt', kkt[:, :, :], [128, 2, 128], [kktb])
            dump(kb, nc, 'vt', vt[:, :], [128, 192], [vtb])
            dump(kb, nc, 'sg', sg[:, :], [128, 192], [sgb])
            dump(kb, nc, 'e1', e1[:, :, :], [128, 2, 256], [e1b])
            dump(kb, nc, 'e2', e2[:, :, :], [128, 2, 128], [e2b])
            dump(kb, nc, 'ar', ar[:, :, :], [128, 2, 256], [arb], BF16)
            dump(kb, nc, 'bt', bt[:, :, :], [128, 2, 128], [btb], BF16)
            dump(kb, nc, 'kt', kt[:, :, :], [128, 2, 128], [ktb], BF16)
            dump(kb, nc, 'bkt', bkt[:, :, :], [128, 2, 192], [bktb], BF16)
            dump(kb, nc, 'bc', bc[:, :], [128, 4], [bcb])
            dump(kb, nc, 'ST', ST[:, d, :, :], [128, 2, 64], STb[d])
            dump(kb, nc, 'yacc', yacc[:, c, :], [128, 192], [yaccb[c]])
        yield
        if done[c] == 2 and DBG.get('stage', 99) >= 8:
            gt, gtb = gtr.next()
            kb.op('sp', lambda e, gt=gt, t0=t0: e.dma_start(out=gt[:, :], in_=GT[t0:t0 + 128, 0:192]), writes=[gtb], dma=True)
            fn, fnb = fnr.next()
            for h in range(3):
                ch0 = 64 * h
                stt, sttb = str_.next()
                jk, jkb = jkr.next()
                kb.op('act', lambda e, stt=stt, jk=jk, c=c, ch0=ch0: e.activation(out=jk[:, :], in_=yacc[:, c, ch0:ch0 + 64], func=AF.Copy, accum_out=stt[:, 0:1]), reads=[yaccb[c]], writes=[sttb, jkb])
                kb.op('act', lambda e, stt=stt, jk=jk, c=c, ch0=ch0: e.activation(out=jk[:, :], in_=yacc[:, c, ch0:ch0 + 64], func=AF.Square, accum_out=stt[:, 1:2]), reads=[yaccb[c]], writes=[sttb, jkb])
                kb.op('dve', lambda e, stt=stt: e.tensor_scalar(out=stt[:, 6:7], in0=stt[:, 0:1], scalar1=1.0 / 64, scalar2=None, op0=ALU.mult), reads=[sttb], writes=[sttb])
                kb.op('dve', lambda e, stt=stt: e.tensor_tensor(out=stt[:, 3:4], in0=stt[:, 6:7], in1=stt[:, 6:7], op=ALU.mult), reads=[sttb], writes=[sttb])
                kb.op('dve', lambda e, stt=stt: e.scalar_tensor_tensor(out=stt[:, 7:8], in0=stt[:, 1:2], scalar=1.0 / 64, in1=stt[:, 3:4], op0=ALU.mult, op1=ALU.subtract), reads=[sttb], writes=[sttb])
                kb.op('act', lambda e, stt=stt: e.activation(out=stt[:, 4:5], in_=stt[:, 7:8], func=AF.Sqrt, bias=gneps[:, 0:1]), reads=[sttb, gnepsb], writes=[sttb])
                kb.op('dve', lambda e, stt=stt: e.reciprocal(out=stt[:, 5:6], in_=stt[:, 4:5]), reads=[sttb], writes=[sttb])
                kb.op('dve', lambda e, stt=stt, fn=fn, c=c, ch0=ch0: e.tensor_scalar(out=fn[:, ch0:ch0 + 64], in0=yacc[:, c, ch0:ch0 + 64], scalar1=stt[:, 6:7], scalar2=stt[:, 5:6], op0=ALU.subtract, op1=ALU.mult),
                      reads=[sttb, yaccb[c]], writes=[fnb])
            if DBG.get('dumpfin') == c:
                dump(kb, nc, 'stt', stt[:, :], [128, 8], [sttb])
                dump(kb, nc, 'fn0', fn[:, :], [128, 192], [fnb])
                dump(kb, nc, 'gt', gt[:, :], [128, 192], [gtb])
                dump(kb, nc, 'yaccf', yacc[:, c, :], [128, 192], [yaccb[c]])
                dump(kb, nc, 'lng', lng[:, :], [128, 192], [lngb])
            kb.op('pool', lambda e, fn=fn: e.tensor_tensor(out=fn[:, :], in0=fn[:, :], in1=lng[:, :], op=ALU.mult), reads=[fnb, lngb], writes=[fnb])
            kb.op('pool', lambda e, fn=fn: e.tensor_tensor(out=fn[:, :], in0=fn[:, :], in1=lnb[:, :], op=ALU.add), reads=[fnb, lnbb], writes=[fnb])
            kb.op('dve', lambda e, fn=fn, gt=gt: e.tensor_tensor(out=fn[:, :], in0=fn[:, :], in1=gt[:, :], op=ALU.mult), reads=[fnb, gtb], writes=[fnb])
            kb.op('pool', lambda e, fn=fn, t0=t0: e.dma_start(out=YR.rows(t0, 0, 192), in_=fn[:, :]), reads=[fnb], dma=True)


    def dir_gen(d):
        for c in (DBG['chunks'][d] if 'chunks' in DBG else CHUNK_ORDER[d]):
            yield from chunk_body(d, c)

    dirs = DBG.get('dirs', [0, 1])
    for d in dirs:
        kb.op('dve', lambda e, d=d: e.memset(ST[:, d, :, :], 0.0), writes=STb[d])
        kb.op('dve', lambda e, d=d: e.memset(STbf[:, d, :, :], 0.0), writes=STb[d])
    gens = [dir_gen(d) for d in dirs]
    if DBG.get('no_interleave'):
        for g_ in gens:
            for _ in g_:
                pass
    else:
        while gens:
            for g_ in list(gens):
                try:
                    next(g_)
                except StopIteration:
                    gens.remove(g_)


FN_PARAMS = {'c128': [128, 128], 'ns128': [128, 128], 'twc': [128, 64], 'tws': [128, 64], 'fl1': [128, 64], 'fl2': [128, 64],
             'c256': [128, 2, 256], 'ns256': [128, 2, 256]}


def phase_fnet(kb, nc, HT, GT, YF, ZS, P, bar):
    sc_l = 1.0 / float(np.sqrt(L * 64.0))
    sc_c = 1.0 / float(np.sqrt(LC * 64.0))
    with ExitStack() as c1:
        def const(name, shape, src):
            t = kb.sb(name, shape, F32, c1)
            b, = load_consts(kb, [(t, src)])
            return t, b
        c128, c128b = const('c128', [128, 128], P['c128'])
        ns128, ns128b = const('ns128', [128, 128], P['ns128'])
        twc, twcb = const('twc', [128, 64], P['twc'])
        tws, twsb = const('tws', [128, 64], P['tws'])
        c256, c256b = const('c256', [128, 2, 256], P['c256'])
        ns256, ns256b = const('ns256', [128, 2, 256], P['ns256'])
        hc = kb.sb('hctx', [128, 2, 128], F32, c1)
        hcb = Buf()
        gc = kb.sb('gctx', [128, 2, 64], F32, c1)
        gcb = Buf()
        for lt in range(2):
            kb.op('sp', lambda e, lt=lt: e.dma_start(out=hc[:, lt, :], in_=HT[lt * 128:(lt + 1) * 128, :]), writes=[hcb], dma=True)
            kb.op('sp', lambda e, lt=lt: e.dma_start(out=gc[:, lt, :], in_=GT[lt * 128:(lt + 1) * 128, 192:256]), writes=[gcb], dma=True)
        pc = Ring(kb, 'pfc', [128, 512], F32, 1, psum=True, es=c1)
        fo = Ring(kb, 'foc', [128, 64], F32, 2, es=c1)
        for lo in range(2):
            ps, psb = pc.next()
            for lt in range(2):
                kb.op('pe', lambda e, ps=ps, lt=lt, lo=lo: e.matmul(ps[:, 0:64], lhsT=c256[:, lt, lo * 128:(lo + 1) * 128], rhs=hc[:, lt, 0:64], start=(lt == 0), stop=False),
                      reads=[c256b, hcb], writes=[psb])
            for lt in range(2):
                kb.op('pe', lambda e, ps=ps, lt=lt, lo=lo: e.matmul(ps[:, 0:64], lhsT=ns256[:, lt, lo * 128:(lo + 1) * 128], rhs=hc[:, lt, 64:128], start=False, stop=(lt == 1)),
                      reads=[ns256b, hcb], writes=[psb])
            f, fb = fo.next()
            kb.op('dve', lambda e, f=f, ps=ps, lo=lo: e.scalar_tensor_tensor(out=f[:, :], in0=ps[:, 0:64], scalar=sc_c, in1=gc[:, lo, :], op0=ALU.mult, op1=ALU.mult),
                  reads=[psb, gcb], writes=[fb])
            kb.op('pool', lambda e, f=f, lo=lo: e.dma_start(out=YF.rows(lo * 128, 192, 256), in_=f[:, :]), reads=[fb], dma=True)
        h1 = kb.sb('h1', [128, 64, 128], F32, c1)
        h1b = Buf()
        HTl = HT[LC:LT, :].rearrange("(a b) c -> a b c", b=64)
        for j in range(4):
            kb.op('sp', lambda e, j=j: e.dma_start(out=h1[:, j * 16:(j + 1) * 16, :], in_=HTl[:, j * 16:(j + 1) * 16, :]), writes=[h1b], dma=True)
        py = Ring(kb, 'py', [128, 512], F32, 4, psum=True, es=c1)
        zt = Ring(kb, 'zt', [128, 4, 128], F32, 6, es=c1)
        zo = Ring(kb, 'zo', [128, 2, 4, 128], F32, 3, es=c1)
        for pc_ in range(16):
            b0 = pc_ * 4
            pr, prb = py.next()
            pi, pib = py.next()
            kb.op('pe', lambda e, pr=pr, b0=b0: e.matmul(pr[:, :], lhsT=c128[:, :], rhs=h1[:, b0:b0 + 4, :], start=True, stop=True), reads=[c128b, h1b], writes=[prb])
            kb.op('pe', lambda e, pi=pi, b0=b0: e.matmul(pi[:, :], lhsT=ns128[:, :], rhs=h1[:, b0:b0 + 4, :], start=True, stop=True), reads=[ns128b, h1b], writes=[pib])
            cb_ = twc[:, b0:b0 + 4].unsqueeze(2).to_broadcast([128, 4, 128])
            sb_ = tws[:, b0:b0 + 4].unsqueeze(2).to_broadcast([128, 4, 128])
            prv = pr[:, :].rearrange("p (b c) -> p b c", c=128)
            piv = pi[:, :].rearrange("p (b c) -> p b c", c=128)
            t1, t1b = zt.next()
            t2, t2b = zt.next()
            z, zb = zo.next()
            kb.op('dve', lambda e, t1=t1, prv=prv, cb_=cb_: e.tensor_tensor(out=t1[:, :, :], in0=prv, in1=cb_, op=ALU.mult), reads=[prb, twcb], writes=[t1b])
            kb.op('dve', lambda e, t2=t2, piv=piv, sb_=sb_: e.tensor_tensor(out=t2[:, :, :], in0=piv, in1=sb_, op=ALU.mult), reads=[pib, twsb], writes=[t2b])
            kb.op('pool', lambda e, z=z, t1=t1, t2=t2: e.tensor_tensor(out=z[:, 0, :, :], in0=t1[:, :, :], in1=t2[:, :, :], op=ALU.add), reads=[t1b, t2b], writes=[zb])
            t3, t3b = zt.next()
            t4, t4b = zt.next()
            kb.op('dve', lambda e, t3=t3, piv=piv, cb_=cb_: e.tensor_tensor(out=t3[:, :, :], in0=piv, in1=cb_, op=ALU.mult), reads=[pib, twcb], writes=[t3b])
            kb.op('dve', lambda e, t4=t4, prv=prv, sb_=sb_: e.tensor_tensor(out=t4[:, :, :], in0=prv, in1=sb_, op=ALU.mult), reads=[prb, twsb], writes=[t4b])
            kb.op('pool', lambda e, z=z, t3=t3, t4=t4: e.tensor_tensor(out=z[:, 1, :, :], in0=t3[:, :, :], in1=t4[:, :, :], op=ALU.subtract), reads=[t3b, t4b], writes=[zb])
            for ri in range(2):
                kb.op('pool', lambda e, z=z, ri=ri, b0=b0: e.dma_start(out=ZS[ri, :, b0:b0 + 4, :], in_=z[:, ri, :, :]), reads=[zb], dma=True)
        kb.barrier(bar[:])
    with ExitStack() as c2:
        fl1 = kb.sb('fl1', [128, 64], F32, c2)
        fl2 = kb.sb('fl2', [128, 64], F32, c2)
        fl1b, fl2b = load_consts(kb, [(fl1, P['fl1']), (fl2, P['fl2'])])
        rz1 = kb.sb('rz1', [128, 128, 64], F32, c2)
        rz2 = kb.sb('rz2', [128, 128, 64], F32, c2)
        g2 = kb.sb('g2', [64, 128, 64], F32, c2)
        fo2 = kb.sb('fo2', [64, 128, 64], F32, c2)
        rz1b = [Buf() for _ in range(4)]
        rz2b = [Buf() for _ in range(4)]
        g2b = Buf()
        fo2b = [Buf() for _ in range(4)]
        for qa in range(4):
            asl = slice(qa * 32, (qa + 1) * 32)
            for ri in range(2):
                src = ZS[ri].rearrange("a b c -> b a c")
                kb.op('sp', lambda e, ri=ri, src=src, asl=asl: e.dma_start(out=rz1[ri * 64:(ri + 1) * 64, asl, :], in_=src[:, asl, 0:64]), writes=[rz1b[qa]], dma=True)
                kb.op('sp', lambda e, ri=ri, src=src, asl=asl: e.dma_start(out=rz2[ri * 64:(ri + 1) * 64, asl, :], in_=src[:, asl, 64:128]), writes=[rz2b[qa]], dma=True)
        kb.op('sp', lambda e: e.dma_start(out=g2[:, :, :], in_=GT[LC:LT, 192:256].rearrange("(b a) c -> b a c", a=128)), writes=[g2b], dma=True)
        pf = Ring(kb, 'pf', [128, 512], F32, 2, psum=True, es=c2)
        for pc_ in range(16):
            a0 = pc_ * 8
            qa = pc_ // 4
            ps, psb = pf.next()
            kb.op('pe', lambda e, ps=ps, a0=a0: e.matmul(ps[0:64, :], lhsT=fl1[:, :], rhs=rz1[:, a0:a0 + 8, :], start=True, stop=False), reads=[fl1b, rz1b[qa]], writes=[psb])
            kb.op('pe', lambda e, ps=ps, a0=a0: e.matmul(ps[0:64, :], lhsT=fl2[:, :], rhs=rz2[:, a0:a0 + 8, :], start=False, stop=True), reads=[fl2b, rz2b[qa]], writes=[psb])
            kb.op('dve', lambda e, ps=ps, a0=a0: e.scalar_tensor_tensor(out=fo2[:, a0:a0 + 8, :], in0=ps[0:64, :].rearrange("p (a c) -> p a c", c=64), scalar=sc_l,
                                                                        in1=g2[:, a0:a0 + 8, :], op0=ALU.mult, op1=ALU.mult), reads=[psb, g2b], writes=[fo2b[qa]])
        allfo = fo2b
        for (dst, b0, b1) in YF.lat_groups():
            kb.op('pool', lambda e, dst=dst, b0=b0, b1=b1: e.dma_start(out=dst, in_=fo2[b0:b1, :, :]), reads=allfo, dma=True)
        kb.barrier(bar[:])


def gate_rows(kb, es, nc, sil, silb, adaw_gate, adab_row, npost_b, ones1, ones1b, keep_es, NG=None):
    if NG is None:
        NG = kb.sb('NG', [128, 2, D], F32, keep_es)
    NGb = Buf()
    wg = kb.sb('adawg', [128, 8, D], F32, es)
    wgb = Buf()
    for k in range(8):
        kb.op('sp', lambda e, k=k: e.dma_start(out=wg[:, k, :], in_=adaw_gate[k * 128:(k + 1) * 128, :]), writes=[wgb], dma=True)
    br = kb.sb('adabr', [1, D], F32, es)
    brb, = load_consts(kb, [(br, adab_row)])
    npb = kb.sb('npostb', [128, D], F32, es)
    npbb, = load_consts(kb, [(npb, npost_b)])
    onesq = kb.sb('onesq', [128, 128], F32, es)
    onesqb = Buf()
    kb.op('dve', lambda e: e.memset(onesq[:], 1.0), writes=[onesqb])
    srep = kb.sb('silrep', [128, 8, 128], F32, es)
    pg_ = Ring(kb, 'pgate', [128, 512], F32, 2, psum=True, es=es)
    for n in range(2):
        srb = Buf()
        for k in range(8):
            kb.op('dve', lambda e, k=k, n=n: e.tensor_scalar(out=srep[:, k, :], in0=onesq[:, :], scalar1=sil[:, k, n:n + 1], scalar2=None, op0=ALU.mult),
                  reads=[onesqb, silb], writes=[srb])
        for half in range(2):
            ps, psb = pg_.next()
            for k in range(8):
                kb.op('pe', lambda e, ps=ps, k=k, half=half: e.matmul(ps[:, :], lhsT=srep[:, k, :], rhs=wg[:, k, half * 512:(half + 1) * 512], start=(k == 0), stop=False),
                      reads=[srb, wgb], writes=[psb])
            kb.op('pe', lambda e, ps=ps, half=half: e.matmul(ps[:, :], lhsT=ones1[0:1, :], rhs=br[0:1, half * 512:(half + 1) * 512], start=False, stop=True),
                  reads=[ones1b, brb], writes=[psb])
            kb.op('dve', lambda e, ps=ps, n=n, half=half: e.tensor_tensor(out=NG[:, n, half * 512:(half + 1) * 512], in0=ps[:, :], in1=npb[:, half * 512:(half + 1) * 512], op=ALU.mult),
                  reads=[psb, npbb], writes=[NGb])
    return NG, NGb


class OutProj:
    def __init__(self, kb, es, nc, yT, w_out_src, xsrc, NG, NGb, epsT, epsb, nslot=3, ysrc=None, ident=None, npol=2):
        self.kb, self.yT, self.xsrc, self.NG, self.NGb, self.epsT, self.epsb = kb, yT, xsrc, NG, NGb, epsT, epsb
        self.ysrc, self.ident = ysrc, ident
        self.pref = {}
        self.ykbs = {}
        if ysrc is not None:
            self.ytok = Ring(kb, 'ytok', [128, 4, 256], F32, 2, es=es)
            self.ytokb = Ring(kb, 'ytokb', [128, 1024], BF16, 2, es=es)
            self.pyt = Ring(kb, 'pyt', [128, 8, 128], BF16, 1, psum=True, es=es)
        self.wo, self.wob = load_weight_bf16(kb, es, nc, 'wout_bf', w_out_src, D, es)
        self.ystg = Ring(kb, 'ystg', [128, 8, 128], F32, 2, es=es)
        self.ybf = Ring(kb, 'ybf', [128, 8, 128], BF16, 2, es=es)
        self.xin = Ring(kb, 'xres', [128, D], F32, 2, es=es)
        self.xout = Ring(kb, 'xnew', [128, D], F32, nslot, es=es)
        self.pol = Ring(kb, 'pol', [128, 512], F32, npol, psum=True, es=es)
        self.st = Ring(kb, 'ost', [128, 8], F32, 4, es=es, strict=True)
        self.junk = kb.sb('ojunk', [128, 512], BF16, es)
        self.junkb = Buf()
        self.tmp = Ring(kb, 'otmp', [128, D], F32, 2, es=es)

    def _pre(self, r0):
        kb = self.kb
        if self.ysrc is None:
            ys, ysb = self.ystg.next()
            yT = self.yT
            kb.op('sp', lambda e, ys=ys: e.dma_start(out=ys[:, :, :], in_=yT[:, r0:r0 + 128].rearrange("(k p) t -> p k t", p=128)), writes=[ysb], dma=True)
            yb, ybb = self.ybf.next()
            kb.op('pool', lambda e, yb=yb, ys=ys: e.tensor_copy(out=yb[:, :, :], in_=ys[:, :, :]), reads=[ysb], writes=[ybb])
        else:
            aps, sbufs = self.ysrc(r0)
            ykb16, ykb0 = self.ytokb.next()
            ykb16b = self.ykbs.setdefault(id(ykb0), [ykb0] + [Buf() for _ in range(3)])
            for r, ap_ in enumerate(aps):
                kb.op('sp', lambda e, ykb16=ykb16, r=r, ap_=ap_: e.dma_start(out=ykb16[:, r * 256:(r + 1) * 256], in_=ap_), reads=sbufs, writes=[ykb16b[r]], dma=True)
            pt, ptb = self.pyt.next()
            idt, idtb = self.ident
            for k in range(8):
                kb.op('pe', lambda e, pt=pt, ykb16=ykb16, k=k: e.transpose(out=pt[:, k, :], in_=ykb16[:, k * 128:(k + 1) * 128], identity=idt[:, :]), reads=[ykb16b, idtb], writes=[ptb])
            yb, ybb = self.ybf.next()
            kb.op('act', lambda e, yb=yb, pt=pt: e.activation(out=yb[:, :, :], in_=pt[:, :, :], func=AF.Copy), reads=[ptb], writes=[ybb])
        xi, xib = self.xin.next()
        kb.op('sp', lambda e, xi=xi: e.dma_start(out=xi[:, :], in_=self.xsrc[r0:r0 + 128, :]), writes=[xib], dma=True)
        self.pref[r0] = (yb, ybb, xi, xib)

    def tile(self, r0, n, nxt=None):
        kb = self.kb
        if r0 not in self.pref:
            self._pre(r0)
        if nxt is not None and nxt not in self.pref:
            self._pre(nxt)
        yb, ybb, xi, xib = self.pref.pop(r0)
        pss = [self.pol.next(), self.pol.next()]
        for half in range(2):
            ps, psb = pss[half]
            for k in range(8):
                kb.op('pe', lambda e, ps=ps, k=k, half=half, yb=yb: e.matmul(ps[:, :], lhsT=yb[:, k, :], rhs=self.wo[:, k, half * 512:(half + 1) * 512],
                                                                           start=(k == 0), stop=(k == 7)), reads=[ybb, self.wob], writes=[psb])
        st, stb = self.st.next()
        for half in range(2):
            ps, psb = pss[half]
            kb.op('act', lambda e, ps=ps, st=st, half=half: e.activation(out=self.junk[:, :], in_=ps[:, :], func=AF.Square, accum_out=st[:, half:half + 1]),
                  reads=[psb], writes=[stb, self.junkb])
        kb.op('dve', lambda e, st=st: e.tensor_tensor(out=st[:, 2:3], in0=st[:, 0:1], in1=st[:, 1:2], op=ALU.add), reads=[stb], writes=[stb])
        kb.op('act', lambda e, st=st: e.activation(out=st[:, 3:4], in_=st[:, 2:3], func=AF.Sqrt, scale=1.0 / D, bias=self.epsT[:, 0:1]), reads=[stb, self.epsb], writes=[stb])
        kb.op('dve', lambda e, st=st: e.reciprocal(out=st[:, 4:5], in_=st[:, 3:4]), reads=[stb], writes=[stb])
        tm_, tmb_ = self.tmp.next()
        for half in range(2):
            ps, psb = pss[half]
            kb.op('dve', lambda e, ps=ps, st=st, tm_=tm_, half=half: e.scalar_tensor_tensor(out=tm_[:, half * 512:(half + 1) * 512], in0=ps[:, :], scalar=st[:, 4:5],
                                                                                       in1=self.NG[:, n, half * 512:(half + 1) * 512], op0=ALU.mult, op1=ALU.mult),
                  reads=[psb, stb, self.NGb], writes=[tmb_])
        xo, xob = self.xout.next()
        kb.op('dve', lambda e, xo=xo, tm_=tm_, xi=xi: e.tensor_tensor(out=xo[:, :], in0=tm_[:, :], in1=xi[:, :], op=ALU.add), reads=[tmb_, xib], writes=[xob])
        return xo, xob


L3_TOK = 2048


def part3(kb, nc, IN, ZG, X1s, OUT, C):
    epsT, epsb, ones1, ones1b = C['epsT'], C['epsb'], C['ones1'], C['ones1b']
    with ExitStack() as g:
        sil = kb.sb('sil3', [128, 8, 2], F32, g)
        silb = Buf(strict=True)
        kb.op('sp', lambda e: e.dma_start(out=sil[:], in_=IN['sil_in']), writes=[silb], dma=True)
        kb.op('act', lambda e: e.activation(out=sil[:], in_=sil[:], func=AF.Silu), reads=[silb], writes=[silb])
        NG = kb.sb('NG3', [128, 2, D], F32, g)
        with ExitStack() as g0:
            NG, NGb = gate_rows(kb, g0, nc, sil, silb, IN['adawg1'], IN['adabr1'], IN['npostb1'], ones1, ones1b, g0, NG=NG)
            kb.barrier(C['bar'][:])
        op = OutProj(kb, g, nc, None, IN['wout1'], X1s, NG, NGb, epsT, epsb, nslot=4, ysrc=ZG.src, ident=(C['ident_bf'], C['identb']), npol=6)
        for i in range(L // 128):
            xo, xob = op.tile(i * 128, 0, nxt=((i + 1) * 128 if (i + 1) * 128 < L else None))
            kb.op('pool', lambda e, xo=xo, i=i: e.dma_start(out=OUT[i * 128:(i + 1) * 128, :], in_=xo[:, :]), reads=[xob], dma=True, final=True)
        kb.barrier(C['bar'][:])


RT_PARAMS = {'logit': [128, 4], 'diffT': [128, 2, 128], 'mask01T': [128, 2, 128], 'posxi': [128, 2, 128], 'poszeta': [128, 2]}


def part2(kb, nc, IN, YG, X1s, ZD, C):
    debug = False
    sil_in, adawg, adabr, npostb, adaw1, adab1, normw1, win1, ropec, ropes = (IN[k_] for k_ in (
        'sil_in', 'adawg0', 'adabr0', 'npostb0', 'adaw1', 'adab1', 'normw1', 'win1', 'ropec', 'ropes'))
    xin, wout0 = IN['xin'], IN['wout0p']
    RP = {k_: IN['r_' + k_] for k_ in RT_PARAMS}
    X1 = X1s
    Z = ZD
    QKV = dram_tmp(nc, 'QKV', [LT, 768], BF16, debug)
    GS = dram_tmp(nc, 'GS', [LT, 256], F32, debug)
    bar, ident_f, identfb, ident_bf, identb, epsT, epsb, ones1, ones1b = (C[k_] for k_ in (
        'bar', 'ident_f', 'identfb', 'ident_bf', 'identb', 'epsT', 'epsb', 'ones1', 'ones1b'))
    with ExitStack() as pa:
        sil = kb.sb('sil', [128, 8, 2], F32, pa)
        silb = Buf(strict=True)
        kb.op('sp', lambda e: e.dma_start(out=sil[:], in_=sil_in), writes=[silb], dma=True)
        kb.op('act', lambda e: e.activation(out=sil[:], in_=sil[:], func=AF.Silu), reads=[silb], writes=[silb])
        NG = kb.sb('NG', [128, 2, D], F32, pa)
        mod1 = kb.sb('mod1', [128, 16, 2], F32, pa)
        G1 = kb.sb('G1', [128, 8, 2], F32, pa)
        with ExitStack() as pg0:
            NG, NGb = gate_rows(kb, pg0, nc, sil, silb, adawg, adabr, npostb, ones1, ones1b, pg0, NG=NG)
            mod1, mod1b = adaln_vectors(kb, pg0, nc, sil_in, adaw1, adab1, normw1, 16, mod_tile=mod1)
            nw = kb.sb('nw1', [128, 8], F32, pg0)
            nwb, = load_consts(kb, [(nw, normw1)])
            G1buf = Buf(strict=True)
            for n in range(2):
                kb.op('dve', lambda e, n=n: e.scalar_tensor_tensor(out=G1[:, :, n], in0=mod1[:, 8:16, n], scalar=1.0, in1=nw[:, :], op0=ALU.add, op1=ALU.mult),
                      reads=[mod1b, nwb], writes=[G1buf])
            kb.barrier(bar[:])
        Gb = {'G': G1buf, 'eps': epsT, 'epsb': epsb}
        op0 = OutProj(kb, pa, nc, None, wout0, xin, NG, NGb, epsT, epsb, nslot=3, ysrc=YG.src, ident=(ident_bf, identb))
        w1, w1b = load_weight_bf16(kb, pa, nc, 'w1_bf', win1, D, pa)
        for k in range(8):
            kb.op('pool', lambda e, k=k: e.tensor_scalar(out=w1[:, k, 256:512], in0=w1[:, k, 256:512], scalar1=float(128 ** -0.5), scalar2=None, op0=ALU.mult), reads=[w1b], writes=[w1b])
        pqk = Ring(kb, 'pqk', [128, 512], F32, 2, psum=True, es=pa)
        csr = Ring(kb, 'cs', [128, 2, 64], F32, 2, es=pa)
        qkvr = Ring(kb, 'qkvt', [128, 768], BF16, 2, es=pa)
        rtmp = Ring(kb, 'rtmp', [128, 4, 64], F32, 4, es=pa)
        gsr = Ring(kb, 'gst', [128, 256], F32, 2, es=pa)

        rlist = [t0_ + sb_ * 128 for (t0_, T_) in TILES for sb_ in range(T_ // 128)]

        def get_x(r0):
            n = 1 if r0 < LC else 0
            ix = rlist.index(r0)
            xo, xob = op0.tile(r0, n, nxt=(rlist[ix + 1] if ix + 1 < len(rlist) else None))
            if r0 >= LC:
                kb.op('pool', lambda e, xo=xo, r0=r0: e.dma_start(out=X1[r0 - LC:r0 - LC + 128, :], in_=xo[:, :]), reads=[xob], dma=True)
            return xo, xob

        def emit_tile(t0, T, hT, hb):
            for sub in range(T // 128):
                r0 = t0 + sub * 128
                psA, psAb = pqk.next()
                psB, psBb = pqk.next()
                for half, (ps, psb) in enumerate([(psA, psAb), (psB, psBb)]):
                    for k in range(8):
                        kb.op('pe', lambda e, ps=ps, k=k, half=half, sub=sub: e.matmul(ps[:, :], lhsT=hT[:, k, sub * 128:(sub + 1) * 128], rhs=w1[:, k, half * 512:(half + 1) * 512],
                                                                                     start=(k == 0), stop=(k == 7)), reads=[hb, w1b], writes=[psb])
                qkv, qkvb = qkvr.next()
                if r0 >= LC:
                    cs, csb = csr.next()
                    kb.op('sp', lambda e, cs=cs, r0=r0: e.dma_start(out=cs[:, 0, :], in_=ropec[r0 - LC:r0 - LC + 128, :]), writes=[csb], dma=True)
                    kb.op('sp', lambda e, cs=cs, r0=r0: e.dma_start(out=cs[:, 1, :], in_=ropes[r0 - LC:r0 - LC + 128, :]), writes=[csb], dma=True)
                    pv_ = psA[:, :].rearrange("p (g h f) -> p g h f", g=4, h=2)
                    ov_ = qkv[:, 0:512].rearrange("p (g h f) -> p g h f", g=4, h=2)
                    cb_ = cs[:, 0, :].unsqueeze(1).to_broadcast([128, 4, 64])
                    sb_ = cs[:, 1, :].unsqueeze(1).to_broadcast([128, 4, 64])
                    ta, tab = rtmp.next()
                    tb, tbb = rtmp.next()
                    kb.op('dve', lambda e, ta=ta, pv_=pv_, cb_=cb_: e.tensor_tensor(out=ta[:, :, :], in0=pv_[:, :, 0, :], in1=cb_, op=ALU.mult), reads=[psAb, csb], writes=[tab])
                    kb.op('dve', lambda e, tb=tb, pv_=pv_, sb_=sb_: e.tensor_tensor(out=tb[:, :, :], in0=pv_[:, :, 1, :], in1=sb_, op=ALU.mult), reads=[psAb, csb], writes=[tbb])
                    kb.op('pool', lambda e, ta=ta, tb=tb, ov_=ov_: e.tensor_tensor(out=ov_[:, :, 0, :], in0=ta[:, :, :], in1=tb[:, :, :], op=ALU.subtract), reads=[tab, tbb], writes=[qkvb])
                    tc_, tcb = rtmp.next()
                    td, tdb = rtmp.next()
                    kb.op('dve', lambda e, tc_=tc_, pv_=pv_, sb_=sb_: e.tensor_tensor(out=tc_[:, :, :], in0=pv_[:, :, 0, :], in1=sb_, op=ALU.mult), reads=[psAb, csb], writes=[tcb])
                    kb.op('dve', lambda e, td=td, pv_=pv_, cb_=cb_: e.tensor_tensor(out=td[:, :, :], in0=pv_[:, :, 1, :], in1=cb_, op=ALU.mult), reads=[psAb, csb], writes=[tdb])
                    kb.op('pool', lambda e, tc_=tc_, td=td, ov_=ov_: e.tensor_tensor(out=ov_[:, :, 1, :], in0=tc_[:, :, :], in1=td[:, :, :], op=ALU.add), reads=[tcb, tdb], writes=[qkvb])
                else:
                    kb.op('act', lambda e, qkv=qkv, psA=psA: e.activation(out=qkv[:, 0:512], in_=psA[:, :], func=AF.Copy), reads=[psAb], writes=[qkvb])
                kb.op('act', lambda e, qkv=qkv, psB=psB: e.activation(out=qkv[:, 512:768], in_=psB[:, 0:256], func=AF.Copy), reads=[psBb], writes=[qkvb])
                gs, gsb = gsr.next()
                kb.op('act', lambda e, gs=gs, psB=psB: e.activation(out=gs[:, :], in_=psB[:, 256:512], func=AF.Silu), reads=[psBb], writes=[gsb])
                kb.op('pool', lambda e, qkv=qkv, r0=r0: e.dma_start(out=QKV[r0:r0 + 128, :], in_=qkv[:, :]), reads=[qkvb], dma=True)
                kb.op('pool', lambda e, gs=gs, r0=r0: e.dma_start(out=GS[r0:r0 + 128, :], in_=gs[:, :]), reads=[gsb], dma=True)

        phase_proj(kb, nc, pa, None, None, G1, mod1, Gb, ident_bf, identb, emit_tile, get_x=get_x)
        kb.barrier(bar[:])
    with ExitStack() as pr:
        phase_ret(kb, nc, pr, QKV, GS, Z, RP, ident_bf, identb, epsT, epsb)
        kb.barrier(bar[:])


def phase_ret(kb, nc, es, QKV, GS, Z, RP, ident_bf, identb, epsT, epsb):
    def const(name, shape, src):
        t = kb.sb(name, shape, F32, es)
        b, = load_consts(kb, [(t, src)])
        return t, b
    lgt, lgtb = const('lgt', [128, 4], RP['logit'])
    lgtb.strict = True
    diffT, diffTb = const('diffT', [128, 2, 128], RP['diffT'])
    m01, m01b = const('m01', [128, 2, 128], RP['mask01T'])
    pxi, pxib = const('pxi', [128, 2, 128], RP['posxi'])
    pze, pzeb = const('pze', [128, 2], RP['poszeta'])
    kb.op('act', lambda e: e.activation(out=lgt[:, :], in_=lgt[:, :], func=AF.Exp, scale=-1.0), reads=[lgtb], writes=[lgtb])
    kb.op('dve', lambda e: e.tensor_scalar(out=lgt[:, :], in0=lgt[:, :], scalar1=1.0, scalar2=None, op0=ALU.add), reads=[lgtb], writes=[lgtb])
    kb.op('act', lambda e: e.activation(out=lgt[:, :], in_=lgt[:, :], func=AF.Ln), reads=[lgtb], writes=[lgtb])
    kb.op('dve', lambda e: e.tensor_scalar(out=lgt[:, :], in0=lgt[:, :], scalar1=-1.0, scalar2=None, op0=ALU.mult), reads=[lgtb], writes=[lgtb])
    dmt = kb.sb('dmt', [128, 4, 128], BF16, es)
    xib = kb.sb('xib', [128, 4, 128], F32, es)
    zet = kb.sb('zet', [128, 8], F32, es)
    tabb = Buf(strict=True)
    tmpd = kb.sb('tmpd', [128, 128], F32, es)
    tmpdb = Buf()
    for hl in range(2):
        for dr in range(2):
            j = 2 * hl + dr
            kb.op('act', lambda e, j=j, dr=dr: e.activation(out=tmpd[:, :], in_=diffT[:, dr, :], func=AF.Exp, scale=lgt[:, j:j + 1]), reads=[diffTb, lgtb], writes=[tmpdb])
            kb.op('dve', lambda e, j=j, dr=dr: e.tensor_tensor(out=dmt[:, j, :], in0=tmpd[:, :], in1=m01[:, dr, :], op=ALU.mult), reads=[tmpdb, m01b], writes=[tabb])
            kb.op('act', lambda e, j=j, dr=dr: e.activation(out=xib[:, j, :], in_=pxi[:, dr, :], func=AF.Exp, scale=lgt[:, j:j + 1]), reads=[pxib, lgtb], writes=[tabb])
            kb.op('act', lambda e, j=j, dr=dr: e.activation(out=zet[:, j:j + 1], in_=pze[:, dr:dr + 1], func=AF.Exp, scale=lgt[:, j:j + 1]), reads=[pzeb, lgtb], writes=[tabb])
            kb.op('act', lambda e, j=j: e.activation(out=zet[:, 4 + j:5 + j], in_=lgt[:, j:j + 1], func=AF.Exp, scale=128.0), reads=[lgtb], writes=[tabb])
    oacc = kb.sb('oacc', [128, NCH, 256], F32, es)
    oaccb = [Buf() for _ in range(NCH)]
    kb.op('pool', lambda e: e.memset(oacc[:], 0.0), writes=oaccb)
    R = kb.sb('Rst', [128, 2, 128], F32, es)
    Rbf = kb.sb('Rbf', [128, 2, 128], BF16, es)
    Rb = [Buf(), Buf()]
    qr = Ring(kb, 'rq', [128, 768], BF16, 3, es=es)
    qtr = Ring(kb, 'rqT', [128, 128], BF16, 3, es=es)
    ktr_ = Ring(kb, 'rkT', [128, 128], BF16, 3, es=es)
    qxr = Ring(kb, 'rqx', [128, 128], BF16, 3, es=es)
    kzr = Ring(kb, 'rkz', [128, 128], BF16, 3, es=es)
    sdr = Ring(kb, 'rsd', [128, 128], BF16, 3, es=es)
    ptT = Ring(kb, 'rptT', [128, 1024], BF16, 2, psum=True, es=es)
    pS = Ring(kb, 'rpS', [128, 512], F32, 2, psum=True, es=es)
    pO = Ring(kb, 'rpO', [128, 512], F32, 2, psum=True, es=es)
    pR = Ring(kb, 'rpR', [128, 512], F32, 2, psum=True, es=es)
    gsr = Ring(kb, 'rgs', [128, 256], F32, 2, es=es)
    zr = Ring(kb, 'rz', [128, 256], F32, 2, es=es)
    sst = Ring(kb, 'rss', [128, 8], F32, 4, es=es, strict=True)
    junk = kb.sb('rjunk', [128, 128], BF16, es)
    junkb = Buf()
    for dr in range(2):
        kb.op('dve', lambda e: e.memset(R[:], 0.0), writes=Rb)
        kb.op('dve', lambda e: e.memset(Rbf[:], 0.0), writes=Rb)
        for c in CHUNK_ORDER[dr]:
            r0 = c * 128
            q, qb = qr.next()
            kb.op('sp', lambda e, q=q, r0=r0: e.dma_start(out=q[:, :], in_=QKV[r0:r0 + 128, :]), writes=[qb], dma=True)
            for hl in range(2):
                j = 2 * hl + dr
                vt_ = q[:, 512 + hl * 128:512 + (hl + 1) * 128]
                ktok = q[:, 256 + hl * 128:256 + (hl + 1) * 128]
                if c >= 2:
                    pt, ptb = ptT.next()
                    kb.op('pe', lambda e, pt=pt, q=q, hl=hl: e.transpose(out=pt[:, 0:128], in_=q[:, hl * 128:(hl + 1) * 128], identity=ident_bf[:, :]), reads=[qb, identb], writes=[ptb])
                    kb.op('pe', lambda e, pt=pt, ktok=ktok: e.transpose(out=pt[:, 128:256], in_=ktok, identity=ident_bf[:, :]), reads=[qb, identb], writes=[ptb])
                    qT, qTb = qtr.next()
                    kT, kTb = ktr_.next()
                    qx, qxb = qxr.next()
                    kb.op('act', lambda e, qT=qT, pt=pt: e.activation(out=qT[:, :], in_=pt[:, 0:128], func=AF.Copy), reads=[ptb], writes=[qTb])
                    kb.op('dve', lambda e, qx=qx, pt=pt, j=j: e.tensor_tensor(out=qx[:, :], in0=pt[:, 0:128], in1=xib[:, j, :], op=ALU.mult), reads=[ptb, tabb], writes=[qxb])
                    kb.op('act', lambda e, kT=kT, pt=pt: e.activation(out=kT[:, :], in_=pt[:, 128:256], func=AF.Copy), reads=[ptb], writes=[kTb])
                    ps, psb = pS.next()
                    kb.op('pe', lambda e, ps=ps, kT=kT, qT=qT: e.matmul(ps[:, 0:128], lhsT=kT[:, :], rhs=qT[:, :], start=True, stop=True), reads=[kTb, qTb], writes=[psb])
                    sd, sdb = sdr.next()
                    kb.op('dve', lambda e, sd=sd, ps=ps, j=j: e.tensor_tensor(out=sd[:, :], in0=ps[:, 0:128], in1=dmt[:, j, :], op=ALU.mult), reads=[psb, tabb], writes=[sdb])
                    po, pob = pO.next()
                    kb.op('pe', lambda e, po=po, sd=sd, vt_=vt_: e.matmul(po[:, 0:128], lhsT=sd[:, :], rhs=vt_, start=True, stop=False), reads=[sdb, qb], writes=[pob])
                    kb.op('pe', lambda e, po=po, qx=qx, hl=hl: e.matmul(po[:, 0:128], lhsT=qx[:, :], rhs=Rbf[:, hl, :], start=False, stop=True), reads=[qxb, Rb[hl]], writes=[pob])
                    if dr == 0:
                        kb.op('act', lambda e, po=po, c=c, hl=hl: e.activation(out=oacc[:, c, hl * 128:(hl + 1) * 128], in_=po[:, 0:128], func=AF.Copy), reads=[pob], writes=[oaccb[c]])
                    else:
                        kb.op('dve', lambda e, po=po, c=c, hl=hl: e.tensor_tensor(out=oacc[:, c, hl * 128:(hl + 1) * 128], in0=po[:, 0:128], in1=oacc[:, c, hl * 128:(hl + 1) * 128], op=ALU.add),
                              reads=[pob, oaccb[c]], writes=[oaccb[c]])
                kz, kzb = kzr.next()
                kb.op('dve', lambda e, kz=kz, ktok=ktok, j=j: e.tensor_scalar(out=kz[:, :], in0=ktok, scalar1=zet[:, j:j + 1], scalar2=None, op0=ALU.mult), reads=[qb, tabb], writes=[kzb])
                pr_, prb = pR.next()
                kb.op('pe', lambda e, pr_=pr_, kz=kz, vt_=vt_: e.matmul(pr_[:, 0:128], lhsT=kz[:, :], rhs=vt_, start=True, stop=True), reads=[kzb, qb], writes=[prb])
                kb.op('dve', lambda e, pr_=pr_, hl=hl, j=j: e.scalar_tensor_tensor(out=R[:, hl, :], in0=R[:, hl, :], scalar=zet[:, 4 + j:5 + j], in1=pr_[:, 0:128], op0=ALU.mult, op1=ALU.add),
                      reads=[prb, tabb, Rb[hl]], writes=[Rb[hl]])
                kb.op('act', lambda e, hl=hl: e.activation(out=Rbf[:, hl, :], in_=R[:, hl, :], func=AF.Copy), reads=[Rb[hl]], writes=[Rb[hl]])
            if dr == 1 and c >= 2:
                gs, gsb = gsr.next()
                kb.op('sp', lambda e, gs=gs, r0=r0: e.dma_start(out=gs[:, :], in_=GS[r0:r0 + 128, :]), writes=[gsb], dma=True)
                zt_, ztb = zr.next()
                for hl in range(2):
                    ss, ssb = sst.next()
                    kb.op('act', lambda e, ss=ss, c=c, hl=hl: e.activation(out=junk[:, :], in_=oacc[:, c, hl * 128:(hl + 1) * 128], func=AF.Square, accum_out=ss[:, 0:1]), reads=[oaccb[c]], writes=[ssb, junkb])
                    kb.op('act', lambda e, ss=ss: e.activation(out=ss[:, 1:2], in_=ss[:, 0:1], func=AF.Sqrt, scale=1.0 / 128, bias=epsT[:, 0:1]), reads=[ssb, epsb], writes=[ssb])
                    kb.op('dve', lambda e, ss=ss: e.reciprocal(out=ss[:, 2:3], in_=ss[:, 1:2]), reads=[ssb], writes=[ssb])
                    kb.op('dve', lambda e, ss=ss, zt_=zt_, gs=gs, c=c, hl=hl: e.scalar_tensor_tensor(out=zt_[:, hl * 128:(hl + 1) * 128], in0=oacc[:, c, hl * 128:(hl + 1) * 128], scalar=ss[:, 2:3],
                                                                                                   in1=gs[:, hl * 128:(hl + 1) * 128], op0=ALU.mult, op1=ALU.mult), reads=[ssb, oaccb[c], gsb], writes=[ztb])
                kb.op('pool', lambda e, zt_=zt_, r0=r0: e.dma_start(out=Z.rows(r0 - LC, 0, 256), in_=zt_[:, :]), reads=[ztb], dma=True)


class ChunkedDram:
    def __init__(self, nc, name, nrows, rows_per, width, ranks=1, dtype=BF16):
        self.rp = rows_per
        self.n = nrows // rows_per
        assert self.n * rows_per == nrows
        self.ranks = ranks
        self.tiles = [dram_tmp(nc, '%s%d' % (name, i), [ranks * rows_per, width], dtype) for i in range(self.n)]
        self.bufs = [Buf() for _ in range(self.n)]

    def rows(self, r0, c0, c1, n=128):
        i, lr = r0 // self.rp, r0 % self.rp
        return self.tiles[i][lr:lr + n, c0:c1]

    def src(self, r0):
        i, lr = r0 // self.rp, r0 % self.rp
        return [self.tiles[i][r * self.rp + lr:r * self.rp + lr + 128, :] for r in range(self.ranks)], [self.bufs[i]]

    def lat_groups(self):
        out = []
        tpc = self.rp // 128
        for i in range(self.n):
            b0, b1 = max(tpc * i - 2, 0), min(tpc * i + tpc - 2, 64)
            if b1 <= b0:
                continue
            lrow = (b0 + 2) * 128 - i * self.rp
            out.append((self.tiles[i][lrow:lrow + (b1 - b0) * 128, 192:256].rearrange("(b a) c -> b a c", a=128), b0, b1))
        return out


GROUPS = [[0, 1, 2, 3], [4, 5, 6, 7]]


def all_gather(kb, src, dst):
    for i in range(src.n):
        kb.op('pool', lambda e, i=i: e.collective_compute("AllGather", ALU.bypass, replica_groups=GROUPS, ins=[src.tiles[i].opt()], outs=[dst.tiles[i].opt()]),
              writes=[dst.bufs[i]], cc=True)


FUSED_INPUTS = {'xin': [LT, D], 'sil_in': [128, 8, 2], 'adaw': [D, 2048], 'adab': [128, 16], 'normw': [128, 8], 'wfm': [D, NFM], 'wg': [D, 256],
                'cs64': [64, 128], 'ident': [128, 128],
                'wout0p': [D, D], 'adawg0': [D, D], 'adabr0': [1, D], 'npostb0': [128, D], 'adaw1': [D, 2048], 'adab1': [128, 16], 'normw1': [128, 8],
                'win1': [D, D], 'ropec': [L, 64], 'ropes': [L, 64],
                'wout1': [D, D], 'adawg1': [D, D], 'adabr1': [1, D], 'npostb1': [128, D]}


def build_fused():
    nc = bass.Bass("TRN2", target_bir_lowering=False)
    kb = KB(nc)
    IN = {k_: dram_in(nc, k_, shp) for k_, shp in FUSED_INPUTS.items()}
    for k_, shp in RW_PARAMS.items():
        IN['p_' + k_] = dram_in(nc, 'p_' + k_, shp)
    for k_, shp in FN_PARAMS.items():
        IN['f_' + k_] = dram_in(nc, 'f_' + k_, shp)
    for k_, shp in RT_PARAMS.items():
        IN['r_' + k_] = dram_in(nc, 'r_' + k_, shp)
    OUT = dram_out(nc, 'out', [L, D])
    YB = ChunkedDram(nc, 'Yb', LT, 1408, 256)
    YG = ChunkedDram(nc, 'Yg', LT, 1408, 256, ranks=4)
    ZB = ChunkedDram(nc, 'Zb', L, 1024, 256)
    ZG = ChunkedDram(nc, 'Zg', L, 1024, 256, ranks=4)
    X1s = dram_tmp(nc, 'X1s', [L, D], F32)
    C = {}
    C['bar'] = kb.sb('bar', [128, 1], F32)
    C['ident_f'] = kb.sb('ident_f', [128, 128], F32)
    C['ident_bf'] = kb.sb('ident_bf', [128, 128], BF16)
    C['identfb'], = load_consts(kb, [(C['ident_f'], IN['ident'])])
    C['identb'] = Buf()
    kb.op('dve', lambda e: e.tensor_copy(out=C['ident_bf'][:], in_=C['ident_f'][:]), reads=[C['identfb']], writes=[C['identb']])
    C['epsT'] = kb.sb('epsT', [128, 1], F32)
    C['epsb'] = Buf()
    kb.op('dve', lambda e: e.memset(C['epsT'][:], EPS), writes=[C['epsb']])
    C['ones1'] = kb.sb('ones1', [1, 128], F32)
    C['ones1b'] = Buf()
    kb.op('dve', lambda e: e.memset(C['ones1'][:], 1.0), writes=[C['ones1b']])
    part1(kb, nc, IN, YB, C)
    all_gather(kb, YB, YG)
    part2(kb, nc, IN, YG, X1s, ZB, C)
    all_gather(kb, ZB, ZG)
    part3(kb, nc, IN, ZG, X1s, OUT, C)
    return nc, kb


def l1_inputs(inp, core):
    b, q = core // 4, core % 4
    f = lambda a: np.ascontiguousarray(a, dtype=np.float32)
    xin = np.concatenate([inp['ctx'][b], inp['x'][b]], axis=0)
    sil = np.stack([inp['c'][b].reshape(8,128).T, inp['c_ctx'].reshape(8,128).T], axis=-1)
    adaw = inp['ada_w'][0][:, :2048]
    adab = inp['ada_b'][0][:2048].reshape(16,128).T
    normw = inp['norm_pre'][0].reshape(8,128).T
    W = inp['ev_w_in'][0]
    cols = np.concatenate([np.arange(192)+192*q, 768+np.arange(192)+192*q, 1536+np.arange(192)+192*q,
                           np.arange(2304,2432), np.arange(2432,2560), 3328+64*q+np.arange(64)])
    gcols = np.concatenate([2560+192*q+np.arange(192), 3584+64*q+np.arange(64)])
    c = np.arange(64)
    ang = 2*np.pi*np.outer(c,c)/64
    cs64 = np.concatenate([np.cos(ang), np.sin(ang)], axis=1)
    return dict(xin=f(xin), sil_in=f(sil), adaw=f(adaw), adab=f(adab), normw=f(normw), wfm=f(W[:, cols]), wg=f(W[:, gcols]),
                cs64=f(cs64), ident=np.eye(128, dtype=np.float32)), cols, gcols

def rw_params(inp, core):
    b, q = core // 4, core % 4
    f = lambda a: np.ascontiguousarray(a, dtype=np.float32)
    chs = 192*q + np.arange(192)
    def blk2(v):
        o = np.zeros((128,2), np.float32); o[:,0] = v[:128]; o[:64,1] = v[128:]; return o
    mu = inp['ev_mu'][0]
    mub = np.zeros((128,8), np.float32)
    for j, base in enumerate([0, 768, 1536]):
        m2 = blk2(mu[base+chs]); mub[:, 2*j] = m2[:,0]; mub[:, 2*j+1] = m2[:,1]
    mub[:, 6] = mu[2304:2432]; mub[:, 7] = mu[2432:2560]
    p = np.arange(128)
    lanem = np.stack([(p%4==0),(p%4==1),(p%4==2),(p%4==3),(p%2==0),(p%2==1)],axis=1).astype(np.float32)
    a0 = np.zeros((128,2,2), np.float32)
    for d in range(2): a0[:, d, :] = blk2(inp['ev_a0'][0][d][chs])
    w2 = np.concatenate([inp['ev_w2'][0][0][:, chs], inp['ev_w2'][0][1][:, chs]], axis=0)
    a2 = np.concatenate([inp['ev_a2'][0][0][:, chs], inp['ev_a2'][0][1][:, chs]], axis=0)
    w0 = np.stack([inp['ev_w0'][0][0][chs], inp['ev_w0'][0][1][chs]])[None]
    s = np.arange(128)[:,None]; t = np.arange(128)[None,:]
    strict = [(s<t), (s>t)]; incl = [(s<=t), (s>=t)]
    blk = lambda bs: (np.arange(128)[:,None]//bs == np.arange(128)[None,:]//bs)
    maskSI2 = np.stack([np.concatenate([strict[d], incl[d]],axis=1) for d in range(2)], axis=1).astype(np.float32)
    maskSI = np.stack([np.concatenate([strict[d] & blk(32), incl[d]],axis=1) for d in range(2)], axis=1).astype(np.float32)
    maskST = np.stack([np.stack([(strict[d] & blk(32)).T, (strict[d] & blk(64) & ~blk(32)).T, (strict[d] & ~blk(64)).T], axis=1) for d in range(2)], axis=1).astype(np.float32)
    cdec = -np.exp(-0.5)
    tri = np.stack([np.concatenate([incl[d], strict[d]],axis=1) for d in range(2)], axis=1).astype(np.float32)*cdec
    eh = np.zeros((128,2,4), np.float32); eh[:64,0,0]=1; eh[64:,0,1]=1; eh[:64,1,2]=1
    obd = np.zeros((128,128), np.float32); obd[:64,:64]=1; obd[64:,64:]=1
    return dict(p_mu=mub, p_lanem=lanem, p_k_k=blk2(inp['ev_k_k'][0][chs]), p_k_a=blk2(inp['ev_k_a'][0][chs]),
                p_r_k=blk2(inp['ev_r_k'][0].reshape(-1)[chs]), p_a0=a0, p_w2=f(w2), p_a2=f(a2), p_w0=f(w0),
                p_maskSI=f(maskSI), p_maskSI2=f(maskSI2), p_maskST=f(maskST), p_tri=f(tri), p_ehead=eh, p_ones_bd=obd,
                p_lnx_g=f(np.tile(inp['ev_lnx_g'][0][chs][None], (128,1))), p_lnx_b=f(np.tile(inp['ev_lnx_b'][0][chs][None], (128,1))))

def fn_params():
    f = lambda a: np.ascontiguousarray(a, dtype=np.float32)
    a = np.arange(128); b = np.arange(64)
    ang128 = 2*np.pi*np.outer(a,a)/128
    tw = 2*np.pi*np.outer(a, b)/8192
    ang64 = 2*np.pi*np.outer(b,b)/64
    fl1 = np.concatenate([np.cos(ang64), np.sin(ang64)], axis=0)
    fl2 = np.concatenate([-np.sin(ang64), np.cos(ang64)], axis=0)
    l = np.arange(256); ang256 = 2*np.pi*np.outer(l,l)/256
    c256 = np.cos(ang256).reshape(2,128,256).transpose(1,0,2); ns256 = (-np.sin(ang256)).reshape(2,128,256).transpose(1,0,2)
    return dict(f_c128=f(np.cos(ang128)), f_ns128=f(-np.sin(ang128)), f_twc=f(np.cos(tw)), f_tws=f(np.sin(tw)), f_fl1=f(fl1), f_fl2=f(fl2),
                f_c256=f(c256), f_ns256=f(ns256))


def gate_inputs(inp, b, layer):
    f = lambda a: np.ascontiguousarray(a, dtype=np.float32)
    sil = np.stack([inp['c'][b].reshape(8,128).T, inp['c_ctx'].reshape(8,128).T], axis=-1)
    return dict(sil_in=f(sil), adawg=f(inp['ada_w'][layer][:, 2048:3072]), adabr=f(inp['ada_b'][layer][2048:3072][None]),
                npostb=f(np.tile(inp['norm_post'][layer][None], (128,1))))
def l3_inputs(inp, core, z_full, x1_full):
    b, qtr = core // 4, core % 4
    f = lambda a: np.ascontiguousarray(a, dtype=np.float32)
    sl = slice(2048*qtr, 2048*(qtr+1))
    m = dict(zT=f(z_full[b][sl].T), x1=f(x1_full[b][sl]), wout=f(inp['od_w_out'][0]))
    m.update(gate_inputs(inp, b, 1))
    return m


def l2_inputs(inp, core, y_full):
    b, p = core // 4, core % 4
    f = lambda a: np.ascontiguousarray(a, dtype=np.float32)
    m = dict(xin=f(np.concatenate([inp['ctx'][b], inp['x'][b]], axis=0)), wout=f(inp['ev_w_out'][0]))
    if y_full is not None:
        m['yT'] = f(y_full[b].T)
    m.update(gate_inputs(inp, b, 0))
    m['adaw1'] = f(inp['ada_w'][1][:, :2048]); m['adab1'] = f(inp['ada_b'][1][:2048].reshape(16,128).T); m['normw1'] = f(inp['norm_pre'][1].reshape(8,128).T)
    cols = np.concatenate([off + 256*p + np.arange(256) for off in (0, 1024, 2048, 3072)])
    m['win1'] = f(inp['od_w_in'][0][:, cols])
    pos = np.arange(8192); row = pos//64; col = pos%64
    inv = 10000.0 ** (-np.arange(0, 64, 2, dtype=np.float32)/64)
    ang = np.concatenate([row[:,None]*inv[None], col[:,None]*inv[None]], axis=1).astype(np.float32)
    m['ropec'] = f(np.cos(ang)); m['ropes'] = f(np.sin(ang))
    m['ident'] = np.eye(128, dtype=np.float32)
    lg = inp['od_decay_logit'][0]
    logit = np.zeros((128,4), np.float32)
    for hl in range(2):
        for dr in range(2): logit[:, 2*hl+dr] = lg[dr][2*p+hl]
    j = np.arange(128)[:,None]; i = np.arange(128)[None,:]
    diffT = np.stack([(i-j)*np.ones((128,128)), (j-i)*np.ones((128,128))], axis=1)
    mask = np.stack([(i>=j), (j>i)], axis=1)
    posxi = np.stack([np.tile((np.arange(128)+1)[None], (128,1)), np.tile((128-np.arange(128))[None], (128,1))], axis=1)
    pze = np.stack([127-np.arange(128), np.arange(128)], axis=1)
    m.update(r_logit=logit, r_diffT=f(diffT), r_mask01T=f(mask), r_posxi=f(posxi), r_poszeta=f(pze))
    return m


def fused_inputs(inp, core):
    b, q = core // 4, core % 4
    f = lambda a: np.ascontiguousarray(a, dtype=np.float32)
    m, _, _ = l1_inputs(inp, core)
    m.update(rw_params(inp, core))
    m.update(fn_params())
    m2 = l2_inputs(inp, core, None)
    g0 = gate_inputs(inp, b, 0)
    g1 = gate_inputs(inp, b, 1)
    perm = np.concatenate([np.concatenate([192 * r + np.arange(192), 768 + 64 * r + np.arange(64)]) for r in range(4)])
    m['wout0p'] = f(inp['ev_w_out'][0][perm])
    m['adawg0'], m['adabr0'], m['npostb0'] = g0['adawg'], g0['adabr'], g0['npostb']
    m['adawg1'], m['adabr1'], m['npostb1'] = g1['adawg'], g1['adabr'], g1['npostb']
    m['wout1'] = f(inp['od_w_out'][0])
    for k_ in ('adaw1', 'adab1', 'normw1', 'win1', 'ropec', 'ropes', 'r_logit', 'r_diffT', 'r_mask01T', 'r_posxi', 'r_poszeta'):
        m[k_] = m2[k_]
    return m


def kernel(**inputs):
    inp = {k: np.asarray(v) for k, v in inputs.items()}
    cores = list(range(8))
    nc, kb = build_fused()
    kb.emit()
    maps = [fused_inputs(inp, core) for core in cores]
    res = run_bass_kernel_spmd(nc, maps, core_ids=cores).results
    return np.stack([res[0]['out'], res[4]['out']]).astype(np.float32)
```

```python
import numpy as np
import concourse.bass as bass
import concourse.mybir as mybir
from contextlib import ExitStack
from concourse.bass_utils import run_bass_kernel_spmd

F32 = mybir.dt.float32
BF16 = mybir.dt.bfloat16
ALU = mybir.AluOpType
AF = mybir.ActivationFunctionType

ENGS = ['pe', 'act', 'dve', 'pool', 'sp']
DBG = {}
NDSEM = 16


class Buf:
    __slots__ = ('name', 'w', 'r', 'excl', 'strict')

    def __init__(self, name='', excl=False, strict=False):
        self.name = name
        self.w = None
        self.r = {}
        self.excl = excl
        self.strict = strict


class Op:
    __slots__ = ('eng', 'fn', 'deps', 'signal', 'sig', 'sem', 'target', 'isdma', 'final', 'cc')


class _Rec:
    def __init__(self):
        self.call = None

    def __getattr__(self, name):
        def f(*a, **k):
            self.call = (name, a, k)
            return self
        return f


def _flat(xs):
    out = []
    for x in xs:
        if isinstance(x, (list, tuple)):
            out.extend(_flat(x))
        else:
            out.append(x)
    return out


class KB:
    def __init__(self, nc):
        self.nc = nc
        self.ops = {e: [] for e in ENGS}
        self.phase = Buf('phase')
        self.es = ExitStack()
        self.nops = 0

    def sb(self, name, shape, dtype, es=None):
        self.nalloc = getattr(self, 'nalloc', 0) + 1
        return (es or self.es).enter_context(self.nc.sbuf_tensor('s%d_%s' % (self.nalloc, name), list(shape), dtype))

    def ps(self, name, shape, dtype=F32, es=None):
        self.nalloc = getattr(self, 'nalloc', 0) + 1
        return (es or self.es).enter_context(self.nc.psum_tensor('p%d_%s' % (self.nalloc, name), list(shape), dtype))

    def op(self, eng, fn, reads=(), writes=(), dma=False, final=False, nophase=False, cc=False):
        dma = dma or cc
        o = Op()
        rec = _Rec()
        fn(rec)
        assert rec.call is not None
        o.eng = eng; o.fn = rec.call; o.isdma = dma; o.signal = dma; o.deps = []; o.sig = 0
        o.sem = None; o.target = 0; o.final = final; o.cc = cc
        reads = _flat(reads)
        writes = _flat(writes)
        for b in list(reads):
            if b.excl:
                reads.remove(b)
                if b not in writes:
                    writes.append(b)
        if not nophase:
            reads.append(self.phase)
        deps = {}
        sdeps = set()
        for b in reads:
            if b.w is not None:
                deps[id(b.w)] = b.w
                if b.strict:
                    sdeps.add(id(b.w))
        for b in writes:
            if b.w is not None:
                deps[id(b.w)] = b.w
                if b.strict:
                    sdeps.add(id(b.w))
            for r in b.r.values():
                deps[id(r)] = r
                if b.strict:
                    sdeps.add(id(r))
        for d in deps.values():
            if d is o:
                continue
            if d.isdma or dma or d.eng != eng or id(d) in sdeps or (eng != 'pe' and DBG.get('strict_all', True)):
                o.deps.append(d)
                d.signal = True
        key = ('dma', self.nops) if dma else eng
        for b in reads:
            b.r[key] = o
        for b in writes:
            b.w = o
            b.r = {}
        self.ops[eng].append(o)
        self.nops += 1
        return o

    def barrier(self, tile_ap):
        self.op('dve', lambda e: e.memset(tile_ap, 0.0), writes=[self.phase], nophase=True)

    def emit(self):
        nc = self.nc
        es = self.es
        csem = {}
        for e in ['pe', 'act', 'dve', 'pool']:
            csem[e] = es.enter_context(nc.semaphore('c_' + e))
        dsem = {}
        for e in ENGS:
            if any(o.isdma for o in self.ops[e]):
                dsem[e] = [es.enter_context(nc.semaphore('d_%s%d' % (e, i))) for i in range(NDSEM)]
        for e in ENGS:
            cnt = 0
            nd = 0
            hist = []
            for o in self.ops[e]:
                if o.cc:
                    self.ncc = getattr(self, 'ncc', 0) + 1
                    o.sem = es.enter_context(nc.semaphore('ccs%d' % self.ncc))
                    o.target = 1
                elif o.isdma:
                    slot = nd % NDSEM
                    o.sem = dsem[e][slot]
                    o.target = 16 * (nd // NDSEM + 1)
                    if nd >= NDSEM:
                        o.deps.append(hist[nd - NDSEM])
                    hist.append(o)
                    nd += 1
                elif o.signal:
                    cnt += 1
                    o.sig = cnt
                    o.sem = csem[e]
                    o.target = cnt
        finals = [o for e in ENGS for o in self.ops[e] if o.final]

        def run(engname):
            def body(e):
                waited = {}
                for o in self.ops[engname]:
                    need = {}
                    for d in o.deps:
                        k = id(d.sem)
                        if waited.get(k, 0) < d.target and need.get(k, (None, 0))[1] < d.target:
                            need[k] = (d.sem, d.target)
                    for k, (sem, tgt) in need.items():
                        e.wait_ge(sem, tgt)
                        waited[k] = tgt
                    nm_, a_, k_ = o.fn
                    ins = getattr(e, nm_)(*a_, **k_)
                    if o.signal:
                        ins.then_inc(o.sem, 16 if (o.isdma and not o.cc) else 1)
                if engname == 'sp':
                    for d in finals:
                        k = id(d.sem)
                        if waited.get(k, 0) < d.target:
                            e.wait_ge(d.sem, d.target)
                            waited[k] = d.target
            return body

        with nc.Block() as block:
            block.tensor(run('pe'))
            block.scalar(run('act'))
            block.vector(run('dve'))
            block.gpsimd(run('pool'))
            block.sync(run('sp'))


class Ring:
    def __init__(self, kb, name, shape, dtype, n, psum=False, es=None, strict=False):
        self.n = n
        self.i = 0
        self.slots = []
        for j in range(n):
            t = (kb.ps if psum else kb.sb)('%s%d' % (name, j), shape, dtype, es=es)
            b = Buf('%s%d' % (name, j), excl=psum, strict=strict)
            self.slots.append((t, b))
            if not psum and DBG.get('init_rings', True):
                kb.op('pool', lambda e, t=t: e.memset(t[:], 0.0), writes=[b])

    def next(self):
        s = self.slots[self.i % self.n]
        self.i += 1
        return s


D = 1024
L = 8192
LC = 256
LT = L + LC
NCH = LT // 128
EPS = 1e-6
GN_EPS = 64e-5
FM_BLOCKS = {'r01': (0, 128), 'r2': (128, 64), 'k01': (192, 128), 'k2': (320, 64),
             'v01': (384, 128), 'v2': (512, 64), 'wl': (576, 128), 'al': (704, 128), 'four': (832, 64)}
NFM = 896
TILES = [(0, 256)] + [(256 + 512 * i, 512) for i in range(16)]


def dram_in(nc, name, shape, dtype=F32):
    return nc.dram_tensor(name, list(shape), dtype, kind="ExternalInput").ap()


def dram_out(nc, name, shape, dtype=F32):
    return nc.dram_tensor(name, list(shape), dtype, kind="ExternalOutput").ap()


def dram_tmp(nc, name, shape, dtype=F32, debug=False):
    return nc.dram_tensor(name, list(shape), dtype, kind="ExternalOutput" if debug else "Internal").ap()


def load_consts(kb, items, eng='sp'):
    bufs = []
    for t, src in items:
        b = Buf()
        kb.op(eng, (lambda e, t=t, src=src: e.dma_start(out=t[:], in_=src)), writes=[b], dma=True)
        bufs.append(b)
    return bufs


def adaln_vectors(kb, es, nc, sil_in, adaw, adab, normw, ncolblk, mod_tile=None):
    sil = kb.sb('sil', [128, 8, 2], F32, es)
    silb = Buf(strict=True)
    kb.op('sp', lambda e: e.dma_start(out=sil[:], in_=sil_in), writes=[silb], dma=True)
    kb.op('act', lambda e: e.activation(out=sil[:], in_=sil[:], func=AF.Silu), reads=[silb], writes=[silb])
    adb = kb.sb('adb', [128, ncolblk], F32, es)
    adbb = Buf(strict=True)
    kb.op('sp', lambda e: e.dma_start(out=adb[:], in_=adab), writes=[adbb], dma=True)
    mod = mod_tile if mod_tile is not None else kb.sb('mod', [128, ncolblk, 2], F32, es)
    modb = Buf(strict=True)
    psm = kb.ps('psmod', [128, ncolblk, 2], F32, es)
    psb = Buf(excl=True)
    wt = kb.sb('adaw', [128, 8, ncolblk * 128], F32, es)
    wb = Buf()
    for k in range(8):
        kb.op('sp', lambda e, k=k: e.dma_start(out=wt[:, k, :], in_=adaw[k * 128:(k + 1) * 128, :]), writes=[wb], dma=True)
    for cb in range(ncolblk):
        for k in range(8):
            kb.op('pe', lambda e, k=k, cb=cb: e.matmul(psm[:, cb, :], lhsT=wt[:, k, cb * 128:(cb + 1) * 128],
                                                      rhs=sil[:, k, :], start=(k == 0), stop=(k == 7)),
                  reads=[wb, silb], writes=[psb])
    for n in range(2):
        kb.op('dve', lambda e, n=n: e.tensor_tensor(out=mod[:, :, n], in0=psm[:, :, n], in1=adb[:, :], op=ALU.add),
              reads=[psb, adbb], writes=[modb])
    return mod, modb


def phase_proj(kb, nc, es, xin, Wsrc_list, G, shiftv, Gb, ident_bf, identb, emit_tile, nmod=2, get_x=None, shift_off=0):
    xring = Ring(kb, 'xt', [128, D], F32, 3, es=es)
    xnring = Ring(kb, 'xn', [128, D], BF16, 2, es=es)
    stat = Ring(kb, 'stat', [128, 4], F32, 4, es=es, strict=True)
    junk = kb.sb('junk', [128, D], BF16, es)
    junkb = Buf()
    hring = Ring(kb, 'hT', [128, 8, 512], BF16, 2, es=es)
    ptr = Ring(kb, 'ptr', [128, 8, 128], BF16, 2, psum=True, es=es)
    for (t0, T) in TILES:
        n = 1 if t0 < LC else 0
        hT, hb = hring.next()
        for sub in range(T // 128):
            r0 = t0 + sub * 128
            if get_x is not None:
                xt, xb = get_x(r0)
            else:
                xt, xb = xring.next()
                kb.op('sp', lambda e, xt=xt, r0=r0: e.dma_start(out=xt[:], in_=xin[r0:r0 + 128, :]), writes=[xb], dma=True)
            st, sb_ = stat.next()
            kb.op('act', lambda e, xt=xt, st=st: e.activation(out=junk[:], in_=xt[:], func=AF.Square, accum_out=st[:, 0:1]),
                  reads=[xb], writes=[junkb, sb_])
            kb.op('act', lambda e, st=st: e.activation(out=st[:, 1:2], in_=st[:, 0:1], func=AF.Sqrt, scale=1.0 / D, bias=Gb['eps'][:, 0:1]),
                  reads=[sb_, Gb['epsb']], writes=[sb_])
            kb.op('dve', lambda e, st=st: e.reciprocal(out=st[:, 2:3], in_=st[:, 1:2]), reads=[sb_], writes=[sb_])
            xn, xnb = xnring.next()
            kb.op('dve', lambda e, xn=xn, xt=xt, st=st: e.tensor_scalar(out=xn[:], in0=xt[:], scalar1=st[:, 2:3], scalar2=None, op0=ALU.mult),
                  reads=[xb, sb_], writes=[xnb])
            pt, pb = ptr.next()
            for k in range(8):
                kb.op('pe', lambda e, pt=pt, xn=xn, k=k: e.transpose(out=pt[:, k, :], in_=xn[:, k * 128:(k + 1) * 128], identity=ident_bf[:]),
                      reads=[xnb, identb], writes=[pb])
            for k in range(8):
                if k % 2 == 0:
                    kb.op('act', lambda e, pt=pt, hT=hT, k=k, sub=sub, n=n: e.activation(
                        out=hT[:, k, sub * 128:(sub + 1) * 128], in_=pt[:, k, :], func=AF.Identity,
                        scale=G[:, k, n:n + 1], bias=shiftv[:, shift_off + k, n:n + 1]), reads=[pb, Gb['G']], writes=[hb])
                else:
                    kb.op('dve', lambda e, pt=pt, hT=hT, k=k, sub=sub, n=n: e.tensor_scalar(
                        out=hT[:, k, sub * 128:(sub + 1) * 128], in0=pt[:, k, :], scalar1=G[:, k, n:n + 1],
                        scalar2=shiftv[:, shift_off + k, n:n + 1], op0=ALU.mult, op1=ALU.add), reads=[pb, Gb['G']], writes=[hb])
        emit_tile(t0, T, hT, hb)


def load_weight_bf16(kb, es, nc, name, src, ncols, es_keep):
    wbf = kb.sb(name, [128, 8, ncols], BF16, es_keep)
    wb = Buf()
    stg = Ring(kb, name + '_stg', [128, ncols], F32, 2, es=es)
    for k in range(8):
        s, sbuf_ = stg.next()
        kb.op('sp', lambda e, s=s, k=k: e.dma_start(out=s[:], in_=src[k * 128:(k + 1) * 128, :]), writes=[sbuf_], dma=True)
        eng = 'dve' if k % 2 == 0 else 'pool'
        kb.op(eng, lambda e, s=s, k=k: e.tensor_copy(out=wbf[:, k, :], in_=s[:]), reads=[sbuf_], writes=[wb])
    return wbf, wb


RW_PARAMS = {'mu': [128, 8], 'lanem': [128, 6], 'k_k': [128, 2], 'k_a': [128, 2], 'r_k': [128, 2], 'a0': [128, 2, 2],
             'w2': [128, 192], 'a2': [128, 192], 'w0': [1, 2, 192], 'maskSI': [128, 2, 256], 'maskSI2': [128, 2, 256], 'maskST': [128, 2, 3, 128],
             'tri': [128, 2, 256], 'ehead': [128, 2, 4], 'ones_bd': [128, 128], 'lnx_g': [128, 192], 'lnx_b': [128, 192]}


def part1(kb, nc, IN, YD, C):
    es = kb.es
    debug = False
    stop_after = None
    xin, sil_in, adaw, adab, normw, wfm, wg, cs64 = (IN[k_] for k_ in ('xin', 'sil_in', 'adaw', 'adab', 'normw', 'wfm', 'wg', 'cs64'))
    U = {nm: dram_tmp(nc, 'U_' + nm, [sz, LT], F32, debug) for nm, (off, sz) in FM_BLOCKS.items() if nm != 'four'}
    GT = dram_tmp(nc, 'GT', [LT, 256], F32, debug)
    HT = dram_tmp(nc, 'HT', [LT, 128], F32, debug)
    bar, ident_f, identfb, ident_bf, identb, epsT, epsb = C['bar'], C['ident_f'], C['identfb'], C['ident_bf'], C['identb'], C['epsT'], C['epsb']
    with ExitStack() as pa:
        mod, modb = adaln_vectors(kb, pa, nc, sil_in, adaw, adab, normw, 16)
        nw = kb.sb('nw', [128, 8], F32, pa)
        nwb, = load_consts(kb, [(nw, normw)])
        G = kb.sb('G', [128, 8, 2], F32, pa)
        Gbuf = Buf(strict=True)
        for n in range(2):
            kb.op('dve', lambda e, n=n: e.scalar_tensor_tensor(out=G[:, :, n], in0=mod[:, 8:16, n], scalar=1.0, in1=nw[:, :],
                                                                op0=ALU.add, op1=ALU.mult), reads=[modb, nwb], writes=[Gbuf])
        Gb = {'G': Gbuf, 'eps': epsT, 'epsb': epsb}
        shiftv = mod
        wbf, wbb = load_weight_bf16(kb, pa, nc, 'wfm_bf', wfm, NFM, pa)
        wgb, wgbb = load_weight_bf16(kb, pa, nc, 'wg_bf', wg, 256, pa)
        cs = kb.sb('cs64', [64, 128], F32, pa)
        csb, = load_consts(kb, [(cs, cs64)])
        pmm = Ring(kb, 'pmm', [128, 512], F32, 3, psum=True, es=pa)
        pg = Ring(kb, 'pg', [128, 512], F32, 1, psum=True, es=pa)
        ustg = Ring(kb, 'ustg', [128, 512], F32, 4, es=pa)
        gstg = Ring(kb, 'gstg', [128, 256], F32, 3, es=pa)
        hstg = Ring(kb, 'hstg', [128, 128], F32, 3, es=pa)
        fstg = Ring(kb, 'fstg', [64, 512], F32, 2, es=pa)
        cnt = [0]

        def emit_tile(t0, T, hT, hb):
            for nm, (off, sz) in FM_BLOCKS.items():
                ps, psb = pmm.next()
                for k in range(8):
                    kb.op('pe', lambda e, ps=ps, k=k, off=off, sz=sz: e.matmul(ps[0:sz, 0:T], lhsT=wbf[:, k, off:off + sz], rhs=hT[:, k, 0:T],
                                                                               start=(k == 0), stop=(k == 7)), reads=[wbb, hb], writes=[psb])
                if nm == 'four':
                    fs, fsb = fstg.next()
                    kb.op('act', lambda e, fs=fs, ps=ps: e.activation(out=fs[:, 0:T], in_=ps[0:64, 0:T], func=AF.Copy), reads=[psb], writes=[fsb])
                    for sub in range(T // 128):
                        pgt, pgb = pg.next()
                        kb.op('pe', lambda e, pgt=pgt, fs=fs, sub=sub: e.matmul(pgt[:, 0:128], lhsT=fs[:, sub * 128:(sub + 1) * 128], rhs=cs[:, :],
                                                                                  start=True, stop=True), reads=[fsb, csb], writes=[pgb])
                        hs, hsb = hstg.next()
                        kb.op('dve', lambda e, hs=hs, pgt=pgt: e.tensor_copy(out=hs[:], in_=pgt[:, 0:128]), reads=[pgb], writes=[hsb])
                        r0 = t0 + sub * 128
                        kb.op('pool', lambda e, hs=hs, r0=r0: e.dma_start(out=HT[r0:r0 + 128, :], in_=hs[:]), reads=[hsb], dma=True)
                else:
                    us, usb = ustg.next()
                    cnt[0] += 1
                    if cnt[0] % 2 == 0:
                        kb.op('act', lambda e, us=us, ps=ps, sz=sz: e.activation(out=us[0:sz, 0:T], in_=ps[0:sz, 0:T], func=AF.Copy), reads=[psb], writes=[usb])
                    else:
                        kb.op('dve', lambda e, us=us, ps=ps, sz=sz: e.tensor_copy(out=us[0:sz, 0:T], in_=ps[0:sz, 0:T]), reads=[psb], writes=[usb])
                    kb.op('pool', lambda e, us=us, nm=nm, sz=sz: e.dma_start(out=U[nm][:, t0:t0 + T], in_=us[0:sz, 0:T]), reads=[usb], dma=True)
            for sub in range(T // 128):
                pgt, pgb = pg.next()
                for k in range(8):
                    kb.op('pe', lambda e, pgt=pgt, k=k, sub=sub: e.matmul(pgt[:, 0:256], lhsT=hT[:, k, sub * 128:(sub + 1) * 128], rhs=wgb[:, k, :],
                                                                           start=(k == 0), stop=(k == 7)), reads=[wgbb, hb], writes=[pgb])
                gs, gsb = gstg.next()
                kb.op('act', lambda e, gs=gs, pgt=pgt: e.activation(out=gs[:], in_=pgt[:, 0:256], func=AF.Silu), reads=[pgb], writes=[gsb])
                r0 = t0 + sub * 128
                kb.op('pool', lambda e, gs=gs, r0=r0: e.dma_start(out=GT[r0:r0 + 128, :], in_=gs[:]), reads=[gsb], dma=True)

        phase_proj(kb, nc, pa, xin, None, G, shiftv, Gb, ident_bf, identb, emit_tile)
        kb.barrier(bar[:])
    BLm = ['r01', 'r2', 'k01', 'k2', 'v01', 'v2', 'wl', 'al']
    S = {nm: dram_tmp(nc, 'S_' + nm, [FM_BLOCKS[nm][1], LT], F32, debug) for nm in BLm}
    with ExitStack() as pm:
        mu_t = kb.sb('mu_m', [128, 8], F32, pm)
        lan_t = kb.sb('lan_m', [128, 6], F32, pm)
        mub_, lanb_ = load_consts(kb, [(mu_t, IN['p_mu']), (lan_t, IN['p_lanem'])])
        cf = kb.sb('coef_m', [128, 8, 7], F32, pm)
        cfb = Buf(strict=True)
        kb.op('dve', lambda e: e.tensor_scalar(out=cf[:, :, 0], in0=mu_t[:, :], scalar1=-1.0, scalar2=1.0, op0=ALU.mult, op1=ALU.add), reads=[mub_], writes=[cfb])
        for l in range(6):
            kb.op('dve', lambda e, l=l: e.tensor_scalar(out=cf[:, :, 1 + l], in0=mu_t[:, :], scalar1=lan_t[:, l:l + 1], scalar2=None, op0=ALU.mult), reads=[mub_, lanb_], writes=[cfb])
        wr = Ring(kb, 'winm', [128, 640], F32, 4, es=pm)
        so = Ring(kb, 'smo', [128, 512], F32, 4, es=pm)
        for (t0, T) in TILES:
            isctx = t0 < LC
            seg_lo, seg_hi = (0, LC) if isctx else (LC, LT)
            halo = 1 if isctx else 64
            lo = max(t0 - halo, seg_lo)
            hi = min(t0 + T + halo, seg_hi)
            for bi, nm in enumerate(BLm):
                sz = FM_BLOCKS[nm][1]
                w_, wb_ = wr.next()
                if lo > t0 - halo or hi < t0 + T + halo:
                    kb.op('pool', lambda e, w_=w_: e.memset(w_[:], 0.0), writes=[wb_])
                kb.op('sp', lambda e, w_=w_, nm=nm, sz=sz, lo=lo, hi=hi, t0=t0: e.dma_start(out=w_[0:sz, 64 + lo - t0:64 + hi - t0], in_=U[nm][:, lo:hi]), writes=[wb_], dma=True)
                o_, ob_ = so.next()
                kb.op('dve', lambda e, o_=o_, w_=w_, sz=sz, bi=bi, T=T: e.tensor_scalar(out=o_[0:sz, 0:T], in0=w_[0:sz, 64:64 + T], scalar1=cf[0:sz, bi, 0:1], scalar2=None, op0=ALU.mult),
                      reads=[wb_, cfb], writes=[ob_])
                terms = [(5, -1, None), (6, +1, None)] if isctx else [(1, -1, 'L'), (2, +1, 'R'), (3, -64, None), (4, +64, None)]
                for (ci, off, kind) in terms:
                    def f(e, o_=o_, w_=w_, sz=sz, bi=bi, ci=ci, off=off, kind=kind, T=T):
                        if kind is None:
                            oo = o_[0:sz, 0:T]
                            ii = w_[0:sz, 64 + off:64 + T + off]
                        else:
                            ov = o_[0:sz, 0:T].rearrange("p (r c) -> p r c", c=64)
                            iv = w_[0:sz, 64:64 + T].rearrange("p (r c) -> p r c", c=64)
                            if kind == 'L':
                                oo, ii = ov[:, :, 1:64], iv[:, :, 0:63]
                            else:
                                oo, ii = ov[:, :, 0:63], iv[:, :, 1:64]
                        return e.scalar_tensor_tensor(out=oo, in0=ii, scalar=cf[0:sz, bi, ci:ci + 1], in1=oo, op0=ALU.mult, op1=ALU.add)
                    kb.op('dve', f, reads=[wb_, cfb, ob_], writes=[ob_])
                kb.op('pool', lambda e, o_=o_, nm=nm, sz=sz, t0=t0, T=T: e.dma_start(out=S[nm][:, t0:t0 + T], in_=o_[0:sz, 0:T]), reads=[ob_], dma=True)
        kb.barrier(bar[:])
    if stop_after == 'A':
        return
    with ExitStack() as pb:
        P = {k_: IN['p_' + k_] for k_ in RW_PARAMS}
        YR = YD
        if stop_after != 'skipB':
            phase_rwkv(kb, nc, pb, S, GT, YR, P, ident_f, identfb, ident_bf, identb)
        kb.barrier(bar[:])
    if DBG.get('skip_fnet'):
        return
    PF = {k_: IN['f_' + k_] for k_ in FN_PARAMS}
    YF = YD
    ZS = dram_tmp(nc, 'ZS', [2, 128, 64, 128], F32, False)
    phase_fnet(kb, nc, HT, GT, YF, ZS, PF, bar)
    return


DUMPS = {}


def dump(kb, nc, name, ap, shape, bufs, dt=F32):
    if name in DUMPS:
        return
    t = nc.dram_tensor('dbg_' + name, list(shape), dt, kind="ExternalOutput").ap()
    DUMPS[name] = t
    kb.op('sp', lambda e: e.dma_start(out=t, in_=ap), reads=bufs, dma=True, final=True)

CHUNK_ORDER = {0: list(range(NCH)), 1: [1, 0] + list(range(NCH - 1, 1, -1))}


def phase_rwkv(kb, nc, es, U, GT, YR, P, ident_f, identfb, ident_bf, identb):
    def sbt(name, shape, dt=F32):
        return kb.sb(name, shape, dt, es)

    def const(name, shape, src, dt=F32):
        t = sbt(name, shape, dt)
        b, = load_consts(kb, [(t, src)])
        return t, b

    mu, mub = const('mu', [128, 8], P['mu'])
    lanem, lanemb = const('lanem', [128, 6], P['lanem'])
    kkp, kkpb = const('kkp', [128, 2], P['k_k'])
    kap, kapb = const('kap', [128, 2], P['k_a'])
    rkp, rkpb = const('rkp', [128, 2], P['r_k'])
    a0p, a0pb = const('a0p', [128, 2, 2], P['a0'])
    w2s, w2sb = const('w2s', [128, 192], P['w2'])
    a2s, a2sb = const('a2s', [128, 192], P['a2'])
    w0r, w0rb = const('w0r', [1, 2, 192], P['w0'])
    msi, msib = const('msi', [128, 2, 256], P['maskSI'])
    mst, mstb = const('mst', [128, 2, 3, 128], P['maskST'])
    msi2, msi2b = const('msi2', [128, 2, 256], P['maskSI2'])
    tri, trib = const('tri', [128, 2, 256], P['tri'])
    ehd, ehdb = const('ehd', [128, 2, 4], P['ehead'])
    obd, obdb = const('obd', [128, 128], P['ones_bd'])
    lng, lngb = const('lng', [128, 192], P['lnx_g'])
    lnb, lnbb = const('lnb', [128, 192], P['lnx_b'])
    ones1 = sbt('ones1', [1, 128])
    ones1b = Buf()
    kb.op('dve', lambda e: e.memset(ones1[:], 1.0), writes=[ones1b])
    coef = sbt('coef', [128, 8, 7])
    coefb = Buf(strict=True)
    kb.op('dve', lambda e: e.tensor_scalar(out=coef[:, :, 0], in0=mu[:, :], scalar1=-1.0, scalar2=1.0, op0=ALU.mult, op1=ALU.add),
          reads=[mub], writes=[coefb])
    for l in range(6):
        kb.op('dve', lambda e, l=l: e.tensor_scalar(out=coef[:, :, 1 + l], in0=mu[:, :], scalar1=lanem[:, l:l + 1], scalar2=None, op0=ALU.mult),
              reads=[mub, lanemb], writes=[coefb])
    omka = sbt('omka', [128, 2])
    omkab = Buf(strict=True)
    kb.op('dve', lambda e: e.tensor_scalar(out=omka[:], in0=kap[:], scalar1=-1.0, scalar2=1.0, op0=ALU.mult, op1=ALU.add), reads=[kapb], writes=[omkab])
    gneps = sbt('gneps', [128, 1])
    gnepsb = Buf()
    kb.op('dve', lambda e: e.memset(gneps[:], GN_EPS), writes=[gnepsb])

    yacc = sbt('yacc', [128, NCH, 192])
    yaccb = [Buf() for _ in range(NCH)]
    kb.op('pool', lambda e: e.memset(yacc[:], 0.0), writes=yaccb)
    ST = sbt('ST', [128, 2, 2, 64])
    STbf = sbt('STbf', [128, 2, 2, 64], BF16)
    STb = [[Buf() for _ in range(3)] for _ in range(2)]

    CR, HR = {}, {}
    for d_ in range(2):
        n_ = 'd%d' % d_
        CR[d_] = dict(
            winr=None, smr=Ring(kb, 'smix' + n_, [128, 8, 128], F32, 2, es=es),
            t32=Ring(kb, 't32' + n_, [128, 256], F32, 9, es=es), kkr=Ring(kb, 'kkt' + n_, [128, 2, 128], F32, 1, es=es),
            vtr=Ring(kb, 'vt32' + n_, [128, 192], F32, 2, es=es), vtbr=Ring(kb, 'vtbf' + n_, [128, 192], BF16, 2, es=es),
            sgr=Ring(kb, 'sg' + n_, [128, 192], F32, 1, es=es), e1r=Ring(kb, 'e1' + n_, [128, 2, 256], F32, 2, es=es),
            e2r=Ring(kb, 'e2' + n_, [128, 2, 128], F32, 1, es=es), arr=Ring(kb, 'ar' + n_, [128, 2, 256], BF16, 2, es=es),
            btr=Ring(kb, 'bt' + n_, [128, 2, 128], BF16, 2, es=es), ktr=Ring(kb, 'kt' + n_, [128, 2, 128], BF16, 2, es=es),
            bktr=Ring(kb, 'bkt' + n_, [128, 2, 192], BF16, 2, es=es), bcr=Ring(kb, 'bc' + n_, [128, 4], F32, 2, es=es, strict=True))
        for h_ in range(3):
            n2 = 'd%dh%d' % (d_, h_)
            HR[(d_, h_)] = dict(
                mm1r=Ring(kb, 'mm1' + n2, [128, 256], BF16, 2, es=es), mm2r=Ring(kb, 'mm2' + n2, [128, 256], BF16, 2, es=es),
                xpr=Ring(kb, 'xp' + n2, [128, 256], BF16, 4, es=es), xtr=Ring(kb, 'xt' + n2, [128, 128], BF16, 4, es=es),
                ivr=Ring(kb, 'iv' + n2, [128, 128], BF16, 8, es=es), tmr=Ring(kb, 'tm' + n2, [128, 128], BF16, 2, es=es),
                w1r=Ring(kb, 'w1' + n2, [128, 64], BF16, 2, es=es), utr=Ring(kb, 'ut' + n2, [128, 64], BF16, 2, es=es),
                tsr=Ring(kb, 'ts' + n2, [128, 64], F32, 3, es=es))
    jkr = Ring(kb, 'jk', [128, 64], F32, 2, es=es)
    gtr = Ring(kb, 'gt', [128, 192], F32, 2, es=es)
    fnr = Ring(kb, 'fn', [128, 192], F32, 2, es=es)
    str_ = Ring(kb, 'stt', [128, 8], F32, 6, es=es, strict=True)
    pprep = Ring(kb, 'pprep', [128, 512], F32, 2, psum=True, es=es)
    ptrb = kb.ps('ptrb', [128, 1024], BF16, es)
    ptrbufs = [Buf(excl=True)] * 4
    ptrc = [0]
    pgram = Ring(kb, 'pgram', [128, 512], F32, 1, psum=True, es=es)
    pinv = Ring(kb, 'pinv', [128, 512], F32, 2, psum=True, es=es)
    pseq = kb.ps('pseq', [128, 512], F32, es)
    pseq2 = kb.ps('pseq2', [128, 512], F32, es)
    pseqb = [Buf(excl=True)] * 4 + [Buf(excl=True)] * 4
    pseqt = [pseq] * 4 + [pseq2] * 4
    pseqc = [0]

    def pseq_next():
        i = pseqc[0] % 8
        pseqc[0] += 1
        return pseqt[i][:, (i % 4) * 64:(i % 4 + 1) * 64], pseqb[i], i

    def ptr_next():
        i = ptrc[0] % 4
        ptrc[0] += 1
        return i, ptrbufs[i]

    blkname = [('r01', 'k01', 'v01'), ('r2', 'k2', 'v2')]
    BL = ['r01', 'r2', 'k01', 'k2', 'v01', 'v2', 'wl', 'al']
    BI = {n: i for i, n in enumerate(BL)}
    BSZ = {n: FM_BLOCKS[n][1] for n in BL}
    alt = [0]

    def ew(fn, reads, writes):
        alt[0] += 1
        r_ = _Rec()
        fn(r_)
        stt_ = r_.call[0] == 'scalar_tensor_tensor'
        return kb.op('dve' if (alt[0] % 3 or stt_) else 'pool', fn, reads=reads, writes=writes)

    done = {}
    inflight = {}
    SMB = {}

    def chunk_body(d, c):
        winr, smr, t32, kkr, vtr, vtbr, sgr, e1r, e2r, arr, btr, ktr, bktr, bcr = (CR[d][k_] for k_ in (
            'winr', 'smr', 't32', 'kkr', 'vtr', 'vtbr', 'sgr', 'e1r', 'e2r', 'arr', 'btr', 'ktr', 'bktr', 'bcr'))
        isctx = c < 2
        t0 = c * 128
        seg_lo, seg_hi = (0, LC) if isctx else (LC, LT)
        halo = 1 if isctx else 64
        lo = max(t0 - halo, seg_lo)
        hi = min(t0 + 128 + halo, seg_hi)
        sm, smb0 = smr.next()
        smb = SMB.setdefault(id(smb0), [smb0] + [Buf() for _ in range(7)])
        for nm in BL:
            sz = BSZ[nm]
            kb.op('sp', lambda e, sm=sm, nm=nm, sz=sz, t0=t0: e.dma_start(out=sm[0:sz, BI[nm], :], in_=U[nm][:, t0:t0 + 128]), writes=[smb[BI[nm]]], dma=True)
        if DBG.get('stage', 99) < 2:
            return

        def S(nm):
            return sm[0:BSZ[nm], BI[nm], :]

        yield
        kkt, kktb = kkr.next()
        for bj, (rn, kn, vn) in enumerate(blkname if not DBG.get('skip_kk') else []):
            sz = BSZ[kn]
            q, qb = t32.next()
            kb.op('dve', lambda e, q=q, kn=kn, sz=sz, bj=bj: e.tensor_scalar(out=q[0:sz, 0:128], in0=S(kn), scalar1=kkp[0:sz, bj:bj + 1], scalar2=None, op0=ALU.mult),
                  reads=[smb, kkpb], writes=[qb])
            kb.op('pool', lambda e, q=q, sz=sz: e.tensor_tensor(out=q[0:sz, 128:256], in0=q[0:sz, 0:128], in1=q[0:sz, 0:128], op=ALU.mult), reads=[qb], writes=[qb])
            pp, ppb = pprep.next()
            kb.op('pe', lambda e, pp=pp, q=q, sz=sz: e.matmul(pp[0:sz, 0:128], lhsT=obd[0:sz, 0:sz], rhs=q[0:sz, 128:256], start=True, stop=True),
                  reads=[qb, obdb], writes=[ppb])
            nr, nrb = t32.next()
            kb.op('act', lambda e, nr=nr, pp=pp, sz=sz: e.activation(out=nr[0:sz, 0:128], in_=pp[0:sz, 0:128], func=AF.Sqrt), reads=[ppb], writes=[nrb])
            kb.op('dve', lambda e, nr=nr, sz=sz: e.tensor_scalar(out=nr[0:sz, 0:128], in0=nr[0:sz, 0:128], scalar1=1e-12, scalar2=None, op0=ALU.max), reads=[nrb], writes=[nrb])
            kb.op('dve', lambda e, nr=nr, sz=sz: e.reciprocal(out=nr[0:sz, 128:256], in_=nr[0:sz, 0:128]), reads=[nrb], writes=[nrb])
            kb.op('dve', lambda e, nr=nr, q=q, sz=sz, bj=bj, kkt=kkt: e.tensor_tensor(out=kkt[0:sz, bj, :], in0=q[0:sz, 0:128], in1=nr[0:sz, 128:256], op=ALU.mult),
                  reads=[nrb, qb], writes=[kktb])
        vt, vtb = vtr.next()
        vtbf, vtbfb = vtbr.next()
        pp, ppb = pprep.next()
        for bj, (rn, kn, vn) in enumerate(blkname if not DBG.get('skip_vt') else []):
            sz = BSZ[vn]
            kb.op('pe', lambda e, pp=pp, vn=vn, sz=sz, bj=bj: e.transpose(out=pp[:, bj * 128:bj * 128 + sz], in_=S(vn), identity=ident_f[0:sz, 0:sz]),
                  reads=[smb, identfb], writes=[ppb])
        kb.op('act', lambda e, pp=pp, vt=vt: e.activation(out=vt[:, :], in_=pp[:, 0:192], func=AF.Copy), reads=[ppb], writes=[vtb])
        kb.op('pool', lambda e, vt=vt, vtbf=vtbf: e.tensor_copy(out=vtbf[:, :], in_=vt[:, :]), reads=[vtb], writes=[vtbfb])

        if DBG.get('stage', 99) < 3:
            return
        yield
        th, thb = t32.next()
        kb.op('act', lambda e, th=th: e.activation(out=th[d * 64:(d + 1) * 64, 0:128], in_=sm[d * 64:(d + 1) * 64, BI['wl'], :], func=AF.Tanh),
              reads=[smb], writes=[thb])
        pp, ppb = pprep.next()
        kb.op('pe', lambda e, pp=pp, th=th: e.matmul(pp[:, 0:192], lhsT=th[d * 64:(d + 1) * 64, 0:128], rhs=w2s[d * 64:(d + 1) * 64, :], start=True, stop=False),
              reads=[thb, w2sb], writes=[ppb])
        kb.op('pe', lambda e, pp=pp: e.matmul(pp[:, 0:192], lhsT=ones1[0:1, :], rhs=w0r[0:1, d, :], start=False, stop=True),
              reads=[ones1b, w0rb], writes=[ppb])
        sg, sgb = sgr.next()
        kb.op('act', lambda e, sg=sg, pp=pp: e.activation(out=sg[:, :], in_=pp[:, 0:192], func=AF.Sigmoid), reads=[ppb], writes=[sgb])
        e1, e1b = e1r.next()
        e2, e2b = e2r.next()
        for bj in range(2):
            sz = 128 if bj == 0 else 64
            pp, ppb = pprep.next()
            kb.op('pe', lambda e, pp=pp, sg=sg, bj=bj, sz=sz: e.matmul(pp[0:sz, 0:256], lhsT=sg[:, bj * 128:bj * 128 + sz], rhs=tri[:, d, :], start=True, stop=True),
                  reads=[sgb, trib], writes=[ppb])
            kb.op('act', lambda e, pp=pp, e1=e1, bj=bj, sz=sz: e.activation(out=e1[0:sz, bj, :], in_=pp[0:sz, 0:256], func=AF.Exp), reads=[ppb], writes=[e1b])
            kb.op('act', lambda e, pp=pp, e2=e2, bj=bj, sz=sz: e.activation(out=e2[0:sz, bj, :], in_=pp[0:sz, 0:128], func=AF.Exp, scale=-1.0), reads=[ppb], writes=[e2b])
        if DBG.get('stage', 99) < 4:
            return
        yield
        ar, arb = arr.next()
        bt, btb = btr.next()
        kt, ktb = ktr.next()
        bc, bcb = bcr.next()
        pbon, pbonb, _ = pseq_next()
        for bj, (rn, kn, vn) in enumerate(blkname):
            sz = BSZ[kn]
            co = bj * 128
            pp, ppb = pprep.next()
            kb.op('pe', lambda e, pp=pp, sz=sz, co=co: e.matmul(pp[0:sz, 0:128], lhsT=a2s[d * 64:(d + 1) * 64, co:co + sz], rhs=sm[d * 64:(d + 1) * 64, BI['al'], :], start=True, stop=True),
                  reads=[a2sb, smb], writes=[ppb])
            av, avb = t32.next()
            kb.op('act', lambda e, av=av, pp=pp, sz=sz, bj=bj: e.activation(out=av[0:sz, 0:128], in_=pp[0:sz, 0:128], func=AF.Sigmoid, bias=a0p[0:sz, d, bj:bj + 1]),
                  reads=[ppb, a0pb], writes=[avb])
            kv, kvb = t32.next()
            ew(lambda e, kv=kv, av=av, sz=sz, bj=bj: e.tensor_scalar(out=kv[0:sz, 0:128], in0=av[0:sz, 0:128], scalar1=kap[0:sz, bj:bj + 1], scalar2=omka[0:sz, bj:bj + 1], op0=ALU.mult, op1=ALU.add),
               [avb, kapb, omkab], [kvb])
            ew(lambda e, kv=kv, kn=kn, sz=sz: e.tensor_tensor(out=kv[0:sz, 0:128], in0=kv[0:sz, 0:128], in1=S(kn), op=ALU.mult), [kvb, smb], [kvb])
            ew(lambda e, kv=kv, kt=kt, e2=e2, sz=sz, bj=bj: e.tensor_tensor(out=kt[0:sz, bj, :], in0=kv[0:sz, 0:128], in1=e2[0:sz, bj, :], op=ALU.mult), [kvb, e2b], [ktb])
            ew(lambda e, av=av, e2=e2, sz=sz, bj=bj: e.tensor_tensor(out=av[0:sz, 128:256], in0=av[0:sz, 0:128], in1=e2[0:sz, bj, :], op=ALU.mult), [avb, e2b], [avb])
            ew(lambda e, av=av, bt=bt, kkt=kkt, sz=sz, bj=bj: e.tensor_tensor(out=bt[0:sz, bj, :], in0=av[0:sz, 128:256], in1=kkt[0:sz, bj, :], op=ALU.mult), [avb, kktb], [btb])
            ew(lambda e, ar=ar, kkt=kkt, e1=e1, sz=sz, bj=bj: e.scalar_tensor_tensor(out=ar[0:sz, bj, 0:128], in0=kkt[0:sz, bj, :], scalar=-1.0, in1=e1[0:sz, bj, 128:256], op0=ALU.mult, op1=ALU.mult),
               [kktb, e1b], [arb])
            ew(lambda e, ar=ar, rn=rn, e1=e1, sz=sz, bj=bj: e.tensor_tensor(out=ar[0:sz, bj, 128:256], in0=S(rn), in1=e1[0:sz, bj, 0:128], op=ALU.mult), [smb, e1b], [arb])
            ew(lambda e, kv=kv, rn=rn, sz=sz, bj=bj: e.scalar_tensor_tensor(out=kv[0:sz, 128:256], in0=S(rn), scalar=rkp[0:sz, bj:bj + 1], in1=kv[0:sz, 0:128], op0=ALU.mult, op1=ALU.mult),
               [smb, rkpb, kvb], [kvb])
            kb.op('pe', lambda e, kv=kv, sz=sz, bj=bj: e.matmul(pbon[:, 0:4], lhsT=kv[0:sz, 128:256], rhs=ehd[0:sz, bj, :], start=(bj == 0), stop=(bj == 1)),
                  reads=[kvb, ehdb], writes=[pbonb])
        kb.op('dve', lambda e, bc=bc: e.tensor_copy(out=bc[:, :], in_=pbon[:, 0:4]), reads=[pbonb], writes=[bcb])
        yield
        bkt, bktb = bktr.next()
        for wi, (src, srcb) in enumerate([(bt, btb), (kt, ktb)]):
            pi, pib = ptr_next()
            for bj in range(2):
                sz = 128 if bj == 0 else 64
                kb.op('pe', lambda e, src=src, pi=pi, bj=bj, sz=sz: e.transpose(out=ptrb[:, pi * 256 + bj * 128:pi * 256 + bj * 128 + sz], in_=src[0:sz, bj, :], identity=ident_bf[0:sz, 0:sz]),
                      reads=[srcb, identb], writes=[pib])
            kb.op('act' if wi == 0 else 'dve',
                  (lambda e, pi=pi, bkt=bkt, wi=wi: e.activation(out=bkt[:, wi, :], in_=ptrb[:, pi * 256:pi * 256 + 192], func=AF.Copy)) if wi == 0 else
                  (lambda e, pi=pi, bkt=bkt, wi=wi: e.tensor_copy(out=bkt[:, wi, :], in_=ptrb[:, pi * 256:pi * 256 + 192])),
                  reads=[pib], writes=[bktb])
        tcol = 127 if d == 0 else 0

        if DBG.get('stage', 99) < 5:
            return
        first = done.get(c, 0) == 0
        assert c not in inflight
        inflight[c] = d

        def head_body(h):
            mm1r, mm2r, xpr, xtr, ivr, tmr, w1r, utr, tsr = (HR[(d, h)][k_] for k_ in ('mm1r', 'mm2r', 'xpr', 'xtr', 'ivr', 'tmr', 'w1r', 'utr', 'tsr'))
            bj = h // 2
            base = (h % 2) * 64
            ch0 = 64 * h
            hp = slice(base, base + 64)
            yield
            pg1, pg1b = pgram.next()
            kb.op('pe', lambda e, pg1=pg1, hp=hp, bj=bj: e.matmul(pg1[:, 0:256], lhsT=bt[hp, bj, :], rhs=ar[hp, bj, :], start=True, stop=True), reads=[btb, arb], writes=[pg1b])
            mm1, mm1b = mm1r.next()
            kb.op('dve', lambda e, mm1=mm1, pg1=pg1: e.tensor_tensor(out=mm1[:, :], in0=pg1[:, 0:256], in1=msi[:, d, :], op=ALU.mult), reads=[pg1b, msib], writes=[mm1b])
            pg2, pg2b = pgram.next()
            kb.op('pe', lambda e, pg2=pg2, hp=hp, bj=bj: e.matmul(pg2[:, 0:256], lhsT=kt[hp, bj, :], rhs=ar[hp, bj, :], start=True, stop=True), reads=[ktb, arb], writes=[pg2b])
            kb.op('pe', lambda e, pg2=pg2, hp=hp, bj=bj: e.matmul(pg2[:, 256:384], lhsT=ar[hp, bj, 0:128], rhs=bt[hp, bj, :], start=True, stop=True), reads=[btb, arb], writes=[pg2b])
            mm2, mm2b = mm2r.next()
            kb.op('dve', lambda e, mm2=mm2, pg2=pg2: e.tensor_tensor(out=mm2[:, :], in0=pg2[:, 0:256], in1=msi2[:, d, :], op=ALU.mult), reads=[pg2b, msi2b], writes=[mm2b])
            xt, xtb = xtr.next()
            kb.op('dve', lambda e, xt=xt, pg2=pg2: e.tensor_tensor(out=xt[:, :], in0=pg2[:, 256:384], in1=mst[:, d, 0, :], op=ALU.mult), reads=[pg2b, mstb], writes=[xtb])
            e1t, e1tb = ivr.next()
            kb.op('dve', lambda e, e1t=e1t, pg2=pg2: e.tensor_tensor(out=e1t[:, :], in0=pg2[:, 256:384], in1=mst[:, d, 1, :], op=ALU.mult), reads=[pg2b, mstb], writes=[e1tb])
            e2t, e2tb = ivr.next()
            kb.op('dve', lambda e, e2t=e2t, pg2=pg2: e.tensor_tensor(out=e2t[:, :], in0=pg2[:, 256:384], in1=mst[:, d, 2, :], op=ALU.mult), reads=[pg2b, mstb], writes=[e2tb])
            if DBG.get('stage', 99) < 6:
                return
            yield
            xp, xpb = xpr.next()
            kb.op('pool', lambda e, xp=xp, mm1=mm1: e.tensor_tensor(out=xp[:, 128:256], in0=mm1[:, 0:128], in1=ident_bf[:, :], op=ALU.add), reads=[mm1b, identb], writes=[xpb])
            pv, pvb = pinv.next()
            kb.op('pe', lambda e, pv=pv, xt=xt, mm1=mm1: e.matmul(pv[:, 0:128], lhsT=xt[:, :], rhs=mm1[:, 0:128], start=True, stop=True), reads=[xtb, mm1b], writes=[pvb])
            kb.op('pe', lambda e, pv=pv, xt=xt, mm1=mm1: e.matmul(pv[:, 256:384], lhsT=mm1[:, 0:128], rhs=xt[:, :], start=True, stop=True), reads=[xtb, mm1b], writes=[pvb])
            xt2, xt2b = xtr.next()
            kb.op('act', lambda e, xp=xp, pv=pv: e.activation(out=xp[:, 0:128], in_=pv[:, 0:128], func=AF.Copy), reads=[pvb], writes=[xpb])
            kb.op('dve', lambda e, xt2=xt2, pv=pv: e.tensor_copy(out=xt2[:, :], in_=pv[:, 256:384]), reads=[pvb], writes=[xt2b])
            curxp, curxpb, curxt, curxtb = xp, xpb, xt2, xt2b
            for lev in range(1, 4):
                pv, pvb = pinv.next()
                kb.op('pe', lambda e, pv=pv, cx=curxp, ct=curxt: e.matmul(pv[:, 0:256], lhsT=ct[:, :], rhs=cx[:, 0:256], start=True, stop=True), reads=[curxpb, curxtb], writes=[pvb])
                kb.op('pe', lambda e, pv=pv, cx=curxp, ct=curxt: e.matmul(pv[:, 256:384], lhsT=cx[:, 0:128], rhs=ct[:, :], start=True, stop=True), reads=[curxpb, curxtb], writes=[pvb])
                nxp, nxpb = xpr.next()
                nxt, nxtb = xtr.next()
                kb.op('act', lambda e, nxp=nxp, pv=pv: e.activation(out=nxp[:, 0:128], in_=pv[:, 0:128], func=AF.Copy), reads=[pvb], writes=[nxpb])
                kb.op('dve', lambda e, nxp=nxp, pv=pv, cx=curxp: e.tensor_tensor(out=nxp[:, 128:256], in0=pv[:, 128:256], in1=cx[:, 128:256], op=ALU.add), reads=[pvb, curxpb], writes=[nxpb])
                kb.op('act', lambda e, nxt=nxt, pv=pv: e.activation(out=nxt[:, :], in_=pv[:, 256:384], func=AF.Copy), reads=[pvb], writes=[nxtb])
                curxp, curxpb, curxt, curxtb = nxp, nxpb, nxt, nxtb
                yield
            pv, pvb = pinv.next()
            kb.op('pe', lambda e, pv=pv, cx=curxp, ct=curxt: e.matmul(pv[:, 0:128], lhsT=ct[:, :], rhs=cx[:, 128:256], start=True, stop=True), reads=[curxpb, curxtb], writes=[pvb])
            t32m, t32mb = ivr.next()
            kb.op('dve', lambda e, t32m=t32m, pv=pv, cx=curxp: e.tensor_tensor(out=t32m[:, :], in0=pv[:, 0:128], in1=cx[:, 128:256], op=ALU.add), reads=[pvb, curxpb], writes=[t32mb])
            yield
            pi, pib = ptr_next()
            kb.op('pe', lambda e, pi=pi, t32m=t32m: e.transpose(out=ptrb[:, pi * 256:pi * 256 + 128], in_=t32m[:, :], identity=ident_bf[:, :]), reads=[t32mb, identb], writes=[pib])
            t32t, t32tb = ivr.next()
            kb.op('act', lambda e, pi=pi, t32t=t32t: e.activation(out=t32t[:, :], in_=ptrb[:, pi * 256:pi * 256 + 128], func=AF.Copy), reads=[pib], writes=[t32tb])
            yield
            pv, pvb = pinv.next()
            kb.op('pe', lambda e, pv=pv, e1t=e1t, t32m=t32m: e.matmul(pv[:, 0:128], lhsT=e1t[:, :], rhs=t32m[:, :], start=True, stop=True), reads=[e1tb, t32mb], writes=[pvb])
            z1, z1b = ivr.next()
            kb.op('act', lambda e, z1=z1, pv=pv: e.activation(out=z1[:, :], in_=pv[:, 0:128], func=AF.Copy), reads=[pvb], writes=[z1b])
            pv, pvb = pinv.next()
            kb.op('pe', lambda e, pv=pv, t32t=t32t, z1=z1: e.matmul(pv[:, 0:128], lhsT=t32t[:, :], rhs=z1[:, :], start=True, stop=True), reads=[t32tb, z1b], writes=[pvb])
            kb.op('pe', lambda e, pv=pv, t32t=t32t, z1=z1: e.matmul(pv[:, 256:384], lhsT=z1[:, :], rhs=t32t[:, :], start=True, stop=True), reads=[t32tb, z1b], writes=[pvb])
            t64, t64b = ivr.next()
            t64t, t64tb = ivr.next()
            kb.op('dve', lambda e, t64=t64, pv=pv, t32m=t32m: e.tensor_tensor(out=t64[:, :], in0=pv[:, 0:128], in1=t32m[:, :], op=ALU.add), reads=[pvb, t32mb], writes=[t64b])
            kb.op('dve', lambda e, t64t=t64t, pv=pv, t32t=t32t: e.tensor_tensor(out=t64t[:, :], in0=pv[:, 256:384], in1=t32t[:, :], op=ALU.add), reads=[pvb, t32tb], writes=[t64tb])
            yield
            pv, pvb = pinv.next()
            kb.op('pe', lambda e, pv=pv, e2t=e2t, t64=t64: e.matmul(pv[:, 0:128], lhsT=e2t[:, :], rhs=t64[:, :], start=True, stop=True), reads=[e2tb, t64b], writes=[pvb])
            z2, z2b = ivr.next()
            kb.op('act', lambda e, z2=z2, pv=pv: e.activation(out=z2[:, :], in_=pv[:, 0:128], func=AF.Copy), reads=[pvb], writes=[z2b])
            pv, pvb = pinv.next()
            kb.op('pe', lambda e, pv=pv, t64t=t64t, z2=z2: e.matmul(pv[:, 0:128], lhsT=t64t[:, :], rhs=z2[:, :], start=True, stop=True), reads=[t64tb, z2b], writes=[pvb])
            tm, tmb = tmr.next()
            kb.op('dve', lambda e, tm=tm, pv=pv, t64=t64: e.tensor_tensor(out=tm[:, :], in0=pv[:, 0:128], in1=t64[:, :], op=ALU.add), reads=[pvb, t64b], writes=[tmb])
            if DBG.get('dump') == (d, c) and h == 0:
                dump(kb, nc, 'mm1', mm1[:, :], [128, 256], [mm1b], BF16)
                dump(kb, nc, 'mm2', mm2[:, :], [128, 256], [mm2b], BF16)
                dump(kb, nc, 'tm', tm[:, :], [128, 128], [tmb], BF16)
            yield
            stb = STb[d][h]
            p1, p1b, _ = pseq_next()
            kb.op('pe', lambda e, p1=p1, hp=hp, bj=bj: e.matmul(p1, lhsT=ar[hp, bj, 0:128], rhs=STbf[hp, d, bj, :], start=True, stop=False), reads=[arb, stb], writes=[p1b])
            kb.op('pe', lambda e, p1=p1, mm2=mm2, ch0=ch0: e.matmul(p1, lhsT=mm2[:, 0:128], rhs=vtbf[:, ch0:ch0 + 64], start=False, stop=True), reads=[mm2b, vtbfb], writes=[p1b])
            w1, w1b = w1r.next()
            kb.op('act', lambda e, w1=w1, p1=p1: e.activation(out=w1[:, :], in_=p1, func=AF.Copy), reads=[p1b], writes=[w1b])
            p2, p2b, _ = pseq_next()
            kb.op('pe', lambda e, p2=p2, tm=tm, w1=w1: e.matmul(p2, lhsT=tm[:, :], rhs=w1[:, :], start=True, stop=True), reads=[tmb, w1b], writes=[p2b])
            ut, utb = utr.next()
            kb.op('act', lambda e, ut=ut, p2=p2: e.activation(out=ut[:, :], in_=p2, func=AF.Copy), reads=[p2b], writes=[utb])
            p3, p3b, _ = pseq_next()
            kb.op('pe', lambda e, p3=p3, hp=hp, bj=bj: e.matmul(p3, lhsT=ar[hp, bj, 128:256], rhs=STbf[hp, d, bj, :], start=True, stop=False), reads=[arb, stb], writes=[p3b])
            kb.op('pe', lambda e, p3=p3, mm1=mm1, ut=ut: e.matmul(p3, lhsT=mm1[:, 128:256], rhs=ut[:, :], start=False, stop=False), reads=[mm1b, utb], writes=[p3b])
            kb.op('pe', lambda e, p3=p3, mm2=mm2, ch0=ch0: e.matmul(p3, lhsT=mm2[:, 128:256], rhs=vtbf[:, ch0:ch0 + 64], start=False, stop=True), reads=[mm2b, vtbfb], writes=[p3b])
            if DBG.get('dump') == (d, c) and h == 0:
                dump(kb, nc, 'w1', w1[:, :], [128, 64], [w1b], BF16)
                dump(kb, nc, 'ut', ut[:, :], [128, 64], [utb], BF16)
            ts, tsb = tsr.next()
            kb.op('dve', lambda e, ts=ts, p3=p3, ch0=ch0, h=h: e.scalar_tensor_tensor(out=ts[:, :], in0=vt[:, ch0:ch0 + 64], scalar=bc[:, h:h + 1], in1=p3, op0=ALU.mult, op1=ALU.add),
                  reads=[vtb, bcb, p3b], writes=[tsb])
            if first:
                kb.op('pool', lambda e, ts=ts, ch0=ch0, c=c: e.tensor_copy(out=yacc[:, c, ch0:ch0 + 64], in_=ts[:, :]), reads=[tsb], writes=[yaccb[c]])
            else:
                kb.op('pool', lambda e, ts=ts, ch0=ch0, c=c: e.tensor_tensor(out=yacc[:, c, ch0:ch0 + 64], in0=yacc[:, c, ch0:ch0 + 64], in1=ts[:, :], op=ALU.add), reads=[tsb, yaccb[c]], writes=[yaccb[c]])
            yield
            p4full, p4b, i4 = pseq_next()
            p4 = pseqt[i4][hp, (i4 % 4) * 64:(i4 % 4 + 1) * 64]
            kb.op('pe', lambda e, p4=p4, bkt=bkt, ut=ut, ch0=ch0: e.matmul(p4, lhsT=bkt[:, 0, ch0:ch0 + 64], rhs=ut[:, :], start=True, stop=False), reads=[bktb, utb], writes=[p4b])
            kb.op('pe', lambda e, p4=p4, bkt=bkt, ch0=ch0: e.matmul(p4, lhsT=bkt[:, 1, ch0:ch0 + 64], rhs=vtbf[:, ch0:ch0 + 64], start=False, stop=True), reads=[bktb, vtbfb], writes=[p4b])
            tq, tqb = tsr.next()
            kb.op('dve', lambda e, tq=tq, p4=p4, hp=hp, bj=bj: e.tensor_tensor(out=tq[hp, :], in0=p4, in1=ST[hp, d, bj, :], op=ALU.add), reads=[p4b, stb], writes=[tqb])
            kb.op('dve', lambda e, tq=tq, hp=hp, bj=bj: e.tensor_scalar(out=ST[hp, d, bj, :], in0=tq[hp, :], scalar1=e1[hp, bj, tcol:tcol + 1], scalar2=None, op0=ALU.mult), reads=[tqb, e1b], writes=[stb])
            kb.op('act', lambda e, tq=tq, hp=hp, bj=bj: e.activation(out=STbf[hp, d, bj, :], in_=tq[hp, :], func=AF.Identity, scale=e1[hp, bj, tcol:tcol + 1]), reads=[tqb, e1b], writes=[stb])

        hg = [head_body(h) for h in range(DBG.get('heads', 3))]
        while hg:
            for g_ in list(hg):
                try:
                    next(g_)
                except StopIteration:
                    hg.remove(g_)
            yield
        del inflight[c]
        done[c] = done.get(c, 0) + 1
        if DBG.get('dump') == (d, c):
            dump(kb, nc, 'sm', sm[:, :, :], [128, 8, 128], [smb])
            dump(kb, nc, 'kkt', kkt[:, :, :], [128, 2, 128], [kktb])
            dump(kb, nc, 'vt', vt[:, :], [128, 192], [vtb])
            dump(kb, nc, 'sg', sg[:, :], [128, 192], [sgb])
            dump(kb, nc, 'e1', e1[:, :, :], [128, 2, 256], [e1b])
            dump(kb, nc, 'e2', e2[:, :, :], [128, 2, 128], [e2b])
            dump(kb, nc, 'ar', ar[:, :, :], [128, 2, 256], [arb], BF16)
            dump(kb, nc, 'bt', bt[:, :, :], [128, 2, 128], [btb], BF16)
            dump(kb, nc, 'kt', kt[:, :, :], [128, 2, 128], [ktb], BF16)
            dump(kb, nc, 'bkt', bkt[:, :, :], [128, 2, 192], [bktb], BF16)
            dump(kb, nc, 'bc', bc[:, :], [128, 4], [bcb])
            dump(kb, nc, 'ST', ST[:, d, :, :], [128, 2, 64], STb[d])
            dump(kb, nc, 'yacc', yacc[:, c, :], [128, 192], [yaccb[c]])
        yield
        if done[c] == 2 and DBG.get('stage', 99) >= 8:
            gt, gtb = gtr.next()
            kb.op('sp', lambda e, gt=gt, t0=t0: e.dma_start(out=gt[:, :], in_=GT[t0:t0 + 128, 0:192]), writes=[gtb], dma=True)
            fn, fnb = fnr.next()
            for h in range(3):
                ch0 = 64 * h
                stt, sttb = str_.next()
                jk, jkb = jkr.next()
                kb.op('act', lambda e, stt=stt, jk=jk, c=c, ch0=ch0: e.activation(out=jk[:, :], in_=yacc[:, c, ch0:ch0 + 64], func=AF.Copy, accum_out=stt[:, 0:1]), reads=[yaccb[c]], writes=[sttb, jkb])
                kb.op('act', lambda e, stt=stt, jk=jk, c=c, ch0=ch0: e.activation(out=jk[:, :], in_=yacc[:, c, ch0:ch0 + 64], func=AF.Square, accum_out=stt[:, 1:2]), reads=[yaccb[c]], writes=[sttb, jkb])
                kb.op('dve', lambda e, stt=stt: e.tensor_scalar(out=stt[:, 6:7], in0=stt[:, 0:1], scalar1=1.0 / 64, scalar2=None, op0=ALU.mult), reads=[sttb], writes=[sttb])
                kb.op('dve', lambda e, stt=stt: e.tensor_tensor(out=stt[:, 3:4], in0=stt[:, 6:7], in1=stt[:, 6:7], op=ALU.mult), reads=[sttb], writes=[sttb])
                kb.op('dve', lambda e, stt=stt: e.scalar_tensor_tensor(out=stt[:, 7:8], in0=stt[:, 1:2], scalar=1.0 / 64, in1=stt[:, 3:4], op0=ALU.mult, op1=ALU.subtract), reads=[sttb], writes=[sttb])
                kb.op('act', lambda e, stt=stt: e.activation(out=stt[:, 4:5], in_=stt[:, 7:8], func=AF.Sqrt, bias=gneps[:, 0:1]), reads=[sttb, gnepsb], writes=[sttb])
                kb.op('dve', lambda e, stt=stt: e.reciprocal(out=stt[:, 5:6], in_=stt[:, 4:5]), reads=[sttb], writes=[sttb])
                kb.op('dve', lambda e, stt=stt, fn=fn, c=c, ch0=ch0: e.tensor_scalar(out=fn[:, ch0:ch0 + 64], in0=yacc[:, c, ch0:ch0 + 64], scalar1=stt[:, 6:7], scalar2=stt[:, 5:6], op0=ALU.subtract, op1=ALU.mult),
                      reads=[sttb, yaccb[c]], writes=[fnb])
            if DBG.get('dumpfin') == c:
                dump(kb, nc, 'stt', stt[:, :], [128, 8], [sttb])
                dump(kb, nc, 'fn0', fn[:, :], [128, 192], [fnb])
                dump(kb, nc, 'gt', gt[:, :], [128, 192], [gtb])
                dump(kb, nc, 'yaccf', yacc[:, c, :], [128, 192], [yaccb[c]])
                dump(kb, nc, 'lng', lng[:, :], [128, 192], [lngb])
            kb.op('pool', lambda e, fn=fn: e.tensor_tensor(out=fn[:, :], in0=fn[:, :], in1=lng[:, :], op=ALU.mult), reads=[fnb, lngb], writes=[fnb])
            kb.op('pool', lambda e, fn=fn: e.tensor_tensor(out=fn[:, :], in0=fn[:, :], in1=lnb[:, :], op=ALU.add), reads=[fnb, lnbb], writes=[fnb])
            kb.op('dve', lambda e, fn=fn, gt=gt: e.tensor_tensor(out=fn[:, :], in0=fn[:, :], in1=gt[:, :], op=ALU.mult), reads=[fnb, gtb], writes=[fnb])
            kb.op('pool', lambda e, fn=fn, t0=t0: e.dma_start(out=YR.rows(t0, 0, 192), in_=fn[:, :]), reads=[fnb], dma=True)


    def dir_gen(d):
        for c in (DBG['chunks'][d] if 'chunks' in DBG else CHUNK_ORDER[d]):
            yield from chunk_body(d, c)

    dirs = DBG.get('dirs', [0, 1])
    for d in dirs:
        kb.op('dve', lambda e, d=d: e.memset(ST[:, d, :, :], 0.0), writes=STb[d])
        kb.op('dve', lambda e, d=d: e.memset(STbf[:, d, :, :], 0.0), writes=STb[d])
    gens = [dir_gen(d) for d in dirs]
    if DBG.get('no_interleave'):
        for g_ in gens:
            for _ in g_:
                pass
    else:
        while gens:
            for g_ in list(gens):
                try:
                    next(g_)
                except StopIteration:
                    gens.remove(g_)


FN_PARAMS = {'c128': [128, 128], 'ns128': [128, 128], 'twc': [128, 64], 'tws': [128, 64], 'fl1': [128, 64], 'fl2': [128, 64],
             'c256': [128, 2, 256], 'ns256': [128, 2, 256]}


def phase_fnet(kb, nc, HT, GT, YF, ZS, P, bar):
    sc_l = 1.0 / float(np.sqrt(L * 64.0))
    sc_c = 1.0 / float(np.sqrt(LC * 64.0))
    with ExitStack() as c1:
        def const(name, shape, src):
            t = kb.sb(name, shape, F32, c1)
            b, = load_consts(kb, [(t, src)])
            return t, b
        c128, c128b = const('c128', [128, 128], P['c128'])
        ns128, ns128b = const('ns128', [128, 128], P['ns128'])
        twc, twcb = const('twc', [128, 64], P['twc'])
        tws, twsb = const('tws', [128, 64], P['tws'])
        c256, c256b = const('c256', [128, 2, 256], P['c256'])
        ns256, ns256b = const('ns256', [128, 2, 256], P['ns256'])
        hc = kb.sb('hctx', [128, 2, 128], F32, c1)
        hcb = Buf()
        gc = kb.sb('gctx', [128, 2, 64], F32, c1)
        gcb = Buf()
        for lt in range(2):
            kb.op('sp', lambda e, lt=lt: e.dma_start(out=hc[:, lt, :], in_=HT[lt * 128:(lt + 1) * 128, :]), writes=[hcb], dma=True)
            kb.op('sp', lambda e, lt=lt: e.dma_start(out=gc[:, lt, :], in_=GT[lt * 128:(lt + 1) * 128, 192:256]), writes=[gcb], dma=True)
        pc = Ring(kb, 'pfc', [128, 512], F32, 1, psum=True, es=c1)
        fo = Ring(kb, 'foc', [128, 64], F32, 2, es=c1)
        for lo in range(2):
            ps, psb = pc.next()
            for lt in range(2):
                kb.op('pe', lambda e, ps=ps, lt=lt, lo=lo: e.matmul(ps[:, 0:64], lhsT=c256[:, lt, lo * 128:(lo + 1) * 128], rhs=hc[:, lt, 0:64], start=(lt == 0), stop=False),
                      reads=[c256b, hcb], writes=[psb])
            for lt in range(2):
                kb.op('pe', lambda e, ps=ps, lt=lt, lo=lo: e.matmul(ps[:, 0:64], lhsT=ns256[:, lt, lo * 128:(lo + 1) * 128], rhs=hc[:, lt, 64:128], start=False, stop=(lt == 1)),
                      reads=[ns256b, hcb], writes=[psb])
            f, fb = fo.next()
            kb.op('dve', lambda e, f=f, ps=ps, lo=lo: e.scalar_tensor_tensor(out=f[:, :], in0=ps[:, 0:64], scalar=sc_c, in1=gc[:, lo, :], op0=ALU.mult, op1=ALU.mult),
                  reads=[psb, gcb], writes=[fb])
            kb.op('pool', lambda e, f=f, lo=lo: e.dma_start(out=YF.rows(lo * 128, 192, 256), in_=f[:, :]), reads=[fb], dma=True)
        h1 = kb.sb('h1', [128, 64, 128], F32, c1)
        h1b = Buf()
        HTl = HT[LC:LT, :].rearrange("(a b) c -> a b c", b=64)
        for j in range(4):
            kb.op('sp', lambda e, j=j: e.dma_start(out=h1[:, j * 16:(j + 1) * 16, :], in_=HTl[:, j * 16:(j + 1) * 16, :]), writes=[h1b], dma=True)
        py = Ring(kb, 'py', [128, 512], F32, 4, psum=True, es=c1)
        zt = Ring(kb, 'zt', [128, 4, 128], F32, 6, es=c1)
        zo = Ring(kb, 'zo', [128, 2, 4, 128], F32, 3, es=c1)
        for pc_ in range(16):
            b0 = pc_ * 4
            pr, prb = py.next()
            pi, pib = py.next()
            kb.op('pe', lambda e, pr=pr, b0=b0: e.matmul(pr[:, :], lhsT=c128[:, :], rhs=h1[:, b0:b0 + 4, :], start=True, stop=True), reads=[c128b, h1b], writes=[prb])
            kb.op('pe', lambda e, pi=pi, b0=b0: e.matmul(pi[:, :], lhsT=ns128[:, :], rhs=h1[:, b0:b0 + 4, :], start=True, stop=True), reads=[ns128b, h1b], writes=[pib])
            cb_ = twc[:, b0:b0 + 4].unsqueeze(2).to_broadcast([128, 4, 128])
            sb_ = tws[:, b0:b0 + 4].unsqueeze(2).to_broadcast([128, 4, 128])
            prv = pr[:, :].rearrange("p (b c) -> p b c", c=128)
            piv = pi[:, :].rearrange("p (b c) -> p b c", c=128)
            t1, t1b = zt.next()
            t2, t2b = zt.next()
            z, zb = zo.next()
            kb.op('dve', lambda e, t1=t1, prv=prv, cb_=cb_: e.tensor_tensor(out=t1[:, :, :], in0=prv, in1=cb_, op=ALU.mult), reads=[prb, twcb], writes=[t1b])
            kb.op('dve', lambda e, t2=t2, piv=piv, sb_=sb_: e.tensor_tensor(out=t2[:, :, :], in0=piv, in1=sb_, op=ALU.mult), reads=[pib, twsb], writes=[t2b])
            kb.op('pool', lambda e, z=z, t1=t1, t2=t2: e.tensor_tensor(out=z[:, 0, :, :], in0=t1[:, :, :], in1=t2[:, :, :], op=ALU.add), reads=[t1b, t2b], writes=[zb])
            t3, t3b = zt.next()
            t4, t4b = zt.next()
            kb.op('dve', lambda e, t3=t3, piv=piv, cb_=cb_: e.tensor_tensor(out=t3[:, :, :], in0=piv, in1=cb_, op=ALU.mult), reads=[pib, twcb], writes=[t3b])
            kb.op('dve', lambda e, t4=t4, prv=prv, sb_=sb_: e.tensor_tensor(out=t4[:, :, :], in0=prv, in1=sb_, op=ALU.mult), reads=[prb, twsb], writes=[t4b])
            kb.op('pool', lambda e, z=z, t3=t3, t4=t4: e.tensor_tensor(out=z[:, 1, :, :], in0=t3[:, :, :], in1=t4[:, :, :], op=ALU.subtract), reads=[t3b, t4b], writes=[zb])
            for ri in range(2):
                kb.op('pool', lambda e, z=z, ri=ri, b0=b0: e.dma_start(out=ZS[ri, :, b0:b0 + 4, :], in_=z[:, ri, :, :]), reads=[zb], dma=True)
        kb.barrier(bar[:])
    with ExitStack() as c2:
        fl1 = kb.sb('fl1', [128, 64], F32, c2)
        fl2 = kb.sb('fl2', [128, 64], F32, c2)
        fl1b, fl2b = load_consts(kb, [(fl1, P['fl1']), (fl2, P['fl2'])])
        rz1 = kb.sb('rz1', [128, 128, 64], F32, c2)
        rz2 = kb.sb('rz2', [128, 128, 64], F32, c2)
        g2 = kb.sb('g2', [64, 128, 64], F32, c2)
        fo2 = kb.sb('fo2', [64, 128, 64], F32, c2)
        rz1b = [Buf() for _ in range(4)]
        rz2b = [Buf() for _ in range(4)]
        g2b = Buf()
        fo2b = [Buf() for _ in range(4)]
        for qa in range(4):
            asl = slice(qa * 32, (qa + 1) * 32)
            for ri in range(2):
                src = ZS[ri].rearrange("a b c -> b a c")
                kb.op('sp', lambda e, ri=ri, src=src, asl=asl: e.dma_start(out=rz1[ri * 64:(ri + 1) * 64, asl, :], in_=src[:, asl, 0:64]), writes=[rz1b[qa]], dma=True)
                kb.op('sp', lambda e, ri=ri, src=src, asl=asl: e.dma_start(out=rz2[ri * 64:(ri + 1) * 64, asl, :], in_=src[:, asl, 64:128]), writes=[rz2b[qa]], dma=True)
        kb.op('sp', lambda e: e.dma_start(out=g2[:, :, :], in_=GT[LC:LT, 192:256].rearrange("(b a) c -> b a c", a=128)), writes=[g2b], dma=True)
        pf = Ring(kb, 'pf', [128, 512], F32, 2, psum=True, es=c2)
        for pc_ in range(16):
            a0 = pc_ * 8
            qa = pc_ // 4
            ps, psb = pf.next()
            kb.op('pe', lambda e, ps=ps, a0=a0: e.matmul(ps[0:64, :], lhsT=fl1[:, :], rhs=rz1[:, a0:a0 + 8, :], start=True, stop=False), reads=[fl1b, rz1b[qa]], writes=[psb])
            kb.op('pe', lambda e, ps=ps, a0=a0: e.matmul(ps[0:64, :], lhsT=fl2[:, :], rhs=rz2[:, a0:a0 + 8, :], start=False, stop=True), reads=[fl2b, rz2b[qa]], writes=[psb])
            kb.op('dve', lambda e, ps=ps, a0=a0: e.scalar_tensor_tensor(out=fo2[:, a0:a0 + 8, :], in0=ps[0:64, :].rearrange("p (a c) -> p a c", c=64), scalar=sc_l,
                                                                        in1=g2[:, a0:a0 + 8, :], op0=ALU.mult, op1=ALU.mult), reads=[psb, g2b], writes=[fo2b[qa]])
        allfo = fo2b
        for (dst, b0, b1) in YF.lat_groups():
            kb.op('pool', lambda e, dst=dst, b0=b0, b1=b1: e.dma_start(out=dst, in_=fo2[b0:b1, :, :]), reads=allfo, dma=True)
        kb.barrier(bar[:])


def gate_rows(kb, es, nc, sil, silb, adaw_gate, adab_row, npost_b, ones1, ones1b, keep_es, NG=None):
    if NG is None:
        NG = kb.sb('NG', [128, 2, D], F32, keep_es)
    NGb = Buf()
    wg = kb.sb('adawg', [128, 8, D], F32, es)
    wgb = Buf()
    for k in range(8):
        kb.op('sp', lambda e, k=k: e.dma_start(out=wg[:, k, :], in_=adaw_gate[k * 128:(k + 1) * 128, :]), writes=[wgb], dma=True)
    br = kb.sb('adabr', [1, D], F32, es)
    brb, = load_consts(kb, [(br, adab_row)])
    npb = kb.sb('npostb', [128, D], F32, es)
    npbb, = load_consts(kb, [(npb, npost_b)])
    onesq = kb.sb('onesq', [128, 128], F32, es)
    onesqb = Buf()
    kb.op('dve', lambda e: e.memset(onesq[:], 1.0), writes=[onesqb])
    srep = kb.sb('silrep', [128, 8, 128], F32, es)
    pg_ = Ring(kb, 'pgate', [128, 512], F32, 2, psum=True, es=es)
    for n in range(2):
        srb = Buf()
        for k in range(8):
            kb.op('dve', lambda e, k=k, n=n: e.tensor_scalar(out=srep[:, k, :], in0=onesq[:, :], scalar1=sil[:, k, n:n + 1], scalar2=None, op0=ALU.mult),
                  reads=[onesqb, silb], writes=[srb])
        for half in range(2):
            ps, psb = pg_.next()
            for k in range(8):
                kb.op('pe', lambda e, ps=ps, k=k, half=half: e.matmul(ps[:, :], lhsT=srep[:, k, :], rhs=wg[:, k, half * 512:(half + 1) * 512], start=(k == 0), stop=False),
                      reads=[srb, wgb], writes=[psb])
            kb.op('pe', lambda e, ps=ps, half=half: e.matmul(ps[:, :], lhsT=ones1[0:1, :], rhs=br[0:1, half * 512:(half + 1) * 512], start=False, stop=True),
                  reads=[ones1b, brb], writes=[psb])
            kb.op('dve', lambda e, ps=ps, n=n, half=half: e.tensor_tensor(out=NG[:, n, half * 512:(half + 1) * 512], in0=ps[:, :], in1=npb[:, half * 512:(half + 1) * 512], op=ALU.mult),
                  reads=[psb, npbb], writes=[NGb])
    return NG, NGb


class OutProj:
    def __init__(self, kb, es, nc, yT, w_out_src, xsrc, NG, NGb, epsT, epsb, nslot=3, ysrc=None, ident=None, npol=2):
        self.kb, self.yT, self.xsrc, self.NG, self.NGb, self.epsT, self.epsb = kb, yT, xsrc, NG, NGb, epsT, epsb
        self.ysrc, self.ident = ysrc, ident
        self.pref = {}
        self.ykbs = {}
        if ysrc is not None:
            self.ytok = Ring(kb, 'ytok', [128, 4, 256], F32, 2, es=es)
            self.ytokb = Ring(kb, 'ytokb', [128, 1024], BF16, 2, es=es)
            self.pyt = Ring(kb, 'pyt', [128, 8, 128], BF16, 1, psum=True, es=es)
        self.wo, self.wob = load_weight_bf16(kb, es, nc, 'wout_bf', w_out_src, D, es)
        self.ystg = Ring(kb, 'ystg', [128, 8, 128], F32, 2, es=es)
        self.ybf = Ring(kb, 'ybf', [128, 8, 128], BF16, 2, es=es)
        self.xin = Ring(kb, 'xres', [128, D], F32, 2, es=es)
        self.xout = Ring(kb, 'xnew', [128, D], F32, nslot, es=es)
        self.pol = Ring(kb, 'pol', [128, 512], F32, npol, psum=True, es=es)
        self.st = Ring(kb, 'ost', [128, 8], F32, 4, es=es, strict=True)
        self.junk = kb.sb('ojunk', [128, 512], BF16, es)
        self.junkb = Buf()
        self.tmp = Ring(kb, 'otmp', [128, D], F32, 2, es=es)

    def _pre(self, r0):
        kb = self.kb
        if self.ysrc is None:
            ys, ysb = self.ystg.next()
            yT = self.yT
            kb.op('sp', lambda e, ys=ys: e.dma_start(out=ys[:, :, :], in_=yT[:, r0:r0 + 128].rearrange("(k p) t -> p k t", p=128)), writes=[ysb], dma=True)
            yb, ybb = self.ybf.next()
            kb.op('pool', lambda e, yb=yb, ys=ys: e.tensor_copy(out=yb[:, :, :], in_=ys[:, :, :]), reads=[ysb], writes=[ybb])
        else:
            aps, sbufs = self.ysrc(r0)
            ykb16, ykb0 = self.ytokb.next()
            ykb16b = self.ykbs.setdefault(id(ykb0), [ykb0] + [Buf() for _ in range(3)])
            for r, ap_ in enumerate(aps):
                kb.op('sp', lambda e, ykb16=ykb16, r=r, ap_=ap_: e.dma_start(out=ykb16[:, r * 256:(r + 1) * 256], in_=ap_), reads=sbufs, writes=[ykb16b[r]], dma=True)
            pt, ptb = self.pyt.next()
            idt, idtb = self.ident
            for k in range(8):
                kb.op('pe', lambda e, pt=pt, ykb16=ykb16, k=k: e.transpose(out=pt[:, k, :], in_=ykb16[:, k * 128:(k + 1) * 128], identity=idt[:, :]), reads=[ykb16b, idtb], writes=[ptb])
            yb, ybb = self.ybf.next()
            kb.op('act', lambda e, yb=yb, pt=pt: e.activation(out=yb[:, :, :], in_=pt[:, :, :], func=AF.Copy), reads=[ptb], writes=[ybb])
        xi, xib = self.xin.next()
        kb.op('sp', lambda e, xi=xi: e.dma_start(out=xi[:, :], in_=self.xsrc[r0:r0 + 128, :]), writes=[xib], dma=True)
        self.pref[r0] = (yb, ybb, xi, xib)

    def tile(self, r0, n, nxt=None):
        kb = self.kb
        if r0 not in self.pref:
            self._pre(r0)
        if nxt is not None and nxt not in self.pref:
            self._pre(nxt)
        yb, ybb, xi, xib = self.pref.pop(r0)
        pss = [self.pol.next(), self.pol.next()]
        for half in range(2):
            ps, psb = pss[half]
            for k in range(8):
                kb.op('pe', lambda e, ps=ps, k=k, half=half, yb=yb: e.matmul(ps[:, :], lhsT=yb[:, k, :], rhs=self.wo[:, k, half * 512:(half + 1) * 512],
                                                                           start=(k == 0), stop=(k == 7)), reads=[ybb, self.wob], writes=[psb])
        st, stb = self.st.next()
        for half in range(2):
            ps, psb = pss[half]
            kb.op('act', lambda e, ps=ps, st=st, half=half: e.activation(out=self.junk[:, :], in_=ps[:, :], func=AF.Square, accum_out=st[:, half:half + 1]),
                  reads=[psb], writes=[stb, self.junkb])
        kb.op('dve', lambda e, st=st: e.tensor_tensor(out=st[:, 2:3], in0=st[:, 0:1], in1=st[:, 1:2], op=ALU.add), reads=[stb], writes=[stb])
        kb.op('act', lambda e, st=st: e.activation(out=st[:, 3:4], in_=st[:, 2:3], func=AF.Sqrt, scale=1.0 / D, bias=self.epsT[:, 0:1]), reads=[stb, self.epsb], writes=[stb])
        kb.op('dve', lambda e, st=st: e.reciprocal(out=st[:, 4:5], in_=st[:, 3:4]), reads=[stb], writes=[stb])
        tm_, tmb_ = self.tmp.next()
        for half in range(2):
            ps, psb = pss[half]
            kb.op('dve', lambda e, ps=ps, st=st, tm_=tm_, half=half: e.scalar_tensor_tensor(out=tm_[:, half * 512:(half + 1) * 512], in0=ps[:, :], scalar=st[:, 4:5],
                                                                                       in1=self.NG[:, n, half * 512:(half + 1) * 512], op0=ALU.mult, op1=ALU.mult),
                  reads=[psb, stb, self.NGb], writes=[tmb_])
        xo, xob = self.xout.next()
        kb.op('dve', lambda e, xo=xo, tm_=tm_, xi=xi: e.tensor_tensor(out=xo[:, :], in0=tm_[:, :], in1=xi[:, :], op=ALU.add), reads=[tmb_, xib], writes=[xob])
        return xo, xob


L3_TOK = 2048


def part3(kb, nc, IN, ZG, X1s, OUT, C):
    epsT, epsb, ones1, ones1b = C['epsT'], C['epsb'], C['ones1'], C['ones1b']
    with ExitStack() as g:
        sil = kb.sb('sil3', [128, 8, 2], F32, g)
        silb = Buf(strict=True)
        kb.op('sp', lambda e: e.dma_start(out=sil[:], in_=IN['sil_in']), writes=[silb], dma=True)
        kb.op('act', lambda e: e.activation(out=sil[:], in_=sil[:], func=AF.Silu), reads=[silb], writes=[silb])
        NG = kb.sb('NG3', [128, 2, D], F32, g)
        with ExitStack() as g0:
            NG, NGb = gate_rows(kb, g0, nc, sil, silb, IN['adawg1'], IN['adabr1'], IN['npostb1'], ones1, ones1b, g0, NG=NG)
            kb.barrier(C['bar'][:])
        op = OutProj(kb, g, nc, None, IN['wout1'], X1s, NG, NGb, epsT, epsb, nslot=4, ysrc=ZG.src, ident=(C['ident_bf'], C['identb']), npol=6)
        for i in range(L // 128):
            xo, xob = op.tile(i * 128, 0, nxt=((i + 1) * 128 if (i + 1) * 128 < L else None))
            kb.op('pool', lambda e, xo=xo, i=i: e.dma_start(out=OUT[i * 128:(i + 1) * 128, :], in_=xo[:, :]), reads=[xob], dma=True, final=True)
        kb.barrier(C['bar'][:])


RT_PARAMS = {'logit': [128, 4], 'diffT': [128, 2, 128], 'mask01T': [128, 2, 128], 'posxi': [128, 2, 128], 'poszeta': [128, 2]}


def part2(kb, nc, IN, YG, X1s, ZD, C):
    debug = False
    sil_in, adawg, adabr, npostb, adaw1, adab1, normw1, win1, ropec, ropes = (IN[k_] for k_ in (
        'sil_in', 'adawg0', 'adabr0', 'npostb0', 'adaw1', 'adab1', 'normw1', 'win1', 'ropec', 'ropes'))
    xin, wout0 = IN['xin'], IN['wout0p']
    RP = {k_: IN['r_' + k_] for k_ in RT_PARAMS}
    X1 = X1s
    Z = ZD
    QKV = dram_tmp(nc, 'QKV', [LT, 768], BF16, debug)
    GS = dram_tmp(nc, 'GS', [LT, 256], F32, debug)
    bar, ident_f, identfb, ident_bf, identb, epsT, epsb, ones1, ones1b = (C[k_] for k_ in (
        'bar', 'ident_f', 'identfb', 'ident_bf', 'identb', 'epsT', 'epsb', 'ones1', 'ones1b'))
    with ExitStack() as pa:
        sil = kb.sb('sil', [128, 8, 2], F32, pa)
        silb = Buf(strict=True)
        kb.op('sp', lambda e: e.dma_start(out=sil[:], in_=sil_in), writes=[silb], dma=True)
        kb.op('act', lambda e: e.activation(out=sil[:], in_=sil[:], func=AF.Silu), reads=[silb], writes=[silb])
        NG = kb.sb('NG', [128, 2, D], F32, pa)
        mod1 = kb.sb('mod1', [128, 16, 2], F32, pa)
        G1 = kb.sb('G1', [128, 8, 2], F32, pa)
        with ExitStack() as pg0:
            NG, NGb = gate_rows(kb, pg0, nc, sil, silb, adawg, adabr, npostb, ones1, ones1b, pg0, NG=NG)
            mod1, mod1b = adaln_vectors(kb, pg0, nc, sil_in, adaw1, adab1, normw1, 16, mod_tile=mod1)
            nw = kb.sb('nw1', [128, 8], F32, pg0)
            nwb, = load_consts(kb, [(nw, normw1)])
            G1buf = Buf(strict=True)
            for n in range(2):
                kb.op('dve', lambda e, n=n: e.scalar_tensor_tensor(out=G1[:, :, n], in0=mod1[:, 8:16, n], scalar=1.0, in1=nw[:, :], op0=ALU.add, op1=ALU.mult),
                      reads=[mod1b, nwb], writes=[G1buf])
            kb.barrier(bar[:])
        Gb = {'G': G1buf, 'eps': epsT, 'epsb': epsb}
        op0 = OutProj(kb, pa, nc, None, wout0, xin, NG, NGb, epsT, epsb, nslot=3, ysrc=YG.src, ident=(ident_bf, identb))
        w1, w1b = load_weight_bf16(kb, pa, nc, 'w1_bf', win1, D, pa)
        for k in range(8):
            kb.op('pool', lambda e, k=k: e.tensor_scalar(out=w1[:, k, 256:512], in0=w1[:, k, 256:512], scalar1=float(128 ** -0.5), scalar2=None, op0=ALU.mult), reads=[w1b], writes=[w1b])
        pqk = Ring(kb, 'pqk', [128, 512], F32, 2, psum=True, es=pa)
        csr = Ring(kb, 'cs', [128, 2, 64], F32, 2, es=pa)
        qkvr = Ring(kb, 'qkvt', [128, 768], BF16, 2, es=pa)
        rtmp = Ring(kb, 'rtmp', [128, 4, 64], F32, 4, es=pa)
        gsr = Ring(kb, 'gst', [128, 256], F32, 2, es=pa)

        rlist = [t0_ + sb_ * 128 for (t0_, T_) in TILES for sb_ in range(T_ // 128)]

        def get_x(r0):
            n = 1 if r0 < LC else 0
            ix = rlist.index(r0)
            xo, xob = op0.tile(r0, n, nxt=(rlist[ix + 1] if ix + 1 < len(rlist) else None))
            if r0 >= LC:
                kb.op('pool', lambda e, xo=xo, r0=r0: e.dma_start(out=X1[r0 - LC:r0 - LC + 128, :], in_=xo[:, :]), reads=[xob], dma=True)
            return xo, xob

        def emit_tile(t0, T, hT, hb):
            for sub in range(T // 128):
                r0 = t0 + sub * 128
                psA, psAb = pqk.next()
                psB, psBb = pqk.next()
                for half, (ps, psb) in enumerate([(psA, psAb), (psB, psBb)]):
                    for k in range(8):
                        kb.op('pe', lambda e, ps=ps, k=k, half=half, sub=sub: e.matmul(ps[:, :], lhsT=hT[:, k, sub * 128:(sub + 1) * 128], rhs=w1[:, k, half * 512:(half + 1) * 512],
                                                                                     start=(k == 0), stop=(k == 7)), reads=[hb, w1b], writes=[psb])
                qkv, qkvb = qkvr.next()
                if r0 >= LC:
                    cs, csb = csr.next()
                    kb.op('sp', lambda e, cs=cs, r0=r0: e.dma_start(out=cs[:, 0, :], in_=ropec[r0 - LC:r0 - LC + 128, :]), writes=[csb], dma=True)
                    kb.op('sp', lambda e, cs=cs, r0=r0: e.dma_start(out=cs[:, 1, :], in_=ropes[r0 - LC:r0 - LC + 128, :]), writes=[csb], dma=True)
                    pv_ = psA[:, :].rearrange("p (g h f) -> p g h f", g=4, h=2)
                    ov_ = qkv[:, 0:512].rearrange("p (g h f) -> p g h f", g=4, h=2)
                    cb_ = cs[:, 0, :].unsqueeze(1).to_broadcast([128, 4, 64])
                    sb_ = cs[:, 1, :].unsqueeze(1).to_broadcast([128, 4, 64])
                    ta, tab = rtmp.next()
                    tb, tbb = rtmp.next()
                    kb.op('dve', lambda e, ta=ta, pv_=pv_, cb_=cb_: e.tensor_tensor(out=ta[:, :, :], in0=pv_[:, :, 0, :], in1=cb_, op=ALU.mult), reads=[psAb, csb], writes=[tab])
                    kb.op('dve', lambda e, tb=tb, pv_=pv_, sb_=sb_: e.tensor_tensor(out=tb[:, :, :], in0=pv_[:, :, 1, :], in1=sb_, op=ALU.mult), reads=[psAb, csb], writes=[tbb])
                    kb.op('pool', lambda e, ta=ta, tb=tb, ov_=ov_: e.tensor_tensor(out=ov_[:, :, 0, :], in0=ta[:, :, :], in1=tb[:, :, :], op=ALU.subtract), reads=[tab, tbb], writes=[qkvb])
                    tc_, tcb = rtmp.next()
                    td, tdb = rtmp.next()
                    kb.op('dve', lambda e, tc_=tc_, pv_=pv_, sb_=sb_: e.tensor_tensor(out=tc_[:, :, :], in0=pv_[:, :, 0, :], in1=sb_, op=ALU.mult), reads=[psAb, csb], writes=[tcb])
                    kb.op('dve', lambda e, td=td, pv_=pv_, cb_=cb_: e.tensor_tensor(out=td[:, :, :], in0=pv_[:, :, 1, :], in1=cb_, op=ALU.mult), reads=[psAb, csb], writes=[tdb])
                    kb.op('pool', lambda e, tc_=tc_, td=td, ov_=ov_: e.tensor_tensor(out=ov_[:, :, 1, :], in0=tc_[:, :, :], in1=td[:, :, :], op=ALU.add), reads=[tcb, tdb], writes=[qkvb])
                else:
                    kb.op('act', lambda e, qkv=qkv, psA=psA: e.activation(out=qkv[:, 0:512], in_=psA[:, :], func=AF.Copy), reads=[psAb], writes=[qkvb])
                kb.op('act', lambda e, qkv=qkv, psB=psB: e.activation(out=qkv[:, 512:768], in_=psB[:, 0:256], func=AF.Copy), reads=[psBb], writes=[qkvb])
                gs, gsb = gsr.next()
                kb.op('act', lambda e, gs=gs, psB=psB: e.activation(out=gs[:, :], in_=psB[:, 256:512], func=AF.Silu), reads=[psBb], writes=[gsb])
                kb.op('pool', lambda e, qkv=qkv, r0=r0: e.dma_start(out=QKV[r0:r0 + 128, :], in_=qkv[:, :]), reads=[qkvb], dma=True)
                kb.op('pool', lambda e, gs=gs, r0=r0: e.dma_start(out=GS[r0:r0 + 128, :], in_=gs[:, :]), reads=[gsb], dma=True)

        phase_proj(kb, nc, pa, None, None, G1, mod1, Gb, ident_bf, identb, emit_tile, get_x=get_x)
        kb.barrier(bar[:])
    with ExitStack() as pr:
        phase_ret(kb, nc, pr, QKV, GS, Z, RP, ident_bf, identb, epsT, epsb)
        kb.barrier(bar[:])


def phase_ret(kb, nc, es, QKV, GS, Z, RP, ident_bf, identb, epsT, epsb):
    def const(name, shape, src):
        t = kb.sb(name, shape, F32, es)
        b, = load_consts(kb, [(t, src)])
        return t, b
    lgt, lgtb = const('lgt', [128, 4], RP['logit'])
    lgtb.strict = True
    diffT, diffTb = const('diffT', [128, 2, 128], RP['diffT'])
    m01, m01b = const('m01', [128, 2, 128], RP['mask01T'])
    pxi, pxib = const('pxi', [128, 2, 128], RP['posxi'])
    pze, pzeb = const('pze', [128, 2], RP['poszeta'])
    kb.op('act', lambda e: e.activation(out=lgt[:, :], in_=lgt[:, :], func=AF.Exp, scale=-1.0), reads=[lgtb], writes=[lgtb])
    kb.op('dve', lambda e: e.tensor_scalar(out=lgt[:, :], in0=lgt[:, :], scalar1=1.0, scalar2=None, op0=ALU.add), reads=[lgtb], writes=[lgtb])
    kb.op('act', lambda e: e.activation(out=lgt[:, :], in_=lgt[:, :], func=AF.Ln), reads=[lgtb], writes=[lgtb])
    kb.op('dve', lambda e: e.tensor_scalar(out=lgt[:, :], in0=lgt[:, :], scalar1=-1.0, scalar2=None, op0=ALU.mult), reads=[lgtb], writes=[lgtb])
    dmt = kb.sb('dmt', [128, 4, 128], BF16, es)
    xib = kb.sb('xib', [128, 4, 128], F32, es)
    zet = kb.sb('zet', [128, 8], F32, es)
    tabb = Buf(strict=True)
    tmpd = kb.sb('tmpd', [128, 128], F32, es)
    tmpdb = Buf()
    for hl in range(2):
        for dr in range(2):
            j = 2 * hl + dr
            kb.op('act', lambda e, j=j, dr=dr: e.activation(out=tmpd[:, :], in_=diffT[:, dr, :], func=AF.Exp, scale=lgt[:, j:j + 1]), reads=[diffTb, lgtb], writes=[tmpdb])
            kb.op('dve', lambda e, j=j, dr=dr: e.tensor_tensor(out=dmt[:, j, :], in0=tmpd[:, :], in1=m01[:, dr, :], op=ALU.mult), reads=[tmpdb, m01b], writes=[tabb])
            kb.op('act', lambda e, j=j, dr=dr: e.activation(out=xib[:, j, :], in_=pxi[:, dr, :], func=AF.Exp, scale=lgt[:, j:j + 1]), reads=[pxib, lgtb], writes=[tabb])
            kb.op('act', lambda e, j=j, dr=dr: e.activation(out=zet[:, j:j + 1], in_=pze[:, dr:dr + 1], func=AF.Exp, scale=lgt[:, j:j + 1]), reads=[pzeb, lgtb], writes=[tabb])
            kb.op('act', lambda e, j=j: e.activation(out=zet[:, 4 + j:5 + j], in_=lgt[:, j:j + 1], func=AF.Exp, scale=128.0), reads=[lgtb], writes=[tabb])
    oacc = kb.sb('oacc', [128, NCH, 256], F32, es)
    oaccb = [Buf() for _ in range(NCH)]
    kb.op('pool', lambda e: e.memset(oacc[:], 0.0), writes=oaccb)
    R = kb.sb('Rst', [128, 2, 128], F32, es)
    Rbf = kb.sb('Rbf', [128, 2, 128], BF16, es)
    Rb = [Buf(), Buf()]
    qr = Ring(kb, 'rq', [128, 768], BF16, 3, es=es)
    qtr = Ring(kb, 'rqT', [128, 128], BF16, 3, es=es)
    ktr_ = Ring(kb, 'rkT', [128, 128], BF16, 3, es=es)
    qxr = Ring(kb, 'rqx', [128, 128], BF16, 3, es=es)
    kzr = Ring(kb, 'rkz', [128, 128], BF16, 3, es=es)
    sdr = Ring(kb, 'rsd', [128, 128], BF16, 3, es=es)
    ptT = Ring(kb, 'rptT', [128, 1024], BF16, 2, psum=True, es=es)
    pS = Ring(kb, 'rpS', [128, 512], F32, 2, psum=True, es=es)
    pO = Ring(kb, 'rpO', [128, 512], F32, 2, psum=True, es=es)
    pR = Ring(kb, 'rpR', [128, 512], F32, 2, psum=True, es=es)
    gsr = Ring(kb, 'rgs', [128, 256], F32, 2, es=es)
    zr = Ring(kb, 'rz', [128, 256], F32, 2, es=es)
    sst = Ring(kb, 'rss', [128, 8], F32, 4, es=es, strict=True)
    junk = kb.sb('rjunk', [128, 128], BF16, es)
    junkb = Buf()
    for dr in range(2):
        kb.op('dve', lambda e: e.memset(R[:], 0.0), writes=Rb)
        kb.op('dve', lambda e: e.memset(Rbf[:], 0.0), writes=Rb)
        for c in CHUNK_ORDER[dr]:
            r0 = c * 128
            q, qb = qr.next()
            kb.op('sp', lambda e, q=q, r0=r0: e.dma_start(out=q[:, :], in_=QKV[r0:r0 + 128, :]), writes=[qb], dma=True)
            for hl in range(2):
                j = 2 * hl + dr
                vt_ = q[:, 512 + hl * 128:512 + (hl + 1) * 128]
                ktok = q[:, 256 + hl * 128:256 + (hl + 1) * 128]
                if c >= 2:
                    pt, ptb = ptT.next()
                    kb.op('pe', lambda e, pt=pt, q=q, hl=hl: e.transpose(out=pt[:, 0:128], in_=q[:, hl * 128:(hl + 1) * 128], identity=ident_bf[:, :]), reads=[qb, identb], writes=[ptb])
                    kb.op('pe', lambda e, pt=pt, ktok=ktok: e.transpose(out=pt[:, 128:256], in_=ktok, identity=ident_bf[:, :]), reads=[qb, identb], writes=[ptb])
                    qT, qTb = qtr.next()
                    kT, kTb = ktr_.next()
                    qx, qxb = qxr.next()
                    kb.op('act', lambda e, qT=qT, pt=pt: e.activation(out=qT[:, :], in_=pt[:, 0:128], func=AF.Copy), reads=[ptb], writes=[qTb])
                    kb.op('dve', lambda e, qx=qx, pt=pt, j=j: e.tensor_tensor(out=qx[:, :], in0=pt[:, 0:128], in1=xib[:, j, :], op=ALU.mult), reads=[ptb, tabb], writes=[qxb])
                    kb.op('act', lambda e, kT=kT, pt=pt: e.activation(out=kT[:, :], in_=pt[:, 128:256], func=AF.Copy), reads=[ptb], writes=[kTb])
                    ps, psb = pS.next()
                    kb.op('pe', lambda e, ps=ps, kT=kT, qT=qT: e.matmul(ps[:, 0:128], lhsT=kT[:, :], rhs=qT[:, :], start=True, stop=True), reads=[kTb, qTb], writes=[psb])
                    sd, sdb = sdr.next()
                    kb.op('dve', lambda e, sd=sd, ps=ps, j=j: e.tensor_tensor(out=sd[:, :], in0=ps[:, 0:128], in1=dmt[:, j, :], op=ALU.mult), reads=[psb, tabb], writes=[sdb])
                    po, pob = pO.next()
                    kb.op('pe', lambda e, po=po, sd=sd, vt_=vt_: e.matmul(po[:, 0:128], lhsT=sd[:, :], rhs=vt_, start=True, stop=False), reads=[sdb, qb], writes=[pob])
                    kb.op('pe', lambda e, po=po, qx=qx, hl=hl: e.matmul(po[:, 0:128], lhsT=qx[:, :], rhs=Rbf[:, hl, :], start=False, stop=True), reads=[qxb, Rb[hl]], writes=[pob])
                    if dr == 0:
                        kb.op('act', lambda e, po=po, c=c, hl=hl: e.activation(out=oacc[:, c, hl * 128:(hl + 1) * 128], in_=po[:, 0:128], func=AF.Copy), reads=[pob], writes=[oaccb[c]])
                    else:
                        kb.op('dve', lambda e, po=po, c=c, hl=hl: e.tensor_tensor(out=oacc[:, c, hl * 128:(hl + 1) * 128], in0=po[:, 0:128], in1=oacc[:, c, hl * 128:(hl + 1) * 128], op=ALU.add),
                              reads=[pob, oaccb[c]], writes=[oaccb[c]])
                kz, kzb = kzr.next()
                kb.op('dve', lambda e, kz=kz, ktok=ktok, j=j: e.tensor_scalar(out=kz[:, :], in0=ktok, scalar1=zet[:, j:j + 1], scalar2=None, op0=ALU.mult), reads=[qb, tabb], writes=[kzb])
                pr_, prb = pR.next()
                kb.op('pe', lambda e, pr_=pr_, kz=kz, vt_=vt_: e.matmul(pr_[:, 0:128], lhsT=kz[:, :], rhs=vt_, start=True, stop=True), reads=[kzb, qb], writes=[prb])
                kb.op('dve', lambda e, pr_=pr_, hl=hl, j=j: e.scalar_tensor_tensor(out=R[:, hl, :], in0=R[:, hl, :], scalar=zet[:, 4 + j:5 + j], in1=pr_[:, 0:128], op0=ALU.mult, op1=ALU.add),
                      reads=[prb, tabb, Rb[hl]], writes=[Rb[hl]])
                kb.op('act', lambda e, hl=hl: e.activation(out=Rbf[:, hl, :], in_=R[:, hl, :], func=AF.Copy), reads=[Rb[hl]], writes=[Rb[hl]])
            if dr == 1 and c >= 2:
                gs, gsb = gsr.next()
                kb.op('sp', lambda e, gs=gs, r0=r0: e.dma_start(out=gs[:, :], in_=GS[r0:r0 + 128, :]), writes=[gsb], dma=True)
                zt_, ztb = zr.next()
                for hl in range(2):
                    ss, ssb = sst.next()
                    kb.op('act', lambda e, ss=ss, c=c, hl=hl: e.activation(out=junk[:, :], in_=oacc[:, c, hl * 128:(hl + 1) * 128], func=AF.Square, accum_out=ss[:, 0:1]), reads=[oaccb[c]], writes=[ssb, junkb])
                    kb.op('act', lambda e, ss=ss: e.activation(out=ss[:, 1:2], in_=ss[:, 0:1], func=AF.Sqrt, scale=1.0 / 128, bias=epsT[:, 0:1]), reads=[ssb, epsb], writes=[ssb])
                    kb.op('dve', lambda e, ss=ss: e.reciprocal(out=ss[:, 2:3], in_=ss[:, 1:2]), reads=[ssb], writes=[ssb])
                    kb.op('dve', lambda e, ss=ss, zt_=zt_, gs=gs, c=c, hl=hl: e.scalar_tensor_tensor(out=zt_[:, hl * 128:(hl + 1) * 128], in0=oacc[:, c, hl * 128:(hl + 1) * 128], scalar=ss[:, 2:3],
                                                                                                   in1=gs[:, hl * 128:(hl + 1) * 128], op0=ALU.mult, op1=ALU.mult), reads=[ssb, oaccb[c], gsb], writes=[ztb])
                kb.op('pool', lambda e, zt_=zt_, r0=r0: e.dma_start(out=Z.rows(r0 - LC, 0, 256), in_=zt_[:, :]), reads=[ztb], dma=True)


class ChunkedDram:
    def __init__(self, nc, name, nrows, rows_per, width, ranks=1, dtype=BF16):
        self.rp = rows_per
        self.n = nrows // rows_per
        assert self.n * rows_per == nrows
        self.ranks = ranks
        self.tiles = [dram_tmp(nc, '%s%d' % (name, i), [ranks * rows_per, width], dtype) for i in range(self.n)]
        self.bufs = [Buf() for _ in range(self.n)]

    def rows(self, r0, c0, c1, n=128):
        i, lr = r0 // self.rp, r0 % self.rp
        return self.tiles[i][lr:lr + n, c0:c1]

    def src(self, r0):
        i, lr = r0 // self.rp, r0 % self.rp
        return [self.tiles[i][r * self.rp + lr:r * self.rp + lr + 128, :] for r in range(self.ranks)], [self.bufs[i]]

    def lat_groups(self):
        out = []
        tpc = self.rp // 128
        for i in range(self.n):
            b0, b1 = max(tpc * i - 2, 0), min(tpc * i + tpc - 2, 64)
            if b1 <= b0:
                continue
            lrow = (b0 + 2) * 128 - i * self.rp
            out.append((self.tiles[i][lrow:lrow + (b1 - b0) * 128, 192:256].rearrange("(b a) c -> b a c", a=128), b0, b1))
        return out


GROUPS = [[0, 1, 2, 3], [4, 5, 6, 7]]


def all_gather(kb, src, dst):
    for i in range(src.n):
        kb.op('pool', lambda e, i=i: e.collective_compute("AllGather", ALU.bypass, replica_groups=GROUPS, ins=[src.tiles[i].opt()], outs=[dst.tiles[i].opt()]),
              writes=[dst.bufs[i]], cc=True)


FUSED_INPUTS = {'xin': [LT, D], 'sil_in': [128, 8, 2], 'adaw': [D, 2048], 'adab': [128, 16], 'normw': [128, 8], 'wfm': [D, NFM], 'wg': [D, 256],
                'cs64': [64, 128], 'ident': [128, 128],
                'wout0p': [D, D], 'adawg0': [D, D], 'adabr0': [1, D], 'npostb0': [128, D], 'adaw1': [D, 2048], 'adab1': [128, 16], 'normw1': [128, 8],
                'win1': [D, D], 'ropec': [L, 64], 'ropes': [L, 64],
                'wout1': [D, D], 'adawg1': [D, D], 'adabr1': [1, D], 'npostb1': [128, D]}


def build_fused():
    nc = bass.Bass("TRN2", target_bir_lowering=False)
    kb = KB(nc)
    IN = {k_: dram_in(nc, k_, shp) for k_, shp in FUSED_INPUTS.items()}
    for k_, shp in RW_PARAMS.items():
        IN['p_' + k_] = dram_in(nc, 'p_' + k_, shp)
    for k_, shp in FN_PARAMS.items():
        IN['f_' + k_] = dram_in(nc, 'f_' + k_, shp)
    for k_, shp in RT_PARAMS.items():
        IN['r_' + k_] = dram_in(nc, 'r_' + k_, shp)
    OUT = dram_out(nc, 'out', [L, D])
    YB = ChunkedDram(nc, 'Yb', LT, 1408, 256)
    YG = ChunkedDram(nc, 'Yg', LT, 1408, 256, ranks=4)
    ZB = ChunkedDram(nc, 'Zb', L, 1024, 256)
    ZG = ChunkedDram(nc, 'Zg', L, 1024, 256, ranks=4)
    X1s = dram_tmp(nc, 'X1s', [L, D], F32)
    C = {}
    C['bar'] = kb.sb('bar', [128, 1], F32)
    C['ident_f'] = kb.sb('ident_f', [128, 128], F32)
    C['ident_bf'] = kb.sb('ident_bf', [128, 128], BF16)
    C['identfb'], = load_consts(kb, [(C['ident_f'], IN['ident'])])
    C['identb'] = Buf()
    kb.op('dve', lambda e: e.tensor_copy(out=C['ident_bf'][:], in_=C['ident_f'][:]), reads=[C['identfb']], writes=[C['identb']])
    C['epsT'] = kb.sb('epsT', [128, 1], F32)
    C['epsb'] = Buf()
    kb.op('dve', lambda e: e.memset(C['epsT'][:], EPS), writes=[C['epsb']])
    C['ones1'] = kb.sb('ones1', [1, 128], F32)
    C['ones1b'] = Buf()
    kb.op('dve', lambda e: e.memset(C['ones1'][:], 1.0), writes=[C['ones1b']])
    part1(kb, nc, IN, YB, C)
    all_gather(kb, YB, YG)
    part2(kb, nc, IN, YG, X1s, ZB, C)
    all_gather(kb, ZB, ZG)
    part3(kb, nc, IN, ZG, X1s, OUT, C)
    return nc, kb


def l1_inputs(inp, core):
    b, q = core // 4, core % 4
    f = lambda a: np.ascontiguousarray(a, dtype=np.float32)
    xin = np.concatenate([inp['ctx'][b], inp['x'][b]], axis=0)
    sil = np.stack([inp['c'][b].reshape(8,128).T, inp['c_ctx'].reshape(8,128).T], axis=-1)
    adaw = inp['ada_w'][0][:, :2048]
    adab = inp['ada_b'][0][:2048].reshape(16,128).T
    normw = inp['norm_pre'][0].reshape(8,128).T
    W = inp['ev_w_in'][0]
    cols = np.concatenate([np.arange(192)+192*q, 768+np.arange(192)+192*q, 1536+np.arange(192)+192*q,
                           np.arange(2304,2432), np.arange(2432,2560), 3328+64*q+np.arange(64)])
    gcols = np.concatenate([2560+192*q+np.arange(192), 3584+64*q+np.arange(64)])
    c = np.arange(64)
    ang = 2*np.pi*np.outer(c,c)/64
    cs64 = np.concatenate([np.cos(ang), np.sin(ang)], axis=1)
    return dict(xin=f(xin), sil_in=f(sil), adaw=f(adaw), adab=f(adab), normw=f(normw), wfm=f(W[:, cols]), wg=f(W[:, gcols]),
                cs64=f(cs64), ident=np.eye(128, dtype=np.float32)), cols, gcols

def rw_params(inp, core):
    b, q = core // 4, core % 4
    f = lambda a: np.ascontiguousarray(a, dtype=np.float32)
    chs = 192*q + np.arange(192)
    def blk2(v):
        o = np.zeros((128,2), np.float32); o[:,0] = v[:128]; o[:64,1] = v[128:]; return o
    mu = inp['ev_mu'][0]
    mub = np.zeros((128,8), np.float32)
    for j, base in enumerate([0, 768, 1536]):
        m2 = blk2(mu[base+chs]); mub[:, 2*j] = m2[:,0]; mub[:, 2*j+1] = m2[:,1]
    mub[:, 6] = mu[2304:2432]; mub[:, 7] = mu[2432:2560]
    p = np.arange(128)
    lanem = np.stack([(p%4==0),(p%4==1),(p%4==2),(p%4==3),(p%2==0),(p%2==1)],axis=1).astype(np.float32)
    a0 = np.zeros((128,2,2), np.float32)
    for d in range(2): a0[:, d, :] = blk2(inp['ev_a0'][0][d][chs])
    w2 = np.concatenate([inp['ev_w2'][0][0][:, chs], inp['ev_w2'][0][1][:, chs]], axis=0)
    a2 = np.concatenate([inp['ev_a2'][0][0][:, chs], inp['ev_a2'][0][1][:, chs]], axis=0)
    w0 = np.stack([inp['ev_w0'][0][0][chs], inp['ev_w0'][0][1][chs]])[None]
    s = np.arange(128)[:,None]; t = np.arange(128)[None,:]
    strict = [(s<t), (s>t)]; incl = [(s<=t), (s>=t)]
    blk = lambda bs: (np.arange(128)[:,None]//bs == np.arange(128)[None,:]//bs)
    maskSI2 = np.stack([np.concatenate([strict[d], incl[d]],axis=1) for d in range(2)], axis=1).astype(np.float32)
    maskSI = np.stack([np.concatenate([strict[d] & blk(32), incl[d]],axis=1) for d in range(2)], axis=1).astype(np.float32)
    maskST = np.stack([np.stack([(strict[d] & blk(32)).T, (strict[d] & blk(64) & ~blk(32)).T, (strict[d] & ~blk(64)).T], axis=1) for d in range(2)], axis=1).astype(np.float32)
    cdec = -np.exp(-0.5)
    tri = np.stack([np.concatenate([incl[d], strict[d]],axis=1) for d in range(2)], axis=1).astype(np.float32)*cdec
    eh = np.zeros((128,2,4), np.float32); eh[:64,0,0]=1; eh[64:,0,1]=1; eh[:64,1,2]=1
    obd = np.zeros((128,128), np.float32); obd[:64,:64]=1; obd[64:,64:]=1
    return dict(p_mu=mub, p_lanem=lanem, p_k_k=blk2(inp['ev_k_k'][0][chs]), p_k_a=blk2(inp['ev_k_a'][0][chs]),
                p_r_k=blk2(inp['ev_r_k'][0].reshape(-1)[chs]), p_a0=a0, p_w2=f(w2), p_a2=f(a2), p_w0=f(w0),
                p_maskSI=f(maskSI), p_maskSI2=f(maskSI2), p_maskST=f(maskST), p_tri=f(tri), p_ehead=eh, p_ones_bd=obd,
                p_lnx_g=f(np.tile(inp['ev_lnx_g'][0][chs][None], (128,1))), p_lnx_b=f(np.tile(inp['ev_lnx_b'][0][chs][None], (128,1))))

def fn_params():
    f = lambda a: np.ascontiguousarray(a, dtype=np.float32)
    a = np.arange(128); b = np.arange(64)
    ang128 = 2*np.pi*np.outer(a,a)/128
    tw = 2*np.pi*np.outer(a, b)/8192
    ang64 = 2*np.pi*np.outer(b,b)/64
    fl1 = np.concatenate([np.cos(ang64), np.sin(ang64)], axis=0)
    fl2 = np.concatenate([-np.sin(ang64), np.cos(ang64)], axis=0)
    l = np.arange(256); ang256 = 2*np.pi*np.outer(l,l)/256
    c256 = np.cos(ang256).reshape(2,128,256).transpose(1,0,2); ns256 = (-np.sin(ang256)).reshape(2,128,256).transpose(1,0,2)
    return dict(f_c128=f(np.cos(ang128)), f_ns128=f(-np.sin(ang128)), f_twc=f(np.cos(tw)), f_tws=f(np.sin(tw)), f_fl1=f(fl1), f_fl2=f(fl2),
                f_c256=f(c256), f_ns256=f(ns256))


def gate_inputs(inp, b, layer):
    f = lambda a: np.ascontiguousarray(a, dtype=np.float32)
    sil = np.stack([inp['c'][b].reshape(8,128).T, inp['c_ctx'].reshape(8,128).T], axis=-1)
    return dict(sil_in=f(sil), adawg=f(inp['ada_w'][layer][:, 2048:3072]), adabr=f(inp['ada_b'][layer][2048:3072][None]),
                npostb=f(np.tile(inp['norm_post'][layer][None], (128,1))))
def l3_inputs(inp, core, z_full, x1_full):
    b, qtr = core // 4, core % 4
    f = lambda a: np.ascontiguousarray(a, dtype=np.float32)
    sl = slice(2048*qtr, 2048*(qtr+1))
    m = dict(zT=f(z_full[b][sl].T), x1=f(x1_full[b][sl]), wout=f(inp['od_w_out'][0]))
    m.update(gate_inputs(inp, b, 1))
    return m


def l2_inputs(inp, core, y_full):
    b, p = core // 4, core % 4
    f = lambda a: np.ascontiguousarray(a, dtype=np.float32)
    m = dict(xin=f(np.concatenate([inp['ctx'][b], inp['x'][b]], axis=0)), wout=f(inp['ev_w_out'][0]))
    if y_full is not None:
        m['yT'] = f(y_full[b].T)
    m.update(gate_inputs(inp, b, 0))
    m['adaw1'] = f(inp['ada_w'][1][:, :2048]); m['adab1'] = f(inp['ada_b'][1][:2048].reshape(16,128).T); m['normw1'] = f(inp['norm_pre'][1].reshape(8,128).T)
    cols = np.concatenate([off + 256*p + np.arange(256) for off in (0, 1024, 2048, 3072)])
    m['win1'] = f(inp['od_w_in'][0][:, cols])
    pos = np.arange(8192); row = pos//64; col = pos%64
    inv = 10000.0 ** (-np.arange(0, 64, 2, dtype=np.float32)/64)
    ang = np.concatenate([row[:,None]*inv[None], col[:,None]*inv[None]], axis=1).astype(np.float32)
    m['ropec'] = f(np.cos(ang)); m['ropes'] = f(np.sin(ang))
    m['ident'] = np.eye(128, dtype=np.float32)
    lg = inp['od_decay_logit'][0]
    logit = np.zeros((128,4), np.float32)
    for hl in range(2):
        for dr in range(2): logit[:, 2*hl+dr] = lg[dr][2*p+hl]
    j = np.arange(128)[:,None]; i = np.arange(128)[None,:]
    diffT = np.stack([(i-j)*np.ones((128,128)), (j-i)*np.ones((128,128))], axis=1)
    mask = np.stack([(i>=j), (j>i)], axis=1)
    posxi = np.stack([np.tile((np.arange(128)+1)[None], (128,1)), np.tile((128-np.arange(128))[None], (128,1))], axis=1)
    pze = np.stack([127-np.arange(128), np.arange(128)], axis=1)
    m.update(r_logit=logit, r_diffT=f(diffT), r_mask01T=f(mask), r_posxi=f(posxi), r_poszeta=f(pze))
    return m


def fused_inputs(inp, core):
    b, q = core // 4, core % 4
    f = lambda a: np.ascontiguousarray(a, dtype=np.float32)
    m, _, _ = l1_inputs(inp, core)
    m.update(rw_params(inp, core))
    m.update(fn_params())
    m2 = l2_inputs(inp, core, None)
    g0 = gate_inputs(inp, b, 0)
    g1 = gate_inputs(inp, b, 1)
    perm = np.concatenate([np.concatenate([192 * r + np.arange(192), 768 + 64 * r + np.arange(64)]) for r in range(4)])
    m['wout0p'] = f(inp['ev_w_out'][0][perm])
    m['adawg0'], m['adabr0'], m['npostb0'] = g0['adawg'], g0['adabr'], g0['npostb']
    m['adawg1'], m['adabr1'], m['npostb1'] = g1['adawg'], g1['adabr'], g1['npostb']
    m['wout1'] = f(inp['od_w_out'][0])
    for k_ in ('adaw1', 'adab1', 'normw1', 'win1', 'ropec', 'ropes', 'r_logit', 'r_diffT', 'r_mask01T', 'r_posxi', 'r_poszeta'):
        m[k_] = m2[k_]
    return m


def kernel(**inputs):
    inp = {k: np.asarray(v) for k, v in inputs.items()}
    cores = list(range(8))
    nc, kb = build_fused()
    kb.emit()
    maps = [fused_inputs(inp, core) for core in cores]
    res = run_bass_kernel_spmd(nc, maps, core_ids=cores).results
    return np.stack([res[0]['out'], res[4]['out']]).astype(np.float32)
```
